# Optimizing a Trainium2 kernel written in Bass

```python
import math
import jax, jax.numpy as jnp
from jax import lax
import numpy as np

D_MODEL = 2048
BATCH = 4
SEQ = 4096
DEPTH = 2

MIX_WIDTH = D_MODEL
HEAD_DIM = 128
ROPE_DIM = HEAD_DIM // 4
ROPE_THETA = 500000.0
EPS = 1e-6

MOBA_HEADS = 4
MOBA_WIDTH = MOBA_HEADS * HEAD_DIM
MOBA_BLOCK = 256
MOBA_TOPK = 3
MOBA_Q_CHUNK = 32

NSA_HEADS = 4
NSA_WIDTH = NSA_HEADS * HEAD_DIM
NSA_KV_WIDTH = HEAD_DIM
NSA_CMP_LEN = 32
NSA_CMP_STRIDE = 16
NSA_SEL_BLOCK = 64
NSA_SEL_TOPN = 16
NSA_WINDOW = 512
NSA_Q_CHUNK = 64

S5_WIDTH = 1024
S5_GROUP = 16
S5_GROUPS = S5_WIDTH // S5_GROUP
S5_STATE = 64

IN_SPLITS = (MOBA_WIDTH, MOBA_WIDTH, MOBA_WIDTH, MOBA_WIDTH,
             NSA_WIDTH,
             NSA_KV_WIDTH, NSA_KV_WIDTH, NSA_KV_WIDTH, NSA_KV_WIDTH,
             NSA_KV_WIDTH, NSA_KV_WIDTH,
             NSA_HEADS * 3, NSA_WIDTH,
             S5_WIDTH, S5_WIDTH)
IN_WIDTH = 4 * MOBA_WIDTH + 2 * NSA_WIDTH + 6 * NSA_KV_WIDTH + NSA_HEADS * 3 + 2 * S5_WIDTH

kernel_name = "hybrid_moba_nsa_s5_parallel_heads"


def rms_norm(x, w):
    xf = x.astype(jnp.float32)
    y = xf * lax.rsqrt(jnp.mean(xf * xf, axis=-1, keepdims=True) + EPS)
    return (y * w.astype(jnp.float32)).astype(x.dtype)


def rope_tables(pos):
    inv = ROPE_THETA ** (-jnp.arange(0, ROPE_DIM, 2, dtype=jnp.float32) / ROPE_DIM)
    ang = pos.astype(jnp.float32)[:, None] * inv[None, :]
    return jnp.cos(ang), jnp.sin(ang)


def apply_rope(x, cos, sin):
    half = ROPE_DIM // 2
    x1, x2, xp = x[..., :half], x[..., half:ROPE_DIM], x[..., ROPE_DIM:]
    c, s = cos.astype(x.dtype), sin.astype(x.dtype)
    return jnp.concatenate([x1 * c - x2 * s, x2 * c + x1 * s, xp], axis=-1)


def moba_attention(q, k, v):
    b, h, s, dh = q.shape
    nb = -(-s // MOBA_BLOCK)
    pad = nb * MOBA_BLOCK - s
    kp = jnp.pad(k, ((0, 0), (0, 0), (0, pad), (0, 0)))
    vp = jnp.pad(v, ((0, 0), (0, 0), (0, pad), (0, 0)))
    kb = kp.reshape(b, h, nb, MOBA_BLOCK, dh)
    vb = vp.reshape(b, h, nb, MOBA_BLOCK, dh)
    k_mean = jnp.mean(kb.astype(jnp.float32), axis=3)
    topk = min(MOBA_TOPK, nb)
    n_sel = topk * MOBA_BLOCK
    scale = dh ** -0.5
    bi = jnp.arange(b)[:, None, None, None]
    hi = jnp.arange(h)[None, :, None, None]
    blk = jnp.arange(nb)

    def chunk(c):
        start = c * MOBA_Q_CHUNK
        t = start + jnp.arange(MOBA_Q_CHUNK)
        own = start // MOBA_BLOCK
        qc = lax.dynamic_slice_in_dim(q, start, MOBA_Q_CHUNK, axis=2)
        gate = jnp.einsum('bhqd,bhnd->bhqn', qc.astype(jnp.float32), k_mean)
        gate = jnp.where(blk < own, gate, -jnp.inf)
        _, idx = lax.top_k(gate, topk)
        keep = idx < own
        k_sel = kb[bi, hi, idx]
        v_sel = vb[bi, hi, idx]
        s_sel = jnp.einsum('bhqd,bhqnkd->bhqnk', qc, k_sel).astype(jnp.float32) * scale
        s_sel = jnp.where(keep[..., None], s_sel, -jnp.inf).reshape(b, h, MOBA_Q_CHUNK, n_sel)
        k_own = lax.dynamic_slice_in_dim(kp, own * MOBA_BLOCK, MOBA_BLOCK, axis=2)
        v_own = lax.dynamic_slice_in_dim(vp, own * MOBA_BLOCK, MOBA_BLOCK, axis=2)
        s_own = jnp.einsum('bhqd,bhkd->bhqk', qc, k_own).astype(jnp.float32) * scale
        kpos = own * MOBA_BLOCK + jnp.arange(MOBA_BLOCK)
        s_own = jnp.where(kpos[None, :] <= t[:, None], s_own, -jnp.inf)
        p = jax.nn.softmax(jnp.concatenate([s_sel, s_own], axis=-1), axis=-1).astype(v.dtype)
        p_sel = p[..., :n_sel].reshape(b, h, MOBA_Q_CHUNK, topk, MOBA_BLOCK)
        return (jnp.einsum('bhqnk,bhqnkd->bhqd', p_sel, v_sel)
                + jnp.einsum('bhqk,bhkd->bhqd', p[..., n_sel:], v_own))

    out = lax.map(chunk, jnp.arange(s // MOBA_Q_CHUNK))
    return out.transpose(1, 2, 0, 3, 4).reshape(b, h, s, dh)


def nsa_attention(q, kc_tok, vc_tok, ks, vs, kw, vw, gates, kc_norm, pe_k, pe_v,
                  ck_w1, ck_w2, cv_w1, cv_w2):
    b, h, s, dh = q.shape
    scale = dh ** -0.5
    t_all = jnp.arange(s)
    n_cmp = (s - NSA_CMP_LEN) // NSA_CMP_STRIDE + 1
    tok = np.arange(n_cmp)[:, None] * NSA_CMP_STRIDE + np.arange(NSA_CMP_LEN)[None, :]
    end = tok[:, -1]

    def compress(x_tok, pe, w1, w2):
        blocks = (x_tok[:, tok] + pe).reshape(b, n_cmp, NSA_CMP_LEN * dh)
        return jax.nn.gelu(blocks @ w1) @ w2

    k_cmp = compress(kc_tok, pe_k, ck_w1, ck_w2)
    v_cmp = compress(vc_tok, pe_v, cv_w1, cv_w2)
    cos_c, sin_c = rope_tables(jnp.asarray(end, jnp.float32))
    k_cmp = apply_rope(rms_norm(k_cmp, kc_norm), cos_c, sin_c)
    valid = jnp.asarray(end)[None, :] <= t_all[:, None]
    s_cmp = jnp.einsum('bhsd,bnd->bhsn', q, k_cmp).astype(jnp.float32) * scale
    p_cmp = jax.nn.softmax(jnp.where(valid, s_cmp, jnp.float32(-1e30)), axis=-1)
    p_cmp = jnp.where(valid, p_cmp, 0.0)
    o_cmp = jnp.einsum('bhsn,bnd->bhsd', p_cmp.astype(v_cmp.dtype), v_cmp)
    n_sel = s // NSA_SEL_BLOCK
    ci = np.arange(n_cmp)[:, None] * NSA_CMP_STRIDE
    sj = np.arange(n_sel)[None, :] * NSA_SEL_BLOCK
    overlap = ((ci < sj + NSA_SEL_BLOCK) & (ci + NSA_CMP_LEN > sj)).astype(np.float32)
    imp = jnp.einsum('bhsn,nj->bsj', p_cmp, jnp.asarray(overlap))
    n_top = min(NSA_SEL_TOPN, n_sel)
    ks_b = ks.reshape(b, n_sel, NSA_SEL_BLOCK, dh)
    vs_b = vs.reshape(b, n_sel, NSA_SEL_BLOCK, dh)
    kw_pad = jnp.pad(kw, ((0, 0), (NSA_WINDOW, 0), (0, 0)))
    vw_pad = jnp.pad(vw, ((0, 0), (NSA_WINDOW, 0), (0, 0)))
    bi = jnp.arange(b)[:, None, None]
    sel_ids = jnp.arange(n_sel)[None, :]

    def chunk(c):
        start = c * NSA_Q_CHUNK
        t = start + jnp.arange(NSA_Q_CHUNK)
        qc = lax.dynamic_slice_in_dim(q, start, NSA_Q_CHUNK, axis=2)
        cur = (t // NSA_SEL_BLOCK)[:, None]
        score = lax.dynamic_slice_in_dim(imp, start, NSA_Q_CHUNK, axis=1)
        score = jnp.where(sel_ids <= cur, score, -jnp.inf)
        forced = (sel_ids == 0) | (sel_ids == cur) | (sel_ids == cur - 1)
        score = jnp.where(forced, jnp.inf, score)
        _, idx = lax.top_k(score, n_top)
        kpos = idx[..., None] * NSA_SEL_BLOCK + jnp.arange(NSA_SEL_BLOCK)
        smask = kpos <= t[None, :, None, None]
        k_g = ks_b[bi, idx]
        v_g = vs_b[bi, idx]
        s_sel = jnp.einsum('bhqd,bqnkd->bhqnk', qc, k_g).astype(jnp.float32) * scale
        s_sel = jnp.where(smask[:, None], s_sel, -jnp.inf).reshape(b, h, NSA_Q_CHUNK, n_top * NSA_SEL_BLOCK)
        p_sel = jax.nn.softmax(s_sel, axis=-1).astype(vs.dtype).reshape(b, h, NSA_Q_CHUNK, n_top, NSA_SEL_BLOCK)
        o_sel = jnp.einsum('bhqnk,bqnkd->bhqd', p_sel, v_g)
        k_win = lax.dynamic_slice_in_dim(kw_pad, start, NSA_WINDOW + NSA_Q_CHUNK, axis=1)
        v_win = lax.dynamic_slice_in_dim(vw_pad, start, NSA_WINDOW + NSA_Q_CHUNK, axis=1)
        wpos = start - NSA_WINDOW + jnp.arange(NSA_WINDOW + NSA_Q_CHUNK)
        wmask = ((wpos[None, :] <= t[:, None]) & (wpos[None, :] > t[:, None] - NSA_WINDOW)
                 & (wpos[None, :] >= 0))
        s_win = jnp.einsum('bhqd,bkd->bhqk', qc, k_win).astype(jnp.float32) * scale
        p_win = jax.nn.softmax(jnp.where(wmask, s_win, -jnp.inf), axis=-1).astype(vw.dtype)
        o_win = jnp.einsum('bhqk,bkd->bhqd', p_win, v_win)
        return o_sel, o_win

    o_sel, o_win = lax.map(chunk, jnp.arange(s // NSA_Q_CHUNK))
    o_sel = o_sel.transpose(1, 2, 0, 3, 4).reshape(b, h, s, dh)
    o_win = o_win.transpose(1, 2, 0, 3, 4).reshape(b, h, s, dh)
    g = gates.astype(q.dtype)
    return g[..., 0:1] * o_cmp + g[..., 1:2] * o_sel + g[..., 2:3] * o_win


def _complex_affine_combine(e1, e2):
    a1r, a1i, b1r, b1i = e1
    a2r, a2i, b2r, b2i = e2
    return (a2r * a1r - a2i * a1i,
            a2r * a1i + a2i * a1r,
            a2r * b1r - a2i * b1i + b2r,
            a2r * b1i + a2i * b1r + b2i)


def s5_ssm(u, a_re, a_im, b_re, b_im, c_re, c_im, d, log_dt):
    bsz, s, _ = u.shape
    uf = u.astype(jnp.float32)
    ug = uf.reshape(bsz, s, S5_GROUPS, S5_GROUP)
    dt = jnp.exp(log_dt.astype(jnp.float32))[:, None]
    ar, ai = a_re.astype(jnp.float32), a_im.astype(jnp.float32)
    mag = jnp.exp(dt * ar)
    ang = dt * ai
    abar_r, abar_i = mag * jnp.cos(ang), mag * jnp.sin(ang)
    nr, ni = abar_r - 1.0, abar_i
    den = ar * ar + ai * ai
    fr = (nr * ar + ni * ai) / den
    fi = (ni * ar - nr * ai) / den
    br, bim = b_re.astype(jnp.float32), b_im.astype(jnp.float32)
    bbar_r = fr[..., None] * br - fi[..., None] * bim
    bbar_i = fr[..., None] * bim + fi[..., None] * br
    bu_r = jnp.einsum('bsgc,gpc->sbgp', ug, bbar_r)
    bu_i = jnp.einsum('bsgc,gpc->sbgp', ug, bbar_i)
    a_r = jnp.broadcast_to(abar_r, (s, 1, S5_GROUPS, S5_STATE))
    a_i = jnp.broadcast_to(abar_i, (s, 1, S5_GROUPS, S5_STATE))
    _, _, x_r, x_i = lax.associative_scan(_complex_affine_combine, (a_r, a_i, bu_r, bu_i), axis=0)
    y = (jnp.einsum('sbgp,gcp->bsgc', x_r, c_re.astype(jnp.float32))
         - jnp.einsum('sbgp,gcp->bsgc', x_i, c_im.astype(jnp.float32)))
    y = y.reshape(bsz, s, S5_WIDTH) + d.astype(jnp.float32) * uf
    return y.astype(u.dtype)


def hybrid_layer(x, norm_w, w_in, w_out, moba_q_norm, moba_k_norm, nsa_q_norm, nsa_kc_norm,
                 nsa_ks_norm, nsa_kw_norm, nsa_pe_k, nsa_pe_v, nsa_cmp_k_w1, nsa_cmp_k_w2,
                 nsa_cmp_v_w1, nsa_cmp_v_w2, s5_a_re, s5_a_im, s5_b_re, s5_b_im, s5_c_re, s5_c_im,
                 s5_d, s5_log_dt, s5_glu_w):
    b, s, _ = x.shape
    hdn = rms_norm(x, norm_w)
    proj = hdn @ w_in
    offsets = np.cumsum(IN_SPLITS)[:-1].tolist()
    (mq, mk, mv, mz, nq, nkc, nvc, nks, nvs, nkw, nvw, ng, nz, su, sz) = jnp.split(proj, offsets, axis=-1)
    cos, sin = rope_tables(jnp.arange(s, dtype=jnp.float32))

    def heads(t, n):
        return t.reshape(b, s, n, HEAD_DIM).transpose(0, 2, 1, 3)

    mq = apply_rope(rms_norm(heads(mq, MOBA_HEADS), moba_q_norm), cos, sin)
    mk = apply_rope(rms_norm(heads(mk, MOBA_HEADS), moba_k_norm), cos, sin)
    o_moba = moba_attention(mq, mk, heads(mv, MOBA_HEADS))
    o_moba = o_moba.transpose(0, 2, 1, 3).reshape(b, s, MOBA_WIDTH)
    nq = apply_rope(rms_norm(heads(nq, NSA_HEADS), nsa_q_norm), cos, sin)
    nks = apply_rope(rms_norm(nks, nsa_ks_norm), cos, sin)
    nkw = apply_rope(rms_norm(nkw, nsa_kw_norm), cos, sin)
    gates = jax.nn.sigmoid(ng).reshape(b, s, NSA_HEADS, 3).transpose(0, 2, 1, 3)
    o_nsa = nsa_attention(nq, nkc, nvc, nks, nvs, nkw, nvw, gates, nsa_kc_norm, nsa_pe_k, nsa_pe_v,
                          nsa_cmp_k_w1, nsa_cmp_k_w2, nsa_cmp_v_w1, nsa_cmp_v_w2)
    o_nsa = o_nsa.transpose(0, 2, 1, 3).reshape(b, s, NSA_WIDTH)
    y5 = jax.nn.gelu(s5_ssm(su, s5_a_re, s5_a_im, s5_b_re, s5_b_im, s5_c_re, s5_c_im, s5_d, s5_log_dt))
    o_s5 = y5 * jax.nn.sigmoid(y5 @ s5_glu_w)
    mixed = jnp.concatenate([o_moba * jax.nn.silu(mz), o_nsa * jax.nn.silu(nz), o_s5 * jax.nn.silu(sz)], axis=-1)
    return x + mixed @ w_out


def setup_inputs(seed: int = 0) -> dict:
    key = jax.random.key(seed)
    k = jax.random.split(key, 32)
    f32 = jnp.float32
    L = DEPTH

    def nrm(kk, shape, scale):
        return scale * jax.random.normal(kk, shape, f32)

    n_idx = jnp.arange(S5_STATE, dtype=f32)[None, None, :]
    return {
        "x": jax.random.normal(k[0], (BATCH, SEQ, D_MODEL), f32),
        "norm_w": 1.0 + nrm(k[1], (L, D_MODEL), 0.02),
        "w_in": nrm(k[2], (L, D_MODEL, IN_WIDTH), D_MODEL ** -0.5),
        "w_out": nrm(k[3], (L, MIX_WIDTH, D_MODEL), MIX_WIDTH ** -0.5),
        "moba_q_norm": 1.0 + nrm(k[4], (L, HEAD_DIM), 0.02),
        "moba_k_norm": 1.0 + nrm(k[5], (L, HEAD_DIM), 0.02),
        "nsa_q_norm": 1.0 + nrm(k[6], (L, HEAD_DIM), 0.02),
        "nsa_kc_norm": 1.0 + nrm(k[7], (L, HEAD_DIM), 0.02),
        "nsa_ks_norm": 1.0 + nrm(k[8], (L, HEAD_DIM), 0.02),
        "nsa_kw_norm": 1.0 + nrm(k[9], (L, HEAD_DIM), 0.02),
        "nsa_pe_k": nrm(k[10], (L, NSA_CMP_LEN, HEAD_DIM), 0.1),
        "nsa_pe_v": nrm(k[11], (L, NSA_CMP_LEN, HEAD_DIM), 0.1),
        "nsa_cmp_k_w1": nrm(k[12], (L, NSA_CMP_LEN * HEAD_DIM, HEAD_DIM), (NSA_CMP_LEN * HEAD_DIM) ** -0.5),
        "nsa_cmp_k_w2": nrm(k[13], (L, HEAD_DIM, HEAD_DIM), HEAD_DIM ** -0.5),
        "nsa_cmp_v_w1": nrm(k[14], (L, NSA_CMP_LEN * HEAD_DIM, HEAD_DIM), (NSA_CMP_LEN * HEAD_DIM) ** -0.5),
        "nsa_cmp_v_w2": nrm(k[15], (L, HEAD_DIM, HEAD_DIM), HEAD_DIM ** -0.5),
        "s5_a_re": -0.5 * jnp.exp(nrm(k[16], (L, S5_GROUPS, S5_STATE), 0.01)),
        "s5_a_im": math.pi * n_idx + nrm(k[17], (L, S5_GROUPS, S5_STATE), 0.01),
        "s5_b_re": nrm(k[18], (L, S5_GROUPS, S5_STATE, S5_GROUP), (2 * S5_GROUP) ** -0.5),
        "s5_b_im": nrm(k[19], (L, S5_GROUPS, S5_STATE, S5_GROUP), (2 * S5_GROUP) ** -0.5),
        "s5_c_re": nrm(k[20], (L, S5_GROUPS, S5_GROUP, S5_STATE), S5_STATE ** -0.5),
        "s5_c_im": nrm(k[21], (L, S5_GROUPS, S5_GROUP, S5_STATE), S5_STATE ** -0.5),
        "s5_d": nrm(k[22], (L, S5_WIDTH), 1.0),
        "s5_log_dt": jax.random.uniform(k[23], (L, S5_GROUPS), f32, math.log(0.001), math.log(0.1)),
        "s5_glu_w": nrm(k[24], (L, S5_WIDTH, S5_WIDTH), S5_WIDTH ** -0.5),
    }


def reference(x, norm_w, w_in, w_out, moba_q_norm, moba_k_norm, nsa_q_norm, nsa_kc_norm,
              nsa_ks_norm, nsa_kw_norm, nsa_pe_k, nsa_pe_v, nsa_cmp_k_w1, nsa_cmp_k_w2,
              nsa_cmp_v_w1, nsa_cmp_v_w2, s5_a_re, s5_a_im, s5_b_re, s5_b_im, s5_c_re, s5_c_im,
              s5_d, s5_log_dt, s5_glu_w):
    for l in range(DEPTH):
        x = hybrid_layer(x, norm_w[l], w_in[l], w_out[l], moba_q_norm[l], moba_k_norm[l],
                         nsa_q_norm[l], nsa_kc_norm[l], nsa_ks_norm[l], nsa_kw_norm[l],
                         nsa_pe_k[l], nsa_pe_v[l], nsa_cmp_k_w1[l], nsa_cmp_k_w2[l],
                         nsa_cmp_v_w1[l], nsa_cmp_v_w2[l], s5_a_re[l], s5_a_im[l], s5_b_re[l],
                         s5_b_im[l], s5_c_re[l], s5_c_im[l], s5_d[l], s5_log_dt[l], s5_glu_w[l])
    return x
```

```python
import math
from contextlib import ExitStack

import numpy as np
import ml_dtypes
import concourse.bass as bass
import concourse.mybir as mybir
from concourse.bass_utils import run_bass_kernel_spmd

F32 = mybir.dt.float32
BF16 = mybir.dt.bfloat16
I32 = mybir.dt.int32
AF = mybir.ActivationFunctionType
ALU = mybir.AluOpType
AX = mybir.AxisListType

S = 4096
D = 2048
NT = S // 128
INW = 5900
TMW = 3852
DEPTH = 2
EPS = 1e-6
C_MQ, C_MK, C_MV, C_MZ, C_NQ = 0, 512, 1024, 1536, 2048
C_NKC, C_NVC, C_NKS, C_NVS, C_NKW, C_NVW = 2560, 2688, 2816, 2944, 3072, 3200
C_NG, C_NZ = 3328, 3340
NEG = -1.0e30


class Dep:
    __slots__ = ("w", "r", "name")

    def __init__(self, name=""):
        self.w = {}
        self.r = {}
        self.name = name


class Eng:
    def __init__(self, nc, eng, name):
        self.eng = eng
        self.name = name
        self.sem = nc.alloc_semaphore("sem_" + name)
        self.count = 0
        self.seen = {}


class KB:
    def __init__(self, nc, n_dma_sems=48):
        self.nc = nc
        self.E = {
            "pe": Eng(nc, nc.tensor, "pe"),
            "act": Eng(nc, nc.scalar, "act"),
            "dve": Eng(nc, nc.vector, "dve"),
            "pool": Eng(nc, nc.gpsimd, "pool"),
            "sp": Eng(nc, nc.sync, "sp"),
        }
        self.dsems = [[nc.alloc_semaphore("dsem%d" % i), 0] for i in range(n_dma_sems)]
        self.dnext = 0
        self.deps = []
        self.n_wait = 0
        self.n_ins = 0

    def dep(self, name=""):
        d = Dep(name)
        self.deps.append(d)
        return d

    def deps_n(self, n, name=""):
        return [self.dep(name + str(i)) for i in range(n)]

    def _wait(self, E, sem, val):
        k = id(sem)
        if E.seen.get(k, 0) < val:
            E.eng.wait_ge(sem, val)
            E.seen[k] = val
            self.n_wait += 1

    def _sync(self, E, reads, writes, own_sem=None):
        for d in reads:
            for k, (s, v) in d.w.items():
                self._wait(E, s, v)
        for d in writes:
            for k, (s, v) in d.w.items():
                if s is own_sem:
                    continue
                self._wait(E, s, v)
            for k, (s, v) in d.r.items():
                if s is own_sem:
                    continue
                self._wait(E, s, v)

    def _record(self, sem, val, reads, writes):
        k = id(sem)
        for d in writes:
            d.w = {k: (sem, val)}
            d.r = {}
        for d in reads:
            d.r[k] = (sem, val)

    def op(self, e, f, reads=(), writes=()):
        E = self.E[e]
        self._sync(E, reads, writes, own_sem=E.sem)
        ins = f(E.eng)
        E.count += 1
        ins.then_inc(E.sem, 1)
        self._record(E.sem, E.count, reads, writes)
        self.n_ins += 1
        return ins

    def dma(self, q, out, in_, reads=(), writes=(), **kw):
        E = self.E[q]
        self._sync(E, reads, writes)
        ent = self.dsems[self.dnext]
        self.dnext = (self.dnext + 1) % len(self.dsems)
        if ent[1] > 0:
            self._wait(E, ent[0], ent[1])
        ent[1] += 16
        ins = E.eng.dma_start(out=out, in_=in_, **kw)
        ins.then_inc(ent[0], 16)
        self._record(ent[0], ent[1], reads, writes)
        self.n_ins += 1
        return ins

    def barrier(self):
        sp = self.E["sp"]
        for n, E in self.E.items():
            if E is not sp and E.count > 0:
                self._wait(sp, E.sem, E.count)
        for s, v in self.dsems:
            if v > 0:
                self._wait(sp, s, v)
        sp.count += 1
        sp.eng.nop().then_inc(sp.sem, 1)
        for n, E in self.E.items():
            if E is not sp:
                self._wait(E, sp.sem, sp.count)
            for n2, E2 in self.E.items():
                E.seen[id(E2.sem)] = E2.count
            for s, v in self.dsems:
                E.seen[id(s)] = v
        for d in self.deps:
            d.w = {}
            d.r = {}
        self.deps = []


class Prog:
    def __init__(self, dbg=None, layers=DEPTH, phases=("A", "S5", "MOBA", "NSA", "F")):
        self.dbg = dbg or ()
        self.layers = layers
        self.phases = phases
        nc = bass.Bass("TRN2", target_bir_lowering=False)
        self.nc = nc
        self.kb = KB(nc)
        ein = lambda n, s, d: nc.dram_tensor(n, list(s), d, kind="ExternalInput").ap()
        L = DEPTH
        self.x = ein("x", [S, D], F32)
        self.norm_w = ein("norm_w", [L, D], F32)
        self.w_in = ein("w_in", [L, D, INW], F32)
        self.w_out = ein("w_out", [L, D, D], F32)
        self.hn = {}
        for n in ("moba_q_norm", "moba_k_norm", "nsa_q_norm", "nsa_kc_norm", "nsa_ks_norm", "nsa_kw_norm"):
            self.hn[n] = ein(n, [L, 128], F32)
        self.pe_k = ein("nsa_pe_k", [L, 32, 128], F32)
        self.pe_v = ein("nsa_pe_v", [L, 32, 128], F32)
        self.ck_w1 = ein("nsa_cmp_k_w1", [L, 4096, 128], F32)
        self.ck_w2 = ein("nsa_cmp_k_w2", [L, 128, 128], F32)
        self.cv_w1 = ein("nsa_cmp_v_w1", [L, 4096, 128], F32)
        self.cv_w2 = ein("nsa_cmp_v_w2", [L, 128, 128], F32)
        self.a_re = ein("s5_a_re", [L, 64, 64], F32)
        self.a_im = ein("s5_a_im", [L, 64, 64], F32)
        self.b_re = ein("s5_b_re", [L, 64, 64, 16], F32)
        self.b_im = ein("s5_b_im", [L, 64, 64, 16], F32)
        self.c_re = ein("s5_c_re", [L, 64, 16, 64], F32)
        self.c_im = ein("s5_c_im", [L, 64, 16, 64], F32)
        self.s5_d = ein("s5_d", [L, 1024], F32)
        self.log_dt = ein("s5_log_dt", [L, 64], F32)
        self.glu_w = ein("s5_glu_w", [L, 1024, 1024], F32)
        self.c_identb = ein("c_identb", [128, 128], BF16)
        self.c_identf = ein("c_identf", [128, 128], F32)
        self.c_rope = ein("c_rope", [S, 32], F32)
        self.c_ropec = ein("c_ropec", [256, 32], F32)
        self.c_tri = ein("c_tri", [128, 128], BF16)
        self.c_iota = ein("c_iota", [128, 512], F32)
        self.c_triu = ein("c_triu", [128, 128], BF16)
        self.c_dkq = ein("c_dkq", [128, 128], F32)
        self.c_selA = ein("c_selA", [NT, 128, 64], F32)
        self.c_selB = ein("c_selB", [NT, 128, 64], F32)
        self.c_esel = ein("c_esel", [64, NT, 128], BF16)
        self.c_ovl = ein("c_ovl", [256, 64], BF16)
        self.out = nc.dram_tensor("out", [S, D], F32, kind="ExternalOutput").ap()
        sk = lambda n: "ExternalOutput" if n in self.dbg else "Internal"
        self.proj_tm = nc.dram_tensor("proj_tm", [S, TMW], BF16, kind=("ExternalInput" if "proj_in" in self.dbg else sk("proj_tm"))).ap()
        self.sT = nc.dram_tensor("sT", [2048, S], BF16, kind=("ExternalInput" if "sT_in" in self.dbg else sk("sT"))).ap()
        self.mixedT = nc.dram_tensor("mixedT", [2048, S], BF16, kind=("ExternalInput" if "mixedT_in" in self.dbg else sk("mixedT"))).ap()
        self.x1 = nc.dram_tensor("x1", [S, D], F32, kind=sk("x1")).ap()
        self.kcmp_tm = nc.dram_tensor("kcmp_tm", [256, 128], BF16, kind=sk("kcmp_tm")).ap()
        self.y5d = nc.dram_tensor("y5d", [1024, S], BF16, kind=sk("y5d")).ap()

    def sbt(self, name, shape, dtype):
        self._uid = getattr(self, "_uid", 0) + 1
        return self.nc.sbuf_tensor("%s_u%d" % (name, self._uid), shape, dtype)

    def build(self):
        nc, kb = self.nc, self.kb
        with ExitStack() as st:
            self.ps = [st.enter_context(nc.psum_tensor("ps%d" % i, [128, 512], F32)) for i in range(8)]
            self.dps = kb.deps_n(8, "ps")
            self.identb = st.enter_context(self.sbt("identb", [128, 128], BF16))
            self.identf = st.enter_context(self.sbt("identf", [128, 128], F32))
            self.d_const = kb.dep("const")
            kb.dma("sp", self.identb[:], self.c_identb, writes=[self.d_const])
            kb.dma("sp", self.identf[:], self.c_identf, writes=[self.d_const])
            kb.barrier()
            for l in range(self.layers):
                src = self.x if l == 0 else self.x1
                dst = self.out if l == self.layers - 1 else self.x1
                if "A" in self.phases:
                    self.phase_A(l, src)
                    kb.barrier()
                if "S5" in self.phases:
                    self.phase_S5(l)
                    kb.barrier()
                if "MOBA" in self.phases:
                    self.phase_MOBA(l)
                    kb.barrier()
                if "NSA" in self.phases:
                    self.phase_NSA(l)
                    kb.barrier()
                if "F" in self.phases:
                    self.phase_F(l, src, dst)
                    kb.barrier()
            kb.barrier()
        return nc

    def phase_A(self, l, src):
        nc, kb = self.nc, self.kb
        ps, dps = self.ps, self.dps
        with ExitStack() as st:
            sb = lambda n, s, d: st.enter_context(self.sbt("A_" + n, s, d))
            hdnT = sb("hdnT", [128, 16, 2048], BF16)
            normw = sb("normw", [128, D], F32)
            xt = [sb("xt%d" % i, [128, D], F32) for i in range(3)]
            junk = sb("junk", [128, D], BF16)
            hb = [sb("hb%d" % i, [128, D], BF16) for i in range(3)]
            wch = [sb("wch%d" % i, [128, 16, 512], BF16) for i in range(2)]
            stg = [sb("stg%d" % i, [128, 512], BF16) for i in range(4)]
            ss = [sb("ss%d" % i, [128, 1], F32) for i in range(2)]
            d_hT = kb.deps_n(16, "hT")
            d_nw = kb.dep("nw")
            d_xt = kb.deps_n(3, "xt")
            d_junk = kb.dep("junk")
            d_hb = kb.deps_n(3, "hb")
            d_w = kb.deps_n(2, "w")
            d_stg = kb.deps_n(4, "stg")
            d_ss = kb.deps_n(2, "ss")
            kb.dma("sp", normw[:], self.norm_w[l:l + 1, :].partition_broadcast(128), writes=[d_nw])
            istg = 0
            iw = 0
            ievac = 0
            for h in range(2):
                for tt in range(16):
                    g = h * 16 + tt
                    i = tt % 2
                    xi = tt % 3
                    kb.dma("sp", xt[xi][:], src[g * 128:(g + 1) * 128, :], writes=[d_xt[xi]])
                    kb.op("act", lambda e: e.activation(out=junk[:], in_=xt[xi][:], func=AF.Square, accum_out=ss[i][:]),
                          reads=[d_xt[xi]], writes=[d_junk, d_ss[i]])
                    kb.op("dve", lambda e: e.tensor_scalar(out=ss[i][:], in0=ss[i][:], scalar1=1.0 / D, scalar2=EPS,
                                                           op0=ALU.mult, op1=ALU.add), reads=[d_ss[i]], writes=[d_ss[i]])
                    kb.op("act", lambda e: e.activation(out=ss[i][:], in_=ss[i][:], func=AF.Sqrt), reads=[d_ss[i]], writes=[d_ss[i]])
                    kb.op("dve", lambda e: e.reciprocal(out=ss[i][:], in_=ss[i][:]), reads=[d_ss[i]], writes=[d_ss[i]])
                    kb.op("dve", lambda e: e.scalar_tensor_tensor(out=hb[xi][:], in0=xt[xi][:], scalar=ss[i][:], in1=normw[:],
                                                                  op0=ALU.mult, op1=ALU.mult),
                          reads=[d_xt[xi], d_ss[i], d_nw], writes=[d_hb[xi]])
                    for half in range(2):
                        tbk = 4 + 2 * (tt % 2) + half
                        pb = ps[tbk][:].bitcast(BF16)
                        for k in range(8):
                            kc = half * 8 + k
                            kb.op("pe", lambda e: e.transpose(out=pb[:, k * 128:(k + 1) * 128], in_=hb[xi][:, kc * 128:(kc + 1) * 128],
                                                              identity=self.identb[:]),
                                  reads=[d_hb[xi], self.d_const], writes=[dps[tbk]])
                        eng = "act" if half == 0 else "dve"
                        dst = hdnT[:, half * 8:(half + 1) * 8, tt * 128:(tt + 1) * 128]
                        srcp = pb[:, 0:1024].rearrange("p (k n) -> p k n", k=8)
                        if eng == "act":
                            kb.op("act", lambda e: e.activation(out=dst, in_=srcp, func=AF.Copy), reads=[dps[tbk]], writes=[d_hT[tt]])
                        else:
                            kb.op("dve", lambda e: e.tensor_copy(out=dst, in_=srcp), reads=[dps[tbk]], writes=[d_hT[tt]])
                chunks = [(c0, min(512, TMW - c0)) for c0 in range(0, TMW, 512)]
                for (c0, cw) in chunks:
                    wi = iw % 2
                    iw += 1
                    kb.dma("pool", wch[wi][:, :, 0:cw], self.w_in[l, :, c0:c0 + cw].rearrange("(k p) n -> p k n", p=128),
                           writes=[d_w[wi]])
                    for tt in range(16):
                        g = h * 16 + tt
                        pbank = ievac % 4
                        for kc in range(16):
                            kb.op("pe", lambda e: e.matmul(ps[pbank][:, 0:cw], lhsT=hdnT[:, kc, tt * 128:(tt + 1) * 128],
                                                           rhs=wch[wi][:, kc, 0:cw], start=(kc == 0), stop=(kc == 15)),
                                  reads=[d_hT[tt], d_w[wi]], writes=[dps[pbank]])
                        si = istg % 4
                        istg += 1
                        if ievac % 2 == 0:
                            kb.op("act", lambda e: e.activation(out=stg[si][:, 0:cw], in_=ps[pbank][:, 0:cw], func=AF.Copy),
                                  reads=[dps[pbank]], writes=[d_stg[si]])
                        else:
                            kb.op("dve", lambda e: e.tensor_copy(out=stg[si][:, 0:cw], in_=ps[pbank][:, 0:cw]),
                                  reads=[dps[pbank]], writes=[d_stg[si]])
                        ievac += 1
                        kb.dma("sp", self.proj_tm[g * 128:(g + 1) * 128, c0:c0 + cw], stg[si][:, 0:cw], reads=[d_stg[si]])
                for fc in range(4):
                    c0 = TMW + fc * 512
                    wi = iw % 2
                    iw += 1
                    kb.dma("pool", wch[wi][:], self.w_in[l, :, c0:c0 + 512].rearrange("(k p) n -> p k n", p=128), writes=[d_w[wi]])
                    for ctl in range(4):
                        row0 = fc * 512 + ctl * 128
                        for tb in range(4):
                            pbank = ievac % 4
                            for kc in range(16):
                                kb.op("pe", lambda e: e.matmul(ps[pbank][:], lhsT=wch[wi][:, kc, ctl * 128:(ctl + 1) * 128],
                                                               rhs=hdnT[:, kc, tb * 512:(tb + 1) * 512], start=(kc == 0), stop=(kc == 15)),
                                      reads=d_hT[tb * 4:(tb + 1) * 4] + [d_w[wi]], writes=[dps[pbank]])
                            si = istg % 4
                            istg += 1
                            if ievac % 2 == 0:
                                kb.op("act", lambda e: e.activation(out=stg[si][:], in_=ps[pbank][:], func=AF.Copy),
                                      reads=[dps[pbank]], writes=[d_stg[si]])
                            else:
                                kb.op("dve", lambda e: e.tensor_copy(out=stg[si][:], in_=ps[pbank][:]),
                                      reads=[dps[pbank]], writes=[d_stg[si]])
                            ievac += 1
                            t0 = h * 2048 + tb * 512
                            kb.dma("sp", self.sT[row0:row0 + 128, t0:t0 + 512], stg[si][:], reads=[d_stg[si]])

    def phase_F(self, l, src, dst):
        nc, kb = self.nc, self.kb
        ps, dps = self.ps, self.dps
        with ExitStack() as st:
            sb = lambda n, s, d: st.enter_context(self.sbt("F_" + n, s, d))
            wo = sb("wo", [128, 16, D], BF16)
            mT = [sb("mT%d" % i, [128, 16, 512], BF16) for i in range(2)]
            xr = [sb("xr%d" % i, [128, D], F32) for i in range(2)]
            ot = [sb("ot%d" % i, [128, D], F32) for i in range(2)]
            d_wo = kb.deps_n(4, "wo")
            d_mT = kb.deps_n(2, "mT")
            d_xr = kb.deps_n(2, "xr")
            d_ot = kb.deps_n(2, "ot")
            for c in range(4):
                kb.dma("pool", wo[:, :, c * 512:(c + 1) * 512], self.w_out[l, :, c * 512:(c + 1) * 512].rearrange("(k p) n -> p k n", p=128),
                       writes=[d_wo[c]])
            ie = 0
            import os
            for tb in range(int(os.environ.get('F_TB', 8))):
                mi = tb % 2
                kb.dma("sp", mT[mi][:], self.mixedT[:, tb * 512:(tb + 1) * 512].rearrange("(k p) n -> p k n", p=128), writes=[d_mT[mi]])
                for t4 in range(4):
                    g = tb * 4 + t4
                    i = g % 2
                    kb.dma("sp", xr[i][:], src[g * 128:(g + 1) * 128, :], writes=[d_xr[i]])
                    for c in range(4):
                        pbank = ie % 4
                        ie += 1
                        for kc in range(16):
                            kb.op("pe", lambda e: e.matmul(ps[pbank][:], lhsT=mT[mi][:, kc, t4 * 128:(t4 + 1) * 128],
                                                           rhs=wo[:, kc, c * 512:(c + 1) * 512], start=(kc == 0), stop=(kc == 15)),
                                  reads=[d_mT[mi], d_wo[c]], writes=[dps[pbank]])
                        kb.op("dve", lambda e: e.tensor_tensor(out=ot[i][:, c * 512:(c + 1) * 512], in0=ps[pbank][:],
                                                               in1=xr[i][:, c * 512:(c + 1) * 512], op=ALU.add),
                              reads=[dps[pbank], d_xr[i]], writes=[d_ot[i]])
                    kb.dma("pool", dst[g * 128:(g + 1) * 128, :], ot[i][:], reads=[d_ot[i]])

    def sincos_turns(self, turns, cos_out, sin_out, tmpf, tmpi, tmpf2, dT, dC, dS, dtmp):
        kb = self.kb
        TWO_PI = 6.283185
        kb.op("dve", lambda e: e.tensor_copy(out=tmpi, in_=turns), reads=[dT], writes=[dtmp])
        kb.op("dve", lambda e: e.tensor_tensor(out=tmpf, in0=turns, in1=tmpi, op=ALU.subtract), reads=[dT, dtmp], writes=[dtmp])
        kb.op("act", lambda e: e.activation(out=sin_out, in_=tmpf, func=AF.Sin, scale=TWO_PI), reads=[dtmp], writes=[dS])
        kb.op("dve", lambda e: e.tensor_scalar(out=tmpf2, in0=tmpf, scalar1=0.25, scalar2=None, op0=ALU.add), reads=[dtmp], writes=[dtmp])
        kb.op("dve", lambda e: e.scalar_tensor_tensor(out=tmpf2, in0=tmpf2, scalar=0.5, in1=tmpf2, op0=ALU.is_gt, op1=ALU.subtract),
              reads=[dtmp], writes=[dtmp])
        kb.op("act", lambda e: e.activation(out=cos_out, in_=tmpf2, func=AF.Sin, scale=-TWO_PI), reads=[dtmp], writes=[dC])

    def phase_S5_v1(self, l):
        nc, kb = self.nc, self.kb
        ps, dps = self.ps, self.dps
        with ExitStack() as st:
            sb = lambda n, s, d: st.enter_context(self.sbt("S_" + n, s, d))
            BT = [sb("BT%d" % i, [128, 32, 128], BF16) for i in range(2)]
            CT = [sb("CT%d" % i, [128, 32, 128], BF16) for i in range(2)]
            prm = sb("prm", [128, 24, 32], F32)
            prmi = sb("prmi", [128, 32], I32)
            Dt = sb("Dt", [128, 8], F32)
            gluw = sb("gluw", [128, 8, 1024], BF16)
            d_BT, d_CT, d_prm, d_Dt, d_glu = kb.deps_n(5, "s5c")
            AR, AI, LDT, DTT, MM, PHI, COS, SIN, FR, FI, C512, S512, T0, T1, T2, T3, T4, T5 = range(18)
            P = lambda i: prm[:, i, :]
            kb.dma("pool", gluw[:], self.glu_w[l].rearrange("(k p) n -> p k n", p=128), writes=[d_glu])
            with ExitStack() as st2:
                sb2 = lambda n, s, d: st2.enter_context(self.sbt("S2_" + n, s, d))
                XA = sb2("XA", [32, 3, 128], F32)
                ld2 = sb2("ld2", [32, 2], F32)
                XD = sb2("XD", [8, 128], F32)
                pads = [sb2("pad%d" % i, [128, 32, 128], F32) for i in range(4)]
                d_XA, d_ld2, d_XD = kb.deps_n(3, "xa")
                d_pad = kb.deps_n(4, "pad")
                kb.dma("sp", XA[:, 0, :], self.a_re[l].rearrange("(q gl) p -> q (gl p)", gl=2), writes=[d_XA])
                kb.dma("sp", XA[:, 1, :], self.a_im[l].rearrange("(q gl) p -> q (gl p)", gl=2), writes=[d_XA])
                kb.dma("sp", ld2[:], self.log_dt[l:l + 1, :].rearrange("o (q gl) -> (o q) gl", gl=2), writes=[d_ld2])
                kb.dma("sp", XD[:], self.s5_d[l:l + 1, :].rearrange("o (c p) -> (o c) p", p=128), writes=[d_XD])
                kb.op("dve", lambda e: e.tensor_copy(out=XA[:, 2, :].rearrange("q (gl p) -> q gl p", gl=2),
                                                     in_=ld2[:].unsqueeze(2).to_broadcast([32, 2, 64])),
                      reads=[d_ld2, d_XA], writes=[d_XA])
                for i in range(4):
                    eng = "dve" if i % 2 == 0 else "pool"
                    kb.op(eng, lambda e: e.memset(pads[i][:].rearrange("p q c -> p (q c)"), 0.0), writes=[d_pad[i]])
                srcB = [self.b_re[l], self.b_im[l]]
                srcC = [self.c_re[l], self.c_im[l]]
                for k in range(4):
                    for gl in range(2):
                        for i in range(2):
                            dstb = pads[i][gl * 64:(gl + 1) * 64, :, :].rearrange("p (ct k) c -> p k ct c", k=4)[:, k, :, 32 * k + 16 * gl:32 * k + 16 * gl + 16]
                            sb_ = srcB[i].rearrange("(ct k gl) p c -> k gl p ct c", k=4, gl=2)[k, gl]
                            kb.dma("sp", dstb, sb_, reads=[d_pad[i]], writes=[d_pad[i]])
                            dstc = pads[2 + i][32 * k + 16 * gl:32 * k + 16 * gl + 16, :, :].rearrange("p (ct k) c -> p k ct c", k=4)[:, k, :, gl * 64:(gl + 1) * 64]
                            sc_ = srcC[i].rearrange("(ct k gl) c p -> k gl c ct p", k=4, gl=2)[k, gl]
                            kb.dma("sp", dstc, sc_, reads=[d_pad[2 + i]], writes=[d_pad[2 + i]])
                for j in range(3):
                    kb.op("pe", lambda e: e.transpose(out=ps[0][:, j * 32:(j + 1) * 32], in_=XA[:, j, :], identity=self.identf[0:32, 0:32]),
                          reads=[d_XA, self.d_const], writes=[dps[0]])
                kb.op("pe", lambda e: e.transpose(out=ps[0][:, 96:104], in_=XD[:], identity=self.identf[0:8, 0:8]),
                      reads=[d_XD, self.d_const], writes=[dps[0]])
                kb.op("dve", lambda e: e.tensor_copy(out=prm[:, 0:3, :].rearrange("p a q -> p (a q)"), in_=ps[0][:, 0:96]), reads=[dps[0]], writes=[d_prm])
                kb.op("dve", lambda e: e.tensor_copy(out=Dt[:], in_=ps[0][:, 96:104]), reads=[dps[0]], writes=[d_Dt])
                R_, W_ = [d_prm], [d_prm]
                tt = lambda o, a, b, op: kb.op("dve", lambda e: e.tensor_tensor(out=P(o), in0=P(a), in1=P(b), op=op), reads=R_, writes=W_)
                kb.op("act", lambda e: e.activation(out=P(DTT), in_=P(LDT), func=AF.Exp), reads=R_, writes=W_)
                tt(T0, DTT, AR, ALU.mult)
                kb.op("act", lambda e: e.activation(out=P(MM), in_=P(T0), func=AF.Exp), reads=R_, writes=W_)
                tt(T0, DTT, AI, ALU.mult)
                kb.op("dve", lambda e: e.tensor_scalar(out=P(T1), in0=P(T0), scalar1=1.0 / (2.0 * math.pi), scalar2=None, op0=ALU.mult), reads=R_, writes=W_)
                kb.op("dve", lambda e: e.tensor_copy(out=prmi[:], in_=P(T1)), reads=R_, writes=W_)
                kb.op("dve", lambda e: e.tensor_tensor(out=P(PHI), in0=P(T1), in1=prmi[:], op=ALU.subtract), reads=R_, writes=W_)
                self.sincos_turns(P(PHI), P(COS), P(SIN), P(T2), prmi[:], P(T3), d_prm, d_prm, d_prm, d_prm)
                kb.op("dve", lambda e: e.tensor_scalar(out=P(T4), in0=P(PHI), scalar1=512.0, scalar2=None, op0=ALU.mult), reads=R_, writes=W_)
                self.sincos_turns(P(T4), P(C512), P(S512), P(T2), prmi[:], P(T3), d_prm, d_prm, d_prm, d_prm)
                tt(T0, MM, COS, ALU.mult)
                tt(T1, MM, SIN, ALU.mult)
                kb.op("dve", lambda e: e.tensor_scalar(out=P(T0), in0=P(T0), scalar1=-1.0, scalar2=None, op0=ALU.add), reads=R_, writes=W_)
                tt(T2, AR, AR, ALU.mult)
                tt(T3, AI, AI, ALU.mult)
                tt(T2, T2, T3, ALU.add)
                kb.op("dve", lambda e: e.reciprocal(out=P(T2), in_=P(T2)), reads=R_, writes=W_)
                tt(T3, T0, AR, ALU.mult)
                tt(T4, T1, AI, ALU.mult)
                tt(T3, T3, T4, ALU.add)
                tt(FR, T3, T2, ALU.mult)
                tt(T3, T1, AR, ALU.mult)
                tt(T4, T0, AI, ALU.mult)
                tt(T3, T3, T4, ALU.subtract)
                tt(FI, T3, T2, ALU.mult)
                ctmp = [sb2("ctmp%d" % i, [128, 4, 128], F32) for i in range(4)]
                d_ctmp = kb.dep("ctmp")
                for q4 in range(8):
                    for i in range(4):
                        bank = 4 + i
                        for k in range(4):
                            q = q4 * 4 + k
                            kb.op("pe", lambda e: e.transpose(out=ps[bank][:, k * 128:(k + 1) * 128], in_=pads[i][:, q, :], identity=self.identf[:]),
                                  reads=[d_pad[i], self.d_const], writes=[dps[bank]])
                    for i in range(2):
                        kb.op("act", lambda e: e.activation(out=BT[i][:, q4 * 4:(q4 + 1) * 4, :].rearrange("p a b -> p (a b)"), in_=ps[4 + i][:], func=AF.Copy),
                              reads=[dps[4 + i]], writes=[d_BT])
                    frb = prm[:, FR, q4 * 4:(q4 + 1) * 4].unsqueeze(2).to_broadcast([128, 4, 128])
                    fib = prm[:, FI, q4 * 4:(q4 + 1) * 4].unsqueeze(2).to_broadcast([128, 4, 128])
                    crp = ps[6][:].rearrange("p (a b) -> p a b", a=4)
                    cip = ps[7][:].rearrange("p (a b) -> p a b", a=4)
                    tmpw = [d_ctmp]
                    kb.op("dve", lambda e: e.tensor_tensor(out=ctmp[0][:], in0=crp, in1=frb, op=ALU.mult), reads=[dps[6], d_prm], writes=tmpw)
                    kb.op("dve", lambda e: e.tensor_tensor(out=ctmp[1][:], in0=cip, in1=fib, op=ALU.mult), reads=[dps[7], d_prm], writes=tmpw)
                    kb.op("dve", lambda e: e.tensor_tensor(out=CT[0][:, q4 * 4:(q4 + 1) * 4, :], in0=ctmp[0][:], in1=ctmp[1][:], op=ALU.subtract),
                          reads=tmpw, writes=[d_CT])
                    kb.op("dve", lambda e: e.tensor_tensor(out=ctmp[2][:], in0=crp, in1=fib, op=ALU.mult), reads=[dps[6], d_prm], writes=tmpw)
                    kb.op("dve", lambda e: e.tensor_tensor(out=ctmp[3][:], in0=cip, in1=frb, op=ALU.mult), reads=[dps[7], d_prm], writes=tmpw)
                    kb.op("dve", lambda e: e.scalar_tensor_tensor(out=CT[1][:, q4 * 4:(q4 + 1) * 4, :], in0=ctmp[2][:], scalar=-1.0, in1=ctmp[3][:],
                                                                  op0=ALU.mult, op1=ALU.subtract), reads=tmpw, writes=[d_CT])
                kb.barrier()
            y5T = sb("y5T", [128, 8, S], BF16)
            cosT = [sb("cosT%d" % i, [128, 4, 512], BF16) for i in range(2)]
            sinT = [sb("sinT%d" % i, [128, 4, 512], BF16) for i in range(2)]
            iota = sb("iota", [128, 512], F32)
            angi = sb("angi", [128, 512], I32)
            uT = [sb("uT%d" % i, [128, 512], BF16) for i in range(3)]
            tf = [sb("tf%d" % i, [128, 512], F32) for i in range(3)]
            tb_ = [sb("tb%d" % i, [128, 512], BF16) for i in range(6)]
            rb_ = [sb("rb%d" % i, [128, 512], BF16) for i in range(4)]
            BuS = [[sb("BuS%d%d" % (i, j), [128, 512], BF16) for j in range(2)] for i in range(2)]
            zf = [[sb("zf%d%d" % (i, j), [128, 512], F32) for j in range(2)] for i in range(2)]
            zb = [[sb("zb%d%d" % (i, j), [128, 512], BF16) for j in range(2)] for i in range(2)]
            X = [[sb("X%d%d" % (i, j), [128, 512], BF16) for j in range(2)] for i in range(2)]
            cz = [sb("cz%d" % i, [128, 2, 32], F32) for i in range(2)]
            czt = sb("czt", [128, 2], F32)
            d_y5 = kb.deps_n(8, "y5")
            d_cos = kb.deps_n(2, "cos")
            d_sin = kb.deps_n(2, "sin")
            d_iota, d_angi, d_czt = kb.deps_n(3, "tab")
            d_uT = kb.deps_n(3, "uT")
            d_tf = kb.deps_n(3, "tf")
            d_tb = kb.deps_n(6, "tb")
            d_rb = kb.deps_n(4, "rb")
            d_BuS = [kb.deps_n(2) for i in range(2)]
            d_zf = [kb.deps_n(2) for i in range(2)]
            d_zb = [kb.deps_n(2) for i in range(2)]
            d_X = [kb.deps_n(2) for i in range(2)]
            d_cz = kb.deps_n(2, "cz")
            kb.dma("sp", iota[:], self.c_iota, writes=[d_iota])
            t1, t2, t3, t4, wr, wi = tb_
            dt1, dt2, dt3, dt4, dwr, dwi = d_tb
            r1, r2, r3, r4 = rb_
            dr1, dr2, dr3, dr4 = d_rb

            def TT(eng, o, do, a, da, b_, db, op):
                kb.op(eng, lambda e: e.tensor_tensor(out=o, in0=a, in1=b_, op=op), reads=da + db, writes=[do])

            pending = []
            it = 0
            icb = 0
            for ct in range(8):
                tbi = ct % 2
                for k in range(4):
                    q = ct * 4 + k
                    kb.op("dve", lambda e: e.tensor_scalar(out=tf[0][:], in0=iota[:], scalar1=prm[:, PHI, q:q + 1], scalar2=None, op0=ALU.mult),
                          reads=[d_iota, d_prm], writes=[d_tf[0]])
                    self.sincos_turns(tf[0][:], cosT[tbi][:, k, :], sinT[tbi][:, k, :], tf[1][:], angi[:], tf[2][:], d_tf[0], d_cos[tbi], d_sin[tbi], d_tf[1])
                kb.op("dve", lambda e: e.memset(cz[0][:].rearrange("p a q -> p (a q)"), 0.0), writes=[d_cz[0]])
                for tb in range(8):
                    ui = icb % 3
                    ybank = 4 + (icb % 2)
                    icb += 1
                    par = tb % 2
                    kb.dma("sp", uT[ui][:], self.sT[ct * 128:(ct + 1) * 128, tb * 512:(tb + 1) * 512], writes=[d_uT[ui]])
                    for k in range(4):
                        q = ct * 4 + k
                        sset = it % 2
                        it += 1
                        c = cosT[tbi][:, k, :]
                        s_ = sinT[tbi][:, k, :]
                        dc, ds = [d_cos[tbi]], [d_sin[tbi]]
                        for i in range(2):
                            kb.op("pe", lambda e: e.matmul(ps[2 * sset + i][:], lhsT=BT[i][:, q, :], rhs=uT[ui][:], start=True, stop=True),
                                  reads=[d_BT, d_uT[ui]], writes=[dps[2 * sset + i]])
                            kb.op("act", lambda e: e.activation(out=BuS[sset][i][:], in_=ps[2 * sset + i][:], func=AF.Copy),
                                  reads=[dps[2 * sset + i]], writes=[d_BuS[sset][i]])
                        Br, Bi = BuS[sset][0][:], BuS[sset][1][:]
                        dBr, dBi = [d_BuS[sset][0]], [d_BuS[sset][1]]
                        TT("dve", t1[:], dt1, Br, dBr, c, dc, ALU.mult)
                        TT("dve", t2[:], dt2, Bi, dBi, s_, ds, ALU.mult)
                        TT("dve", wr[:], dwr, t1[:], [dt1], t2[:], [dt2], ALU.add)
                        TT("dve", t3[:], dt3, Bi, dBi, c, dc, ALU.mult)
                        TT("dve", t4[:], dt4, Br, dBr, s_, ds, ALU.mult)
                        TT("dve", wi[:], dwi, t3[:], [dt3], t4[:], [dt4], ALU.subtract)
                        mb = prm[:, MM, q:q + 1].to_broadcast([128, 512])
                        zr, zi = zf[sset][0], zf[sset][1]
                        dzr, dzi = d_zf[sset][0], d_zf[sset][1]
                        kb.op("dve", lambda e: e.tensor_tensor_scan(out=zr[:], data0=mb, data1=wr[:], initial=cz[par][:, 0, q:q + 1], op0=ALU.mult, op1=ALU.add),
                              reads=[d_prm, dwr, d_cz[par]], writes=[dzr])
                        kb.op("dve", lambda e: e.tensor_tensor_scan(out=zi[:], data0=mb, data1=wi[:], initial=cz[par][:, 1, q:q + 1], op0=ALU.mult, op1=ALU.add),
                              reads=[d_prm, dwi, d_cz[par]], writes=[dzi])
                        for i in range(2):
                            kb.op("act", lambda e: e.activation(out=zb[sset][i][:], in_=zf[sset][i][:], func=AF.Copy),
                                  reads=[d_zf[sset][i]], writes=[d_zb[sset][i]])
                        zr_l, zi_l = zr[:, 511:512], zi[:, 511:512]
                        c5, s5 = prm[:, C512, q:q + 1], prm[:, S512, q:q + 1]
                        nx = 1 - par
                        kb.op("dve", lambda e: e.tensor_scalar(out=czt[:, 0:1], in0=zi_l, scalar1=s5, scalar2=None, op0=ALU.mult), reads=[dzi, d_prm], writes=[d_czt])
                        kb.op("dve", lambda e: e.scalar_tensor_tensor(out=cz[nx][:, 0, q:q + 1], in0=zr_l, scalar=c5, in1=czt[:, 0:1], op0=ALU.mult, op1=ALU.subtract),
                              reads=[dzr, d_prm, d_czt], writes=[d_cz[nx]])
                        kb.op("dve", lambda e: e.tensor_scalar(out=czt[:, 1:2], in0=zi_l, scalar1=c5, scalar2=None, op0=ALU.mult), reads=[dzi, d_prm], writes=[d_czt])
                        kb.op("dve", lambda e: e.scalar_tensor_tensor(out=cz[nx][:, 1, q:q + 1], in0=zr_l, scalar=s5, in1=czt[:, 1:2], op0=ALU.mult, op1=ALU.add),
                              reads=[dzr, d_prm, d_czt], writes=[d_cz[nx]])

                        def back(sset=sset, c=c, s_=s_, dc=dc, ds=ds, q=q, k=k, ct=ct, tb=tb, ui=ui, ybank=ybank):
                            zbr, zbi = zb[sset][0][:], zb[sset][1][:]
                            dzbr, dzbi = [d_zb[sset][0]], [d_zb[sset][1]]
                            TT("pool", r1[:], dr1, zbr, dzbr, c, dc, ALU.mult)
                            TT("pool", r2[:], dr2, zbi, dzbi, s_, ds, ALU.mult)
                            TT("pool", X[sset][0][:], d_X[sset][0], r1[:], [dr1], r2[:], [dr2], ALU.subtract)
                            TT("dve", r3[:], dr3, zbr, dzbr, s_, ds, ALU.mult)
                            TT("dve", r4[:], dr4, zbi, dzbi, c, dc, ALU.mult)
                            TT("dve", X[sset][1][:], d_X[sset][1], r3[:], [dr3], r4[:], [dr4], ALU.add)
                            for i in range(2):
                                kb.op("pe", lambda e: e.matmul(ps[ybank][:], lhsT=CT[i][:, q, :], rhs=X[sset][i][:], start=(k == 0 and i == 0), stop=(k == 3 and i == 1)),
                                      reads=[d_CT, d_X[sset][i]], writes=[dps[ybank]])
                            if k == 3:
                                kb.op("dve", lambda e: e.scalar_tensor_tensor(out=tf[0][:], in0=uT[ui][:], scalar=Dt[:, ct:ct + 1], in1=ps[ybank][:], op0=ALU.mult, op1=ALU.add),
                                      reads=[d_uT[ui], d_Dt, dps[ybank]], writes=[d_tf[0]])
                                kb.op("act", lambda e: e.activation(out=tf[1][:], in_=tf[0][:], func=AF.Square), reads=[d_tf[0]], writes=[d_tf[1]])
                                kb.op("pool", lambda e: e.tensor_scalar(out=tf[1][:], in0=tf[1][:], scalar1=0.044715, scalar2=1.0, op0=ALU.mult, op1=ALU.add),
                                      reads=[d_tf[1]], writes=[d_tf[1]])
                                kb.op("pool", lambda e: e.tensor_tensor(out=tf[1][:], in0=tf[1][:], in1=tf[0][:], op=ALU.mult), reads=[d_tf[1], d_tf[0]], writes=[d_tf[1]])
                                kb.op("act", lambda e: e.activation(out=tf[2][:], in_=tf[1][:], func=AF.Sigmoid, scale=1.5957691216057308), reads=[d_tf[1]], writes=[d_tf[2]])
                                kb.op("pool", lambda e: e.tensor_tensor(out=y5T[:, ct, tb * 512:(tb + 1) * 512], in0=tf[0][:], in1=tf[2][:], op=ALU.mult),
                                      reads=[d_tf[0], d_tf[2]], writes=[d_y5[tb]])

                        if pending:
                            pending.pop(0)()
                        pending.append(back)
            while pending:
                pending.pop(0)()
            szT = [sb("szT%d" % i, [128, 512], BF16) for i in range(2)]
            og = [sb("og%d" % i, [128, 512], BF16) for i in range(2)]
            d_sz = kb.deps_n(2, "sz")
            d_og = kb.deps_n(2, "og")
            ig = 0
            for tb in range(8):
                for co in range(8):
                    i = ig % 2
                    ig += 1
                    bank = 4 + (ig % 4)
                    kb.dma("sp", szT[i][:], self.sT[1024 + co * 128:1024 + (co + 1) * 128, tb * 512:(tb + 1) * 512], writes=[d_sz[i]])
                    for ci in range(8):
                        kb.op("pe", lambda e: e.matmul(ps[bank][:], lhsT=gluw[:, ci, co * 128:(co + 1) * 128], rhs=y5T[:, ci, tb * 512:(tb + 1) * 512],
                                                       start=(ci == 0), stop=(ci == 7)), reads=[d_glu, d_y5[tb]], writes=[dps[bank]])
                    kb.op("act", lambda e: e.activation(out=tf[0][:], in_=ps[bank][:], func=AF.Sigmoid), reads=[dps[bank]], writes=[d_tf[0]])
                    kb.op("act", lambda e: e.activation(out=tf[1][:], in_=szT[i][:], func=AF.Silu), reads=[d_sz[i]], writes=[d_tf[1]])
                    kb.op("dve", lambda e: e.tensor_tensor(out=tf[0][:], in0=tf[0][:], in1=y5T[:, co, tb * 512:(tb + 1) * 512], op=ALU.mult),
                          reads=[d_tf[0], d_y5[tb]], writes=[d_tf[0]])
                    kb.op("dve", lambda e: e.tensor_tensor(out=og[i][:], in0=tf[0][:], in1=tf[1][:], op=ALU.mult), reads=[d_tf[0], d_tf[1]], writes=[d_og[i]])
                    kb.dma("sp", self.mixedT[1024 + co * 128:1024 + (co + 1) * 128, tb * 512:(tb + 1) * 512], og[i][:], reads=[d_og[i]])

    def phase_S5(self, l):
        nc, kb = self.nc, self.kb
        ps, dps = self.ps, self.dps
        Lc = 4
        NCH = S // Lc
        NH = NCH // 512
        with ExitStack() as st:
            sb = lambda n, s, d: st.enter_context(self.sbt("S_" + n, s, d))
            CT = [sb("CT%d" % i, [128, 32, 128], BF16) for i in range(2)]
            Bp = [sb("Bp%d" % i, [128, 32, 128], BF16) for i in range(2)]
            prm = sb("prm", [128, 24, 32], F32)
            apw = sb("apw", [128, 9, 2, 32], F32)
            prmi = sb("prmi", [128, 32], I32)
            Dt = sb("Dt", [128, 8], F32)
            d_CT, d_prm, d_Dt, d_apw = kb.deps_n(4, "s5c")
            d_Bp = kb.deps_n(2, "Bp")
            AR, AI, LDT, DTT, MM, PHI, COS, SIN, FR, FI, M8, PHI8, C512, S512, T0, T1, T2, T3, T4, T5 = range(20)
            P = lambda i: prm[:, i, :]
            with ExitStack() as st2:
                sb2 = lambda n, s, d: st2.enter_context(self.sbt("S2_" + n, s, d))
                XA = sb2("XA", [32, 3, 128], F32)
                ld2 = sb2("ld2", [32, 2], F32)
                XD = sb2("XD", [8, 128], F32)
                Cp = [sb2("Cp%d" % i, [128, 32, 128], F32) for i in range(2)]
                Bf = [sb2("Bf%d" % i, [128, 32, 128], F32) for i in range(2)]
                pads = [Bf[0], Bf[1], Cp[0], Cp[1]]
                d_XA, d_ld2, d_XD = kb.deps_n(3, "xa")
                d_Cp = kb.deps_n(2, "Cp")
                d_Bf = kb.deps_n(2, "Bf")
                d_pad = [d_Bf[0], d_Bf[1], d_Cp[0], d_Cp[1]]
                kb.dma("sp", XA[:, 0, :], self.a_re[l].rearrange("(q gl) p -> q (gl p)", gl=2), writes=[d_XA])
                kb.dma("sp", XA[:, 1, :], self.a_im[l].rearrange("(q gl) p -> q (gl p)", gl=2), writes=[d_XA])
                kb.dma("sp", ld2[:], self.log_dt[l:l + 1, :].rearrange("o (q gl) -> (o q) gl", gl=2), writes=[d_ld2])
                kb.dma("sp", XD[:], self.s5_d[l:l + 1, :].rearrange("o (c p) -> (o c) p", p=128), writes=[d_XD])
                kb.op("dve", lambda e: e.tensor_copy(out=XA[:, 2, :].rearrange("q (gl p) -> q gl p", gl=2),
                                                     in_=ld2[:].unsqueeze(2).to_broadcast([32, 2, 64])),
                      reads=[d_ld2, d_XA], writes=[d_XA])
                for i in range(4):
                    eng = "dve" if i % 2 == 0 else "pool"
                    kb.op(eng, lambda e: e.memset(pads[i][:].rearrange("p q c -> p (q c)"), 0.0), writes=[d_pad[i]])
                srcB = [self.b_re[l], self.b_im[l]]
                srcC = [self.c_re[l], self.c_im[l]]
                for k in range(4):
                    for gl in range(2):
                        for i in range(2):
                            dstb = pads[i][gl * 64:(gl + 1) * 64, :, :].rearrange("p (ct k) c -> p k ct c", k=4)[:, k, :, 32 * k + 16 * gl:32 * k + 16 * gl + 16]
                            sb_ = srcB[i].rearrange("(ct k gl) p c -> k gl p ct c", k=4, gl=2)[k, gl]
                            kb.dma("sp", dstb, sb_, reads=[d_pad[i]], writes=[d_pad[i]])
                            dstc = pads[2 + i][32 * k + 16 * gl:32 * k + 16 * gl + 16, :, :].rearrange("p (ct k) c -> p k ct c", k=4)[:, k, :, gl * 64:(gl + 1) * 64]
                            sc_ = srcC[i].rearrange("(ct k gl) c p -> k gl c ct p", k=4, gl=2)[k, gl]
                            kb.dma("sp", dstc, sc_, reads=[d_pad[2 + i]], writes=[d_pad[2 + i]])
                for j in range(3):
                    kb.op("pe", lambda e: e.transpose(out=ps[0][:, j * 32:(j + 1) * 32], in_=XA[:, j, :], identity=self.identf[0:32, 0:32]),
                          reads=[d_XA, self.d_const], writes=[dps[0]])
                kb.op("pe", lambda e: e.transpose(out=ps[0][:, 96:104], in_=XD[:], identity=self.identf[0:8, 0:8]),
                      reads=[d_XD, self.d_const], writes=[dps[0]])
                kb.op("dve", lambda e: e.tensor_copy(out=prm[:, 0:3, :].rearrange("p a q -> p (a q)"), in_=ps[0][:, 0:96]), reads=[dps[0]], writes=[d_prm])
                kb.op("dve", lambda e: e.tensor_copy(out=Dt[:], in_=ps[0][:, 96:104]), reads=[dps[0]], writes=[d_Dt])
                kb.op("act", lambda e: e.activation(out=Bp[0][:].rearrange("p q c -> p (q c)"), in_=Bf[0][:].rearrange("p q c -> p (q c)"), func=AF.Copy),
                      reads=[d_Bf[0]], writes=[d_Bp[0]])
                kb.op("pool", lambda e: e.tensor_copy(out=Bp[1][:].rearrange("p q c -> p (q c)"), in_=Bf[1][:].rearrange("p q c -> p (q c)")),
                      reads=[d_Bf[1]], writes=[d_Bp[1]])
                R_, W_ = [d_prm], [d_prm]
                tt = lambda o, a, b, op: kb.op("dve", lambda e: e.tensor_tensor(out=P(o), in0=P(a), in1=P(b), op=op), reads=R_, writes=W_)
                kb.op("act", lambda e: e.activation(out=P(DTT), in_=P(LDT), func=AF.Exp), reads=R_, writes=W_)
                tt(T0, DTT, AR, ALU.mult)
                kb.op("act", lambda e: e.activation(out=P(MM), in_=P(T0), func=AF.Exp), reads=R_, writes=W_)
                kb.op("act", lambda e: e.activation(out=P(M8), in_=P(T0), func=AF.Exp, scale=float(Lc)), reads=R_, writes=W_)
                tt(T0, DTT, AI, ALU.mult)
                kb.op("dve", lambda e: e.tensor_scalar(out=P(T1), in0=P(T0), scalar1=1.0 / (2.0 * math.pi), scalar2=None, op0=ALU.mult), reads=R_, writes=W_)
                kb.op("dve", lambda e: e.tensor_copy(out=prmi[:], in_=P(T1)), reads=R_, writes=W_)
                kb.op("dve", lambda e: e.tensor_tensor(out=P(PHI), in0=P(T1), in1=prmi[:], op=ALU.subtract), reads=R_, writes=W_)
                self.sincos_turns(P(PHI), P(COS), P(SIN), P(T2), prmi[:], P(T3), d_prm, d_prm, d_prm, d_prm)
                kb.op("dve", lambda e: e.tensor_scalar(out=P(T4), in0=P(PHI), scalar1=float(Lc), scalar2=None, op0=ALU.mult), reads=R_, writes=W_)
                kb.op("dve", lambda e: e.tensor_copy(out=prmi[:], in_=P(T4)), reads=R_, writes=W_)
                kb.op("dve", lambda e: e.tensor_tensor(out=P(PHI8), in0=P(T4), in1=prmi[:], op=ALU.subtract), reads=R_, writes=W_)
                kb.op("dve", lambda e: e.tensor_scalar(out=P(T4), in0=P(PHI8), scalar1=512.0, scalar2=None, op0=ALU.mult), reads=R_, writes=W_)
                self.sincos_turns(P(T4), P(C512), P(S512), P(T2), prmi[:], P(T3), d_prm, d_prm, d_prm, d_prm)
                tt(T0, MM, COS, ALU.mult)
                tt(T1, MM, SIN, ALU.mult)
                RW = [d_prm, d_apw]
                kb.op("dve", lambda e: e.memset(apw[:, 0, 0, :], 1.0), reads=RW, writes=[d_apw])
                kb.op("dve", lambda e: e.memset(apw[:, 0, 1, :], 0.0), reads=RW, writes=[d_apw])
                kb.op("dve", lambda e: e.tensor_copy(out=apw[:, 1, 0, :], in_=P(T0)), reads=RW, writes=[d_apw])
                kb.op("dve", lambda e: e.tensor_copy(out=apw[:, 1, 1, :], in_=P(T1)), reads=RW, writes=[d_apw])
                for m in range(1, Lc):
                    ar_, ai_ = apw[:, m, 0, :], apw[:, m, 1, :]
                    kb.op("dve", lambda e: e.tensor_tensor(out=P(T2), in0=ar_, in1=P(T0), op=ALU.mult), reads=RW, writes=W_)
                    kb.op("dve", lambda e: e.tensor_tensor(out=P(T3), in0=ai_, in1=P(T1), op=ALU.mult), reads=RW, writes=W_)
                    kb.op("dve", lambda e: e.tensor_tensor(out=apw[:, m + 1, 0, :], in0=P(T2), in1=P(T3), op=ALU.subtract), reads=RW, writes=[d_apw])
                    kb.op("dve", lambda e: e.tensor_tensor(out=P(T2), in0=ar_, in1=P(T1), op=ALU.mult), reads=RW, writes=W_)
                    kb.op("dve", lambda e: e.tensor_tensor(out=P(T3), in0=ai_, in1=P(T0), op=ALU.mult), reads=RW, writes=W_)
                    kb.op("dve", lambda e: e.tensor_tensor(out=apw[:, m + 1, 1, :], in0=P(T2), in1=P(T3), op=ALU.add), reads=RW, writes=[d_apw])
                kb.op("dve", lambda e: e.tensor_scalar(out=P(T0), in0=P(T0), scalar1=-1.0, scalar2=None, op0=ALU.add), reads=R_, writes=W_)
                tt(T2, AR, AR, ALU.mult)
                tt(T3, AI, AI, ALU.mult)
                tt(T2, T2, T3, ALU.add)
                kb.op("dve", lambda e: e.reciprocal(out=P(T2), in_=P(T2)), reads=R_, writes=W_)
                tt(T3, T0, AR, ALU.mult)
                tt(T4, T1, AI, ALU.mult)
                tt(T3, T3, T4, ALU.add)
                tt(FR, T3, T2, ALU.mult)
                tt(T3, T1, AR, ALU.mult)
                tt(T4, T0, AI, ALU.mult)
                tt(T3, T3, T4, ALU.subtract)
                tt(FI, T3, T2, ALU.mult)
                ctmp = [sb2("ctmp%d" % i, [128, 4, 128], F32) for i in range(4)]
                d_ctmp = kb.dep("ctmp")
                for q4 in range(8):
                    for i in range(2):
                        bank = 6 + i
                        for k in range(4):
                            q = q4 * 4 + k
                            kb.op("pe", lambda e: e.transpose(out=ps[bank][:, k * 128:(k + 1) * 128], in_=Cp[i][:, q, :], identity=self.identf[:]),
                                  reads=[d_Cp[i], self.d_const], writes=[dps[bank]])
                    frb = prm[:, FR, q4 * 4:(q4 + 1) * 4].unsqueeze(2).to_broadcast([128, 4, 128])
                    fib = prm[:, FI, q4 * 4:(q4 + 1) * 4].unsqueeze(2).to_broadcast([128, 4, 128])
                    crp = ps[6][:].rearrange("p (a b) -> p a b", a=4)
                    cip = ps[7][:].rearrange("p (a b) -> p a b", a=4)
                    tmpw = [d_ctmp]
                    kb.op("dve", lambda e: e.tensor_tensor(out=ctmp[0][:], in0=crp, in1=frb, op=ALU.mult), reads=[dps[6], d_prm], writes=tmpw)
                    kb.op("dve", lambda e: e.tensor_tensor(out=ctmp[1][:], in0=cip, in1=fib, op=ALU.mult), reads=[dps[7], d_prm], writes=tmpw)
                    kb.op("dve", lambda e: e.tensor_tensor(out=CT[0][:, q4 * 4:(q4 + 1) * 4, :], in0=ctmp[0][:], in1=ctmp[1][:], op=ALU.subtract),
                          reads=tmpw, writes=[d_CT])
                    kb.op("dve", lambda e: e.tensor_tensor(out=ctmp[2][:], in0=crp, in1=fib, op=ALU.mult), reads=[dps[6], d_prm], writes=tmpw)
                    kb.op("dve", lambda e: e.tensor_tensor(out=ctmp[3][:], in0=cip, in1=frb, op=ALU.mult), reads=[dps[7], d_prm], writes=tmpw)
                    kb.op("dve", lambda e: e.scalar_tensor_tensor(out=CT[1][:, q4 * 4:(q4 + 1) * 4, :], in0=ctmp[2][:], scalar=-1.0, in1=ctmp[3][:],
                                                                  op0=ALU.mult, op1=ALU.subtract), reads=tmpw, writes=[d_CT])
                kb.barrier()
            with ExitStack() as st3:
                sb3 = lambda n, s, d: st3.enter_context(self.sbt("S3_" + n, s, d))
                cosT = [sb3("cosT%d" % i, [128, 4, 512], BF16) for i in range(2)]
                sinT = [sb3("sinT%d" % i, [128, 4, 512], BF16) for i in range(2)]
                iota = sb3("iota", [128, 512], F32)
                angi = sb3("angi", [128, 512], I32)
                zero = sb3("zero", [128, 2], F32)
                czc = [sb3("czc%d" % i, [128, 2, 4], F32) for i in range(2)]
                czt = sb3("czt", [128, 2], F32)
                zl = sb3("zlast", [128, 2], F32)
                d_czc = kb.deps_n(2, "czc")
                d_czt = kb.dep("czt")
                d_zl = kb.dep("zl")
                uTf = [sb3("uTf%d" % i, [128, S], BF16) for i in range(2)]
                W1 = sb3("W1", [128, Lc, 8, 128], BF16)
                CA = [sb3("CA%d" % i, [128, Lc, 8, 128], BF16) for i in range(2)]
                SN = [sb3("SN%d" % i, [128, 8, 128], BF16) for i in range(2)]
                KT = [sb3("KT%d" % i, [128, Lc, 128], BF16) for i in range(2)]
                uu = [sb3("uu%d" % i, [128, 4, 128], F32) for i in range(4)]
                Xs = [[[sb3("Xs%d%d%d" % (c_, k, i), [128, NCH + 2], BF16) for i in range(2)] for k in range(4)] for c_ in range(2)]
                tf = [sb3("tf%d" % i, [128, 512], F32) for i in range(3)]
                tg = [sb3("tg%d" % i, [128, 512], F32) for i in range(3)]
                tb_ = [sb3("tb%d" % i, [128, 512], BF16) for i in range(6)]
                rb_ = [sb3("rb%d" % i, [128, 512], BF16) for i in range(4)]
                BuS = [[sb3("BuS%d%d" % (i, j), [128, 512], BF16) for j in range(2)] for i in range(2)]
                zb = [[sb3("zb%d%d" % (i, j), [128, 512], BF16) for j in range(2)] for i in range(2)]
                y5s = [sb3("y5s%d" % i, [128, 512], BF16) for i in range(2)]
                d_cos = kb.deps_n(2, "cos")
                d_sin = kb.deps_n(2, "sin")
                d_iota, d_angi, d_zero, d_W1 = kb.deps_n(4, "tab")
                d_CA = kb.deps_n(2, "CA")
                d_KT = kb.deps_n(2, "KT")
                d_uTf = kb.deps_n(2, "uTf")
                d_SN = kb.deps_n(2, "SN")
                d_uu = kb.deps_n(4, "uu")
                d_Xs = [[kb.deps_n(2) for k in range(4)] for c_ in range(2)]
                d_tf = kb.deps_n(3, "tf")
                d_tg = kb.deps_n(3, "tg")
                d_tb = kb.deps_n(6, "tb")
                d_rb = kb.deps_n(4, "rb")
                d_BuS = [kb.deps_n(2) for i in range(2)]
                d_zb = [kb.deps_n(2) for i in range(2)]
                d_y5s = kb.deps_n(2, "y5s")
                kb.dma("sp", iota[:], self.c_iota, writes=[d_iota])
                kb.op("dve", lambda e: e.memset(zero[:], 0.0), writes=[d_zero])
                for c_ in range(2):
                    for k in range(4):
                        for i in range(2):
                            kb.op("pool", lambda e: e.memset(Xs[c_][k][i][:], 0.0), writes=[d_Xs[c_][k][i]])
                t1, t2, t3, t4, wr, wi = tb_
                dt1, dt2, dt3, dt4, dwr, dwi = d_tb
                r1, r2, r3, r4 = rb_
                dr1, dr2, dr3, dr4 = d_rb

                def TT(eng, o, do, a, da, b_, db, op):
                    kb.op(eng, lambda e: e.tensor_tensor(out=o, in0=a, in1=b_, op=op), reads=da + db, writes=[do])

                def E_slice(ct, step):
                    cp = ct % 2
                    q0 = ct * 4
                    if step == 0:
                        kb.dma("sp", uTf[cp][:], self.sT[ct * 128:(ct + 1) * 128, :], writes=[d_uTf[cp]])
                    if step < 4:
                        k = step
                        q = q0 + k
                        kb.op("dve", lambda e: e.tensor_scalar(out=tg[0][:], in0=iota[:], scalar1=prm[:, PHI8, q:q + 1], scalar2=None, op0=ALU.mult),
                              reads=[d_iota, d_prm], writes=[d_tg[0]])
                        self.sincos_turns(tg[0][:], cosT[cp][:, k, :], sinT[cp][:, k, :], tg[1][:], angi[:], tg[2][:], d_tg[0], d_cos[cp], d_sin[cp], d_tg[1])
                    m = step
                    mi = m % 2
                    Brv = Bp[0][:, q0:q0 + 4, :]
                    Biv = Bp[1][:, q0:q0 + 4, :]
                    Arb = apw[:, m, 0, q0:q0 + 4].unsqueeze(2).to_broadcast([128, 4, 128])
                    Aib = apw[:, m, 1, q0:q0 + 4].unsqueeze(2).to_broadcast([128, 4, 128])
                    TT("dve", uu[0][:], d_uu[0], Brv, [d_Bp[0]], Arb, [d_apw], ALU.mult)
                    TT("pool", uu[1][:], d_uu[1], Biv, [d_Bp[1]], Aib, [d_apw], ALU.mult)
                    TT("dve", SN[mi][:, 0:4, :], d_SN[mi], uu[0][:], [d_uu[0]], uu[1][:], [d_uu[1]], ALU.subtract)
                    TT("pool", uu[2][:], d_uu[2], Biv, [d_Bp[1]], Arb, [d_apw], ALU.mult)
                    TT("dve", uu[3][:], d_uu[3], Brv, [d_Bp[0]], Aib, [d_apw], ALU.mult)
                    TT("pool", SN[mi][:, 4:8, :], d_SN[mi], uu[2][:], [d_uu[2]], uu[3][:], [d_uu[3]], ALU.add)
                    j = step
                    C0v = CT[0][:, q0:q0 + 4, :]
                    C1v = CT[1][:, q0:q0 + 4, :]
                    Arb = apw[:, j + 1, 0, q0:q0 + 4].unsqueeze(2).to_broadcast([128, 4, 128])
                    Aib = apw[:, j + 1, 1, q0:q0 + 4].unsqueeze(2).to_broadcast([128, 4, 128])
                    TT("pool", uu[0][:], d_uu[0], C0v, [d_CT], Arb, [d_apw], ALU.mult)
                    TT("dve", uu[1][:], d_uu[1], C1v, [d_CT], Aib, [d_apw], ALU.mult)
                    TT("pool", CA[cp][:, j, 0:4, :], d_CA[cp], uu[0][:], [d_uu[0]], uu[1][:], [d_uu[1]], ALU.add)
                    TT("dve", uu[2][:], d_uu[2], C1v, [d_CT], Arb, [d_apw], ALU.mult)
                    TT("pool", uu[3][:], d_uu[3], C0v, [d_CT], Aib, [d_apw], ALU.mult)
                    TT("dve", CA[cp][:, j, 4:8, :], d_CA[cp], uu[2][:], [d_uu[2]], uu[3][:], [d_uu[3]], ALU.subtract)

                def T_slice(ct, step):
                    cp = ct % 2
                    q0 = ct * 4
                    m = step
                    mi = m % 2
                    pb = ps[6][:].bitcast(BF16)
                    for j8 in range(8):
                        kb.op("pe", lambda e: e.transpose(out=pb[:, j8 * 128:(j8 + 1) * 128], in_=SN[mi][:, j8, :], identity=self.identb[:]),
                              reads=[d_SN[mi], self.d_const], writes=[dps[6]])
                    kb.op("act", lambda e: e.activation(out=W1[:, m, :, :].rearrange("p a b -> p (a b)"), in_=pb[:, 0:1024], func=AF.Copy),
                          reads=[dps[6]], writes=[d_W1])
                    ksl = ps[7][:, 0:128]
                    for j8 in range(8):
                        i_, k_ = j8 // 4, j8 % 4
                        kb.op("pe", lambda e: e.matmul(ksl, lhsT=SN[mi][:, j8, :], rhs=CT[i_][:, q0 + k_, :], start=(j8 == 0), stop=(j8 == 7)),
                              reads=[d_SN[mi], d_CT], writes=[dps[7]])
                    kb.op("act", lambda e: e.activation(out=KT[cp][:, m, :], in_=ksl, func=AF.Copy), reads=[dps[7]], writes=[d_KT[cp]])

                it_box = [0]

                def H_stage(ct):
                    cp = ct % 2
                    q0 = ct * 4
                    pending = []
                    for hh in range(NH):
                        for k in range(4):
                            q = q0 + k
                            sset = it_box[0] % 2
                            it_box[0] += 1
                            c = cosT[cp][:, k, :]
                            s_ = sinT[cp][:, k, :]
                            dc, ds = [d_cos[cp]], [d_sin[cp]]
                            for i in range(2):
                                bnk = 2 * sset + i
                                for j in range(Lc):
                                    kb.op("pe", lambda e: e.matmul(ps[bnk][:], lhsT=W1[:, Lc - 1 - j, i * 4 + k, :],
                                                                   rhs=uTf[cp][:, hh * 512 * Lc + j:(hh + 1) * 512 * Lc:Lc], start=(j == 0), stop=(j == Lc - 1)),
                                          reads=[d_W1, d_uTf[cp]], writes=[dps[bnk]])
                                kb.op("act", lambda e: e.activation(out=BuS[sset][i][:], in_=ps[bnk][:], func=AF.Copy), reads=[dps[bnk]], writes=[d_BuS[sset][i]])
                            Br, Bi = BuS[sset][0][:], BuS[sset][1][:]
                            dBr, dBi = [d_BuS[sset][0]], [d_BuS[sset][1]]
                            TT("dve", t1[:], dt1, Br, dBr, c, dc, ALU.mult)
                            TT("pool", t2[:], dt2, Bi, dBi, s_, ds, ALU.mult)
                            TT("dve", wr[:], dwr, t1[:], [dt1], t2[:], [dt2], ALU.add)
                            TT("dve", t3[:], dt3, Bi, dBi, c, dc, ALU.mult)
                            TT("pool", t4[:], dt4, Br, dBr, s_, ds, ALU.mult)
                            TT("dve", wi[:], dwi, t3[:], [dt3], t4[:], [dt4], ALU.subtract)
                            mb = prm[:, M8, q:q + 1].to_broadcast([128, 512])
                            par = hh % 2
                            for i, w_, dw_ in ((0, wr, dwr), (1, wi, dwi)):
                                init = zero[:, i:i + 1] if hh == 0 else czc[par][:, i, k:k + 1]
                                rd = [d_prm, dw_, d_zero] if hh == 0 else [d_prm, dw_, d_czc[par]]
                                kb.op("dve", lambda e: e.tensor_tensor_scan(out=zb[sset][i][:], data0=mb, data1=w_[:], initial=init, op0=ALU.mult, op1=ALU.add),
                                      reads=rd, writes=[d_zb[sset][i]])
                            if hh + 1 < NH:
                                nx = 1 - par
                                kb.op("dve", lambda e: e.tensor_copy(out=zl[:, 0:1], in_=zb[sset][0][:, 511:512]), reads=[d_zb[sset][0]], writes=[d_zl])
                                kb.op("dve", lambda e: e.tensor_copy(out=zl[:, 1:2], in_=zb[sset][1][:, 511:512]), reads=[d_zb[sset][1]], writes=[d_zl])
                                c5, s5 = prm[:, C512, q:q + 1], prm[:, S512, q:q + 1]
                                kb.op("dve", lambda e: e.tensor_scalar(out=czt[:, 0:1], in0=zl[:, 1:2], scalar1=s5, scalar2=None, op0=ALU.mult), reads=[d_zl, d_prm], writes=[d_czt])
                                kb.op("dve", lambda e: e.scalar_tensor_tensor(out=czc[nx][:, 0, k:k + 1], in0=zl[:, 0:1], scalar=c5, in1=czt[:, 0:1], op0=ALU.mult, op1=ALU.subtract),
                                      reads=[d_zl, d_prm, d_czt], writes=[d_czc[nx]])
                                kb.op("dve", lambda e: e.tensor_scalar(out=czt[:, 1:2], in0=zl[:, 1:2], scalar1=c5, scalar2=None, op0=ALU.mult), reads=[d_zl, d_prm], writes=[d_czt])
                                kb.op("dve", lambda e: e.scalar_tensor_tensor(out=czc[nx][:, 1, k:k + 1], in0=zl[:, 0:1], scalar=s5, in1=czt[:, 1:2], op0=ALU.mult, op1=ALU.add),
                                      reads=[d_zl, d_prm, d_czt], writes=[d_czc[nx]])

                            def back(sset=sset, c=c, s_=s_, dc=dc, ds=ds, k=k, hh=hh):
                                zbr, zbi = zb[sset][0][:], zb[sset][1][:]
                                dzbr, dzbi = [d_zb[sset][0]], [d_zb[sset][1]]
                                o0 = 1 + hh * 512
                                TT("pool", r1[:], dr1, zbr, dzbr, c, dc, ALU.mult)
                                TT("dve", r2[:], dr2, zbi, dzbi, s_, ds, ALU.mult)
                                TT("pool", Xs[cp][k][0][:, o0:o0 + 512], d_Xs[cp][k][0], r1[:], [dr1], r2[:], [dr2], ALU.subtract)
                                TT("dve", r3[:], dr3, zbr, dzbr, s_, ds, ALU.mult)
                                TT("pool", r4[:], dr4, zbi, dzbi, c, dc, ALU.mult)
                                TT("dve", Xs[cp][k][1][:, o0:o0 + 512], d_Xs[cp][k][1], r3[:], [dr3], r4[:], [dr4], ALU.add)

                            if pending:
                                pending.pop(0)()
                            pending.append(back)
                    while pending:
                        pending.pop(0)()

                iy_box = [0]

                def Y_mm(ct, blk):
                    cp = ct % 2
                    yb = 4 + (iy_box[0] % 2)
                    for j in range(Lc):
                        osl = ps[yb][:, j:512:Lc]
                        nck = 512 // Lc
                        for tau in range(j + 1):
                            kb.op("pe", lambda e: e.matmul(osl, lhsT=KT[cp][:, tau, :], rhs=uTf[cp][:, blk * 512 + j - tau:blk * 512 + 512:Lc], start=(tau == 0), stop=False),
                                  reads=[d_KT[cp], d_uTf[cp]], writes=[dps[yb]])
                        for k in range(4):
                            for i in range(2):
                                kb.op("pe", lambda e: e.matmul(osl, lhsT=CA[cp][:, j, i * 4 + k, :], rhs=Xs[cp][k][i][:, blk * nck:(blk + 1) * nck], start=False, stop=(k == 3 and i == 1)),
                                      reads=[d_CA[cp], d_Xs[cp][k][i]], writes=[dps[yb]])

                def Y_epi(ct, blk):
                    cp = ct % 2
                    yb = 4 + (iy_box[0] % 2)
                    yi = iy_box[0] % 2
                    iy_box[0] += 1
                    kb.op("dve", lambda e: e.scalar_tensor_tensor(out=tf[0][:], in0=uTf[cp][:, blk * 512:(blk + 1) * 512], scalar=Dt[:, ct:ct + 1], in1=ps[yb][:],
                                                                  op0=ALU.mult, op1=ALU.add), reads=[d_uTf[cp], d_Dt, dps[yb]], writes=[d_tf[0]])
                    kb.op("act", lambda e: e.activation(out=tf[1][:], in_=tf[0][:], func=AF.Square), reads=[d_tf[0]], writes=[d_tf[1]])
                    kb.op("pool", lambda e: e.tensor_scalar(out=tf[1][:], in0=tf[1][:], scalar1=0.044715, scalar2=1.0, op0=ALU.mult, op1=ALU.add),
                          reads=[d_tf[1]], writes=[d_tf[1]])
                    kb.op("dve", lambda e: e.tensor_tensor(out=tf[1][:], in0=tf[1][:], in1=tf[0][:], op=ALU.mult), reads=[d_tf[1], d_tf[0]], writes=[d_tf[1]])
                    kb.op("act", lambda e: e.activation(out=tf[2][:], in_=tf[1][:], func=AF.Sigmoid, scale=1.5957691216057308), reads=[d_tf[1]], writes=[d_tf[2]])
                    kb.op("dve", lambda e: e.tensor_tensor(out=y5s[yi][:], in0=tf[0][:], in1=tf[2][:], op=ALU.mult), reads=[d_tf[0], d_tf[2]], writes=[d_y5s[yi]])
                    kb.dma("sp", self.y5d[ct * 128:(ct + 1) * 128, blk * 512:(blk + 1) * 512], y5s[yi][:], reads=[d_y5s[yi]])

                for step in range(Lc):
                    E_slice(0, step)
                    T_slice(0, step)
                H_stage(0)
                for ct in range(8):
                    nxt = ct + 1 < 8
                    if nxt:
                        E_slice(ct + 1, 0)
                    for blk in range(8):
                        if nxt and blk < Lc:
                            T_slice(ct + 1, blk)
                        Y_mm(ct, blk)
                        if nxt and blk + 1 < Lc:
                            E_slice(ct + 1, blk + 1)
                        Y_epi(ct, blk)
                    if nxt:
                        H_stage(ct + 1)
                kb.barrier()
            with ExitStack() as st4:
                sb4 = lambda n, s, d: st4.enter_context(self.sbt("S4_" + n, s, d))
                gluw = sb4("gluw", [128, 8, 1024], BF16)
                y5b = [sb4("y5b%d" % i, [128, 8, 512], BF16) for i in range(2)]
                NB = 3
                szT = [sb4("szT%d" % i, [128, 512], BF16) for i in range(NB)]
                og = [sb4("og%d" % i, [128, 512], BF16) for i in range(NB)]
                g1 = [sb4("g1%d" % i, [128, 512], BF16) for i in range(NB)]
                g2 = [sb4("g2%d" % i, [128, 512], BF16) for i in range(NB)]
                d_glu = kb.dep("glu")
                d_y5b = kb.deps_n(2, "y5b")
                d_sz = kb.deps_n(NB, "sz")
                d_og = kb.deps_n(NB, "og")
                d_g1 = kb.deps_n(NB, "g1")
                d_g2 = kb.deps_n(NB, "g2")
                kb.dma("pool", gluw[:], self.glu_w[l].rearrange("(k p) n -> p k n", p=128), writes=[d_glu])
                ig = 0
                for tb in range(8):
                    yi = tb % 2
                    kb.dma("sp", y5b[yi][:], self.y5d[:, tb * 512:(tb + 1) * 512].rearrange("(c p) t -> p c t", p=128), writes=[d_y5b[yi]])
                    for co in range(8):
                        i = ig % NB
                        bank = ig % 4
                        ig += 1
                        kb.dma("sp", szT[i][:], self.sT[1024 + co * 128:1024 + (co + 1) * 128, tb * 512:(tb + 1) * 512], writes=[d_sz[i]])
                        for ci in range(8):
                            kb.op("pe", lambda e: e.matmul(ps[bank][:], lhsT=gluw[:, ci, co * 128:(co + 1) * 128], rhs=y5b[yi][:, ci, :],
                                                           start=(ci == 0), stop=(ci == 7)), reads=[d_glu, d_y5b[yi]], writes=[dps[bank]])
                        kb.op("act", lambda e: e.activation(out=g1[i][:], in_=ps[bank][:], func=AF.Sigmoid), reads=[dps[bank]], writes=[d_g1[i]])
                        kb.op("act", lambda e: e.activation(out=g2[i][:], in_=szT[i][:], func=AF.Sigmoid), reads=[d_sz[i]], writes=[d_g2[i]])
                        kb.op("dve", lambda e: e.tensor_tensor(out=g2[i][:], in0=g2[i][:], in1=szT[i][:], op=ALU.mult), reads=[d_g2[i], d_sz[i]], writes=[d_g2[i]])
                        kb.op("dve", lambda e: e.tensor_tensor(out=g1[i][:], in0=g1[i][:], in1=y5b[yi][:, co, :], op=ALU.mult),
                              reads=[d_g1[i], d_y5b[yi]], writes=[d_g1[i]])
                        kb.op("dve", lambda e: e.tensor_tensor(out=og[i][:], in0=g1[i][:], in1=g2[i][:], op=ALU.mult), reads=[d_g1[i], d_g2[i]], writes=[d_og[i]])
                        kb.dma("pool", self.mixedT[1024 + co * 128:1024 + (co + 1) * 128, tb * 512:(tb + 1) * 512], og[i][:], reads=[d_og[i]])

    def qk_prep(self, tag, src, col0, nh, normw_dram, l, dstT, d_dst, rope_tab, d_rope, ntiles=NT, bank=4):
        nc, kb = self.nc, self.kb
        ps, dps = self.ps, self.dps
        G = 4 if ntiles % 4 == 0 else 2
        NBUF = 4
        with ExitStack() as st:
            sb = lambda n, s, d: st.enter_context(self.sbt("P_%s_%s" % (tag, n), s, d))
            W = nh * 128
            GH = G * nh
            nw = sb("nw", [128, 128], F32)
            qraw = [sb("qraw%d" % i, [128, G, W], BF16) for i in range(NBUF)]
            sq = [sb("sq%d" % i, [128, GH, 128], BF16) for i in range(NBUF)]
            ss = [sb("ss%d" % i, [128, GH], F32) for i in range(NBUF)]
            qn = [sb("qn%d" % i, [128, GH, 128], F32) for i in range(NBUF)]
            rt = [sb("rt%d" % i, [128, 4, GH, 16], F32) for i in range(NBUF)]
            qb = [sb("qb%d" % i, [128, GH, 128], BF16) for i in range(NBUF)]
            d_nw = kb.dep()
            d_qraw = kb.deps_n(NBUF)
            d_sq = kb.deps_n(NBUF)
            d_ss = kb.deps_n(NBUF)
            d_qn = kb.deps_n(NBUF)
            d_rt = kb.deps_n(NBUF)
            d_qb = kb.deps_n(NBUF)
            kb.dma("sp", nw[:], normw_dram[l:l + 1, :].partition_broadcast(128), writes=[d_nw])
            for g in range(ntiles // G):
                i = g % NBUF
                t0 = g * G
                kb.dma("sp", qraw[i][:], src[t0 * 128:(t0 + G) * 128, col0:col0 + W].rearrange("(g p) c -> p g c", p=128), writes=[d_qraw[i]])
                qv = qraw[i][:].rearrange("p g (h c) -> p (g h) c", c=128)
                kb.op("pool", lambda e: e.tensor_tensor(out=sq[i][:], in0=qv, in1=qv, op=ALU.mult), reads=[d_qraw[i]], writes=[d_sq[i]])
                kb.op("dve", lambda e: e.tensor_reduce(out=ss[i][:], in_=sq[i][:], axis=AX.X, op=ALU.add), reads=[d_sq[i]], writes=[d_ss[i]])
                kb.op("dve", lambda e: e.tensor_scalar(out=ss[i][:], in0=ss[i][:], scalar1=1.0 / 128, scalar2=EPS, op0=ALU.mult, op1=ALU.add),
                      reads=[d_ss[i]], writes=[d_ss[i]])
                kb.op("act", lambda e: e.activation(out=ss[i][:], in_=ss[i][:], func=AF.Sqrt), reads=[d_ss[i]], writes=[d_ss[i]])
                kb.op("dve", lambda e: e.reciprocal(out=ss[i][:], in_=ss[i][:]), reads=[d_ss[i]], writes=[d_ss[i]])
                kb.op("dve", lambda e: e.tensor_tensor(out=qn[i][:], in0=qv, in1=ss[i][:].unsqueeze(2).to_broadcast([128, GH, 128]), op=ALU.mult),
                      reads=[d_qraw[i], d_ss[i]], writes=[d_qn[i]])
                kb.op("pool", lambda e: e.tensor_tensor(out=qn[i][:], in0=qn[i][:], in1=nw[:].unsqueeze(1).to_broadcast([128, GH, 128]), op=ALU.mult),
                      reads=[d_qn[i], d_nw], writes=[d_qn[i]])
                cb = rope_tab[:, t0:t0 + G, 0:16].unsqueeze(2).to_broadcast([128, G, nh, 16])
                sbb = rope_tab[:, t0:t0 + G, 16:32].unsqueeze(2).to_broadcast([128, G, nh, 16])
                q4 = qn[i][:].rearrange("p (g h) c -> p g h c", g=G)
                x1 = q4[:, :, :, 0:16]
                x2 = q4[:, :, :, 16:32]
                rv = lambda j: rt[i][:, j].rearrange("p (g h) c -> p g h c", g=G)
                R_ = [d_qn[i], d_rope]
                kb.op("dve", lambda e: e.tensor_tensor(out=rv(0), in0=x1, in1=cb, op=ALU.mult), reads=R_, writes=[d_rt[i]])
                kb.op("dve", lambda e: e.tensor_tensor(out=rv(1), in0=x2, in1=sbb, op=ALU.mult), reads=R_, writes=[d_rt[i]])
                kb.op("dve", lambda e: e.tensor_tensor(out=rv(2), in0=x2, in1=cb, op=ALU.mult), reads=R_, writes=[d_rt[i]])
                kb.op("dve", lambda e: e.tensor_tensor(out=rv(3), in0=x1, in1=sbb, op=ALU.mult), reads=R_, writes=[d_rt[i]])
                kb.op("dve", lambda e: e.tensor_tensor(out=qb[i][:, :, 0:16], in0=rt[i][:, 0], in1=rt[i][:, 1], op=ALU.subtract), reads=[d_rt[i]], writes=[d_qb[i]])
                kb.op("dve", lambda e: e.tensor_tensor(out=qb[i][:, :, 16:32], in0=rt[i][:, 2], in1=rt[i][:, 3], op=ALU.add), reads=[d_rt[i]], writes=[d_qb[i]])
                kb.op("act", lambda e: e.activation(out=qb[i][:, :, 32:128], in_=qn[i][:, :, 32:128], func=AF.Copy), reads=[d_qn[i]], writes=[d_qb[i]])
                nb = (GH + 7) // 8
                for bi in range(nb):
                    bk = bank + ((g * nb + bi) % 4)
                    pb = ps[bk][:].bitcast(BF16)
                    n_here = min(8, GH - bi * 8)
                    for j in range(n_here):
                        kb.op("pe", lambda e: e.transpose(out=pb[:, j * 128:(j + 1) * 128], in_=qb[i][:, bi * 8 + j, :], identity=self.identb[:]),
                              reads=[d_qb[i], self.d_const], writes=[dps[bk]])
                    ng = n_here // nh
                    gt0 = t0 + (bi * 8) // nh
                    dst = dstT[:, :, gt0 * 128:(gt0 + ng) * 128].rearrange("p h (g n) -> p g h n", g=ng)
                    srcp = pb[:, 0:n_here * 128].rearrange("p (g h n) -> p g h n", g=ng, h=nh)
                    eng = "act" if bi % 2 == 0 else "dve"
                    if eng == "act":
                        kb.op("act", lambda e: e.activation(out=dst, in_=srcp, func=AF.Copy), reads=[dps[bk]], writes=[d_dst])
                    else:
                        kb.op("dve", lambda e: e.tensor_copy(out=dst, in_=srcp), reads=[dps[bk]], writes=[d_dst])
            kb.barrier()

    def phase_MOBA(self, l):
        nc, kb = self.nc, self.kb
        ps, dps = self.ps, self.dps
        SC = 1.0 / math.sqrt(128.0)
        with ExitStack() as st:
            sb = lambda n, s, d: st.enter_context(self.sbt("M_" + n, s, d))
            QT = sb("QT", [128, 4, S], BF16)
            KT = sb("KT", [128, 4, S], BF16)
            Vp = sb("Vp", [128, NT, 4, 130], BF16)
            rope = sb("rope", [128, NT, 32], F32)
            tri = sb("tri", [128, 128], BF16)
            kmf = sb("kmf", [128, 4, 16], F32)
            kmT = sb("kmT", [128, 4, 16], BF16)
            d_QT, d_KT, d_Vp, d_SEL, d_OM, d_rope, d_tri, d_km = kb.deps_n(8, "mb")
            kb.dma("sp", rope[:], self.c_rope.rearrange("(t p) c -> p t c", p=128), writes=[d_rope])
            kb.dma("sp", tri[:], self.c_tri, writes=[d_tri])
            kb.op("pool", lambda e: e.memset(Vp[:].rearrange("p a b c -> p (a b c)"), 1.0), writes=[d_Vp])
            for h in range(4):
                kb.dma("sp", Vp[:, :, h, 0:128], self.proj_tm[:, C_MV + h * 128:C_MV + (h + 1) * 128].rearrange("(t p) c -> p t c", p=128),
                       reads=[d_Vp], writes=[d_Vp])
            self.qk_prep("mq", self.proj_tm, C_MQ, 4, self.hn["moba_q_norm"], l, QT, d_QT, rope, d_rope)
            self.qk_prep("mk", self.proj_tm, C_MK, 4, self.hn["moba_k_norm"], l, KT, d_KT, rope, d_rope)
            kb.barrier()
            SEL = sb("SEL", [128, NT, 4, 16], F32)
            OM = sb("OM", [128, NT, 512], BF16)
            kb.op("dve", lambda e: e.memset(SEL[:].rearrange("p a b c -> p (a b c)"), 1.0), writes=[d_SEL])
            for h in range(4):
                kb.op("dve", lambda e: e.tensor_reduce(out=kmf[:, h, :], in_=KT[:, h, :].rearrange("p (n k) -> p n k", k=256), axis=AX.X, op=ALU.add),
                      reads=[d_KT], writes=[d_km])
            kb.op("dve", lambda e: e.tensor_scalar(out=kmT[:], in0=kmf[:], scalar1=1.0 / 256, scalar2=None, op0=ALU.mult), reads=[d_km], writes=[d_km])
            with ExitStack() as st2:
                sb2 = lambda n, s, d: st2.enter_context(self.sbt("M2_" + n, s, d))
                gt = [sb2("gt%d" % i, [128, 4, 16], F32) for i in range(2)]
                m8 = [sb2("m8%d" % i, [128, 4, 8], F32) for i in range(2)]
                d_gt = kb.deps_n(2)
                d_m8 = kb.deps_n(2)
                for tt in range(8, NT):
                    own = tt // 2
                    i = tt % 2
                    for h in range(4):
                        kb.op("pe", lambda e: e.matmul(ps[4][:, h * 16:(h + 1) * 16], lhsT=QT[:, h, tt * 128:(tt + 1) * 128], rhs=kmT[:, h, :], start=True, stop=True),
                              reads=[d_QT, d_km], writes=[dps[4]])
                    kb.op("dve", lambda e: e.tensor_copy(out=gt[i][:].rearrange("p a b -> p (a b)"), in_=ps[4][:, 0:64]), reads=[dps[4]], writes=[d_gt[i]])
                    kb.op("dve", lambda e: e.memset(gt[i][:, :, own:16], NEG), reads=[d_gt[i]], writes=[d_gt[i]])
                    for h in range(4):
                        kb.op("dve", lambda e: e.max(out=m8[i][:, h, :], in_=gt[i][:, h, :]), reads=[d_gt[i]], writes=[d_m8[i]])
                    for h in range(4):
                        kb.op("dve", lambda e: e.tensor_scalar(out=SEL[:, tt, h, :], in0=gt[i][:, h, :], scalar1=m8[i][:, h, 2:3], scalar2=None, op0=ALU.is_ge),
                              reads=[d_gt[i], d_m8[i]], writes=[d_SEL])
            kb.barrier()
            PT = [sb("PT%d" % i, [128, 512], BF16) for i in range(4)]
            acc = [sb("acc%d" % i, [128, 2, 130], F32) for i in range(2)]
            rr = [sb("rr%d" % i, [128, 2], F32) for i in range(2)]
            d_PT = kb.deps_n(4)
            d_acc = kb.deps_n(2)
            d_rr = kb.deps_n(2)
            iters = [(h, qb, n) for h in range(4) for qb in range(16) for n in range(qb + 1)]

            def front(idx):
                h, qb, n = iters[idx]
                pi = idx % 4
                sbank = idx % 3
                for kt in range(2):
                    kb.op("pe", lambda e: e.matmul(ps[sbank][:, kt * 256:(kt + 1) * 256], lhsT=KT[:, h, (2 * n + kt) * 128:(2 * n + kt + 1) * 128],
                                                   rhs=QT[:, h, qb * 256:(qb + 1) * 256], start=True, stop=True),
                          reads=[d_KT, d_QT], writes=[dps[sbank]])
                kb.op("act", lambda e: e.activation(out=PT[pi][:], in_=ps[sbank][:], func=AF.Exp, scale=SC), reads=[dps[sbank]], writes=[d_PT[pi]])
                if n == qb:
                    kb.op("pool", lambda e: e.tensor_tensor(out=PT[pi][:, 0:128], in0=PT[pi][:, 0:128], in1=tri[:], op=ALU.mult),
                          reads=[d_PT[pi], d_tri], writes=[d_PT[pi]])
                    kb.op("pool", lambda e: e.tensor_tensor(out=PT[pi][:, 384:512], in0=PT[pi][:, 384:512], in1=tri[:], op=ALU.mult),
                          reads=[d_PT[pi], d_tri], writes=[d_PT[pi]])

            def back(idx):
                h, qb, n = iters[idx]
                pi = idx % 4
                obank = 3 + (idx % 2)
                ai = (h * 16 + qb) % 2
                if n == 0:
                    kb.op("pool", lambda e: e.memset(acc[ai][:].rearrange("p a b -> p (a b)"), 0.0), writes=[d_acc[ai]])
                if n < qb:
                    for qt in range(2):
                        for kt in range(2):
                            kb.op("pe", lambda e: e.matmul(ps[obank][:, qt * 256:qt * 256 + 129], lhsT=PT[pi][:, kt * 256 + qt * 128:kt * 256 + (qt + 1) * 128],
                                                           rhs=Vp[:, 2 * n + kt, h, 0:129], start=(kt == 0), stop=(kt == 1)),
                                  reads=[d_PT[pi], d_Vp], writes=[dps[obank]])
                    for qt in range(2):
                        kb.op("dve", lambda e: e.scalar_tensor_tensor(out=acc[ai][:, qt, 0:129], in0=ps[obank][:, qt * 256:qt * 256 + 129],
                                                                      scalar=SEL[:, 2 * qb + qt, h, n:n + 1], in1=acc[ai][:, qt, 0:129],
                                                                      op0=ALU.mult, op1=ALU.add),
                              reads=[dps[obank], d_SEL, d_acc[ai]], writes=[d_acc[ai]])
                else:
                    kb.op("pe", lambda e: e.matmul(ps[obank][:, 0:129], lhsT=PT[pi][:, 0:128], rhs=Vp[:, 2 * qb, h, 0:129], start=True, stop=True),
                          reads=[d_PT[pi], d_Vp], writes=[dps[obank]])
                    kb.op("pe", lambda e: e.matmul(ps[obank][:, 256:256 + 129], lhsT=PT[pi][:, 128:256], rhs=Vp[:, 2 * qb, h, 0:129], start=True, stop=False),
                          reads=[d_PT[pi], d_Vp], writes=[dps[obank]])
                    kb.op("pe", lambda e: e.matmul(ps[obank][:, 256:256 + 129], lhsT=PT[pi][:, 384:512], rhs=Vp[:, 2 * qb + 1, h, 0:129], start=False, stop=True),
                          reads=[d_PT[pi], d_Vp], writes=[dps[obank]])
                    for qt in range(2):
                        kb.op("dve", lambda e: e.tensor_tensor(out=acc[ai][:, qt, 0:129], in0=ps[obank][:, qt * 256:qt * 256 + 129], in1=acc[ai][:, qt, 0:129], op=ALU.add),
                              reads=[dps[obank], d_acc[ai]], writes=[d_acc[ai]])
                    kb.op("dve", lambda e: e.reciprocal(out=rr[ai][:], in_=acc[ai][:, :, 128]), reads=[d_acc[ai]], writes=[d_rr[ai]])
                    for qt in range(2):
                        kb.op("dve", lambda e: e.tensor_scalar(out=OM[:, 2 * qb + qt, h * 128:(h + 1) * 128], in0=acc[ai][:, qt, 0:128], scalar1=rr[ai][:, qt:qt + 1],
                                                               scalar2=None, op0=ALU.mult), reads=[d_acc[ai], d_rr[ai]], writes=[d_OM])

            SK = 2
            for idx in range(min(SK, len(iters))):
                front(idx)
            for idx in range(len(iters)):
                if idx + SK < len(iters):
                    front(idx + SK)
                back(idx)
            kb.barrier()
            self.gate_and_store(OM, d_OM, C_MZ, 0)

    def gate_and_store(self, OM, d_OM, zcol, row0):
        nc, kb = self.nc, self.kb
        ps, dps = self.ps, self.dps
        with ExitStack() as st:
            sb = lambda n, s, d: st.enter_context(self.sbt("G_" + n, s, d))
            zt = [sb("zt%d" % i, [128, 512], BF16) for i in range(4)]
            sl = [sb("sl%d" % i, [128, 512], F32) for i in range(4)]
            gg = [sb("gg%d" % i, [128, 512], BF16) for i in range(4)]
            oT = [sb("oT%d" % i, [128, 4, 128], BF16) for i in range(4)]
            d_zt = kb.deps_n(4)
            d_sl = kb.deps_n(4)
            d_gg = kb.deps_n(4)
            d_oT = kb.deps_n(4)
            for tt in range(NT):
                i = tt % 4
                kb.dma("sp", zt[i][:], self.proj_tm[tt * 128:(tt + 1) * 128, zcol:zcol + 512], writes=[d_zt[i]])
                kb.op("act", lambda e: e.activation(out=sl[i][:], in_=zt[i][:], func=AF.Silu), reads=[d_zt[i]], writes=[d_sl[i]])
                kb.op("dve", lambda e: e.tensor_tensor(out=gg[i][:], in0=OM[:, tt, :], in1=sl[i][:], op=ALU.mult), reads=[d_OM, d_sl[i]], writes=[d_gg[i]])
                bk = 4 + i
                pb = ps[bk][:].bitcast(BF16)
                for h in range(4):
                    kb.op("pe", lambda e: e.transpose(out=pb[:, h * 128:(h + 1) * 128], in_=gg[i][:, h * 128:(h + 1) * 128], identity=self.identb[:]),
                          reads=[d_gg[i], self.d_const], writes=[dps[bk]])
                kb.op("act", lambda e: e.activation(out=oT[i][:].rearrange("p h n -> p (h n)"), in_=pb[:, 0:512], func=AF.Copy), reads=[dps[bk]], writes=[d_oT[i]])
                kb.dma("pool", self.mixedT[row0:row0 + 512, tt * 128:(tt + 1) * 128].rearrange("(h p) n -> p h n", p=128), oT[i][:], reads=[d_oT[i]])

    def gelu_tanh(self, x, dx, tmp, dtmp, out, dout):
        kb = self.kb
        kb.op("act", lambda e: e.activation(out=tmp, in_=x, func=AF.Square), reads=[dx], writes=[dtmp])
        kb.op("dve", lambda e: e.tensor_scalar(out=tmp, in0=tmp, scalar1=0.044715, scalar2=1.0, op0=ALU.mult, op1=ALU.add), reads=[dtmp], writes=[dtmp])
        kb.op("dve", lambda e: e.tensor_tensor(out=tmp, in0=tmp, in1=x, op=ALU.mult), reads=[dtmp, dx], writes=[dtmp])
        kb.op("act", lambda e: e.activation(out=tmp, in_=tmp, func=AF.Sigmoid, scale=1.5957691216057308), reads=[dtmp], writes=[dtmp])
        kb.op("dve", lambda e: e.tensor_tensor(out=out, in0=x, in1=tmp, op=ALU.mult), reads=[dx, dtmp], writes=[dout])

    def phase_NSA(self, l):
        nc, kb = self.nc, self.kb
        ps, dps = self.ps, self.dps
        SC = 1.0 / math.sqrt(128.0)
        with ExitStack() as st:
            sb = lambda n, s, d: st.enter_context(self.sbt("N_" + n, s, d))
            NQT = sb("NQT", [128, 4, S], BF16)
            KST = sb("KST", [128, 1, S], BF16)
            KWT = sb("KWT", [128, 1, S], BF16)
            KCT = sb("KCT", [128, 1, 256], BF16)
            VS = sb("VS", [128, NT, 130], BF16)
            VW = sb("VW", [128, NT, 130], BF16)
            RC = sb("RC", [128, 2, 196], BF16)
            SELT = sb("SELT", [64, NT, 128], BF16)
            ESEL = sb("ESEL", [64, NT, 128], BF16)
            G = sb("G", [128, NT, 12], F32)
            rope = sb("rope", [128, NT, 32], F32)
            ropec = sb("ropec", [128, 2, 32], F32)
            tri = sb("tri", [128, 128], BF16)
            triu = sb("triu", [128, 128], BF16)
            dkq = sb("dkq", [128, 128], F32)
            d_NQT, d_KST, d_KWT, d_KCT, d_VS, d_VW, d_RC, d_ONS, d_SELT, d_G, d_rope, d_cst = kb.deps_n(12, "ns")
            kb.dma("sp", rope[:], self.c_rope.rearrange("(t p) c -> p t c", p=128), writes=[d_rope])
            kb.dma("sp", ropec[:], self.c_ropec.rearrange("(t p) c -> p t c", p=128), writes=[d_rope])
            kb.dma("sp", tri[:], self.c_tri, writes=[d_cst])
            kb.dma("sp", triu[:], self.c_triu, writes=[d_cst])
            kb.dma("sp", dkq[:], self.c_dkq, writes=[d_cst])
            kb.dma("sp", ESEL[:], self.c_esel, writes=[d_cst])
            kb.op("pool", lambda e: e.memset(VS[:].rearrange("p a b -> p (a b)"), 1.0), writes=[d_VS])
            kb.op("pool", lambda e: e.memset(VW[:].rearrange("p a b -> p (a b)"), 1.0), writes=[d_VW])
            kb.op("pool", lambda e: e.memset(RC[:].rearrange("p a b -> p (a b)"), 1.0), writes=[d_RC])
            kb.dma("sp", VS[:, :, 0:128], self.proj_tm[:, C_NVS:C_NVS + 128].rearrange("(t p) c -> p t c", p=128), reads=[d_VS], writes=[d_VS])
            kb.dma("sp", VW[:, :, 0:128], self.proj_tm[:, C_NVW:C_NVW + 128].rearrange("(t p) c -> p t c", p=128), reads=[d_VW], writes=[d_VW])
            kb.dma("sp", RC[:, :, 129:193], self.c_ovl.rearrange("(t p) j -> p t j", p=128), reads=[d_RC], writes=[d_RC])
            kb.dma("pool", G[:], self.proj_tm[:, C_NG:C_NG + 12].rearrange("(t p) c -> p t c", p=128), writes=[d_G])
            kb.op("act", lambda e: e.activation(out=G[:].rearrange("p a b -> p (a b)"), in_=G[:].rearrange("p a b -> p (a b)"), func=AF.Sigmoid),
                  reads=[d_G], writes=[d_G])
            self.qk_prep("nq", self.proj_tm, C_NQ, 4, self.hn["nsa_q_norm"], l, NQT, d_NQT, rope, d_rope)
            self.qk_prep("nks", self.proj_tm, C_NKS, 1, self.hn["nsa_ks_norm"], l, KST, d_KST, rope, d_rope)
            self.qk_prep("nkw", self.proj_tm, C_NKW, 1, self.hn["nsa_kw_norm"], l, KWT, d_KWT, rope, d_rope)
            with ExitStack() as st2:
                sb2 = lambda n, s, d: st2.enter_context(self.sbt("N2_" + n, s, d))
                XcT = sb2("XcT", [128, 2, S], BF16)
                w1b = [sb2("w1b%d" % i, [128, 32, 128], BF16) for i in range(2)]
                w2b = [sb2("w2b%d" % i, [128, 128], BF16) for i in range(2)]
                pef = sb2("pef", [32, 2, 128], F32)
                peT = sb2("peT", [128, 2, 32], BF16)
                cvec = sb2("cvec", [128, 2], F32)
                xr = [sb2("xr%d" % i, [128, 256], BF16) for i in range(2)]
                hs = sb2("hs", [128, 256], F32)
                htmp = sb2("htmp", [128, 256], F32)
                hT = [sb2("hT%d" % i, [128, 256], BF16) for i in range(2)]
                kcs = sb2("kcs", [128, 2, 128], BF16)
                d_XcT, d_w1, d_w2, d_pef, d_peT, d_cvec, d_hs, d_htmp, d_kcs = kb.deps_n(9, "cm")
                d_xr = kb.deps_n(2)
                d_hT = kb.deps_n(2)
                w1s = [self.ck_w1, self.cv_w1]
                w2s = [self.ck_w2, self.cv_w2]
                pes = [self.pe_k, self.pe_v]
                for j in range(2):
                    kb.dma("pool", w1b[j][:], w1s[j][l].rearrange("(l d) o -> d l o", d=128), writes=[d_w1])
                    kb.dma("pool", w2b[j][:], w2s[j][l], writes=[d_w2])
                    kb.dma("sp", pef[:, j, :], pes[j][l], writes=[d_pef])
                for j in range(2):
                    kb.op("pe", lambda e: e.transpose(out=ps[0][:, j * 32:(j + 1) * 32], in_=pef[:, j, :], identity=self.identf[0:32, 0:32]),
                          reads=[d_pef, self.d_const], writes=[dps[0]])
                kb.op("dve", lambda e: e.tensor_copy(out=peT[:].rearrange("p a b -> p (a b)"), in_=ps[0][:, 0:64]), reads=[dps[0]], writes=[d_peT])
                for tt in range(NT):
                    i = tt % 2
                    kb.dma("sp", xr[i][:], self.proj_tm[tt * 128:(tt + 1) * 128, C_NKC:C_NKC + 256], writes=[d_xr[i]])
                    bk = 5 + i
                    pb = ps[bk][:].bitcast(BF16)
                    for j in range(2):
                        kb.op("pe", lambda e: e.transpose(out=pb[:, j * 128:(j + 1) * 128], in_=xr[i][:, j * 128:(j + 1) * 128], identity=self.identb[:]),
                              reads=[d_xr[i], self.d_const], writes=[dps[bk]])
                    kb.op("dve", lambda e: e.tensor_copy(out=XcT[:, :, tt * 128:(tt + 1) * 128], in_=pb[:, 0:256].rearrange("p (j n) -> p j n", j=2)),
                          reads=[dps[bk]], writes=[d_XcT])
                for j in range(2):
                    for ll in range(32):
                        kb.op("pe", lambda e: e.matmul(ps[1][:, j:j + 1], lhsT=w1b[j][:, ll, :], rhs=peT[:, j, ll:ll + 1], start=(ll == 0), stop=(ll == 31)),
                              reads=[d_w1, d_peT], writes=[dps[1]])
                    kb.op("dve", lambda e: e.tensor_copy(out=cvec[:, j:j + 1], in_=ps[1][:, j:j + 1]), reads=[dps[1]], writes=[d_cvec])
                    for ll in range(32):
                        kb.op("pe", lambda e: e.matmul(ps[2 + j][:, 0:255], lhsT=w1b[j][:, ll, :], rhs=XcT[:, j, ll:ll + 16 * 254 + 1:16], start=(ll == 0), stop=(ll == 31)),
                              reads=[d_w1, d_XcT], writes=[dps[2 + j]])
                    kb.op("dve", lambda e: e.tensor_scalar(out=hs[:, 0:255], in0=ps[2 + j][:, 0:255], scalar1=cvec[:, j:j + 1], scalar2=None, op0=ALU.add),
                          reads=[dps[2 + j], d_cvec], writes=[d_hs])
                    kb.op("pool", lambda e: e.memset(hT[j][:], 0.0), writes=[d_hT[j]])
                    self.gelu_tanh(hs[:, 0:255], d_hs, htmp[:, 0:255], d_htmp, hT[j][:, 0:255], d_hT[j])
                    for it in range(2):
                        kb.op("pe", lambda e: e.matmul(ps[4][:, it * 128:(it + 1) * 128], lhsT=hT[j][:, it * 128:(it + 1) * 128], rhs=w2b[j][:], start=True, stop=True),
                              reads=[d_hT[j], d_w2], writes=[dps[4]])
                    if j == 0:
                        kb.op("dve", lambda e: e.tensor_copy(out=kcs[:].rearrange("p a b -> p (a b)"), in_=ps[4][:, 0:256]), reads=[dps[4]], writes=[d_kcs])
                        kb.dma("sp", self.kcmp_tm.rearrange("(t p) c -> p t c", p=128), kcs[:], reads=[d_kcs])
                    else:
                        kb.op("dve", lambda e: e.tensor_copy(out=RC[:, :, 0:128], in_=ps[4][:, 0:256].rearrange("p (a b) -> p a b", a=2)),
                              reads=[dps[4], d_RC], writes=[d_RC])
                kb.barrier()
            self.qk_prep("nkc", self.kcmp_tm, 0, 1, self.hn["nsa_kc_norm"], l, KCT, d_KCT, ropec, d_rope, ntiles=2)
            ONS = sb("ONS", [128, NT, 512], F32)
            PT = [sb("PT%d" % i, [128, 4, 128], BF16) for i in range(7)]
            M2s = [sb("M2s%d" % i, [128, 128], BF16) for i in range(4)]
            sA = [sb("sA%d" % i, [128, 64], F32) for i in range(2)]
            sB = [sb("sB%d" % i, [128, 64], F32) for i in range(2)]
            imp = [sb("imp%d" % i, [128, 64], F32) for i in range(2)]
            sc = [sb("sc%d" % i, [128, 2, 64], F32) for i in range(2)]
            m8 = [sb("m8%d" % i, [128, 2, 8], F32) for i in range(2)]
            selq = [sb("selq%d" % i, [128, 64], BF16) for i in range(2)]
            rr = [sb("rr%d" % i, [128, 8], F32) for i in range(2)]
            d_PT = kb.deps_n(7)
            d_M2s = kb.deps_n(4)
            d_m2p = kb.deps_n(4)
            d_sA = kb.deps_n(2)
            d_sB = kb.deps_n(2)
            d_imp = kb.deps_n(2)
            d_sc = kb.deps_n(2)
            d_m8 = kb.deps_n(2)
            d_selq = kb.deps_n(2)
            d_rr = kb.deps_n(2)
            ip = 0

            def obank(tt, h):
                return 5 + h // 2, (h % 2) * 256

            zl = sb("zl", [128, 128], BF16)
            zr_ = sb("zr", [128, 512], BF16)
            d_z = kb.dep("zeros")
            kb.op("pool", lambda e: e.memset(zl[:], 0.0), writes=[d_z])
            kb.op("pool", lambda e: e.memset(zr_[:], 0.0), writes=[d_z])

            def zero_obanks():
                for b in (5, 6):
                    kb.op("pe", lambda e: e.matmul(ps[b][:], lhsT=zl[:], rhs=zr_[:], start=True, stop=True), reads=[d_z], writes=[dps[b]])

            def finalize(tt, branch, first):
                i = tt % 2
                for h in range(4):
                    b, c0 = obank(tt, h)
                    kb.op("dve", lambda e: e.tensor_scalar(out=rr[i][:, h:h + 1], in0=ps[b][:, c0 + 128:c0 + 129], scalar1=1e-30, scalar2=None, op0=ALU.max),
                          reads=[dps[b]], writes=[d_rr[i]])
                kb.op("dve", lambda e: e.reciprocal(out=rr[i][:, 0:4], in_=rr[i][:, 0:4]), reads=[d_rr[i]], writes=[d_rr[i]])
                kb.op("dve", lambda e: e.tensor_tensor(out=rr[i][:, 4:8], in0=rr[i][:, 0:4], in1=G[:, tt, branch:12:3], op=ALU.mult), reads=[d_rr[i], d_G], writes=[d_rr[i]])
                for h in range(4):
                    b, c0 = obank(tt, h)
                    dst = ONS[:, tt, h * 128:(h + 1) * 128]
                    if first:
                        kb.op("dve", lambda e: e.tensor_scalar(out=dst, in0=ps[b][:, c0:c0 + 128], scalar1=rr[i][:, 4 + h:5 + h], scalar2=None, op0=ALU.mult),
                              reads=[dps[b], d_rr[i]], writes=[d_ONS])
                    else:
                        kb.op("dve", lambda e: e.scalar_tensor_tensor(out=dst, in0=ps[b][:, c0:c0 + 128], scalar=rr[i][:, 4 + h:5 + h], in1=dst, op0=ALU.mult, op1=ALU.add),
                              reads=[dps[b], d_rr[i], d_ONS], writes=[d_ONS])

            it_cmp = [(tt, it) for tt in range(NT) for it in range(1 if tt < 16 else 2)]

            def c_front(idx):
                tt, it = it_cmp[idx]
                pi = idx % 3
                sbank = idx % 2
                i = tt % 2
                if it == 0:
                    kb.dma("sp", sA[i][:], self.c_selA[tt], writes=[d_sA[i]])
                    kb.dma("sp", sB[i][:], self.c_selB[tt], writes=[d_sB[i]])
                kb.op("pe", lambda e: e.matmul(ps[sbank][:], lhsT=KCT[:, 0, it * 128:(it + 1) * 128], rhs=NQT[:, :, tt * 128:(tt + 1) * 128], start=True, stop=True),
                      reads=[d_KCT, d_NQT], writes=[dps[sbank]])
                kb.op("act", lambda e: e.activation(out=PT[pi][:].rearrange("p a b -> p (a b)"), in_=ps[sbank][:], func=AF.Exp, scale=SC),
                      reads=[dps[sbank]], writes=[d_PT[pi]])
                thr = float(31 + 2048 * it - 128 * tt)
                kb.op("dve", lambda e: e.scalar_tensor_tensor(out=PT[pi][:], in0=dkq[:].unsqueeze(1).to_broadcast([128, 4, 128]), scalar=thr, in1=PT[pi][:],
                                                              op0=ALU.is_ge, op1=ALU.mult), reads=[d_PT[pi], d_cst], writes=[d_PT[pi]])

            def c_back(idx):
                tt, it = it_cmp[idx]
                pi = idx % 3
                i = tt % 2
                n_it = 1 if tt < 16 else 2
                for h in range(4):
                    b, c0 = obank(tt, h)
                    if it == 0 and h == 0:
                        zero_obanks()
                    kb.op("pe", lambda e: e.matmul(ps[b][:, c0:c0 + 193], lhsT=PT[pi][:, h, :], rhs=RC[:, it, 0:193], start=False, stop=(it == n_it - 1)),
                          reads=[d_PT[pi], d_RC], writes=[dps[b]])
                if it != n_it - 1:
                    return
                finalize(tt, 0, True)
                for h in range(4):
                    b, c0 = obank(tt, h)
                    if h == 0:
                        kb.op("dve", lambda e: e.tensor_scalar(out=imp[i][:], in0=ps[b][:, c0 + 129:c0 + 193], scalar1=rr[i][:, h:h + 1], scalar2=None, op0=ALU.mult),
                              reads=[dps[b], d_rr[i]], writes=[d_imp[i]])
                    else:
                        kb.op("dve", lambda e: e.scalar_tensor_tensor(out=imp[i][:], in0=ps[b][:, c0 + 129:c0 + 193], scalar=rr[i][:, h:h + 1], in1=imp[i][:],
                                                                      op0=ALU.mult, op1=ALU.add), reads=[dps[b], d_rr[i], d_imp[i]], writes=[d_imp[i]])
                kb.op("dve", lambda e: e.tensor_tensor(out=sc[i][:, 0, :], in0=imp[i][:], in1=sA[i][:], op=ALU.mult), reads=[d_imp[i], d_sA[i]], writes=[d_sc[i]])
                kb.op("dve", lambda e: e.tensor_tensor(out=sc[i][:, 0, :], in0=sc[i][:, 0, :], in1=sB[i][:], op=ALU.add), reads=[d_sc[i], d_sB[i]], writes=[d_sc[i]])
                kb.op("dve", lambda e: e.max(out=m8[i][:, 0, :], in_=sc[i][:, 0, :]), reads=[d_sc[i]], writes=[d_m8[i]])
                kb.op("dve", lambda e: e.match_replace(out=sc[i][:, 1, :], in_to_replace=m8[i][:, 0, :], in_values=sc[i][:, 0, :], imm_value=NEG),
                      reads=[d_sc[i], d_m8[i]], writes=[d_sc[i]])
                kb.op("dve", lambda e: e.max(out=m8[i][:, 1, :], in_=sc[i][:, 1, :]), reads=[d_sc[i]], writes=[d_m8[i]])
                kb.op("dve", lambda e: e.scalar_tensor_tensor(out=selq[i][:], in0=sc[i][:, 0, :], scalar=m8[i][:, 1, 7:8], in1=sA[i][:], op0=ALU.is_ge, op1=ALU.mult),
                      reads=[d_sc[i], d_m8[i], d_sA[i]], writes=[d_selq[i]])
                pb = ps[2 + i][:].bitcast(BF16)
                kb.op("pe", lambda e: e.transpose(out=pb[0:64, 0:128], in_=selq[i][:], identity=self.identb[:]), reads=[d_selq[i], self.d_const], writes=[dps[2 + i]])
                kb.op("act", lambda e: e.activation(out=SELT[:, tt, :], in_=pb[0:64, 0:128], func=AF.Copy), reads=[dps[2 + i]], writes=[d_SELT])

            c_front(0)
            for idx in range(len(it_cmp)):
                if idx + 1 < len(it_cmp):
                    c_front(idx + 1)
                c_back(idx)

            its = []
            for branch in (1, 2):
                for tt in range(NT):
                    kts = list(range(0, tt + 1)) if branch == 1 else list(range(max(0, tt - 4), tt + 1))
                    for ki, kt in enumerate(kts):
                        its.append((branch, tt, ki, kt, len(kts)))

            def a_front(idx):
                branch, tt, ki, kt, nk = its[idx]
                pi = 3 + idx % 4
                sbank = idx % 3
                KT_ = KST if branch == 1 else KWT
                d_KT_ = d_KST if branch == 1 else d_KWT
                kb.op("pe", lambda e: e.matmul(ps[sbank][:], lhsT=KT_[:, 0, kt * 128:(kt + 1) * 128], rhs=NQT[:, :, tt * 128:(tt + 1) * 128], start=True, stop=True),
                      reads=[d_KT_, d_NQT], writes=[dps[sbank]])
                kb.op("act", lambda e: e.activation(out=PT[pi][:].rearrange("p a b -> p (a b)"), in_=ps[sbank][:], func=AF.Exp, scale=SC),
                      reads=[dps[sbank]], writes=[d_PT[pi]])
                if branch == 1:
                    mb = 3 + (idx % 2)
                    mi = idx % 4
                    msl = ps[mb][:, 0:128]
                    kb.op("pe", lambda e: e.matmul(msl, lhsT=ESEL[:, kt, :], rhs=SELT[:, tt, :], start=True, stop=True),
                          reads=[d_cst, d_SELT], writes=[dps[mb]])
                    if kt == tt:
                        kb.op("dve", lambda e: e.tensor_tensor(out=M2s[mi][:], in0=msl, in1=tri[:], op=ALU.mult), reads=[dps[mb], d_cst], writes=[d_M2s[mi]])
                    else:
                        kb.op("dve", lambda e: e.tensor_copy(out=M2s[mi][:], in_=msl), reads=[dps[mb]], writes=[d_M2s[mi]])
                    meng = "pool" if idx % 3 == 0 else "dve"
                    kb.op(meng, lambda e: e.tensor_tensor(out=PT[pi][:], in0=PT[pi][:], in1=M2s[mi][:].unsqueeze(1).to_broadcast([128, 4, 128]), op=ALU.mult),
                          reads=[d_PT[pi], d_M2s[mi]], writes=[d_PT[pi]])
                else:
                    if kt == tt:
                        kb.op("pool", lambda e: e.tensor_tensor(out=PT[pi][:], in0=PT[pi][:], in1=tri[:].unsqueeze(1).to_broadcast([128, 4, 128]), op=ALU.mult),
                              reads=[d_PT[pi], d_cst], writes=[d_PT[pi]])
                    elif kt == tt - 4:
                        kb.op("pool", lambda e: e.tensor_tensor(out=PT[pi][:], in0=PT[pi][:], in1=triu[:].unsqueeze(1).to_broadcast([128, 4, 128]), op=ALU.mult),
                              reads=[d_PT[pi], d_cst], writes=[d_PT[pi]])

            def a_back(idx):
                branch, tt, ki, kt, nk = its[idx]
                pi = 3 + idx % 4
                V_ = VS if branch == 1 else VW
                d_V_ = d_VS if branch == 1 else d_VW
                for h in range(4):
                    b, c0 = obank(tt, h)
                    if ki == 0 and h == 0:
                        zero_obanks()
                    kb.op("pe", lambda e: e.matmul(ps[b][:, c0:c0 + 129], lhsT=PT[pi][:, h, :], rhs=V_[:, kt, 0:129], start=False, stop=(ki == nk - 1)),
                          reads=[d_PT[pi], d_V_], writes=[dps[b]])
                if ki == nk - 1:
                    finalize(tt, branch, False)

            SK = 2
            for idx in range(min(SK, len(its))):
                a_front(idx)
            for idx in range(len(its)):
                if idx + SK < len(its):
                    a_front(idx + SK)
                a_back(idx)
            kb.barrier()
            self.gate_and_store(ONS, d_ONS, C_NZ, 512)


def host_consts():
    c = {}
    c["c_identb"] = np.eye(128, dtype=np.float32).astype(ml_dtypes.bfloat16)
    c["c_identf"] = np.eye(128, dtype=np.float32)
    inv = 500000.0 ** (-np.arange(0, 32, 2, dtype=np.float32) / 32.0)
    pos = np.arange(S, dtype=np.float32)
    ang = pos[:, None] * inv[None, :].astype(np.float32)
    c["c_rope"] = np.concatenate([np.cos(ang), np.sin(ang)], axis=1).astype(np.float32)
    posc = (np.arange(256) * 16 + 31).astype(np.float32)
    angc = posc[:, None] * inv[None, :].astype(np.float32)
    c["c_ropec"] = np.concatenate([np.cos(angc), np.sin(angc)], axis=1).astype(np.float32)
    kk = np.arange(128)
    c["c_tri"] = (kk[:, None] <= kk[None, :]).astype(np.float32).astype(ml_dtypes.bfloat16)
    c["c_triu"] = (kk[:, None] > kk[None, :]).astype(np.float32).astype(ml_dtypes.bfloat16)
    c["c_iota"] = np.broadcast_to(np.arange(512, dtype=np.float32)[None, :], (128, 512)).copy()
    c["c_dkq"] = (kk[None, :] - 16 * kk[:, None]).astype(np.float32)
    selA = np.zeros((NT, 128, 64), np.float32)
    selB = np.zeros((NT, 128, 64), np.float32)
    j = np.arange(64)[None, :]
    for tt in range(NT):
        t = tt * 128 + np.arange(128)
        cur = (t // 64)[:, None]
        valid = j <= cur
        forced = (j == 0) | (j == cur) | (j == cur - 1)
        selA[tt] = valid.astype(np.float32)
        selB[tt] = np.where(forced, 1.0e30, np.where(valid, 0.0, -1.0e30))
    c["c_selA"] = selA
    c["c_selB"] = selB
    es = np.zeros((64, NT, 128), np.float32)
    for kt in range(NT):
        for key in range(128):
            es[2 * kt + key // 64, kt, key] = 1.0
    c["c_esel"] = es.astype(ml_dtypes.bfloat16)
    ci = np.arange(256)[:, None] * 16
    sj = np.arange(64)[None, :] * 64
    ov = ((ci < sj + 64) & (ci + 32 > sj)).astype(np.float32)
    ov[255] = 0.0
    c["c_ovl"] = ov.astype(ml_dtypes.bfloat16)
    return c


_PROG = None


def kernel(**inputs):
    global _PROG
    if _PROG is None:
        _PROG = Prog().build()
    nc = _PROG
    consts = host_consts()
    x = np.ascontiguousarray(inputs["x"], dtype=np.float32)
    B = x.shape[0]
    shared = {k: np.ascontiguousarray(v) for k, v in inputs.items() if k != "x"}
    in_maps = []
    for c in range(8):
        m = dict(shared)
        m.update(consts)
        m["x"] = x[c % B]
        in_maps.append(m)
    res = run_bass_kernel_spmd(nc, in_maps, core_ids=list(range(8)))
    out = np.stack([res.results[b]["out"] for b in range(B)], axis=0)
    return out.astype(np.float32, copy=False)
```

```python
import math
from contextlib import ExitStack

import numpy as np
import ml_dtypes
import concourse.bass as bass
import concourse.mybir as mybir
from concourse.bass_utils import run_bass_kernel_spmd

F32 = mybir.dt.float32
BF16 = mybir.dt.bfloat16
I32 = mybir.dt.int32
AF = mybir.ActivationFunctionType
ALU = mybir.AluOpType
AX = mybir.AxisListType

S = 4096
D = 2048
NT = S // 128
INW = 5900
TMW = 3852
DEPTH = 2
EPS = 1e-6
C_MQ, C_MK, C_MV, C_MZ, C_NQ = 0, 512, 1024, 1536, 2048
C_NKC, C_NVC, C_NKS, C_NVS, C_NKW, C_NVW = 2560, 2688, 2816, 2944, 3072, 3200
C_NG, C_NZ = 3328, 3340
NEG = -1.0e30


class Dep:
    __slots__ = ("w", "r", "name")

    def __init__(self, name=""):
        self.w = {}
        self.r = {}
        self.name = name


class Eng:
    def __init__(self, nc, eng, name):
        self.eng = eng
        self.name = name
        self.sem = nc.alloc_semaphore("sem_" + name)
        self.count = 0
        self.seen = {}


class KB:
    def __init__(self, nc, n_dma_sems=48):
        self.nc = nc
        self.E = {
            "pe": Eng(nc, nc.tensor, "pe"),
            "act": Eng(nc, nc.scalar, "act"),
            "dve": Eng(nc, nc.vector, "dve"),
            "pool": Eng(nc, nc.gpsimd, "pool"),
            "sp": Eng(nc, nc.sync, "sp"),
        }
        self.dsems = [[nc.alloc_semaphore("dsem%d" % i), 0] for i in range(n_dma_sems)]
        self.dnext = 0
        self.deps = []
        self.n_wait = 0
        self.n_ins = 0

    def dep(self, name=""):
        d = Dep(name)
        self.deps.append(d)
        return d

    def deps_n(self, n, name=""):
        return [self.dep(name + str(i)) for i in range(n)]

    def _wait(self, E, sem, val):
        k = id(sem)
        if E.seen.get(k, 0) < val:
            E.eng.wait_ge(sem, val)
            E.seen[k] = val
            self.n_wait += 1

    def _sync(self, E, reads, writes, own_sem=None):
        for d in reads:
            for k, (s, v) in d.w.items():
                self._wait(E, s, v)
        for d in writes:
            for k, (s, v) in d.w.items():
                if s is own_sem:
                    continue
                self._wait(E, s, v)
            for k, (s, v) in d.r.items():
                if s is own_sem:
                    continue
                self._wait(E, s, v)

    def _record(self, sem, val, reads, writes):
        k = id(sem)
        for d in writes:
            d.w = {k: (sem, val)}
            d.r = {}
        for d in reads:
            d.r[k] = (sem, val)

    def op(self, e, f, reads=(), writes=()):
        E = self.E[e]
        self._sync(E, reads, writes, own_sem=E.sem)
        ins = f(E.eng)
        E.count += 1
        ins.then_inc(E.sem, 1)
        self._record(E.sem, E.count, reads, writes)
        self.n_ins += 1
        return ins

    def dma(self, q, out, in_, reads=(), writes=(), **kw):
        E = self.E[q]
        self._sync(E, reads, writes)
        ent = self.dsems[self.dnext]
        self.dnext = (self.dnext + 1) % len(self.dsems)
        if ent[1] > 0:
            self._wait(E, ent[0], ent[1])
        ent[1] += 16
        ins = E.eng.dma_start(out=out, in_=in_, **kw)
        ins.then_inc(ent[0], 16)
        self._record(ent[0], ent[1], reads, writes)
        self.n_ins += 1
        return ins

    def barrier(self):
        sp = self.E["sp"]
        for n, E in self.E.items():
            if E is not sp and E.count > 0:
                self._wait(sp, E.sem, E.count)
        for s, v in self.dsems:
            if v > 0:
                self._wait(sp, s, v)
        sp.count += 1
        sp.eng.nop().then_inc(sp.sem, 1)
        for n, E in self.E.items():
            if E is not sp:
                self._wait(E, sp.sem, sp.count)
            for n2, E2 in self.E.items():
                E.seen[id(E2.sem)] = E2.count
            for s, v in self.dsems:
                E.seen[id(s)] = v
        for d in self.deps:
            d.w = {}
            d.r = {}
        self.deps = []


class Prog:
    def __init__(self, dbg=None, layers=DEPTH, phases=("A", "S5", "MOBA", "NSA", "F")):
        self.dbg = dbg or ()
        self.layers = layers
        self.phases = phases
        nc = bass.Bass("TRN2", target_bir_lowering=False)
        self.nc = nc
        self.kb = KB(nc)
        ein = lambda n, s, d: nc.dram_tensor(n, list(s), d, kind="ExternalInput").ap()
        L = DEPTH
        self.x = ein("x", [S, D], F32)
        self.norm_w = ein("norm_w", [L, D], F32)
        self.w_in = ein("w_in", [L, D, INW], F32)
        self.w_out = ein("w_out", [L, D, D], F32)
        self.hn = {}
        for n in ("moba_q_norm", "moba_k_norm", "nsa_q_norm", "nsa_kc_norm", "nsa_ks_norm", "nsa_kw_norm"):
            self.hn[n] = ein(n, [L, 128], F32)
        self.pe_k = ein("nsa_pe_k", [L, 32, 128], F32)
        self.pe_v = ein("nsa_pe_v", [L, 32, 128], F32)
        self.ck_w1 = ein("nsa_cmp_k_w1", [L, 4096, 128], F32)
        self.ck_w2 = ein("nsa_cmp_k_w2", [L, 128, 128], F32)
        self.cv_w1 = ein("nsa_cmp_v_w1", [L, 4096, 128], F32)
        self.cv_w2 = ein("nsa_cmp_v_w2", [L, 128, 128], F32)
        self.a_re = ein("s5_a_re", [L, 64, 64], F32)
        self.a_im = ein("s5_a_im", [L, 64, 64], F32)
        self.b_re = ein("s5_b_re", [L, 64, 64, 16], F32)
        self.b_im = ein("s5_b_im", [L, 64, 64, 16], F32)
        self.c_re = ein("s5_c_re", [L, 64, 16, 64], F32)
        self.c_im = ein("s5_c_im", [L, 64, 16, 64], F32)
        self.s5_d = ein("s5_d", [L, 1024], F32)
        self.log_dt = ein("s5_log_dt", [L, 64], F32)
        self.glu_w = ein("s5_glu_w", [L, 1024, 1024], F32)
        self.c_identb = ein("c_identb", [128, 128], BF16)
        self.c_identf = ein("c_identf", [128, 128], F32)
        self.c_rope = ein("c_rope", [S, 32], F32)
        self.c_ropec = ein("c_ropec", [256, 32], F32)
        self.c_tri = ein("c_tri", [128, 128], BF16)
        self.c_iota = ein("c_iota", [128, 512], F32)
        self.c_triu = ein("c_triu", [128, 128], BF16)
        self.c_dkq = ein("c_dkq", [128, 128], F32)
        self.c_selA = ein("c_selA", [NT, 128, 64], F32)
        self.c_selB = ein("c_selB", [NT, 128, 64], F32)
        self.c_esel = ein("c_esel", [64, NT, 128], BF16)
        self.c_ovl = ein("c_ovl", [256, 64], BF16)
        self.out = nc.dram_tensor("out", [S, D], F32, kind="ExternalOutput").ap()
        sk = lambda n: "ExternalOutput" if n in self.dbg else "Internal"
        self.proj_tm = nc.dram_tensor("proj_tm", [S, TMW], BF16, kind=("ExternalInput" if "proj_in" in self.dbg else sk("proj_tm"))).ap()
        self.sT = nc.dram_tensor("sT", [2048, S], BF16, kind=("ExternalInput" if "sT_in" in self.dbg else sk("sT"))).ap()
        self.mixedT = nc.dram_tensor("mixedT", [2048, S], BF16, kind=("ExternalInput" if "mixedT_in" in self.dbg else sk("mixedT"))).ap()
        self.x1 = nc.dram_tensor("x1", [S, D], F32, kind=sk("x1")).ap()
        self.kcmp_tm = nc.dram_tensor("kcmp_tm", [256, 128], BF16, kind=sk("kcmp_tm")).ap()
        self.y5d = nc.dram_tensor("y5d", [1024, S], BF16, kind=sk("y5d")).ap()

    @staticmethod
    def emit_pipelined(n, stages):
        ns = len(stages)
        for t in range(n + ns - 1):
            for s_idx in range(ns - 1, -1, -1):
                i = t - s_idx
                if 0 <= i < n:
                    stages[s_idx](i)

    def sbt(self, name, shape, dtype):
        self._uid = getattr(self, "_uid", 0) + 1
        return self.nc.sbuf_tensor("%s_u%d" % (name, self._uid), shape, dtype)

    def build(self):
        nc, kb = self.nc, self.kb
        with ExitStack() as st:
            self.ps = [st.enter_context(nc.psum_tensor("ps%d" % i, [128, 512], F32)) for i in range(8)]
            self.dps = kb.deps_n(8, "ps")
            self.identb = st.enter_context(self.sbt("identb", [128, 128], BF16))
            self.identf = st.enter_context(self.sbt("identf", [128, 128], F32))
            self.d_const = kb.dep("const")
            kb.dma("sp", self.identb[:], self.c_identb, writes=[self.d_const])
            kb.dma("sp", self.identf[:], self.c_identf, writes=[self.d_const])
            kb.barrier()
            for l in range(self.layers):
                src = self.x if l == 0 else self.x1
                dst = self.out if l == self.layers - 1 else self.x1
                if "A" in self.phases:
                    self.phase_A(l, src)
                    kb.barrier()
                if "S5" in self.phases:
                    self.phase_S5(l)
                    kb.barrier()
                if "MOBA" in self.phases:
                    self.phase_MOBA(l)
                    kb.barrier()
                if "NSA" in self.phases:
                    self.phase_NSA(l)
                    kb.barrier()
                if "F" in self.phases:
                    self.phase_F(l, src, dst)
                    kb.barrier()
            kb.barrier()
        return nc

    def phase_A(self, l, src):
        nc, kb = self.nc, self.kb
        ps, dps = self.ps, self.dps
        with ExitStack() as st:
            sb = lambda n, s, d: st.enter_context(self.sbt("A_" + n, s, d))
            hdnT = sb("hdnT", [128, 16, 2048], BF16)
            normw = sb("normw", [128, D], F32)
            xt = [sb("xt%d" % i, [128, D], F32) for i in range(3)]
            junk = sb("junk", [128, D], BF16)
            hb = [sb("hb%d" % i, [128, D], BF16) for i in range(3)]
            wch = [sb("wch%d" % i, [128, 16, 512], BF16) for i in range(2)]
            stg = [sb("stg%d" % i, [128, 512], BF16) for i in range(4)]
            ss = [sb("ss%d" % i, [128, 1], F32) for i in range(3)]
            d_hT = kb.deps_n(16, "hT")
            d_nw = kb.dep("nw")
            d_xt = kb.deps_n(3, "xt")
            d_junk = kb.dep("junk")
            d_hb = kb.deps_n(3, "hb")
            d_w = kb.deps_n(2, "w")
            d_stg = kb.deps_n(4, "stg")
            d_ss = kb.deps_n(3, "ss")
            kb.dma("sp", normw[:], self.norm_w[l:l + 1, :].partition_broadcast(128), writes=[d_nw])
            istg = 0
            iw = 0
            ievac = 0
            for h in range(2):
                def a1(tt, h=h):
                    g = h * 16 + tt
                    i = tt % 3
                    xi = tt % 3
                    kb.dma("sp", xt[xi][:], src[g * 128:(g + 1) * 128, :], writes=[d_xt[xi]])
                    kb.op("act", lambda e: e.activation(out=junk[:], in_=xt[xi][:], func=AF.Square, accum_out=ss[i][:]),
                          reads=[d_xt[xi]], writes=[d_junk, d_ss[i]])
                    kb.op("dve", lambda e: e.tensor_scalar(out=ss[i][:], in0=ss[i][:], scalar1=1.0 / D, scalar2=EPS,
                                                           op0=ALU.mult, op1=ALU.add), reads=[d_ss[i]], writes=[d_ss[i]])
                    kb.op("act", lambda e: e.activation(out=ss[i][:], in_=ss[i][:], func=AF.Sqrt), reads=[d_ss[i]], writes=[d_ss[i]])
                    kb.op("dve", lambda e: e.reciprocal(out=ss[i][:], in_=ss[i][:]), reads=[d_ss[i]], writes=[d_ss[i]])
                def a2(tt, h=h):
                    i = tt % 3
                    xi = tt % 3
                    kb.op("dve", lambda e: e.scalar_tensor_tensor(out=hb[xi][:], in0=xt[xi][:], scalar=ss[i][:], in1=normw[:],
                                                                  op0=ALU.mult, op1=ALU.mult),
                          reads=[d_xt[xi], d_ss[i], d_nw], writes=[d_hb[xi]])
                    for half in range(2):
                        tbk = 4 + 2 * (tt % 2) + half
                        pb = ps[tbk][:].bitcast(BF16)
                        for k in range(8):
                            kc = half * 8 + k
                            kb.op("pe", lambda e: e.transpose(out=pb[:, k * 128:(k + 1) * 128], in_=hb[xi][:, kc * 128:(kc + 1) * 128],
                                                              identity=self.identb[:]),
                                  reads=[d_hb[xi], self.d_const], writes=[dps[tbk]])
                def a3(tt, h=h):
                    for half in range(2):
                        tbk = 4 + 2 * (tt % 2) + half
                        pb = ps[tbk][:].bitcast(BF16)
                        eng = "act" if half == 0 else "dve"
                        dst = hdnT[:, half * 8:(half + 1) * 8, tt * 128:(tt + 1) * 128]
                        srcp = pb[:, 0:1024].rearrange("p (k n) -> p k n", k=8)
                        if eng == "act":
                            kb.op("act", lambda e: e.activation(out=dst, in_=srcp, func=AF.Copy), reads=[dps[tbk]], writes=[d_hT[tt]])
                        else:
                            kb.op("dve", lambda e: e.tensor_copy(out=dst, in_=srcp), reads=[dps[tbk]], writes=[d_hT[tt]])
                self.emit_pipelined(16, [a1, a2, a3])
                chunks = [(c0, min(512, TMW - c0)) for c0 in range(0, TMW, 512)]
                for (c0, cw) in chunks:
                    wi = iw % 2
                    iw += 1
                    kb.dma("pool", wch[wi][:, :, 0:cw], self.w_in[l, :, c0:c0 + cw].rearrange("(k p) n -> p k n", p=128),
                           writes=[d_w[wi]])
                    for tt in range(16):
                        g = h * 16 + tt
                        pbank = ievac % 4
                        for kc in range(16):
                            kb.op("pe", lambda e: e.matmul(ps[pbank][:, 0:cw], lhsT=hdnT[:, kc, tt * 128:(tt + 1) * 128],
                                                           rhs=wch[wi][:, kc, 0:cw], start=(kc == 0), stop=(kc == 15)),
                                  reads=[d_hT[tt], d_w[wi]], writes=[dps[pbank]])
                        si = istg % 4
                        istg += 1
                        if ievac % 2 == 0:
                            kb.op("act", lambda e: e.activation(out=stg[si][:, 0:cw], in_=ps[pbank][:, 0:cw], func=AF.Copy),
                                  reads=[dps[pbank]], writes=[d_stg[si]])
                        else:
                            kb.op("dve", lambda e: e.tensor_copy(out=stg[si][:, 0:cw], in_=ps[pbank][:, 0:cw]),
                                  reads=[dps[pbank]], writes=[d_stg[si]])
                        ievac += 1
                        kb.dma("sp", self.proj_tm[g * 128:(g + 1) * 128, c0:c0 + cw], stg[si][:, 0:cw], reads=[d_stg[si]])
                for fc in range(4):
                    c0 = TMW + fc * 512
                    wi = iw % 2
                    iw += 1
                    kb.dma("pool", wch[wi][:], self.w_in[l, :, c0:c0 + 512].rearrange("(k p) n -> p k n", p=128), writes=[d_w[wi]])
                    for ctl in range(4):
                        row0 = fc * 512 + ctl * 128
                        for tb in range(4):
                            pbank = ievac % 4
                            for kc in range(16):
                                kb.op("pe", lambda e: e.matmul(ps[pbank][:], lhsT=wch[wi][:, kc, ctl * 128:(ctl + 1) * 128],
                                                               rhs=hdnT[:, kc, tb * 512:(tb + 1) * 512], start=(kc == 0), stop=(kc == 15)),
                                      reads=d_hT[tb * 4:(tb + 1) * 4] + [d_w[wi]], writes=[dps[pbank]])
                            si = istg % 4
                            istg += 1
                            if ievac % 2 == 0:
                                kb.op("act", lambda e: e.activation(out=stg[si][:], in_=ps[pbank][:], func=AF.Copy),
                                      reads=[dps[pbank]], writes=[d_stg[si]])
                            else:
                                kb.op("dve", lambda e: e.tensor_copy(out=stg[si][:], in_=ps[pbank][:]),
                                      reads=[dps[pbank]], writes=[d_stg[si]])
                            ievac += 1
                            t0 = h * 2048 + tb * 512
                            kb.dma("sp", self.sT[row0:row0 + 128, t0:t0 + 512], stg[si][:], reads=[d_stg[si]])

    def phase_F(self, l, src, dst):
        nc, kb = self.nc, self.kb
        ps, dps = self.ps, self.dps
        with ExitStack() as st:
            sb = lambda n, s, d: st.enter_context(self.sbt("F_" + n, s, d))
            wo = sb("wo", [128, 16, D], BF16)
            mT = [sb("mT%d" % i, [128, 16, 512], BF16) for i in range(2)]
            xr = [sb("xr%d" % i, [128, D], F32) for i in range(2)]
            ot = [sb("ot%d" % i, [128, D], F32) for i in range(2)]
            d_wo = kb.deps_n(4, "wo")
            d_mT = kb.deps_n(2, "mT")
            d_xr = kb.deps_n(2, "xr")
            d_ot = kb.deps_n(2, "ot")
            for c in range(4):
                kb.dma("pool", wo[:, :, c * 512:(c + 1) * 512], self.w_out[l, :, c * 512:(c + 1) * 512].rearrange("(k p) n -> p k n", p=128),
                       writes=[d_wo[c]])
            ie = 0
            import os
            for tb in range(int(os.environ.get('F_TB', 8))):
                mi = tb % 2
                kb.dma("sp", mT[mi][:], self.mixedT[:, tb * 512:(tb + 1) * 512].rearrange("(k p) n -> p k n", p=128), writes=[d_mT[mi]])
                for t4 in range(4):
                    g = tb * 4 + t4
                    i = g % 2
                    kb.dma("sp", xr[i][:], src[g * 128:(g + 1) * 128, :], writes=[d_xr[i]])
                    for c in range(4):
                        pbank = ie % 4
                        ie += 1
                        for kc in range(16):
                            kb.op("pe", lambda e: e.matmul(ps[pbank][:], lhsT=mT[mi][:, kc, t4 * 128:(t4 + 1) * 128],
                                                           rhs=wo[:, kc, c * 512:(c + 1) * 512], start=(kc == 0), stop=(kc == 15)),
                                  reads=[d_mT[mi], d_wo[c]], writes=[dps[pbank]])
                        kb.op("dve", lambda e: e.tensor_tensor(out=ot[i][:, c * 512:(c + 1) * 512], in0=ps[pbank][:],
                                                               in1=xr[i][:, c * 512:(c + 1) * 512], op=ALU.add),
                              reads=[dps[pbank], d_xr[i]], writes=[d_ot[i]])
                    kb.dma("pool", dst[g * 128:(g + 1) * 128, :], ot[i][:], reads=[d_ot[i]])

    def sincos_turns(self, turns, cos_out, sin_out, tmpf, tmpi, tmpf2, dT, dC, dS, dtmp):
        kb = self.kb
        TWO_PI = 6.283185
        kb.op("dve", lambda e: e.tensor_copy(out=tmpi, in_=turns), reads=[dT], writes=[dtmp])
        kb.op("dve", lambda e: e.tensor_tensor(out=tmpf, in0=turns, in1=tmpi, op=ALU.subtract), reads=[dT, dtmp], writes=[dtmp])
        kb.op("act", lambda e: e.activation(out=sin_out, in_=tmpf, func=AF.Sin, scale=TWO_PI), reads=[dtmp], writes=[dS])
        kb.op("dve", lambda e: e.tensor_scalar(out=tmpf2, in0=tmpf, scalar1=0.25, scalar2=None, op0=ALU.add), reads=[dtmp], writes=[dtmp])
        kb.op("dve", lambda e: e.scalar_tensor_tensor(out=tmpf2, in0=tmpf2, scalar=0.5, in1=tmpf2, op0=ALU.is_gt, op1=ALU.subtract),
              reads=[dtmp], writes=[dtmp])
        kb.op("act", lambda e: e.activation(out=cos_out, in_=tmpf2, func=AF.Sin, scale=-TWO_PI), reads=[dtmp], writes=[dC])

    def phase_S5_v1(self, l):
        nc, kb = self.nc, self.kb
        ps, dps = self.ps, self.dps
        with ExitStack() as st:
            sb = lambda n, s, d: st.enter_context(self.sbt("S_" + n, s, d))
            BT = [sb("BT%d" % i, [128, 32, 128], BF16) for i in range(2)]
            CT = [sb("CT%d" % i, [128, 32, 128], BF16) for i in range(2)]
            prm = sb("prm", [128, 24, 32], F32)
            prmi = sb("prmi", [128, 32], I32)
            Dt = sb("Dt", [128, 8], F32)
            gluw = sb("gluw", [128, 8, 1024], BF16)
            d_BT, d_CT, d_prm, d_Dt, d_glu = kb.deps_n(5, "s5c")
            AR, AI, LDT, DTT, MM, PHI, COS, SIN, FR, FI, C512, S512, T0, T1, T2, T3, T4, T5 = range(18)
            P = lambda i: prm[:, i, :]
            kb.dma("pool", gluw[:], self.glu_w[l].rearrange("(k p) n -> p k n", p=128), writes=[d_glu])
            with ExitStack() as st2:
                sb2 = lambda n, s, d: st2.enter_context(self.sbt("S2_" + n, s, d))
                XA = sb2("XA", [32, 3, 128], F32)
                ld2 = sb2("ld2", [32, 2], F32)
                XD = sb2("XD", [8, 128], F32)
                pads = [sb2("pad%d" % i, [128, 32, 128], F32) for i in range(4)]
                d_XA, d_ld2, d_XD = kb.deps_n(3, "xa")
                d_pad = kb.deps_n(4, "pad")
                kb.dma("sp", XA[:, 0, :], self.a_re[l].rearrange("(q gl) p -> q (gl p)", gl=2), writes=[d_XA])
                kb.dma("sp", XA[:, 1, :], self.a_im[l].rearrange("(q gl) p -> q (gl p)", gl=2), writes=[d_XA])
                kb.dma("sp", ld2[:], self.log_dt[l:l + 1, :].rearrange("o (q gl) -> (o q) gl", gl=2), writes=[d_ld2])
                kb.dma("sp", XD[:], self.s5_d[l:l + 1, :].rearrange("o (c p) -> (o c) p", p=128), writes=[d_XD])
                kb.op("dve", lambda e: e.tensor_copy(out=XA[:, 2, :].rearrange("q (gl p) -> q gl p", gl=2),
                                                     in_=ld2[:].unsqueeze(2).to_broadcast([32, 2, 64])),
                      reads=[d_ld2, d_XA], writes=[d_XA])
                for i in range(4):
                    eng = "dve" if i % 2 == 0 else "pool"
                    kb.op(eng, lambda e: e.memset(pads[i][:].rearrange("p q c -> p (q c)"), 0.0), writes=[d_pad[i]])
                srcB = [self.b_re[l], self.b_im[l]]
                srcC = [self.c_re[l], self.c_im[l]]
                for k in range(4):
                    for gl in range(2):
                        for i in range(2):
                            dstb = pads[i][gl * 64:(gl + 1) * 64, :, :].rearrange("p (ct k) c -> p k ct c", k=4)[:, k, :, 32 * k + 16 * gl:32 * k + 16 * gl + 16]
                            sb_ = srcB[i].rearrange("(ct k gl) p c -> k gl p ct c", k=4, gl=2)[k, gl]
                            kb.dma("sp", dstb, sb_, reads=[d_pad[i]], writes=[d_pad[i]])
                            dstc = pads[2 + i][32 * k + 16 * gl:32 * k + 16 * gl + 16, :, :].rearrange("p (ct k) c -> p k ct c", k=4)[:, k, :, gl * 64:(gl + 1) * 64]
                            sc_ = srcC[i].rearrange("(ct k gl) c p -> k gl c ct p", k=4, gl=2)[k, gl]
                            kb.dma("sp", dstc, sc_, reads=[d_pad[2 + i]], writes=[d_pad[2 + i]])
                for j in range(3):
                    kb.op("pe", lambda e: e.transpose(out=ps[0][:, j * 32:(j + 1) * 32], in_=XA[:, j, :], identity=self.identf[0:32, 0:32]),
                          reads=[d_XA, self.d_const], writes=[dps[0]])
                kb.op("pe", lambda e: e.transpose(out=ps[0][:, 96:104], in_=XD[:], identity=self.identf[0:8, 0:8]),
                      reads=[d_XD, self.d_const], writes=[dps[0]])
                kb.op("dve", lambda e: e.tensor_copy(out=prm[:, 0:3, :].rearrange("p a q -> p (a q)"), in_=ps[0][:, 0:96]), reads=[dps[0]], writes=[d_prm])
                kb.op("dve", lambda e: e.tensor_copy(out=Dt[:], in_=ps[0][:, 96:104]), reads=[dps[0]], writes=[d_Dt])
                R_, W_ = [d_prm], [d_prm]
                tt = lambda o, a, b, op: kb.op("dve", lambda e: e.tensor_tensor(out=P(o), in0=P(a), in1=P(b), op=op), reads=R_, writes=W_)
                kb.op("act", lambda e: e.activation(out=P(DTT), in_=P(LDT), func=AF.Exp), reads=R_, writes=W_)
                tt(T0, DTT, AR, ALU.mult)
                kb.op("act", lambda e: e.activation(out=P(MM), in_=P(T0), func=AF.Exp), reads=R_, writes=W_)
                tt(T0, DTT, AI, ALU.mult)
                kb.op("dve", lambda e: e.tensor_scalar(out=P(T1), in0=P(T0), scalar1=1.0 / (2.0 * math.pi), scalar2=None, op0=ALU.mult), reads=R_, writes=W_)
                kb.op("dve", lambda e: e.tensor_copy(out=prmi[:], in_=P(T1)), reads=R_, writes=W_)
                kb.op("dve", lambda e: e.tensor_tensor(out=P(PHI), in0=P(T1), in1=prmi[:], op=ALU.subtract), reads=R_, writes=W_)
                self.sincos_turns(P(PHI), P(COS), P(SIN), P(T2), prmi[:], P(T3), d_prm, d_prm, d_prm, d_prm)
                kb.op("dve", lambda e: e.tensor_scalar(out=P(T4), in0=P(PHI), scalar1=512.0, scalar2=None, op0=ALU.mult), reads=R_, writes=W_)
                self.sincos_turns(P(T4), P(C512), P(S512), P(T2), prmi[:], P(T3), d_prm, d_prm, d_prm, d_prm)
                tt(T0, MM, COS, ALU.mult)
                tt(T1, MM, SIN, ALU.mult)
                kb.op("dve", lambda e: e.tensor_scalar(out=P(T0), in0=P(T0), scalar1=-1.0, scalar2=None, op0=ALU.add), reads=R_, writes=W_)
                tt(T2, AR, AR, ALU.mult)
                tt(T3, AI, AI, ALU.mult)
                tt(T2, T2, T3, ALU.add)
                kb.op("dve", lambda e: e.reciprocal(out=P(T2), in_=P(T2)), reads=R_, writes=W_)
                tt(T3, T0, AR, ALU.mult)
                tt(T4, T1, AI, ALU.mult)
                tt(T3, T3, T4, ALU.add)
                tt(FR, T3, T2, ALU.mult)
                tt(T3, T1, AR, ALU.mult)
                tt(T4, T0, AI, ALU.mult)
                tt(T3, T3, T4, ALU.subtract)
                tt(FI, T3, T2, ALU.mult)
                ctmp = [sb2("ctmp%d" % i, [128, 4, 128], F32) for i in range(4)]
                d_ctmp = kb.dep("ctmp")
                for q4 in range(8):
                    for i in range(4):
                        bank = 4 + i
                        for k in range(4):
                            q = q4 * 4 + k
                            kb.op("pe", lambda e: e.transpose(out=ps[bank][:, k * 128:(k + 1) * 128], in_=pads[i][:, q, :], identity=self.identf[:]),
                                  reads=[d_pad[i], self.d_const], writes=[dps[bank]])
                    for i in range(2):
                        kb.op("act", lambda e: e.activation(out=BT[i][:, q4 * 4:(q4 + 1) * 4, :].rearrange("p a b -> p (a b)"), in_=ps[4 + i][:], func=AF.Copy),
                              reads=[dps[4 + i]], writes=[d_BT])
                    frb = prm[:, FR, q4 * 4:(q4 + 1) * 4].unsqueeze(2).to_broadcast([128, 4, 128])
                    fib = prm[:, FI, q4 * 4:(q4 + 1) * 4].unsqueeze(2).to_broadcast([128, 4, 128])
                    crp = ps[6][:].rearrange("p (a b) -> p a b", a=4)
                    cip = ps[7][:].rearrange("p (a b) -> p a b", a=4)
                    tmpw = [d_ctmp]
                    kb.op("dve", lambda e: e.tensor_tensor(out=ctmp[0][:], in0=crp, in1=frb, op=ALU.mult), reads=[dps[6], d_prm], writes=tmpw)
                    kb.op("dve", lambda e: e.tensor_tensor(out=ctmp[1][:], in0=cip, in1=fib, op=ALU.mult), reads=[dps[7], d_prm], writes=tmpw)
                    kb.op("dve", lambda e: e.tensor_tensor(out=CT[0][:, q4 * 4:(q4 + 1) * 4, :], in0=ctmp[0][:], in1=ctmp[1][:], op=ALU.subtract),
                          reads=tmpw, writes=[d_CT])
                    kb.op("dve", lambda e: e.tensor_tensor(out=ctmp[2][:], in0=crp, in1=fib, op=ALU.mult), reads=[dps[6], d_prm], writes=tmpw)
                    kb.op("dve", lambda e: e.tensor_tensor(out=ctmp[3][:], in0=cip, in1=frb, op=ALU.mult), reads=[dps[7], d_prm], writes=tmpw)
                    kb.op("dve", lambda e: e.scalar_tensor_tensor(out=CT[1][:, q4 * 4:(q4 + 1) * 4, :], in0=ctmp[2][:], scalar=-1.0, in1=ctmp[3][:],
                                                                  op0=ALU.mult, op1=ALU.subtract), reads=tmpw, writes=[d_CT])
                kb.barrier()
            y5T = sb("y5T", [128, 8, S], BF16)
            cosT = [sb("cosT%d" % i, [128, 4, 512], BF16) for i in range(2)]
            sinT = [sb("sinT%d" % i, [128, 4, 512], BF16) for i in range(2)]
            iota = sb("iota", [128, 512], F32)
            angi = sb("angi", [128, 512], I32)
            uT = [sb("uT%d" % i, [128, 512], BF16) for i in range(3)]
            tf = [sb("tf%d" % i, [128, 512], F32) for i in range(3)]
            tb_ = [sb("tb%d" % i, [128, 512], BF16) for i in range(6)]
            rb_ = [sb("rb%d" % i, [128, 512], BF16) for i in range(4)]
            BuS = [[sb("BuS%d%d" % (i, j), [128, 512], BF16) for j in range(2)] for i in range(2)]
            zf = [[sb("zf%d%d" % (i, j), [128, 512], F32) for j in range(2)] for i in range(2)]
            zb = [[sb("zb%d%d" % (i, j), [128, 512], BF16) for j in range(2)] for i in range(2)]
            X = [[sb("X%d%d" % (i, j), [128, 512], BF16) for j in range(2)] for i in range(2)]
            cz = [sb("cz%d" % i, [128, 2, 32], F32) for i in range(2)]
            czt = sb("czt", [128, 2], F32)
            d_y5 = kb.deps_n(8, "y5")
            d_cos = kb.deps_n(2, "cos")
            d_sin = kb.deps_n(2, "sin")
            d_iota, d_angi, d_czt = kb.deps_n(3, "tab")
            d_uT = kb.deps_n(3, "uT")
            d_tf = kb.deps_n(3, "tf")
            d_tb = kb.deps_n(6, "tb")
            d_rb = kb.deps_n(4, "rb")
            d_BuS = [kb.deps_n(2) for i in range(2)]
            d_zf = [kb.deps_n(2) for i in range(2)]
            d_zb = [kb.deps_n(2) for i in range(2)]
            d_X = [kb.deps_n(2) for i in range(2)]
            d_cz = kb.deps_n(2, "cz")
            kb.dma("sp", iota[:], self.c_iota, writes=[d_iota])
            t1, t2, t3, t4, wr, wi = tb_
            dt1, dt2, dt3, dt4, dwr, dwi = d_tb
            r1, r2, r3, r4 = rb_
            dr1, dr2, dr3, dr4 = d_rb

            def TT(eng, o, do, a, da, b_, db, op):
                kb.op(eng, lambda e: e.tensor_tensor(out=o, in0=a, in1=b_, op=op), reads=da + db, writes=[do])

            pending = []
            it = 0
            icb = 0
            for ct in range(8):
                tbi = ct % 2
                for k in range(4):
                    q = ct * 4 + k
                    kb.op("dve", lambda e: e.tensor_scalar(out=tf[0][:], in0=iota[:], scalar1=prm[:, PHI, q:q + 1], scalar2=None, op0=ALU.mult),
                          reads=[d_iota, d_prm], writes=[d_tf[0]])
                    self.sincos_turns(tf[0][:], cosT[tbi][:, k, :], sinT[tbi][:, k, :], tf[1][:], angi[:], tf[2][:], d_tf[0], d_cos[tbi], d_sin[tbi], d_tf[1])
                kb.op("dve", lambda e: e.memset(cz[0][:].rearrange("p a q -> p (a q)"), 0.0), writes=[d_cz[0]])
                for tb in range(8):
                    ui = icb % 3
                    ybank = 4 + (icb % 2)
                    icb += 1
                    par = tb % 2
                    kb.dma("sp", uT[ui][:], self.sT[ct * 128:(ct + 1) * 128, tb * 512:(tb + 1) * 512], writes=[d_uT[ui]])
                    for k in range(4):
                        q = ct * 4 + k
                        sset = it % 2
                        it += 1
                        c = cosT[tbi][:, k, :]
                        s_ = sinT[tbi][:, k, :]
                        dc, ds = [d_cos[tbi]], [d_sin[tbi]]
                        for i in range(2):
                            kb.op("pe", lambda e: e.matmul(ps[2 * sset + i][:], lhsT=BT[i][:, q, :], rhs=uT[ui][:], start=True, stop=True),
                                  reads=[d_BT, d_uT[ui]], writes=[dps[2 * sset + i]])
                            kb.op("act", lambda e: e.activation(out=BuS[sset][i][:], in_=ps[2 * sset + i][:], func=AF.Copy),
                                  reads=[dps[2 * sset + i]], writes=[d_BuS[sset][i]])
                        Br, Bi = BuS[sset][0][:], BuS[sset][1][:]
                        dBr, dBi = [d_BuS[sset][0]], [d_BuS[sset][1]]
                        TT("dve", t1[:], dt1, Br, dBr, c, dc, ALU.mult)
                        TT("dve", t2[:], dt2, Bi, dBi, s_, ds, ALU.mult)
                        TT("dve", wr[:], dwr, t1[:], [dt1], t2[:], [dt2], ALU.add)
                        TT("dve", t3[:], dt3, Bi, dBi, c, dc, ALU.mult)
                        TT("dve", t4[:], dt4, Br, dBr, s_, ds, ALU.mult)
                        TT("dve", wi[:], dwi, t3[:], [dt3], t4[:], [dt4], ALU.subtract)
                        mb = prm[:, MM, q:q + 1].to_broadcast([128, 512])
                        zr, zi = zf[sset][0], zf[sset][1]
                        dzr, dzi = d_zf[sset][0], d_zf[sset][1]
                        kb.op("dve", lambda e: e.tensor_tensor_scan(out=zr[:], data0=mb, data1=wr[:], initial=cz[par][:, 0, q:q + 1], op0=ALU.mult, op1=ALU.add),
                              reads=[d_prm, dwr, d_cz[par]], writes=[dzr])
                        kb.op("dve", lambda e: e.tensor_tensor_scan(out=zi[:], data0=mb, data1=wi[:], initial=cz[par][:, 1, q:q + 1], op0=ALU.mult, op1=ALU.add),
                              reads=[d_prm, dwi, d_cz[par]], writes=[dzi])
                        for i in range(2):
                            kb.op("act", lambda e: e.activation(out=zb[sset][i][:], in_=zf[sset][i][:], func=AF.Copy),
                                  reads=[d_zf[sset][i]], writes=[d_zb[sset][i]])
                        zr_l, zi_l = zr[:, 511:512], zi[:, 511:512]
                        c5, s5 = prm[:, C512, q:q + 1], prm[:, S512, q:q + 1]
                        nx = 1 - par
                        kb.op("dve", lambda e: e.tensor_scalar(out=czt[:, 0:1], in0=zi_l, scalar1=s5, scalar2=None, op0=ALU.mult), reads=[dzi, d_prm], writes=[d_czt])
                        kb.op("dve", lambda e: e.scalar_tensor_tensor(out=cz[nx][:, 0, q:q + 1], in0=zr_l, scalar=c5, in1=czt[:, 0:1], op0=ALU.mult, op1=ALU.subtract),
                              reads=[dzr, d_prm, d_czt], writes=[d_cz[nx]])
                        kb.op("dve", lambda e: e.tensor_scalar(out=czt[:, 1:2], in0=zi_l, scalar1=c5, scalar2=None, op0=ALU.mult), reads=[dzi, d_prm], writes=[d_czt])
                        kb.op("dve", lambda e: e.scalar_tensor_tensor(out=cz[nx][:, 1, q:q + 1], in0=zr_l, scalar=s5, in1=czt[:, 1:2], op0=ALU.mult, op1=ALU.add),
                              reads=[dzr, d_prm, d_czt], writes=[d_cz[nx]])

                        def back(sset=sset, c=c, s_=s_, dc=dc, ds=ds, q=q, k=k, ct=ct, tb=tb, ui=ui, ybank=ybank):
                            zbr, zbi = zb[sset][0][:], zb[sset][1][:]
                            dzbr, dzbi = [d_zb[sset][0]], [d_zb[sset][1]]
                            TT("pool", r1[:], dr1, zbr, dzbr, c, dc, ALU.mult)
                            TT("pool", r2[:], dr2, zbi, dzbi, s_, ds, ALU.mult)
                            TT("pool", X[sset][0][:], d_X[sset][0], r1[:], [dr1], r2[:], [dr2], ALU.subtract)
                            TT("dve", r3[:], dr3, zbr, dzbr, s_, ds, ALU.mult)
                            TT("dve", r4[:], dr4, zbi, dzbi, c, dc, ALU.mult)
                            TT("dve", X[sset][1][:], d_X[sset][1], r3[:], [dr3], r4[:], [dr4], ALU.add)
                            for i in range(2):
                                kb.op("pe", lambda e: e.matmul(ps[ybank][:], lhsT=CT[i][:, q, :], rhs=X[sset][i][:], start=(k == 0 and i == 0), stop=(k == 3 and i == 1)),
                                      reads=[d_CT, d_X[sset][i]], writes=[dps[ybank]])
                            if k == 3:
                                kb.op("dve", lambda e: e.scalar_tensor_tensor(out=tf[0][:], in0=uT[ui][:], scalar=Dt[:, ct:ct + 1], in1=ps[ybank][:], op0=ALU.mult, op1=ALU.add),
                                      reads=[d_uT[ui], d_Dt, dps[ybank]], writes=[d_tf[0]])
                                kb.op("act", lambda e: e.activation(out=tf[1][:], in_=tf[0][:], func=AF.Square), reads=[d_tf[0]], writes=[d_tf[1]])
                                kb.op("pool", lambda e: e.tensor_scalar(out=tf[1][:], in0=tf[1][:], scalar1=0.044715, scalar2=1.0, op0=ALU.mult, op1=ALU.add),
                                      reads=[d_tf[1]], writes=[d_tf[1]])
                                kb.op("pool", lambda e: e.tensor_tensor(out=tf[1][:], in0=tf[1][:], in1=tf[0][:], op=ALU.mult), reads=[d_tf[1], d_tf[0]], writes=[d_tf[1]])
                                kb.op("act", lambda e: e.activation(out=tf[2][:], in_=tf[1][:], func=AF.Sigmoid, scale=1.5957691216057308), reads=[d_tf[1]], writes=[d_tf[2]])
                                kb.op("pool", lambda e: e.tensor_tensor(out=y5T[:, ct, tb * 512:(tb + 1) * 512], in0=tf[0][:], in1=tf[2][:], op=ALU.mult),
                                      reads=[d_tf[0], d_tf[2]], writes=[d_y5[tb]])

                        if pending:
                            pending.pop(0)()
                        pending.append(back)
            while pending:
                pending.pop(0)()
            szT = [sb("szT%d" % i, [128, 512], BF16) for i in range(2)]
            og = [sb("og%d" % i, [128, 512], BF16) for i in range(2)]
            d_sz = kb.deps_n(2, "sz")
            d_og = kb.deps_n(2, "og")
            ig = 0
            for tb in range(8):
                for co in range(8):
                    i = ig % 2
                    ig += 1
                    bank = 4 + (ig % 4)
                    kb.dma("sp", szT[i][:], self.sT[1024 + co * 128:1024 + (co + 1) * 128, tb * 512:(tb + 1) * 512], writes=[d_sz[i]])
                    for ci in range(8):
                        kb.op("pe", lambda e: e.matmul(ps[bank][:], lhsT=gluw[:, ci, co * 128:(co + 1) * 128], rhs=y5T[:, ci, tb * 512:(tb + 1) * 512],
                                                       start=(ci == 0), stop=(ci == 7)), reads=[d_glu, d_y5[tb]], writes=[dps[bank]])
                    kb.op("act", lambda e: e.activation(out=tf[0][:], in_=ps[bank][:], func=AF.Sigmoid), reads=[dps[bank]], writes=[d_tf[0]])
                    kb.op("act", lambda e: e.activation(out=tf[1][:], in_=szT[i][:], func=AF.Silu), reads=[d_sz[i]], writes=[d_tf[1]])
                    kb.op("dve", lambda e: e.tensor_tensor(out=tf[0][:], in0=tf[0][:], in1=y5T[:, co, tb * 512:(tb + 1) * 512], op=ALU.mult),
                          reads=[d_tf[0], d_y5[tb]], writes=[d_tf[0]])
                    kb.op("dve", lambda e: e.tensor_tensor(out=og[i][:], in0=tf[0][:], in1=tf[1][:], op=ALU.mult), reads=[d_tf[0], d_tf[1]], writes=[d_og[i]])
                    kb.dma("sp", self.mixedT[1024 + co * 128:1024 + (co + 1) * 128, tb * 512:(tb + 1) * 512], og[i][:], reads=[d_og[i]])

    def phase_S5(self, l):
        nc, kb = self.nc, self.kb
        ps, dps = self.ps, self.dps
        Lc = 4
        NCH = S // Lc
        NH = NCH // 512
        with ExitStack() as st:
            sb = lambda n, s, d: st.enter_context(self.sbt("S_" + n, s, d))
            CT = [sb("CT%d" % i, [128, 32, 128], BF16) for i in range(2)]
            Bp = [sb("Bp%d" % i, [128, 32, 128], BF16) for i in range(2)]
            prm = sb("prm", [128, 24, 32], F32)
            apw = sb("apw", [128, 9, 2, 32], F32)
            prmi = sb("prmi", [128, 32], I32)
            Dt = sb("Dt", [128, 8], F32)
            d_CT, d_prm, d_Dt, d_apw = kb.deps_n(4, "s5c")
            d_Bp = kb.deps_n(2, "Bp")
            AR, AI, LDT, DTT, MM, PHI, COS, SIN, FR, FI, M8, PHI8, C512, S512, T0, T1, T2, T3, T4, T5 = range(20)
            P = lambda i: prm[:, i, :]
            with ExitStack() as st2:
                sb2 = lambda n, s, d: st2.enter_context(self.sbt("S2_" + n, s, d))
                XA = sb2("XA", [32, 3, 128], F32)
                ld2 = sb2("ld2", [32, 2], F32)
                XD = sb2("XD", [8, 128], F32)
                Cp = [sb2("Cp%d" % i, [128, 32, 128], F32) for i in range(2)]
                Bf = [sb2("Bf%d" % i, [128, 32, 128], F32) for i in range(2)]
                pads = [Bf[0], Bf[1], Cp[0], Cp[1]]
                d_XA, d_ld2, d_XD = kb.deps_n(3, "xa")
                d_Cp = kb.deps_n(2, "Cp")
                d_Bf = kb.deps_n(2, "Bf")
                d_pad = [d_Bf[0], d_Bf[1], d_Cp[0], d_Cp[1]]
                kb.dma("sp", XA[:, 0, :], self.a_re[l].rearrange("(q gl) p -> q (gl p)", gl=2), writes=[d_XA])
                kb.dma("sp", XA[:, 1, :], self.a_im[l].rearrange("(q gl) p -> q (gl p)", gl=2), writes=[d_XA])
                kb.dma("sp", ld2[:], self.log_dt[l:l + 1, :].rearrange("o (q gl) -> (o q) gl", gl=2), writes=[d_ld2])
                kb.dma("sp", XD[:], self.s5_d[l:l + 1, :].rearrange("o (c p) -> (o c) p", p=128), writes=[d_XD])
                kb.op("dve", lambda e: e.tensor_copy(out=XA[:, 2, :].rearrange("q (gl p) -> q gl p", gl=2),
                                                     in_=ld2[:].unsqueeze(2).to_broadcast([32, 2, 64])),
                      reads=[d_ld2, d_XA], writes=[d_XA])
                for i in range(4):
                    eng = "dve" if i % 2 == 0 else "pool"
                    kb.op(eng, lambda e: e.memset(pads[i][:].rearrange("p q c -> p (q c)"), 0.0), writes=[d_pad[i]])
                srcB = [self.b_re[l], self.b_im[l]]
                srcC = [self.c_re[l], self.c_im[l]]
                for k in range(4):
                    for gl in range(2):
                        for i in range(2):
                            dstb = pads[i][gl * 64:(gl + 1) * 64, :, :].rearrange("p (ct k) c -> p k ct c", k=4)[:, k, :, 32 * k + 16 * gl:32 * k + 16 * gl + 16]
                            sb_ = srcB[i].rearrange("(ct k gl) p c -> k gl p ct c", k=4, gl=2)[k, gl]
                            kb.dma("sp", dstb, sb_, reads=[d_pad[i]], writes=[d_pad[i]])
                            dstc = pads[2 + i][32 * k + 16 * gl:32 * k + 16 * gl + 16, :, :].rearrange("p (ct k) c -> p k ct c", k=4)[:, k, :, gl * 64:(gl + 1) * 64]
                            sc_ = srcC[i].rearrange("(ct k gl) c p -> k gl c ct p", k=4, gl=2)[k, gl]
                            kb.dma("sp", dstc, sc_, reads=[d_pad[2 + i]], writes=[d_pad[2 + i]])
                for j in range(3):
                    kb.op("pe", lambda e: e.transpose(out=ps[0][:, j * 32:(j + 1) * 32], in_=XA[:, j, :], identity=self.identf[0:32, 0:32]),
                          reads=[d_XA, self.d_const], writes=[dps[0]])
                kb.op("pe", lambda e: e.transpose(out=ps[0][:, 96:104], in_=XD[:], identity=self.identf[0:8, 0:8]),
                      reads=[d_XD, self.d_const], writes=[dps[0]])
                kb.op("dve", lambda e: e.tensor_copy(out=prm[:, 0:3, :].rearrange("p a q -> p (a q)"), in_=ps[0][:, 0:96]), reads=[dps[0]], writes=[d_prm])
                kb.op("dve", lambda e: e.tensor_copy(out=Dt[:], in_=ps[0][:, 96:104]), reads=[dps[0]], writes=[d_Dt])
                kb.op("act", lambda e: e.activation(out=Bp[0][:].rearrange("p q c -> p (q c)"), in_=Bf[0][:].rearrange("p q c -> p (q c)"), func=AF.Copy),
                      reads=[d_Bf[0]], writes=[d_Bp[0]])
                kb.op("pool", lambda e: e.tensor_copy(out=Bp[1][:].rearrange("p q c -> p (q c)"), in_=Bf[1][:].rearrange("p q c -> p (q c)")),
                      reads=[d_Bf[1]], writes=[d_Bp[1]])
                R_, W_ = [d_prm], [d_prm]
                tt = lambda o, a, b, op: kb.op("dve", lambda e: e.tensor_tensor(out=P(o), in0=P(a), in1=P(b), op=op), reads=R_, writes=W_)
                kb.op("act", lambda e: e.activation(out=P(DTT), in_=P(LDT), func=AF.Exp), reads=R_, writes=W_)
                tt(T0, DTT, AR, ALU.mult)
                kb.op("act", lambda e: e.activation(out=P(MM), in_=P(T0), func=AF.Exp), reads=R_, writes=W_)
                kb.op("act", lambda e: e.activation(out=P(M8), in_=P(T0), func=AF.Exp, scale=float(Lc)), reads=R_, writes=W_)
                tt(T0, DTT, AI, ALU.mult)
                kb.op("dve", lambda e: e.tensor_scalar(out=P(T1), in0=P(T0), scalar1=1.0 / (2.0 * math.pi), scalar2=None, op0=ALU.mult), reads=R_, writes=W_)
                kb.op("dve", lambda e: e.tensor_copy(out=prmi[:], in_=P(T1)), reads=R_, writes=W_)
                kb.op("dve", lambda e: e.tensor_tensor(out=P(PHI), in0=P(T1), in1=prmi[:], op=ALU.subtract), reads=R_, writes=W_)
                self.sincos_turns(P(PHI), P(COS), P(SIN), P(T2), prmi[:], P(T3), d_prm, d_prm, d_prm, d_prm)
                kb.op("dve", lambda e: e.tensor_scalar(out=P(T4), in0=P(PHI), scalar1=float(Lc), scalar2=None, op0=ALU.mult), reads=R_, writes=W_)
                kb.op("dve", lambda e: e.tensor_copy(out=prmi[:], in_=P(T4)), reads=R_, writes=W_)
                kb.op("dve", lambda e: e.tensor_tensor(out=P(PHI8), in0=P(T4), in1=prmi[:], op=ALU.subtract), reads=R_, writes=W_)
                kb.op("dve", lambda e: e.tensor_scalar(out=P(T4), in0=P(PHI8), scalar1=512.0, scalar2=None, op0=ALU.mult), reads=R_, writes=W_)
                self.sincos_turns(P(T4), P(C512), P(S512), P(T2), prmi[:], P(T3), d_prm, d_prm, d_prm, d_prm)
                tt(T0, MM, COS, ALU.mult)
                tt(T1, MM, SIN, ALU.mult)
                RW = [d_prm, d_apw]
                kb.op("dve", lambda e: e.memset(apw[:, 0, 0, :], 1.0), reads=RW, writes=[d_apw])
                kb.op("dve", lambda e: e.memset(apw[:, 0, 1, :], 0.0), reads=RW, writes=[d_apw])
                kb.op("dve", lambda e: e.tensor_copy(out=apw[:, 1, 0, :], in_=P(T0)), reads=RW, writes=[d_apw])
                kb.op("dve", lambda e: e.tensor_copy(out=apw[:, 1, 1, :], in_=P(T1)), reads=RW, writes=[d_apw])
                for m in range(1, Lc):
                    ar_, ai_ = apw[:, m, 0, :], apw[:, m, 1, :]
                    kb.op("dve", lambda e: e.tensor_tensor(out=P(T2), in0=ar_, in1=P(T0), op=ALU.mult), reads=RW, writes=W_)
                    kb.op("dve", lambda e: e.tensor_tensor(out=P(T3), in0=ai_, in1=P(T1), op=ALU.mult), reads=RW, writes=W_)
                    kb.op("dve", lambda e: e.tensor_tensor(out=apw[:, m + 1, 0, :], in0=P(T2), in1=P(T3), op=ALU.subtract), reads=RW, writes=[d_apw])
                    kb.op("dve", lambda e: e.tensor_tensor(out=P(T2), in0=ar_, in1=P(T1), op=ALU.mult), reads=RW, writes=W_)
                    kb.op("dve", lambda e: e.tensor_tensor(out=P(T3), in0=ai_, in1=P(T0), op=ALU.mult), reads=RW, writes=W_)
                    kb.op("dve", lambda e: e.tensor_tensor(out=apw[:, m + 1, 1, :], in0=P(T2), in1=P(T3), op=ALU.add), reads=RW, writes=[d_apw])
                kb.op("dve", lambda e: e.tensor_scalar(out=P(T0), in0=P(T0), scalar1=-1.0, scalar2=None, op0=ALU.add), reads=R_, writes=W_)
                tt(T2, AR, AR, ALU.mult)
                tt(T3, AI, AI, ALU.mult)
                tt(T2, T2, T3, ALU.add)
                kb.op("dve", lambda e: e.reciprocal(out=P(T2), in_=P(T2)), reads=R_, writes=W_)
                tt(T3, T0, AR, ALU.mult)
                tt(T4, T1, AI, ALU.mult)
                tt(T3, T3, T4, ALU.add)
                tt(FR, T3, T2, ALU.mult)
                tt(T3, T1, AR, ALU.mult)
                tt(T4, T0, AI, ALU.mult)
                tt(T3, T3, T4, ALU.subtract)
                tt(FI, T3, T2, ALU.mult)
                ctmp = [sb2("ctmp%d" % i, [128, 4, 128], F32) for i in range(4)]
                d_ctmp = kb.dep("ctmp")
                for q4 in range(8):
                    for i in range(2):
                        bank = 6 + i
                        for k in range(4):
                            q = q4 * 4 + k
                            kb.op("pe", lambda e: e.transpose(out=ps[bank][:, k * 128:(k + 1) * 128], in_=Cp[i][:, q, :], identity=self.identf[:]),
                                  reads=[d_Cp[i], self.d_const], writes=[dps[bank]])
                    frb = prm[:, FR, q4 * 4:(q4 + 1) * 4].unsqueeze(2).to_broadcast([128, 4, 128])
                    fib = prm[:, FI, q4 * 4:(q4 + 1) * 4].unsqueeze(2).to_broadcast([128, 4, 128])
                    crp = ps[6][:].rearrange("p (a b) -> p a b", a=4)
                    cip = ps[7][:].rearrange("p (a b) -> p a b", a=4)
                    tmpw = [d_ctmp]
                    kb.op("dve", lambda e: e.tensor_tensor(out=ctmp[0][:], in0=crp, in1=frb, op=ALU.mult), reads=[dps[6], d_prm], writes=tmpw)
                    kb.op("dve", lambda e: e.tensor_tensor(out=ctmp[1][:], in0=cip, in1=fib, op=ALU.mult), reads=[dps[7], d_prm], writes=tmpw)
                    kb.op("dve", lambda e: e.tensor_tensor(out=CT[0][:, q4 * 4:(q4 + 1) * 4, :], in0=ctmp[0][:], in1=ctmp[1][:], op=ALU.subtract),
                          reads=tmpw, writes=[d_CT])
                    kb.op("dve", lambda e: e.tensor_tensor(out=ctmp[2][:], in0=crp, in1=fib, op=ALU.mult), reads=[dps[6], d_prm], writes=tmpw)
                    kb.op("dve", lambda e: e.tensor_tensor(out=ctmp[3][:], in0=cip, in1=frb, op=ALU.mult), reads=[dps[7], d_prm], writes=tmpw)
                    kb.op("dve", lambda e: e.scalar_tensor_tensor(out=CT[1][:, q4 * 4:(q4 + 1) * 4, :], in0=ctmp[2][:], scalar=-1.0, in1=ctmp[3][:],
                                                                  op0=ALU.mult, op1=ALU.subtract), reads=tmpw, writes=[d_CT])
                kb.barrier()
            with ExitStack() as st3:
                sb3 = lambda n, s, d: st3.enter_context(self.sbt("S3_" + n, s, d))
                cosT = [sb3("cosT%d" % i, [128, 4, 512], BF16) for i in range(2)]
                sinT = [sb3("sinT%d" % i, [128, 4, 512], BF16) for i in range(2)]
                iota = sb3("iota", [128, 512], F32)
                angi = sb3("angi", [128, 512], I32)
                zero = sb3("zero", [128, 2], F32)
                czc = [sb3("czc%d" % i, [128, 2, 4], F32) for i in range(2)]
                czt = sb3("czt", [128, 2], F32)
                zl = sb3("zlast", [128, 2], F32)
                d_czc = kb.deps_n(2, "czc")
                d_czt = kb.dep("czt")
                d_zl = kb.dep("zl")
                uTf = [sb3("uTf%d" % i, [128, S], BF16) for i in range(2)]
                W1 = sb3("W1", [128, Lc, 8, 128], BF16)
                CA = [sb3("CA%d" % i, [128, Lc, 8, 128], BF16) for i in range(2)]
                SN = [sb3("SN%d" % i, [128, 8, 128], BF16) for i in range(2)]
                KT = [sb3("KT%d" % i, [128, Lc, 128], BF16) for i in range(2)]
                uu = [sb3("uu%d" % i, [128, 4, 128], F32) for i in range(4)]
                Xs = [[[sb3("Xs%d%d%d" % (c_, k, i), [128, NCH + 2], BF16) for i in range(2)] for k in range(4)] for c_ in range(2)]
                tf = [sb3("tf%d" % i, [128, 512], F32) for i in range(3)]
                tg = [sb3("tg%d" % i, [128, 512], F32) for i in range(3)]
                tb_ = [sb3("tb%d" % i, [128, 512], BF16) for i in range(6)]
                rb_ = [sb3("rb%d" % i, [128, 512], BF16) for i in range(4)]
                BuS = [[sb3("BuS%d%d" % (i, j), [128, 512], BF16) for j in range(2)] for i in range(2)]
                zb = [[sb3("zb%d%d" % (i, j), [128, 512], BF16) for j in range(2)] for i in range(2)]
                y5s = [sb3("y5s%d" % i, [128, 512], BF16) for i in range(2)]
                d_cos = kb.deps_n(2, "cos")
                d_sin = kb.deps_n(2, "sin")
                d_iota, d_angi, d_zero, d_W1 = kb.deps_n(4, "tab")
                d_CA = kb.deps_n(2, "CA")
                d_KT = kb.deps_n(2, "KT")
                d_uTf = kb.deps_n(2, "uTf")
                d_SN = kb.deps_n(2, "SN")
                d_uu = kb.deps_n(4, "uu")
                d_Xs = [[kb.deps_n(2) for k in range(4)] for c_ in range(2)]
                d_tf = kb.deps_n(3, "tf")
                d_tg = kb.deps_n(3, "tg")
                d_tb = kb.deps_n(6, "tb")
                d_rb = kb.deps_n(4, "rb")
                d_BuS = [kb.deps_n(2) for i in range(2)]
                d_zb = [kb.deps_n(2) for i in range(2)]
                d_y5s = kb.deps_n(2, "y5s")
                kb.dma("sp", iota[:], self.c_iota, writes=[d_iota])
                kb.op("dve", lambda e: e.memset(zero[:], 0.0), writes=[d_zero])
                for c_ in range(2):
                    for k in range(4):
                        for i in range(2):
                            kb.op("pool", lambda e: e.memset(Xs[c_][k][i][:], 0.0), writes=[d_Xs[c_][k][i]])
                t1, t2, t3, t4, wr, wi = tb_
                dt1, dt2, dt3, dt4, dwr, dwi = d_tb
                r1, r2, r3, r4 = rb_
                dr1, dr2, dr3, dr4 = d_rb

                def TT(eng, o, do, a, da, b_, db, op):
                    kb.op(eng, lambda e: e.tensor_tensor(out=o, in0=a, in1=b_, op=op), reads=da + db, writes=[do])

                def E_slice(ct, step):
                    cp = ct % 2
                    q0 = ct * 4
                    if step == 0:
                        kb.dma("sp", uTf[cp][:], self.sT[ct * 128:(ct + 1) * 128, :], writes=[d_uTf[cp]])
                    if step < 4:
                        k = step
                        q = q0 + k
                        kb.op("dve", lambda e: e.tensor_scalar(out=tg[0][:], in0=iota[:], scalar1=prm[:, PHI8, q:q + 1], scalar2=None, op0=ALU.mult),
                              reads=[d_iota, d_prm], writes=[d_tg[0]])
                        self.sincos_turns(tg[0][:], cosT[cp][:, k, :], sinT[cp][:, k, :], tg[1][:], angi[:], tg[2][:], d_tg[0], d_cos[cp], d_sin[cp], d_tg[1])
                    m = step
                    mi = m % 2
                    Brv = Bp[0][:, q0:q0 + 4, :]
                    Biv = Bp[1][:, q0:q0 + 4, :]
                    Arb = apw[:, m, 0, q0:q0 + 4].unsqueeze(2).to_broadcast([128, 4, 128])
                    Aib = apw[:, m, 1, q0:q0 + 4].unsqueeze(2).to_broadcast([128, 4, 128])
                    TT("dve", uu[0][:], d_uu[0], Brv, [d_Bp[0]], Arb, [d_apw], ALU.mult)
                    TT("pool", uu[1][:], d_uu[1], Biv, [d_Bp[1]], Aib, [d_apw], ALU.mult)
                    TT("dve", SN[mi][:, 0:4, :], d_SN[mi], uu[0][:], [d_uu[0]], uu[1][:], [d_uu[1]], ALU.subtract)
                    TT("pool", uu[2][:], d_uu[2], Biv, [d_Bp[1]], Arb, [d_apw], ALU.mult)
                    TT("dve", uu[3][:], d_uu[3], Brv, [d_Bp[0]], Aib, [d_apw], ALU.mult)
                    TT("pool", SN[mi][:, 4:8, :], d_SN[mi], uu[2][:], [d_uu[2]], uu[3][:], [d_uu[3]], ALU.add)
                    j = step
                    C0v = CT[0][:, q0:q0 + 4, :]
                    C1v = CT[1][:, q0:q0 + 4, :]
                    Arb = apw[:, j + 1, 0, q0:q0 + 4].unsqueeze(2).to_broadcast([128, 4, 128])
                    Aib = apw[:, j + 1, 1, q0:q0 + 4].unsqueeze(2).to_broadcast([128, 4, 128])
                    TT("pool", uu[0][:], d_uu[0], C0v, [d_CT], Arb, [d_apw], ALU.mult)
                    TT("dve", uu[1][:], d_uu[1], C1v, [d_CT], Aib, [d_apw], ALU.mult)
                    TT("pool", CA[cp][:, j, 0:4, :], d_CA[cp], uu[0][:], [d_uu[0]], uu[1][:], [d_uu[1]], ALU.add)
                    TT("dve", uu[2][:], d_uu[2], C1v, [d_CT], Arb, [d_apw], ALU.mult)
                    TT("pool", uu[3][:], d_uu[3], C0v, [d_CT], Aib, [d_apw], ALU.mult)
                    TT("dve", CA[cp][:, j, 4:8, :], d_CA[cp], uu[2][:], [d_uu[2]], uu[3][:], [d_uu[3]], ALU.subtract)

                def T_slice(ct, step):
                    cp = ct % 2
                    q0 = ct * 4
                    m = step
                    mi = m % 2
                    pb = ps[6][:].bitcast(BF16)
                    for j8 in range(8):
                        kb.op("pe", lambda e: e.transpose(out=pb[:, j8 * 128:(j8 + 1) * 128], in_=SN[mi][:, j8, :], identity=self.identb[:]),
                              reads=[d_SN[mi], self.d_const], writes=[dps[6]])
                    kb.op("act", lambda e: e.activation(out=W1[:, m, :, :].rearrange("p a b -> p (a b)"), in_=pb[:, 0:1024], func=AF.Copy),
                          reads=[dps[6]], writes=[d_W1])
                    ksl = ps[7][:, 0:128]
                    for j8 in range(8):
                        i_, k_ = j8 // 4, j8 % 4
                        kb.op("pe", lambda e: e.matmul(ksl, lhsT=SN[mi][:, j8, :], rhs=CT[i_][:, q0 + k_, :], start=(j8 == 0), stop=(j8 == 7)),
                              reads=[d_SN[mi], d_CT], writes=[dps[7]])
                    kb.op("act", lambda e: e.activation(out=KT[cp][:, m, :], in_=ksl, func=AF.Copy), reads=[dps[7]], writes=[d_KT[cp]])

                it_box = [0]

                def H_stage(ct):
                    cp = ct % 2
                    q0 = ct * 4
                    pending = []
                    for hh in range(NH):
                        for k in range(4):
                            q = q0 + k
                            sset = it_box[0] % 2
                            it_box[0] += 1
                            c = cosT[cp][:, k, :]
                            s_ = sinT[cp][:, k, :]
                            dc, ds = [d_cos[cp]], [d_sin[cp]]
                            for i in range(2):
                                bnk = 2 * sset + i
                                for j in range(Lc):
                                    kb.op("pe", lambda e: e.matmul(ps[bnk][:], lhsT=W1[:, Lc - 1 - j, i * 4 + k, :],
                                                                   rhs=uTf[cp][:, hh * 512 * Lc + j:(hh + 1) * 512 * Lc:Lc], start=(j == 0), stop=(j == Lc - 1)),
                                          reads=[d_W1, d_uTf[cp]], writes=[dps[bnk]])
                                kb.op("act", lambda e: e.activation(out=BuS[sset][i][:], in_=ps[bnk][:], func=AF.Copy), reads=[dps[bnk]], writes=[d_BuS[sset][i]])
                            Br, Bi = BuS[sset][0][:], BuS[sset][1][:]
                            dBr, dBi = [d_BuS[sset][0]], [d_BuS[sset][1]]
                            TT("dve", t1[:], dt1, Br, dBr, c, dc, ALU.mult)
                            TT("pool", t2[:], dt2, Bi, dBi, s_, ds, ALU.mult)
                            TT("dve", wr[:], dwr, t1[:], [dt1], t2[:], [dt2], ALU.add)
                            TT("dve", t3[:], dt3, Bi, dBi, c, dc, ALU.mult)
                            TT("pool", t4[:], dt4, Br, dBr, s_, ds, ALU.mult)
                            TT("dve", wi[:], dwi, t3[:], [dt3], t4[:], [dt4], ALU.subtract)
                            mb = prm[:, M8, q:q + 1].to_broadcast([128, 512])
                            par = hh % 2
                            for i, w_, dw_ in ((0, wr, dwr), (1, wi, dwi)):
                                init = zero[:, i:i + 1] if hh == 0 else czc[par][:, i, k:k + 1]
                                rd = [d_prm, dw_, d_zero] if hh == 0 else [d_prm, dw_, d_czc[par]]
                                kb.op("dve", lambda e: e.tensor_tensor_scan(out=zb[sset][i][:], data0=mb, data1=w_[:], initial=init, op0=ALU.mult, op1=ALU.add),
                                      reads=rd, writes=[d_zb[sset][i]])
                            if hh + 1 < NH:
                                nx = 1 - par
                                kb.op("dve", lambda e: e.tensor_copy(out=zl[:, 0:1], in_=zb[sset][0][:, 511:512]), reads=[d_zb[sset][0]], writes=[d_zl])
                                kb.op("dve", lambda e: e.tensor_copy(out=zl[:, 1:2], in_=zb[sset][1][:, 511:512]), reads=[d_zb[sset][1]], writes=[d_zl])
                                c5, s5 = prm[:, C512, q:q + 1], prm[:, S512, q:q + 1]
                                kb.op("dve", lambda e: e.tensor_scalar(out=czt[:, 0:1], in0=zl[:, 1:2], scalar1=s5, scalar2=None, op0=ALU.mult), reads=[d_zl, d_prm], writes=[d_czt])
                                kb.op("dve", lambda e: e.scalar_tensor_tensor(out=czc[nx][:, 0, k:k + 1], in0=zl[:, 0:1], scalar=c5, in1=czt[:, 0:1], op0=ALU.mult, op1=ALU.subtract),
                                      reads=[d_zl, d_prm, d_czt], writes=[d_czc[nx]])
                                kb.op("dve", lambda e: e.tensor_scalar(out=czt[:, 1:2], in0=zl[:, 1:2], scalar1=c5, scalar2=None, op0=ALU.mult), reads=[d_zl, d_prm], writes=[d_czt])
                                kb.op("dve", lambda e: e.scalar_tensor_tensor(out=czc[nx][:, 1, k:k + 1], in0=zl[:, 0:1], scalar=s5, in1=czt[:, 1:2], op0=ALU.mult, op1=ALU.add),
                                      reads=[d_zl, d_prm, d_czt], writes=[d_czc[nx]])

                            def back(sset=sset, c=c, s_=s_, dc=dc, ds=ds, k=k, hh=hh):
                                zbr, zbi = zb[sset][0][:], zb[sset][1][:]
                                dzbr, dzbi = [d_zb[sset][0]], [d_zb[sset][1]]
                                o0 = 1 + hh * 512
                                TT("pool", r1[:], dr1, zbr, dzbr, c, dc, ALU.mult)
                                TT("dve", r2[:], dr2, zbi, dzbi, s_, ds, ALU.mult)
                                TT("pool", Xs[cp][k][0][:, o0:o0 + 512], d_Xs[cp][k][0], r1[:], [dr1], r2[:], [dr2], ALU.subtract)
                                TT("dve", r3[:], dr3, zbr, dzbr, s_, ds, ALU.mult)
                                TT("pool", r4[:], dr4, zbi, dzbi, c, dc, ALU.mult)
                                TT("dve", Xs[cp][k][1][:, o0:o0 + 512], d_Xs[cp][k][1], r3[:], [dr3], r4[:], [dr4], ALU.add)

                            if pending:
                                pending.pop(0)()
                            pending.append(back)
                    while pending:
                        pending.pop(0)()

                iy_box = [0]

                def Y_mm(ct, blk):
                    cp = ct % 2
                    yb = 4 + (iy_box[0] % 2)
                    for j in range(Lc):
                        osl = ps[yb][:, j:512:Lc]
                        nck = 512 // Lc
                        for tau in range(j + 1):
                            kb.op("pe", lambda e: e.matmul(osl, lhsT=KT[cp][:, tau, :], rhs=uTf[cp][:, blk * 512 + j - tau:blk * 512 + 512:Lc], start=(tau == 0), stop=False),
                                  reads=[d_KT[cp], d_uTf[cp]], writes=[dps[yb]])
                        for k in range(4):
                            for i in range(2):
                                kb.op("pe", lambda e: e.matmul(osl, lhsT=CA[cp][:, j, i * 4 + k, :], rhs=Xs[cp][k][i][:, blk * nck:(blk + 1) * nck], start=False, stop=(k == 3 and i == 1)),
                                      reads=[d_CA[cp], d_Xs[cp][k][i]], writes=[dps[yb]])

                def Y_epi(ct, blk):
                    cp = ct % 2
                    yb = 4 + (iy_box[0] % 2)
                    yi = iy_box[0] % 2
                    iy_box[0] += 1
                    kb.op("dve", lambda e: e.scalar_tensor_tensor(out=tf[0][:], in0=uTf[cp][:, blk * 512:(blk + 1) * 512], scalar=Dt[:, ct:ct + 1], in1=ps[yb][:],
                                                                  op0=ALU.mult, op1=ALU.add), reads=[d_uTf[cp], d_Dt, dps[yb]], writes=[d_tf[0]])
                    kb.op("act", lambda e: e.activation(out=tf[1][:], in_=tf[0][:], func=AF.Square), reads=[d_tf[0]], writes=[d_tf[1]])
                    kb.op("pool", lambda e: e.tensor_scalar(out=tf[1][:], in0=tf[1][:], scalar1=0.044715, scalar2=1.0, op0=ALU.mult, op1=ALU.add),
                          reads=[d_tf[1]], writes=[d_tf[1]])
                    kb.op("dve", lambda e: e.tensor_tensor(out=tf[1][:], in0=tf[1][:], in1=tf[0][:], op=ALU.mult), reads=[d_tf[1], d_tf[0]], writes=[d_tf[1]])
                    kb.op("act", lambda e: e.activation(out=tf[2][:], in_=tf[1][:], func=AF.Sigmoid, scale=1.5957691216057308), reads=[d_tf[1]], writes=[d_tf[2]])
                    kb.op("dve", lambda e: e.tensor_tensor(out=y5s[yi][:], in0=tf[0][:], in1=tf[2][:], op=ALU.mult), reads=[d_tf[0], d_tf[2]], writes=[d_y5s[yi]])
                    kb.dma("sp", self.y5d[ct * 128:(ct + 1) * 128, blk * 512:(blk + 1) * 512], y5s[yi][:], reads=[d_y5s[yi]])

                for step in range(Lc):
                    E_slice(0, step)
                    T_slice(0, step)
                H_stage(0)
                for ct in range(8):
                    nxt = ct + 1 < 8
                    if nxt:
                        E_slice(ct + 1, 0)
                    for blk in range(8):
                        if nxt and blk < Lc:
                            T_slice(ct + 1, blk)
                        Y_mm(ct, blk)
                        if nxt and blk + 1 < Lc:
                            E_slice(ct + 1, blk + 1)
                        Y_epi(ct, blk)
                    if nxt:
                        H_stage(ct + 1)
                kb.barrier()
            with ExitStack() as st4:
                sb4 = lambda n, s, d: st4.enter_context(self.sbt("S4_" + n, s, d))
                gluw = sb4("gluw", [128, 8, 1024], BF16)
                y5b = [sb4("y5b%d" % i, [128, 8, 512], BF16) for i in range(2)]
                NB = 3
                szT = [sb4("szT%d" % i, [128, 512], BF16) for i in range(NB)]
                og = [sb4("og%d" % i, [128, 512], BF16) for i in range(NB)]
                g1 = [sb4("g1%d" % i, [128, 512], BF16) for i in range(NB)]
                g2 = [sb4("g2%d" % i, [128, 512], BF16) for i in range(NB)]
                d_glu = kb.dep("glu")
                d_y5b = kb.deps_n(2, "y5b")
                d_sz = kb.deps_n(NB, "sz")
                d_og = kb.deps_n(NB, "og")
                d_g1 = kb.deps_n(NB, "g1")
                d_g2 = kb.deps_n(NB, "g2")
                kb.dma("pool", gluw[:], self.glu_w[l].rearrange("(k p) n -> p k n", p=128), writes=[d_glu])
                ig = 0
                for tb in range(8):
                    yi = tb % 2
                    kb.dma("sp", y5b[yi][:], self.y5d[:, tb * 512:(tb + 1) * 512].rearrange("(c p) t -> p c t", p=128), writes=[d_y5b[yi]])
                    for co in range(8):
                        i = ig % NB
                        bank = ig % 4
                        ig += 1
                        kb.dma("sp", szT[i][:], self.sT[1024 + co * 128:1024 + (co + 1) * 128, tb * 512:(tb + 1) * 512], writes=[d_sz[i]])
                        for ci in range(8):
                            kb.op("pe", lambda e: e.matmul(ps[bank][:], lhsT=gluw[:, ci, co * 128:(co + 1) * 128], rhs=y5b[yi][:, ci, :],
                                                           start=(ci == 0), stop=(ci == 7)), reads=[d_glu, d_y5b[yi]], writes=[dps[bank]])
                        kb.op("act", lambda e: e.activation(out=g1[i][:], in_=ps[bank][:], func=AF.Sigmoid), reads=[dps[bank]], writes=[d_g1[i]])
                        kb.op("act", lambda e: e.activation(out=g2[i][:], in_=szT[i][:], func=AF.Sigmoid), reads=[d_sz[i]], writes=[d_g2[i]])
                        kb.op("dve", lambda e: e.tensor_tensor(out=g2[i][:], in0=g2[i][:], in1=szT[i][:], op=ALU.mult), reads=[d_g2[i], d_sz[i]], writes=[d_g2[i]])
                        kb.op("dve", lambda e: e.tensor_tensor(out=g1[i][:], in0=g1[i][:], in1=y5b[yi][:, co, :], op=ALU.mult),
                              reads=[d_g1[i], d_y5b[yi]], writes=[d_g1[i]])
                        kb.op("dve", lambda e: e.tensor_tensor(out=og[i][:], in0=g1[i][:], in1=g2[i][:], op=ALU.mult), reads=[d_g1[i], d_g2[i]], writes=[d_og[i]])
                        kb.dma("pool", self.mixedT[1024 + co * 128:1024 + (co + 1) * 128, tb * 512:(tb + 1) * 512], og[i][:], reads=[d_og[i]])

    def qk_prep(self, tag, src, col0, nh, normw_dram, l, dstT, d_dst, rope_tab, d_rope, ntiles=NT, bank=4):
        nc, kb = self.nc, self.kb
        ps, dps = self.ps, self.dps
        G = 4 if ntiles % 4 == 0 else 2
        NBUF = 4
        with ExitStack() as st:
            sb = lambda n, s, d: st.enter_context(self.sbt("P_%s_%s" % (tag, n), s, d))
            W = nh * 128
            GH = G * nh
            nw = sb("nw", [128, 128], F32)
            qraw = [sb("qraw%d" % i, [128, G, W], BF16) for i in range(NBUF)]
            sq = [sb("sq%d" % i, [128, GH, 128], BF16) for i in range(NBUF)]
            ss = [sb("ss%d" % i, [128, GH], F32) for i in range(NBUF)]
            qn = [sb("qn%d" % i, [128, GH, 128], F32) for i in range(NBUF)]
            rt = [sb("rt%d" % i, [128, 4, GH, 16], F32) for i in range(NBUF)]
            qb = [sb("qb%d" % i, [128, GH, 128], BF16) for i in range(NBUF)]
            d_nw = kb.dep()
            d_qraw = kb.deps_n(NBUF)
            d_sq = kb.deps_n(NBUF)
            d_ss = kb.deps_n(NBUF)
            d_qn = kb.deps_n(NBUF)
            d_rt = kb.deps_n(NBUF)
            d_qb = kb.deps_n(NBUF)
            kb.dma("sp", nw[:], normw_dram[l:l + 1, :].partition_broadcast(128), writes=[d_nw])
            def f1(g):
                i = g % NBUF
                t0 = g * G
                kb.dma("sp", qraw[i][:], src[t0 * 128:(t0 + G) * 128, col0:col0 + W].rearrange("(g p) c -> p g c", p=128), writes=[d_qraw[i]])
                qv = qraw[i][:].rearrange("p g (h c) -> p (g h) c", c=128)
                kb.op("pool", lambda e: e.tensor_tensor(out=sq[i][:], in0=qv, in1=qv, op=ALU.mult), reads=[d_qraw[i]], writes=[d_sq[i]])
                kb.op("dve", lambda e: e.tensor_reduce(out=ss[i][:], in_=sq[i][:], axis=AX.X, op=ALU.add), reads=[d_sq[i]], writes=[d_ss[i]])
                kb.op("dve", lambda e: e.tensor_scalar(out=ss[i][:], in0=ss[i][:], scalar1=1.0 / 128, scalar2=EPS, op0=ALU.mult, op1=ALU.add),
                      reads=[d_ss[i]], writes=[d_ss[i]])
                kb.op("act", lambda e: e.activation(out=ss[i][:], in_=ss[i][:], func=AF.Sqrt), reads=[d_ss[i]], writes=[d_ss[i]])
                kb.op("dve", lambda e: e.reciprocal(out=ss[i][:], in_=ss[i][:]), reads=[d_ss[i]], writes=[d_ss[i]])
            def f2(g):
                i = g % NBUF
                t0 = g * G
                qv = qraw[i][:].rearrange("p g (h c) -> p (g h) c", c=128)
                kb.op("dve", lambda e: e.tensor_tensor(out=qn[i][:], in0=qv, in1=ss[i][:].unsqueeze(2).to_broadcast([128, GH, 128]), op=ALU.mult),
                      reads=[d_qraw[i], d_ss[i]], writes=[d_qn[i]])
                kb.op("pool", lambda e: e.tensor_tensor(out=qn[i][:], in0=qn[i][:], in1=nw[:].unsqueeze(1).to_broadcast([128, GH, 128]), op=ALU.mult),
                      reads=[d_qn[i], d_nw], writes=[d_qn[i]])
                cb = rope_tab[:, t0:t0 + G, 0:16].unsqueeze(2).to_broadcast([128, G, nh, 16])
                sbb = rope_tab[:, t0:t0 + G, 16:32].unsqueeze(2).to_broadcast([128, G, nh, 16])
                q4 = qn[i][:].rearrange("p (g h) c -> p g h c", g=G)
                x1 = q4[:, :, :, 0:16]
                x2 = q4[:, :, :, 16:32]
                rv = lambda j: rt[i][:, j].rearrange("p (g h) c -> p g h c", g=G)
                R_ = [d_qn[i], d_rope]
                kb.op("dve", lambda e: e.tensor_tensor(out=rv(0), in0=x1, in1=cb, op=ALU.mult), reads=R_, writes=[d_rt[i]])
                kb.op("dve", lambda e: e.tensor_tensor(out=rv(1), in0=x2, in1=sbb, op=ALU.mult), reads=R_, writes=[d_rt[i]])
                kb.op("dve", lambda e: e.tensor_tensor(out=rv(2), in0=x2, in1=cb, op=ALU.mult), reads=R_, writes=[d_rt[i]])
                kb.op("dve", lambda e: e.tensor_tensor(out=rv(3), in0=x1, in1=sbb, op=ALU.mult), reads=R_, writes=[d_rt[i]])
                kb.op("dve", lambda e: e.tensor_tensor(out=qb[i][:, :, 0:16], in0=rt[i][:, 0], in1=rt[i][:, 1], op=ALU.subtract), reads=[d_rt[i]], writes=[d_qb[i]])
                kb.op("dve", lambda e: e.tensor_tensor(out=qb[i][:, :, 16:32], in0=rt[i][:, 2], in1=rt[i][:, 3], op=ALU.add), reads=[d_rt[i]], writes=[d_qb[i]])
                kb.op("act", lambda e: e.activation(out=qb[i][:, :, 32:128], in_=qn[i][:, :, 32:128], func=AF.Copy), reads=[d_qn[i]], writes=[d_qb[i]])
            def f3(g):
                i = g % NBUF
                t0 = g * G
                nb = (GH + 7) // 8
                for bi in range(nb):
                    bk = bank + ((g * nb + bi) % 4)
                    pb = ps[bk][:].bitcast(BF16)
                    n_here = min(8, GH - bi * 8)
                    for j in range(n_here):
                        kb.op("pe", lambda e: e.transpose(out=pb[:, j * 128:(j + 1) * 128], in_=qb[i][:, bi * 8 + j, :], identity=self.identb[:]),
                              reads=[d_qb[i], self.d_const], writes=[dps[bk]])
                    ng = n_here // nh
                    gt0 = t0 + (bi * 8) // nh
                    dst = dstT[:, :, gt0 * 128:(gt0 + ng) * 128].rearrange("p h (g n) -> p g h n", g=ng)
                    srcp = pb[:, 0:n_here * 128].rearrange("p (g h n) -> p g h n", g=ng, h=nh)
                    eng = "act" if bi % 2 == 0 else "dve"
                    if eng == "act":
                        kb.op("act", lambda e: e.activation(out=dst, in_=srcp, func=AF.Copy), reads=[dps[bk]], writes=[d_dst])
                    else:
                        kb.op("dve", lambda e: e.tensor_copy(out=dst, in_=srcp), reads=[dps[bk]], writes=[d_dst])
            self.emit_pipelined(ntiles // G, [f1, f2, f3])
            kb.barrier()

    def phase_MOBA(self, l):
        nc, kb = self.nc, self.kb
        ps, dps = self.ps, self.dps
        SC = 1.0 / math.sqrt(128.0)
        with ExitStack() as st:
            sb = lambda n, s, d: st.enter_context(self.sbt("M_" + n, s, d))
            QT = sb("QT", [128, 4, S], BF16)
            KT = sb("KT", [128, 4, S], BF16)
            Vp = sb("Vp", [128, NT, 4, 130], BF16)
            rope = sb("rope", [128, NT, 32], F32)
            tri = sb("tri", [128, 128], BF16)
            kmf = sb("kmf", [128, 4, 16], F32)
            kmT = sb("kmT", [128, 4, 16], BF16)
            d_QT, d_KT, d_Vp, d_SEL, d_OM, d_rope, d_tri, d_km = kb.deps_n(8, "mb")
            kb.dma("sp", rope[:], self.c_rope.rearrange("(t p) c -> p t c", p=128), writes=[d_rope])
            kb.dma("sp", tri[:], self.c_tri, writes=[d_tri])
            kb.op("pool", lambda e: e.memset(Vp[:].rearrange("p a b c -> p (a b c)"), 1.0), writes=[d_Vp])
            for h in range(4):
                kb.dma("sp", Vp[:, :, h, 0:128], self.proj_tm[:, C_MV + h * 128:C_MV + (h + 1) * 128].rearrange("(t p) c -> p t c", p=128),
                       reads=[d_Vp], writes=[d_Vp])
            self.qk_prep("mq", self.proj_tm, C_MQ, 4, self.hn["moba_q_norm"], l, QT, d_QT, rope, d_rope)
            self.qk_prep("mk", self.proj_tm, C_MK, 4, self.hn["moba_k_norm"], l, KT, d_KT, rope, d_rope)
            kb.barrier()
            SEL = sb("SEL", [128, NT, 4, 16], F32)
            OM = sb("OM", [128, NT, 512], BF16)
            kb.op("dve", lambda e: e.memset(SEL[:].rearrange("p a b c -> p (a b c)"), 1.0), writes=[d_SEL])
            for h in range(4):
                kb.op("dve", lambda e: e.tensor_reduce(out=kmf[:, h, :], in_=KT[:, h, :].rearrange("p (n k) -> p n k", k=256), axis=AX.X, op=ALU.add),
                      reads=[d_KT], writes=[d_km])
            kb.op("dve", lambda e: e.tensor_scalar(out=kmT[:], in0=kmf[:], scalar1=1.0 / 256, scalar2=None, op0=ALU.mult), reads=[d_km], writes=[d_km])
            with ExitStack() as st2:
                sb2 = lambda n, s, d: st2.enter_context(self.sbt("M2_" + n, s, d))
                gt = [sb2("gt%d" % i, [128, 4, 16], F32) for i in range(2)]
                m8 = [sb2("m8%d" % i, [128, 4, 8], F32) for i in range(2)]
                d_gt = kb.deps_n(2)
                d_m8 = kb.deps_n(2)
                for tt in range(8, NT):
                    own = tt // 2
                    i = tt % 2
                    for h in range(4):
                        kb.op("pe", lambda e: e.matmul(ps[4][:, h * 16:(h + 1) * 16], lhsT=QT[:, h, tt * 128:(tt + 1) * 128], rhs=kmT[:, h, :], start=True, stop=True),
                              reads=[d_QT, d_km], writes=[dps[4]])
                    kb.op("dve", lambda e: e.tensor_copy(out=gt[i][:].rearrange("p a b -> p (a b)"), in_=ps[4][:, 0:64]), reads=[dps[4]], writes=[d_gt[i]])
                    kb.op("dve", lambda e: e.memset(gt[i][:, :, own:16], NEG), reads=[d_gt[i]], writes=[d_gt[i]])
                    for h in range(4):
                        kb.op("dve", lambda e: e.max(out=m8[i][:, h, :], in_=gt[i][:, h, :]), reads=[d_gt[i]], writes=[d_m8[i]])
                    for h in range(4):
                        kb.op("dve", lambda e: e.tensor_scalar(out=SEL[:, tt, h, :], in0=gt[i][:, h, :], scalar1=m8[i][:, h, 2:3], scalar2=None, op0=ALU.is_ge),
                              reads=[d_gt[i], d_m8[i]], writes=[d_SEL])
            kb.barrier()
            PT = [sb("PT%d" % i, [128, 512], BF16) for i in range(4)]
            acc = [sb("acc%d" % i, [128, 2, 130], F32) for i in range(2)]
            rr = [sb("rr%d" % i, [128, 2], F32) for i in range(2)]
            d_PT = kb.deps_n(4)
            d_acc = kb.deps_n(2)
            d_rr = kb.deps_n(2)
            iters = [(h, qb, n) for h in range(4) for qb in range(16) for n in range(qb + 1)]

            def front(idx):
                h, qb, n = iters[idx]
                pi = idx % 4
                sbank = idx % 3
                for kt in range(2):
                    kb.op("pe", lambda e: e.matmul(ps[sbank][:, kt * 256:(kt + 1) * 256], lhsT=KT[:, h, (2 * n + kt) * 128:(2 * n + kt + 1) * 128],
                                                   rhs=QT[:, h, qb * 256:(qb + 1) * 256], start=True, stop=True),
                          reads=[d_KT, d_QT], writes=[dps[sbank]])
                kb.op("act", lambda e: e.activation(out=PT[pi][:], in_=ps[sbank][:], func=AF.Exp, scale=SC), reads=[dps[sbank]], writes=[d_PT[pi]])
                if n == qb:
                    kb.op("pool", lambda e: e.tensor_tensor(out=PT[pi][:, 0:128], in0=PT[pi][:, 0:128], in1=tri[:], op=ALU.mult),
                          reads=[d_PT[pi], d_tri], writes=[d_PT[pi]])
                    kb.op("pool", lambda e: e.tensor_tensor(out=PT[pi][:, 384:512], in0=PT[pi][:, 384:512], in1=tri[:], op=ALU.mult),
                          reads=[d_PT[pi], d_tri], writes=[d_PT[pi]])

            def back(idx):
                h, qb, n = iters[idx]
                pi = idx % 4
                obank = 3 + (idx % 2)
                ai = (h * 16 + qb) % 2
                if n == 0:
                    kb.op("pool", lambda e: e.memset(acc[ai][:].rearrange("p a b -> p (a b)"), 0.0), writes=[d_acc[ai]])
                if n < qb:
                    for qt in range(2):
                        for kt in range(2):
                            kb.op("pe", lambda e: e.matmul(ps[obank][:, qt * 256:qt * 256 + 129], lhsT=PT[pi][:, kt * 256 + qt * 128:kt * 256 + (qt + 1) * 128],
                                                           rhs=Vp[:, 2 * n + kt, h, 0:129], start=(kt == 0), stop=(kt == 1)),
                                  reads=[d_PT[pi], d_Vp], writes=[dps[obank]])
                    for qt in range(2):
                        kb.op("dve", lambda e: e.scalar_tensor_tensor(out=acc[ai][:, qt, 0:129], in0=ps[obank][:, qt * 256:qt * 256 + 129],
                                                                      scalar=SEL[:, 2 * qb + qt, h, n:n + 1], in1=acc[ai][:, qt, 0:129],
                                                                      op0=ALU.mult, op1=ALU.add),
                              reads=[dps[obank], d_SEL, d_acc[ai]], writes=[d_acc[ai]])
                else:
                    kb.op("pe", lambda e: e.matmul(ps[obank][:, 0:129], lhsT=PT[pi][:, 0:128], rhs=Vp[:, 2 * qb, h, 0:129], start=True, stop=True),
                          reads=[d_PT[pi], d_Vp], writes=[dps[obank]])
                    kb.op("pe", lambda e: e.matmul(ps[obank][:, 256:256 + 129], lhsT=PT[pi][:, 128:256], rhs=Vp[:, 2 * qb, h, 0:129], start=True, stop=False),
                          reads=[d_PT[pi], d_Vp], writes=[dps[obank]])
                    kb.op("pe", lambda e: e.matmul(ps[obank][:, 256:256 + 129], lhsT=PT[pi][:, 384:512], rhs=Vp[:, 2 * qb + 1, h, 0:129], start=False, stop=True),
                          reads=[d_PT[pi], d_Vp], writes=[dps[obank]])
                    for qt in range(2):
                        kb.op("dve", lambda e: e.tensor_tensor(out=acc[ai][:, qt, 0:129], in0=ps[obank][:, qt * 256:qt * 256 + 129], in1=acc[ai][:, qt, 0:129], op=ALU.add),
                              reads=[dps[obank], d_acc[ai]], writes=[d_acc[ai]])
                    kb.op("dve", lambda e: e.reciprocal(out=rr[ai][:], in_=acc[ai][:, :, 128]), reads=[d_acc[ai]], writes=[d_rr[ai]])
                    for qt in range(2):
                        kb.op("dve", lambda e: e.tensor_scalar(out=OM[:, 2 * qb + qt, h * 128:(h + 1) * 128], in0=acc[ai][:, qt, 0:128], scalar1=rr[ai][:, qt:qt + 1],
                                                               scalar2=None, op0=ALU.mult), reads=[d_acc[ai], d_rr[ai]], writes=[d_OM])

            SK = 2
            for idx in range(min(SK, len(iters))):
                front(idx)
            for idx in range(len(iters)):
                if idx + SK < len(iters):
                    front(idx + SK)
                back(idx)
            kb.barrier()
            self.gate_and_store(OM, d_OM, C_MZ, 0)

    def gate_and_store(self, OM, d_OM, zcol, row0):
        nc, kb = self.nc, self.kb
        ps, dps = self.ps, self.dps
        with ExitStack() as st:
            sb = lambda n, s, d: st.enter_context(self.sbt("G_" + n, s, d))
            zt = [sb("zt%d" % i, [128, 512], BF16) for i in range(4)]
            sl = [sb("sl%d" % i, [128, 512], F32) for i in range(4)]
            gg = [sb("gg%d" % i, [128, 512], BF16) for i in range(4)]
            oT = [sb("oT%d" % i, [128, 4, 128], BF16) for i in range(4)]
            d_zt = kb.deps_n(4)
            d_sl = kb.deps_n(4)
            d_gg = kb.deps_n(4)
            d_oT = kb.deps_n(4)
            def g1(tt):
                i = tt % 4
                kb.dma("sp", zt[i][:], self.proj_tm[tt * 128:(tt + 1) * 128, zcol:zcol + 512], writes=[d_zt[i]])
                kb.op("act", lambda e: e.activation(out=sl[i][:], in_=zt[i][:], func=AF.Silu), reads=[d_zt[i]], writes=[d_sl[i]])
                kb.op("dve", lambda e: e.tensor_tensor(out=gg[i][:], in0=OM[:, tt, :], in1=sl[i][:], op=ALU.mult), reads=[d_OM, d_sl[i]], writes=[d_gg[i]])
            def g2(tt):
                i = tt % 4
                bk = 4 + i
                pb = ps[bk][:].bitcast(BF16)
                for h in range(4):
                    kb.op("pe", lambda e: e.transpose(out=pb[:, h * 128:(h + 1) * 128], in_=gg[i][:, h * 128:(h + 1) * 128], identity=self.identb[:]),
                          reads=[d_gg[i], self.d_const], writes=[dps[bk]])
                kb.op("act", lambda e: e.activation(out=oT[i][:].rearrange("p h n -> p (h n)"), in_=pb[:, 0:512], func=AF.Copy), reads=[dps[bk]], writes=[d_oT[i]])
                kb.dma("pool", self.mixedT[row0:row0 + 512, tt * 128:(tt + 1) * 128].rearrange("(h p) n -> p h n", p=128), oT[i][:], reads=[d_oT[i]])
            self.emit_pipelined(NT, [g1, g2])

    def gelu_tanh(self, x, dx, tmp, dtmp, out, dout):
        kb = self.kb
        kb.op("act", lambda e: e.activation(out=tmp, in_=x, func=AF.Square), reads=[dx], writes=[dtmp])
        kb.op("dve", lambda e: e.tensor_scalar(out=tmp, in0=tmp, scalar1=0.044715, scalar2=1.0, op0=ALU.mult, op1=ALU.add), reads=[dtmp], writes=[dtmp])
        kb.op("dve", lambda e: e.tensor_tensor(out=tmp, in0=tmp, in1=x, op=ALU.mult), reads=[dtmp, dx], writes=[dtmp])
        kb.op("act", lambda e: e.activation(out=tmp, in_=tmp, func=AF.Sigmoid, scale=1.5957691216057308), reads=[dtmp], writes=[dtmp])
        kb.op("dve", lambda e: e.tensor_tensor(out=out, in0=x, in1=tmp, op=ALU.mult), reads=[dx, dtmp], writes=[dout])

    def phase_NSA(self, l):
        nc, kb = self.nc, self.kb
        ps, dps = self.ps, self.dps
        SC = 1.0 / math.sqrt(128.0)
        with ExitStack() as st:
            sb = lambda n, s, d: st.enter_context(self.sbt("N_" + n, s, d))
            NQT = sb("NQT", [128, 4, S], BF16)
            KST = sb("KST", [128, 1, S], BF16)
            KWT = sb("KWT", [128, 1, S], BF16)
            KCT = sb("KCT", [128, 1, 256], BF16)
            VS = sb("VS", [128, NT, 130], BF16)
            VW = sb("VW", [128, NT, 130], BF16)
            RC = sb("RC", [128, 2, 196], BF16)
            SELT = sb("SELT", [64, NT, 128], BF16)
            ESEL = sb("ESEL", [64, NT, 128], BF16)
            G = sb("G", [128, NT, 12], F32)
            rope = sb("rope", [128, NT, 32], F32)
            ropec = sb("ropec", [128, 2, 32], F32)
            tri = sb("tri", [128, 128], BF16)
            triu = sb("triu", [128, 128], BF16)
            dkq = sb("dkq", [128, 128], F32)
            d_NQT, d_KST, d_KWT, d_KCT, d_VS, d_VW, d_RC, d_ONS, d_SELT, d_G, d_rope, d_cst = kb.deps_n(12, "ns")
            kb.dma("sp", rope[:], self.c_rope.rearrange("(t p) c -> p t c", p=128), writes=[d_rope])
            kb.dma("sp", ropec[:], self.c_ropec.rearrange("(t p) c -> p t c", p=128), writes=[d_rope])
            kb.dma("sp", tri[:], self.c_tri, writes=[d_cst])
            kb.dma("sp", triu[:], self.c_triu, writes=[d_cst])
            kb.dma("sp", dkq[:], self.c_dkq, writes=[d_cst])
            kb.dma("sp", ESEL[:], self.c_esel, writes=[d_cst])
            kb.op("pool", lambda e: e.memset(VS[:].rearrange("p a b -> p (a b)"), 1.0), writes=[d_VS])
            kb.op("pool", lambda e: e.memset(VW[:].rearrange("p a b -> p (a b)"), 1.0), writes=[d_VW])
            kb.op("pool", lambda e: e.memset(RC[:].rearrange("p a b -> p (a b)"), 1.0), writes=[d_RC])
            kb.dma("sp", VS[:, :, 0:128], self.proj_tm[:, C_NVS:C_NVS + 128].rearrange("(t p) c -> p t c", p=128), reads=[d_VS], writes=[d_VS])
            kb.dma("sp", VW[:, :, 0:128], self.proj_tm[:, C_NVW:C_NVW + 128].rearrange("(t p) c -> p t c", p=128), reads=[d_VW], writes=[d_VW])
            kb.dma("sp", RC[:, :, 129:193], self.c_ovl.rearrange("(t p) j -> p t j", p=128), reads=[d_RC], writes=[d_RC])
            kb.dma("pool", G[:], self.proj_tm[:, C_NG:C_NG + 12].rearrange("(t p) c -> p t c", p=128), writes=[d_G])
            kb.op("act", lambda e: e.activation(out=G[:].rearrange("p a b -> p (a b)"), in_=G[:].rearrange("p a b -> p (a b)"), func=AF.Sigmoid),
                  reads=[d_G], writes=[d_G])
            self.qk_prep("nq", self.proj_tm, C_NQ, 4, self.hn["nsa_q_norm"], l, NQT, d_NQT, rope, d_rope)
            self.qk_prep("nks", self.proj_tm, C_NKS, 1, self.hn["nsa_ks_norm"], l, KST, d_KST, rope, d_rope)
            self.qk_prep("nkw", self.proj_tm, C_NKW, 1, self.hn["nsa_kw_norm"], l, KWT, d_KWT, rope, d_rope)
            with ExitStack() as st2:
                sb2 = lambda n, s, d: st2.enter_context(self.sbt("N2_" + n, s, d))
                XcT = sb2("XcT", [128, 2, S], BF16)
                w1b = [sb2("w1b%d" % i, [128, 32, 128], BF16) for i in range(2)]
                w2b = [sb2("w2b%d" % i, [128, 128], BF16) for i in range(2)]
                pef = sb2("pef", [32, 2, 128], F32)
                peT = sb2("peT", [128, 2, 32], BF16)
                cvec = sb2("cvec", [128, 2], F32)
                xr = [sb2("xr%d" % i, [128, 256], BF16) for i in range(2)]
                hs = sb2("hs", [128, 256], F32)
                htmp = sb2("htmp", [128, 256], F32)
                hT = [sb2("hT%d" % i, [128, 256], BF16) for i in range(2)]
                kcs = sb2("kcs", [128, 2, 128], BF16)
                d_XcT, d_w1, d_w2, d_pef, d_peT, d_cvec, d_hs, d_htmp, d_kcs = kb.deps_n(9, "cm")
                d_xr = kb.deps_n(2)
                d_hT = kb.deps_n(2)
                w1s = [self.ck_w1, self.cv_w1]
                w2s = [self.ck_w2, self.cv_w2]
                pes = [self.pe_k, self.pe_v]
                for j in range(2):
                    kb.dma("pool", w1b[j][:], w1s[j][l].rearrange("(l d) o -> d l o", d=128), writes=[d_w1])
                    kb.dma("pool", w2b[j][:], w2s[j][l], writes=[d_w2])
                    kb.dma("sp", pef[:, j, :], pes[j][l], writes=[d_pef])
                for j in range(2):
                    kb.op("pe", lambda e: e.transpose(out=ps[0][:, j * 32:(j + 1) * 32], in_=pef[:, j, :], identity=self.identf[0:32, 0:32]),
                          reads=[d_pef, self.d_const], writes=[dps[0]])
                kb.op("dve", lambda e: e.tensor_copy(out=peT[:].rearrange("p a b -> p (a b)"), in_=ps[0][:, 0:64]), reads=[dps[0]], writes=[d_peT])
                for tt in range(NT):
                    i = tt % 2
                    kb.dma("sp", xr[i][:], self.proj_tm[tt * 128:(tt + 1) * 128, C_NKC:C_NKC + 256], writes=[d_xr[i]])
                    bk = 5 + i
                    pb = ps[bk][:].bitcast(BF16)
                    for j in range(2):
                        kb.op("pe", lambda e: e.transpose(out=pb[:, j * 128:(j + 1) * 128], in_=xr[i][:, j * 128:(j + 1) * 128], identity=self.identb[:]),
                              reads=[d_xr[i], self.d_const], writes=[dps[bk]])
                    kb.op("dve", lambda e: e.tensor_copy(out=XcT[:, :, tt * 128:(tt + 1) * 128], in_=pb[:, 0:256].rearrange("p (j n) -> p j n", j=2)),
                          reads=[dps[bk]], writes=[d_XcT])
                for j in range(2):
                    for ll in range(32):
                        kb.op("pe", lambda e: e.matmul(ps[1][:, j:j + 1], lhsT=w1b[j][:, ll, :], rhs=peT[:, j, ll:ll + 1], start=(ll == 0), stop=(ll == 31)),
                              reads=[d_w1, d_peT], writes=[dps[1]])
                    kb.op("dve", lambda e: e.tensor_copy(out=cvec[:, j:j + 1], in_=ps[1][:, j:j + 1]), reads=[dps[1]], writes=[d_cvec])
                    for ll in range(32):
                        kb.op("pe", lambda e: e.matmul(ps[2 + j][:, 0:255], lhsT=w1b[j][:, ll, :], rhs=XcT[:, j, ll:ll + 16 * 254 + 1:16], start=(ll == 0), stop=(ll == 31)),
                              reads=[d_w1, d_XcT], writes=[dps[2 + j]])
                    kb.op("dve", lambda e: e.tensor_scalar(out=hs[:, 0:255], in0=ps[2 + j][:, 0:255], scalar1=cvec[:, j:j + 1], scalar2=None, op0=ALU.add),
                          reads=[dps[2 + j], d_cvec], writes=[d_hs])
                    kb.op("pool", lambda e: e.memset(hT[j][:], 0.0), writes=[d_hT[j]])
                    self.gelu_tanh(hs[:, 0:255], d_hs, htmp[:, 0:255], d_htmp, hT[j][:, 0:255], d_hT[j])
                    for it in range(2):
                        kb.op("pe", lambda e: e.matmul(ps[4][:, it * 128:(it + 1) * 128], lhsT=hT[j][:, it * 128:(it + 1) * 128], rhs=w2b[j][:], start=True, stop=True),
                              reads=[d_hT[j], d_w2], writes=[dps[4]])
                    if j == 0:
                        kb.op("dve", lambda e: e.tensor_copy(out=kcs[:].rearrange("p a b -> p (a b)"), in_=ps[4][:, 0:256]), reads=[dps[4]], writes=[d_kcs])
                        kb.dma("sp", self.kcmp_tm.rearrange("(t p) c -> p t c", p=128), kcs[:], reads=[d_kcs])
                    else:
                        kb.op("dve", lambda e: e.tensor_copy(out=RC[:, :, 0:128], in_=ps[4][:, 0:256].rearrange("p (a b) -> p a b", a=2)),
                              reads=[dps[4], d_RC], writes=[d_RC])
                kb.barrier()
            self.qk_prep("nkc", self.kcmp_tm, 0, 1, self.hn["nsa_kc_norm"], l, KCT, d_KCT, ropec, d_rope, ntiles=2)
            ONS = sb("ONS", [128, NT, 512], F32)
            PT = [sb("PT%d" % i, [128, 4, 128], BF16) for i in range(7)]
            M2s = [sb("M2s%d" % i, [128, 128], BF16) for i in range(4)]
            sA = [sb("sA%d" % i, [128, 64], F32) for i in range(2)]
            sB = [sb("sB%d" % i, [128, 64], F32) for i in range(2)]
            imp = [sb("imp%d" % i, [128, 64], F32) for i in range(2)]
            sc = [sb("sc%d" % i, [128, 2, 64], F32) for i in range(2)]
            m8 = [sb("m8%d" % i, [128, 2, 8], F32) for i in range(2)]
            selq = [sb("selq%d" % i, [128, 64], BF16) for i in range(2)]
            rr = [sb("rr%d" % i, [128, 8], F32) for i in range(2)]
            d_PT = kb.deps_n(7)
            d_M2s = kb.deps_n(4)
            d_m2p = kb.deps_n(4)
            d_sA = kb.deps_n(2)
            d_sB = kb.deps_n(2)
            d_imp = kb.deps_n(2)
            d_sc = kb.deps_n(2)
            d_m8 = kb.deps_n(2)
            d_selq = kb.deps_n(2)
            d_rr = kb.deps_n(2)
            ip = 0

            def obank(tt, h):
                return 5 + h // 2, (h % 2) * 256

            zl = sb("zl", [128, 128], BF16)
            zr_ = sb("zr", [128, 512], BF16)
            d_z = kb.dep("zeros")
            kb.op("pool", lambda e: e.memset(zl[:], 0.0), writes=[d_z])
            kb.op("pool", lambda e: e.memset(zr_[:], 0.0), writes=[d_z])

            def zero_obanks():
                for b in (5, 6):
                    kb.op("pe", lambda e: e.matmul(ps[b][:], lhsT=zl[:], rhs=zr_[:], start=True, stop=True), reads=[d_z], writes=[dps[b]])

            def finalize(tt, branch, first):
                i = tt % 2
                for h in range(4):
                    b, c0 = obank(tt, h)
                    kb.op("dve", lambda e: e.tensor_scalar(out=rr[i][:, h:h + 1], in0=ps[b][:, c0 + 128:c0 + 129], scalar1=1e-30, scalar2=None, op0=ALU.max),
                          reads=[dps[b]], writes=[d_rr[i]])
                kb.op("dve", lambda e: e.reciprocal(out=rr[i][:, 0:4], in_=rr[i][:, 0:4]), reads=[d_rr[i]], writes=[d_rr[i]])
                kb.op("dve", lambda e: e.tensor_tensor(out=rr[i][:, 4:8], in0=rr[i][:, 0:4], in1=G[:, tt, branch:12:3], op=ALU.mult), reads=[d_rr[i], d_G], writes=[d_rr[i]])
                for h in range(4):
                    b, c0 = obank(tt, h)
                    dst = ONS[:, tt, h * 128:(h + 1) * 128]
                    if first:
                        kb.op("dve", lambda e: e.tensor_scalar(out=dst, in0=ps[b][:, c0:c0 + 128], scalar1=rr[i][:, 4 + h:5 + h], scalar2=None, op0=ALU.mult),
                              reads=[dps[b], d_rr[i]], writes=[d_ONS])
                    else:
                        kb.op("dve", lambda e: e.scalar_tensor_tensor(out=dst, in0=ps[b][:, c0:c0 + 128], scalar=rr[i][:, 4 + h:5 + h], in1=dst, op0=ALU.mult, op1=ALU.add),
                              reads=[dps[b], d_rr[i], d_ONS], writes=[d_ONS])

            it_cmp = [(tt, it) for tt in range(NT) for it in range(1 if tt < 16 else 2)]

            def c_front(idx):
                tt, it = it_cmp[idx]
                pi = idx % 3
                sbank = idx % 2
                i = tt % 2
                if it == 0:
                    kb.dma("sp", sA[i][:], self.c_selA[tt], writes=[d_sA[i]])
                    kb.dma("sp", sB[i][:], self.c_selB[tt], writes=[d_sB[i]])
                kb.op("pe", lambda e: e.matmul(ps[sbank][:], lhsT=KCT[:, 0, it * 128:(it + 1) * 128], rhs=NQT[:, :, tt * 128:(tt + 1) * 128], start=True, stop=True),
                      reads=[d_KCT, d_NQT], writes=[dps[sbank]])
                kb.op("act", lambda e: e.activation(out=PT[pi][:].rearrange("p a b -> p (a b)"), in_=ps[sbank][:], func=AF.Exp, scale=SC),
                      reads=[dps[sbank]], writes=[d_PT[pi]])
                thr = float(31 + 2048 * it - 128 * tt)
                kb.op("dve", lambda e: e.scalar_tensor_tensor(out=PT[pi][:], in0=dkq[:].unsqueeze(1).to_broadcast([128, 4, 128]), scalar=thr, in1=PT[pi][:],
                                                              op0=ALU.is_ge, op1=ALU.mult), reads=[d_PT[pi], d_cst], writes=[d_PT[pi]])

            def c_back(idx):
                tt, it = it_cmp[idx]
                pi = idx % 3
                i = tt % 2
                n_it = 1 if tt < 16 else 2
                for h in range(4):
                    b, c0 = obank(tt, h)
                    if it == 0 and h == 0:
                        zero_obanks()
                    kb.op("pe", lambda e: e.matmul(ps[b][:, c0:c0 + 193], lhsT=PT[pi][:, h, :], rhs=RC[:, it, 0:193], start=False, stop=(it == n_it - 1)),
                          reads=[d_PT[pi], d_RC], writes=[dps[b]])
                if it != n_it - 1:
                    return
                finalize(tt, 0, True)
                for h in range(4):
                    b, c0 = obank(tt, h)
                    if h == 0:
                        kb.op("dve", lambda e: e.tensor_scalar(out=imp[i][:], in0=ps[b][:, c0 + 129:c0 + 193], scalar1=rr[i][:, h:h + 1], scalar2=None, op0=ALU.mult),
                              reads=[dps[b], d_rr[i]], writes=[d_imp[i]])
                    else:
                        kb.op("dve", lambda e: e.scalar_tensor_tensor(out=imp[i][:], in0=ps[b][:, c0 + 129:c0 + 193], scalar=rr[i][:, h:h + 1], in1=imp[i][:],
                                                                      op0=ALU.mult, op1=ALU.add), reads=[dps[b], d_rr[i], d_imp[i]], writes=[d_imp[i]])
                kb.op("dve", lambda e: e.tensor_tensor(out=sc[i][:, 0, :], in0=imp[i][:], in1=sA[i][:], op=ALU.mult), reads=[d_imp[i], d_sA[i]], writes=[d_sc[i]])
                kb.op("dve", lambda e: e.tensor_tensor(out=sc[i][:, 0, :], in0=sc[i][:, 0, :], in1=sB[i][:], op=ALU.add), reads=[d_sc[i], d_sB[i]], writes=[d_sc[i]])
                kb.op("dve", lambda e: e.max(out=m8[i][:, 0, :], in_=sc[i][:, 0, :]), reads=[d_sc[i]], writes=[d_m8[i]])
                kb.op("dve", lambda e: e.match_replace(out=sc[i][:, 1, :], in_to_replace=m8[i][:, 0, :], in_values=sc[i][:, 0, :], imm_value=NEG),
                      reads=[d_sc[i], d_m8[i]], writes=[d_sc[i]])
                kb.op("dve", lambda e: e.max(out=m8[i][:, 1, :], in_=sc[i][:, 1, :]), reads=[d_sc[i]], writes=[d_m8[i]])
                kb.op("dve", lambda e: e.scalar_tensor_tensor(out=selq[i][:], in0=sc[i][:, 0, :], scalar=m8[i][:, 1, 7:8], in1=sA[i][:], op0=ALU.is_ge, op1=ALU.mult),
                      reads=[d_sc[i], d_m8[i], d_sA[i]], writes=[d_selq[i]])
                pb = ps[2 + i][:].bitcast(BF16)
                kb.op("pe", lambda e: e.transpose(out=pb[0:64, 0:128], in_=selq[i][:], identity=self.identb[:]), reads=[d_selq[i], self.d_const], writes=[dps[2 + i]])
                kb.op("act", lambda e: e.activation(out=SELT[:, tt, :], in_=pb[0:64, 0:128], func=AF.Copy), reads=[dps[2 + i]], writes=[d_SELT])

            c_front(0)
            for idx in range(len(it_cmp)):
                if idx + 1 < len(it_cmp):
                    c_front(idx + 1)
                c_back(idx)

            its = []
            for branch in (1, 2):
                for tt in range(NT):
                    kts = list(range(0, tt + 1)) if branch == 1 else list(range(max(0, tt - 4), tt + 1))
                    for ki, kt in enumerate(kts):
                        its.append((branch, tt, ki, kt, len(kts)))

            def a_front(idx):
                branch, tt, ki, kt, nk = its[idx]
                pi = 3 + idx % 4
                sbank = idx % 3
                KT_ = KST if branch == 1 else KWT
                d_KT_ = d_KST if branch == 1 else d_KWT
                kb.op("pe", lambda e: e.matmul(ps[sbank][:], lhsT=KT_[:, 0, kt * 128:(kt + 1) * 128], rhs=NQT[:, :, tt * 128:(tt + 1) * 128], start=True, stop=True),
                      reads=[d_KT_, d_NQT], writes=[dps[sbank]])
                kb.op("act", lambda e: e.activation(out=PT[pi][:].rearrange("p a b -> p (a b)"), in_=ps[sbank][:], func=AF.Exp, scale=SC),
                      reads=[dps[sbank]], writes=[d_PT[pi]])
                if branch == 1:
                    mb = 3 + (idx % 2)
                    mi = idx % 4
                    msl = ps[mb][:, 0:128]
                    kb.op("pe", lambda e: e.matmul(msl, lhsT=ESEL[:, kt, :], rhs=SELT[:, tt, :], start=True, stop=True),
                          reads=[d_cst, d_SELT], writes=[dps[mb]])
                    if kt == tt:
                        kb.op("dve", lambda e: e.tensor_tensor(out=M2s[mi][:], in0=msl, in1=tri[:], op=ALU.mult), reads=[dps[mb], d_cst], writes=[d_M2s[mi]])
                    else:
                        kb.op("dve", lambda e: e.tensor_copy(out=M2s[mi][:], in_=msl), reads=[dps[mb]], writes=[d_M2s[mi]])
                    meng = "pool" if idx % 3 == 0 else "dve"
                    kb.op(meng, lambda e: e.tensor_tensor(out=PT[pi][:], in0=PT[pi][:], in1=M2s[mi][:].unsqueeze(1).to_broadcast([128, 4, 128]), op=ALU.mult),
                          reads=[d_PT[pi], d_M2s[mi]], writes=[d_PT[pi]])
                else:
                    if kt == tt:
                        kb.op("pool", lambda e: e.tensor_tensor(out=PT[pi][:], in0=PT[pi][:], in1=tri[:].unsqueeze(1).to_broadcast([128, 4, 128]), op=ALU.mult),
                              reads=[d_PT[pi], d_cst], writes=[d_PT[pi]])
                    elif kt == tt - 4:
                        kb.op("pool", lambda e: e.tensor_tensor(out=PT[pi][:], in0=PT[pi][:], in1=triu[:].unsqueeze(1).to_broadcast([128, 4, 128]), op=ALU.mult),
                              reads=[d_PT[pi], d_cst], writes=[d_PT[pi]])

            def a_back(idx):
                branch, tt, ki, kt, nk = its[idx]
                pi = 3 + idx % 4
                V_ = VS if branch == 1 else VW
                d_V_ = d_VS if branch == 1 else d_VW
                for h in range(4):
                    b, c0 = obank(tt, h)
                    if ki == 0 and h == 0:
                        zero_obanks()
                    kb.op("pe", lambda e: e.matmul(ps[b][:, c0:c0 + 129], lhsT=PT[pi][:, h, :], rhs=V_[:, kt, 0:129], start=False, stop=(ki == nk - 1)),
                          reads=[d_PT[pi], d_V_], writes=[dps[b]])
                if ki == nk - 1:
                    finalize(tt, branch, False)

            SK = 2
            for idx in range(min(SK, len(its))):
                a_front(idx)
            for idx in range(len(its)):
                if idx + SK < len(its):
                    a_front(idx + SK)
                a_back(idx)
            kb.barrier()
            self.gate_and_store(ONS, d_ONS, C_NZ, 512)


def host_consts():
    c = {}
    c["c_identb"] = np.eye(128, dtype=np.float32).astype(ml_dtypes.bfloat16)
    c["c_identf"] = np.eye(128, dtype=np.float32)
    inv = 500000.0 ** (-np.arange(0, 32, 2, dtype=np.float32) / 32.0)
    pos = np.arange(S, dtype=np.float32)
    ang = pos[:, None] * inv[None, :].astype(np.float32)
    c["c_rope"] = np.concatenate([np.cos(ang), np.sin(ang)], axis=1).astype(np.float32)
    posc = (np.arange(256) * 16 + 31).astype(np.float32)
    angc = posc[:, None] * inv[None, :].astype(np.float32)
    c["c_ropec"] = np.concatenate([np.cos(angc), np.sin(angc)], axis=1).astype(np.float32)
    kk = np.arange(128)
    c["c_tri"] = (kk[:, None] <= kk[None, :]).astype(np.float32).astype(ml_dtypes.bfloat16)
    c["c_triu"] = (kk[:, None] > kk[None, :]).astype(np.float32).astype(ml_dtypes.bfloat16)
    c["c_iota"] = np.broadcast_to(np.arange(512, dtype=np.float32)[None, :], (128, 512)).copy()
    c["c_dkq"] = (kk[None, :] - 16 * kk[:, None]).astype(np.float32)
    selA = np.zeros((NT, 128, 64), np.float32)
    selB = np.zeros((NT, 128, 64), np.float32)
    j = np.arange(64)[None, :]
    for tt in range(NT):
        t = tt * 128 + np.arange(128)
        cur = (t // 64)[:, None]
        valid = j <= cur
        forced = (j == 0) | (j == cur) | (j == cur - 1)
        selA[tt] = valid.astype(np.float32)
        selB[tt] = np.where(forced, 1.0e30, np.where(valid, 0.0, -1.0e30))
    c["c_selA"] = selA
    c["c_selB"] = selB
    es = np.zeros((64, NT, 128), np.float32)
    for kt in range(NT):
        for key in range(128):
            es[2 * kt + key // 64, kt, key] = 1.0
    c["c_esel"] = es.astype(ml_dtypes.bfloat16)
    ci = np.arange(256)[:, None] * 16
    sj = np.arange(64)[None, :] * 64
    ov = ((ci < sj + 64) & (ci + 32 > sj)).astype(np.float32)
    ov[255] = 0.0
    c["c_ovl"] = ov.astype(ml_dtypes.bfloat16)
    return c


_PROG = None


def kernel(**inputs):
    global _PROG
    if _PROG is None:
        _PROG = Prog().build()
    nc = _PROG
    consts = host_consts()
    x = np.ascontiguousarray(inputs["x"], dtype=np.float32)
    B = x.shape[0]
    shared = {k: np.ascontiguousarray(v) for k, v in inputs.items() if k != "x"}
    in_maps = []
    for c in range(8):
        m = dict(shared)
        m.update(consts)
        m["x"] = x[c % B]
        in_maps.append(m)
    res = run_bass_kernel_spmd(nc, in_maps, core_ids=list(range(8)))
    out = np.stack([res.results[b]["out"] for b in range(B)], axis=0)
    return out.astype(np.float32, copy=False)
```

```python
import math
from contextlib import ExitStack

import numpy as np
import ml_dtypes
import concourse.bass as bass
import concourse.mybir as mybir
from concourse.bass_utils import run_bass_kernel_spmd

F32 = mybir.dt.float32
BF16 = mybir.dt.bfloat16
I32 = mybir.dt.int32
AF = mybir.ActivationFunctionType
ALU = mybir.AluOpType
AX = mybir.AxisListType

S = 4096
D = 2048
NT = S // 128
INW = 5900
TMW = 3852
DEPTH = 2
EPS = 1e-6
C_MQ, C_MK, C_MV, C_MZ, C_NQ = 0, 512, 1024, 1536, 2048
C_NKC, C_NVC, C_NKS, C_NVS, C_NKW, C_NVW = 2560, 2688, 2816, 2944, 3072, 3200
C_NG, C_NZ = 3328, 3340
NEG = -1.0e30


class Dep:
    __slots__ = ("w", "r", "name")

    def __init__(self, name=""):
        self.w = {}
        self.r = {}
        self.name = name


class Eng:
    def __init__(self, nc, eng, name):
        self.eng = eng
        self.name = name
        self.sem = nc.alloc_semaphore("sem_" + name)
        self.count = 0
        self.seen = {}


class KB:
    def __init__(self, nc, n_dma_sems=48):
        self.nc = nc
        self.E = {
            "pe": Eng(nc, nc.tensor, "pe"),
            "act": Eng(nc, nc.scalar, "act"),
            "dve": Eng(nc, nc.vector, "dve"),
            "pool": Eng(nc, nc.gpsimd, "pool"),
            "sp": Eng(nc, nc.sync, "sp"),
        }
        self.dsems = [[nc.alloc_semaphore("dsem%d" % i), 0] for i in range(n_dma_sems)]
        self.dnext = 0
        self.deps = []
        self.n_wait = 0
        self.n_ins = 0

    def dep(self, name=""):
        d = Dep(name)
        self.deps.append(d)
        return d

    def deps_n(self, n, name=""):
        return [self.dep(name + str(i)) for i in range(n)]

    def _wait(self, E, sem, val):
        k = id(sem)
        if E.seen.get(k, 0) < val:
            E.eng.wait_ge(sem, val)
            E.seen[k] = val
            self.n_wait += 1

    def _sync(self, E, reads, writes, own_sem=None):
        for d in reads:
            for k, (s, v) in d.w.items():
                self._wait(E, s, v)
        for d in writes:
            for k, (s, v) in d.w.items():
                if s is own_sem:
                    continue
                self._wait(E, s, v)
            for k, (s, v) in d.r.items():
                if s is own_sem:
                    continue
                self._wait(E, s, v)

    def _record(self, sem, val, reads, writes):
        k = id(sem)
        for d in writes:
            d.w = {k: (sem, val)}
            d.r = {}
        for d in reads:
            d.r[k] = (sem, val)

    def op(self, e, f, reads=(), writes=()):
        E = self.E[e]
        self._sync(E, reads, writes, own_sem=E.sem)
        ins = f(E.eng)
        E.count += 1
        ins.then_inc(E.sem, 1)
        self._record(E.sem, E.count, reads, writes)
        self.n_ins += 1
        return ins

    def dma(self, q, out, in_, reads=(), writes=(), **kw):
        E = self.E[q]
        self._sync(E, reads, writes)
        ent = self.dsems[self.dnext]
        self.dnext = (self.dnext + 1) % len(self.dsems)
        if ent[1] > 0:
            self._wait(E, ent[0], ent[1])
        ent[1] += 16
        ins = E.eng.dma_start(out=out, in_=in_, **kw)
        ins.then_inc(ent[0], 16)
        self._record(ent[0], ent[1], reads, writes)
        self.n_ins += 1
        return ins

    def barrier(self):
        sp = self.E["sp"]
        for n, E in self.E.items():
            if E is not sp and E.count > 0:
                self._wait(sp, E.sem, E.count)
        for s, v in self.dsems:
            if v > 0:
                self._wait(sp, s, v)
        sp.count += 1
        sp.eng.nop().then_inc(sp.sem, 1)
        for n, E in self.E.items():
            if E is not sp:
                self._wait(E, sp.sem, sp.count)
            for n2, E2 in self.E.items():
                E.seen[id(E2.sem)] = E2.count
            for s, v in self.dsems:
                E.seen[id(s)] = v
        for d in self.deps:
            d.w = {}
            d.r = {}
        self.deps = []


class Prog:
    def __init__(self, dbg=None, layers=DEPTH, phases=("A", "S5", "MOBA", "NSA", "F")):
        self.dbg = dbg or ()
        self.layers = layers
        self.phases = phases
        nc = bass.Bass("TRN2", target_bir_lowering=False)
        self.nc = nc
        self.kb = KB(nc)
        ein = lambda n, s, d: nc.dram_tensor(n, list(s), d, kind="ExternalInput").ap()
        L = DEPTH
        self.x = ein("x", [S, D], F32)
        self.norm_w = ein("norm_w", [L, D], F32)
        self.w_in = ein("w_in", [L, D, INW], F32)
        self.w_out = ein("w_out", [L, D, D], F32)
        self.hn = {}
        for n in ("moba_q_norm", "moba_k_norm", "nsa_q_norm", "nsa_kc_norm", "nsa_ks_norm", "nsa_kw_norm"):
            self.hn[n] = ein(n, [L, 128], F32)
        self.pe_k = ein("nsa_pe_k", [L, 32, 128], F32)
        self.pe_v = ein("nsa_pe_v", [L, 32, 128], F32)
        self.ck_w1 = ein("nsa_cmp_k_w1", [L, 4096, 128], F32)
        self.ck_w2 = ein("nsa_cmp_k_w2", [L, 128, 128], F32)
        self.cv_w1 = ein("nsa_cmp_v_w1", [L, 4096, 128], F32)
        self.cv_w2 = ein("nsa_cmp_v_w2", [L, 128, 128], F32)
        self.a_re = ein("s5_a_re", [L, 64, 64], F32)
        self.a_im = ein("s5_a_im", [L, 64, 64], F32)
        self.b_re = ein("s5_b_re", [L, 64, 64, 16], F32)
        self.b_im = ein("s5_b_im", [L, 64, 64, 16], F32)
        self.c_re = ein("s5_c_re", [L, 64, 16, 64], F32)
        self.c_im = ein("s5_c_im", [L, 64, 16, 64], F32)
        self.s5_d = ein("s5_d", [L, 1024], F32)
        self.log_dt = ein("s5_log_dt", [L, 64], F32)
        self.glu_w = ein("s5_glu_w", [L, 1024, 1024], F32)
        self.c_identb = ein("c_identb", [128, 128], BF16)
        self.c_identf = ein("c_identf", [128, 128], F32)
        self.c_rope = ein("c_rope", [S, 32], F32)
        self.c_ropec = ein("c_ropec", [256, 32], F32)
        self.c_tri = ein("c_tri", [128, 128], BF16)
        self.c_iota = ein("c_iota", [128, 512], F32)
        self.c_triu = ein("c_triu", [128, 128], BF16)
        self.c_dkq = ein("c_dkq", [128, 128], F32)
        self.c_selA = ein("c_selA", [NT, 128, 64], F32)
        self.c_selB = ein("c_selB", [NT, 128, 64], F32)
        self.c_esel = ein("c_esel", [64, NT, 128], BF16)
        self.c_ovl = ein("c_ovl", [256, 64], BF16)
        self.out = nc.dram_tensor("out", [S, D], F32, kind="ExternalOutput").ap()
        sk = lambda n: "ExternalOutput" if n in self.dbg else "Internal"
        self.proj_tm = nc.dram_tensor("proj_tm", [S, TMW], BF16, kind=("ExternalInput" if "proj_in" in self.dbg else sk("proj_tm"))).ap()
        self.sT = nc.dram_tensor("sT", [2048, S], BF16, kind=("ExternalInput" if "sT_in" in self.dbg else sk("sT"))).ap()
        self.mixedT = nc.dram_tensor("mixedT", [2048, S], BF16, kind=("ExternalInput" if "mixedT_in" in self.dbg else sk("mixedT"))).ap()
        self.x1 = nc.dram_tensor("x1", [S, D], F32, kind=sk("x1")).ap()
        self.kcmp_tm = nc.dram_tensor("kcmp_tm", [256, 128], BF16, kind=sk("kcmp_tm")).ap()
        self.y5d = nc.dram_tensor("y5d", [1024, S], BF16, kind=sk("y5d")).ap()

    @staticmethod
    def emit_pipelined(n, stages):
        ns = len(stages)
        for t in range(n + ns - 1):
            for s_idx in range(ns - 1, -1, -1):
                i = t - s_idx
                if 0 <= i < n:
                    stages[s_idx](i)

    def sbt(self, name, shape, dtype):
        self._uid = getattr(self, "_uid", 0) + 1
        return self.nc.sbuf_tensor("%s_u%d" % (name, self._uid), shape, dtype)

    def build(self):
        nc, kb = self.nc, self.kb
        with ExitStack() as st:
            self.ps = [st.enter_context(nc.psum_tensor("ps%d" % i, [128, 512], F32)) for i in range(8)]
            self.dps = kb.deps_n(8, "ps")
            self.identb = st.enter_context(self.sbt("identb", [128, 128], BF16))
            self.identf = st.enter_context(self.sbt("identf", [128, 128], F32))
            self.d_const = kb.dep("const")
            kb.dma("sp", self.identb[:], self.c_identb, writes=[self.d_const])
            kb.dma("sp", self.identf[:], self.c_identf, writes=[self.d_const])
            kb.barrier()
            for l in range(self.layers):
                src = self.x if l == 0 else self.x1
                dst = self.out if l == self.layers - 1 else self.x1
                if "A" in self.phases:
                    self.phase_A(l, src)
                    kb.barrier()
                if "S5" in self.phases:
                    self.phase_S5(l)
                    kb.barrier()
                if "MOBA" in self.phases:
                    self.phase_MOBA(l)
                    kb.barrier()
                if "NSA" in self.phases:
                    self.phase_NSA(l)
                    kb.barrier()
                if "F" in self.phases:
                    self.phase_F(l, src, dst)
                    kb.barrier()
            kb.barrier()
        return nc

    def phase_A(self, l, src):
        nc, kb = self.nc, self.kb
        ps, dps = self.ps, self.dps
        with ExitStack() as st:
            sb = lambda n, s, d: st.enter_context(self.sbt("A_" + n, s, d))
            hdnT = sb("hdnT", [128, 16, 2048], BF16)
            normw = sb("normw", [128, D], F32)
            xt = [sb("xt%d" % i, [128, D], F32) for i in range(3)]
            junk = sb("junk", [128, D], BF16)
            hb = [sb("hb%d" % i, [128, D], BF16) for i in range(3)]
            wch = [sb("wch%d" % i, [128, 16, 512], BF16) for i in range(2)]
            stg = [sb("stg%d" % i, [128, 512], BF16) for i in range(4)]
            ss = [sb("ss%d" % i, [128, 1], F32) for i in range(3)]
            d_hT = kb.deps_n(16, "hT")
            d_nw = kb.dep("nw")
            d_xt = kb.deps_n(3, "xt")
            d_junk = kb.dep("junk")
            d_hb = kb.deps_n(3, "hb")
            d_w = kb.deps_n(2, "w")
            d_stg = kb.deps_n(4, "stg")
            d_ss = kb.deps_n(3, "ss")
            kb.dma("sp", normw[:], self.norm_w[l:l + 1, :].partition_broadcast(128), writes=[d_nw])
            istg = 0
            iw = 0
            ievac = 0
            for h in range(2):
                def a1(tt, h=h):
                    g = h * 16 + tt
                    i = tt % 3
                    xi = tt % 3
                    kb.dma("sp", xt[xi][:], src[g * 128:(g + 1) * 128, :], writes=[d_xt[xi]])
                    kb.op("act", lambda e: e.activation(out=junk[:], in_=xt[xi][:], func=AF.Square, accum_out=ss[i][:]),
                          reads=[d_xt[xi]], writes=[d_junk, d_ss[i]])
                    kb.op("dve", lambda e: e.tensor_scalar(out=ss[i][:], in0=ss[i][:], scalar1=1.0 / D, scalar2=EPS,
                                                           op0=ALU.mult, op1=ALU.add), reads=[d_ss[i]], writes=[d_ss[i]])
                    kb.op("act", lambda e: e.activation(out=ss[i][:], in_=ss[i][:], func=AF.Sqrt), reads=[d_ss[i]], writes=[d_ss[i]])
                    kb.op("dve", lambda e: e.reciprocal(out=ss[i][:], in_=ss[i][:]), reads=[d_ss[i]], writes=[d_ss[i]])
                def a2(tt, h=h):
                    i = tt % 3
                    xi = tt % 3
                    kb.op("dve", lambda e: e.scalar_tensor_tensor(out=hb[xi][:], in0=xt[xi][:], scalar=ss[i][:], in1=normw[:],
                                                                  op0=ALU.mult, op1=ALU.mult),
                          reads=[d_xt[xi], d_ss[i], d_nw], writes=[d_hb[xi]])
                    for half in range(2):
                        tbk = 4 + 2 * (tt % 2) + half
                        pb = ps[tbk][:].bitcast(BF16)
                        for k in range(8):
                            kc = half * 8 + k
                            kb.op("pe", lambda e: e.transpose(out=pb[:, k * 128:(k + 1) * 128], in_=hb[xi][:, kc * 128:(kc + 1) * 128],
                                                              identity=self.identb[:]),
                                  reads=[d_hb[xi], self.d_const], writes=[dps[tbk]])
                def a3(tt, h=h):
                    for half in range(2):
                        tbk = 4 + 2 * (tt % 2) + half
                        pb = ps[tbk][:].bitcast(BF16)
                        eng = "act" if half == 0 else "dve"
                        dst = hdnT[:, half * 8:(half + 1) * 8, tt * 128:(tt + 1) * 128]
                        srcp = pb[:, 0:1024].rearrange("p (k n) -> p k n", k=8)
                        if eng == "act":
                            kb.op("act", lambda e: e.activation(out=dst, in_=srcp, func=AF.Copy), reads=[dps[tbk]], writes=[d_hT[tt]])
                        else:
                            kb.op("dve", lambda e: e.tensor_copy(out=dst, in_=srcp), reads=[dps[tbk]], writes=[d_hT[tt]])
                self.emit_pipelined(16, [a1, a2, a3])
                chunks = [(c0, min(512, TMW - c0)) for c0 in range(0, TMW, 512)]
                for (c0, cw) in chunks:
                    wi = iw % 2
                    iw += 1
                    kb.dma("pool", wch[wi][:, :, 0:cw], self.w_in[l, :, c0:c0 + cw].rearrange("(k p) n -> p k n", p=128),
                           writes=[d_w[wi]])
                    for tt in range(16):
                        g = h * 16 + tt
                        pbank = ievac % 4
                        for kc in range(16):
                            kb.op("pe", lambda e: e.matmul(ps[pbank][:, 0:cw], lhsT=hdnT[:, kc, tt * 128:(tt + 1) * 128],
                                                           rhs=wch[wi][:, kc, 0:cw], start=(kc == 0), stop=(kc == 15)),
                                  reads=[d_hT[tt], d_w[wi]], writes=[dps[pbank]])
                        si = istg % 4
                        istg += 1
                        if ievac % 2 == 0:
                            kb.op("act", lambda e: e.activation(out=stg[si][:, 0:cw], in_=ps[pbank][:, 0:cw], func=AF.Copy),
                                  reads=[dps[pbank]], writes=[d_stg[si]])
                        else:
                            kb.op("dve", lambda e: e.tensor_copy(out=stg[si][:, 0:cw], in_=ps[pbank][:, 0:cw]),
                                  reads=[dps[pbank]], writes=[d_stg[si]])
                        ievac += 1
                        kb.dma("sp", self.proj_tm[g * 128:(g + 1) * 128, c0:c0 + cw], stg[si][:, 0:cw], reads=[d_stg[si]])
                for fc in range(4):
                    c0 = TMW + fc * 512
                    wi = iw % 2
                    iw += 1
                    kb.dma("pool", wch[wi][:], self.w_in[l, :, c0:c0 + 512].rearrange("(k p) n -> p k n", p=128), writes=[d_w[wi]])
                    for ctl in range(4):
                        row0 = fc * 512 + ctl * 128
                        for tb in range(4):
                            pbank = ievac % 4
                            for kc in range(16):
                                kb.op("pe", lambda e: e.matmul(ps[pbank][:], lhsT=wch[wi][:, kc, ctl * 128:(ctl + 1) * 128],
                                                               rhs=hdnT[:, kc, tb * 512:(tb + 1) * 512], start=(kc == 0), stop=(kc == 15)),
                                      reads=d_hT[tb * 4:(tb + 1) * 4] + [d_w[wi]], writes=[dps[pbank]])
                            si = istg % 4
                            istg += 1
                            if ievac % 2 == 0:
                                kb.op("act", lambda e: e.activation(out=stg[si][:], in_=ps[pbank][:], func=AF.Copy),
                                      reads=[dps[pbank]], writes=[d_stg[si]])
                            else:
                                kb.op("dve", lambda e: e.tensor_copy(out=stg[si][:], in_=ps[pbank][:]),
                                      reads=[dps[pbank]], writes=[d_stg[si]])
                            ievac += 1
                            t0 = h * 2048 + tb * 512
                            kb.dma("sp", self.sT[row0:row0 + 128, t0:t0 + 512], stg[si][:], reads=[d_stg[si]])

    def phase_F(self, l, src, dst):
        nc, kb = self.nc, self.kb
        ps, dps = self.ps, self.dps
        with ExitStack() as st:
            sb = lambda n, s, d: st.enter_context(self.sbt("F_" + n, s, d))
            wo = sb("wo", [128, 16, D], BF16)
            mT = [sb("mT%d" % i, [128, 16, 512], BF16) for i in range(2)]
            xr = [sb("xr%d" % i, [128, D], F32) for i in range(2)]
            ot = [sb("ot%d" % i, [128, D], F32) for i in range(2)]
            d_wo = kb.deps_n(4, "wo")
            d_mT = kb.deps_n(2, "mT")
            d_xr = kb.deps_n(2, "xr")
            d_ot = kb.deps_n(2, "ot")
            for c in range(4):
                kb.dma("pool", wo[:, :, c * 512:(c + 1) * 512], self.w_out[l, :, c * 512:(c + 1) * 512].rearrange("(k p) n -> p k n", p=128),
                       writes=[d_wo[c]])
            ie = 0
            import os
            for tb in range(int(os.environ.get('F_TB', 8))):
                mi = tb % 2
                kb.dma("sp", mT[mi][:], self.mixedT[:, tb * 512:(tb + 1) * 512].rearrange("(k p) n -> p k n", p=128), writes=[d_mT[mi]])
                for t4 in range(4):
                    g = tb * 4 + t4
                    i = g % 2
                    kb.dma("sp", xr[i][:], src[g * 128:(g + 1) * 128, :], writes=[d_xr[i]])
                    for c in range(4):
                        pbank = ie % 4
                        ie += 1
                        for kc in range(16):
                            kb.op("pe", lambda e: e.matmul(ps[pbank][:], lhsT=mT[mi][:, kc, t4 * 128:(t4 + 1) * 128],
                                                           rhs=wo[:, kc, c * 512:(c + 1) * 512], start=(kc == 0), stop=(kc == 15)),
                                  reads=[d_mT[mi], d_wo[c]], writes=[dps[pbank]])
                        kb.op("dve", lambda e: e.tensor_tensor(out=ot[i][:, c * 512:(c + 1) * 512], in0=ps[pbank][:],
                                                               in1=xr[i][:, c * 512:(c + 1) * 512], op=ALU.add),
                              reads=[dps[pbank], d_xr[i]], writes=[d_ot[i]])
                    kb.dma("pool", dst[g * 128:(g + 1) * 128, :], ot[i][:], reads=[d_ot[i]])

    def sincos_turns(self, turns, cos_out, sin_out, tmpf, tmpi, tmpf2, dT, dC, dS, dtmp):
        kb = self.kb
        TWO_PI = 6.283185
        kb.op("dve", lambda e: e.tensor_copy(out=tmpi, in_=turns), reads=[dT], writes=[dtmp])
        kb.op("dve", lambda e: e.tensor_tensor(out=tmpf, in0=turns, in1=tmpi, op=ALU.subtract), reads=[dT, dtmp], writes=[dtmp])
        kb.op("act", lambda e: e.activation(out=sin_out, in_=tmpf, func=AF.Sin, scale=TWO_PI), reads=[dtmp], writes=[dS])
        kb.op("dve", lambda e: e.tensor_scalar(out=tmpf2, in0=tmpf, scalar1=0.25, scalar2=None, op0=ALU.add), reads=[dtmp], writes=[dtmp])
        kb.op("dve", lambda e: e.scalar_tensor_tensor(out=tmpf2, in0=tmpf2, scalar=0.5, in1=tmpf2, op0=ALU.is_gt, op1=ALU.subtract),
              reads=[dtmp], writes=[dtmp])
        kb.op("act", lambda e: e.activation(out=cos_out, in_=tmpf2, func=AF.Sin, scale=-TWO_PI), reads=[dtmp], writes=[dC])

    def phase_S5_v1(self, l):
        nc, kb = self.nc, self.kb
        ps, dps = self.ps, self.dps
        with ExitStack() as st:
            sb = lambda n, s, d: st.enter_context(self.sbt("S_" + n, s, d))
            BT = [sb("BT%d" % i, [128, 32, 128], BF16) for i in range(2)]
            CT = [sb("CT%d" % i, [128, 32, 128], BF16) for i in range(2)]
            prm = sb("prm", [128, 24, 32], F32)
            prmi = sb("prmi", [128, 32], I32)
            Dt = sb("Dt", [128, 8], F32)
            gluw = sb("gluw", [128, 8, 1024], BF16)
            d_BT, d_CT, d_prm, d_Dt, d_glu = kb.deps_n(5, "s5c")
            AR, AI, LDT, DTT, MM, PHI, COS, SIN, FR, FI, C512, S512, T0, T1, T2, T3, T4, T5 = range(18)
            P = lambda i: prm[:, i, :]
            kb.dma("pool", gluw[:], self.glu_w[l].rearrange("(k p) n -> p k n", p=128), writes=[d_glu])
            with ExitStack() as st2:
                sb2 = lambda n, s, d: st2.enter_context(self.sbt("S2_" + n, s, d))
                XA = sb2("XA", [32, 3, 128], F32)
                ld2 = sb2("ld2", [32, 2], F32)
                XD = sb2("XD", [8, 128], F32)
                pads = [sb2("pad%d" % i, [128, 32, 128], F32) for i in range(4)]
                d_XA, d_ld2, d_XD = kb.deps_n(3, "xa")
                d_pad = kb.deps_n(4, "pad")
                kb.dma("sp", XA[:, 0, :], self.a_re[l].rearrange("(q gl) p -> q (gl p)", gl=2), writes=[d_XA])
                kb.dma("sp", XA[:, 1, :], self.a_im[l].rearrange("(q gl) p -> q (gl p)", gl=2), writes=[d_XA])
                kb.dma("sp", ld2[:], self.log_dt[l:l + 1, :].rearrange("o (q gl) -> (o q) gl", gl=2), writes=[d_ld2])
                kb.dma("sp", XD[:], self.s5_d[l:l + 1, :].rearrange("o (c p) -> (o c) p", p=128), writes=[d_XD])
                kb.op("dve", lambda e: e.tensor_copy(out=XA[:, 2, :].rearrange("q (gl p) -> q gl p", gl=2),
                                                     in_=ld2[:].unsqueeze(2).to_broadcast([32, 2, 64])),
                      reads=[d_ld2, d_XA], writes=[d_XA])
                for i in range(4):
                    eng = "dve" if i % 2 == 0 else "pool"
                    kb.op(eng, lambda e: e.memset(pads[i][:].rearrange("p q c -> p (q c)"), 0.0), writes=[d_pad[i]])
                srcB = [self.b_re[l], self.b_im[l]]
                srcC = [self.c_re[l], self.c_im[l]]
                for k in range(4):
                    for gl in range(2):
                        for i in range(2):
                            dstb = pads[i][gl * 64:(gl + 1) * 64, :, :].rearrange("p (ct k) c -> p k ct c", k=4)[:, k, :, 32 * k + 16 * gl:32 * k + 16 * gl + 16]
                            sb_ = srcB[i].rearrange("(ct k gl) p c -> k gl p ct c", k=4, gl=2)[k, gl]
                            kb.dma("sp", dstb, sb_, reads=[d_pad[i]], writes=[d_pad[i]])
                            dstc = pads[2 + i][32 * k + 16 * gl:32 * k + 16 * gl + 16, :, :].rearrange("p (ct k) c -> p k ct c", k=4)[:, k, :, gl * 64:(gl + 1) * 64]
                            sc_ = srcC[i].rearrange("(ct k gl) c p -> k gl c ct p", k=4, gl=2)[k, gl]
                            kb.dma("sp", dstc, sc_, reads=[d_pad[2 + i]], writes=[d_pad[2 + i]])
                for j in range(3):
                    kb.op("pe", lambda e: e.transpose(out=ps[0][:, j * 32:(j + 1) * 32], in_=XA[:, j, :], identity=self.identf[0:32, 0:32]),
                          reads=[d_XA, self.d_const], writes=[dps[0]])
                kb.op("pe", lambda e: e.transpose(out=ps[0][:, 96:104], in_=XD[:], identity=self.identf[0:8, 0:8]),
                      reads=[d_XD, self.d_const], writes=[dps[0]])
                kb.op("dve", lambda e: e.tensor_copy(out=prm[:, 0:3, :].rearrange("p a q -> p (a q)"), in_=ps[0][:, 0:96]), reads=[dps[0]], writes=[d_prm])
                kb.op("dve", lambda e: e.tensor_copy(out=Dt[:], in_=ps[0][:, 96:104]), reads=[dps[0]], writes=[d_Dt])
                R_, W_ = [d_prm], [d_prm]
                tt = lambda o, a, b, op: kb.op("dve", lambda e: e.tensor_tensor(out=P(o), in0=P(a), in1=P(b), op=op), reads=R_, writes=W_)
                kb.op("act", lambda e: e.activation(out=P(DTT), in_=P(LDT), func=AF.Exp), reads=R_, writes=W_)
                tt(T0, DTT, AR, ALU.mult)
                kb.op("act", lambda e: e.activation(out=P(MM), in_=P(T0), func=AF.Exp), reads=R_, writes=W_)
                tt(T0, DTT, AI, ALU.mult)
                kb.op("dve", lambda e: e.tensor_scalar(out=P(T1), in0=P(T0), scalar1=1.0 / (2.0 * math.pi), scalar2=None, op0=ALU.mult), reads=R_, writes=W_)
                kb.op("dve", lambda e: e.tensor_copy(out=prmi[:], in_=P(T1)), reads=R_, writes=W_)
                kb.op("dve", lambda e: e.tensor_tensor(out=P(PHI), in0=P(T1), in1=prmi[:], op=ALU.subtract), reads=R_, writes=W_)
                self.sincos_turns(P(PHI), P(COS), P(SIN), P(T2), prmi[:], P(T3), d_prm, d_prm, d_prm, d_prm)
                kb.op("dve", lambda e: e.tensor_scalar(out=P(T4), in0=P(PHI), scalar1=512.0, scalar2=None, op0=ALU.mult), reads=R_, writes=W_)
                self.sincos_turns(P(T4), P(C512), P(S512), P(T2), prmi[:], P(T3), d_prm, d_prm, d_prm, d_prm)
                tt(T0, MM, COS, ALU.mult)
                tt(T1, MM, SIN, ALU.mult)
                kb.op("dve", lambda e: e.tensor_scalar(out=P(T0), in0=P(T0), scalar1=-1.0, scalar2=None, op0=ALU.add), reads=R_, writes=W_)
                tt(T2, AR, AR, ALU.mult)
                tt(T3, AI, AI, ALU.mult)
                tt(T2, T2, T3, ALU.add)
                kb.op("dve", lambda e: e.reciprocal(out=P(T2), in_=P(T2)), reads=R_, writes=W_)
                tt(T3, T0, AR, ALU.mult)
                tt(T4, T1, AI, ALU.mult)
                tt(T3, T3, T4, ALU.add)
                tt(FR, T3, T2, ALU.mult)
                tt(T3, T1, AR, ALU.mult)
                tt(T4, T0, AI, ALU.mult)
                tt(T3, T3, T4, ALU.subtract)
                tt(FI, T3, T2, ALU.mult)
                ctmp = [sb2("ctmp%d" % i, [128, 4, 128], F32) for i in range(4)]
                d_ctmp = kb.dep("ctmp")
                for q4 in range(8):
                    for i in range(4):
                        bank = 4 + i
                        for k in range(4):
                            q = q4 * 4 + k
                            kb.op("pe", lambda e: e.transpose(out=ps[bank][:, k * 128:(k + 1) * 128], in_=pads[i][:, q, :], identity=self.identf[:]),
                                  reads=[d_pad[i], self.d_const], writes=[dps[bank]])
                    for i in range(2):
                        kb.op("act", lambda e: e.activation(out=BT[i][:, q4 * 4:(q4 + 1) * 4, :].rearrange("p a b -> p (a b)"), in_=ps[4 + i][:], func=AF.Copy),
                              reads=[dps[4 + i]], writes=[d_BT])
                    frb = prm[:, FR, q4 * 4:(q4 + 1) * 4].unsqueeze(2).to_broadcast([128, 4, 128])
                    fib = prm[:, FI, q4 * 4:(q4 + 1) * 4].unsqueeze(2).to_broadcast([128, 4, 128])
                    crp = ps[6][:].rearrange("p (a b) -> p a b", a=4)
                    cip = ps[7][:].rearrange("p (a b) -> p a b", a=4)
                    tmpw = [d_ctmp]
                    kb.op("dve", lambda e: e.tensor_tensor(out=ctmp[0][:], in0=crp, in1=frb, op=ALU.mult), reads=[dps[6], d_prm], writes=tmpw)
                    kb.op("dve", lambda e: e.tensor_tensor(out=ctmp[1][:], in0=cip, in1=fib, op=ALU.mult), reads=[dps[7], d_prm], writes=tmpw)
                    kb.op("dve", lambda e: e.tensor_tensor(out=CT[0][:, q4 * 4:(q4 + 1) * 4, :], in0=ctmp[0][:], in1=ctmp[1][:], op=ALU.subtract),
                          reads=tmpw, writes=[d_CT])
                    kb.op("dve", lambda e: e.tensor_tensor(out=ctmp[2][:], in0=crp, in1=fib, op=ALU.mult), reads=[dps[6], d_prm], writes=tmpw)
                    kb.op("dve", lambda e: e.tensor_tensor(out=ctmp[3][:], in0=cip, in1=frb, op=ALU.mult), reads=[dps[7], d_prm], writes=tmpw)
                    kb.op("dve", lambda e: e.scalar_tensor_tensor(out=CT[1][:, q4 * 4:(q4 + 1) * 4, :], in0=ctmp[2][:], scalar=-1.0, in1=ctmp[3][:],
                                                                  op0=ALU.mult, op1=ALU.subtract), reads=tmpw, writes=[d_CT])
                kb.barrier()
            y5T = sb("y5T", [128, 8, S], BF16)
            cosT = [sb("cosT%d" % i, [128, 4, 512], BF16) for i in range(2)]
            sinT = [sb("sinT%d" % i, [128, 4, 512], BF16) for i in range(2)]
            iota = sb("iota", [128, 512], F32)
            angi = sb("angi", [128, 512], I32)
            uT = [sb("uT%d" % i, [128, 512], BF16) for i in range(3)]
            tf = [sb("tf%d" % i, [128, 512], F32) for i in range(3)]
            tb_ = [sb("tb%d" % i, [128, 512], BF16) for i in range(6)]
            rb_ = [sb("rb%d" % i, [128, 512], BF16) for i in range(4)]
            BuS = [[sb("BuS%d%d" % (i, j), [128, 512], BF16) for j in range(2)] for i in range(2)]
            zf = [[sb("zf%d%d" % (i, j), [128, 512], F32) for j in range(2)] for i in range(2)]
            zb = [[sb("zb%d%d" % (i, j), [128, 512], BF16) for j in range(2)] for i in range(2)]
            X = [[sb("X%d%d" % (i, j), [128, 512], BF16) for j in range(2)] for i in range(2)]
            cz = [sb("cz%d" % i, [128, 2, 32], F32) for i in range(2)]
            czt = sb("czt", [128, 2], F32)
            d_y5 = kb.deps_n(8, "y5")
            d_cos = kb.deps_n(2, "cos")
            d_sin = kb.deps_n(2, "sin")
            d_iota, d_angi, d_czt = kb.deps_n(3, "tab")
            d_uT = kb.deps_n(3, "uT")
            d_tf = kb.deps_n(3, "tf")
            d_tb = kb.deps_n(6, "tb")
            d_rb = kb.deps_n(4, "rb")
            d_BuS = [kb.deps_n(2) for i in range(2)]
            d_zf = [kb.deps_n(2) for i in range(2)]
            d_zb = [kb.deps_n(2) for i in range(2)]
            d_X = [kb.deps_n(2) for i in range(2)]
            d_cz = kb.deps_n(2, "cz")
            kb.dma("sp", iota[:], self.c_iota, writes=[d_iota])
            t1, t2, t3, t4, wr, wi = tb_
            dt1, dt2, dt3, dt4, dwr, dwi = d_tb
            r1, r2, r3, r4 = rb_
            dr1, dr2, dr3, dr4 = d_rb

            def TT(eng, o, do, a, da, b_, db, op):
                kb.op(eng, lambda e: e.tensor_tensor(out=o, in0=a, in1=b_, op=op), reads=da + db, writes=[do])

            pending = []
            it = 0
            icb = 0
            for ct in range(8):
                tbi = ct % 2
                for k in range(4):
                    q = ct * 4 + k
                    kb.op("dve", lambda e: e.tensor_scalar(out=tf[0][:], in0=iota[:], scalar1=prm[:, PHI, q:q + 1], scalar2=None, op0=ALU.mult),
                          reads=[d_iota, d_prm], writes=[d_tf[0]])
                    self.sincos_turns(tf[0][:], cosT[tbi][:, k, :], sinT[tbi][:, k, :], tf[1][:], angi[:], tf[2][:], d_tf[0], d_cos[tbi], d_sin[tbi], d_tf[1])
                kb.op("dve", lambda e: e.memset(cz[0][:].rearrange("p a q -> p (a q)"), 0.0), writes=[d_cz[0]])
                for tb in range(8):
                    ui = icb % 3
                    ybank = 4 + (icb % 2)
                    icb += 1
                    par = tb % 2
                    kb.dma("sp", uT[ui][:], self.sT[ct * 128:(ct + 1) * 128, tb * 512:(tb + 1) * 512], writes=[d_uT[ui]])
                    for k in range(4):
                        q = ct * 4 + k
                        sset = it % 2
                        it += 1
                        c = cosT[tbi][:, k, :]
                        s_ = sinT[tbi][:, k, :]
                        dc, ds = [d_cos[tbi]], [d_sin[tbi]]
                        for i in range(2):
                            kb.op("pe", lambda e: e.matmul(ps[2 * sset + i][:], lhsT=BT[i][:, q, :], rhs=uT[ui][:], start=True, stop=True),
                                  reads=[d_BT, d_uT[ui]], writes=[dps[2 * sset + i]])
                            kb.op("act", lambda e: e.activation(out=BuS[sset][i][:], in_=ps[2 * sset + i][:], func=AF.Copy),
                                  reads=[dps[2 * sset + i]], writes=[d_BuS[sset][i]])
                        Br, Bi = BuS[sset][0][:], BuS[sset][1][:]
                        dBr, dBi = [d_BuS[sset][0]], [d_BuS[sset][1]]
                        TT("dve", t1[:], dt1, Br, dBr, c, dc, ALU.mult)
                        TT("dve", t2[:], dt2, Bi, dBi, s_, ds, ALU.mult)
                        TT("dve", wr[:], dwr, t1[:], [dt1], t2[:], [dt2], ALU.add)
                        TT("dve", t3[:], dt3, Bi, dBi, c, dc, ALU.mult)
                        TT("dve", t4[:], dt4, Br, dBr, s_, ds, ALU.mult)
                        TT("dve", wi[:], dwi, t3[:], [dt3], t4[:], [dt4], ALU.subtract)
                        mb = prm[:, MM, q:q + 1].to_broadcast([128, 512])
                        zr, zi = zf[sset][0], zf[sset][1]
                        dzr, dzi = d_zf[sset][0], d_zf[sset][1]
                        kb.op("dve", lambda e: e.tensor_tensor_scan(out=zr[:], data0=mb, data1=wr[:], initial=cz[par][:, 0, q:q + 1], op0=ALU.mult, op1=ALU.add),
                              reads=[d_prm, dwr, d_cz[par]], writes=[dzr])
                        kb.op("dve", lambda e: e.tensor_tensor_scan(out=zi[:], data0=mb, data1=wi[:], initial=cz[par][:, 1, q:q + 1], op0=ALU.mult, op1=ALU.add),
                              reads=[d_prm, dwi, d_cz[par]], writes=[dzi])
                        for i in range(2):
                            kb.op("act", lambda e: e.activation(out=zb[sset][i][:], in_=zf[sset][i][:], func=AF.Copy),
                                  reads=[d_zf[sset][i]], writes=[d_zb[sset][i]])
                        zr_l, zi_l = zr[:, 511:512], zi[:, 511:512]
                        c5, s5 = prm[:, C512, q:q + 1], prm[:, S512, q:q + 1]
                        nx = 1 - par
                        kb.op("dve", lambda e: e.tensor_scalar(out=czt[:, 0:1], in0=zi_l, scalar1=s5, scalar2=None, op0=ALU.mult), reads=[dzi, d_prm], writes=[d_czt])
                        kb.op("dve", lambda e: e.scalar_tensor_tensor(out=cz[nx][:, 0, q:q + 1], in0=zr_l, scalar=c5, in1=czt[:, 0:1], op0=ALU.mult, op1=ALU.subtract),
                              reads=[dzr, d_prm, d_czt], writes=[d_cz[nx]])
                        kb.op("dve", lambda e: e.tensor_scalar(out=czt[:, 1:2], in0=zi_l, scalar1=c5, scalar2=None, op0=ALU.mult), reads=[dzi, d_prm], writes=[d_czt])
                        kb.op("dve", lambda e: e.scalar_tensor_tensor(out=cz[nx][:, 1, q:q + 1], in0=zr_l, scalar=s5, in1=czt[:, 1:2], op0=ALU.mult, op1=ALU.add),
                              reads=[dzr, d_prm, d_czt], writes=[d_cz[nx]])

                        def back(sset=sset, c=c, s_=s_, dc=dc, ds=ds, q=q, k=k, ct=ct, tb=tb, ui=ui, ybank=ybank):
                            zbr, zbi = zb[sset][0][:], zb[sset][1][:]
                            dzbr, dzbi = [d_zb[sset][0]], [d_zb[sset][1]]
                            TT("pool", r1[:], dr1, zbr, dzbr, c, dc, ALU.mult)
                            TT("pool", r2[:], dr2, zbi, dzbi, s_, ds, ALU.mult)
                            TT("pool", X[sset][0][:], d_X[sset][0], r1[:], [dr1], r2[:], [dr2], ALU.subtract)
                            TT("dve", r3[:], dr3, zbr, dzbr, s_, ds, ALU.mult)
                            TT("dve", r4[:], dr4, zbi, dzbi, c, dc, ALU.mult)
                            TT("dve", X[sset][1][:], d_X[sset][1], r3[:], [dr3], r4[:], [dr4], ALU.add)
                            for i in range(2):
                                kb.op("pe", lambda e: e.matmul(ps[ybank][:], lhsT=CT[i][:, q, :], rhs=X[sset][i][:], start=(k == 0 and i == 0), stop=(k == 3 and i == 1)),
                                      reads=[d_CT, d_X[sset][i]], writes=[dps[ybank]])
                            if k == 3:
                                kb.op("dve", lambda e: e.scalar_tensor_tensor(out=tf[0][:], in0=uT[ui][:], scalar=Dt[:, ct:ct + 1], in1=ps[ybank][:], op0=ALU.mult, op1=ALU.add),
                                      reads=[d_uT[ui], d_Dt, dps[ybank]], writes=[d_tf[0]])
                                kb.op("act", lambda e: e.activation(out=tf[1][:], in_=tf[0][:], func=AF.Square), reads=[d_tf[0]], writes=[d_tf[1]])
                                kb.op("pool", lambda e: e.tensor_scalar(out=tf[1][:], in0=tf[1][:], scalar1=0.044715, scalar2=1.0, op0=ALU.mult, op1=ALU.add),
                                      reads=[d_tf[1]], writes=[d_tf[1]])
                                kb.op("pool", lambda e: e.tensor_tensor(out=tf[1][:], in0=tf[1][:], in1=tf[0][:], op=ALU.mult), reads=[d_tf[1], d_tf[0]], writes=[d_tf[1]])
                                kb.op("act", lambda e: e.activation(out=tf[2][:], in_=tf[1][:], func=AF.Sigmoid, scale=1.5957691216057308), reads=[d_tf[1]], writes=[d_tf[2]])
                                kb.op("pool", lambda e: e.tensor_tensor(out=y5T[:, ct, tb * 512:(tb + 1) * 512], in0=tf[0][:], in1=tf[2][:], op=ALU.mult),
                                      reads=[d_tf[0], d_tf[2]], writes=[d_y5[tb]])

                        if pending:
                            pending.pop(0)()
                        pending.append(back)
            while pending:
                pending.pop(0)()
            szT = [sb("szT%d" % i, [128, 512], BF16) for i in range(2)]
            og = [sb("og%d" % i, [128, 512], BF16) for i in range(2)]
            d_sz = kb.deps_n(2, "sz")
            d_og = kb.deps_n(2, "og")
            ig = 0
            for tb in range(8):
                for co in range(8):
                    i = ig % 2
                    ig += 1
                    bank = 4 + (ig % 4)
                    kb.dma("sp", szT[i][:], self.sT[1024 + co * 128:1024 + (co + 1) * 128, tb * 512:(tb + 1) * 512], writes=[d_sz[i]])
                    for ci in range(8):
                        kb.op("pe", lambda e: e.matmul(ps[bank][:], lhsT=gluw[:, ci, co * 128:(co + 1) * 128], rhs=y5T[:, ci, tb * 512:(tb + 1) * 512],
                                                       start=(ci == 0), stop=(ci == 7)), reads=[d_glu, d_y5[tb]], writes=[dps[bank]])
                    kb.op("act", lambda e: e.activation(out=tf[0][:], in_=ps[bank][:], func=AF.Sigmoid), reads=[dps[bank]], writes=[d_tf[0]])
                    kb.op("act", lambda e: e.activation(out=tf[1][:], in_=szT[i][:], func=AF.Silu), reads=[d_sz[i]], writes=[d_tf[1]])
                    kb.op("dve", lambda e: e.tensor_tensor(out=tf[0][:], in0=tf[0][:], in1=y5T[:, co, tb * 512:(tb + 1) * 512], op=ALU.mult),
                          reads=[d_tf[0], d_y5[tb]], writes=[d_tf[0]])
                    kb.op("dve", lambda e: e.tensor_tensor(out=og[i][:], in0=tf[0][:], in1=tf[1][:], op=ALU.mult), reads=[d_tf[0], d_tf[1]], writes=[d_og[i]])
                    kb.dma("sp", self.mixedT[1024 + co * 128:1024 + (co + 1) * 128, tb * 512:(tb + 1) * 512], og[i][:], reads=[d_og[i]])

    def phase_S5(self, l):
        nc, kb = self.nc, self.kb
        ps, dps = self.ps, self.dps
        Lc = 4
        NCH = S // Lc
        NH = NCH // 512
        with ExitStack() as st:
            sb = lambda n, s, d: st.enter_context(self.sbt("S_" + n, s, d))
            CT = [sb("CT%d" % i, [128, 32, 128], BF16) for i in range(2)]
            Bp = [sb("Bp%d" % i, [128, 32, 128], BF16) for i in range(2)]
            prm = sb("prm", [128, 24, 32], F32)
            apw = sb("apw", [128, 9, 2, 32], F32)
            prmi = sb("prmi", [128, 32], I32)
            Dt = sb("Dt", [128, 8], F32)
            d_CT, d_prm, d_Dt, d_apw = kb.deps_n(4, "s5c")
            d_Bp = kb.deps_n(2, "Bp")
            AR, AI, LDT, DTT, MM, PHI, COS, SIN, FR, FI, M8, PHI8, C512, S512, T0, T1, T2, T3, T4, T5 = range(20)
            P = lambda i: prm[:, i, :]
            with ExitStack() as st2:
                sb2 = lambda n, s, d: st2.enter_context(self.sbt("S2_" + n, s, d))
                XA = sb2("XA", [32, 3, 128], F32)
                ld2 = sb2("ld2", [32, 2], F32)
                XD = sb2("XD", [8, 128], F32)
                Cp = [sb2("Cp%d" % i, [128, 32, 128], F32) for i in range(2)]
                Bf = [sb2("Bf%d" % i, [128, 32, 128], F32) for i in range(2)]
                pads = [Bf[0], Bf[1], Cp[0], Cp[1]]
                d_XA, d_ld2, d_XD = kb.deps_n(3, "xa")
                d_Cp = kb.deps_n(2, "Cp")
                d_Bf = kb.deps_n(2, "Bf")
                d_pad = [d_Bf[0], d_Bf[1], d_Cp[0], d_Cp[1]]
                kb.dma("sp", XA[:, 0, :], self.a_re[l].rearrange("(q gl) p -> q (gl p)", gl=2), writes=[d_XA])
                kb.dma("sp", XA[:, 1, :], self.a_im[l].rearrange("(q gl) p -> q (gl p)", gl=2), writes=[d_XA])
                kb.dma("sp", ld2[:], self.log_dt[l:l + 1, :].rearrange("o (q gl) -> (o q) gl", gl=2), writes=[d_ld2])
                kb.dma("sp", XD[:], self.s5_d[l:l + 1, :].rearrange("o (c p) -> (o c) p", p=128), writes=[d_XD])
                kb.op("dve", lambda e: e.tensor_copy(out=XA[:, 2, :].rearrange("q (gl p) -> q gl p", gl=2),
                                                     in_=ld2[:].unsqueeze(2).to_broadcast([32, 2, 64])),
                      reads=[d_ld2, d_XA], writes=[d_XA])
                for i in range(4):
                    eng = "dve" if i % 2 == 0 else "pool"
                    kb.op(eng, lambda e: e.memset(pads[i][:].rearrange("p q c -> p (q c)"), 0.0), writes=[d_pad[i]])
                srcB = [self.b_re[l], self.b_im[l]]
                srcC = [self.c_re[l], self.c_im[l]]
                for k in range(4):
                    for gl in range(2):
                        for i in range(2):
                            dstb = pads[i][gl * 64:(gl + 1) * 64, :, :].rearrange("p (ct k) c -> p k ct c", k=4)[:, k, :, 32 * k + 16 * gl:32 * k + 16 * gl + 16]
                            sb_ = srcB[i].rearrange("(ct k gl) p c -> k gl p ct c", k=4, gl=2)[k, gl]
                            kb.dma("sp", dstb, sb_, reads=[d_pad[i]], writes=[d_pad[i]])
                            dstc = pads[2 + i][32 * k + 16 * gl:32 * k + 16 * gl + 16, :, :].rearrange("p (ct k) c -> p k ct c", k=4)[:, k, :, gl * 64:(gl + 1) * 64]
                            sc_ = srcC[i].rearrange("(ct k gl) c p -> k gl c ct p", k=4, gl=2)[k, gl]
                            kb.dma("sp", dstc, sc_, reads=[d_pad[2 + i]], writes=[d_pad[2 + i]])
                for j in range(3):
                    kb.op("pe", lambda e: e.transpose(out=ps[0][:, j * 32:(j + 1) * 32], in_=XA[:, j, :], identity=self.identf[0:32, 0:32]),
                          reads=[d_XA, self.d_const], writes=[dps[0]])
                kb.op("pe", lambda e: e.transpose(out=ps[0][:, 96:104], in_=XD[:], identity=self.identf[0:8, 0:8]),
                      reads=[d_XD, self.d_const], writes=[dps[0]])
                kb.op("dve", lambda e: e.tensor_copy(out=prm[:, 0:3, :].rearrange("p a q -> p (a q)"), in_=ps[0][:, 0:96]), reads=[dps[0]], writes=[d_prm])
                kb.op("dve", lambda e: e.tensor_copy(out=Dt[:], in_=ps[0][:, 96:104]), reads=[dps[0]], writes=[d_Dt])
                kb.op("act", lambda e: e.activation(out=Bp[0][:].rearrange("p q c -> p (q c)"), in_=Bf[0][:].rearrange("p q c -> p (q c)"), func=AF.Copy),
                      reads=[d_Bf[0]], writes=[d_Bp[0]])
                kb.op("pool", lambda e: e.tensor_copy(out=Bp[1][:].rearrange("p q c -> p (q c)"), in_=Bf[1][:].rearrange("p q c -> p (q c)")),
                      reads=[d_Bf[1]], writes=[d_Bp[1]])
                R_, W_ = [d_prm], [d_prm]
                tt = lambda o, a, b, op: kb.op("dve", lambda e: e.tensor_tensor(out=P(o), in0=P(a), in1=P(b), op=op), reads=R_, writes=W_)
                kb.op("act", lambda e: e.activation(out=P(DTT), in_=P(LDT), func=AF.Exp), reads=R_, writes=W_)
                tt(T0, DTT, AR, ALU.mult)
                kb.op("act", lambda e: e.activation(out=P(MM), in_=P(T0), func=AF.Exp), reads=R_, writes=W_)
                kb.op("act", lambda e: e.activation(out=P(M8), in_=P(T0), func=AF.Exp, scale=float(Lc)), reads=R_, writes=W_)
                tt(T0, DTT, AI, ALU.mult)
                kb.op("dve", lambda e: e.tensor_scalar(out=P(T1), in0=P(T0), scalar1=1.0 / (2.0 * math.pi), scalar2=None, op0=ALU.mult), reads=R_, writes=W_)
                kb.op("dve", lambda e: e.tensor_copy(out=prmi[:], in_=P(T1)), reads=R_, writes=W_)
                kb.op("dve", lambda e: e.tensor_tensor(out=P(PHI), in0=P(T1), in1=prmi[:], op=ALU.subtract), reads=R_, writes=W_)
                self.sincos_turns(P(PHI), P(COS), P(SIN), P(T2), prmi[:], P(T3), d_prm, d_prm, d_prm, d_prm)
                kb.op("dve", lambda e: e.tensor_scalar(out=P(T4), in0=P(PHI), scalar1=float(Lc), scalar2=None, op0=ALU.mult), reads=R_, writes=W_)
                kb.op("dve", lambda e: e.tensor_copy(out=prmi[:], in_=P(T4)), reads=R_, writes=W_)
                kb.op("dve", lambda e: e.tensor_tensor(out=P(PHI8), in0=P(T4), in1=prmi[:], op=ALU.subtract), reads=R_, writes=W_)
                kb.op("dve", lambda e: e.tensor_scalar(out=P(T4), in0=P(PHI8), scalar1=512.0, scalar2=None, op0=ALU.mult), reads=R_, writes=W_)
                self.sincos_turns(P(T4), P(C512), P(S512), P(T2), prmi[:], P(T3), d_prm, d_prm, d_prm, d_prm)
                tt(T0, MM, COS, ALU.mult)
                tt(T1, MM, SIN, ALU.mult)
                RW = [d_prm, d_apw]
                kb.op("dve", lambda e: e.memset(apw[:, 0, 0, :], 1.0), reads=RW, writes=[d_apw])
                kb.op("dve", lambda e: e.memset(apw[:, 0, 1, :], 0.0), reads=RW, writes=[d_apw])
                kb.op("dve", lambda e: e.tensor_copy(out=apw[:, 1, 0, :], in_=P(T0)), reads=RW, writes=[d_apw])
                kb.op("dve", lambda e: e.tensor_copy(out=apw[:, 1, 1, :], in_=P(T1)), reads=RW, writes=[d_apw])
                for m in range(1, Lc):
                    ar_, ai_ = apw[:, m, 0, :], apw[:, m, 1, :]
                    kb.op("dve", lambda e: e.tensor_tensor(out=P(T2), in0=ar_, in1=P(T0), op=ALU.mult), reads=RW, writes=W_)
                    kb.op("dve", lambda e: e.tensor_tensor(out=P(T3), in0=ai_, in1=P(T1), op=ALU.mult), reads=RW, writes=W_)
                    kb.op("dve", lambda e: e.tensor_tensor(out=apw[:, m + 1, 0, :], in0=P(T2), in1=P(T3), op=ALU.subtract), reads=RW, writes=[d_apw])
                    kb.op("dve", lambda e: e.tensor_tensor(out=P(T2), in0=ar_, in1=P(T1), op=ALU.mult), reads=RW, writes=W_)
                    kb.op("dve", lambda e: e.tensor_tensor(out=P(T3), in0=ai_, in1=P(T0), op=ALU.mult), reads=RW, writes=W_)
                    kb.op("dve", lambda e: e.tensor_tensor(out=apw[:, m + 1, 1, :], in0=P(T2), in1=P(T3), op=ALU.add), reads=RW, writes=[d_apw])
                kb.op("dve", lambda e: e.tensor_scalar(out=P(T0), in0=P(T0), scalar1=-1.0, scalar2=None, op0=ALU.add), reads=R_, writes=W_)
                tt(T2, AR, AR, ALU.mult)
                tt(T3, AI, AI, ALU.mult)
                tt(T2, T2, T3, ALU.add)
                kb.op("dve", lambda e: e.reciprocal(out=P(T2), in_=P(T2)), reads=R_, writes=W_)
                tt(T3, T0, AR, ALU.mult)
                tt(T4, T1, AI, ALU.mult)
                tt(T3, T3, T4, ALU.add)
                tt(FR, T3, T2, ALU.mult)
                tt(T3, T1, AR, ALU.mult)
                tt(T4, T0, AI, ALU.mult)
                tt(T3, T3, T4, ALU.subtract)
                tt(FI, T3, T2, ALU.mult)
                ctmp = [sb2("ctmp%d" % i, [128, 4, 128], F32) for i in range(4)]
                d_ctmp = kb.dep("ctmp")
                for q4 in range(8):
                    for i in range(2):
                        bank = 6 + i
                        for k in range(4):
                            q = q4 * 4 + k
                            kb.op("pe", lambda e: e.transpose(out=ps[bank][:, k * 128:(k + 1) * 128], in_=Cp[i][:, q, :], identity=self.identf[:]),
                                  reads=[d_Cp[i], self.d_const], writes=[dps[bank]])
                    frb = prm[:, FR, q4 * 4:(q4 + 1) * 4].unsqueeze(2).to_broadcast([128, 4, 128])
                    fib = prm[:, FI, q4 * 4:(q4 + 1) * 4].unsqueeze(2).to_broadcast([128, 4, 128])
                    crp = ps[6][:].rearrange("p (a b) -> p a b", a=4)
                    cip = ps[7][:].rearrange("p (a b) -> p a b", a=4)
                    tmpw = [d_ctmp]
                    kb.op("dve", lambda e: e.tensor_tensor(out=ctmp[0][:], in0=crp, in1=frb, op=ALU.mult), reads=[dps[6], d_prm], writes=tmpw)
                    kb.op("dve", lambda e: e.tensor_tensor(out=ctmp[1][:], in0=cip, in1=fib, op=ALU.mult), reads=[dps[7], d_prm], writes=tmpw)
                    kb.op("dve", lambda e: e.tensor_tensor(out=CT[0][:, q4 * 4:(q4 + 1) * 4, :], in0=ctmp[0][:], in1=ctmp[1][:], op=ALU.subtract),
                          reads=tmpw, writes=[d_CT])
                    kb.op("dve", lambda e: e.tensor_tensor(out=ctmp[2][:], in0=crp, in1=fib, op=ALU.mult), reads=[dps[6], d_prm], writes=tmpw)
                    kb.op("dve", lambda e: e.tensor_tensor(out=ctmp[3][:], in0=cip, in1=frb, op=ALU.mult), reads=[dps[7], d_prm], writes=tmpw)
                    kb.op("dve", lambda e: e.scalar_tensor_tensor(out=CT[1][:, q4 * 4:(q4 + 1) * 4, :], in0=ctmp[2][:], scalar=-1.0, in1=ctmp[3][:],
                                                                  op0=ALU.mult, op1=ALU.subtract), reads=tmpw, writes=[d_CT])
                kb.barrier()
            with ExitStack() as st3:
                sb3 = lambda n, s, d: st3.enter_context(self.sbt("S3_" + n, s, d))
                cosT = [sb3("cosT%d" % i, [128, 4, 512], BF16) for i in range(2)]
                sinT = [sb3("sinT%d" % i, [128, 4, 512], BF16) for i in range(2)]
                iota = sb3("iota", [128, 512], F32)
                angi = sb3("angi", [128, 512], I32)
                zero = sb3("zero", [128, 2], F32)
                czc = [sb3("czc%d" % i, [128, 2, 4], F32) for i in range(2)]
                czt = sb3("czt", [128, 2], F32)
                zl = sb3("zlast", [128, 2], F32)
                d_czc = kb.deps_n(2, "czc")
                d_czt = kb.dep("czt")
                d_zl = kb.dep("zl")
                uTf = [sb3("uTf%d" % i, [128, S], BF16) for i in range(2)]
                W1 = sb3("W1", [128, Lc, 8, 128], BF16)
                CA = [sb3("CA%d" % i, [128, Lc, 8, 128], BF16) for i in range(2)]
                SN = [sb3("SN%d" % i, [128, 8, 128], BF16) for i in range(2)]
                KT = [sb3("KT%d" % i, [128, Lc, 128], BF16) for i in range(2)]
                uu = [sb3("uu%d" % i, [128, 4, 128], F32) for i in range(4)]
                Xs = [[[sb3("Xs%d%d%d" % (c_, k, i), [128, NCH + 2], BF16) for i in range(2)] for k in range(4)] for c_ in range(2)]
                tf = [sb3("tf%d" % i, [128, 512], F32) for i in range(3)]
                tg = [sb3("tg%d" % i, [128, 512], F32) for i in range(3)]
                tb_ = [sb3("tb%d" % i, [128, 512], BF16) for i in range(6)]
                rb_ = [sb3("rb%d" % i, [128, 512], BF16) for i in range(4)]
                BuS = [[sb3("BuS%d%d" % (i, j), [128, 512], BF16) for j in range(2)] for i in range(2)]
                zb = [[sb3("zb%d%d" % (i, j), [128, 512], BF16) for j in range(2)] for i in range(2)]
                y5s = [sb3("y5s%d" % i, [128, 512], BF16) for i in range(2)]
                d_cos = kb.deps_n(2, "cos")
                d_sin = kb.deps_n(2, "sin")
                d_iota, d_angi, d_zero, d_W1 = kb.deps_n(4, "tab")
                d_CA = kb.deps_n(2, "CA")
                d_KT = kb.deps_n(2, "KT")
                d_uTf = kb.deps_n(2, "uTf")
                d_SN = kb.deps_n(2, "SN")
                d_uu = kb.deps_n(4, "uu")
                d_Xs = [[kb.deps_n(2) for k in range(4)] for c_ in range(2)]
                d_tf = kb.deps_n(3, "tf")
                d_tg = kb.deps_n(3, "tg")
                d_tb = kb.deps_n(6, "tb")
                d_rb = kb.deps_n(4, "rb")
                d_BuS = [kb.deps_n(2) for i in range(2)]
                d_zb = [kb.deps_n(2) for i in range(2)]
                d_y5s = kb.deps_n(2, "y5s")
                kb.dma("sp", iota[:], self.c_iota, writes=[d_iota])
                kb.op("dve", lambda e: e.memset(zero[:], 0.0), writes=[d_zero])
                for c_ in range(2):
                    for k in range(4):
                        for i in range(2):
                            kb.op("pool", lambda e: e.memset(Xs[c_][k][i][:], 0.0), writes=[d_Xs[c_][k][i]])
                t1, t2, t3, t4, wr, wi = tb_
                dt1, dt2, dt3, dt4, dwr, dwi = d_tb
                r1, r2, r3, r4 = rb_
                dr1, dr2, dr3, dr4 = d_rb

                def TT(eng, o, do, a, da, b_, db, op):
                    kb.op(eng, lambda e: e.tensor_tensor(out=o, in0=a, in1=b_, op=op), reads=da + db, writes=[do])

                def E_slice(ct, step):
                    cp = ct % 2
                    q0 = ct * 4
                    if step == 0:
                        kb.dma("sp", uTf[cp][:], self.sT[ct * 128:(ct + 1) * 128, :], writes=[d_uTf[cp]])
                    if step < 4:
                        k = step
                        q = q0 + k
                        kb.op("dve", lambda e: e.tensor_scalar(out=tg[0][:], in0=iota[:], scalar1=prm[:, PHI8, q:q + 1], scalar2=None, op0=ALU.mult),
                              reads=[d_iota, d_prm], writes=[d_tg[0]])
                        self.sincos_turns(tg[0][:], cosT[cp][:, k, :], sinT[cp][:, k, :], tg[1][:], angi[:], tg[2][:], d_tg[0], d_cos[cp], d_sin[cp], d_tg[1])
                    m = step
                    mi = m % 2
                    Brv = Bp[0][:, q0:q0 + 4, :]
                    Biv = Bp[1][:, q0:q0 + 4, :]
                    Arb = apw[:, m, 0, q0:q0 + 4].unsqueeze(2).to_broadcast([128, 4, 128])
                    Aib = apw[:, m, 1, q0:q0 + 4].unsqueeze(2).to_broadcast([128, 4, 128])
                    TT("dve", uu[0][:], d_uu[0], Brv, [d_Bp[0]], Arb, [d_apw], ALU.mult)
                    TT("pool", uu[1][:], d_uu[1], Biv, [d_Bp[1]], Aib, [d_apw], ALU.mult)
                    TT("dve", SN[mi][:, 0:4, :], d_SN[mi], uu[0][:], [d_uu[0]], uu[1][:], [d_uu[1]], ALU.subtract)
                    TT("pool", uu[2][:], d_uu[2], Biv, [d_Bp[1]], Arb, [d_apw], ALU.mult)
                    TT("dve", uu[3][:], d_uu[3], Brv, [d_Bp[0]], Aib, [d_apw], ALU.mult)
                    TT("pool", SN[mi][:, 4:8, :], d_SN[mi], uu[2][:], [d_uu[2]], uu[3][:], [d_uu[3]], ALU.add)
                    j = step
                    C0v = CT[0][:, q0:q0 + 4, :]
                    C1v = CT[1][:, q0:q0 + 4, :]
                    Arb = apw[:, j + 1, 0, q0:q0 + 4].unsqueeze(2).to_broadcast([128, 4, 128])
                    Aib = apw[:, j + 1, 1, q0:q0 + 4].unsqueeze(2).to_broadcast([128, 4, 128])
                    TT("pool", uu[0][:], d_uu[0], C0v, [d_CT], Arb, [d_apw], ALU.mult)
                    TT("dve", uu[1][:], d_uu[1], C1v, [d_CT], Aib, [d_apw], ALU.mult)
                    TT("pool", CA[cp][:, j, 0:4, :], d_CA[cp], uu[0][:], [d_uu[0]], uu[1][:], [d_uu[1]], ALU.add)
                    TT("dve", uu[2][:], d_uu[2], C1v, [d_CT], Arb, [d_apw], ALU.mult)
                    TT("pool", uu[3][:], d_uu[3], C0v, [d_CT], Aib, [d_apw], ALU.mult)
                    TT("dve", CA[cp][:, j, 4:8, :], d_CA[cp], uu[2][:], [d_uu[2]], uu[3][:], [d_uu[3]], ALU.subtract)

                def T_slice(ct, step):
                    cp = ct % 2
                    q0 = ct * 4
                    m = step
                    mi = m % 2
                    pb = ps[6][:].bitcast(BF16)
                    for j8 in range(8):
                        kb.op("pe", lambda e: e.transpose(out=pb[:, j8 * 128:(j8 + 1) * 128], in_=SN[mi][:, j8, :], identity=self.identb[:]),
                              reads=[d_SN[mi], self.d_const], writes=[dps[6]])
                    kb.op("act", lambda e: e.activation(out=W1[:, m, :, :].rearrange("p a b -> p (a b)"), in_=pb[:, 0:1024], func=AF.Copy),
                          reads=[dps[6]], writes=[d_W1])
                    ksl = ps[7][:, 0:128]
                    for j8 in range(8):
                        i_, k_ = j8 // 4, j8 % 4
                        kb.op("pe", lambda e: e.matmul(ksl, lhsT=SN[mi][:, j8, :], rhs=CT[i_][:, q0 + k_, :], start=(j8 == 0), stop=(j8 == 7)),
                              reads=[d_SN[mi], d_CT], writes=[dps[7]])
                    kb.op("act", lambda e: e.activation(out=KT[cp][:, m, :], in_=ksl, func=AF.Copy), reads=[dps[7]], writes=[d_KT[cp]])

                it_box = [0]

                def H_stage(ct):
                    cp = ct % 2
                    q0 = ct * 4
                    pending = []
                    for hh in range(NH):
                        for k in range(4):
                            q = q0 + k
                            sset = it_box[0] % 2
                            it_box[0] += 1
                            c = cosT[cp][:, k, :]
                            s_ = sinT[cp][:, k, :]
                            dc, ds = [d_cos[cp]], [d_sin[cp]]
                            for i in range(2):
                                bnk = 2 * sset + i
                                for j in range(Lc):
                                    kb.op("pe", lambda e: e.matmul(ps[bnk][:], lhsT=W1[:, Lc - 1 - j, i * 4 + k, :],
                                                                   rhs=uTf[cp][:, hh * 512 * Lc + j:(hh + 1) * 512 * Lc:Lc], start=(j == 0), stop=(j == Lc - 1)),
                                          reads=[d_W1, d_uTf[cp]], writes=[dps[bnk]])
                                kb.op("act", lambda e: e.activation(out=BuS[sset][i][:], in_=ps[bnk][:], func=AF.Copy), reads=[dps[bnk]], writes=[d_BuS[sset][i]])
                            Br, Bi = BuS[sset][0][:], BuS[sset][1][:]
                            dBr, dBi = [d_BuS[sset][0]], [d_BuS[sset][1]]
                            TT("dve", t1[:], dt1, Br, dBr, c, dc, ALU.mult)
                            TT("pool", t2[:], dt2, Bi, dBi, s_, ds, ALU.mult)
                            TT("dve", wr[:], dwr, t1[:], [dt1], t2[:], [dt2], ALU.add)
                            TT("dve", t3[:], dt3, Bi, dBi, c, dc, ALU.mult)
                            TT("pool", t4[:], dt4, Br, dBr, s_, ds, ALU.mult)
                            TT("dve", wi[:], dwi, t3[:], [dt3], t4[:], [dt4], ALU.subtract)
                            mb = prm[:, M8, q:q + 1].to_broadcast([128, 512])
                            par = hh % 2
                            for i, w_, dw_ in ((0, wr, dwr), (1, wi, dwi)):
                                init = zero[:, i:i + 1] if hh == 0 else czc[par][:, i, k:k + 1]
                                rd = [d_prm, dw_, d_zero] if hh == 0 else [d_prm, dw_, d_czc[par]]
                                kb.op("dve", lambda e: e.tensor_tensor_scan(out=zb[sset][i][:], data0=mb, data1=w_[:], initial=init, op0=ALU.mult, op1=ALU.add),
                                      reads=rd, writes=[d_zb[sset][i]])
                            if hh + 1 < NH:
                                nx = 1 - par
                                kb.op("dve", lambda e: e.tensor_copy(out=zl[:, 0:1], in_=zb[sset][0][:, 511:512]), reads=[d_zb[sset][0]], writes=[d_zl])
                                kb.op("dve", lambda e: e.tensor_copy(out=zl[:, 1:2], in_=zb[sset][1][:, 511:512]), reads=[d_zb[sset][1]], writes=[d_zl])
                                c5, s5 = prm[:, C512, q:q + 1], prm[:, S512, q:q + 1]
                                kb.op("dve", lambda e: e.tensor_scalar(out=czt[:, 0:1], in0=zl[:, 1:2], scalar1=s5, scalar2=None, op0=ALU.mult), reads=[d_zl, d_prm], writes=[d_czt])
                                kb.op("dve", lambda e: e.scalar_tensor_tensor(out=czc[nx][:, 0, k:k + 1], in0=zl[:, 0:1], scalar=c5, in1=czt[:, 0:1], op0=ALU.mult, op1=ALU.subtract),
                                      reads=[d_zl, d_prm, d_czt], writes=[d_czc[nx]])
                                kb.op("dve", lambda e: e.tensor_scalar(out=czt[:, 1:2], in0=zl[:, 1:2], scalar1=c5, scalar2=None, op0=ALU.mult), reads=[d_zl, d_prm], writes=[d_czt])
                                kb.op("dve", lambda e: e.scalar_tensor_tensor(out=czc[nx][:, 1, k:k + 1], in0=zl[:, 0:1], scalar=s5, in1=czt[:, 1:2], op0=ALU.mult, op1=ALU.add),
                                      reads=[d_zl, d_prm, d_czt], writes=[d_czc[nx]])

                            def back(sset=sset, c=c, s_=s_, dc=dc, ds=ds, k=k, hh=hh):
                                zbr, zbi = zb[sset][0][:], zb[sset][1][:]
                                dzbr, dzbi = [d_zb[sset][0]], [d_zb[sset][1]]
                                o0 = 1 + hh * 512
                                TT("pool", r1[:], dr1, zbr, dzbr, c, dc, ALU.mult)
                                TT("dve", r2[:], dr2, zbi, dzbi, s_, ds, ALU.mult)
                                TT("pool", Xs[cp][k][0][:, o0:o0 + 512], d_Xs[cp][k][0], r1[:], [dr1], r2[:], [dr2], ALU.subtract)
                                TT("dve", r3[:], dr3, zbr, dzbr, s_, ds, ALU.mult)
                                TT("pool", r4[:], dr4, zbi, dzbi, c, dc, ALU.mult)
                                TT("dve", Xs[cp][k][1][:, o0:o0 + 512], d_Xs[cp][k][1], r3[:], [dr3], r4[:], [dr4], ALU.add)

                            if pending:
                                pending.pop(0)()
                            pending.append(back)
                    while pending:
                        pending.pop(0)()

                iy_box = [0]

                def Y_mm(ct, blk):
                    cp = ct % 2
                    yb = 4 + (iy_box[0] % 2)
                    for j in range(Lc):
                        osl = ps[yb][:, j:512:Lc]
                        nck = 512 // Lc
                        for tau in range(j + 1):
                            kb.op("pe", lambda e: e.matmul(osl, lhsT=KT[cp][:, tau, :], rhs=uTf[cp][:, blk * 512 + j - tau:blk * 512 + 512:Lc], start=(tau == 0), stop=False),
                                  reads=[d_KT[cp], d_uTf[cp]], writes=[dps[yb]])
                        for k in range(4):
                            for i in range(2):
                                kb.op("pe", lambda e: e.matmul(osl, lhsT=CA[cp][:, j, i * 4 + k, :], rhs=Xs[cp][k][i][:, blk * nck:(blk + 1) * nck], start=False, stop=(k == 3 and i == 1)),
                                      reads=[d_CA[cp], d_Xs[cp][k][i]], writes=[dps[yb]])

                def Y_epi(ct, blk):
                    cp = ct % 2
                    yb = 4 + (iy_box[0] % 2)
                    yi = iy_box[0] % 2
                    iy_box[0] += 1
                    kb.op("dve", lambda e: e.scalar_tensor_tensor(out=tf[0][:], in0=uTf[cp][:, blk * 512:(blk + 1) * 512], scalar=Dt[:, ct:ct + 1], in1=ps[yb][:],
                                                                  op0=ALU.mult, op1=ALU.add), reads=[d_uTf[cp], d_Dt, dps[yb]], writes=[d_tf[0]])
                    kb.op("act", lambda e: e.activation(out=tf[1][:], in_=tf[0][:], func=AF.Square), reads=[d_tf[0]], writes=[d_tf[1]])
                    kb.op("pool", lambda e: e.tensor_scalar(out=tf[1][:], in0=tf[1][:], scalar1=0.044715, scalar2=1.0, op0=ALU.mult, op1=ALU.add),
                          reads=[d_tf[1]], writes=[d_tf[1]])
                    kb.op("dve", lambda e: e.tensor_tensor(out=tf[1][:], in0=tf[1][:], in1=tf[0][:], op=ALU.mult), reads=[d_tf[1], d_tf[0]], writes=[d_tf[1]])
                    kb.op("act", lambda e: e.activation(out=tf[2][:], in_=tf[1][:], func=AF.Sigmoid, scale=1.5957691216057308), reads=[d_tf[1]], writes=[d_tf[2]])
                    kb.op("dve", lambda e: e.tensor_tensor(out=y5s[yi][:], in0=tf[0][:], in1=tf[2][:], op=ALU.mult), reads=[d_tf[0], d_tf[2]], writes=[d_y5s[yi]])
                    kb.dma("sp", self.y5d[ct * 128:(ct + 1) * 128, blk * 512:(blk + 1) * 512], y5s[yi][:], reads=[d_y5s[yi]])

                for step in range(Lc):
                    E_slice(0, step)
                    T_slice(0, step)
                H_stage(0)
                for ct in range(8):
                    nxt = ct + 1 < 8
                    if nxt:
                        E_slice(ct + 1, 0)
                    for blk in range(8):
                        if nxt and blk < Lc:
                            T_slice(ct + 1, blk)
                        Y_mm(ct, blk)
                        if nxt and blk + 1 < Lc:
                            E_slice(ct + 1, blk + 1)
                        Y_epi(ct, blk)
                    if nxt:
                        H_stage(ct + 1)
                kb.barrier()
            with ExitStack() as st4:
                sb4 = lambda n, s, d: st4.enter_context(self.sbt("S4_" + n, s, d))
                gluw = sb4("gluw", [128, 8, 1024], BF16)
                y5b = [sb4("y5b%d" % i, [128, 8, 512], BF16) for i in range(2)]
                NB = 3
                szT = [sb4("szT%d" % i, [128, 512], BF16) for i in range(NB)]
                og = [sb4("og%d" % i, [128, 512], BF16) for i in range(NB)]
                g1 = [sb4("g1%d" % i, [128, 512], BF16) for i in range(NB)]
                g2 = [sb4("g2%d" % i, [128, 512], BF16) for i in range(NB)]
                d_glu = kb.dep("glu")
                d_y5b = kb.deps_n(2, "y5b")
                d_sz = kb.deps_n(NB, "sz")
                d_og = kb.deps_n(NB, "og")
                d_g1 = kb.deps_n(NB, "g1")
                d_g2 = kb.deps_n(NB, "g2")
                kb.dma("pool", gluw[:], self.glu_w[l].rearrange("(k p) n -> p k n", p=128), writes=[d_glu])
                ig = 0
                for tb in range(8):
                    yi = tb % 2
                    kb.dma("sp", y5b[yi][:], self.y5d[:, tb * 512:(tb + 1) * 512].rearrange("(c p) t -> p c t", p=128), writes=[d_y5b[yi]])
                    for co in range(8):
                        i = ig % NB
                        bank = ig % 4
                        ig += 1
                        kb.dma("sp", szT[i][:], self.sT[1024 + co * 128:1024 + (co + 1) * 128, tb * 512:(tb + 1) * 512], writes=[d_sz[i]])
                        for ci in range(8):
                            kb.op("pe", lambda e: e.matmul(ps[bank][:], lhsT=gluw[:, ci, co * 128:(co + 1) * 128], rhs=y5b[yi][:, ci, :],
                                                           start=(ci == 0), stop=(ci == 7)), reads=[d_glu, d_y5b[yi]], writes=[dps[bank]])
                        kb.op("act", lambda e: e.activation(out=g1[i][:], in_=ps[bank][:], func=AF.Sigmoid), reads=[dps[bank]], writes=[d_g1[i]])
                        kb.op("act", lambda e: e.activation(out=g2[i][:], in_=szT[i][:], func=AF.Sigmoid), reads=[d_sz[i]], writes=[d_g2[i]])
                        kb.op("dve", lambda e: e.tensor_tensor(out=g2[i][:], in0=g2[i][:], in1=szT[i][:], op=ALU.mult), reads=[d_g2[i], d_sz[i]], writes=[d_g2[i]])
                        kb.op("dve", lambda e: e.tensor_tensor(out=g1[i][:], in0=g1[i][:], in1=y5b[yi][:, co, :], op=ALU.mult),
                              reads=[d_g1[i], d_y5b[yi]], writes=[d_g1[i]])
                        kb.op("dve", lambda e: e.tensor_tensor(out=og[i][:], in0=g1[i][:], in1=g2[i][:], op=ALU.mult), reads=[d_g1[i], d_g2[i]], writes=[d_og[i]])
                        kb.dma("pool", self.mixedT[1024 + co * 128:1024 + (co + 1) * 128, tb * 512:(tb + 1) * 512], og[i][:], reads=[d_og[i]])

    def qk_prep(self, tag, src, col0, nh, normw_dram, l, dstT, d_dst, rope_tab, d_rope, ntiles=NT, bank=4):
        nc, kb = self.nc, self.kb
        ps, dps = self.ps, self.dps
        G = 4 if ntiles % 4 == 0 else 2
        NBUF = 4
        with ExitStack() as st:
            sb = lambda n, s, d: st.enter_context(self.sbt("P_%s_%s" % (tag, n), s, d))
            W = nh * 128
            GH = G * nh
            nw = sb("nw", [128, 128], F32)
            qraw = [sb("qraw%d" % i, [128, G, W], BF16) for i in range(NBUF)]
            sq = [sb("sq%d" % i, [128, GH, 128], BF16) for i in range(NBUF)]
            ss = [sb("ss%d" % i, [128, GH], F32) for i in range(NBUF)]
            qn = [sb("qn%d" % i, [128, GH, 128], F32) for i in range(NBUF)]
            rt = [sb("rt%d" % i, [128, 4, GH, 16], F32) for i in range(NBUF)]
            qb = [sb("qb%d" % i, [128, GH, 128], BF16) for i in range(NBUF)]
            d_nw = kb.dep()
            d_qraw = kb.deps_n(NBUF)
            d_sq = kb.deps_n(NBUF)
            d_ss = kb.deps_n(NBUF)
            d_qn = kb.deps_n(NBUF)
            d_rt = kb.deps_n(NBUF)
            d_qb = kb.deps_n(NBUF)
            kb.dma("sp", nw[:], normw_dram[l:l + 1, :].partition_broadcast(128), writes=[d_nw])
            def f1(g):
                i = g % NBUF
                t0 = g * G
                kb.dma("sp", qraw[i][:], src[t0 * 128:(t0 + G) * 128, col0:col0 + W].rearrange("(g p) c -> p g c", p=128), writes=[d_qraw[i]])
                qv = qraw[i][:].rearrange("p g (h c) -> p (g h) c", c=128)
                kb.op("pool", lambda e: e.tensor_tensor(out=sq[i][:], in0=qv, in1=qv, op=ALU.mult), reads=[d_qraw[i]], writes=[d_sq[i]])
                kb.op("dve", lambda e: e.tensor_reduce(out=ss[i][:], in_=sq[i][:], axis=AX.X, op=ALU.add), reads=[d_sq[i]], writes=[d_ss[i]])
                kb.op("dve", lambda e: e.tensor_scalar(out=ss[i][:], in0=ss[i][:], scalar1=1.0 / 128, scalar2=EPS, op0=ALU.mult, op1=ALU.add),
                      reads=[d_ss[i]], writes=[d_ss[i]])
                kb.op("act", lambda e: e.activation(out=ss[i][:], in_=ss[i][:], func=AF.Sqrt), reads=[d_ss[i]], writes=[d_ss[i]])
                kb.op("dve", lambda e: e.reciprocal(out=ss[i][:], in_=ss[i][:]), reads=[d_ss[i]], writes=[d_ss[i]])
            def f2(g):
                i = g % NBUF
                t0 = g * G
                qv = qraw[i][:].rearrange("p g (h c) -> p (g h) c", c=128)
                kb.op("dve", lambda e: e.tensor_tensor(out=qn[i][:], in0=qv, in1=ss[i][:].unsqueeze(2).to_broadcast([128, GH, 128]), op=ALU.mult),
                      reads=[d_qraw[i], d_ss[i]], writes=[d_qn[i]])
                kb.op("pool", lambda e: e.tensor_tensor(out=qn[i][:], in0=qn[i][:], in1=nw[:].unsqueeze(1).to_broadcast([128, GH, 128]), op=ALU.mult),
                      reads=[d_qn[i], d_nw], writes=[d_qn[i]])
                cb = rope_tab[:, t0:t0 + G, 0:16].unsqueeze(2).to_broadcast([128, G, nh, 16])
                sbb = rope_tab[:, t0:t0 + G, 16:32].unsqueeze(2).to_broadcast([128, G, nh, 16])
                q4 = qn[i][:].rearrange("p (g h) c -> p g h c", g=G)
                x1 = q4[:, :, :, 0:16]
                x2 = q4[:, :, :, 16:32]
                rv = lambda j: rt[i][:, j].rearrange("p (g h) c -> p g h c", g=G)
                R_ = [d_qn[i], d_rope]
                kb.op("dve", lambda e: e.tensor_tensor(out=rv(0), in0=x1, in1=cb, op=ALU.mult), reads=R_, writes=[d_rt[i]])
                kb.op("dve", lambda e: e.tensor_tensor(out=rv(1), in0=x2, in1=sbb, op=ALU.mult), reads=R_, writes=[d_rt[i]])
                kb.op("dve", lambda e: e.tensor_tensor(out=rv(2), in0=x2, in1=cb, op=ALU.mult), reads=R_, writes=[d_rt[i]])
                kb.op("dve", lambda e: e.tensor_tensor(out=rv(3), in0=x1, in1=sbb, op=ALU.mult), reads=R_, writes=[d_rt[i]])
                kb.op("dve", lambda e: e.tensor_tensor(out=qb[i][:, :, 0:16], in0=rt[i][:, 0], in1=rt[i][:, 1], op=ALU.subtract), reads=[d_rt[i]], writes=[d_qb[i]])
                kb.op("dve", lambda e: e.tensor_tensor(out=qb[i][:, :, 16:32], in0=rt[i][:, 2], in1=rt[i][:, 3], op=ALU.add), reads=[d_rt[i]], writes=[d_qb[i]])
                kb.op("act", lambda e: e.activation(out=qb[i][:, :, 32:128], in_=qn[i][:, :, 32:128], func=AF.Copy), reads=[d_qn[i]], writes=[d_qb[i]])
            def f3(g):
                i = g % NBUF
                t0 = g * G
                nb = (GH + 7) // 8
                for bi in range(nb):
                    bk = bank + ((g * nb + bi) % 4)
                    pb = ps[bk][:].bitcast(BF16)
                    n_here = min(8, GH - bi * 8)
                    for j in range(n_here):
                        kb.op("pe", lambda e: e.transpose(out=pb[:, j * 128:(j + 1) * 128], in_=qb[i][:, bi * 8 + j, :], identity=self.identb[:]),
                              reads=[d_qb[i], self.d_const], writes=[dps[bk]])
                    ng = n_here // nh
                    gt0 = t0 + (bi * 8) // nh
                    dst = dstT[:, :, gt0 * 128:(gt0 + ng) * 128].rearrange("p h (g n) -> p g h n", g=ng)
                    srcp = pb[:, 0:n_here * 128].rearrange("p (g h n) -> p g h n", g=ng, h=nh)
                    eng = "act" if bi % 2 == 0 else "dve"
                    if eng == "act":
                        kb.op("act", lambda e: e.activation(out=dst, in_=srcp, func=AF.Copy), reads=[dps[bk]], writes=[d_dst])
                    else:
                        kb.op("dve", lambda e: e.tensor_copy(out=dst, in_=srcp), reads=[dps[bk]], writes=[d_dst])
            self.emit_pipelined(ntiles // G, [f1, f2, f3])
            kb.barrier()

    def phase_MOBA(self, l):
        nc, kb = self.nc, self.kb
        ps, dps = self.ps, self.dps
        SC = 1.0 / math.sqrt(128.0)
        with ExitStack() as st:
            sb = lambda n, s, d: st.enter_context(self.sbt("M_" + n, s, d))
            QT = sb("QT", [128, 4, S], BF16)
            KT = sb("KT", [128, 4, S], BF16)
            Vp = sb("Vp", [128, NT, 4, 130], BF16)
            rope = sb("rope", [128, NT, 32], F32)
            tri = sb("tri", [128, 128], BF16)
            kmf = sb("kmf", [128, 4, 16], F32)
            kmT = sb("kmT", [128, 4, 16], BF16)
            d_QT, d_KT, d_Vp, d_SEL, d_OM, d_rope, d_tri, d_km = kb.deps_n(8, "mb")
            kb.dma("sp", rope[:], self.c_rope.rearrange("(t p) c -> p t c", p=128), writes=[d_rope])
            kb.dma("sp", tri[:], self.c_tri, writes=[d_tri])
            kb.op("pool", lambda e: e.memset(Vp[:].rearrange("p a b c -> p (a b c)"), 1.0), writes=[d_Vp])
            for h in range(4):
                kb.dma("sp", Vp[:, :, h, 0:128], self.proj_tm[:, C_MV + h * 128:C_MV + (h + 1) * 128].rearrange("(t p) c -> p t c", p=128),
                       reads=[d_Vp], writes=[d_Vp])
            self.qk_prep("mq", self.proj_tm, C_MQ, 4, self.hn["moba_q_norm"], l, QT, d_QT, rope, d_rope)
            self.qk_prep("mk", self.proj_tm, C_MK, 4, self.hn["moba_k_norm"], l, KT, d_KT, rope, d_rope)
            kb.barrier()
            SEL = sb("SEL", [128, NT, 4, 16], F32)
            OM = sb("OM", [128, NT, 512], BF16)
            kb.op("dve", lambda e: e.memset(SEL[:].rearrange("p a b c -> p (a b c)"), 1.0), writes=[d_SEL])
            for h in range(4):
                kb.op("dve", lambda e: e.tensor_reduce(out=kmf[:, h, :], in_=KT[:, h, :].rearrange("p (n k) -> p n k", k=256), axis=AX.X, op=ALU.add),
                      reads=[d_KT], writes=[d_km])
            kb.op("dve", lambda e: e.tensor_scalar(out=kmT[:], in0=kmf[:], scalar1=1.0 / 256, scalar2=None, op0=ALU.mult), reads=[d_km], writes=[d_km])
            with ExitStack() as st2:
                sb2 = lambda n, s, d: st2.enter_context(self.sbt("M2_" + n, s, d))
                gt = [sb2("gt%d" % i, [128, 4, 16], F32) for i in range(2)]
                m8 = [sb2("m8%d" % i, [128, 4, 8], F32) for i in range(2)]
                d_gt = kb.deps_n(2)
                d_m8 = kb.deps_n(2)
                for tt in range(8, NT):
                    own = tt // 2
                    i = tt % 2
                    for h in range(4):
                        kb.op("pe", lambda e: e.matmul(ps[4][:, h * 16:(h + 1) * 16], lhsT=QT[:, h, tt * 128:(tt + 1) * 128], rhs=kmT[:, h, :], start=True, stop=True),
                              reads=[d_QT, d_km], writes=[dps[4]])
                    kb.op("dve", lambda e: e.tensor_copy(out=gt[i][:].rearrange("p a b -> p (a b)"), in_=ps[4][:, 0:64]), reads=[dps[4]], writes=[d_gt[i]])
                    kb.op("dve", lambda e: e.memset(gt[i][:, :, own:16], NEG), reads=[d_gt[i]], writes=[d_gt[i]])
                    for h in range(4):
                        kb.op("dve", lambda e: e.max(out=m8[i][:, h, :], in_=gt[i][:, h, :]), reads=[d_gt[i]], writes=[d_m8[i]])
                    for h in range(4):
                        kb.op("dve", lambda e: e.tensor_scalar(out=SEL[:, tt, h, :], in0=gt[i][:, h, :], scalar1=m8[i][:, h, 2:3], scalar2=None, op0=ALU.is_ge),
                              reads=[d_gt[i], d_m8[i]], writes=[d_SEL])
            kb.barrier()
            PT = [sb("PT%d" % i, [128, 512], BF16) for i in range(5)]
            acc = [sb("acc%d" % i, [128, 2, 130], F32) for i in range(2)]
            rr = [sb("rr%d" % i, [128, 2], F32) for i in range(2)]
            d_PT = kb.deps_n(5)
            d_acc = kb.deps_n(2)
            d_rr = kb.deps_n(2)
            iters = [(h, qb, n) for h in range(4) for qb in range(16) for n in range(qb + 1)]

            def front(idx):
                h, qb, n = iters[idx]
                pi = idx % 5
                sbank = (0, 1, 2, 5)[idx % 4]
                for kt in range(2):
                    kb.op("pe", lambda e: e.matmul(ps[sbank][:, kt * 256:(kt + 1) * 256], lhsT=KT[:, h, (2 * n + kt) * 128:(2 * n + kt + 1) * 128],
                                                   rhs=QT[:, h, qb * 256:(qb + 1) * 256], start=True, stop=True),
                          reads=[d_KT, d_QT], writes=[dps[sbank]])
                kb.op("act", lambda e: e.activation(out=PT[pi][:], in_=ps[sbank][:], func=AF.Exp, scale=SC), reads=[dps[sbank]], writes=[d_PT[pi]])
                if n == qb:
                    kb.op("pool", lambda e: e.tensor_tensor(out=PT[pi][:, 0:128], in0=PT[pi][:, 0:128], in1=tri[:], op=ALU.mult),
                          reads=[d_PT[pi], d_tri], writes=[d_PT[pi]])
                    kb.op("pool", lambda e: e.tensor_tensor(out=PT[pi][:, 384:512], in0=PT[pi][:, 384:512], in1=tri[:], op=ALU.mult),
                          reads=[d_PT[pi], d_tri], writes=[d_PT[pi]])

            def back(idx):
                h, qb, n = iters[idx]
                pi = idx % 5
                obank = (3, 4, 6)[idx % 3]
                ai = (h * 16 + qb) % 2
                if n == 0:
                    kb.op("pool", lambda e: e.memset(acc[ai][:].rearrange("p a b -> p (a b)"), 0.0), writes=[d_acc[ai]])
                if n < qb:
                    for qt in range(2):
                        for kt in range(2):
                            kb.op("pe", lambda e: e.matmul(ps[obank][:, qt * 256:qt * 256 + 129], lhsT=PT[pi][:, kt * 256 + qt * 128:kt * 256 + (qt + 1) * 128],
                                                           rhs=Vp[:, 2 * n + kt, h, 0:129], start=(kt == 0), stop=(kt == 1)),
                                  reads=[d_PT[pi], d_Vp], writes=[dps[obank]])
                    for qt in range(2):
                        kb.op("dve", lambda e: e.scalar_tensor_tensor(out=acc[ai][:, qt, 0:129], in0=ps[obank][:, qt * 256:qt * 256 + 129],
                                                                      scalar=SEL[:, 2 * qb + qt, h, n:n + 1], in1=acc[ai][:, qt, 0:129],
                                                                      op0=ALU.mult, op1=ALU.add),
                              reads=[dps[obank], d_SEL, d_acc[ai]], writes=[d_acc[ai]])
                else:
                    kb.op("pe", lambda e: e.matmul(ps[obank][:, 0:129], lhsT=PT[pi][:, 0:128], rhs=Vp[:, 2 * qb, h, 0:129], start=True, stop=True),
                          reads=[d_PT[pi], d_Vp], writes=[dps[obank]])
                    kb.op("pe", lambda e: e.matmul(ps[obank][:, 256:256 + 129], lhsT=PT[pi][:, 128:256], rhs=Vp[:, 2 * qb, h, 0:129], start=True, stop=False),
                          reads=[d_PT[pi], d_Vp], writes=[dps[obank]])
                    kb.op("pe", lambda e: e.matmul(ps[obank][:, 256:256 + 129], lhsT=PT[pi][:, 384:512], rhs=Vp[:, 2 * qb + 1, h, 0:129], start=False, stop=True),
                          reads=[d_PT[pi], d_Vp], writes=[dps[obank]])
                    for qt in range(2):
                        kb.op("dve", lambda e: e.tensor_tensor(out=acc[ai][:, qt, 0:129], in0=ps[obank][:, qt * 256:qt * 256 + 129], in1=acc[ai][:, qt, 0:129], op=ALU.add),
                              reads=[dps[obank], d_acc[ai]], writes=[d_acc[ai]])
                    kb.op("dve", lambda e: e.reciprocal(out=rr[ai][:], in_=acc[ai][:, :, 128]), reads=[d_acc[ai]], writes=[d_rr[ai]])
                    for qt in range(2):
                        kb.op("dve", lambda e: e.tensor_scalar(out=OM[:, 2 * qb + qt, h * 128:(h + 1) * 128], in0=acc[ai][:, qt, 0:128], scalar1=rr[ai][:, qt:qt + 1],
                                                               scalar2=None, op0=ALU.mult), reads=[d_acc[ai], d_rr[ai]], writes=[d_OM])

            SK = 3
            for idx in range(min(SK, len(iters))):
                front(idx)
            for idx in range(len(iters)):
                if idx + SK < len(iters):
                    front(idx + SK)
                back(idx)
            kb.barrier()
            self.gate_and_store(OM, d_OM, C_MZ, 0)

    def gate_and_store(self, OM, d_OM, zcol, row0):
        nc, kb = self.nc, self.kb
        ps, dps = self.ps, self.dps
        with ExitStack() as st:
            sb = lambda n, s, d: st.enter_context(self.sbt("G_" + n, s, d))
            zt = [sb("zt%d" % i, [128, 512], BF16) for i in range(4)]
            sl = [sb("sl%d" % i, [128, 512], F32) for i in range(4)]
            gg = [sb("gg%d" % i, [128, 512], BF16) for i in range(4)]
            oT = [sb("oT%d" % i, [128, 4, 128], BF16) for i in range(4)]
            d_zt = kb.deps_n(4)
            d_sl = kb.deps_n(4)
            d_gg = kb.deps_n(4)
            d_oT = kb.deps_n(4)
            def g1(tt):
                i = tt % 4
                kb.dma("sp", zt[i][:], self.proj_tm[tt * 128:(tt + 1) * 128, zcol:zcol + 512], writes=[d_zt[i]])
                kb.op("act", lambda e: e.activation(out=sl[i][:], in_=zt[i][:], func=AF.Silu), reads=[d_zt[i]], writes=[d_sl[i]])
                kb.op("dve", lambda e: e.tensor_tensor(out=gg[i][:], in0=OM[:, tt, :], in1=sl[i][:], op=ALU.mult), reads=[d_OM, d_sl[i]], writes=[d_gg[i]])
            def g2(tt):
                i = tt % 4
                bk = 4 + i
                pb = ps[bk][:].bitcast(BF16)
                for h in range(4):
                    kb.op("pe", lambda e: e.transpose(out=pb[:, h * 128:(h + 1) * 128], in_=gg[i][:, h * 128:(h + 1) * 128], identity=self.identb[:]),
                          reads=[d_gg[i], self.d_const], writes=[dps[bk]])
                kb.op("act", lambda e: e.activation(out=oT[i][:].rearrange("p h n -> p (h n)"), in_=pb[:, 0:512], func=AF.Copy), reads=[dps[bk]], writes=[d_oT[i]])
                kb.dma("pool", self.mixedT[row0:row0 + 512, tt * 128:(tt + 1) * 128].rearrange("(h p) n -> p h n", p=128), oT[i][:], reads=[d_oT[i]])
            self.emit_pipelined(NT, [g1, g2])

    def gelu_tanh(self, x, dx, tmp, dtmp, out, dout):
        kb = self.kb
        kb.op("act", lambda e: e.activation(out=tmp, in_=x, func=AF.Square), reads=[dx], writes=[dtmp])
        kb.op("dve", lambda e: e.tensor_scalar(out=tmp, in0=tmp, scalar1=0.044715, scalar2=1.0, op0=ALU.mult, op1=ALU.add), reads=[dtmp], writes=[dtmp])
        kb.op("dve", lambda e: e.tensor_tensor(out=tmp, in0=tmp, in1=x, op=ALU.mult), reads=[dtmp, dx], writes=[dtmp])
        kb.op("act", lambda e: e.activation(out=tmp, in_=tmp, func=AF.Sigmoid, scale=1.5957691216057308), reads=[dtmp], writes=[dtmp])
        kb.op("dve", lambda e: e.tensor_tensor(out=out, in0=x, in1=tmp, op=ALU.mult), reads=[dx, dtmp], writes=[dout])

    def phase_NSA(self, l):
        nc, kb = self.nc, self.kb
        ps, dps = self.ps, self.dps
        SC = 1.0 / math.sqrt(128.0)
        with ExitStack() as st:
            sb = lambda n, s, d: st.enter_context(self.sbt("N_" + n, s, d))
            NQT = sb("NQT", [128, 4, S], BF16)
            KST = sb("KST", [128, 1, S], BF16)
            KWT = sb("KWT", [128, 1, S], BF16)
            KCT = sb("KCT", [128, 1, 256], BF16)
            VS = sb("VS", [128, NT, 130], BF16)
            VW = sb("VW", [128, NT, 130], BF16)
            RC = sb("RC", [128, 2, 196], BF16)
            SELT = sb("SELT", [64, NT, 128], BF16)
            ESEL = sb("ESEL", [64, NT, 128], BF16)
            G = sb("G", [128, NT, 12], F32)
            rope = sb("rope", [128, NT, 32], F32)
            ropec = sb("ropec", [128, 2, 32], F32)
            tri = sb("tri", [128, 128], BF16)
            triu = sb("triu", [128, 128], BF16)
            dkq = sb("dkq", [128, 128], F32)
            d_NQT, d_KST, d_KWT, d_KCT, d_VS, d_VW, d_RC, d_ONS, d_SELT, d_G, d_rope, d_cst = kb.deps_n(12, "ns")
            kb.dma("sp", rope[:], self.c_rope.rearrange("(t p) c -> p t c", p=128), writes=[d_rope])
            kb.dma("sp", ropec[:], self.c_ropec.rearrange("(t p) c -> p t c", p=128), writes=[d_rope])
            kb.dma("sp", tri[:], self.c_tri, writes=[d_cst])
            kb.dma("sp", triu[:], self.c_triu, writes=[d_cst])
            kb.dma("sp", dkq[:], self.c_dkq, writes=[d_cst])
            kb.dma("sp", ESEL[:], self.c_esel, writes=[d_cst])
            kb.op("pool", lambda e: e.memset(VS[:].rearrange("p a b -> p (a b)"), 1.0), writes=[d_VS])
            kb.op("pool", lambda e: e.memset(VW[:].rearrange("p a b -> p (a b)"), 1.0), writes=[d_VW])
            kb.op("pool", lambda e: e.memset(RC[:].rearrange("p a b -> p (a b)"), 1.0), writes=[d_RC])
            kb.dma("sp", VS[:, :, 0:128], self.proj_tm[:, C_NVS:C_NVS + 128].rearrange("(t p) c -> p t c", p=128), reads=[d_VS], writes=[d_VS])
            kb.dma("sp", VW[:, :, 0:128], self.proj_tm[:, C_NVW:C_NVW + 128].rearrange("(t p) c -> p t c", p=128), reads=[d_VW], writes=[d_VW])
            kb.dma("sp", RC[:, :, 129:193], self.c_ovl.rearrange("(t p) j -> p t j", p=128), reads=[d_RC], writes=[d_RC])
            kb.dma("pool", G[:], self.proj_tm[:, C_NG:C_NG + 12].rearrange("(t p) c -> p t c", p=128), writes=[d_G])
            kb.op("act", lambda e: e.activation(out=G[:].rearrange("p a b -> p (a b)"), in_=G[:].rearrange("p a b -> p (a b)"), func=AF.Sigmoid),
                  reads=[d_G], writes=[d_G])
            self.qk_prep("nq", self.proj_tm, C_NQ, 4, self.hn["nsa_q_norm"], l, NQT, d_NQT, rope, d_rope)
            self.qk_prep("nks", self.proj_tm, C_NKS, 1, self.hn["nsa_ks_norm"], l, KST, d_KST, rope, d_rope)
            self.qk_prep("nkw", self.proj_tm, C_NKW, 1, self.hn["nsa_kw_norm"], l, KWT, d_KWT, rope, d_rope)
            with ExitStack() as st2:
                sb2 = lambda n, s, d: st2.enter_context(self.sbt("N2_" + n, s, d))
                XcT = sb2("XcT", [128, 2, S], BF16)
                w1b = [sb2("w1b%d" % i, [128, 32, 128], BF16) for i in range(2)]
                w2b = [sb2("w2b%d" % i, [128, 128], BF16) for i in range(2)]
                pef = sb2("pef", [32, 2, 128], F32)
                peT = sb2("peT", [128, 2, 32], BF16)
                cvec = sb2("cvec", [128, 2], F32)
                xr = [sb2("xr%d" % i, [128, 256], BF16) for i in range(2)]
                hs = sb2("hs", [128, 256], F32)
                htmp = sb2("htmp", [128, 256], F32)
                hT = [sb2("hT%d" % i, [128, 256], BF16) for i in range(2)]
                kcs = sb2("kcs", [128, 2, 128], BF16)
                d_XcT, d_w1, d_w2, d_pef, d_peT, d_cvec, d_hs, d_htmp, d_kcs = kb.deps_n(9, "cm")
                d_xr = kb.deps_n(2)
                d_hT = kb.deps_n(2)
                w1s = [self.ck_w1, self.cv_w1]
                w2s = [self.ck_w2, self.cv_w2]
                pes = [self.pe_k, self.pe_v]
                for j in range(2):
                    kb.dma("pool", w1b[j][:], w1s[j][l].rearrange("(l d) o -> d l o", d=128), writes=[d_w1])
                    kb.dma("pool", w2b[j][:], w2s[j][l], writes=[d_w2])
                    kb.dma("sp", pef[:, j, :], pes[j][l], writes=[d_pef])
                for j in range(2):
                    kb.op("pe", lambda e: e.transpose(out=ps[0][:, j * 32:(j + 1) * 32], in_=pef[:, j, :], identity=self.identf[0:32, 0:32]),
                          reads=[d_pef, self.d_const], writes=[dps[0]])
                kb.op("dve", lambda e: e.tensor_copy(out=peT[:].rearrange("p a b -> p (a b)"), in_=ps[0][:, 0:64]), reads=[dps[0]], writes=[d_peT])
                for tt in range(NT):
                    i = tt % 2
                    kb.dma("sp", xr[i][:], self.proj_tm[tt * 128:(tt + 1) * 128, C_NKC:C_NKC + 256], writes=[d_xr[i]])
                    bk = 5 + i
                    pb = ps[bk][:].bitcast(BF16)
                    for j in range(2):
                        kb.op("pe", lambda e: e.transpose(out=pb[:, j * 128:(j + 1) * 128], in_=xr[i][:, j * 128:(j + 1) * 128], identity=self.identb[:]),
                              reads=[d_xr[i], self.d_const], writes=[dps[bk]])
                    kb.op("dve", lambda e: e.tensor_copy(out=XcT[:, :, tt * 128:(tt + 1) * 128], in_=pb[:, 0:256].rearrange("p (j n) -> p j n", j=2)),
                          reads=[dps[bk]], writes=[d_XcT])
                for j in range(2):
                    for ll in range(32):
                        kb.op("pe", lambda e: e.matmul(ps[1][:, j:j + 1], lhsT=w1b[j][:, ll, :], rhs=peT[:, j, ll:ll + 1], start=(ll == 0), stop=(ll == 31)),
                              reads=[d_w1, d_peT], writes=[dps[1]])
                    kb.op("dve", lambda e: e.tensor_copy(out=cvec[:, j:j + 1], in_=ps[1][:, j:j + 1]), reads=[dps[1]], writes=[d_cvec])
                    for ll in range(32):
                        kb.op("pe", lambda e: e.matmul(ps[2 + j][:, 0:255], lhsT=w1b[j][:, ll, :], rhs=XcT[:, j, ll:ll + 16 * 254 + 1:16], start=(ll == 0), stop=(ll == 31)),
                              reads=[d_w1, d_XcT], writes=[dps[2 + j]])
                    kb.op("dve", lambda e: e.tensor_scalar(out=hs[:, 0:255], in0=ps[2 + j][:, 0:255], scalar1=cvec[:, j:j + 1], scalar2=None, op0=ALU.add),
                          reads=[dps[2 + j], d_cvec], writes=[d_hs])
                    kb.op("pool", lambda e: e.memset(hT[j][:], 0.0), writes=[d_hT[j]])
                    self.gelu_tanh(hs[:, 0:255], d_hs, htmp[:, 0:255], d_htmp, hT[j][:, 0:255], d_hT[j])
                    for it in range(2):
                        kb.op("pe", lambda e: e.matmul(ps[4][:, it * 128:(it + 1) * 128], lhsT=hT[j][:, it * 128:(it + 1) * 128], rhs=w2b[j][:], start=True, stop=True),
                              reads=[d_hT[j], d_w2], writes=[dps[4]])
                    if j == 0:
                        kb.op("dve", lambda e: e.tensor_copy(out=kcs[:].rearrange("p a b -> p (a b)"), in_=ps[4][:, 0:256]), reads=[dps[4]], writes=[d_kcs])
                        kb.dma("sp", self.kcmp_tm.rearrange("(t p) c -> p t c", p=128), kcs[:], reads=[d_kcs])
                    else:
                        kb.op("dve", lambda e: e.tensor_copy(out=RC[:, :, 0:128], in_=ps[4][:, 0:256].rearrange("p (a b) -> p a b", a=2)),
                              reads=[dps[4], d_RC], writes=[d_RC])
                kb.barrier()
            self.qk_prep("nkc", self.kcmp_tm, 0, 1, self.hn["nsa_kc_norm"], l, KCT, d_KCT, ropec, d_rope, ntiles=2)
            ONS = sb("ONS", [128, NT, 512], F32)
            PT = [sb("PT%d" % i, [128, 4, 128], BF16) for i in range(8)]
            M2s = [sb("M2s%d" % i, [128, 128], BF16) for i in range(4)]
            sA = [sb("sA%d" % i, [128, 64], F32) for i in range(2)]
            sB = [sb("sB%d" % i, [128, 64], F32) for i in range(2)]
            imp = [sb("imp%d" % i, [128, 64], F32) for i in range(2)]
            sc = [sb("sc%d" % i, [128, 2, 64], F32) for i in range(2)]
            m8 = [sb("m8%d" % i, [128, 2, 8], F32) for i in range(2)]
            selq = [sb("selq%d" % i, [128, 64], BF16) for i in range(2)]
            rr = [sb("rr%d" % i, [128, 8], F32) for i in range(2)]
            d_PT = kb.deps_n(8)
            d_M2s = kb.deps_n(4)
            d_m2p = kb.deps_n(4)
            d_sA = kb.deps_n(2)
            d_sB = kb.deps_n(2)
            d_imp = kb.deps_n(2)
            d_sc = kb.deps_n(2)
            d_m8 = kb.deps_n(2)
            d_selq = kb.deps_n(2)
            d_rr = kb.deps_n(2)
            ip = 0

            def obank(tt, h):
                return 5 + h // 2, (h % 2) * 256

            zl = sb("zl", [128, 128], BF16)
            zr_ = sb("zr", [128, 512], BF16)
            d_z = kb.dep("zeros")
            kb.op("pool", lambda e: e.memset(zl[:], 0.0), writes=[d_z])
            kb.op("pool", lambda e: e.memset(zr_[:], 0.0), writes=[d_z])

            def zero_obanks():
                for b in (5, 6):
                    kb.op("pe", lambda e: e.matmul(ps[b][:], lhsT=zl[:], rhs=zr_[:], start=True, stop=True), reads=[d_z], writes=[dps[b]])

            def finalize(tt, branch, first):
                i = tt % 2
                for h in range(4):
                    b, c0 = obank(tt, h)
                    kb.op("dve", lambda e: e.tensor_scalar(out=rr[i][:, h:h + 1], in0=ps[b][:, c0 + 128:c0 + 129], scalar1=1e-30, scalar2=None, op0=ALU.max),
                          reads=[dps[b]], writes=[d_rr[i]])
                kb.op("dve", lambda e: e.reciprocal(out=rr[i][:, 0:4], in_=rr[i][:, 0:4]), reads=[d_rr[i]], writes=[d_rr[i]])
                kb.op("dve", lambda e: e.tensor_tensor(out=rr[i][:, 4:8], in0=rr[i][:, 0:4], in1=G[:, tt, branch:12:3], op=ALU.mult), reads=[d_rr[i], d_G], writes=[d_rr[i]])
                for h in range(4):
                    b, c0 = obank(tt, h)
                    dst = ONS[:, tt, h * 128:(h + 1) * 128]
                    if first:
                        kb.op("dve", lambda e: e.tensor_scalar(out=dst, in0=ps[b][:, c0:c0 + 128], scalar1=rr[i][:, 4 + h:5 + h], scalar2=None, op0=ALU.mult),
                              reads=[dps[b], d_rr[i]], writes=[d_ONS])
                    else:
                        kb.op("dve", lambda e: e.scalar_tensor_tensor(out=dst, in0=ps[b][:, c0:c0 + 128], scalar=rr[i][:, 4 + h:5 + h], in1=dst, op0=ALU.mult, op1=ALU.add),
                              reads=[dps[b], d_rr[i], d_ONS], writes=[d_ONS])

            it_cmp = [(tt, it) for tt in range(NT) for it in range(1 if tt < 16 else 2)]

            def c_front(idx):
                tt, it = it_cmp[idx]
                pi = idx % 3
                sbank = idx % 2
                i = tt % 2
                if it == 0:
                    kb.dma("sp", sA[i][:], self.c_selA[tt], writes=[d_sA[i]])
                    kb.dma("sp", sB[i][:], self.c_selB[tt], writes=[d_sB[i]])
                kb.op("pe", lambda e: e.matmul(ps[sbank][:], lhsT=KCT[:, 0, it * 128:(it + 1) * 128], rhs=NQT[:, :, tt * 128:(tt + 1) * 128], start=True, stop=True),
                      reads=[d_KCT, d_NQT], writes=[dps[sbank]])
                kb.op("act", lambda e: e.activation(out=PT[pi][:].rearrange("p a b -> p (a b)"), in_=ps[sbank][:], func=AF.Exp, scale=SC),
                      reads=[dps[sbank]], writes=[d_PT[pi]])
                thr = float(31 + 2048 * it - 128 * tt)
                kb.op("dve", lambda e: e.scalar_tensor_tensor(out=PT[pi][:], in0=dkq[:].unsqueeze(1).to_broadcast([128, 4, 128]), scalar=thr, in1=PT[pi][:],
                                                              op0=ALU.is_ge, op1=ALU.mult), reads=[d_PT[pi], d_cst], writes=[d_PT[pi]])

            def c_back(idx):
                tt, it = it_cmp[idx]
                pi = idx % 3
                i = tt % 2
                n_it = 1 if tt < 16 else 2
                for h in range(4):
                    b, c0 = obank(tt, h)
                    if it == 0 and h == 0:
                        zero_obanks()
                    kb.op("pe", lambda e: e.matmul(ps[b][:, c0:c0 + 193], lhsT=PT[pi][:, h, :], rhs=RC[:, it, 0:193], start=False, stop=(it == n_it - 1)),
                          reads=[d_PT[pi], d_RC], writes=[dps[b]])
                if it != n_it - 1:
                    return
                finalize(tt, 0, True)
                for h in range(4):
                    b, c0 = obank(tt, h)
                    if h == 0:
                        kb.op("dve", lambda e: e.tensor_scalar(out=imp[i][:], in0=ps[b][:, c0 + 129:c0 + 193], scalar1=rr[i][:, h:h + 1], scalar2=None, op0=ALU.mult),
                              reads=[dps[b], d_rr[i]], writes=[d_imp[i]])
                    else:
                        kb.op("dve", lambda e: e.scalar_tensor_tensor(out=imp[i][:], in0=ps[b][:, c0 + 129:c0 + 193], scalar=rr[i][:, h:h + 1], in1=imp[i][:],
                                                                      op0=ALU.mult, op1=ALU.add), reads=[dps[b], d_rr[i], d_imp[i]], writes=[d_imp[i]])
                kb.op("dve", lambda e: e.tensor_tensor(out=sc[i][:, 0, :], in0=imp[i][:], in1=sA[i][:], op=ALU.mult), reads=[d_imp[i], d_sA[i]], writes=[d_sc[i]])
                kb.op("dve", lambda e: e.tensor_tensor(out=sc[i][:, 0, :], in0=sc[i][:, 0, :], in1=sB[i][:], op=ALU.add), reads=[d_sc[i], d_sB[i]], writes=[d_sc[i]])
                kb.op("dve", lambda e: e.max(out=m8[i][:, 0, :], in_=sc[i][:, 0, :]), reads=[d_sc[i]], writes=[d_m8[i]])
                kb.op("dve", lambda e: e.match_replace(out=sc[i][:, 1, :], in_to_replace=m8[i][:, 0, :], in_values=sc[i][:, 0, :], imm_value=NEG),
                      reads=[d_sc[i], d_m8[i]], writes=[d_sc[i]])
                kb.op("dve", lambda e: e.max(out=m8[i][:, 1, :], in_=sc[i][:, 1, :]), reads=[d_sc[i]], writes=[d_m8[i]])
                kb.op("dve", lambda e: e.scalar_tensor_tensor(out=selq[i][:], in0=sc[i][:, 0, :], scalar=m8[i][:, 1, 7:8], in1=sA[i][:], op0=ALU.is_ge, op1=ALU.mult),
                      reads=[d_sc[i], d_m8[i], d_sA[i]], writes=[d_selq[i]])
                pb = ps[2 + i][:].bitcast(BF16)
                kb.op("pe", lambda e: e.transpose(out=pb[0:64, 0:128], in_=selq[i][:], identity=self.identb[:]), reads=[d_selq[i], self.d_const], writes=[dps[2 + i]])
                kb.op("act", lambda e: e.activation(out=SELT[:, tt, :], in_=pb[0:64, 0:128], func=AF.Copy), reads=[dps[2 + i]], writes=[d_SELT])

            c_front(0)
            for idx in range(len(it_cmp)):
                if idx + 1 < len(it_cmp):
                    c_front(idx + 1)
                c_back(idx)

            its = []
            for branch in (1, 2):
                for tt in range(NT):
                    kts = list(range(0, tt + 1)) if branch == 1 else list(range(max(0, tt - 4), tt + 1))
                    for ki, kt in enumerate(kts):
                        its.append((branch, tt, ki, kt, len(kts)))

            def a_front(idx):
                branch, tt, ki, kt, nk = its[idx]
                pi = 3 + idx % 5
                sbank = (0, 1, 2, 7)[idx % 4]
                KT_ = KST if branch == 1 else KWT
                d_KT_ = d_KST if branch == 1 else d_KWT
                kb.op("pe", lambda e: e.matmul(ps[sbank][:], lhsT=KT_[:, 0, kt * 128:(kt + 1) * 128], rhs=NQT[:, :, tt * 128:(tt + 1) * 128], start=True, stop=True),
                      reads=[d_KT_, d_NQT], writes=[dps[sbank]])
                kb.op("act", lambda e: e.activation(out=PT[pi][:].rearrange("p a b -> p (a b)"), in_=ps[sbank][:], func=AF.Exp, scale=SC),
                      reads=[dps[sbank]], writes=[d_PT[pi]])
                if branch == 1:
                    mb = 3 + (idx % 2)
                    mi = idx % 4
                    msl = ps[mb][:, 0:128]
                    kb.op("pe", lambda e: e.matmul(msl, lhsT=ESEL[:, kt, :], rhs=SELT[:, tt, :], start=True, stop=True),
                          reads=[d_cst, d_SELT], writes=[dps[mb]])
                    if kt == tt:
                        kb.op("dve", lambda e: e.tensor_tensor(out=M2s[mi][:], in0=msl, in1=tri[:], op=ALU.mult), reads=[dps[mb], d_cst], writes=[d_M2s[mi]])
                    else:
                        kb.op("dve", lambda e: e.tensor_copy(out=M2s[mi][:], in_=msl), reads=[dps[mb]], writes=[d_M2s[mi]])
                    meng = "pool" if idx % 3 == 0 else "dve"
                    kb.op(meng, lambda e: e.tensor_tensor(out=PT[pi][:], in0=PT[pi][:], in1=M2s[mi][:].unsqueeze(1).to_broadcast([128, 4, 128]), op=ALU.mult),
                          reads=[d_PT[pi], d_M2s[mi]], writes=[d_PT[pi]])
                else:
                    if kt == tt:
                        kb.op("pool", lambda e: e.tensor_tensor(out=PT[pi][:], in0=PT[pi][:], in1=tri[:].unsqueeze(1).to_broadcast([128, 4, 128]), op=ALU.mult),
                              reads=[d_PT[pi], d_cst], writes=[d_PT[pi]])
                    elif kt == tt - 4:
                        kb.op("pool", lambda e: e.tensor_tensor(out=PT[pi][:], in0=PT[pi][:], in1=triu[:].unsqueeze(1).to_broadcast([128, 4, 128]), op=ALU.mult),
                              reads=[d_PT[pi], d_cst], writes=[d_PT[pi]])

            def a_back(idx):
                branch, tt, ki, kt, nk = its[idx]
                pi = 3 + idx % 5
                V_ = VS if branch == 1 else VW
                d_V_ = d_VS if branch == 1 else d_VW
                for h in range(4):
                    b, c0 = obank(tt, h)
                    if ki == 0 and h == 0:
                        zero_obanks()
                    kb.op("pe", lambda e: e.matmul(ps[b][:, c0:c0 + 129], lhsT=PT[pi][:, h, :], rhs=V_[:, kt, 0:129], start=False, stop=(ki == nk - 1)),
                          reads=[d_PT[pi], d_V_], writes=[dps[b]])
                if ki == nk - 1:
                    finalize(tt, branch, False)

            SK = 3
            for idx in range(min(SK, len(its))):
                a_front(idx)
            for idx in range(len(its)):
                if idx + SK < len(its):
                    a_front(idx + SK)
                a_back(idx)
            kb.barrier()
            self.gate_and_store(ONS, d_ONS, C_NZ, 512)


def host_consts():
    c = {}
    c["c_identb"] = np.eye(128, dtype=np.float32).astype(ml_dtypes.bfloat16)
    c["c_identf"] = np.eye(128, dtype=np.float32)
    inv = 500000.0 ** (-np.arange(0, 32, 2, dtype=np.float32) / 32.0)
    pos = np.arange(S, dtype=np.float32)
    ang = pos[:, None] * inv[None, :].astype(np.float32)
    c["c_rope"] = np.concatenate([np.cos(ang), np.sin(ang)], axis=1).astype(np.float32)
    posc = (np.arange(256) * 16 + 31).astype(np.float32)
    angc = posc[:, None] * inv[None, :].astype(np.float32)
    c["c_ropec"] = np.concatenate([np.cos(angc), np.sin(angc)], axis=1).astype(np.float32)
    kk = np.arange(128)
    c["c_tri"] = (kk[:, None] <= kk[None, :]).astype(np.float32).astype(ml_dtypes.bfloat16)
    c["c_triu"] = (kk[:, None] > kk[None, :]).astype(np.float32).astype(ml_dtypes.bfloat16)
    c["c_iota"] = np.broadcast_to(np.arange(512, dtype=np.float32)[None, :], (128, 512)).copy()
    c["c_dkq"] = (kk[None, :] - 16 * kk[:, None]).astype(np.float32)
    selA = np.zeros((NT, 128, 64), np.float32)
    selB = np.zeros((NT, 128, 64), np.float32)
    j = np.arange(64)[None, :]
    for tt in range(NT):
        t = tt * 128 + np.arange(128)
        cur = (t // 64)[:, None]
        valid = j <= cur
        forced = (j == 0) | (j == cur) | (j == cur - 1)
        selA[tt] = valid.astype(np.float32)
        selB[tt] = np.where(forced, 1.0e30, np.where(valid, 0.0, -1.0e30))
    c["c_selA"] = selA
    c["c_selB"] = selB
    es = np.zeros((64, NT, 128), np.float32)
    for kt in range(NT):
        for key in range(128):
            es[2 * kt + key // 64, kt, key] = 1.0
    c["c_esel"] = es.astype(ml_dtypes.bfloat16)
    ci = np.arange(256)[:, None] * 16
    sj = np.arange(64)[None, :] * 64
    ov = ((ci < sj + 64) & (ci + 32 > sj)).astype(np.float32)
    ov[255] = 0.0
    c["c_ovl"] = ov.astype(ml_dtypes.bfloat16)
    return c


_PROG = None


def kernel(**inputs):
    global _PROG
    if _PROG is None:
        _PROG = Prog().build()
    nc = _PROG
    consts = host_consts()
    x = np.ascontiguousarray(inputs["x"], dtype=np.float32)
    B = x.shape[0]
    shared = {k: np.ascontiguousarray(v) for k, v in inputs.items() if k != "x"}
    in_maps = []
    for c in range(8):
        m = dict(shared)
        m.update(consts)
        m["x"] = x[c % B]
        in_maps.append(m)
    res = run_bass_kernel_spmd(nc, in_maps, core_ids=list(range(8)))
    out = np.stack([res.results[b]["out"] for b in range(B)], axis=0)
    return out.astype(np.float32, copy=False)
```

```python
import math
from contextlib import ExitStack

import numpy as np
import ml_dtypes
import concourse.bass as bass
import concourse.mybir as mybir
from concourse.bass_utils import run_bass_kernel_spmd

F32 = mybir.dt.float32
BF16 = mybir.dt.bfloat16
I32 = mybir.dt.int32
AF = mybir.ActivationFunctionType
ALU = mybir.AluOpType
AX = mybir.AxisListType

S = 4096
D = 2048
NT = S // 128
INW = 5900
TMW = 3852
DEPTH = 2
EPS = 1e-6
C_MQ, C_MK, C_MV, C_MZ, C_NQ = 0, 512, 1024, 1536, 2048
C_NKC, C_NVC, C_NKS, C_NVS, C_NKW, C_NVW = 2560, 2688, 2816, 2944, 3072, 3200
C_NG, C_NZ = 3328, 3340
NEG = -1.0e30


class Dep:
    __slots__ = ("w", "r", "name")

    def __init__(self, name=""):
        self.w = {}
        self.r = {}
        self.name = name


class Eng:
    def __init__(self, nc, eng, name):
        self.eng = eng
        self.name = name
        self.sem = nc.alloc_semaphore("sem_" + name)
        self.count = 0
        self.seen = {}


class KB:
    def __init__(self, nc, n_dma_sems=48):
        self.nc = nc
        self.E = {
            "pe": Eng(nc, nc.tensor, "pe"),
            "act": Eng(nc, nc.scalar, "act"),
            "dve": Eng(nc, nc.vector, "dve"),
            "pool": Eng(nc, nc.gpsimd, "pool"),
            "sp": Eng(nc, nc.sync, "sp"),
        }
        self.dsems = [[nc.alloc_semaphore("dsem%d" % i), 0] for i in range(n_dma_sems)]
        self.dnext = 0
        self.deps = []
        self.n_wait = 0
        self.n_ins = 0

    def dep(self, name=""):
        d = Dep(name)
        self.deps.append(d)
        return d

    def deps_n(self, n, name=""):
        return [self.dep(name + str(i)) for i in range(n)]

    def _wait(self, E, sem, val):
        k = id(sem)
        if E.seen.get(k, 0) < val:
            E.eng.wait_ge(sem, val)
            E.seen[k] = val
            self.n_wait += 1

    def _sync(self, E, reads, writes, own_sem=None):
        for d in reads:
            for k, (s, v) in d.w.items():
                self._wait(E, s, v)
        for d in writes:
            for k, (s, v) in d.w.items():
                if s is own_sem:
                    continue
                self._wait(E, s, v)
            for k, (s, v) in d.r.items():
                if s is own_sem:
                    continue
                self._wait(E, s, v)

    def _record(self, sem, val, reads, writes):
        k = id(sem)
        for d in writes:
            d.w = {k: (sem, val)}
            d.r = {}
        for d in reads:
            d.r[k] = (sem, val)

    def op(self, e, f, reads=(), writes=()):
        E = self.E[e]
        self._sync(E, reads, writes, own_sem=E.sem)
        ins = f(E.eng)
        E.count += 1
        ins.then_inc(E.sem, 1)
        self._record(E.sem, E.count, reads, writes)
        self.n_ins += 1
        return ins

    def dma(self, q, out, in_, reads=(), writes=(), **kw):
        E = self.E[q]
        self._sync(E, reads, writes)
        ent = self.dsems[self.dnext]
        self.dnext = (self.dnext + 1) % len(self.dsems)
        if ent[1] > 0:
            self._wait(E, ent[0], ent[1])
        ent[1] += 16
        ins = E.eng.dma_start(out=out, in_=in_, **kw)
        ins.then_inc(ent[0], 16)
        self._record(ent[0], ent[1], reads, writes)
        self.n_ins += 1
        return ins

    def barrier(self):
        sp = self.E["sp"]
        for n, E in self.E.items():
            if E is not sp and E.count > 0:
                self._wait(sp, E.sem, E.count)
        for s, v in self.dsems:
            if v > 0:
                self._wait(sp, s, v)
        sp.count += 1
        sp.eng.nop().then_inc(sp.sem, 1)
        for n, E in self.E.items():
            if E is not sp:
                self._wait(E, sp.sem, sp.count)
            for n2, E2 in self.E.items():
                E.seen[id(E2.sem)] = E2.count
            for s, v in self.dsems:
                E.seen[id(s)] = v
        for d in self.deps:
            d.w = {}
            d.r = {}
        self.deps = []


class Prog:
    def __init__(self, dbg=None, layers=DEPTH, phases=("A", "S5", "MOBA", "NSA", "F")):
        self.dbg = dbg or ()
        self.layers = layers
        self.phases = phases
        nc = bass.Bass("TRN2", target_bir_lowering=False)
        self.nc = nc
        self.kb = KB(nc)
        ein = lambda n, s, d: nc.dram_tensor(n, list(s), d, kind="ExternalInput").ap()
        L = DEPTH
        self.x = ein("x", [S, D], F32)
        self.norm_w = ein("norm_w", [L, D], F32)
        self.w_in = ein("w_in", [L, D, INW], F32)
        self.w_out = ein("w_out", [L, D, D], F32)
        self.hn = {}
        for n in ("moba_q_norm", "moba_k_norm", "nsa_q_norm", "nsa_kc_norm", "nsa_ks_norm", "nsa_kw_norm"):
            self.hn[n] = ein(n, [L, 128], F32)
        self.pe_k = ein("nsa_pe_k", [L, 32, 128], F32)
        self.pe_v = ein("nsa_pe_v", [L, 32, 128], F32)
        self.ck_w1 = ein("nsa_cmp_k_w1", [L, 4096, 128], F32)
        self.ck_w2 = ein("nsa_cmp_k_w2", [L, 128, 128], F32)
        self.cv_w1 = ein("nsa_cmp_v_w1", [L, 4096, 128], F32)
        self.cv_w2 = ein("nsa_cmp_v_w2", [L, 128, 128], F32)
        self.a_re = ein("s5_a_re", [L, 64, 64], F32)
        self.a_im = ein("s5_a_im", [L, 64, 64], F32)
        self.b_re = ein("s5_b_re", [L, 64, 64, 16], F32)
        self.b_im = ein("s5_b_im", [L, 64, 64, 16], F32)
        self.c_re = ein("s5_c_re", [L, 64, 16, 64], F32)
        self.c_im = ein("s5_c_im", [L, 64, 16, 64], F32)
        self.s5_d = ein("s5_d", [L, 1024], F32)
        self.log_dt = ein("s5_log_dt", [L, 64], F32)
        self.glu_w = ein("s5_glu_w", [L, 1024, 1024], F32)
        self.c_identb = ein("c_identb", [128, 128], BF16)
        self.c_identf = ein("c_identf", [128, 128], F32)
        self.c_rope = ein("c_rope", [S, 32], F32)
        self.c_ropec = ein("c_ropec", [256, 32], F32)
        self.c_tri = ein("c_tri", [128, 128], BF16)
        self.c_iota = ein("c_iota", [128, 512], F32)
        self.c_triu = ein("c_triu", [128, 128], BF16)
        self.c_dkq = ein("c_dkq", [128, 128], F32)
        self.c_selA = ein("c_selA", [NT, 128, 64], F32)
        self.c_selB = ein("c_selB", [NT, 128, 64], F32)
        self.c_esel = ein("c_esel", [64, NT, 128], BF16)
        self.c_ovl = ein("c_ovl", [256, 64], BF16)
        self.out = nc.dram_tensor("out", [S, D], F32, kind="ExternalOutput").ap()
        sk = lambda n: "ExternalOutput" if n in self.dbg else "Internal"
        self.proj_tm = nc.dram_tensor("proj_tm", [S, TMW], BF16, kind=("ExternalInput" if "proj_in" in self.dbg else sk("proj_tm"))).ap()
        self.sT = nc.dram_tensor("sT", [2048, S], BF16, kind=("ExternalInput" if "sT_in" in self.dbg else sk("sT"))).ap()
        self.mixedT = nc.dram_tensor("mixedT", [2048, S], BF16, kind=("ExternalInput" if "mixedT_in" in self.dbg else sk("mixedT"))).ap()
        self.x1 = nc.dram_tensor("x1", [S, D], F32, kind=sk("x1")).ap()
        self.kcmp_tm = nc.dram_tensor("kcmp_tm", [256, 128], BF16, kind=sk("kcmp_tm")).ap()
        self.y5d = nc.dram_tensor("y5d", [1024, S], BF16, kind=sk("y5d")).ap()

    @staticmethod
    def emit_pipelined(n, stages):
        ns = len(stages)
        for t in range(n + ns - 1):
            for s_idx in range(ns - 1, -1, -1):
                i = t - s_idx
                if 0 <= i < n:
                    stages[s_idx](i)

    def sbt(self, name, shape, dtype):
        self._uid = getattr(self, "_uid", 0) + 1
        return self.nc.sbuf_tensor("%s_u%d" % (name, self._uid), shape, dtype)

    def build(self):
        nc, kb = self.nc, self.kb
        with ExitStack() as st:
            self.ps = [st.enter_context(nc.psum_tensor("ps%d" % i, [128, 512], F32)) for i in range(8)]
            self.dps = kb.deps_n(8, "ps")
            self.identb = st.enter_context(self.sbt("identb", [128, 128], BF16))
            self.identf = st.enter_context(self.sbt("identf", [128, 128], F32))
            self.d_const = kb.dep("const")
            kb.dma("sp", self.identb[:], self.c_identb, writes=[self.d_const])
            kb.dma("sp", self.identf[:], self.c_identf, writes=[self.d_const])
            kb.barrier()
            for l in range(self.layers):
                src = self.x if l == 0 else self.x1
                dst = self.out if l == self.layers - 1 else self.x1
                if "A" in self.phases:
                    self.phase_A(l, src)
                    kb.barrier()
                if "S5" in self.phases:
                    self.phase_S5(l)
                    kb.barrier()
                if "MOBA" in self.phases:
                    self.phase_MOBA(l)
                    kb.barrier()
                if "NSA" in self.phases:
                    self.phase_NSA(l)
                    kb.barrier()
                if "F" in self.phases:
                    self.phase_F(l, src, dst)
                    kb.barrier()
            kb.barrier()
        return nc

    def phase_A(self, l, src):
        nc, kb = self.nc, self.kb
        ps, dps = self.ps, self.dps
        with ExitStack() as st:
            sb = lambda n, s, d: st.enter_context(self.sbt("A_" + n, s, d))
            hdnT = sb("hdnT", [128, 16, 2048], BF16)
            normw = sb("normw", [128, D], F32)
            xt = [sb("xt%d" % i, [128, D], F32) for i in range(3)]
            junk = sb("junk", [128, D], BF16)
            hb = [sb("hb%d" % i, [128, D], BF16) for i in range(3)]
            wch = [sb("wch%d" % i, [128, 16, 512], BF16) for i in range(2)]
            stg = [sb("stg%d" % i, [128, 512], BF16) for i in range(4)]
            ss = [sb("ss%d" % i, [128, 1], F32) for i in range(3)]
            d_hT = kb.deps_n(16, "hT")
            d_nw = kb.dep("nw")
            d_xt = kb.deps_n(3, "xt")
            d_junk = kb.dep("junk")
            d_hb = kb.deps_n(3, "hb")
            d_w = kb.deps_n(2, "w")
            d_stg = kb.deps_n(4, "stg")
            d_ss = kb.deps_n(3, "ss")
            kb.dma("sp", normw[:], self.norm_w[l:l + 1, :].partition_broadcast(128), writes=[d_nw])
            istg = 0
            iw = 0
            ievac = 0
            for h in range(2):
                def a1(tt, h=h):
                    g = h * 16 + tt
                    i = tt % 3
                    xi = tt % 3
                    kb.dma("sp", xt[xi][:], src[g * 128:(g + 1) * 128, :], writes=[d_xt[xi]])
                    kb.op("act", lambda e: e.activation(out=junk[:], in_=xt[xi][:], func=AF.Square, accum_out=ss[i][:]),
                          reads=[d_xt[xi]], writes=[d_junk, d_ss[i]])
                    kb.op("dve", lambda e: e.tensor_scalar(out=ss[i][:], in0=ss[i][:], scalar1=1.0 / D, scalar2=EPS,
                                                           op0=ALU.mult, op1=ALU.add), reads=[d_ss[i]], writes=[d_ss[i]])
                    kb.op("act", lambda e: e.activation(out=ss[i][:], in_=ss[i][:], func=AF.Sqrt), reads=[d_ss[i]], writes=[d_ss[i]])
                    kb.op("dve", lambda e: e.reciprocal(out=ss[i][:], in_=ss[i][:]), reads=[d_ss[i]], writes=[d_ss[i]])
                def a2(tt, h=h):
                    i = tt % 3
                    xi = tt % 3
                    kb.op("dve", lambda e: e.scalar_tensor_tensor(out=hb[xi][:], in0=xt[xi][:], scalar=ss[i][:], in1=normw[:],
                                                                  op0=ALU.mult, op1=ALU.mult),
                          reads=[d_xt[xi], d_ss[i], d_nw], writes=[d_hb[xi]])
                    for half in range(2):
                        tbk = 4 + 2 * (tt % 2) + half
                        pb = ps[tbk][:].bitcast(BF16)
                        for k in range(8):
                            kc = half * 8 + k
                            kb.op("pe", lambda e: e.transpose(out=pb[:, k * 128:(k + 1) * 128], in_=hb[xi][:, kc * 128:(kc + 1) * 128],
                                                              identity=self.identb[:]),
                                  reads=[d_hb[xi], self.d_const], writes=[dps[tbk]])
                def a3(tt, h=h):
                    for half in range(2):
                        tbk = 4 + 2 * (tt % 2) + half
                        pb = ps[tbk][:].bitcast(BF16)
                        eng = "act" if half == 0 else "dve"
                        dst = hdnT[:, half * 8:(half + 1) * 8, tt * 128:(tt + 1) * 128]
                        srcp = pb[:, 0:1024].rearrange("p (k n) -> p k n", k=8)
                        if eng == "act":
                            kb.op("act", lambda e: e.activation(out=dst, in_=srcp, func=AF.Copy), reads=[dps[tbk]], writes=[d_hT[tt]])
                        else:
                            kb.op("dve", lambda e: e.tensor_copy(out=dst, in_=srcp), reads=[dps[tbk]], writes=[d_hT[tt]])
                self.emit_pipelined(16, [a1, a2, a3])
                chunks = [(c0, min(512, TMW - c0)) for c0 in range(0, TMW, 512)]
                for (c0, cw) in chunks:
                    wi = iw % 2
                    iw += 1
                    kb.dma("pool", wch[wi][:, :, 0:cw], self.w_in[l, :, c0:c0 + cw].rearrange("(k p) n -> p k n", p=128),
                           writes=[d_w[wi]])
                    for tt in range(16):
                        g = h * 16 + tt
                        pbank = ievac % 4
                        for kc in range(16):
                            kb.op("pe", lambda e: e.matmul(ps[pbank][:, 0:cw], lhsT=hdnT[:, kc, tt * 128:(tt + 1) * 128],
                                                           rhs=wch[wi][:, kc, 0:cw], start=(kc == 0), stop=(kc == 15)),
                                  reads=[d_hT[tt], d_w[wi]], writes=[dps[pbank]])
                        si = istg % 4
                        istg += 1
                        if ievac % 2 == 0:
                            kb.op("act", lambda e: e.activation(out=stg[si][:, 0:cw], in_=ps[pbank][:, 0:cw], func=AF.Copy),
                                  reads=[dps[pbank]], writes=[d_stg[si]])
                        else:
                            kb.op("dve", lambda e: e.tensor_copy(out=stg[si][:, 0:cw], in_=ps[pbank][:, 0:cw]),
                                  reads=[dps[pbank]], writes=[d_stg[si]])
                        ievac += 1
                        kb.dma("sp", self.proj_tm[g * 128:(g + 1) * 128, c0:c0 + cw], stg[si][:, 0:cw], reads=[d_stg[si]])
                for fc in range(4):
                    c0 = TMW + fc * 512
                    wi = iw % 2
                    iw += 1
                    kb.dma("pool", wch[wi][:], self.w_in[l, :, c0:c0 + 512].rearrange("(k p) n -> p k n", p=128), writes=[d_w[wi]])
                    for ctl in range(4):
                        row0 = fc * 512 + ctl * 128
                        for tb in range(4):
                            pbank = ievac % 4
                            for kc in range(16):
                                kb.op("pe", lambda e: e.matmul(ps[pbank][:], lhsT=wch[wi][:, kc, ctl * 128:(ctl + 1) * 128],
                                                               rhs=hdnT[:, kc, tb * 512:(tb + 1) * 512], start=(kc == 0), stop=(kc == 15)),
                                      reads=d_hT[tb * 4:(tb + 1) * 4] + [d_w[wi]], writes=[dps[pbank]])
                            si = istg % 4
                            istg += 1
                            if ievac % 2 == 0:
                                kb.op("act", lambda e: e.activation(out=stg[si][:], in_=ps[pbank][:], func=AF.Copy),
                                      reads=[dps[pbank]], writes=[d_stg[si]])
                            else:
                                kb.op("dve", lambda e: e.tensor_copy(out=stg[si][:], in_=ps[pbank][:]),
                                      reads=[dps[pbank]], writes=[d_stg[si]])
                            ievac += 1
                            t0 = h * 2048 + tb * 512
                            kb.dma("sp", self.sT[row0:row0 + 128, t0:t0 + 512], stg[si][:], reads=[d_stg[si]])

    def phase_F(self, l, src, dst):
        nc, kb = self.nc, self.kb
        ps, dps = self.ps, self.dps
        with ExitStack() as st:
            sb = lambda n, s, d: st.enter_context(self.sbt("F_" + n, s, d))
            wo = sb("wo", [128, 16, D], BF16)
            mT = [sb("mT%d" % i, [128, 16, 512], BF16) for i in range(2)]
            xr = [sb("xr%d" % i, [128, D], F32) for i in range(2)]
            ot = [sb("ot%d" % i, [128, D], F32) for i in range(2)]
            d_wo = kb.deps_n(4, "wo")
            d_mT = kb.deps_n(2, "mT")
            d_xr = kb.deps_n(2, "xr")
            d_ot = kb.deps_n(2, "ot")
            for c in range(4):
                kb.dma("pool", wo[:, :, c * 512:(c + 1) * 512], self.w_out[l, :, c * 512:(c + 1) * 512].rearrange("(k p) n -> p k n", p=128),
                       writes=[d_wo[c]])
            ie = 0
            import os
            for tb in range(int(os.environ.get('F_TB', 8))):
                mi = tb % 2
                kb.dma("sp", mT[mi][:], self.mixedT[:, tb * 512:(tb + 1) * 512].rearrange("(k p) n -> p k n", p=128), writes=[d_mT[mi]])
                for t4 in range(4):
                    g = tb * 4 + t4
                    i = g % 2
                    kb.dma("sp", xr[i][:], src[g * 128:(g + 1) * 128, :], writes=[d_xr[i]])
                    for c in range(4):
                        pbank = ie % 4
                        ie += 1
                        for kc in range(16):
                            kb.op("pe", lambda e: e.matmul(ps[pbank][:], lhsT=mT[mi][:, kc, t4 * 128:(t4 + 1) * 128],
                                                           rhs=wo[:, kc, c * 512:(c + 1) * 512], start=(kc == 0), stop=(kc == 15)),
                                  reads=[d_mT[mi], d_wo[c]], writes=[dps[pbank]])
                        kb.op("dve", lambda e: e.tensor_tensor(out=ot[i][:, c * 512:(c + 1) * 512], in0=ps[pbank][:],
                                                               in1=xr[i][:, c * 512:(c + 1) * 512], op=ALU.add),
                              reads=[dps[pbank], d_xr[i]], writes=[d_ot[i]])
                    kb.dma("pool", dst[g * 128:(g + 1) * 128, :], ot[i][:], reads=[d_ot[i]])

    def sincos_turns(self, turns, cos_out, sin_out, tmpf, tmpi, tmpf2, dT, dC, dS, dtmp):
        kb = self.kb
        TWO_PI = 6.283185
        kb.op("dve", lambda e: e.tensor_copy(out=tmpi, in_=turns), reads=[dT], writes=[dtmp])
        kb.op("dve", lambda e: e.tensor_tensor(out=tmpf, in0=turns, in1=tmpi, op=ALU.subtract), reads=[dT, dtmp], writes=[dtmp])
        kb.op("act", lambda e: e.activation(out=sin_out, in_=tmpf, func=AF.Sin, scale=TWO_PI), reads=[dtmp], writes=[dS])
        kb.op("dve", lambda e: e.tensor_scalar(out=tmpf2, in0=tmpf, scalar1=0.25, scalar2=None, op0=ALU.add), reads=[dtmp], writes=[dtmp])
        kb.op("dve", lambda e: e.scalar_tensor_tensor(out=tmpf2, in0=tmpf2, scalar=0.5, in1=tmpf2, op0=ALU.is_gt, op1=ALU.subtract),
              reads=[dtmp], writes=[dtmp])
        kb.op("act", lambda e: e.activation(out=cos_out, in_=tmpf2, func=AF.Sin, scale=-TWO_PI), reads=[dtmp], writes=[dC])

    def phase_S5_v1(self, l):
        nc, kb = self.nc, self.kb
        ps, dps = self.ps, self.dps
        with ExitStack() as st:
            sb = lambda n, s, d: st.enter_context(self.sbt("S_" + n, s, d))
            BT = [sb("BT%d" % i, [128, 32, 128], BF16) for i in range(2)]
            CT = [sb("CT%d" % i, [128, 32, 128], BF16) for i in range(2)]
            prm = sb("prm", [128, 24, 32], F32)
            prmi = sb("prmi", [128, 32], I32)
            Dt = sb("Dt", [128, 8], F32)
            gluw = sb("gluw", [128, 8, 1024], BF16)
            d_BT, d_CT, d_prm, d_Dt, d_glu = kb.deps_n(5, "s5c")
            AR, AI, LDT, DTT, MM, PHI, COS, SIN, FR, FI, C512, S512, T0, T1, T2, T3, T4, T5 = range(18)
            P = lambda i: prm[:, i, :]
            kb.dma("pool", gluw[:], self.glu_w[l].rearrange("(k p) n -> p k n", p=128), writes=[d_glu])
            with ExitStack() as st2:
                sb2 = lambda n, s, d: st2.enter_context(self.sbt("S2_" + n, s, d))
                XA = sb2("XA", [32, 3, 128], F32)
                ld2 = sb2("ld2", [32, 2], F32)
                XD = sb2("XD", [8, 128], F32)
                pads = [sb2("pad%d" % i, [128, 32, 128], F32) for i in range(4)]
                d_XA, d_ld2, d_XD = kb.deps_n(3, "xa")
                d_pad = kb.deps_n(4, "pad")
                kb.dma("sp", XA[:, 0, :], self.a_re[l].rearrange("(q gl) p -> q (gl p)", gl=2), writes=[d_XA])
                kb.dma("sp", XA[:, 1, :], self.a_im[l].rearrange("(q gl) p -> q (gl p)", gl=2), writes=[d_XA])
                kb.dma("sp", ld2[:], self.log_dt[l:l + 1, :].rearrange("o (q gl) -> (o q) gl", gl=2), writes=[d_ld2])
                kb.dma("sp", XD[:], self.s5_d[l:l + 1, :].rearrange("o (c p) -> (o c) p", p=128), writes=[d_XD])
                kb.op("dve", lambda e: e.tensor_copy(out=XA[:, 2, :].rearrange("q (gl p) -> q gl p", gl=2),
                                                     in_=ld2[:].unsqueeze(2).to_broadcast([32, 2, 64])),
                      reads=[d_ld2, d_XA], writes=[d_XA])
                for i in range(4):
                    eng = "dve" if i % 2 == 0 else "pool"
                    kb.op(eng, lambda e: e.memset(pads[i][:].rearrange("p q c -> p (q c)"), 0.0), writes=[d_pad[i]])
                srcB = [self.b_re[l], self.b_im[l]]
                srcC = [self.c_re[l], self.c_im[l]]
                for k in range(4):
                    for gl in range(2):
                        for i in range(2):
                            dstb = pads[i][gl * 64:(gl + 1) * 64, :, :].rearrange("p (ct k) c -> p k ct c", k=4)[:, k, :, 32 * k + 16 * gl:32 * k + 16 * gl + 16]
                            sb_ = srcB[i].rearrange("(ct k gl) p c -> k gl p ct c", k=4, gl=2)[k, gl]
                            kb.dma("sp", dstb, sb_, reads=[d_pad[i]], writes=[d_pad[i]])
                            dstc = pads[2 + i][32 * k + 16 * gl:32 * k + 16 * gl + 16, :, :].rearrange("p (ct k) c -> p k ct c", k=4)[:, k, :, gl * 64:(gl + 1) * 64]
                            sc_ = srcC[i].rearrange("(ct k gl) c p -> k gl c ct p", k=4, gl=2)[k, gl]
                            kb.dma("sp", dstc, sc_, reads=[d_pad[2 + i]], writes=[d_pad[2 + i]])
                for j in range(3):
                    kb.op("pe", lambda e: e.transpose(out=ps[0][:, j * 32:(j + 1) * 32], in_=XA[:, j, :], identity=self.identf[0:32, 0:32]),
                          reads=[d_XA, self.d_const], writes=[dps[0]])
                kb.op("pe", lambda e: e.transpose(out=ps[0][:, 96:104], in_=XD[:], identity=self.identf[0:8, 0:8]),
                      reads=[d_XD, self.d_const], writes=[dps[0]])
                kb.op("dve", lambda e: e.tensor_copy(out=prm[:, 0:3, :].rearrange("p a q -> p (a q)"), in_=ps[0][:, 0:96]), reads=[dps[0]], writes=[d_prm])
                kb.op("dve", lambda e: e.tensor_copy(out=Dt[:], in_=ps[0][:, 96:104]), reads=[dps[0]], writes=[d_Dt])
                R_, W_ = [d_prm], [d_prm]
                tt = lambda o, a, b, op: kb.op("dve", lambda e: e.tensor_tensor(out=P(o), in0=P(a), in1=P(b), op=op), reads=R_, writes=W_)
                kb.op("act", lambda e: e.activation(out=P(DTT), in_=P(LDT), func=AF.Exp), reads=R_, writes=W_)
                tt(T0, DTT, AR, ALU.mult)
                kb.op("act", lambda e: e.activation(out=P(MM), in_=P(T0), func=AF.Exp), reads=R_, writes=W_)
                tt(T0, DTT, AI, ALU.mult)
                kb.op("dve", lambda e: e.tensor_scalar(out=P(T1), in0=P(T0), scalar1=1.0 / (2.0 * math.pi), scalar2=None, op0=ALU.mult), reads=R_, writes=W_)
                kb.op("dve", lambda e: e.tensor_copy(out=prmi[:], in_=P(T1)), reads=R_, writes=W_)
                kb.op("dve", lambda e: e.tensor_tensor(out=P(PHI), in0=P(T1), in1=prmi[:], op=ALU.subtract), reads=R_, writes=W_)
                self.sincos_turns(P(PHI), P(COS), P(SIN), P(T2), prmi[:], P(T3), d_prm, d_prm, d_prm, d_prm)
                kb.op("dve", lambda e: e.tensor_scalar(out=P(T4), in0=P(PHI), scalar1=512.0, scalar2=None, op0=ALU.mult), reads=R_, writes=W_)
                self.sincos_turns(P(T4), P(C512), P(S512), P(T2), prmi[:], P(T3), d_prm, d_prm, d_prm, d_prm)
                tt(T0, MM, COS, ALU.mult)
                tt(T1, MM, SIN, ALU.mult)
                kb.op("dve", lambda e: e.tensor_scalar(out=P(T0), in0=P(T0), scalar1=-1.0, scalar2=None, op0=ALU.add), reads=R_, writes=W_)
                tt(T2, AR, AR, ALU.mult)
                tt(T3, AI, AI, ALU.mult)
                tt(T2, T2, T3, ALU.add)
                kb.op("dve", lambda e: e.reciprocal(out=P(T2), in_=P(T2)), reads=R_, writes=W_)
                tt(T3, T0, AR, ALU.mult)
                tt(T4, T1, AI, ALU.mult)
                tt(T3, T3, T4, ALU.add)
                tt(FR, T3, T2, ALU.mult)
                tt(T3, T1, AR, ALU.mult)
                tt(T4, T0, AI, ALU.mult)
                tt(T3, T3, T4, ALU.subtract)
                tt(FI, T3, T2, ALU.mult)
                ctmp = [sb2("ctmp%d" % i, [128, 4, 128], F32) for i in range(4)]
                d_ctmp = kb.dep("ctmp")
                for q4 in range(8):
                    for i in range(4):
                        bank = 4 + i
                        for k in range(4):
                            q = q4 * 4 + k
                            kb.op("pe", lambda e: e.transpose(out=ps[bank][:, k * 128:(k + 1) * 128], in_=pads[i][:, q, :], identity=self.identf[:]),
                                  reads=[d_pad[i], self.d_const], writes=[dps[bank]])
                    for i in range(2):
                        kb.op("act", lambda e: e.activation(out=BT[i][:, q4 * 4:(q4 + 1) * 4, :].rearrange("p a b -> p (a b)"), in_=ps[4 + i][:], func=AF.Copy),
                              reads=[dps[4 + i]], writes=[d_BT])
                    frb = prm[:, FR, q4 * 4:(q4 + 1) * 4].unsqueeze(2).to_broadcast([128, 4, 128])
                    fib = prm[:, FI, q4 * 4:(q4 + 1) * 4].unsqueeze(2).to_broadcast([128, 4, 128])
                    crp = ps[6][:].rearrange("p (a b) -> p a b", a=4)
                    cip = ps[7][:].rearrange("p (a b) -> p a b", a=4)
                    tmpw = [d_ctmp]
                    kb.op("dve", lambda e: e.tensor_tensor(out=ctmp[0][:], in0=crp, in1=frb, op=ALU.mult), reads=[dps[6], d_prm], writes=tmpw)
                    kb.op("dve", lambda e: e.tensor_tensor(out=ctmp[1][:], in0=cip, in1=fib, op=ALU.mult), reads=[dps[7], d_prm], writes=tmpw)
                    kb.op("dve", lambda e: e.tensor_tensor(out=CT[0][:, q4 * 4:(q4 + 1) * 4, :], in0=ctmp[0][:], in1=ctmp[1][:], op=ALU.subtract),
                          reads=tmpw, writes=[d_CT])
                    kb.op("dve", lambda e: e.tensor_tensor(out=ctmp[2][:], in0=crp, in1=fib, op=ALU.mult), reads=[dps[6], d_prm], writes=tmpw)
                    kb.op("dve", lambda e: e.tensor_tensor(out=ctmp[3][:], in0=cip, in1=frb, op=ALU.mult), reads=[dps[7], d_prm], writes=tmpw)
                    kb.op("dve", lambda e: e.scalar_tensor_tensor(out=CT[1][:, q4 * 4:(q4 + 1) * 4, :], in0=ctmp[2][:], scalar=-1.0, in1=ctmp[3][:],
                                                                  op0=ALU.mult, op1=ALU.subtract), reads=tmpw, writes=[d_CT])
                kb.barrier()
            y5T = sb("y5T", [128, 8, S], BF16)
            cosT = [sb("cosT%d" % i, [128, 4, 512], BF16) for i in range(2)]
            sinT = [sb("sinT%d" % i, [128, 4, 512], BF16) for i in range(2)]
            iota = sb("iota", [128, 512], F32)
            angi = sb("angi", [128, 512], I32)
            uT = [sb("uT%d" % i, [128, 512], BF16) for i in range(3)]
            tf = [sb("tf%d" % i, [128, 512], F32) for i in range(3)]
            tb_ = [sb("tb%d" % i, [128, 512], BF16) for i in range(6)]
            rb_ = [sb("rb%d" % i, [128, 512], BF16) for i in range(4)]
            BuS = [[sb("BuS%d%d" % (i, j), [128, 512], BF16) for j in range(2)] for i in range(2)]
            zf = [[sb("zf%d%d" % (i, j), [128, 512], F32) for j in range(2)] for i in range(2)]
            zb = [[sb("zb%d%d" % (i, j), [128, 512], BF16) for j in range(2)] for i in range(2)]
            X = [[sb("X%d%d" % (i, j), [128, 512], BF16) for j in range(2)] for i in range(2)]
            cz = [sb("cz%d" % i, [128, 2, 32], F32) for i in range(2)]
            czt = sb("czt", [128, 2], F32)
            d_y5 = kb.deps_n(8, "y5")
            d_cos = kb.deps_n(2, "cos")
            d_sin = kb.deps_n(2, "sin")
            d_iota, d_angi, d_czt = kb.deps_n(3, "tab")
            d_uT = kb.deps_n(3, "uT")
            d_tf = kb.deps_n(3, "tf")
            d_tb = kb.deps_n(6, "tb")
            d_rb = kb.deps_n(4, "rb")
            d_BuS = [kb.deps_n(2) for i in range(2)]
            d_zf = [kb.deps_n(2) for i in range(2)]
            d_zb = [kb.deps_n(2) for i in range(2)]
            d_X = [kb.deps_n(2) for i in range(2)]
            d_cz = kb.deps_n(2, "cz")
            kb.dma("sp", iota[:], self.c_iota, writes=[d_iota])
            t1, t2, t3, t4, wr, wi = tb_
            dt1, dt2, dt3, dt4, dwr, dwi = d_tb
            r1, r2, r3, r4 = rb_
            dr1, dr2, dr3, dr4 = d_rb

            def TT(eng, o, do, a, da, b_, db, op):
                kb.op(eng, lambda e: e.tensor_tensor(out=o, in0=a, in1=b_, op=op), reads=da + db, writes=[do])

            pending = []
            it = 0
            icb = 0
            for ct in range(8):
                tbi = ct % 2
                for k in range(4):
                    q = ct * 4 + k
                    kb.op("dve", lambda e: e.tensor_scalar(out=tf[0][:], in0=iota[:], scalar1=prm[:, PHI, q:q + 1], scalar2=None, op0=ALU.mult),
                          reads=[d_iota, d_prm], writes=[d_tf[0]])
                    self.sincos_turns(tf[0][:], cosT[tbi][:, k, :], sinT[tbi][:, k, :], tf[1][:], angi[:], tf[2][:], d_tf[0], d_cos[tbi], d_sin[tbi], d_tf[1])
                kb.op("dve", lambda e: e.memset(cz[0][:].rearrange("p a q -> p (a q)"), 0.0), writes=[d_cz[0]])
                for tb in range(8):
                    ui = icb % 3
                    ybank = 4 + (icb % 2)
                    icb += 1
                    par = tb % 2
                    kb.dma("sp", uT[ui][:], self.sT[ct * 128:(ct + 1) * 128, tb * 512:(tb + 1) * 512], writes=[d_uT[ui]])
                    for k in range(4):
                        q = ct * 4 + k
                        sset = it % 2
                        it += 1
                        c = cosT[tbi][:, k, :]
                        s_ = sinT[tbi][:, k, :]
                        dc, ds = [d_cos[tbi]], [d_sin[tbi]]
                        for i in range(2):
                            kb.op("pe", lambda e: e.matmul(ps[2 * sset + i][:], lhsT=BT[i][:, q, :], rhs=uT[ui][:], start=True, stop=True),
                                  reads=[d_BT, d_uT[ui]], writes=[dps[2 * sset + i]])
                            kb.op("act", lambda e: e.activation(out=BuS[sset][i][:], in_=ps[2 * sset + i][:], func=AF.Copy),
                                  reads=[dps[2 * sset + i]], writes=[d_BuS[sset][i]])
                        Br, Bi = BuS[sset][0][:], BuS[sset][1][:]
                        dBr, dBi = [d_BuS[sset][0]], [d_BuS[sset][1]]
                        TT("dve", t1[:], dt1, Br, dBr, c, dc, ALU.mult)
                        TT("dve", t2[:], dt2, Bi, dBi, s_, ds, ALU.mult)
                        TT("dve", wr[:], dwr, t1[:], [dt1], t2[:], [dt2], ALU.add)
                        TT("dve", t3[:], dt3, Bi, dBi, c, dc, ALU.mult)
                        TT("dve", t4[:], dt4, Br, dBr, s_, ds, ALU.mult)
                        TT("dve", wi[:], dwi, t3[:], [dt3], t4[:], [dt4], ALU.subtract)
                        mb = prm[:, MM, q:q + 1].to_broadcast([128, 512])
                        zr, zi = zf[sset][0], zf[sset][1]
                        dzr, dzi = d_zf[sset][0], d_zf[sset][1]
                        kb.op("dve", lambda e: e.tensor_tensor_scan(out=zr[:], data0=mb, data1=wr[:], initial=cz[par][:, 0, q:q + 1], op0=ALU.mult, op1=ALU.add),
                              reads=[d_prm, dwr, d_cz[par]], writes=[dzr])
                        kb.op("dve", lambda e: e.tensor_tensor_scan(out=zi[:], data0=mb, data1=wi[:], initial=cz[par][:, 1, q:q + 1], op0=ALU.mult, op1=ALU.add),
                              reads=[d_prm, dwi, d_cz[par]], writes=[dzi])
                        for i in range(2):
                            kb.op("act", lambda e: e.activation(out=zb[sset][i][:], in_=zf[sset][i][:], func=AF.Copy),
                                  reads=[d_zf[sset][i]], writes=[d_zb[sset][i]])
                        zr_l, zi_l = zr[:, 511:512], zi[:, 511:512]
                        c5, s5 = prm[:, C512, q:q + 1], prm[:, S512, q:q + 1]
                        nx = 1 - par
                        kb.op("dve", lambda e: e.tensor_scalar(out=czt[:, 0:1], in0=zi_l, scalar1=s5, scalar2=None, op0=ALU.mult), reads=[dzi, d_prm], writes=[d_czt])
                        kb.op("dve", lambda e: e.scalar_tensor_tensor(out=cz[nx][:, 0, q:q + 1], in0=zr_l, scalar=c5, in1=czt[:, 0:1], op0=ALU.mult, op1=ALU.subtract),
                              reads=[dzr, d_prm, d_czt], writes=[d_cz[nx]])
                        kb.op("dve", lambda e: e.tensor_scalar(out=czt[:, 1:2], in0=zi_l, scalar1=c5, scalar2=None, op0=ALU.mult), reads=[dzi, d_prm], writes=[d_czt])
                        kb.op("dve", lambda e: e.scalar_tensor_tensor(out=cz[nx][:, 1, q:q + 1], in0=zr_l, scalar=s5, in1=czt[:, 1:2], op0=ALU.mult, op1=ALU.add),
                              reads=[dzr, d_prm, d_czt], writes=[d_cz[nx]])

                        def back(sset=sset, c=c, s_=s_, dc=dc, ds=ds, q=q, k=k, ct=ct, tb=tb, ui=ui, ybank=ybank):
                            zbr, zbi = zb[sset][0][:], zb[sset][1][:]
                            dzbr, dzbi = [d_zb[sset][0]], [d_zb[sset][1]]
                            TT("pool", r1[:], dr1, zbr, dzbr, c, dc, ALU.mult)
                            TT("pool", r2[:], dr2, zbi, dzbi, s_, ds, ALU.mult)
                            TT("pool", X[sset][0][:], d_X[sset][0], r1[:], [dr1], r2[:], [dr2], ALU.subtract)
                            TT("dve", r3[:], dr3, zbr, dzbr, s_, ds, ALU.mult)
                            TT("dve", r4[:], dr4, zbi, dzbi, c, dc, ALU.mult)
                            TT("dve", X[sset][1][:], d_X[sset][1], r3[:], [dr3], r4[:], [dr4], ALU.add)
                            for i in range(2):
                                kb.op("pe", lambda e: e.matmul(ps[ybank][:], lhsT=CT[i][:, q, :], rhs=X[sset][i][:], start=(k == 0 and i == 0), stop=(k == 3 and i == 1)),
                                      reads=[d_CT, d_X[sset][i]], writes=[dps[ybank]])
                            if k == 3:
                                kb.op("dve", lambda e: e.scalar_tensor_tensor(out=tf[0][:], in0=uT[ui][:], scalar=Dt[:, ct:ct + 1], in1=ps[ybank][:], op0=ALU.mult, op1=ALU.add),
                                      reads=[d_uT[ui], d_Dt, dps[ybank]], writes=[d_tf[0]])
                                kb.op("act", lambda e: e.activation(out=tf[1][:], in_=tf[0][:], func=AF.Square), reads=[d_tf[0]], writes=[d_tf[1]])
                                kb.op("pool", lambda e: e.tensor_scalar(out=tf[1][:], in0=tf[1][:], scalar1=0.044715, scalar2=1.0, op0=ALU.mult, op1=ALU.add),
                                      reads=[d_tf[1]], writes=[d_tf[1]])
                                kb.op("pool", lambda e: e.tensor_tensor(out=tf[1][:], in0=tf[1][:], in1=tf[0][:], op=ALU.mult), reads=[d_tf[1], d_tf[0]], writes=[d_tf[1]])
                                kb.op("act", lambda e: e.activation(out=tf[2][:], in_=tf[1][:], func=AF.Sigmoid, scale=1.5957691216057308), reads=[d_tf[1]], writes=[d_tf[2]])
                                kb.op("pool", lambda e: e.tensor_tensor(out=y5T[:, ct, tb * 512:(tb + 1) * 512], in0=tf[0][:], in1=tf[2][:], op=ALU.mult),
                                      reads=[d_tf[0], d_tf[2]], writes=[d_y5[tb]])

                        if pending:
                            pending.pop(0)()
                        pending.append(back)
            while pending:
                pending.pop(0)()
            szT = [sb("szT%d" % i, [128, 512], BF16) for i in range(2)]
            og = [sb("og%d" % i, [128, 512], BF16) for i in range(2)]
            d_sz = kb.deps_n(2, "sz")
            d_og = kb.deps_n(2, "og")
            ig = 0
            for tb in range(8):
                for co in range(8):
                    i = ig % 2
                    ig += 1
                    bank = 4 + (ig % 4)
                    kb.dma("sp", szT[i][:], self.sT[1024 + co * 128:1024 + (co + 1) * 128, tb * 512:(tb + 1) * 512], writes=[d_sz[i]])
                    for ci in range(8):
                        kb.op("pe", lambda e: e.matmul(ps[bank][:], lhsT=gluw[:, ci, co * 128:(co + 1) * 128], rhs=y5T[:, ci, tb * 512:(tb + 1) * 512],
                                                       start=(ci == 0), stop=(ci == 7)), reads=[d_glu, d_y5[tb]], writes=[dps[bank]])
                    kb.op("act", lambda e: e.activation(out=tf[0][:], in_=ps[bank][:], func=AF.Sigmoid), reads=[dps[bank]], writes=[d_tf[0]])
                    kb.op("act", lambda e: e.activation(out=tf[1][:], in_=szT[i][:], func=AF.Silu), reads=[d_sz[i]], writes=[d_tf[1]])
                    kb.op("dve", lambda e: e.tensor_tensor(out=tf[0][:], in0=tf[0][:], in1=y5T[:, co, tb * 512:(tb + 1) * 512], op=ALU.mult),
                          reads=[d_tf[0], d_y5[tb]], writes=[d_tf[0]])
                    kb.op("dve", lambda e: e.tensor_tensor(out=og[i][:], in0=tf[0][:], in1=tf[1][:], op=ALU.mult), reads=[d_tf[0], d_tf[1]], writes=[d_og[i]])
                    kb.dma("sp", self.mixedT[1024 + co * 128:1024 + (co + 1) * 128, tb * 512:(tb + 1) * 512], og[i][:], reads=[d_og[i]])

    def phase_S5(self, l):
        nc, kb = self.nc, self.kb
        ps, dps = self.ps, self.dps
        Lc = 4
        NCH = S // Lc
        NH = NCH // 512
        with ExitStack() as st:
            sb = lambda n, s, d: st.enter_context(self.sbt("S_" + n, s, d))
            CT = [sb("CT%d" % i, [128, 32, 128], BF16) for i in range(2)]
            Bp = [sb("Bp%d" % i, [128, 32, 128], BF16) for i in range(2)]
            prm = sb("prm", [128, 24, 32], F32)
            apw = sb("apw", [128, 9, 2, 32], F32)
            prmi = sb("prmi", [128, 32], I32)
            Dt = sb("Dt", [128, 8], F32)
            d_CT, d_prm, d_Dt, d_apw = kb.deps_n(4, "s5c")
            d_Bp = kb.deps_n(2, "Bp")
            AR, AI, LDT, DTT, MM, PHI, COS, SIN, FR, FI, M8, PHI8, C512, S512, T0, T1, T2, T3, T4, T5 = range(20)
            P = lambda i: prm[:, i, :]
            with ExitStack() as st2:
                sb2 = lambda n, s, d: st2.enter_context(self.sbt("S2_" + n, s, d))
                XA = sb2("XA", [32, 3, 128], F32)
                ld2 = sb2("ld2", [32, 2], F32)
                XD = sb2("XD", [8, 128], F32)
                Cp = [sb2("Cp%d" % i, [128, 32, 128], F32) for i in range(2)]
                Bf = [sb2("Bf%d" % i, [128, 32, 128], F32) for i in range(2)]
                pads = [Bf[0], Bf[1], Cp[0], Cp[1]]
                d_XA, d_ld2, d_XD = kb.deps_n(3, "xa")
                d_Cp = kb.deps_n(2, "Cp")
                d_Bf = kb.deps_n(2, "Bf")
                d_pad = [d_Bf[0], d_Bf[1], d_Cp[0], d_Cp[1]]
                kb.dma("sp", XA[:, 0, :], self.a_re[l].rearrange("(q gl) p -> q (gl p)", gl=2), writes=[d_XA])
                kb.dma("sp", XA[:, 1, :], self.a_im[l].rearrange("(q gl) p -> q (gl p)", gl=2), writes=[d_XA])
                kb.dma("sp", ld2[:], self.log_dt[l:l + 1, :].rearrange("o (q gl) -> (o q) gl", gl=2), writes=[d_ld2])
                kb.dma("sp", XD[:], self.s5_d[l:l + 1, :].rearrange("o (c p) -> (o c) p", p=128), writes=[d_XD])
                kb.op("dve", lambda e: e.tensor_copy(out=XA[:, 2, :].rearrange("q (gl p) -> q gl p", gl=2),
                                                     in_=ld2[:].unsqueeze(2).to_broadcast([32, 2, 64])),
                      reads=[d_ld2, d_XA], writes=[d_XA])
                for i in range(4):
                    eng = "dve" if i % 2 == 0 else "pool"
                    kb.op(eng, lambda e: e.memset(pads[i][:].rearrange("p q c -> p (q c)"), 0.0), writes=[d_pad[i]])
                srcB = [self.b_re[l], self.b_im[l]]
                srcC = [self.c_re[l], self.c_im[l]]
                for k in range(4):
                    for gl in range(2):
                        for i in range(2):
                            dstb = pads[i][gl * 64:(gl + 1) * 64, :, :].rearrange("p (ct k) c -> p k ct c", k=4)[:, k, :, 32 * k + 16 * gl:32 * k + 16 * gl + 16]
                            sb_ = srcB[i].rearrange("(ct k gl) p c -> k gl p ct c", k=4, gl=2)[k, gl]
                            kb.dma("sp", dstb, sb_, reads=[d_pad[i]], writes=[d_pad[i]])
                            dstc = pads[2 + i][32 * k + 16 * gl:32 * k + 16 * gl + 16, :, :].rearrange("p (ct k) c -> p k ct c", k=4)[:, k, :, gl * 64:(gl + 1) * 64]
                            sc_ = srcC[i].rearrange("(ct k gl) c p -> k gl c ct p", k=4, gl=2)[k, gl]
                            kb.dma("sp", dstc, sc_, reads=[d_pad[2 + i]], writes=[d_pad[2 + i]])
                for j in range(3):
                    kb.op("pe", lambda e: e.transpose(out=ps[0][:, j * 32:(j + 1) * 32], in_=XA[:, j, :], identity=self.identf[0:32, 0:32]),
                          reads=[d_XA, self.d_const], writes=[dps[0]])
                kb.op("pe", lambda e: e.transpose(out=ps[0][:, 96:104], in_=XD[:], identity=self.identf[0:8, 0:8]),
                      reads=[d_XD, self.d_const], writes=[dps[0]])
                kb.op("dve", lambda e: e.tensor_copy(out=prm[:, 0:3, :].rearrange("p a q -> p (a q)"), in_=ps[0][:, 0:96]), reads=[dps[0]], writes=[d_prm])
                kb.op("dve", lambda e: e.tensor_copy(out=Dt[:], in_=ps[0][:, 96:104]), reads=[dps[0]], writes=[d_Dt])
                kb.op("act", lambda e: e.activation(out=Bp[0][:].rearrange("p q c -> p (q c)"), in_=Bf[0][:].rearrange("p q c -> p (q c)"), func=AF.Copy),
                      reads=[d_Bf[0]], writes=[d_Bp[0]])
                kb.op("pool", lambda e: e.tensor_copy(out=Bp[1][:].rearrange("p q c -> p (q c)"), in_=Bf[1][:].rearrange("p q c -> p (q c)")),
                      reads=[d_Bf[1]], writes=[d_Bp[1]])
                R_, W_ = [d_prm], [d_prm]
                tt = lambda o, a, b, op: kb.op("dve", lambda e: e.tensor_tensor(out=P(o), in0=P(a), in1=P(b), op=op), reads=R_, writes=W_)
                kb.op("act", lambda e: e.activation(out=P(DTT), in_=P(LDT), func=AF.Exp), reads=R_, writes=W_)
                tt(T0, DTT, AR, ALU.mult)
                kb.op("act", lambda e: e.activation(out=P(MM), in_=P(T0), func=AF.Exp), reads=R_, writes=W_)
                kb.op("act", lambda e: e.activation(out=P(M8), in_=P(T0), func=AF.Exp, scale=float(Lc)), reads=R_, writes=W_)
                tt(T0, DTT, AI, ALU.mult)
                kb.op("dve", lambda e: e.tensor_scalar(out=P(T1), in0=P(T0), scalar1=1.0 / (2.0 * math.pi), scalar2=None, op0=ALU.mult), reads=R_, writes=W_)
                kb.op("dve", lambda e: e.tensor_copy(out=prmi[:], in_=P(T1)), reads=R_, writes=W_)
                kb.op("dve", lambda e: e.tensor_tensor(out=P(PHI), in0=P(T1), in1=prmi[:], op=ALU.subtract), reads=R_, writes=W_)
                self.sincos_turns(P(PHI), P(COS), P(SIN), P(T2), prmi[:], P(T3), d_prm, d_prm, d_prm, d_prm)
                kb.op("dve", lambda e: e.tensor_scalar(out=P(T4), in0=P(PHI), scalar1=float(Lc), scalar2=None, op0=ALU.mult), reads=R_, writes=W_)
                kb.op("dve", lambda e: e.tensor_copy(out=prmi[:], in_=P(T4)), reads=R_, writes=W_)
                kb.op("dve", lambda e: e.tensor_tensor(out=P(PHI8), in0=P(T4), in1=prmi[:], op=ALU.subtract), reads=R_, writes=W_)
                kb.op("dve", lambda e: e.tensor_scalar(out=P(T4), in0=P(PHI8), scalar1=512.0, scalar2=None, op0=ALU.mult), reads=R_, writes=W_)
                self.sincos_turns(P(T4), P(C512), P(S512), P(T2), prmi[:], P(T3), d_prm, d_prm, d_prm, d_prm)
                tt(T0, MM, COS, ALU.mult)
                tt(T1, MM, SIN, ALU.mult)
                RW = [d_prm, d_apw]
                kb.op("dve", lambda e: e.memset(apw[:, 0, 0, :], 1.0), reads=RW, writes=[d_apw])
                kb.op("dve", lambda e: e.memset(apw[:, 0, 1, :], 0.0), reads=RW, writes=[d_apw])
                kb.op("dve", lambda e: e.tensor_copy(out=apw[:, 1, 0, :], in_=P(T0)), reads=RW, writes=[d_apw])
                kb.op("dve", lambda e: e.tensor_copy(out=apw[:, 1, 1, :], in_=P(T1)), reads=RW, writes=[d_apw])
                for m in range(1, Lc):
                    ar_, ai_ = apw[:, m, 0, :], apw[:, m, 1, :]
                    kb.op("dve", lambda e: e.tensor_tensor(out=P(T2), in0=ar_, in1=P(T0), op=ALU.mult), reads=RW, writes=W_)
                    kb.op("dve", lambda e: e.tensor_tensor(out=P(T3), in0=ai_, in1=P(T1), op=ALU.mult), reads=RW, writes=W_)
                    kb.op("dve", lambda e: e.tensor_tensor(out=apw[:, m + 1, 0, :], in0=P(T2), in1=P(T3), op=ALU.subtract), reads=RW, writes=[d_apw])
                    kb.op("dve", lambda e: e.tensor_tensor(out=P(T2), in0=ar_, in1=P(T1), op=ALU.mult), reads=RW, writes=W_)
                    kb.op("dve", lambda e: e.tensor_tensor(out=P(T3), in0=ai_, in1=P(T0), op=ALU.mult), reads=RW, writes=W_)
                    kb.op("dve", lambda e: e.tensor_tensor(out=apw[:, m + 1, 1, :], in0=P(T2), in1=P(T3), op=ALU.add), reads=RW, writes=[d_apw])
                kb.op("dve", lambda e: e.tensor_scalar(out=P(T0), in0=P(T0), scalar1=-1.0, scalar2=None, op0=ALU.add), reads=R_, writes=W_)
                tt(T2, AR, AR, ALU.mult)
                tt(T3, AI, AI, ALU.mult)
                tt(T2, T2, T3, ALU.add)
                kb.op("dve", lambda e: e.reciprocal(out=P(T2), in_=P(T2)), reads=R_, writes=W_)
                tt(T3, T0, AR, ALU.mult)
                tt(T4, T1, AI, ALU.mult)
                tt(T3, T3, T4, ALU.add)
                tt(FR, T3, T2, ALU.mult)
                tt(T3, T1, AR, ALU.mult)
                tt(T4, T0, AI, ALU.mult)
                tt(T3, T3, T4, ALU.subtract)
                tt(FI, T3, T2, ALU.mult)
                ctmp = [sb2("ctmp%d" % i, [128, 4, 128], F32) for i in range(4)]
                d_ctmp = kb.dep("ctmp")
                for q4 in range(8):
                    for i in range(2):
                        bank = 6 + i
                        for k in range(4):
                            q = q4 * 4 + k
                            kb.op("pe", lambda e: e.transpose(out=ps[bank][:, k * 128:(k + 1) * 128], in_=Cp[i][:, q, :], identity=self.identf[:]),
                                  reads=[d_Cp[i], self.d_const], writes=[dps[bank]])
                    frb = prm[:, FR, q4 * 4:(q4 + 1) * 4].unsqueeze(2).to_broadcast([128, 4, 128])
                    fib = prm[:, FI, q4 * 4:(q4 + 1) * 4].unsqueeze(2).to_broadcast([128, 4, 128])
                    crp = ps[6][:].rearrange("p (a b) -> p a b", a=4)
                    cip = ps[7][:].rearrange("p (a b) -> p a b", a=4)
                    tmpw = [d_ctmp]
                    kb.op("dve", lambda e: e.tensor_tensor(out=ctmp[0][:], in0=crp, in1=frb, op=ALU.mult), reads=[dps[6], d_prm], writes=tmpw)
                    kb.op("dve", lambda e: e.tensor_tensor(out=ctmp[1][:], in0=cip, in1=fib, op=ALU.mult), reads=[dps[7], d_prm], writes=tmpw)
                    kb.op("dve", lambda e: e.tensor_tensor(out=CT[0][:, q4 * 4:(q4 + 1) * 4, :], in0=ctmp[0][:], in1=ctmp[1][:], op=ALU.subtract),
                          reads=tmpw, writes=[d_CT])
                    kb.op("dve", lambda e: e.tensor_tensor(out=ctmp[2][:], in0=crp, in1=fib, op=ALU.mult), reads=[dps[6], d_prm], writes=tmpw)
                    kb.op("dve", lambda e: e.tensor_tensor(out=ctmp[3][:], in0=cip, in1=frb, op=ALU.mult), reads=[dps[7], d_prm], writes=tmpw)
                    kb.op("dve", lambda e: e.scalar_tensor_tensor(out=CT[1][:, q4 * 4:(q4 + 1) * 4, :], in0=ctmp[2][:], scalar=-1.0, in1=ctmp[3][:],
                                                                  op0=ALU.mult, op1=ALU.subtract), reads=tmpw, writes=[d_CT])
                kb.barrier()
            with ExitStack() as st3:
                sb3 = lambda n, s, d: st3.enter_context(self.sbt("S3_" + n, s, d))
                cosT = [sb3("cosT%d" % i, [128, 4, 512], BF16) for i in range(2)]
                sinT = [sb3("sinT%d" % i, [128, 4, 512], BF16) for i in range(2)]
                iota = sb3("iota", [128, 512], F32)
                angi = sb3("angi", [128, 512], I32)
                zero = sb3("zero", [128, 2], F32)
                czc = [sb3("czc%d" % i, [128, 2, 4], F32) for i in range(2)]
                czt = sb3("czt", [128, 2], F32)
                zl = sb3("zlast", [128, 2], F32)
                d_czc = kb.deps_n(2, "czc")
                d_czt = kb.dep("czt")
                d_zl = kb.dep("zl")
                uTf = [sb3("uTf%d" % i, [128, S], BF16) for i in range(2)]
                W1 = sb3("W1", [128, Lc, 8, 128], BF16)
                CA = [sb3("CA%d" % i, [128, Lc, 8, 128], BF16) for i in range(2)]
                SN = [sb3("SN%d" % i, [128, 8, 128], BF16) for i in range(2)]
                KT = [sb3("KT%d" % i, [128, Lc, 128], BF16) for i in range(2)]
                uu = [sb3("uu%d" % i, [128, 4, 128], F32) for i in range(4)]
                Xs = [[[sb3("Xs%d%d%d" % (c_, k, i), [128, NCH + 2], BF16) for i in range(2)] for k in range(4)] for c_ in range(2)]
                tf = [sb3("tf%d" % i, [128, 512], F32) for i in range(3)]
                tg = [sb3("tg%d" % i, [128, 512], F32) for i in range(3)]
                tb_ = [sb3("tb%d" % i, [128, 512], BF16) for i in range(6)]
                rb_ = [sb3("rb%d" % i, [128, 512], BF16) for i in range(4)]
                BuS = [[sb3("BuS%d%d" % (i, j), [128, 512], BF16) for j in range(2)] for i in range(2)]
                zb = [[sb3("zb%d%d" % (i, j), [128, 512], BF16) for j in range(2)] for i in range(2)]
                y5s = [sb3("y5s%d" % i, [128, 512], BF16) for i in range(2)]
                d_cos = kb.deps_n(2, "cos")
                d_sin = kb.deps_n(2, "sin")
                d_iota, d_angi, d_zero, d_W1 = kb.deps_n(4, "tab")
                d_CA = kb.deps_n(2, "CA")
                d_KT = kb.deps_n(2, "KT")
                d_uTf = kb.deps_n(2, "uTf")
                d_SN = kb.deps_n(2, "SN")
                d_SNr = kb.deps_n(2, "SNr")
                d_CAr = kb.deps_n(2, "CAr")
                d_uu = kb.deps_n(4, "uu")
                d_Xs = [[kb.deps_n(2) for k in range(4)] for c_ in range(2)]
                d_tf = kb.deps_n(3, "tf")
                d_tg = kb.deps_n(3, "tg")
                d_tb = kb.deps_n(6, "tb")
                d_rb = kb.deps_n(4, "rb")
                d_BuS = [kb.deps_n(2) for i in range(2)]
                d_zb = [kb.deps_n(2) for i in range(2)]
                d_y5s = kb.deps_n(2, "y5s")
                kb.dma("sp", iota[:], self.c_iota, writes=[d_iota])
                kb.op("dve", lambda e: e.memset(zero[:], 0.0), writes=[d_zero])
                for c_ in range(2):
                    for k in range(4):
                        for i in range(2):
                            kb.op("pool", lambda e: e.memset(Xs[c_][k][i][:], 0.0), writes=[d_Xs[c_][k][i]])
                t1, t2, t3, t4, wr, wi = tb_
                dt1, dt2, dt3, dt4, dwr, dwi = d_tb
                r1, r2, r3, r4 = rb_
                dr1, dr2, dr3, dr4 = d_rb

                def TT(eng, o, do, a, da, b_, db, op):
                    kb.op(eng, lambda e: e.tensor_tensor(out=o, in0=a, in1=b_, op=op), reads=da + db, writes=[do])

                def E_slice(ct, step):
                    cp = ct % 2
                    q0 = ct * 4
                    if step == 0:
                        kb.dma("sp", uTf[cp][:], self.sT[ct * 128:(ct + 1) * 128, :], writes=[d_uTf[cp]])
                    if step < 4:
                        k = step
                        q = q0 + k
                        kb.op("dve", lambda e: e.tensor_scalar(out=tg[0][:], in0=iota[:], scalar1=prm[:, PHI8, q:q + 1], scalar2=None, op0=ALU.mult),
                              reads=[d_iota, d_prm], writes=[d_tg[0]])
                        self.sincos_turns(tg[0][:], cosT[cp][:, k, :], sinT[cp][:, k, :], tg[1][:], angi[:], tg[2][:], d_tg[0], d_cos[cp], d_sin[cp], d_tg[1])
                    m = step
                    mi = m % 2
                    Brv = Bp[0][:, q0:q0 + 4, :]
                    Biv = Bp[1][:, q0:q0 + 4, :]
                    Arb = apw[:, m, 0, q0:q0 + 4].unsqueeze(2).to_broadcast([128, 4, 128])
                    Aib = apw[:, m, 1, q0:q0 + 4].unsqueeze(2).to_broadcast([128, 4, 128])
                    TT("dve", uu[0][:], d_uu[0], Brv, [d_Bp[0]], Arb, [d_apw], ALU.mult)
                    TT("dve", uu[1][:], d_uu[1], Biv, [d_Bp[1]], Aib, [d_apw], ALU.mult)
                    TT("dve", SN[mi][:, 0:4, :], d_SNr[mi], uu[0][:], [d_uu[0]], uu[1][:], [d_uu[1]], ALU.subtract)
                    TT("pool", uu[2][:], d_uu[2], Biv, [d_Bp[1]], Arb, [d_apw], ALU.mult)
                    TT("pool", uu[3][:], d_uu[3], Brv, [d_Bp[0]], Aib, [d_apw], ALU.mult)
                    TT("pool", SN[mi][:, 4:8, :], d_SN[mi], uu[2][:], [d_uu[2]], uu[3][:], [d_uu[3]], ALU.add)
                    j = step
                    C0v = CT[0][:, q0:q0 + 4, :]
                    C1v = CT[1][:, q0:q0 + 4, :]
                    Arb = apw[:, j + 1, 0, q0:q0 + 4].unsqueeze(2).to_broadcast([128, 4, 128])
                    Aib = apw[:, j + 1, 1, q0:q0 + 4].unsqueeze(2).to_broadcast([128, 4, 128])
                    TT("dve", uu[0][:], d_uu[0], C0v, [d_CT], Arb, [d_apw], ALU.mult)
                    TT("dve", uu[1][:], d_uu[1], C1v, [d_CT], Aib, [d_apw], ALU.mult)
                    TT("dve", CA[cp][:, j, 0:4, :], d_CAr[cp], uu[0][:], [d_uu[0]], uu[1][:], [d_uu[1]], ALU.add)
                    TT("pool", uu[2][:], d_uu[2], C1v, [d_CT], Arb, [d_apw], ALU.mult)
                    TT("pool", uu[3][:], d_uu[3], C0v, [d_CT], Aib, [d_apw], ALU.mult)
                    TT("pool", CA[cp][:, j, 4:8, :], d_CA[cp], uu[2][:], [d_uu[2]], uu[3][:], [d_uu[3]], ALU.subtract)

                def T_slice(ct, step):
                    cp = ct % 2
                    q0 = ct * 4
                    m = step
                    mi = m % 2
                    pb = ps[6][:].bitcast(BF16)
                    for j8 in range(8):
                        kb.op("pe", lambda e: e.transpose(out=pb[:, j8 * 128:(j8 + 1) * 128], in_=SN[mi][:, j8, :], identity=self.identb[:]),
                              reads=[d_SN[mi], d_SNr[mi], self.d_const], writes=[dps[6]])
                    kb.op("act", lambda e: e.activation(out=W1[:, m, :, :].rearrange("p a b -> p (a b)"), in_=pb[:, 0:1024], func=AF.Copy),
                          reads=[dps[6]], writes=[d_W1])
                    ksl = ps[7][:, 0:128]
                    for j8 in range(8):
                        i_, k_ = j8 // 4, j8 % 4
                        kb.op("pe", lambda e: e.matmul(ksl, lhsT=SN[mi][:, j8, :], rhs=CT[i_][:, q0 + k_, :], start=(j8 == 0), stop=(j8 == 7)),
                              reads=[d_SN[mi], d_SNr[mi], d_CT], writes=[dps[7]])
                    kb.op("act", lambda e: e.activation(out=KT[cp][:, m, :], in_=ksl, func=AF.Copy), reads=[dps[7]], writes=[d_KT[cp]])

                it_box = [0]

                def H_stage(ct):
                    cp = ct % 2
                    q0 = ct * 4
                    pending = []
                    for hh in range(NH):
                        for k in range(4):
                            q = q0 + k
                            sset = it_box[0] % 2
                            it_box[0] += 1
                            c = cosT[cp][:, k, :]
                            s_ = sinT[cp][:, k, :]
                            dc, ds = [d_cos[cp]], [d_sin[cp]]
                            for i in range(2):
                                bnk = 2 * sset + i
                                for j in range(Lc):
                                    kb.op("pe", lambda e: e.matmul(ps[bnk][:], lhsT=W1[:, Lc - 1 - j, i * 4 + k, :],
                                                                   rhs=uTf[cp][:, hh * 512 * Lc + j:(hh + 1) * 512 * Lc:Lc], start=(j == 0), stop=(j == Lc - 1)),
                                          reads=[d_W1, d_uTf[cp]], writes=[dps[bnk]])
                                kb.op("act", lambda e: e.activation(out=BuS[sset][i][:], in_=ps[bnk][:], func=AF.Copy), reads=[dps[bnk]], writes=[d_BuS[sset][i]])
                            Br, Bi = BuS[sset][0][:], BuS[sset][1][:]
                            dBr, dBi = [d_BuS[sset][0]], [d_BuS[sset][1]]
                            TT("dve", t1[:], dt1, Br, dBr, c, dc, ALU.mult)
                            TT("dve", t2[:], dt2, Bi, dBi, s_, ds, ALU.mult)
                            TT("dve", wr[:], dwr, t1[:], [dt1], t2[:], [dt2], ALU.add)
                            TT("pool", t3[:], dt3, Bi, dBi, c, dc, ALU.mult)
                            TT("pool", t4[:], dt4, Br, dBr, s_, ds, ALU.mult)
                            TT("pool", wi[:], dwi, t3[:], [dt3], t4[:], [dt4], ALU.subtract)
                            mb = prm[:, M8, q:q + 1].to_broadcast([128, 512])
                            par = hh % 2
                            for i, w_, dw_ in ((0, wr, dwr), (1, wi, dwi)):
                                init = zero[:, i:i + 1] if hh == 0 else czc[par][:, i, k:k + 1]
                                rd = [d_prm, dw_, d_zero] if hh == 0 else [d_prm, dw_, d_czc[par]]
                                kb.op("dve", lambda e: e.tensor_tensor_scan(out=zb[sset][i][:], data0=mb, data1=w_[:], initial=init, op0=ALU.mult, op1=ALU.add),
                                      reads=rd, writes=[d_zb[sset][i]])
                            if hh + 1 < NH:
                                nx = 1 - par
                                kb.op("dve", lambda e: e.tensor_copy(out=zl[:, 0:1], in_=zb[sset][0][:, 511:512]), reads=[d_zb[sset][0]], writes=[d_zl])
                                kb.op("dve", lambda e: e.tensor_copy(out=zl[:, 1:2], in_=zb[sset][1][:, 511:512]), reads=[d_zb[sset][1]], writes=[d_zl])
                                c5, s5 = prm[:, C512, q:q + 1], prm[:, S512, q:q + 1]
                                kb.op("dve", lambda e: e.tensor_scalar(out=czt[:, 0:1], in0=zl[:, 1:2], scalar1=s5, scalar2=None, op0=ALU.mult), reads=[d_zl, d_prm], writes=[d_czt])
                                kb.op("dve", lambda e: e.scalar_tensor_tensor(out=czc[nx][:, 0, k:k + 1], in0=zl[:, 0:1], scalar=c5, in1=czt[:, 0:1], op0=ALU.mult, op1=ALU.subtract),
                                      reads=[d_zl, d_prm, d_czt], writes=[d_czc[nx]])
                                kb.op("dve", lambda e: e.tensor_scalar(out=czt[:, 1:2], in0=zl[:, 1:2], scalar1=c5, scalar2=None, op0=ALU.mult), reads=[d_zl, d_prm], writes=[d_czt])
                                kb.op("dve", lambda e: e.scalar_tensor_tensor(out=czc[nx][:, 1, k:k + 1], in0=zl[:, 0:1], scalar=s5, in1=czt[:, 1:2], op0=ALU.mult, op1=ALU.add),
                                      reads=[d_zl, d_prm, d_czt], writes=[d_czc[nx]])

                            def back(sset=sset, c=c, s_=s_, dc=dc, ds=ds, k=k, hh=hh):
                                zbr, zbi = zb[sset][0][:], zb[sset][1][:]
                                dzbr, dzbi = [d_zb[sset][0]], [d_zb[sset][1]]
                                o0 = 1 + hh * 512
                                TT("pool", r1[:], dr1, zbr, dzbr, c, dc, ALU.mult)
                                TT("pool", r2[:], dr2, zbi, dzbi, s_, ds, ALU.mult)
                                TT("pool", Xs[cp][k][0][:, o0:o0 + 512], d_Xs[cp][k][0], r1[:], [dr1], r2[:], [dr2], ALU.subtract)
                                TT("dve", r3[:], dr3, zbr, dzbr, s_, ds, ALU.mult)
                                TT("dve", r4[:], dr4, zbi, dzbi, c, dc, ALU.mult)
                                TT("dve", Xs[cp][k][1][:, o0:o0 + 512], d_Xs[cp][k][1], r3[:], [dr3], r4[:], [dr4], ALU.add)

                            if pending:
                                pending.pop(0)()
                            pending.append(back)
                    while pending:
                        pending.pop(0)()

                iy_box = [0]

                def Y_mm(ct, blk):
                    cp = ct % 2
                    yb = 4 + (iy_box[0] % 2)
                    for j in range(Lc):
                        osl = ps[yb][:, j:512:Lc]
                        nck = 512 // Lc
                        for tau in range(j + 1):
                            kb.op("pe", lambda e: e.matmul(osl, lhsT=KT[cp][:, tau, :], rhs=uTf[cp][:, blk * 512 + j - tau:blk * 512 + 512:Lc], start=(tau == 0), stop=False),
                                  reads=[d_KT[cp], d_uTf[cp]], writes=[dps[yb]])
                        for k in range(4):
                            for i in range(2):
                                kb.op("pe", lambda e: e.matmul(osl, lhsT=CA[cp][:, j, i * 4 + k, :], rhs=Xs[cp][k][i][:, blk * nck:(blk + 1) * nck], start=False, stop=(k == 3 and i == 1)),
                                      reads=[d_CA[cp], d_CAr[cp], d_Xs[cp][k][i]], writes=[dps[yb]])

                def Y_epi(ct, blk):
                    cp = ct % 2
                    yb = 4 + (iy_box[0] % 2)
                    yi = iy_box[0] % 2
                    iy_box[0] += 1
                    kb.op("dve", lambda e: e.scalar_tensor_tensor(out=tf[0][:], in0=uTf[cp][:, blk * 512:(blk + 1) * 512], scalar=Dt[:, ct:ct + 1], in1=ps[yb][:],
                                                                  op0=ALU.mult, op1=ALU.add), reads=[d_uTf[cp], d_Dt, dps[yb]], writes=[d_tf[0]])
                    kb.op("act", lambda e: e.activation(out=tf[1][:], in_=tf[0][:], func=AF.Square), reads=[d_tf[0]], writes=[d_tf[1]])
                    kb.op("pool", lambda e: e.tensor_scalar(out=tf[1][:], in0=tf[1][:], scalar1=0.044715, scalar2=1.0, op0=ALU.mult, op1=ALU.add),
                          reads=[d_tf[1]], writes=[d_tf[1]])
                    kb.op("dve", lambda e: e.tensor_tensor(out=tf[1][:], in0=tf[1][:], in1=tf[0][:], op=ALU.mult), reads=[d_tf[1], d_tf[0]], writes=[d_tf[1]])
                    kb.op("act", lambda e: e.activation(out=tf[2][:], in_=tf[1][:], func=AF.Sigmoid, scale=1.5957691216057308), reads=[d_tf[1]], writes=[d_tf[2]])
                    kb.op("dve", lambda e: e.tensor_tensor(out=y5s[yi][:], in0=tf[0][:], in1=tf[2][:], op=ALU.mult), reads=[d_tf[0], d_tf[2]], writes=[d_y5s[yi]])
                    kb.dma("sp", self.y5d[ct * 128:(ct + 1) * 128, blk * 512:(blk + 1) * 512], y5s[yi][:], reads=[d_y5s[yi]])

                for step in range(Lc):
                    E_slice(0, step)
                    T_slice(0, step)
                H_stage(0)
                for ct in range(8):
                    nxt = ct + 1 < 8
                    if nxt:
                        E_slice(ct + 1, 0)
                    for blk in range(8):
                        if nxt and blk < Lc:
                            T_slice(ct + 1, blk)
                        Y_mm(ct, blk)
                        if nxt and blk + 1 < Lc:
                            E_slice(ct + 1, blk + 1)
                        Y_epi(ct, blk)
                    if nxt:
                        H_stage(ct + 1)
                kb.barrier()
            with ExitStack() as st4:
                sb4 = lambda n, s, d: st4.enter_context(self.sbt("S4_" + n, s, d))
                gluw = sb4("gluw", [128, 8, 1024], BF16)
                y5b = [sb4("y5b%d" % i, [128, 8, 512], BF16) for i in range(2)]
                NB = 3
                szT = [sb4("szT%d" % i, [128, 512], BF16) for i in range(NB)]
                og = [sb4("og%d" % i, [128, 512], BF16) for i in range(NB)]
                g1 = [sb4("g1%d" % i, [128, 512], BF16) for i in range(NB)]
                g2 = [sb4("g2%d" % i, [128, 512], BF16) for i in range(NB)]
                d_glu = kb.dep("glu")
                d_y5b = kb.deps_n(2, "y5b")
                d_sz = kb.deps_n(NB, "sz")
                d_og = kb.deps_n(NB, "og")
                d_g1 = kb.deps_n(NB, "g1")
                d_g2 = kb.deps_n(NB, "g2")
                kb.dma("pool", gluw[:], self.glu_w[l].rearrange("(k p) n -> p k n", p=128), writes=[d_glu])
                ig = 0
                for tb in range(8):
                    yi = tb % 2
                    kb.dma("sp", y5b[yi][:], self.y5d[:, tb * 512:(tb + 1) * 512].rearrange("(c p) t -> p c t", p=128), writes=[d_y5b[yi]])
                    for co in range(8):
                        i = ig % NB
                        bank = ig % 4
                        ig += 1
                        kb.dma("sp", szT[i][:], self.sT[1024 + co * 128:1024 + (co + 1) * 128, tb * 512:(tb + 1) * 512], writes=[d_sz[i]])
                        for ci in range(8):
                            kb.op("pe", lambda e: e.matmul(ps[bank][:], lhsT=gluw[:, ci, co * 128:(co + 1) * 128], rhs=y5b[yi][:, ci, :],
                                                           start=(ci == 0), stop=(ci == 7)), reads=[d_glu, d_y5b[yi]], writes=[dps[bank]])
                        kb.op("act", lambda e: e.activation(out=g1[i][:], in_=ps[bank][:], func=AF.Sigmoid), reads=[dps[bank]], writes=[d_g1[i]])
                        kb.op("act", lambda e: e.activation(out=g2[i][:], in_=szT[i][:], func=AF.Sigmoid), reads=[d_sz[i]], writes=[d_g2[i]])
                        kb.op("dve", lambda e: e.tensor_tensor(out=g2[i][:], in0=g2[i][:], in1=szT[i][:], op=ALU.mult), reads=[d_g2[i], d_sz[i]], writes=[d_g2[i]])
                        kb.op("dve", lambda e: e.tensor_tensor(out=g1[i][:], in0=g1[i][:], in1=y5b[yi][:, co, :], op=ALU.mult),
                              reads=[d_g1[i], d_y5b[yi]], writes=[d_g1[i]])
                        kb.op("dve", lambda e: e.tensor_tensor(out=og[i][:], in0=g1[i][:], in1=g2[i][:], op=ALU.mult), reads=[d_g1[i], d_g2[i]], writes=[d_og[i]])
                        kb.dma("pool", self.mixedT[1024 + co * 128:1024 + (co + 1) * 128, tb * 512:(tb + 1) * 512], og[i][:], reads=[d_og[i]])

    def qk_prep(self, tag, src, col0, nh, normw_dram, l, dstT, d_dst, rope_tab, d_rope, ntiles=NT, bank=4):
        nc, kb = self.nc, self.kb
        ps, dps = self.ps, self.dps
        G = 4 if ntiles % 4 == 0 else 2
        NBUF = 4
        with ExitStack() as st:
            sb = lambda n, s, d: st.enter_context(self.sbt("P_%s_%s" % (tag, n), s, d))
            W = nh * 128
            GH = G * nh
            nw = sb("nw", [128, 128], F32)
            qraw = [sb("qraw%d" % i, [128, G, W], BF16) for i in range(NBUF)]
            sq = [sb("sq%d" % i, [128, GH, 128], BF16) for i in range(NBUF)]
            ss = [sb("ss%d" % i, [128, GH], F32) for i in range(NBUF)]
            qn = [sb("qn%d" % i, [128, GH, 128], F32) for i in range(NBUF)]
            rt = [sb("rt%d" % i, [128, 4, GH, 16], F32) for i in range(NBUF)]
            qb = [sb("qb%d" % i, [128, GH, 128], BF16) for i in range(NBUF)]
            d_nw = kb.dep()
            d_qraw = kb.deps_n(NBUF)
            d_sq = kb.deps_n(NBUF)
            d_ss = kb.deps_n(NBUF)
            d_qn = kb.deps_n(NBUF)
            d_rt = kb.deps_n(NBUF)
            d_qb = kb.deps_n(NBUF)
            kb.dma("sp", nw[:], normw_dram[l:l + 1, :].partition_broadcast(128), writes=[d_nw])
            def f1(g):
                i = g % NBUF
                t0 = g * G
                kb.dma("sp", qraw[i][:], src[t0 * 128:(t0 + G) * 128, col0:col0 + W].rearrange("(g p) c -> p g c", p=128), writes=[d_qraw[i]])
                qv = qraw[i][:].rearrange("p g (h c) -> p (g h) c", c=128)
                kb.op("pool", lambda e: e.tensor_tensor(out=sq[i][:], in0=qv, in1=qv, op=ALU.mult), reads=[d_qraw[i]], writes=[d_sq[i]])
                kb.op("dve", lambda e: e.tensor_reduce(out=ss[i][:], in_=sq[i][:], axis=AX.X, op=ALU.add), reads=[d_sq[i]], writes=[d_ss[i]])
                kb.op("dve", lambda e: e.tensor_scalar(out=ss[i][:], in0=ss[i][:], scalar1=1.0 / 128, scalar2=EPS, op0=ALU.mult, op1=ALU.add),
                      reads=[d_ss[i]], writes=[d_ss[i]])
                kb.op("act", lambda e: e.activation(out=ss[i][:], in_=ss[i][:], func=AF.Sqrt), reads=[d_ss[i]], writes=[d_ss[i]])
                kb.op("dve", lambda e: e.reciprocal(out=ss[i][:], in_=ss[i][:]), reads=[d_ss[i]], writes=[d_ss[i]])
            def f2(g):
                i = g % NBUF
                t0 = g * G
                qv = qraw[i][:].rearrange("p g (h c) -> p (g h) c", c=128)
                kb.op("dve", lambda e: e.tensor_tensor(out=qn[i][:], in0=qv, in1=ss[i][:].unsqueeze(2).to_broadcast([128, GH, 128]), op=ALU.mult),
                      reads=[d_qraw[i], d_ss[i]], writes=[d_qn[i]])
                kb.op("pool", lambda e: e.tensor_tensor(out=qn[i][:], in0=qn[i][:], in1=nw[:].unsqueeze(1).to_broadcast([128, GH, 128]), op=ALU.mult),
                      reads=[d_qn[i], d_nw], writes=[d_qn[i]])
                cb = rope_tab[:, t0:t0 + G, 0:16].unsqueeze(2).to_broadcast([128, G, nh, 16])
                sbb = rope_tab[:, t0:t0 + G, 16:32].unsqueeze(2).to_broadcast([128, G, nh, 16])
                q4 = qn[i][:].rearrange("p (g h) c -> p g h c", g=G)
                x1 = q4[:, :, :, 0:16]
                x2 = q4[:, :, :, 16:32]
                rv = lambda j: rt[i][:, j].rearrange("p (g h) c -> p g h c", g=G)
                R_ = [d_qn[i], d_rope]
                kb.op("dve", lambda e: e.tensor_tensor(out=rv(0), in0=x1, in1=cb, op=ALU.mult), reads=R_, writes=[d_rt[i]])
                kb.op("dve", lambda e: e.tensor_tensor(out=rv(1), in0=x2, in1=sbb, op=ALU.mult), reads=R_, writes=[d_rt[i]])
                kb.op("dve", lambda e: e.tensor_tensor(out=rv(2), in0=x2, in1=cb, op=ALU.mult), reads=R_, writes=[d_rt[i]])
                kb.op("dve", lambda e: e.tensor_tensor(out=rv(3), in0=x1, in1=sbb, op=ALU.mult), reads=R_, writes=[d_rt[i]])
                kb.op("dve", lambda e: e.tensor_tensor(out=qb[i][:, :, 0:16], in0=rt[i][:, 0], in1=rt[i][:, 1], op=ALU.subtract), reads=[d_rt[i]], writes=[d_qb[i]])
                kb.op("dve", lambda e: e.tensor_tensor(out=qb[i][:, :, 16:32], in0=rt[i][:, 2], in1=rt[i][:, 3], op=ALU.add), reads=[d_rt[i]], writes=[d_qb[i]])
                kb.op("act", lambda e: e.activation(out=qb[i][:, :, 32:128], in_=qn[i][:, :, 32:128], func=AF.Copy), reads=[d_qn[i]], writes=[d_qb[i]])
            def f3(g):
                i = g % NBUF
                t0 = g * G
                nb = (GH + 7) // 8
                for bi in range(nb):
                    bk = bank + ((g * nb + bi) % 4)
                    pb = ps[bk][:].bitcast(BF16)
                    n_here = min(8, GH - bi * 8)
                    for j in range(n_here):
                        kb.op("pe", lambda e: e.transpose(out=pb[:, j * 128:(j + 1) * 128], in_=qb[i][:, bi * 8 + j, :], identity=self.identb[:]),
                              reads=[d_qb[i], self.d_const], writes=[dps[bk]])
                    ng = n_here // nh
                    gt0 = t0 + (bi * 8) // nh
                    dst = dstT[:, :, gt0 * 128:(gt0 + ng) * 128].rearrange("p h (g n) -> p g h n", g=ng)
                    srcp = pb[:, 0:n_here * 128].rearrange("p (g h n) -> p g h n", g=ng, h=nh)
                    eng = "act" if bi % 2 == 0 else "dve"
                    if eng == "act":
                        kb.op("act", lambda e: e.activation(out=dst, in_=srcp, func=AF.Copy), reads=[dps[bk]], writes=[d_dst])
                    else:
                        kb.op("dve", lambda e: e.tensor_copy(out=dst, in_=srcp), reads=[dps[bk]], writes=[d_dst])
            self.emit_pipelined(ntiles // G, [f1, f2, f3])
            kb.barrier()

    def phase_MOBA(self, l):
        nc, kb = self.nc, self.kb
        ps, dps = self.ps, self.dps
        SC = 1.0 / math.sqrt(128.0)
        with ExitStack() as st:
            sb = lambda n, s, d: st.enter_context(self.sbt("M_" + n, s, d))
            QT = sb("QT", [128, 4, S], BF16)
            KT = sb("KT", [128, 4, S], BF16)
            Vp = sb("Vp", [128, NT, 4, 130], BF16)
            rope = sb("rope", [128, NT, 32], F32)
            tri = sb("tri", [128, 128], BF16)
            kmf = sb("kmf", [128, 4, 16], F32)
            kmT = sb("kmT", [128, 4, 16], BF16)
            d_QT, d_KT, d_Vp, d_SEL, d_OM, d_rope, d_tri, d_km = kb.deps_n(8, "mb")
            kb.dma("sp", rope[:], self.c_rope.rearrange("(t p) c -> p t c", p=128), writes=[d_rope])
            kb.dma("sp", tri[:], self.c_tri, writes=[d_tri])
            kb.op("pool", lambda e: e.memset(Vp[:].rearrange("p a b c -> p (a b c)"), 1.0), writes=[d_Vp])
            for h in range(4):
                kb.dma("sp", Vp[:, :, h, 0:128], self.proj_tm[:, C_MV + h * 128:C_MV + (h + 1) * 128].rearrange("(t p) c -> p t c", p=128),
                       reads=[d_Vp], writes=[d_Vp])
            self.qk_prep("mq", self.proj_tm, C_MQ, 4, self.hn["moba_q_norm"], l, QT, d_QT, rope, d_rope)
            self.qk_prep("mk", self.proj_tm, C_MK, 4, self.hn["moba_k_norm"], l, KT, d_KT, rope, d_rope)
            kb.barrier()
            SEL = sb("SEL", [128, NT, 4, 16], F32)
            OM = sb("OM", [128, NT, 512], BF16)
            kb.op("dve", lambda e: e.memset(SEL[:].rearrange("p a b c -> p (a b c)"), 1.0), writes=[d_SEL])
            for h in range(4):
                kb.op("dve", lambda e: e.tensor_reduce(out=kmf[:, h, :], in_=KT[:, h, :].rearrange("p (n k) -> p n k", k=256), axis=AX.X, op=ALU.add),
                      reads=[d_KT], writes=[d_km])
            kb.op("dve", lambda e: e.tensor_scalar(out=kmT[:], in0=kmf[:], scalar1=1.0 / 256, scalar2=None, op0=ALU.mult), reads=[d_km], writes=[d_km])
            with ExitStack() as st2:
                sb2 = lambda n, s, d: st2.enter_context(self.sbt("M2_" + n, s, d))
                gt = [sb2("gt%d" % i, [128, 4, 16], F32) for i in range(2)]
                m8 = [sb2("m8%d" % i, [128, 4, 8], F32) for i in range(2)]
                d_gt = kb.deps_n(2)
                d_m8 = kb.deps_n(2)
                for tt in range(8, NT):
                    own = tt // 2
                    i = tt % 2
                    for h in range(4):
                        kb.op("pe", lambda e: e.matmul(ps[4][:, h * 16:(h + 1) * 16], lhsT=QT[:, h, tt * 128:(tt + 1) * 128], rhs=kmT[:, h, :], start=True, stop=True),
                              reads=[d_QT, d_km], writes=[dps[4]])
                    kb.op("dve", lambda e: e.tensor_copy(out=gt[i][:].rearrange("p a b -> p (a b)"), in_=ps[4][:, 0:64]), reads=[dps[4]], writes=[d_gt[i]])
                    kb.op("dve", lambda e: e.memset(gt[i][:, :, own:16], NEG), reads=[d_gt[i]], writes=[d_gt[i]])
                    for h in range(4):
                        kb.op("dve", lambda e: e.max(out=m8[i][:, h, :], in_=gt[i][:, h, :]), reads=[d_gt[i]], writes=[d_m8[i]])
                    for h in range(4):
                        kb.op("dve", lambda e: e.tensor_scalar(out=SEL[:, tt, h, :], in0=gt[i][:, h, :], scalar1=m8[i][:, h, 2:3], scalar2=None, op0=ALU.is_ge),
                              reads=[d_gt[i], d_m8[i]], writes=[d_SEL])
            kb.barrier()
            PT = [sb("PT%d" % i, [128, 512], BF16) for i in range(5)]
            acc = [sb("acc%d" % i, [128, 2, 130], F32) for i in range(2)]
            rr = [sb("rr%d" % i, [128, 2], F32) for i in range(2)]
            d_PT = kb.deps_n(5)
            d_acc = kb.deps_n(2)
            d_rr = kb.deps_n(2)
            iters = [(h, qb, n) for h in range(4) for qb in range(16) for n in range(qb + 1)]

            def front(idx):
                h, qb, n = iters[idx]
                pi = idx % 5
                sbank = (0, 1, 2, 5)[idx % 4]
                for kt in range(2):
                    kb.op("pe", lambda e: e.matmul(ps[sbank][:, kt * 256:(kt + 1) * 256], lhsT=KT[:, h, (2 * n + kt) * 128:(2 * n + kt + 1) * 128],
                                                   rhs=QT[:, h, qb * 256:(qb + 1) * 256], start=True, stop=True),
                          reads=[d_KT, d_QT], writes=[dps[sbank]])
                kb.op("act", lambda e: e.activation(out=PT[pi][:], in_=ps[sbank][:], func=AF.Exp, scale=SC), reads=[dps[sbank]], writes=[d_PT[pi]])
                if n == qb:
                    kb.op("pool", lambda e: e.tensor_tensor(out=PT[pi][:, 0:128], in0=PT[pi][:, 0:128], in1=tri[:], op=ALU.mult),
                          reads=[d_PT[pi], d_tri], writes=[d_PT[pi]])
                    kb.op("pool", lambda e: e.tensor_tensor(out=PT[pi][:, 384:512], in0=PT[pi][:, 384:512], in1=tri[:], op=ALU.mult),
                          reads=[d_PT[pi], d_tri], writes=[d_PT[pi]])

            def back(idx):
                h, qb, n = iters[idx]
                pi = idx % 5
                obank = (3, 4, 6)[idx % 3]
                ai = (h * 16 + qb) % 2
                if n == 0:
                    kb.op("pool", lambda e: e.memset(acc[ai][:].rearrange("p a b -> p (a b)"), 0.0), writes=[d_acc[ai]])
                if n < qb:
                    for qt in range(2):
                        for kt in range(2):
                            kb.op("pe", lambda e: e.matmul(ps[obank][:, qt * 256:qt * 256 + 129], lhsT=PT[pi][:, kt * 256 + qt * 128:kt * 256 + (qt + 1) * 128],
                                                           rhs=Vp[:, 2 * n + kt, h, 0:129], start=(kt == 0), stop=(kt == 1)),
                                  reads=[d_PT[pi], d_Vp], writes=[dps[obank]])
                    for qt in range(2):
                        kb.op("dve", lambda e: e.scalar_tensor_tensor(out=acc[ai][:, qt, 0:129], in0=ps[obank][:, qt * 256:qt * 256 + 129],
                                                                      scalar=SEL[:, 2 * qb + qt, h, n:n + 1], in1=acc[ai][:, qt, 0:129],
                                                                      op0=ALU.mult, op1=ALU.add),
                              reads=[dps[obank], d_SEL, d_acc[ai]], writes=[d_acc[ai]])
                else:
                    kb.op("pe", lambda e: e.matmul(ps[obank][:, 0:129], lhsT=PT[pi][:, 0:128], rhs=Vp[:, 2 * qb, h, 0:129], start=True, stop=True),
                          reads=[d_PT[pi], d_Vp], writes=[dps[obank]])
                    kb.op("pe", lambda e: e.matmul(ps[obank][:, 256:256 + 129], lhsT=PT[pi][:, 128:256], rhs=Vp[:, 2 * qb, h, 0:129], start=True, stop=False),
                          reads=[d_PT[pi], d_Vp], writes=[dps[obank]])
                    kb.op("pe", lambda e: e.matmul(ps[obank][:, 256:256 + 129], lhsT=PT[pi][:, 384:512], rhs=Vp[:, 2 * qb + 1, h, 0:129], start=False, stop=True),
                          reads=[d_PT[pi], d_Vp], writes=[dps[obank]])
                    for qt in range(2):
                        kb.op("dve", lambda e: e.tensor_tensor(out=acc[ai][:, qt, 0:129], in0=ps[obank][:, qt * 256:qt * 256 + 129], in1=acc[ai][:, qt, 0:129], op=ALU.add),
                              reads=[dps[obank], d_acc[ai]], writes=[d_acc[ai]])
                    kb.op("dve", lambda e: e.reciprocal(out=rr[ai][:], in_=acc[ai][:, :, 128]), reads=[d_acc[ai]], writes=[d_rr[ai]])
                    for qt in range(2):
                        kb.op("dve", lambda e: e.tensor_scalar(out=OM[:, 2 * qb + qt, h * 128:(h + 1) * 128], in0=acc[ai][:, qt, 0:128], scalar1=rr[ai][:, qt:qt + 1],
                                                               scalar2=None, op0=ALU.mult), reads=[d_acc[ai], d_rr[ai]], writes=[d_OM])

            SK = 3
            for idx in range(min(SK, len(iters))):
                front(idx)
            for idx in range(len(iters)):
                if idx + SK < len(iters):
                    front(idx + SK)
                back(idx)
            kb.barrier()
            self.gate_and_store(OM, d_OM, C_MZ, 0)

    def gate_and_store(self, OM, d_OM, zcol, row0):
        nc, kb = self.nc, self.kb
        ps, dps = self.ps, self.dps
        with ExitStack() as st:
            sb = lambda n, s, d: st.enter_context(self.sbt("G_" + n, s, d))
            zt = [sb("zt%d" % i, [128, 512], BF16) for i in range(4)]
            sl = [sb("sl%d" % i, [128, 512], F32) for i in range(4)]
            gg = [sb("gg%d" % i, [128, 512], BF16) for i in range(4)]
            oT = [sb("oT%d" % i, [128, 4, 128], BF16) for i in range(4)]
            d_zt = kb.deps_n(4)
            d_sl = kb.deps_n(4)
            d_gg = kb.deps_n(4)
            d_oT = kb.deps_n(4)
            def g1(tt):
                i = tt % 4
                kb.dma("sp", zt[i][:], self.proj_tm[tt * 128:(tt + 1) * 128, zcol:zcol + 512], writes=[d_zt[i]])
                kb.op("act", lambda e: e.activation(out=sl[i][:], in_=zt[i][:], func=AF.Silu), reads=[d_zt[i]], writes=[d_sl[i]])
                kb.op("dve", lambda e: e.tensor_tensor(out=gg[i][:], in0=OM[:, tt, :], in1=sl[i][:], op=ALU.mult), reads=[d_OM, d_sl[i]], writes=[d_gg[i]])
            def g2(tt):
                i = tt % 4
                bk = 4 + i
                pb = ps[bk][:].bitcast(BF16)
                for h in range(4):
                    kb.op("pe", lambda e: e.transpose(out=pb[:, h * 128:(h + 1) * 128], in_=gg[i][:, h * 128:(h + 1) * 128], identity=self.identb[:]),
                          reads=[d_gg[i], self.d_const], writes=[dps[bk]])
                kb.op("act", lambda e: e.activation(out=oT[i][:].rearrange("p h n -> p (h n)"), in_=pb[:, 0:512], func=AF.Copy), reads=[dps[bk]], writes=[d_oT[i]])
                kb.dma("pool", self.mixedT[row0:row0 + 512, tt * 128:(tt + 1) * 128].rearrange("(h p) n -> p h n", p=128), oT[i][:], reads=[d_oT[i]])
            self.emit_pipelined(NT, [g1, g2])

    def gelu_tanh(self, x, dx, tmp, dtmp, out, dout):
        kb = self.kb
        kb.op("act", lambda e: e.activation(out=tmp, in_=x, func=AF.Square), reads=[dx], writes=[dtmp])
        kb.op("dve", lambda e: e.tensor_scalar(out=tmp, in0=tmp, scalar1=0.044715, scalar2=1.0, op0=ALU.mult, op1=ALU.add), reads=[dtmp], writes=[dtmp])
        kb.op("dve", lambda e: e.tensor_tensor(out=tmp, in0=tmp, in1=x, op=ALU.mult), reads=[dtmp, dx], writes=[dtmp])
        kb.op("act", lambda e: e.activation(out=tmp, in_=tmp, func=AF.Sigmoid, scale=1.5957691216057308), reads=[dtmp], writes=[dtmp])
        kb.op("dve", lambda e: e.tensor_tensor(out=out, in0=x, in1=tmp, op=ALU.mult), reads=[dx, dtmp], writes=[dout])

    def phase_NSA(self, l):
        nc, kb = self.nc, self.kb
        ps, dps = self.ps, self.dps
        SC = 1.0 / math.sqrt(128.0)
        with ExitStack() as st:
            sb = lambda n, s, d: st.enter_context(self.sbt("N_" + n, s, d))
            NQT = sb("NQT", [128, 4, S], BF16)
            KST = sb("KST", [128, 1, S], BF16)
            KWT = sb("KWT", [128, 1, S], BF16)
            KCT = sb("KCT", [128, 1, 256], BF16)
            VS = sb("VS", [128, NT, 130], BF16)
            VW = sb("VW", [128, NT, 130], BF16)
            RC = sb("RC", [128, 2, 196], BF16)
            SELT = sb("SELT", [64, NT, 128], BF16)
            ESEL = sb("ESEL", [64, NT, 128], BF16)
            G = sb("G", [128, NT, 12], F32)
            rope = sb("rope", [128, NT, 32], F32)
            ropec = sb("ropec", [128, 2, 32], F32)
            tri = sb("tri", [128, 128], BF16)
            triu = sb("triu", [128, 128], BF16)
            dkq = sb("dkq", [128, 128], F32)
            d_NQT, d_KST, d_KWT, d_KCT, d_VS, d_VW, d_RC, d_ONS, d_SELT, d_G, d_rope, d_cst = kb.deps_n(12, "ns")
            kb.dma("sp", rope[:], self.c_rope.rearrange("(t p) c -> p t c", p=128), writes=[d_rope])
            kb.dma("sp", ropec[:], self.c_ropec.rearrange("(t p) c -> p t c", p=128), writes=[d_rope])
            kb.dma("sp", tri[:], self.c_tri, writes=[d_cst])
            kb.dma("sp", triu[:], self.c_triu, writes=[d_cst])
            kb.dma("sp", dkq[:], self.c_dkq, writes=[d_cst])
            kb.dma("sp", ESEL[:], self.c_esel, writes=[d_cst])
            kb.op("pool", lambda e: e.memset(VS[:].rearrange("p a b -> p (a b)"), 1.0), writes=[d_VS])
            kb.op("pool", lambda e: e.memset(VW[:].rearrange("p a b -> p (a b)"), 1.0), writes=[d_VW])
            kb.op("pool", lambda e: e.memset(RC[:].rearrange("p a b -> p (a b)"), 1.0), writes=[d_RC])
            kb.dma("sp", VS[:, :, 0:128], self.proj_tm[:, C_NVS:C_NVS + 128].rearrange("(t p) c -> p t c", p=128), reads=[d_VS], writes=[d_VS])
            kb.dma("sp", VW[:, :, 0:128], self.proj_tm[:, C_NVW:C_NVW + 128].rearrange("(t p) c -> p t c", p=128), reads=[d_VW], writes=[d_VW])
            kb.dma("sp", RC[:, :, 129:193], self.c_ovl.rearrange("(t p) j -> p t j", p=128), reads=[d_RC], writes=[d_RC])
            kb.dma("pool", G[:], self.proj_tm[:, C_NG:C_NG + 12].rearrange("(t p) c -> p t c", p=128), writes=[d_G])
            kb.op("act", lambda e: e.activation(out=G[:].rearrange("p a b -> p (a b)"), in_=G[:].rearrange("p a b -> p (a b)"), func=AF.Sigmoid),
                  reads=[d_G], writes=[d_G])
            self.qk_prep("nq", self.proj_tm, C_NQ, 4, self.hn["nsa_q_norm"], l, NQT, d_NQT, rope, d_rope)
            self.qk_prep("nks", self.proj_tm, C_NKS, 1, self.hn["nsa_ks_norm"], l, KST, d_KST, rope, d_rope)
            self.qk_prep("nkw", self.proj_tm, C_NKW, 1, self.hn["nsa_kw_norm"], l, KWT, d_KWT, rope, d_rope)
            with ExitStack() as st2:
                sb2 = lambda n, s, d: st2.enter_context(self.sbt("N2_" + n, s, d))
                XcT = sb2("XcT", [128, 2, S], BF16)
                w1b = [sb2("w1b%d" % i, [128, 32, 128], BF16) for i in range(2)]
                w2b = [sb2("w2b%d" % i, [128, 128], BF16) for i in range(2)]
                pef = sb2("pef", [32, 2, 128], F32)
                peT = sb2("peT", [128, 2, 32], BF16)
                cvec = sb2("cvec", [128, 2], F32)
                xr = [sb2("xr%d" % i, [128, 256], BF16) for i in range(2)]
                hs = sb2("hs", [128, 256], F32)
                htmp = sb2("htmp", [128, 256], F32)
                hT = [sb2("hT%d" % i, [128, 256], BF16) for i in range(2)]
                kcs = sb2("kcs", [128, 2, 128], BF16)
                d_XcT, d_w1, d_w2, d_pef, d_peT, d_cvec, d_hs, d_htmp, d_kcs = kb.deps_n(9, "cm")
                d_xr = kb.deps_n(2)
                d_hT = kb.deps_n(2)
                w1s = [self.ck_w1, self.cv_w1]
                w2s = [self.ck_w2, self.cv_w2]
                pes = [self.pe_k, self.pe_v]
                for j in range(2):
                    kb.dma("pool", w1b[j][:], w1s[j][l].rearrange("(l d) o -> d l o", d=128), writes=[d_w1])
                    kb.dma("pool", w2b[j][:], w2s[j][l], writes=[d_w2])
                    kb.dma("sp", pef[:, j, :], pes[j][l], writes=[d_pef])
                for j in range(2):
                    kb.op("pe", lambda e: e.transpose(out=ps[0][:, j * 32:(j + 1) * 32], in_=pef[:, j, :], identity=self.identf[0:32, 0:32]),
                          reads=[d_pef, self.d_const], writes=[dps[0]])
                kb.op("dve", lambda e: e.tensor_copy(out=peT[:].rearrange("p a b -> p (a b)"), in_=ps[0][:, 0:64]), reads=[dps[0]], writes=[d_peT])
                for tt in range(NT):
                    i = tt % 2
                    kb.dma("sp", xr[i][:], self.proj_tm[tt * 128:(tt + 1) * 128, C_NKC:C_NKC + 256], writes=[d_xr[i]])
                    bk = 5 + i
                    pb = ps[bk][:].bitcast(BF16)
                    for j in range(2):
                        kb.op("pe", lambda e: e.transpose(out=pb[:, j * 128:(j + 1) * 128], in_=xr[i][:, j * 128:(j + 1) * 128], identity=self.identb[:]),
                              reads=[d_xr[i], self.d_const], writes=[dps[bk]])
                    kb.op("dve", lambda e: e.tensor_copy(out=XcT[:, :, tt * 128:(tt + 1) * 128], in_=pb[:, 0:256].rearrange("p (j n) -> p j n", j=2)),
                          reads=[dps[bk]], writes=[d_XcT])
                for j in range(2):
                    for ll in range(32):
                        kb.op("pe", lambda e: e.matmul(ps[1][:, j:j + 1], lhsT=w1b[j][:, ll, :], rhs=peT[:, j, ll:ll + 1], start=(ll == 0), stop=(ll == 31)),
                              reads=[d_w1, d_peT], writes=[dps[1]])
                    kb.op("dve", lambda e: e.tensor_copy(out=cvec[:, j:j + 1], in_=ps[1][:, j:j + 1]), reads=[dps[1]], writes=[d_cvec])
                    for ll in range(32):
                        kb.op("pe", lambda e: e.matmul(ps[2 + j][:, 0:255], lhsT=w1b[j][:, ll, :], rhs=XcT[:, j, ll:ll + 16 * 254 + 1:16], start=(ll == 0), stop=(ll == 31)),
                              reads=[d_w1, d_XcT], writes=[dps[2 + j]])
                    kb.op("dve", lambda e: e.tensor_scalar(out=hs[:, 0:255], in0=ps[2 + j][:, 0:255], scalar1=cvec[:, j:j + 1], scalar2=None, op0=ALU.add),
                          reads=[dps[2 + j], d_cvec], writes=[d_hs])
                    kb.op("pool", lambda e: e.memset(hT[j][:], 0.0), writes=[d_hT[j]])
                    self.gelu_tanh(hs[:, 0:255], d_hs, htmp[:, 0:255], d_htmp, hT[j][:, 0:255], d_hT[j])
                    for it in range(2):
                        kb.op("pe", lambda e: e.matmul(ps[4][:, it * 128:(it + 1) * 128], lhsT=hT[j][:, it * 128:(it + 1) * 128], rhs=w2b[j][:], start=True, stop=True),
                              reads=[d_hT[j], d_w2], writes=[dps[4]])
                    if j == 0:
                        kb.op("dve", lambda e: e.tensor_copy(out=kcs[:].rearrange("p a b -> p (a b)"), in_=ps[4][:, 0:256]), reads=[dps[4]], writes=[d_kcs])
                        kb.dma("sp", self.kcmp_tm.rearrange("(t p) c -> p t c", p=128), kcs[:], reads=[d_kcs])
                    else:
                        kb.op("dve", lambda e: e.tensor_copy(out=RC[:, :, 0:128], in_=ps[4][:, 0:256].rearrange("p (a b) -> p a b", a=2)),
                              reads=[dps[4], d_RC], writes=[d_RC])
                kb.barrier()
            self.qk_prep("nkc", self.kcmp_tm, 0, 1, self.hn["nsa_kc_norm"], l, KCT, d_KCT, ropec, d_rope, ntiles=2)
            ONS = sb("ONS", [128, NT, 512], F32)
            PT = [sb("PT%d" % i, [128, 4, 128], BF16) for i in range(8)]
            M2s = [sb("M2s%d" % i, [128, 128], BF16) for i in range(4)]
            sA = [sb("sA%d" % i, [128, 64], F32) for i in range(2)]
            sB = [sb("sB%d" % i, [128, 64], F32) for i in range(2)]
            imp = [sb("imp%d" % i, [128, 64], F32) for i in range(2)]
            sc = [sb("sc%d" % i, [128, 2, 64], F32) for i in range(2)]
            m8 = [sb("m8%d" % i, [128, 2, 8], F32) for i in range(2)]
            selq = [sb("selq%d" % i, [128, 64], BF16) for i in range(2)]
            rr = [sb("rr%d" % i, [128, 8], F32) for i in range(2)]
            d_PT = kb.deps_n(8)
            d_M2s = kb.deps_n(4)
            d_m2p = kb.deps_n(4)
            d_sA = kb.deps_n(2)
            d_sB = kb.deps_n(2)
            d_imp = kb.deps_n(2)
            d_sc = kb.deps_n(2)
            d_m8 = kb.deps_n(2)
            d_selq = kb.deps_n(2)
            d_rr = kb.deps_n(2)
            ip = 0

            def obank(tt, h):
                return 5 + h // 2, (h % 2) * 256

            zl = sb("zl", [128, 128], BF16)
            zr_ = sb("zr", [128, 512], BF16)
            d_z = kb.dep("zeros")
            kb.op("pool", lambda e: e.memset(zl[:], 0.0), writes=[d_z])
            kb.op("pool", lambda e: e.memset(zr_[:], 0.0), writes=[d_z])

            def zero_obanks():
                for b in (5, 6):
                    kb.op("pe", lambda e: e.matmul(ps[b][:], lhsT=zl[:], rhs=zr_[:], start=True, stop=True), reads=[d_z], writes=[dps[b]])

            def finalize(tt, branch, first):
                i = tt % 2
                for h in range(4):
                    b, c0 = obank(tt, h)
                    kb.op("dve", lambda e: e.tensor_scalar(out=rr[i][:, h:h + 1], in0=ps[b][:, c0 + 128:c0 + 129], scalar1=1e-30, scalar2=None, op0=ALU.max),
                          reads=[dps[b]], writes=[d_rr[i]])
                kb.op("dve", lambda e: e.reciprocal(out=rr[i][:, 0:4], in_=rr[i][:, 0:4]), reads=[d_rr[i]], writes=[d_rr[i]])
                kb.op("dve", lambda e: e.tensor_tensor(out=rr[i][:, 4:8], in0=rr[i][:, 0:4], in1=G[:, tt, branch:12:3], op=ALU.mult), reads=[d_rr[i], d_G], writes=[d_rr[i]])
                for h in range(4):
                    b, c0 = obank(tt, h)
                    dst = ONS[:, tt, h * 128:(h + 1) * 128]
                    if first:
                        kb.op("dve", lambda e: e.tensor_scalar(out=dst, in0=ps[b][:, c0:c0 + 128], scalar1=rr[i][:, 4 + h:5 + h], scalar2=None, op0=ALU.mult),
                              reads=[dps[b], d_rr[i]], writes=[d_ONS])
                    else:
                        kb.op("dve", lambda e: e.scalar_tensor_tensor(out=dst, in0=ps[b][:, c0:c0 + 128], scalar=rr[i][:, 4 + h:5 + h], in1=dst, op0=ALU.mult, op1=ALU.add),
                              reads=[dps[b], d_rr[i], d_ONS], writes=[d_ONS])

            it_cmp = [(tt, it) for tt in range(NT) for it in range(1 if tt < 16 else 2)]

            def c_front(idx):
                tt, it = it_cmp[idx]
                pi = idx % 3
                sbank = idx % 2
                i = tt % 2
                if it == 0:
                    kb.dma("sp", sA[i][:], self.c_selA[tt], writes=[d_sA[i]])
                    kb.dma("sp", sB[i][:], self.c_selB[tt], writes=[d_sB[i]])
                kb.op("pe", lambda e: e.matmul(ps[sbank][:], lhsT=KCT[:, 0, it * 128:(it + 1) * 128], rhs=NQT[:, :, tt * 128:(tt + 1) * 128], start=True, stop=True),
                      reads=[d_KCT, d_NQT], writes=[dps[sbank]])
                kb.op("act", lambda e: e.activation(out=PT[pi][:].rearrange("p a b -> p (a b)"), in_=ps[sbank][:], func=AF.Exp, scale=SC),
                      reads=[dps[sbank]], writes=[d_PT[pi]])
                thr = float(31 + 2048 * it - 128 * tt)
                kb.op("dve", lambda e: e.scalar_tensor_tensor(out=PT[pi][:], in0=dkq[:].unsqueeze(1).to_broadcast([128, 4, 128]), scalar=thr, in1=PT[pi][:],
                                                              op0=ALU.is_ge, op1=ALU.mult), reads=[d_PT[pi], d_cst], writes=[d_PT[pi]])

            def c_back(idx):
                tt, it = it_cmp[idx]
                pi = idx % 3
                i = tt % 2
                n_it = 1 if tt < 16 else 2
                for h in range(4):
                    b, c0 = obank(tt, h)
                    if it == 0 and h == 0:
                        zero_obanks()
                    kb.op("pe", lambda e: e.matmul(ps[b][:, c0:c0 + 193], lhsT=PT[pi][:, h, :], rhs=RC[:, it, 0:193], start=False, stop=(it == n_it - 1)),
                          reads=[d_PT[pi], d_RC], writes=[dps[b]])
                if it != n_it - 1:
                    return
                finalize(tt, 0, True)
                for h in range(4):
                    b, c0 = obank(tt, h)
                    if h == 0:
                        kb.op("dve", lambda e: e.tensor_scalar(out=imp[i][:], in0=ps[b][:, c0 + 129:c0 + 193], scalar1=rr[i][:, h:h + 1], scalar2=None, op0=ALU.mult),
                              reads=[dps[b], d_rr[i]], writes=[d_imp[i]])
                    else:
                        kb.op("dve", lambda e: e.scalar_tensor_tensor(out=imp[i][:], in0=ps[b][:, c0 + 129:c0 + 193], scalar=rr[i][:, h:h + 1], in1=imp[i][:],
                                                                      op0=ALU.mult, op1=ALU.add), reads=[dps[b], d_rr[i], d_imp[i]], writes=[d_imp[i]])
                kb.op("dve", lambda e: e.tensor_tensor(out=sc[i][:, 0, :], in0=imp[i][:], in1=sA[i][:], op=ALU.mult), reads=[d_imp[i], d_sA[i]], writes=[d_sc[i]])
                kb.op("dve", lambda e: e.tensor_tensor(out=sc[i][:, 0, :], in0=sc[i][:, 0, :], in1=sB[i][:], op=ALU.add), reads=[d_sc[i], d_sB[i]], writes=[d_sc[i]])
                kb.op("dve", lambda e: e.max(out=m8[i][:, 0, :], in_=sc[i][:, 0, :]), reads=[d_sc[i]], writes=[d_m8[i]])
                kb.op("dve", lambda e: e.match_replace(out=sc[i][:, 1, :], in_to_replace=m8[i][:, 0, :], in_values=sc[i][:, 0, :], imm_value=NEG),
                      reads=[d_sc[i], d_m8[i]], writes=[d_sc[i]])
                kb.op("dve", lambda e: e.max(out=m8[i][:, 1, :], in_=sc[i][:, 1, :]), reads=[d_sc[i]], writes=[d_m8[i]])
                kb.op("dve", lambda e: e.scalar_tensor_tensor(out=selq[i][:], in0=sc[i][:, 0, :], scalar=m8[i][:, 1, 7:8], in1=sA[i][:], op0=ALU.is_ge, op1=ALU.mult),
                      reads=[d_sc[i], d_m8[i], d_sA[i]], writes=[d_selq[i]])
                pb = ps[2 + i][:].bitcast(BF16)
                kb.op("pe", lambda e: e.transpose(out=pb[0:64, 0:128], in_=selq[i][:], identity=self.identb[:]), reads=[d_selq[i], self.d_const], writes=[dps[2 + i]])
                kb.op("act", lambda e: e.activation(out=SELT[:, tt, :], in_=pb[0:64, 0:128], func=AF.Copy), reads=[dps[2 + i]], writes=[d_SELT])

            c_front(0)
            for idx in range(len(it_cmp)):
                if idx + 1 < len(it_cmp):
                    c_front(idx + 1)
                c_back(idx)

            its = []
            for branch in (1, 2):
                for tt in range(NT):
                    kts = list(range(0, tt + 1)) if branch == 1 else list(range(max(0, tt - 4), tt + 1))
                    for ki, kt in enumerate(kts):
                        its.append((branch, tt, ki, kt, len(kts)))

            def a_front(idx):
                branch, tt, ki, kt, nk = its[idx]
                pi = 3 + idx % 5
                sbank = (0, 1, 2, 7)[idx % 4]
                KT_ = KST if branch == 1 else KWT
                d_KT_ = d_KST if branch == 1 else d_KWT
                kb.op("pe", lambda e: e.matmul(ps[sbank][:], lhsT=KT_[:, 0, kt * 128:(kt + 1) * 128], rhs=NQT[:, :, tt * 128:(tt + 1) * 128], start=True, stop=True),
                      reads=[d_KT_, d_NQT], writes=[dps[sbank]])
                kb.op("act", lambda e: e.activation(out=PT[pi][:].rearrange("p a b -> p (a b)"), in_=ps[sbank][:], func=AF.Exp, scale=SC),
                      reads=[dps[sbank]], writes=[d_PT[pi]])
                if branch == 1:
                    mb = 3 + (idx % 2)
                    mi = idx % 4
                    msl = ps[mb][:, 0:128]
                    kb.op("pe", lambda e: e.matmul(msl, lhsT=ESEL[:, kt, :], rhs=SELT[:, tt, :], start=True, stop=True),
                          reads=[d_cst, d_SELT], writes=[dps[mb]])
                    if kt == tt:
                        kb.op("dve", lambda e: e.tensor_tensor(out=M2s[mi][:], in0=msl, in1=tri[:], op=ALU.mult), reads=[dps[mb], d_cst], writes=[d_M2s[mi]])
                    else:
                        kb.op("dve", lambda e: e.tensor_copy(out=M2s[mi][:], in_=msl), reads=[dps[mb]], writes=[d_M2s[mi]])
                    meng = "pool" if idx % 3 == 0 else "dve"
                    kb.op(meng, lambda e: e.tensor_tensor(out=PT[pi][:], in0=PT[pi][:], in1=M2s[mi][:].unsqueeze(1).to_broadcast([128, 4, 128]), op=ALU.mult),
                          reads=[d_PT[pi], d_M2s[mi]], writes=[d_PT[pi]])
                else:
                    if kt == tt:
                        kb.op("pool", lambda e: e.tensor_tensor(out=PT[pi][:], in0=PT[pi][:], in1=tri[:].unsqueeze(1).to_broadcast([128, 4, 128]), op=ALU.mult),
                              reads=[d_PT[pi], d_cst], writes=[d_PT[pi]])
                    elif kt == tt - 4:
                        kb.op("pool", lambda e: e.tensor_tensor(out=PT[pi][:], in0=PT[pi][:], in1=triu[:].unsqueeze(1).to_broadcast([128, 4, 128]), op=ALU.mult),
                              reads=[d_PT[pi], d_cst], writes=[d_PT[pi]])

            def a_back(idx):
                branch, tt, ki, kt, nk = its[idx]
                pi = 3 + idx % 5
                V_ = VS if branch == 1 else VW
                d_V_ = d_VS if branch == 1 else d_VW
                for h in range(4):
                    b, c0 = obank(tt, h)
                    if ki == 0 and h == 0:
                        zero_obanks()
                    kb.op("pe", lambda e: e.matmul(ps[b][:, c0:c0 + 129], lhsT=PT[pi][:, h, :], rhs=V_[:, kt, 0:129], start=False, stop=(ki == nk - 1)),
                          reads=[d_PT[pi], d_V_], writes=[dps[b]])
                if ki == nk - 1:
                    finalize(tt, branch, False)

            SK = 3
            for idx in range(min(SK, len(its))):
                a_front(idx)
            for idx in range(len(its)):
                if idx + SK < len(its):
                    a_front(idx + SK)
                a_back(idx)
            kb.barrier()
            self.gate_and_store(ONS, d_ONS, C_NZ, 512)


def host_consts():
    c = {}
    c["c_identb"] = np.eye(128, dtype=np.float32).astype(ml_dtypes.bfloat16)
    c["c_identf"] = np.eye(128, dtype=np.float32)
    inv = 500000.0 ** (-np.arange(0, 32, 2, dtype=np.float32) / 32.0)
    pos = np.arange(S, dtype=np.float32)
    ang = pos[:, None] * inv[None, :].astype(np.float32)
    c["c_rope"] = np.concatenate([np.cos(ang), np.sin(ang)], axis=1).astype(np.float32)
    posc = (np.arange(256) * 16 + 31).astype(np.float32)
    angc = posc[:, None] * inv[None, :].astype(np.float32)
    c["c_ropec"] = np.concatenate([np.cos(angc), np.sin(angc)], axis=1).astype(np.float32)
    kk = np.arange(128)
    c["c_tri"] = (kk[:, None] <= kk[None, :]).astype(np.float32).astype(ml_dtypes.bfloat16)
    c["c_triu"] = (kk[:, None] > kk[None, :]).astype(np.float32).astype(ml_dtypes.bfloat16)
    c["c_iota"] = np.broadcast_to(np.arange(512, dtype=np.float32)[None, :], (128, 512)).copy()
    c["c_dkq"] = (kk[None, :] - 16 * kk[:, None]).astype(np.float32)
    selA = np.zeros((NT, 128, 64), np.float32)
    selB = np.zeros((NT, 128, 64), np.float32)
    j = np.arange(64)[None, :]
    for tt in range(NT):
        t = tt * 128 + np.arange(128)
        cur = (t // 64)[:, None]
        valid = j <= cur
        forced = (j == 0) | (j == cur) | (j == cur - 1)
        selA[tt] = valid.astype(np.float32)
        selB[tt] = np.where(forced, 1.0e30, np.where(valid, 0.0, -1.0e30))
    c["c_selA"] = selA
    c["c_selB"] = selB
    es = np.zeros((64, NT, 128), np.float32)
    for kt in range(NT):
        for key in range(128):
            es[2 * kt + key // 64, kt, key] = 1.0
    c["c_esel"] = es.astype(ml_dtypes.bfloat16)
    ci = np.arange(256)[:, None] * 16
    sj = np.arange(64)[None, :] * 64
    ov = ((ci < sj + 64) & (ci + 32 > sj)).astype(np.float32)
    ov[255] = 0.0
    c["c_ovl"] = ov.astype(ml_dtypes.bfloat16)
    return c


_PROG = None


def kernel(**inputs):
    global _PROG
    if _PROG is None:
        _PROG = Prog().build()
    nc = _PROG
    consts = host_consts()
    x = np.ascontiguousarray(inputs["x"], dtype=np.float32)
    B = x.shape[0]
    shared = {k: np.ascontiguousarray(v) for k, v in inputs.items() if k != "x"}
    in_maps = []
    for c in range(8):
        m = dict(shared)
        m.update(consts)
        m["x"] = x[c % B]
        in_maps.append(m)
    res = run_bass_kernel_spmd(nc, in_maps, core_ids=list(range(8)))
    out = np.stack([res.results[b]["out"] for b in range(B)], axis=0)
    return out.astype(np.float32, copy=False)
```

```python
import math
from contextlib import ExitStack

import numpy as np
import ml_dtypes
import concourse.bass as bass
import concourse.mybir as mybir
from concourse.bass_utils import run_bass_kernel_spmd

F32 = mybir.dt.float32
BF16 = mybir.dt.bfloat16
I32 = mybir.dt.int32
AF = mybir.ActivationFunctionType
ALU = mybir.AluOpType
AX = mybir.AxisListType

S = 4096
D = 2048
NT = S // 128
INW = 5900
TMW = 3852
DEPTH = 2
EPS = 1e-6
C_MQ, C_MK, C_MV, C_MZ, C_NQ = 0, 512, 1024, 1536, 2048
C_NKC, C_NVC, C_NKS, C_NVS, C_NKW, C_NVW = 2560, 2688, 2816, 2944, 3072, 3200
C_NG, C_NZ = 3328, 3340
NEG = -1.0e30


class Dep:
    __slots__ = ("w", "r", "name")

    def __init__(self, name=""):
        self.w = {}
        self.r = {}
        self.name = name


class Eng:
    def __init__(self, nc, eng, name):
        self.eng = eng
        self.name = name
        self.sem = nc.alloc_semaphore("sem_" + name)
        self.count = 0
        self.seen = {}


class KB:
    def __init__(self, nc, n_dma_sems=48):
        self.nc = nc
        self.E = {
            "pe": Eng(nc, nc.tensor, "pe"),
            "act": Eng(nc, nc.scalar, "act"),
            "dve": Eng(nc, nc.vector, "dve"),
            "pool": Eng(nc, nc.gpsimd, "pool"),
            "sp": Eng(nc, nc.sync, "sp"),
        }
        self.dsems = [[nc.alloc_semaphore("dsem%d" % i), 0] for i in range(n_dma_sems)]
        self.dnext = 0
        self.deps = []
        self.n_wait = 0
        self.n_ins = 0

    def dep(self, name=""):
        d = Dep(name)
        self.deps.append(d)
        return d

    def deps_n(self, n, name=""):
        return [self.dep(name + str(i)) for i in range(n)]

    def _wait(self, E, sem, val):
        k = id(sem)
        if E.seen.get(k, 0) < val:
            E.eng.wait_ge(sem, val)
            E.seen[k] = val
            self.n_wait += 1

    def _sync(self, E, reads, writes, own_sem=None):
        for d in reads:
            for k, (s, v) in d.w.items():
                self._wait(E, s, v)
        for d in writes:
            for k, (s, v) in d.w.items():
                if s is own_sem:
                    continue
                self._wait(E, s, v)
            for k, (s, v) in d.r.items():
                if s is own_sem:
                    continue
                self._wait(E, s, v)

    def _record(self, sem, val, reads, writes):
        k = id(sem)
        for d in writes:
            d.w = {k: (sem, val)}
            d.r = {}
        for d in reads:
            d.r[k] = (sem, val)

    def op(self, e, f, reads=(), writes=()):
        E = self.E[e]
        self._sync(E, reads, writes, own_sem=E.sem)
        ins = f(E.eng)
        E.count += 1
        ins.then_inc(E.sem, 1)
        self._record(E.sem, E.count, reads, writes)
        self.n_ins += 1
        return ins

    def dma(self, q, out, in_, reads=(), writes=(), **kw):
        E = self.E[q]
        self._sync(E, reads, writes)
        ent = self.dsems[self.dnext]
        self.dnext = (self.dnext + 1) % len(self.dsems)
        if ent[1] > 0:
            self._wait(E, ent[0], ent[1])
        ent[1] += 16
        ins = E.eng.dma_start(out=out, in_=in_, **kw)
        ins.then_inc(ent[0], 16)
        self._record(ent[0], ent[1], reads, writes)
        self.n_ins += 1
        return ins

    def barrier(self):
        sp = self.E["sp"]
        for n, E in self.E.items():
            if E is not sp and E.count > 0:
                self._wait(sp, E.sem, E.count)
        for s, v in self.dsems:
            if v > 0:
                self._wait(sp, s, v)
        sp.count += 1
        sp.eng.nop().then_inc(sp.sem, 1)
        for n, E in self.E.items():
            if E is not sp:
                self._wait(E, sp.sem, sp.count)
            for n2, E2 in self.E.items():
                E.seen[id(E2.sem)] = E2.count
            for s, v in self.dsems:
                E.seen[id(s)] = v
        for d in self.deps:
            d.w = {}
            d.r = {}
        self.deps = []


class Prog:
    def __init__(self, dbg=None, layers=DEPTH, phases=("A", "S5", "MOBA", "NSA", "F")):
        self.dbg = dbg or ()
        self.layers = layers
        self.phases = phases
        nc = bass.Bass("TRN2", target_bir_lowering=False)
        self.nc = nc
        self.kb = KB(nc)
        ein = lambda n, s, d: nc.dram_tensor(n, list(s), d, kind="ExternalInput").ap()
        L = DEPTH
        self.x = ein("x", [S, D], F32)
        self.norm_w = ein("norm_w", [L, D], F32)
        self.w_in = ein("w_in", [L, D, INW], F32)
        self.w_out = ein("w_out", [L, D, D], F32)
        self.hn = {}
        for n in ("moba_q_norm", "moba_k_norm", "nsa_q_norm", "nsa_kc_norm", "nsa_ks_norm", "nsa_kw_norm"):
            self.hn[n] = ein(n, [L, 128], F32)
        self.pe_k = ein("nsa_pe_k", [L, 32, 128], F32)
        self.pe_v = ein("nsa_pe_v", [L, 32, 128], F32)
        self.ck_w1 = ein("nsa_cmp_k_w1", [L, 4096, 128], F32)
        self.ck_w2 = ein("nsa_cmp_k_w2", [L, 128, 128], F32)
        self.cv_w1 = ein("nsa_cmp_v_w1", [L, 4096, 128], F32)
        self.cv_w2 = ein("nsa_cmp_v_w2", [L, 128, 128], F32)
        self.a_re = ein("s5_a_re", [L, 64, 64], F32)
        self.a_im = ein("s5_a_im", [L, 64, 64], F32)
        self.b_re = ein("s5_b_re", [L, 64, 64, 16], F32)
        self.b_im = ein("s5_b_im", [L, 64, 64, 16], F32)
        self.c_re = ein("s5_c_re", [L, 64, 16, 64], F32)
        self.c_im = ein("s5_c_im", [L, 64, 16, 64], F32)
        self.s5_d = ein("s5_d", [L, 1024], F32)
        self.log_dt = ein("s5_log_dt", [L, 64], F32)
        self.glu_w = ein("s5_glu_w", [L, 1024, 1024], F32)
        self.c_identb = ein("c_identb", [128, 128], BF16)
        self.c_identf = ein("c_identf", [128, 128], F32)
        self.c_rope = ein("c_rope", [S, 32], F32)
        self.c_ropec = ein("c_ropec", [256, 32], F32)
        self.c_tri = ein("c_tri", [128, 128], BF16)
        self.c_iota = ein("c_iota", [128, 512], F32)
        self.c_triu = ein("c_triu", [128, 128], BF16)
        self.c_dkq = ein("c_dkq", [128, 128], F32)
        self.c_selA = ein("c_selA", [NT, 128, 64], F32)
        self.c_selB = ein("c_selB", [NT, 128, 64], F32)
        self.c_esel = ein("c_esel", [64, NT, 128], BF16)
        self.c_ovl = ein("c_ovl", [256, 64], BF16)
        self.out = nc.dram_tensor("out", [S, D], F32, kind="ExternalOutput").ap()
        sk = lambda n: "ExternalOutput" if n in self.dbg else "Internal"
        self.proj_tm = nc.dram_tensor("proj_tm", [S, TMW], BF16, kind=("ExternalInput" if "proj_in" in self.dbg else sk("proj_tm"))).ap()
        self.sT = nc.dram_tensor("sT", [2048, S], BF16, kind=("ExternalInput" if "sT_in" in self.dbg else sk("sT"))).ap()
        self.mixedT = nc.dram_tensor("mixedT", [2048, S], BF16, kind=("ExternalInput" if "mixedT_in" in self.dbg else sk("mixedT"))).ap()
        self.x1 = nc.dram_tensor("x1", [S, D], F32, kind=sk("x1")).ap()
        self.kcmp_tm = nc.dram_tensor("kcmp_tm", [256, 128], BF16, kind=sk("kcmp_tm")).ap()
        self.y5d = nc.dram_tensor("y5d", [1024, S], BF16, kind=sk("y5d")).ap()

    @staticmethod
    def emit_pipelined(n, stages):
        ns = len(stages)
        for t in range(n + ns - 1):
            for s_idx in range(ns - 1, -1, -1):
                i = t - s_idx
                if 0 <= i < n:
                    stages[s_idx](i)

    def sbt(self, name, shape, dtype):
        self._uid = getattr(self, "_uid", 0) + 1
        return self.nc.sbuf_tensor("%s_u%d" % (name, self._uid), shape, dtype)

    def build(self):
        nc, kb = self.nc, self.kb
        with ExitStack() as st:
            self.ps = [st.enter_context(nc.psum_tensor("ps%d" % i, [128, 512], F32)) for i in range(8)]
            self.dps = kb.deps_n(8, "ps")
            self.identb = st.enter_context(self.sbt("identb", [128, 128], BF16))
            self.identf = st.enter_context(self.sbt("identf", [128, 128], F32))
            self.d_const = kb.dep("const")
            kb.dma("sp", self.identb[:], self.c_identb, writes=[self.d_const])
            kb.dma("sp", self.identf[:], self.c_identf, writes=[self.d_const])
            kb.barrier()
            for l in range(self.layers):
                src = self.x if l == 0 else self.x1
                dst = self.out if l == self.layers - 1 else self.x1
                if "A" in self.phases:
                    self.phase_A(l, src)
                    kb.barrier()
                if "S5" in self.phases:
                    self.phase_S5(l)
                    kb.barrier()
                if "MOBA" in self.phases:
                    self.phase_MOBA(l)
                    kb.barrier()
                if "NSA" in self.phases:
                    self.phase_NSA(l)
                    kb.barrier()
                if "F" in self.phases:
                    self.phase_F(l, src, dst)
                    kb.barrier()
            kb.barrier()
        return nc

    def phase_A(self, l, src):
        nc, kb = self.nc, self.kb
        ps, dps = self.ps, self.dps
        with ExitStack() as st:
            sb = lambda n, s, d: st.enter_context(self.sbt("A_" + n, s, d))
            hdnT = sb("hdnT", [128, 16, 2048], BF16)
            normw = sb("normw", [128, D], F32)
            xt = [sb("xt%d" % i, [128, D], F32) for i in range(3)]
            junk = sb("junk", [128, D], BF16)
            hb = [sb("hb%d" % i, [128, D], BF16) for i in range(3)]
            wch = [sb("wch%d" % i, [128, 16, 512], BF16) for i in range(2)]
            stg = [sb("stg%d" % i, [128, 512], BF16) for i in range(4)]
            ss = [sb("ss%d" % i, [128, 1], F32) for i in range(3)]
            d_hT = kb.deps_n(16, "hT")
            d_nw = kb.dep("nw")
            d_xt = kb.deps_n(3, "xt")
            d_junk = kb.dep("junk")
            d_hb = kb.deps_n(3, "hb")
            d_w = kb.deps_n(2, "w")
            d_stg = kb.deps_n(4, "stg")
            d_ss = kb.deps_n(3, "ss")
            kb.dma("sp", normw[:], self.norm_w[l:l + 1, :].partition_broadcast(128), writes=[d_nw])
            istg = 0
            iw = 0
            ievac = 0
            for h in range(2):
                def a1(tt, h=h):
                    g = h * 16 + tt
                    i = tt % 3
                    xi = tt % 3
                    kb.dma("sp", xt[xi][:], src[g * 128:(g + 1) * 128, :], writes=[d_xt[xi]])
                    kb.op("act", lambda e: e.activation(out=junk[:], in_=xt[xi][:], func=AF.Square, accum_out=ss[i][:]),
                          reads=[d_xt[xi]], writes=[d_junk, d_ss[i]])
                    kb.op("dve", lambda e: e.tensor_scalar(out=ss[i][:], in0=ss[i][:], scalar1=1.0 / D, scalar2=EPS,
                                                           op0=ALU.mult, op1=ALU.add), reads=[d_ss[i]], writes=[d_ss[i]])
                    kb.op("act", lambda e: e.activation(out=ss[i][:], in_=ss[i][:], func=AF.Sqrt), reads=[d_ss[i]], writes=[d_ss[i]])
                    kb.op("dve", lambda e: e.reciprocal(out=ss[i][:], in_=ss[i][:]), reads=[d_ss[i]], writes=[d_ss[i]])
                def a2(tt, h=h):
                    i = tt % 3
                    xi = tt % 3
                    kb.op("dve", lambda e: e.scalar_tensor_tensor(out=hb[xi][:], in0=xt[xi][:], scalar=ss[i][:], in1=normw[:],
                                                                  op0=ALU.mult, op1=ALU.mult),
                          reads=[d_xt[xi], d_ss[i], d_nw], writes=[d_hb[xi]])
                    for half in range(2):
                        tbk = 4 + 2 * (tt % 2) + half
                        pb = ps[tbk][:].bitcast(BF16)
                        for k in range(8):
                            kc = half * 8 + k
                            kb.op("pe", lambda e: e.transpose(out=pb[:, k * 128:(k + 1) * 128], in_=hb[xi][:, kc * 128:(kc + 1) * 128],
                                                              identity=self.identb[:]),
                                  reads=[d_hb[xi], self.d_const], writes=[dps[tbk]])
                def a3(tt, h=h):
                    for half in range(2):
                        tbk = 4 + 2 * (tt % 2) + half
                        pb = ps[tbk][:].bitcast(BF16)
                        eng = "act" if half == 0 else "dve"
                        dst = hdnT[:, half * 8:(half + 1) * 8, tt * 128:(tt + 1) * 128]
                        srcp = pb[:, 0:1024].rearrange("p (k n) -> p k n", k=8)
                        if eng == "act":
                            kb.op("act", lambda e: e.activation(out=dst, in_=srcp, func=AF.Copy), reads=[dps[tbk]], writes=[d_hT[tt]])
                        else:
                            kb.op("dve", lambda e: e.tensor_copy(out=dst, in_=srcp), reads=[dps[tbk]], writes=[d_hT[tt]])
                self.emit_pipelined(16, [a1, a2, a3])
                chunks = [(c0, min(512, TMW - c0)) for c0 in range(0, TMW, 512)]
                for (c0, cw) in chunks:
                    wi = iw % 2
                    iw += 1
                    kb.dma("pool", wch[wi][:, :, 0:cw], self.w_in[l, :, c0:c0 + cw].rearrange("(k p) n -> p k n", p=128),
                           writes=[d_w[wi]])
                    for tt in range(16):
                        g = h * 16 + tt
                        pbank = ievac % 4
                        for kc in range(16):
                            kb.op("pe", lambda e: e.matmul(ps[pbank][:, 0:cw], lhsT=hdnT[:, kc, tt * 128:(tt + 1) * 128],
                                                           rhs=wch[wi][:, kc, 0:cw], start=(kc == 0), stop=(kc == 15)),
                                  reads=[d_hT[tt], d_w[wi]], writes=[dps[pbank]])
                        si = istg % 4
                        istg += 1
                        if ievac % 2 == 0:
                            kb.op("act", lambda e: e.activation(out=stg[si][:, 0:cw], in_=ps[pbank][:, 0:cw], func=AF.Copy),
                                  reads=[dps[pbank]], writes=[d_stg[si]])
                        else:
                            kb.op("dve", lambda e: e.tensor_copy(out=stg[si][:, 0:cw], in_=ps[pbank][:, 0:cw]),
                                  reads=[dps[pbank]], writes=[d_stg[si]])
                        ievac += 1
                        kb.dma("sp", self.proj_tm[g * 128:(g + 1) * 128, c0:c0 + cw], stg[si][:, 0:cw], reads=[d_stg[si]])
                for fc in range(4):
                    c0 = TMW + fc * 512
                    wi = iw % 2
                    iw += 1
                    kb.dma("pool", wch[wi][:], self.w_in[l, :, c0:c0 + 512].rearrange("(k p) n -> p k n", p=128), writes=[d_w[wi]])
                    for ctl in range(4):
                        row0 = fc * 512 + ctl * 128
                        for tb in range(4):
                            pbank = ievac % 4
                            for kc in range(16):
                                kb.op("pe", lambda e: e.matmul(ps[pbank][:], lhsT=wch[wi][:, kc, ctl * 128:(ctl + 1) * 128],
                                                               rhs=hdnT[:, kc, tb * 512:(tb + 1) * 512], start=(kc == 0), stop=(kc == 15)),
                                      reads=d_hT[tb * 4:(tb + 1) * 4] + [d_w[wi]], writes=[dps[pbank]])
                            si = istg % 4
                            istg += 1
                            if ievac % 2 == 0:
                                kb.op("act", lambda e: e.activation(out=stg[si][:], in_=ps[pbank][:], func=AF.Copy),
                                      reads=[dps[pbank]], writes=[d_stg[si]])
                            else:
                                kb.op("dve", lambda e: e.tensor_copy(out=stg[si][:], in_=ps[pbank][:]),
                                      reads=[dps[pbank]], writes=[d_stg[si]])
                            ievac += 1
                            t0 = h * 2048 + tb * 512
                            kb.dma("sp", self.sT[row0:row0 + 128, t0:t0 + 512], stg[si][:], reads=[d_stg[si]])

    def phase_F(self, l, src, dst):
        nc, kb = self.nc, self.kb
        ps, dps = self.ps, self.dps
        with ExitStack() as st:
            sb = lambda n, s, d: st.enter_context(self.sbt("F_" + n, s, d))
            wo = sb("wo", [128, 16, D], BF16)
            mT = [sb("mT%d" % i, [128, 16, 512], BF16) for i in range(2)]
            xr = [sb("xr%d" % i, [128, D], F32) for i in range(2)]
            ot = [sb("ot%d" % i, [128, D], F32) for i in range(2)]
            d_wo = kb.deps_n(4, "wo")
            d_mT = kb.deps_n(2, "mT")
            d_xr = kb.deps_n(2, "xr")
            d_ot = kb.deps_n(2, "ot")
            for c in range(4):
                kb.dma("pool", wo[:, :, c * 512:(c + 1) * 512], self.w_out[l, :, c * 512:(c + 1) * 512].rearrange("(k p) n -> p k n", p=128),
                       writes=[d_wo[c]])
            ie = 0
            import os
            for tb in range(int(os.environ.get('F_TB', 8))):
                mi = tb % 2
                kb.dma("sp", mT[mi][:], self.mixedT[:, tb * 512:(tb + 1) * 512].rearrange("(k p) n -> p k n", p=128), writes=[d_mT[mi]])
                for t4 in range(4):
                    g = tb * 4 + t4
                    i = g % 2
                    kb.dma("sp", xr[i][:], src[g * 128:(g + 1) * 128, :], writes=[d_xr[i]])
                    for c in range(4):
                        pbank = ie % 4
                        ie += 1
                        for kc in range(16):
                            kb.op("pe", lambda e: e.matmul(ps[pbank][:], lhsT=mT[mi][:, kc, t4 * 128:(t4 + 1) * 128],
                                                           rhs=wo[:, kc, c * 512:(c + 1) * 512], start=(kc == 0), stop=(kc == 15)),
                                  reads=[d_mT[mi], d_wo[c]], writes=[dps[pbank]])
                        kb.op("dve", lambda e: e.tensor_tensor(out=ot[i][:, c * 512:(c + 1) * 512], in0=ps[pbank][:],
                                                               in1=xr[i][:, c * 512:(c + 1) * 512], op=ALU.add),
                              reads=[dps[pbank], d_xr[i]], writes=[d_ot[i]])
                    kb.dma("pool", dst[g * 128:(g + 1) * 128, :], ot[i][:], reads=[d_ot[i]])

    def sincos_turns(self, turns, cos_out, sin_out, tmpf, tmpi, tmpf2, dT, dC, dS, dtmp):
        kb = self.kb
        TWO_PI = 6.283185
        kb.op("dve", lambda e: e.tensor_copy(out=tmpi, in_=turns), reads=[dT], writes=[dtmp])
        kb.op("dve", lambda e: e.tensor_tensor(out=tmpf, in0=turns, in1=tmpi, op=ALU.subtract), reads=[dT, dtmp], writes=[dtmp])
        kb.op("act", lambda e: e.activation(out=sin_out, in_=tmpf, func=AF.Sin, scale=TWO_PI), reads=[dtmp], writes=[dS])
        kb.op("dve", lambda e: e.tensor_scalar(out=tmpf2, in0=tmpf, scalar1=0.25, scalar2=None, op0=ALU.add), reads=[dtmp], writes=[dtmp])
        kb.op("dve", lambda e: e.scalar_tensor_tensor(out=tmpf2, in0=tmpf2, scalar=0.5, in1=tmpf2, op0=ALU.is_gt, op1=ALU.subtract),
              reads=[dtmp], writes=[dtmp])
        kb.op("act", lambda e: e.activation(out=cos_out, in_=tmpf2, func=AF.Sin, scale=-TWO_PI), reads=[dtmp], writes=[dC])

    def phase_S5_v1(self, l):
        nc, kb = self.nc, self.kb
        ps, dps = self.ps, self.dps
        with ExitStack() as st:
            sb = lambda n, s, d: st.enter_context(self.sbt("S_" + n, s, d))
            BT = [sb("BT%d" % i, [128, 32, 128], BF16) for i in range(2)]
            CT = [sb("CT%d" % i, [128, 32, 128], BF16) for i in range(2)]
            prm = sb("prm", [128, 24, 32], F32)
            prmi = sb("prmi", [128, 32], I32)
            Dt = sb("Dt", [128, 8], F32)
            gluw = sb("gluw", [128, 8, 1024], BF16)
            d_BT, d_CT, d_prm, d_Dt, d_glu = kb.deps_n(5, "s5c")
            AR, AI, LDT, DTT, MM, PHI, COS, SIN, FR, FI, C512, S512, T0, T1, T2, T3, T4, T5 = range(18)
            P = lambda i: prm[:, i, :]
            kb.dma("pool", gluw[:], self.glu_w[l].rearrange("(k p) n -> p k n", p=128), writes=[d_glu])
            with ExitStack() as st2:
                sb2 = lambda n, s, d: st2.enter_context(self.sbt("S2_" + n, s, d))
                XA = sb2("XA", [32, 3, 128], F32)
                ld2 = sb2("ld2", [32, 2], F32)
                XD = sb2("XD", [8, 128], F32)
                pads = [sb2("pad%d" % i, [128, 32, 128], F32) for i in range(4)]
                d_XA, d_ld2, d_XD = kb.deps_n(3, "xa")
                d_pad = kb.deps_n(4, "pad")
                kb.dma("sp", XA[:, 0, :], self.a_re[l].rearrange("(q gl) p -> q (gl p)", gl=2), writes=[d_XA])
                kb.dma("sp", XA[:, 1, :], self.a_im[l].rearrange("(q gl) p -> q (gl p)", gl=2), writes=[d_XA])
                kb.dma("sp", ld2[:], self.log_dt[l:l + 1, :].rearrange("o (q gl) -> (o q) gl", gl=2), writes=[d_ld2])
                kb.dma("sp", XD[:], self.s5_d[l:l + 1, :].rearrange("o (c p) -> (o c) p", p=128), writes=[d_XD])
                kb.op("dve", lambda e: e.tensor_copy(out=XA[:, 2, :].rearrange("q (gl p) -> q gl p", gl=2),
                                                     in_=ld2[:].unsqueeze(2).to_broadcast([32, 2, 64])),
                      reads=[d_ld2, d_XA], writes=[d_XA])
                for i in range(4):
                    eng = "dve" if i % 2 == 0 else "pool"
                    kb.op(eng, lambda e: e.memset(pads[i][:].rearrange("p q c -> p (q c)"), 0.0), writes=[d_pad[i]])
                srcB = [self.b_re[l], self.b_im[l]]
                srcC = [self.c_re[l], self.c_im[l]]
                for k in range(4):
                    for gl in range(2):
                        for i in range(2):
                            dstb = pads[i][gl * 64:(gl + 1) * 64, :, :].rearrange("p (ct k) c -> p k ct c", k=4)[:, k, :, 32 * k + 16 * gl:32 * k + 16 * gl + 16]
                            sb_ = srcB[i].rearrange("(ct k gl) p c -> k gl p ct c", k=4, gl=2)[k, gl]
                            kb.dma("sp", dstb, sb_, reads=[d_pad[i]], writes=[d_pad[i]])
                            dstc = pads[2 + i][32 * k + 16 * gl:32 * k + 16 * gl + 16, :, :].rearrange("p (ct k) c -> p k ct c", k=4)[:, k, :, gl * 64:(gl + 1) * 64]
                            sc_ = srcC[i].rearrange("(ct k gl) c p -> k gl c ct p", k=4, gl=2)[k, gl]
                            kb.dma("sp", dstc, sc_, reads=[d_pad[2 + i]], writes=[d_pad[2 + i]])
                for j in range(3):
                    kb.op("pe", lambda e: e.transpose(out=ps[0][:, j * 32:(j + 1) * 32], in_=XA[:, j, :], identity=self.identf[0:32, 0:32]),
                          reads=[d_XA, self.d_const], writes=[dps[0]])
                kb.op("pe", lambda e: e.transpose(out=ps[0][:, 96:104], in_=XD[:], identity=self.identf[0:8, 0:8]),
                      reads=[d_XD, self.d_const], writes=[dps[0]])
                kb.op("dve", lambda e: e.tensor_copy(out=prm[:, 0:3, :].rearrange("p a q -> p (a q)"), in_=ps[0][:, 0:96]), reads=[dps[0]], writes=[d_prm])
                kb.op("dve", lambda e: e.tensor_copy(out=Dt[:], in_=ps[0][:, 96:104]), reads=[dps[0]], writes=[d_Dt])
                R_, W_ = [d_prm], [d_prm]
                tt = lambda o, a, b, op: kb.op("dve", lambda e: e.tensor_tensor(out=P(o), in0=P(a), in1=P(b), op=op), reads=R_, writes=W_)
                kb.op("act", lambda e: e.activation(out=P(DTT), in_=P(LDT), func=AF.Exp), reads=R_, writes=W_)
                tt(T0, DTT, AR, ALU.mult)
                kb.op("act", lambda e: e.activation(out=P(MM), in_=P(T0), func=AF.Exp), reads=R_, writes=W_)
                tt(T0, DTT, AI, ALU.mult)
                kb.op("dve", lambda e: e.tensor_scalar(out=P(T1), in0=P(T0), scalar1=1.0 / (2.0 * math.pi), scalar2=None, op0=ALU.mult), reads=R_, writes=W_)
                kb.op("dve", lambda e: e.tensor_copy(out=prmi[:], in_=P(T1)), reads=R_, writes=W_)
                kb.op("dve", lambda e: e.tensor_tensor(out=P(PHI), in0=P(T1), in1=prmi[:], op=ALU.subtract), reads=R_, writes=W_)
                self.sincos_turns(P(PHI), P(COS), P(SIN), P(T2), prmi[:], P(T3), d_prm, d_prm, d_prm, d_prm)
                kb.op("dve", lambda e: e.tensor_scalar(out=P(T4), in0=P(PHI), scalar1=512.0, scalar2=None, op0=ALU.mult), reads=R_, writes=W_)
                self.sincos_turns(P(T4), P(C512), P(S512), P(T2), prmi[:], P(T3), d_prm, d_prm, d_prm, d_prm)
                tt(T0, MM, COS, ALU.mult)
                tt(T1, MM, SIN, ALU.mult)
                kb.op("dve", lambda e: e.tensor_scalar(out=P(T0), in0=P(T0), scalar1=-1.0, scalar2=None, op0=ALU.add), reads=R_, writes=W_)
                tt(T2, AR, AR, ALU.mult)
                tt(T3, AI, AI, ALU.mult)
                tt(T2, T2, T3, ALU.add)
                kb.op("dve", lambda e: e.reciprocal(out=P(T2), in_=P(T2)), reads=R_, writes=W_)
                tt(T3, T0, AR, ALU.mult)
                tt(T4, T1, AI, ALU.mult)
                tt(T3, T3, T4, ALU.add)
                tt(FR, T3, T2, ALU.mult)
                tt(T3, T1, AR, ALU.mult)
                tt(T4, T0, AI, ALU.mult)
                tt(T3, T3, T4, ALU.subtract)
                tt(FI, T3, T2, ALU.mult)
                ctmp = [sb2("ctmp%d" % i, [128, 4, 128], F32) for i in range(4)]
                d_ctmp = kb.dep("ctmp")
                for q4 in range(8):
                    for i in range(4):
                        bank = 4 + i
                        for k in range(4):
                            q = q4 * 4 + k
                            kb.op("pe", lambda e: e.transpose(out=ps[bank][:, k * 128:(k + 1) * 128], in_=pads[i][:, q, :], identity=self.identf[:]),
                                  reads=[d_pad[i], self.d_const], writes=[dps[bank]])
                    for i in range(2):
                        kb.op("act", lambda e: e.activation(out=BT[i][:, q4 * 4:(q4 + 1) * 4, :].rearrange("p a b -> p (a b)"), in_=ps[4 + i][:], func=AF.Copy),
                              reads=[dps[4 + i]], writes=[d_BT])
                    frb = prm[:, FR, q4 * 4:(q4 + 1) * 4].unsqueeze(2).to_broadcast([128, 4, 128])
                    fib = prm[:, FI, q4 * 4:(q4 + 1) * 4].unsqueeze(2).to_broadcast([128, 4, 128])
                    crp = ps[6][:].rearrange("p (a b) -> p a b", a=4)
                    cip = ps[7][:].rearrange("p (a b) -> p a b", a=4)
                    tmpw = [d_ctmp]
                    kb.op("dve", lambda e: e.tensor_tensor(out=ctmp[0][:], in0=crp, in1=frb, op=ALU.mult), reads=[dps[6], d_prm], writes=tmpw)
                    kb.op("dve", lambda e: e.tensor_tensor(out=ctmp[1][:], in0=cip, in1=fib, op=ALU.mult), reads=[dps[7], d_prm], writes=tmpw)
                    kb.op("dve", lambda e: e.tensor_tensor(out=CT[0][:, q4 * 4:(q4 + 1) * 4, :], in0=ctmp[0][:], in1=ctmp[1][:], op=ALU.subtract),
                          reads=tmpw, writes=[d_CT])
                    kb.op("dve", lambda e: e.tensor_tensor(out=ctmp[2][:], in0=crp, in1=fib, op=ALU.mult), reads=[dps[6], d_prm], writes=tmpw)
                    kb.op("dve", lambda e: e.tensor_tensor(out=ctmp[3][:], in0=cip, in1=frb, op=ALU.mult), reads=[dps[7], d_prm], writes=tmpw)
                    kb.op("dve", lambda e: e.scalar_tensor_tensor(out=CT[1][:, q4 * 4:(q4 + 1) * 4, :], in0=ctmp[2][:], scalar=-1.0, in1=ctmp[3][:],
                                                                  op0=ALU.mult, op1=ALU.subtract), reads=tmpw, writes=[d_CT])
                kb.barrier()
            y5T = sb("y5T", [128, 8, S], BF16)
            cosT = [sb("cosT%d" % i, [128, 4, 512], BF16) for i in range(2)]
            sinT = [sb("sinT%d" % i, [128, 4, 512], BF16) for i in range(2)]
            iota = sb("iota", [128, 512], F32)
            angi = sb("angi", [128, 512], I32)
            uT = [sb("uT%d" % i, [128, 512], BF16) for i in range(3)]
            tf = [sb("tf%d" % i, [128, 512], F32) for i in range(3)]
            tb_ = [sb("tb%d" % i, [128, 512], BF16) for i in range(6)]
            rb_ = [sb("rb%d" % i, [128, 512], BF16) for i in range(4)]
            BuS = [[sb("BuS%d%d" % (i, j), [128, 512], BF16) for j in range(2)] for i in range(2)]
            zf = [[sb("zf%d%d" % (i, j), [128, 512], F32) for j in range(2)] for i in range(2)]
            zb = [[sb("zb%d%d" % (i, j), [128, 512], BF16) for j in range(2)] for i in range(2)]
            X = [[sb("X%d%d" % (i, j), [128, 512], BF16) for j in range(2)] for i in range(2)]
            cz = [sb("cz%d" % i, [128, 2, 32], F32) for i in range(2)]
            czt = sb("czt", [128, 2], F32)
            d_y5 = kb.deps_n(8, "y5")
            d_cos = kb.deps_n(2, "cos")
            d_sin = kb.deps_n(2, "sin")
            d_iota, d_angi, d_czt = kb.deps_n(3, "tab")
            d_uT = kb.deps_n(3, "uT")
            d_tf = kb.deps_n(3, "tf")
            d_tb = kb.deps_n(6, "tb")
            d_rb = kb.deps_n(4, "rb")
            d_BuS = [kb.deps_n(2) for i in range(2)]
            d_zf = [kb.deps_n(2) for i in range(2)]
            d_zb = [kb.deps_n(2) for i in range(2)]
            d_X = [kb.deps_n(2) for i in range(2)]
            d_cz = kb.deps_n(2, "cz")
            kb.dma("sp", iota[:], self.c_iota, writes=[d_iota])
            t1, t2, t3, t4, wr, wi = tb_
            dt1, dt2, dt3, dt4, dwr, dwi = d_tb
            r1, r2, r3, r4 = rb_
            dr1, dr2, dr3, dr4 = d_rb

            def TT(eng, o, do, a, da, b_, db, op):
                kb.op(eng, lambda e: e.tensor_tensor(out=o, in0=a, in1=b_, op=op), reads=da + db, writes=[do])

            pending = []
            it = 0
            icb = 0
            for ct in range(8):
                tbi = ct % 2
                for k in range(4):
                    q = ct * 4 + k
                    kb.op("dve", lambda e: e.tensor_scalar(out=tf[0][:], in0=iota[:], scalar1=prm[:, PHI, q:q + 1], scalar2=None, op0=ALU.mult),
                          reads=[d_iota, d_prm], writes=[d_tf[0]])
                    self.sincos_turns(tf[0][:], cosT[tbi][:, k, :], sinT[tbi][:, k, :], tf[1][:], angi[:], tf[2][:], d_tf[0], d_cos[tbi], d_sin[tbi], d_tf[1])
                kb.op("dve", lambda e: e.memset(cz[0][:].rearrange("p a q -> p (a q)"), 0.0), writes=[d_cz[0]])
                for tb in range(8):
                    ui = icb % 3
                    ybank = 4 + (icb % 2)
                    icb += 1
                    par = tb % 2
                    kb.dma("sp", uT[ui][:], self.sT[ct * 128:(ct + 1) * 128, tb * 512:(tb + 1) * 512], writes=[d_uT[ui]])
                    for k in range(4):
                        q = ct * 4 + k
                        sset = it % 2
                        it += 1
                        c = cosT[tbi][:, k, :]
                        s_ = sinT[tbi][:, k, :]
                        dc, ds = [d_cos[tbi]], [d_sin[tbi]]
                        for i in range(2):
                            kb.op("pe", lambda e: e.matmul(ps[2 * sset + i][:], lhsT=BT[i][:, q, :], rhs=uT[ui][:], start=True, stop=True),
                                  reads=[d_BT, d_uT[ui]], writes=[dps[2 * sset + i]])
                            kb.op("act", lambda e: e.activation(out=BuS[sset][i][:], in_=ps[2 * sset + i][:], func=AF.Copy),
                                  reads=[dps[2 * sset + i]], writes=[d_BuS[sset][i]])
                        Br, Bi = BuS[sset][0][:], BuS[sset][1][:]
                        dBr, dBi = [d_BuS[sset][0]], [d_BuS[sset][1]]
                        TT("dve", t1[:], dt1, Br, dBr, c, dc, ALU.mult)
                        TT("dve", t2[:], dt2, Bi, dBi, s_, ds, ALU.mult)
                        TT("dve", wr[:], dwr, t1[:], [dt1], t2[:], [dt2], ALU.add)
                        TT("dve", t3[:], dt3, Bi, dBi, c, dc, ALU.mult)
                        TT("dve", t4[:], dt4, Br, dBr, s_, ds, ALU.mult)
                        TT("dve", wi[:], dwi, t3[:], [dt3], t4[:], [dt4], ALU.subtract)
                        mb = prm[:, MM, q:q + 1].to_broadcast([128, 512])
                        zr, zi = zf[sset][0], zf[sset][1]
                        dzr, dzi = d_zf[sset][0], d_zf[sset][1]
                        kb.op("dve", lambda e: e.tensor_tensor_scan(out=zr[:], data0=mb, data1=wr[:], initial=cz[par][:, 0, q:q + 1], op0=ALU.mult, op1=ALU.add),
                              reads=[d_prm, dwr, d_cz[par]], writes=[dzr])
                        kb.op("dve", lambda e: e.tensor_tensor_scan(out=zi[:], data0=mb, data1=wi[:], initial=cz[par][:, 1, q:q + 1], op0=ALU.mult, op1=ALU.add),
                              reads=[d_prm, dwi, d_cz[par]], writes=[dzi])
                        for i in range(2):
                            kb.op("act", lambda e: e.activation(out=zb[sset][i][:], in_=zf[sset][i][:], func=AF.Copy),
                                  reads=[d_zf[sset][i]], writes=[d_zb[sset][i]])
                        zr_l, zi_l = zr[:, 511:512], zi[:, 511:512]
                        c5, s5 = prm[:, C512, q:q + 1], prm[:, S512, q:q + 1]
                        nx = 1 - par
                        kb.op("dve", lambda e: e.tensor_scalar(out=czt[:, 0:1], in0=zi_l, scalar1=s5, scalar2=None, op0=ALU.mult), reads=[dzi, d_prm], writes=[d_czt])
                        kb.op("dve", lambda e: e.scalar_tensor_tensor(out=cz[nx][:, 0, q:q + 1], in0=zr_l, scalar=c5, in1=czt[:, 0:1], op0=ALU.mult, op1=ALU.subtract),
                              reads=[dzr, d_prm, d_czt], writes=[d_cz[nx]])
                        kb.op("dve", lambda e: e.tensor_scalar(out=czt[:, 1:2], in0=zi_l, scalar1=c5, scalar2=None, op0=ALU.mult), reads=[dzi, d_prm], writes=[d_czt])
                        kb.op("dve", lambda e: e.scalar_tensor_tensor(out=cz[nx][:, 1, q:q + 1], in0=zr_l, scalar=s5, in1=czt[:, 1:2], op0=ALU.mult, op1=ALU.add),
                              reads=[dzr, d_prm, d_czt], writes=[d_cz[nx]])

                        def back(sset=sset, c=c, s_=s_, dc=dc, ds=ds, q=q, k=k, ct=ct, tb=tb, ui=ui, ybank=ybank):
                            zbr, zbi = zb[sset][0][:], zb[sset][1][:]
                            dzbr, dzbi = [d_zb[sset][0]], [d_zb[sset][1]]
                            TT("pool", r1[:], dr1, zbr, dzbr, c, dc, ALU.mult)
                            TT("pool", r2[:], dr2, zbi, dzbi, s_, ds, ALU.mult)
                            TT("pool", X[sset][0][:], d_X[sset][0], r1[:], [dr1], r2[:], [dr2], ALU.subtract)
                            TT("dve", r3[:], dr3, zbr, dzbr, s_, ds, ALU.mult)
                            TT("dve", r4[:], dr4, zbi, dzbi, c, dc, ALU.mult)
                            TT("dve", X[sset][1][:], d_X[sset][1], r3[:], [dr3], r4[:], [dr4], ALU.add)
                            for i in range(2):
                                kb.op("pe", lambda e: e.matmul(ps[ybank][:], lhsT=CT[i][:, q, :], rhs=X[sset][i][:], start=(k == 0 and i == 0), stop=(k == 3 and i == 1)),
                                      reads=[d_CT, d_X[sset][i]], writes=[dps[ybank]])
                            if k == 3:
                                kb.op("dve", lambda e: e.scalar_tensor_tensor(out=tf[0][:], in0=uT[ui][:], scalar=Dt[:, ct:ct + 1], in1=ps[ybank][:], op0=ALU.mult, op1=ALU.add),
                                      reads=[d_uT[ui], d_Dt, dps[ybank]], writes=[d_tf[0]])
                                kb.op("act", lambda e: e.activation(out=tf[1][:], in_=tf[0][:], func=AF.Square), reads=[d_tf[0]], writes=[d_tf[1]])
                                kb.op("pool", lambda e: e.tensor_scalar(out=tf[1][:], in0=tf[1][:], scalar1=0.044715, scalar2=1.0, op0=ALU.mult, op1=ALU.add),
                                      reads=[d_tf[1]], writes=[d_tf[1]])
                                kb.op("pool", lambda e: e.tensor_tensor(out=tf[1][:], in0=tf[1][:], in1=tf[0][:], op=ALU.mult), reads=[d_tf[1], d_tf[0]], writes=[d_tf[1]])
                                kb.op("act", lambda e: e.activation(out=tf[2][:], in_=tf[1][:], func=AF.Sigmoid, scale=1.5957691216057308), reads=[d_tf[1]], writes=[d_tf[2]])
                                kb.op("pool", lambda e: e.tensor_tensor(out=y5T[:, ct, tb * 512:(tb + 1) * 512], in0=tf[0][:], in1=tf[2][:], op=ALU.mult),
                                      reads=[d_tf[0], d_tf[2]], writes=[d_y5[tb]])

                        if pending:
                            pending.pop(0)()
                        pending.append(back)
            while pending:
                pending.pop(0)()
            szT = [sb("szT%d" % i, [128, 512], BF16) for i in range(2)]
            og = [sb("og%d" % i, [128, 512], BF16) for i in range(2)]
            d_sz = kb.deps_n(2, "sz")
            d_og = kb.deps_n(2, "og")
            ig = 0
            for tb in range(8):
                for co in range(8):
                    i = ig % 2
                    ig += 1
                    bank = 4 + (ig % 4)
                    kb.dma("sp", szT[i][:], self.sT[1024 + co * 128:1024 + (co + 1) * 128, tb * 512:(tb + 1) * 512], writes=[d_sz[i]])
                    for ci in range(8):
                        kb.op("pe", lambda e: e.matmul(ps[bank][:], lhsT=gluw[:, ci, co * 128:(co + 1) * 128], rhs=y5T[:, ci, tb * 512:(tb + 1) * 512],
                                                       start=(ci == 0), stop=(ci == 7)), reads=[d_glu, d_y5[tb]], writes=[dps[bank]])
                    kb.op("act", lambda e: e.activation(out=tf[0][:], in_=ps[bank][:], func=AF.Sigmoid), reads=[dps[bank]], writes=[d_tf[0]])
                    kb.op("act", lambda e: e.activation(out=tf[1][:], in_=szT[i][:], func=AF.Silu), reads=[d_sz[i]], writes=[d_tf[1]])
                    kb.op("dve", lambda e: e.tensor_tensor(out=tf[0][:], in0=tf[0][:], in1=y5T[:, co, tb * 512:(tb + 1) * 512], op=ALU.mult),
                          reads=[d_tf[0], d_y5[tb]], writes=[d_tf[0]])
                    kb.op("dve", lambda e: e.tensor_tensor(out=og[i][:], in0=tf[0][:], in1=tf[1][:], op=ALU.mult), reads=[d_tf[0], d_tf[1]], writes=[d_og[i]])
                    kb.dma("sp", self.mixedT[1024 + co * 128:1024 + (co + 1) * 128, tb * 512:(tb + 1) * 512], og[i][:], reads=[d_og[i]])

    def phase_S5(self, l):
        nc, kb = self.nc, self.kb
        ps, dps = self.ps, self.dps
        Lc = 4
        NCH = S // Lc
        NH = NCH // 512
        with ExitStack() as st:
            sb = lambda n, s, d: st.enter_context(self.sbt("S_" + n, s, d))
            CT = [sb("CT%d" % i, [128, 32, 128], BF16) for i in range(2)]
            Bp = [sb("Bp%d" % i, [128, 32, 128], BF16) for i in range(2)]
            prm = sb("prm", [128, 24, 32], F32)
            apw = sb("apw", [128, 9, 2, 32], F32)
            prmi = sb("prmi", [128, 32], I32)
            Dt = sb("Dt", [128, 8], F32)
            d_CT, d_prm, d_Dt, d_apw = kb.deps_n(4, "s5c")
            d_Bp = kb.deps_n(2, "Bp")
            AR, AI, LDT, DTT, MM, PHI, COS, SIN, FR, FI, M8, PHI8, C512, S512, T0, T1, T2, T3, T4, T5 = range(20)
            P = lambda i: prm[:, i, :]
            with ExitStack() as st2:
                sb2 = lambda n, s, d: st2.enter_context(self.sbt("S2_" + n, s, d))
                XA = sb2("XA", [32, 3, 128], F32)
                ld2 = sb2("ld2", [32, 2], F32)
                XD = sb2("XD", [8, 128], F32)
                Cp = [sb2("Cp%d" % i, [128, 32, 128], F32) for i in range(2)]
                Bf = [sb2("Bf%d" % i, [128, 32, 128], F32) for i in range(2)]
                pads = [Bf[0], Bf[1], Cp[0], Cp[1]]
                d_XA, d_ld2, d_XD = kb.deps_n(3, "xa")
                d_Cp = kb.deps_n(2, "Cp")
                d_Bf = kb.deps_n(2, "Bf")
                d_pad = [d_Bf[0], d_Bf[1], d_Cp[0], d_Cp[1]]
                kb.dma("sp", XA[:, 0, :], self.a_re[l].rearrange("(q gl) p -> q (gl p)", gl=2), writes=[d_XA])
                kb.dma("sp", XA[:, 1, :], self.a_im[l].rearrange("(q gl) p -> q (gl p)", gl=2), writes=[d_XA])
                kb.dma("sp", ld2[:], self.log_dt[l:l + 1, :].rearrange("o (q gl) -> (o q) gl", gl=2), writes=[d_ld2])
                kb.dma("sp", XD[:], self.s5_d[l:l + 1, :].rearrange("o (c p) -> (o c) p", p=128), writes=[d_XD])
                kb.op("dve", lambda e: e.tensor_copy(out=XA[:, 2, :].rearrange("q (gl p) -> q gl p", gl=2),
                                                     in_=ld2[:].unsqueeze(2).to_broadcast([32, 2, 64])),
                      reads=[d_ld2, d_XA], writes=[d_XA])
                for i in range(4):
                    eng = "dve" if i % 2 == 0 else "pool"
                    kb.op(eng, lambda e: e.memset(pads[i][:].rearrange("p q c -> p (q c)"), 0.0), writes=[d_pad[i]])
                srcB = [self.b_re[l], self.b_im[l]]
                srcC = [self.c_re[l], self.c_im[l]]
                for k in range(4):
                    for gl in range(2):
                        for i in range(2):
                            dstb = pads[i][gl * 64:(gl + 1) * 64, :, :].rearrange("p (ct k) c -> p k ct c", k=4)[:, k, :, 32 * k + 16 * gl:32 * k + 16 * gl + 16]
                            sb_ = srcB[i].rearrange("(ct k gl) p c -> k gl p ct c", k=4, gl=2)[k, gl]
                            kb.dma("sp", dstb, sb_, reads=[d_pad[i]], writes=[d_pad[i]])
                            dstc = pads[2 + i][32 * k + 16 * gl:32 * k + 16 * gl + 16, :, :].rearrange("p (ct k) c -> p k ct c", k=4)[:, k, :, gl * 64:(gl + 1) * 64]
                            sc_ = srcC[i].rearrange("(ct k gl) c p -> k gl c ct p", k=4, gl=2)[k, gl]
                            kb.dma("sp", dstc, sc_, reads=[d_pad[2 + i]], writes=[d_pad[2 + i]])
                for j in range(3):
                    kb.op("pe", lambda e: e.transpose(out=ps[0][:, j * 32:(j + 1) * 32], in_=XA[:, j, :], identity=self.identf[0:32, 0:32]),
                          reads=[d_XA, self.d_const], writes=[dps[0]])
                kb.op("pe", lambda e: e.transpose(out=ps[0][:, 96:104], in_=XD[:], identity=self.identf[0:8, 0:8]),
                      reads=[d_XD, self.d_const], writes=[dps[0]])
                kb.op("dve", lambda e: e.tensor_copy(out=prm[:, 0:3, :].rearrange("p a q -> p (a q)"), in_=ps[0][:, 0:96]), reads=[dps[0]], writes=[d_prm])
                kb.op("dve", lambda e: e.tensor_copy(out=Dt[:], in_=ps[0][:, 96:104]), reads=[dps[0]], writes=[d_Dt])
                kb.op("act", lambda e: e.activation(out=Bp[0][:].rearrange("p q c -> p (q c)"), in_=Bf[0][:].rearrange("p q c -> p (q c)"), func=AF.Copy),
                      reads=[d_Bf[0]], writes=[d_Bp[0]])
                kb.op("pool", lambda e: e.tensor_copy(out=Bp[1][:].rearrange("p q c -> p (q c)"), in_=Bf[1][:].rearrange("p q c -> p (q c)")),
                      reads=[d_Bf[1]], writes=[d_Bp[1]])
                R_, W_ = [d_prm], [d_prm]
                tt = lambda o, a, b, op: kb.op("dve", lambda e: e.tensor_tensor(out=P(o), in0=P(a), in1=P(b), op=op), reads=R_, writes=W_)
                kb.op("act", lambda e: e.activation(out=P(DTT), in_=P(LDT), func=AF.Exp), reads=R_, writes=W_)
                tt(T0, DTT, AR, ALU.mult)
                kb.op("act", lambda e: e.activation(out=P(MM), in_=P(T0), func=AF.Exp), reads=R_, writes=W_)
                kb.op("act", lambda e: e.activation(out=P(M8), in_=P(T0), func=AF.Exp, scale=float(Lc)), reads=R_, writes=W_)
                tt(T0, DTT, AI, ALU.mult)
                kb.op("dve", lambda e: e.tensor_scalar(out=P(T1), in0=P(T0), scalar1=1.0 / (2.0 * math.pi), scalar2=None, op0=ALU.mult), reads=R_, writes=W_)
                kb.op("dve", lambda e: e.tensor_copy(out=prmi[:], in_=P(T1)), reads=R_, writes=W_)
                kb.op("dve", lambda e: e.tensor_tensor(out=P(PHI), in0=P(T1), in1=prmi[:], op=ALU.subtract), reads=R_, writes=W_)
                self.sincos_turns(P(PHI), P(COS), P(SIN), P(T2), prmi[:], P(T3), d_prm, d_prm, d_prm, d_prm)
                kb.op("dve", lambda e: e.tensor_scalar(out=P(T4), in0=P(PHI), scalar1=float(Lc), scalar2=None, op0=ALU.mult), reads=R_, writes=W_)
                kb.op("dve", lambda e: e.tensor_copy(out=prmi[:], in_=P(T4)), reads=R_, writes=W_)
                kb.op("dve", lambda e: e.tensor_tensor(out=P(PHI8), in0=P(T4), in1=prmi[:], op=ALU.subtract), reads=R_, writes=W_)
                kb.op("dve", lambda e: e.tensor_scalar(out=P(T4), in0=P(PHI8), scalar1=512.0, scalar2=None, op0=ALU.mult), reads=R_, writes=W_)
                self.sincos_turns(P(T4), P(C512), P(S512), P(T2), prmi[:], P(T3), d_prm, d_prm, d_prm, d_prm)
                tt(T0, MM, COS, ALU.mult)
                tt(T1, MM, SIN, ALU.mult)
                RW = [d_prm, d_apw]
                kb.op("dve", lambda e: e.memset(apw[:, 0, 0, :], 1.0), reads=RW, writes=[d_apw])
                kb.op("dve", lambda e: e.memset(apw[:, 0, 1, :], 0.0), reads=RW, writes=[d_apw])
                kb.op("dve", lambda e: e.tensor_copy(out=apw[:, 1, 0, :], in_=P(T0)), reads=RW, writes=[d_apw])
                kb.op("dve", lambda e: e.tensor_copy(out=apw[:, 1, 1, :], in_=P(T1)), reads=RW, writes=[d_apw])
                for m in range(1, Lc):
                    ar_, ai_ = apw[:, m, 0, :], apw[:, m, 1, :]
                    kb.op("dve", lambda e: e.tensor_tensor(out=P(T2), in0=ar_, in1=P(T0), op=ALU.mult), reads=RW, writes=W_)
                    kb.op("dve", lambda e: e.tensor_tensor(out=P(T3), in0=ai_, in1=P(T1), op=ALU.mult), reads=RW, writes=W_)
                    kb.op("dve", lambda e: e.tensor_tensor(out=apw[:, m + 1, 0, :], in0=P(T2), in1=P(T3), op=ALU.subtract), reads=RW, writes=[d_apw])
                    kb.op("dve", lambda e: e.tensor_tensor(out=P(T2), in0=ar_, in1=P(T1), op=ALU.mult), reads=RW, writes=W_)
                    kb.op("dve", lambda e: e.tensor_tensor(out=P(T3), in0=ai_, in1=P(T0), op=ALU.mult), reads=RW, writes=W_)
                    kb.op("dve", lambda e: e.tensor_tensor(out=apw[:, m + 1, 1, :], in0=P(T2), in1=P(T3), op=ALU.add), reads=RW, writes=[d_apw])
                kb.op("dve", lambda e: e.tensor_scalar(out=P(T0), in0=P(T0), scalar1=-1.0, scalar2=None, op0=ALU.add), reads=R_, writes=W_)
                tt(T2, AR, AR, ALU.mult)
                tt(T3, AI, AI, ALU.mult)
                tt(T2, T2, T3, ALU.add)
                kb.op("dve", lambda e: e.reciprocal(out=P(T2), in_=P(T2)), reads=R_, writes=W_)
                tt(T3, T0, AR, ALU.mult)
                tt(T4, T1, AI, ALU.mult)
                tt(T3, T3, T4, ALU.add)
                tt(FR, T3, T2, ALU.mult)
                tt(T3, T1, AR, ALU.mult)
                tt(T4, T0, AI, ALU.mult)
                tt(T3, T3, T4, ALU.subtract)
                tt(FI, T3, T2, ALU.mult)
                ctmp = [sb2("ctmp%d" % i, [128, 4, 128], F32) for i in range(4)]
                d_ctmp = kb.dep("ctmp")
                for q4 in range(8):
                    for i in range(2):
                        bank = 6 + i
                        for k in range(4):
                            q = q4 * 4 + k
                            kb.op("pe", lambda e: e.transpose(out=ps[bank][:, k * 128:(k + 1) * 128], in_=Cp[i][:, q, :], identity=self.identf[:]),
                                  reads=[d_Cp[i], self.d_const], writes=[dps[bank]])
                    frb = prm[:, FR, q4 * 4:(q4 + 1) * 4].unsqueeze(2).to_broadcast([128, 4, 128])
                    fib = prm[:, FI, q4 * 4:(q4 + 1) * 4].unsqueeze(2).to_broadcast([128, 4, 128])
                    crp = ps[6][:].rearrange("p (a b) -> p a b", a=4)
                    cip = ps[7][:].rearrange("p (a b) -> p a b", a=4)
                    tmpw = [d_ctmp]
                    kb.op("dve", lambda e: e.tensor_tensor(out=ctmp[0][:], in0=crp, in1=frb, op=ALU.mult), reads=[dps[6], d_prm], writes=tmpw)
                    kb.op("dve", lambda e: e.tensor_tensor(out=ctmp[1][:], in0=cip, in1=fib, op=ALU.mult), reads=[dps[7], d_prm], writes=tmpw)
                    kb.op("dve", lambda e: e.tensor_tensor(out=CT[0][:, q4 * 4:(q4 + 1) * 4, :], in0=ctmp[0][:], in1=ctmp[1][:], op=ALU.subtract),
                          reads=tmpw, writes=[d_CT])
                    kb.op("dve", lambda e: e.tensor_tensor(out=ctmp[2][:], in0=crp, in1=fib, op=ALU.mult), reads=[dps[6], d_prm], writes=tmpw)
                    kb.op("dve", lambda e: e.tensor_tensor(out=ctmp[3][:], in0=cip, in1=frb, op=ALU.mult), reads=[dps[7], d_prm], writes=tmpw)
                    kb.op("dve", lambda e: e.scalar_tensor_tensor(out=CT[1][:, q4 * 4:(q4 + 1) * 4, :], in0=ctmp[2][:], scalar=-1.0, in1=ctmp[3][:],
                                                                  op0=ALU.mult, op1=ALU.subtract), reads=tmpw, writes=[d_CT])
                kb.barrier()
            with ExitStack() as st3:
                sb3 = lambda n, s, d: st3.enter_context(self.sbt("S3_" + n, s, d))
                cosT = [sb3("cosT%d" % i, [128, 4, 512], BF16) for i in range(2)]
                sinT = [sb3("sinT%d" % i, [128, 4, 512], BF16) for i in range(2)]
                iota = sb3("iota", [128, 512], F32)
                angi = sb3("angi", [128, 512], I32)
                zero = sb3("zero", [128, 2], F32)
                czc = [sb3("czc%d" % i, [128, 2, 4], F32) for i in range(2)]
                czt = sb3("czt", [128, 2], F32)
                zl = sb3("zlast", [128, 2], F32)
                d_czc = kb.deps_n(2, "czc")
                d_czt = kb.dep("czt")
                d_zl = kb.dep("zl")
                uTf = [sb3("uTf%d" % i, [128, S], BF16) for i in range(2)]
                W1 = sb3("W1", [128, Lc, 8, 128], BF16)
                CA = [sb3("CA%d" % i, [128, Lc, 8, 128], BF16) for i in range(2)]
                SN = [sb3("SN%d" % i, [128, 8, 128], BF16) for i in range(2)]
                KT = [sb3("KT%d" % i, [128, Lc, 128], BF16) for i in range(2)]
                uu = [sb3("uu%d" % i, [128, 4, 128], F32) for i in range(4)]
                Xs = [[[sb3("Xs%d%d%d" % (c_, k, i), [128, NCH + 2], BF16) for i in range(2)] for k in range(4)] for c_ in range(2)]
                tf = [sb3("tf%d" % i, [128, 512], F32) for i in range(3)]
                tg = [sb3("tg%d" % i, [128, 512], F32) for i in range(3)]
                tb_ = [sb3("tb%d" % i, [128, 512], BF16) for i in range(6)]
                rb_ = [sb3("rb%d" % i, [128, 512], BF16) for i in range(4)]
                BuS = [[sb3("BuS%d%d" % (i, j), [128, 512], BF16) for j in range(2)] for i in range(2)]
                zb = [[sb3("zb%d%d" % (i, j), [128, 512], BF16) for j in range(2)] for i in range(2)]
                y5s = [sb3("y5s%d" % i, [128, 512], BF16) for i in range(2)]
                d_cos = kb.deps_n(2, "cos")
                d_sin = kb.deps_n(2, "sin")
                d_iota, d_angi, d_zero, d_W1 = kb.deps_n(4, "tab")
                d_CA = kb.deps_n(2, "CA")
                d_KT = kb.deps_n(2, "KT")
                d_uTf = kb.deps_n(2, "uTf")
                d_SN = kb.deps_n(2, "SN")
                d_SNr = kb.deps_n(2, "SNr")
                d_CAr = kb.deps_n(2, "CAr")
                d_uu = kb.deps_n(4, "uu")
                d_Xs = [[kb.deps_n(2) for k in range(4)] for c_ in range(2)]
                d_tf = kb.deps_n(3, "tf")
                d_tg = kb.deps_n(3, "tg")
                d_tb = kb.deps_n(6, "tb")
                d_rb = kb.deps_n(4, "rb")
                d_BuS = [kb.deps_n(2) for i in range(2)]
                d_zb = [kb.deps_n(2) for i in range(2)]
                d_y5s = kb.deps_n(2, "y5s")
                kb.dma("sp", iota[:], self.c_iota, writes=[d_iota])
                kb.op("dve", lambda e: e.memset(zero[:], 0.0), writes=[d_zero])
                for c_ in range(2):
                    for k in range(4):
                        for i in range(2):
                            kb.op("pool", lambda e: e.memset(Xs[c_][k][i][:], 0.0), writes=[d_Xs[c_][k][i]])
                t1, t2, t3, t4, wr, wi = tb_
                dt1, dt2, dt3, dt4, dwr, dwi = d_tb
                r1, r2, r3, r4 = rb_
                dr1, dr2, dr3, dr4 = d_rb

                def TT(eng, o, do, a, da, b_, db, op):
                    kb.op(eng, lambda e: e.tensor_tensor(out=o, in0=a, in1=b_, op=op), reads=da + db, writes=[do])

                def E_slice(ct, step):
                    cp = ct % 2
                    q0 = ct * 4
                    if step == 0:
                        kb.dma("sp", uTf[cp][:], self.sT[ct * 128:(ct + 1) * 128, :], writes=[d_uTf[cp]])
                    if step < 4:
                        k = step
                        q = q0 + k
                        kb.op("dve", lambda e: e.tensor_scalar(out=tg[0][:], in0=iota[:], scalar1=prm[:, PHI8, q:q + 1], scalar2=None, op0=ALU.mult),
                              reads=[d_iota, d_prm], writes=[d_tg[0]])
                        self.sincos_turns(tg[0][:], cosT[cp][:, k, :], sinT[cp][:, k, :], tg[1][:], angi[:], tg[2][:], d_tg[0], d_cos[cp], d_sin[cp], d_tg[1])
                    m = step
                    mi = m % 2
                    Brv = Bp[0][:, q0:q0 + 4, :]
                    Biv = Bp[1][:, q0:q0 + 4, :]
                    Arb = apw[:, m, 0, q0:q0 + 4].unsqueeze(2).to_broadcast([128, 4, 128])
                    Aib = apw[:, m, 1, q0:q0 + 4].unsqueeze(2).to_broadcast([128, 4, 128])
                    TT("dve", uu[0][:], d_uu[0], Brv, [d_Bp[0]], Arb, [d_apw], ALU.mult)
                    TT("dve", uu[1][:], d_uu[1], Biv, [d_Bp[1]], Aib, [d_apw], ALU.mult)
                    TT("dve", SN[mi][:, 0:4, :], d_SNr[mi], uu[0][:], [d_uu[0]], uu[1][:], [d_uu[1]], ALU.subtract)
                    TT("pool", uu[2][:], d_uu[2], Biv, [d_Bp[1]], Arb, [d_apw], ALU.mult)
                    TT("pool", uu[3][:], d_uu[3], Brv, [d_Bp[0]], Aib, [d_apw], ALU.mult)
                    TT("pool", SN[mi][:, 4:8, :], d_SN[mi], uu[2][:], [d_uu[2]], uu[3][:], [d_uu[3]], ALU.add)
                    j = step
                    C0v = CT[0][:, q0:q0 + 4, :]
                    C1v = CT[1][:, q0:q0 + 4, :]
                    Arb = apw[:, j + 1, 0, q0:q0 + 4].unsqueeze(2).to_broadcast([128, 4, 128])
                    Aib = apw[:, j + 1, 1, q0:q0 + 4].unsqueeze(2).to_broadcast([128, 4, 128])
                    TT("dve", uu[0][:], d_uu[0], C0v, [d_CT], Arb, [d_apw], ALU.mult)
                    TT("dve", uu[1][:], d_uu[1], C1v, [d_CT], Aib, [d_apw], ALU.mult)
                    TT("dve", CA[cp][:, j, 0:4, :], d_CAr[cp], uu[0][:], [d_uu[0]], uu[1][:], [d_uu[1]], ALU.add)
                    TT("pool", uu[2][:], d_uu[2], C1v, [d_CT], Arb, [d_apw], ALU.mult)
                    TT("pool", uu[3][:], d_uu[3], C0v, [d_CT], Aib, [d_apw], ALU.mult)
                    TT("pool", CA[cp][:, j, 4:8, :], d_CA[cp], uu[2][:], [d_uu[2]], uu[3][:], [d_uu[3]], ALU.subtract)

                def T_slice(ct, step):
                    cp = ct % 2
                    q0 = ct * 4
                    m = step
                    mi = m % 2
                    pb = ps[6][:].bitcast(BF16)
                    for j8 in range(8):
                        kb.op("pe", lambda e: e.transpose(out=pb[:, j8 * 128:(j8 + 1) * 128], in_=SN[mi][:, j8, :], identity=self.identb[:]),
                              reads=[d_SN[mi], d_SNr[mi], self.d_const], writes=[dps[6]])
                    kb.op("act", lambda e: e.activation(out=W1[:, m, :, :].rearrange("p a b -> p (a b)"), in_=pb[:, 0:1024], func=AF.Copy),
                          reads=[dps[6]], writes=[d_W1])
                    ksl = ps[7][:, 0:128]
                    for j8 in range(8):
                        i_, k_ = j8 // 4, j8 % 4
                        kb.op("pe", lambda e: e.matmul(ksl, lhsT=SN[mi][:, j8, :], rhs=CT[i_][:, q0 + k_, :], start=(j8 == 0), stop=(j8 == 7)),
                              reads=[d_SN[mi], d_SNr[mi], d_CT], writes=[dps[7]])
                    kb.op("act", lambda e: e.activation(out=KT[cp][:, m, :], in_=ksl, func=AF.Copy), reads=[dps[7]], writes=[d_KT[cp]])

                it_box = [0]

                def H_stage(ct):
                    cp = ct % 2
                    q0 = ct * 4
                    pending = []
                    for hh in range(NH):
                        for k in range(4):
                            q = q0 + k
                            sset = it_box[0] % 2
                            it_box[0] += 1
                            c = cosT[cp][:, k, :]
                            s_ = sinT[cp][:, k, :]
                            dc, ds = [d_cos[cp]], [d_sin[cp]]
                            for i in range(2):
                                bnk = 2 * sset + i
                                for j in range(Lc):
                                    kb.op("pe", lambda e: e.matmul(ps[bnk][:], lhsT=W1[:, Lc - 1 - j, i * 4 + k, :],
                                                                   rhs=uTf[cp][:, hh * 512 * Lc + j:(hh + 1) * 512 * Lc:Lc], start=(j == 0), stop=(j == Lc - 1)),
                                          reads=[d_W1, d_uTf[cp]], writes=[dps[bnk]])
                                kb.op("act", lambda e: e.activation(out=BuS[sset][i][:], in_=ps[bnk][:], func=AF.Copy), reads=[dps[bnk]], writes=[d_BuS[sset][i]])
                            Br, Bi = BuS[sset][0][:], BuS[sset][1][:]
                            dBr, dBi = [d_BuS[sset][0]], [d_BuS[sset][1]]
                            TT("dve", t1[:], dt1, Br, dBr, c, dc, ALU.mult)
                            TT("dve", t2[:], dt2, Bi, dBi, s_, ds, ALU.mult)
                            TT("dve", wr[:], dwr, t1[:], [dt1], t2[:], [dt2], ALU.add)
                            TT("pool", t3[:], dt3, Bi, dBi, c, dc, ALU.mult)
                            TT("pool", t4[:], dt4, Br, dBr, s_, ds, ALU.mult)
                            TT("pool", wi[:], dwi, t3[:], [dt3], t4[:], [dt4], ALU.subtract)
                            mb = prm[:, M8, q:q + 1].to_broadcast([128, 512])
                            par = hh % 2
                            for i, w_, dw_ in ((0, wr, dwr), (1, wi, dwi)):
                                init = zero[:, i:i + 1] if hh == 0 else czc[par][:, i, k:k + 1]
                                rd = [d_prm, dw_, d_zero] if hh == 0 else [d_prm, dw_, d_czc[par]]
                                kb.op("dve", lambda e: e.tensor_tensor_scan(out=zb[sset][i][:], data0=mb, data1=w_[:], initial=init, op0=ALU.mult, op1=ALU.add),
                                      reads=rd, writes=[d_zb[sset][i]])
                            if hh + 1 < NH:
                                nx = 1 - par
                                kb.op("dve", lambda e: e.tensor_copy(out=zl[:, 0:1], in_=zb[sset][0][:, 511:512]), reads=[d_zb[sset][0]], writes=[d_zl])
                                kb.op("dve", lambda e: e.tensor_copy(out=zl[:, 1:2], in_=zb[sset][1][:, 511:512]), reads=[d_zb[sset][1]], writes=[d_zl])
                                c5, s5 = prm[:, C512, q:q + 1], prm[:, S512, q:q + 1]
                                kb.op("dve", lambda e: e.tensor_scalar(out=czt[:, 0:1], in0=zl[:, 1:2], scalar1=s5, scalar2=None, op0=ALU.mult), reads=[d_zl, d_prm], writes=[d_czt])
                                kb.op("dve", lambda e: e.scalar_tensor_tensor(out=czc[nx][:, 0, k:k + 1], in0=zl[:, 0:1], scalar=c5, in1=czt[:, 0:1], op0=ALU.mult, op1=ALU.subtract),
                                      reads=[d_zl, d_prm, d_czt], writes=[d_czc[nx]])
                                kb.op("dve", lambda e: e.tensor_scalar(out=czt[:, 1:2], in0=zl[:, 1:2], scalar1=c5, scalar2=None, op0=ALU.mult), reads=[d_zl, d_prm], writes=[d_czt])
                                kb.op("dve", lambda e: e.scalar_tensor_tensor(out=czc[nx][:, 1, k:k + 1], in0=zl[:, 0:1], scalar=s5, in1=czt[:, 1:2], op0=ALU.mult, op1=ALU.add),
                                      reads=[d_zl, d_prm, d_czt], writes=[d_czc[nx]])

                            def back(sset=sset, c=c, s_=s_, dc=dc, ds=ds, k=k, hh=hh):
                                zbr, zbi = zb[sset][0][:], zb[sset][1][:]
                                dzbr, dzbi = [d_zb[sset][0]], [d_zb[sset][1]]
                                o0 = 1 + hh * 512
                                TT("pool", r1[:], dr1, zbr, dzbr, c, dc, ALU.mult)
                                TT("pool", r2[:], dr2, zbi, dzbi, s_, ds, ALU.mult)
                                TT("pool", Xs[cp][k][0][:, o0:o0 + 512], d_Xs[cp][k][0], r1[:], [dr1], r2[:], [dr2], ALU.subtract)
                                TT("dve", r3[:], dr3, zbr, dzbr, s_, ds, ALU.mult)
                                TT("dve", r4[:], dr4, zbi, dzbi, c, dc, ALU.mult)
                                TT("dve", Xs[cp][k][1][:, o0:o0 + 512], d_Xs[cp][k][1], r3[:], [dr3], r4[:], [dr4], ALU.add)

                            if pending:
                                pending.pop(0)()
                            pending.append(back)
                    while pending:
                        pending.pop(0)()

                iy_box = [0]

                def Y_mm(ct, blk):
                    cp = ct % 2
                    yb = 4 + (iy_box[0] % 2)
                    for j in range(Lc):
                        osl = ps[yb][:, j:512:Lc]
                        nck = 512 // Lc
                        for tau in range(j + 1):
                            kb.op("pe", lambda e: e.matmul(osl, lhsT=KT[cp][:, tau, :], rhs=uTf[cp][:, blk * 512 + j - tau:blk * 512 + 512:Lc], start=(tau == 0), stop=False),
                                  reads=[d_KT[cp], d_uTf[cp]], writes=[dps[yb]])
                        for k in range(4):
                            for i in range(2):
                                kb.op("pe", lambda e: e.matmul(osl, lhsT=CA[cp][:, j, i * 4 + k, :], rhs=Xs[cp][k][i][:, blk * nck:(blk + 1) * nck], start=False, stop=(k == 3 and i == 1)),
                                      reads=[d_CA[cp], d_CAr[cp], d_Xs[cp][k][i]], writes=[dps[yb]])

                def Y_epi(ct, blk):
                    cp = ct % 2
                    yb = 4 + (iy_box[0] % 2)
                    yi = iy_box[0] % 2
                    iy_box[0] += 1
                    kb.op("dve", lambda e: e.scalar_tensor_tensor(out=tf[0][:], in0=uTf[cp][:, blk * 512:(blk + 1) * 512], scalar=Dt[:, ct:ct + 1], in1=ps[yb][:],
                                                                  op0=ALU.mult, op1=ALU.add), reads=[d_uTf[cp], d_Dt, dps[yb]], writes=[d_tf[0]])
                    kb.op("act", lambda e: e.activation(out=tf[1][:], in_=tf[0][:], func=AF.Square), reads=[d_tf[0]], writes=[d_tf[1]])
                    kb.op("pool", lambda e: e.tensor_scalar(out=tf[1][:], in0=tf[1][:], scalar1=0.044715, scalar2=1.0, op0=ALU.mult, op1=ALU.add),
                          reads=[d_tf[1]], writes=[d_tf[1]])
                    kb.op("dve", lambda e: e.tensor_tensor(out=tf[1][:], in0=tf[1][:], in1=tf[0][:], op=ALU.mult), reads=[d_tf[1], d_tf[0]], writes=[d_tf[1]])
                    kb.op("act", lambda e: e.activation(out=tf[2][:], in_=tf[1][:], func=AF.Sigmoid, scale=1.5957691216057308), reads=[d_tf[1]], writes=[d_tf[2]])
                    kb.op("dve", lambda e: e.tensor_tensor(out=y5s[yi][:], in0=tf[0][:], in1=tf[2][:], op=ALU.mult), reads=[d_tf[0], d_tf[2]], writes=[d_y5s[yi]])
                    kb.dma("sp", self.y5d[ct * 128:(ct + 1) * 128, blk * 512:(blk + 1) * 512], y5s[yi][:], reads=[d_y5s[yi]])

                for step in range(Lc):
                    E_slice(0, step)
                    T_slice(0, step)
                H_stage(0)
                for ct in range(8):
                    nxt = ct + 1 < 8
                    if nxt:
                        E_slice(ct + 1, 0)
                    for blk in range(8):
                        if nxt and blk < Lc:
                            T_slice(ct + 1, blk)
                        Y_mm(ct, blk)
                        if nxt and blk + 1 < Lc:
                            E_slice(ct + 1, blk + 1)
                        Y_epi(ct, blk)
                    if nxt:
                        H_stage(ct + 1)
                kb.barrier()
            with ExitStack() as st4:
                sb4 = lambda n, s, d: st4.enter_context(self.sbt("S4_" + n, s, d))
                gluw = sb4("gluw", [128, 8, 1024], BF16)
                y5b = [sb4("y5b%d" % i, [128, 8, 512], BF16) for i in range(2)]
                NB = 3
                szT = [sb4("szT%d" % i, [128, 512], BF16) for i in range(NB)]
                og = [sb4("og%d" % i, [128, 512], BF16) for i in range(NB)]
                g1 = [sb4("g1%d" % i, [128, 512], BF16) for i in range(NB)]
                g2 = [sb4("g2%d" % i, [128, 512], BF16) for i in range(NB)]
                d_glu = kb.dep("glu")
                d_y5b = kb.deps_n(2, "y5b")
                d_sz = kb.deps_n(NB, "sz")
                d_og = kb.deps_n(NB, "og")
                d_g1 = kb.deps_n(NB, "g1")
                d_g2 = kb.deps_n(NB, "g2")
                kb.dma("pool", gluw[:], self.glu_w[l].rearrange("(k p) n -> p k n", p=128), writes=[d_glu])
                ig = 0
                for tb in range(8):
                    yi = tb % 2
                    kb.dma("sp", y5b[yi][:], self.y5d[:, tb * 512:(tb + 1) * 512].rearrange("(c p) t -> p c t", p=128), writes=[d_y5b[yi]])
                    for co in range(8):
                        i = ig % NB
                        bank = ig % 4
                        ig += 1
                        kb.dma("sp", szT[i][:], self.sT[1024 + co * 128:1024 + (co + 1) * 128, tb * 512:(tb + 1) * 512], writes=[d_sz[i]])
                        for ci in range(8):
                            kb.op("pe", lambda e: e.matmul(ps[bank][:], lhsT=gluw[:, ci, co * 128:(co + 1) * 128], rhs=y5b[yi][:, ci, :],
                                                           start=(ci == 0), stop=(ci == 7)), reads=[d_glu, d_y5b[yi]], writes=[dps[bank]])
                        kb.op("act", lambda e: e.activation(out=g1[i][:], in_=ps[bank][:], func=AF.Sigmoid), reads=[dps[bank]], writes=[d_g1[i]])
                        kb.op("act", lambda e: e.activation(out=g2[i][:], in_=szT[i][:], func=AF.Sigmoid), reads=[d_sz[i]], writes=[d_g2[i]])
                        kb.op("dve", lambda e: e.tensor_tensor(out=g2[i][:], in0=g2[i][:], in1=szT[i][:], op=ALU.mult), reads=[d_g2[i], d_sz[i]], writes=[d_g2[i]])
                        kb.op("dve", lambda e: e.tensor_tensor(out=g1[i][:], in0=g1[i][:], in1=y5b[yi][:, co, :], op=ALU.mult),
                              reads=[d_g1[i], d_y5b[yi]], writes=[d_g1[i]])
                        kb.op("dve", lambda e: e.tensor_tensor(out=og[i][:], in0=g1[i][:], in1=g2[i][:], op=ALU.mult), reads=[d_g1[i], d_g2[i]], writes=[d_og[i]])
                        kb.dma("pool", self.mixedT[1024 + co * 128:1024 + (co + 1) * 128, tb * 512:(tb + 1) * 512], og[i][:], reads=[d_og[i]])

    def qk_prep(self, tag, src, col0, nh, normw_dram, l, dstT, d_dst, rope_tab, d_rope, ntiles=NT, bank=4):
        nc, kb = self.nc, self.kb
        ps, dps = self.ps, self.dps
        G = 4 if ntiles % 4 == 0 else 2
        NBUF = 4
        with ExitStack() as st:
            sb = lambda n, s, d: st.enter_context(self.sbt("P_%s_%s" % (tag, n), s, d))
            W = nh * 128
            GH = G * nh
            nw = sb("nw", [128, 128], F32)
            qraw = [sb("qraw%d" % i, [128, G, W], BF16) for i in range(NBUF)]
            sq = [sb("sq%d" % i, [128, GH, 128], BF16) for i in range(NBUF)]
            ss = [sb("ss%d" % i, [128, GH], F32) for i in range(NBUF)]
            qn = [sb("qn%d" % i, [128, GH, 128], F32) for i in range(NBUF)]
            rt = [sb("rt%d" % i, [128, 4, GH, 16], F32) for i in range(NBUF)]
            qb = [sb("qb%d" % i, [128, GH, 128], BF16) for i in range(NBUF)]
            d_nw = kb.dep()
            d_qraw = kb.deps_n(NBUF)
            d_sq = kb.deps_n(NBUF)
            d_ss = kb.deps_n(NBUF)
            d_qn = kb.deps_n(NBUF)
            d_rt = kb.deps_n(NBUF)
            d_qb = kb.deps_n(NBUF)
            kb.dma("sp", nw[:], normw_dram[l:l + 1, :].partition_broadcast(128), writes=[d_nw])
            def f1(g):
                i = g % NBUF
                t0 = g * G
                kb.dma("sp", qraw[i][:], src[t0 * 128:(t0 + G) * 128, col0:col0 + W].rearrange("(g p) c -> p g c", p=128), writes=[d_qraw[i]])
                qv = qraw[i][:].rearrange("p g (h c) -> p (g h) c", c=128)
                kb.op("pool", lambda e: e.tensor_tensor(out=sq[i][:], in0=qv, in1=qv, op=ALU.mult), reads=[d_qraw[i]], writes=[d_sq[i]])
                kb.op("dve", lambda e: e.tensor_reduce(out=ss[i][:], in_=sq[i][:], axis=AX.X, op=ALU.add), reads=[d_sq[i]], writes=[d_ss[i]])
                kb.op("dve", lambda e: e.tensor_scalar(out=ss[i][:], in0=ss[i][:], scalar1=1.0 / 128, scalar2=EPS, op0=ALU.mult, op1=ALU.add),
                      reads=[d_ss[i]], writes=[d_ss[i]])
                kb.op("act", lambda e: e.activation(out=ss[i][:], in_=ss[i][:], func=AF.Sqrt), reads=[d_ss[i]], writes=[d_ss[i]])
                kb.op("dve", lambda e: e.reciprocal(out=ss[i][:], in_=ss[i][:]), reads=[d_ss[i]], writes=[d_ss[i]])
            def f2(g):
                i = g % NBUF
                t0 = g * G
                qv = qraw[i][:].rearrange("p g (h c) -> p (g h) c", c=128)
                kb.op("dve", lambda e: e.tensor_tensor(out=qn[i][:], in0=qv, in1=ss[i][:].unsqueeze(2).to_broadcast([128, GH, 128]), op=ALU.mult),
                      reads=[d_qraw[i], d_ss[i]], writes=[d_qn[i]])
                kb.op("pool", lambda e: e.tensor_tensor(out=qn[i][:], in0=qn[i][:], in1=nw[:].unsqueeze(1).to_broadcast([128, GH, 128]), op=ALU.mult),
                      reads=[d_qn[i], d_nw], writes=[d_qn[i]])
                cb = rope_tab[:, t0:t0 + G, 0:16].unsqueeze(2).to_broadcast([128, G, nh, 16])
                sbb = rope_tab[:, t0:t0 + G, 16:32].unsqueeze(2).to_broadcast([128, G, nh, 16])
                q4 = qn[i][:].rearrange("p (g h) c -> p g h c", g=G)
                x1 = q4[:, :, :, 0:16]
                x2 = q4[:, :, :, 16:32]
                rv = lambda j: rt[i][:, j].rearrange("p (g h) c -> p g h c", g=G)
                R_ = [d_qn[i], d_rope]
                kb.op("dve", lambda e: e.tensor_tensor(out=rv(0), in0=x1, in1=cb, op=ALU.mult), reads=R_, writes=[d_rt[i]])
                kb.op("dve", lambda e: e.tensor_tensor(out=rv(1), in0=x2, in1=sbb, op=ALU.mult), reads=R_, writes=[d_rt[i]])
                kb.op("dve", lambda e: e.tensor_tensor(out=rv(2), in0=x2, in1=cb, op=ALU.mult), reads=R_, writes=[d_rt[i]])
                kb.op("dve", lambda e: e.tensor_tensor(out=rv(3), in0=x1, in1=sbb, op=ALU.mult), reads=R_, writes=[d_rt[i]])
                kb.op("dve", lambda e: e.tensor_tensor(out=qb[i][:, :, 0:16], in0=rt[i][:, 0], in1=rt[i][:, 1], op=ALU.subtract), reads=[d_rt[i]], writes=[d_qb[i]])
                kb.op("dve", lambda e: e.tensor_tensor(out=qb[i][:, :, 16:32], in0=rt[i][:, 2], in1=rt[i][:, 3], op=ALU.add), reads=[d_rt[i]], writes=[d_qb[i]])
                kb.op("act", lambda e: e.activation(out=qb[i][:, :, 32:128], in_=qn[i][:, :, 32:128], func=AF.Copy), reads=[d_qn[i]], writes=[d_qb[i]])
            def f3(g):
                i = g % NBUF
                t0 = g * G
                nb = (GH + 7) // 8
                for bi in range(nb):
                    bk = bank + ((g * nb + bi) % 4)
                    pb = ps[bk][:].bitcast(BF16)
                    n_here = min(8, GH - bi * 8)
                    for j in range(n_here):
                        kb.op("pe", lambda e: e.transpose(out=pb[:, j * 128:(j + 1) * 128], in_=qb[i][:, bi * 8 + j, :], identity=self.identb[:]),
                              reads=[d_qb[i], self.d_const], writes=[dps[bk]])
                    ng = n_here // nh
                    gt0 = t0 + (bi * 8) // nh
                    dst = dstT[:, :, gt0 * 128:(gt0 + ng) * 128].rearrange("p h (g n) -> p g h n", g=ng)
                    srcp = pb[:, 0:n_here * 128].rearrange("p (g h n) -> p g h n", g=ng, h=nh)
                    eng = "act" if bi % 2 == 0 else "dve"
                    if eng == "act":
                        kb.op("act", lambda e: e.activation(out=dst, in_=srcp, func=AF.Copy), reads=[dps[bk]], writes=[d_dst])
                    else:
                        kb.op("dve", lambda e: e.tensor_copy(out=dst, in_=srcp), reads=[dps[bk]], writes=[d_dst])
            self.emit_pipelined(ntiles // G, [f1, f2, f3])
            kb.barrier()

    def phase_MOBA(self, l):
        nc, kb = self.nc, self.kb
        ps, dps = self.ps, self.dps
        SC = 1.0 / math.sqrt(128.0)
        with ExitStack() as st:
            sb = lambda n, s, d: st.enter_context(self.sbt("M_" + n, s, d))
            QT = sb("QT", [128, 4, S], BF16)
            KT = sb("KT", [128, 4, S], BF16)
            Vp = sb("Vp", [128, NT, 4, 130], BF16)
            rope = sb("rope", [128, NT, 32], F32)
            tri = sb("tri", [128, 128], BF16)
            kmf = sb("kmf", [128, 4, 16], F32)
            kmT = sb("kmT", [128, 4, 16], BF16)
            d_QT, d_KT, d_Vp, d_SEL, d_OM, d_rope, d_tri, d_km = kb.deps_n(8, "mb")
            kb.dma("sp", rope[:], self.c_rope.rearrange("(t p) c -> p t c", p=128), writes=[d_rope])
            kb.dma("sp", tri[:], self.c_tri, writes=[d_tri])
            kb.op("pool", lambda e: e.memset(Vp[:].rearrange("p a b c -> p (a b c)"), 1.0), writes=[d_Vp])
            for h in range(4):
                kb.dma("sp", Vp[:, :, h, 0:128], self.proj_tm[:, C_MV + h * 128:C_MV + (h + 1) * 128].rearrange("(t p) c -> p t c", p=128),
                       reads=[d_Vp], writes=[d_Vp])
            self.qk_prep("mq", self.proj_tm, C_MQ, 4, self.hn["moba_q_norm"], l, QT, d_QT, rope, d_rope)
            self.qk_prep("mk", self.proj_tm, C_MK, 4, self.hn["moba_k_norm"], l, KT, d_KT, rope, d_rope)
            kb.barrier()
            SEL = sb("SEL", [128, NT, 4, 16], F32)
            OM = sb("OM", [128, NT, 512], BF16)
            kb.op("dve", lambda e: e.memset(SEL[:].rearrange("p a b c -> p (a b c)"), 1.0), writes=[d_SEL])
            for h in range(4):
                kb.op("dve", lambda e: e.tensor_reduce(out=kmf[:, h, :], in_=KT[:, h, :].rearrange("p (n k) -> p n k", k=256), axis=AX.X, op=ALU.add),
                      reads=[d_KT], writes=[d_km])
            kb.op("dve", lambda e: e.tensor_scalar(out=kmT[:], in0=kmf[:], scalar1=1.0 / 256, scalar2=None, op0=ALU.mult), reads=[d_km], writes=[d_km])
            with ExitStack() as st2:
                sb2 = lambda n, s, d: st2.enter_context(self.sbt("M2_" + n, s, d))
                gt = [sb2("gt%d" % i, [128, 4, 16], F32) for i in range(2)]
                m8 = [sb2("m8%d" % i, [128, 4, 8], F32) for i in range(2)]
                d_gt = kb.deps_n(2)
                d_m8 = kb.deps_n(2)
                for tt in range(8, NT):
                    own = tt // 2
                    i = tt % 2
                    for h in range(4):
                        kb.op("pe", lambda e: e.matmul(ps[4][:, h * 16:(h + 1) * 16], lhsT=QT[:, h, tt * 128:(tt + 1) * 128], rhs=kmT[:, h, :], start=True, stop=True),
                              reads=[d_QT, d_km], writes=[dps[4]])
                    kb.op("dve", lambda e: e.tensor_copy(out=gt[i][:].rearrange("p a b -> p (a b)"), in_=ps[4][:, 0:64]), reads=[dps[4]], writes=[d_gt[i]])
                    kb.op("dve", lambda e: e.memset(gt[i][:, :, own:16], NEG), reads=[d_gt[i]], writes=[d_gt[i]])
                    for h in range(4):
                        kb.op("dve", lambda e: e.max(out=m8[i][:, h, :], in_=gt[i][:, h, :]), reads=[d_gt[i]], writes=[d_m8[i]])
                    for h in range(4):
                        kb.op("dve", lambda e: e.tensor_scalar(out=SEL[:, tt, h, :], in0=gt[i][:, h, :], scalar1=m8[i][:, h, 2:3], scalar2=None, op0=ALU.is_ge),
                              reads=[d_gt[i], d_m8[i]], writes=[d_SEL])
            kb.barrier()
            PT = [sb("PT%d" % i, [128, 512], BF16) for i in range(5)]
            acc = [sb("acc%d" % i, [128, 2, 130], F32) for i in range(2)]
            rr = [sb("rr%d" % i, [128, 2], F32) for i in range(2)]
            acc1 = [sb("acc1%d" % i, [128, 130], F32) for i in range(2)]
            wtmp = [sb("wtmp%d" % i, [128, 129], F32) for i in range(3)]
            d_acc1 = kb.deps_n(2)
            d_wtmp = kb.deps_n(3)
            d_PT = kb.deps_n(5)
            d_acc = kb.deps_n(2)
            d_rr = kb.deps_n(2)
            iters = [(h, qb, n) for h in range(4) for qb in range(16) for n in range(qb + 1)]

            def front(idx):
                h, qb, n = iters[idx]
                pi = idx % 5
                sbank = (0, 1, 2, 5)[idx % 4]
                for kt in range(2):
                    kb.op("pe", lambda e: e.matmul(ps[sbank][:, kt * 256:(kt + 1) * 256], lhsT=KT[:, h, (2 * n + kt) * 128:(2 * n + kt + 1) * 128],
                                                   rhs=QT[:, h, qb * 256:(qb + 1) * 256], start=True, stop=True),
                          reads=[d_KT, d_QT], writes=[dps[sbank]])
                kb.op("act", lambda e: e.activation(out=PT[pi][:], in_=ps[sbank][:], func=AF.Exp, scale=SC), reads=[dps[sbank]], writes=[d_PT[pi]])
                if n == qb:
                    kb.op("pool", lambda e: e.tensor_tensor(out=PT[pi][:, 0:128], in0=PT[pi][:, 0:128], in1=tri[:], op=ALU.mult),
                          reads=[d_PT[pi], d_tri], writes=[d_PT[pi]])
                    kb.op("pool", lambda e: e.tensor_tensor(out=PT[pi][:, 384:512], in0=PT[pi][:, 384:512], in1=tri[:], op=ALU.mult),
                          reads=[d_PT[pi], d_tri], writes=[d_PT[pi]])

            def back(idx):
                h, qb, n = iters[idx]
                pi = idx % 5
                oA = 3 + (idx % 2)
                oB = 6 + (idx % 2)
                ai = (h * 16 + qb) % 2
                if n == 0:
                    kb.op("pool", lambda e: e.memset(acc[ai][:].rearrange("p a b -> p (a b)"), 0.0), writes=[d_acc[ai]])
                    kb.op("pool", lambda e: e.memset(acc1[ai][:], 0.0), writes=[d_acc1[ai]])
                if n < qb:
                    for qt, ob in ((0, oA), (1, oB)):
                        for kt in range(2):
                            kb.op("pe", lambda e: e.matmul(ps[ob][:, 0:129], lhsT=PT[pi][:, kt * 256 + qt * 128:kt * 256 + (qt + 1) * 128],
                                                           rhs=Vp[:, 2 * n + kt, h, 0:129], start=(kt == 0), stop=(kt == 1)),
                                  reads=[d_PT[pi], d_Vp], writes=[dps[ob]])
                    kb.op("dve", lambda e: e.scalar_tensor_tensor(out=acc[ai][:, 0, 0:129], in0=ps[oA][:, 0:129],
                                                                  scalar=SEL[:, 2 * qb, h, n:n + 1], in1=acc[ai][:, 0, 0:129],
                                                                  op0=ALU.mult, op1=ALU.add),
                          reads=[dps[oA], d_SEL, d_acc[ai]], writes=[d_acc[ai]])
                    wi_ = idx % 3
                    kb.op("act", lambda e: e.activation(out=wtmp[wi_][:], in_=ps[oB][:, 0:129], func=AF.Copy, scale=SEL[:, 2 * qb + 1, h, n:n + 1]),
                          reads=[dps[oB], d_SEL], writes=[d_wtmp[wi_]])
                    kb.op("pool", lambda e: e.tensor_tensor(out=acc1[ai][:, 0:129], in0=acc1[ai][:, 0:129], in1=wtmp[wi_][:], op=ALU.add),
                          reads=[d_wtmp[wi_], d_acc1[ai]], writes=[d_acc1[ai]])
                else:
                    kb.op("pe", lambda e: e.matmul(ps[oA][:, 0:129], lhsT=PT[pi][:, 0:128], rhs=Vp[:, 2 * qb, h, 0:129], start=True, stop=True),
                          reads=[d_PT[pi], d_Vp], writes=[dps[oA]])
                    kb.op("pe", lambda e: e.matmul(ps[oB][:, 0:129], lhsT=PT[pi][:, 128:256], rhs=Vp[:, 2 * qb, h, 0:129], start=True, stop=False),
                          reads=[d_PT[pi], d_Vp], writes=[dps[oB]])
                    kb.op("pe", lambda e: e.matmul(ps[oB][:, 0:129], lhsT=PT[pi][:, 384:512], rhs=Vp[:, 2 * qb + 1, h, 0:129], start=False, stop=True),
                          reads=[d_PT[pi], d_Vp], writes=[dps[oB]])
                    kb.op("dve", lambda e: e.tensor_tensor(out=acc[ai][:, 0, 0:129], in0=ps[oA][:, 0:129], in1=acc[ai][:, 0, 0:129], op=ALU.add),
                          reads=[dps[oA], d_acc[ai]], writes=[d_acc[ai]])
                    kb.op("dve", lambda e: e.tensor_tensor(out=acc[ai][:, 1, 0:129], in0=ps[oB][:, 0:129], in1=acc1[ai][:, 0:129], op=ALU.add),
                          reads=[dps[oB], d_acc1[ai], d_acc[ai]], writes=[d_acc[ai]])
                    kb.op("dve", lambda e: e.reciprocal(out=rr[ai][:], in_=acc[ai][:, :, 128]), reads=[d_acc[ai]], writes=[d_rr[ai]])
                    for qt in range(2):
                        kb.op("dve", lambda e: e.tensor_scalar(out=OM[:, 2 * qb + qt, h * 128:(h + 1) * 128], in0=acc[ai][:, qt, 0:128], scalar1=rr[ai][:, qt:qt + 1],
                                                               scalar2=None, op0=ALU.mult), reads=[d_acc[ai], d_rr[ai]], writes=[d_OM])

            SK = 3
            for idx in range(min(SK, len(iters))):
                front(idx)
            for idx in range(len(iters)):
                if idx + SK < len(iters):
                    front(idx + SK)
                back(idx)
            kb.barrier()
            self.gate_and_store(OM, d_OM, C_MZ, 0)

    def gate_and_store(self, OM, d_OM, zcol, row0):
        nc, kb = self.nc, self.kb
        ps, dps = self.ps, self.dps
        with ExitStack() as st:
            sb = lambda n, s, d: st.enter_context(self.sbt("G_" + n, s, d))
            zt = [sb("zt%d" % i, [128, 512], BF16) for i in range(4)]
            sl = [sb("sl%d" % i, [128, 512], F32) for i in range(4)]
            gg = [sb("gg%d" % i, [128, 512], BF16) for i in range(4)]
            oT = [sb("oT%d" % i, [128, 4, 128], BF16) for i in range(4)]
            d_zt = kb.deps_n(4)
            d_sl = kb.deps_n(4)
            d_gg = kb.deps_n(4)
            d_oT = kb.deps_n(4)
            def g1(tt):
                i = tt % 4
                kb.dma("sp", zt[i][:], self.proj_tm[tt * 128:(tt + 1) * 128, zcol:zcol + 512], writes=[d_zt[i]])
                kb.op("act", lambda e: e.activation(out=sl[i][:], in_=zt[i][:], func=AF.Silu), reads=[d_zt[i]], writes=[d_sl[i]])
                kb.op("dve", lambda e: e.tensor_tensor(out=gg[i][:], in0=OM[:, tt, :], in1=sl[i][:], op=ALU.mult), reads=[d_OM, d_sl[i]], writes=[d_gg[i]])
            def g2(tt):
                i = tt % 4
                bk = 4 + i
                pb = ps[bk][:].bitcast(BF16)
                for h in range(4):
                    kb.op("pe", lambda e: e.transpose(out=pb[:, h * 128:(h + 1) * 128], in_=gg[i][:, h * 128:(h + 1) * 128], identity=self.identb[:]),
                          reads=[d_gg[i], self.d_const], writes=[dps[bk]])
                kb.op("act", lambda e: e.activation(out=oT[i][:].rearrange("p h n -> p (h n)"), in_=pb[:, 0:512], func=AF.Copy), reads=[dps[bk]], writes=[d_oT[i]])
                kb.dma("pool", self.mixedT[row0:row0 + 512, tt * 128:(tt + 1) * 128].rearrange("(h p) n -> p h n", p=128), oT[i][:], reads=[d_oT[i]])
            self.emit_pipelined(NT, [g1, g2])

    def gelu_tanh(self, x, dx, tmp, dtmp, out, dout):
        kb = self.kb
        kb.op("act", lambda e: e.activation(out=tmp, in_=x, func=AF.Square), reads=[dx], writes=[dtmp])
        kb.op("dve", lambda e: e.tensor_scalar(out=tmp, in0=tmp, scalar1=0.044715, scalar2=1.0, op0=ALU.mult, op1=ALU.add), reads=[dtmp], writes=[dtmp])
        kb.op("dve", lambda e: e.tensor_tensor(out=tmp, in0=tmp, in1=x, op=ALU.mult), reads=[dtmp, dx], writes=[dtmp])
        kb.op("act", lambda e: e.activation(out=tmp, in_=tmp, func=AF.Sigmoid, scale=1.5957691216057308), reads=[dtmp], writes=[dtmp])
        kb.op("dve", lambda e: e.tensor_tensor(out=out, in0=x, in1=tmp, op=ALU.mult), reads=[dx, dtmp], writes=[dout])

    def phase_NSA(self, l):
        nc, kb = self.nc, self.kb
        ps, dps = self.ps, self.dps
        SC = 1.0 / math.sqrt(128.0)
        with ExitStack() as st:
            sb = lambda n, s, d: st.enter_context(self.sbt("N_" + n, s, d))
            NQT = sb("NQT", [128, 4, S], BF16)
            KST = sb("KST", [128, 1, S], BF16)
            KWT = sb("KWT", [128, 1, S], BF16)
            KCT = sb("KCT", [128, 1, 256], BF16)
            VS = sb("VS", [128, NT, 130], BF16)
            VW = sb("VW", [128, NT, 130], BF16)
            RC = sb("RC", [128, 2, 196], BF16)
            SELT = sb("SELT", [64, NT, 128], BF16)
            ESEL = sb("ESEL", [64, NT, 128], BF16)
            G = sb("G", [128, NT, 12], F32)
            rope = sb("rope", [128, NT, 32], F32)
            ropec = sb("ropec", [128, 2, 32], F32)
            tri = sb("tri", [128, 128], BF16)
            triu = sb("triu", [128, 128], BF16)
            dkq = sb("dkq", [128, 128], F32)
            d_NQT, d_KST, d_KWT, d_KCT, d_VS, d_VW, d_RC, d_ONS, d_SELT, d_G, d_rope, d_cst = kb.deps_n(12, "ns")
            kb.dma("sp", rope[:], self.c_rope.rearrange("(t p) c -> p t c", p=128), writes=[d_rope])
            kb.dma("sp", ropec[:], self.c_ropec.rearrange("(t p) c -> p t c", p=128), writes=[d_rope])
            kb.dma("sp", tri[:], self.c_tri, writes=[d_cst])
            kb.dma("sp", triu[:], self.c_triu, writes=[d_cst])
            kb.dma("sp", dkq[:], self.c_dkq, writes=[d_cst])
            kb.dma("sp", ESEL[:], self.c_esel, writes=[d_cst])
            kb.op("pool", lambda e: e.memset(VS[:].rearrange("p a b -> p (a b)"), 1.0), writes=[d_VS])
            kb.op("pool", lambda e: e.memset(VW[:].rearrange("p a b -> p (a b)"), 1.0), writes=[d_VW])
            kb.op("pool", lambda e: e.memset(RC[:].rearrange("p a b -> p (a b)"), 1.0), writes=[d_RC])
            kb.dma("sp", VS[:, :, 0:128], self.proj_tm[:, C_NVS:C_NVS + 128].rearrange("(t p) c -> p t c", p=128), reads=[d_VS], writes=[d_VS])
            kb.dma("sp", VW[:, :, 0:128], self.proj_tm[:, C_NVW:C_NVW + 128].rearrange("(t p) c -> p t c", p=128), reads=[d_VW], writes=[d_VW])
            kb.dma("sp", RC[:, :, 129:193], self.c_ovl.rearrange("(t p) j -> p t j", p=128), reads=[d_RC], writes=[d_RC])
            kb.dma("pool", G[:], self.proj_tm[:, C_NG:C_NG + 12].rearrange("(t p) c -> p t c", p=128), writes=[d_G])
            kb.op("act", lambda e: e.activation(out=G[:].rearrange("p a b -> p (a b)"), in_=G[:].rearrange("p a b -> p (a b)"), func=AF.Sigmoid),
                  reads=[d_G], writes=[d_G])
            self.qk_prep("nq", self.proj_tm, C_NQ, 4, self.hn["nsa_q_norm"], l, NQT, d_NQT, rope, d_rope)
            self.qk_prep("nks", self.proj_tm, C_NKS, 1, self.hn["nsa_ks_norm"], l, KST, d_KST, rope, d_rope)
            self.qk_prep("nkw", self.proj_tm, C_NKW, 1, self.hn["nsa_kw_norm"], l, KWT, d_KWT, rope, d_rope)
            with ExitStack() as st2:
                sb2 = lambda n, s, d: st2.enter_context(self.sbt("N2_" + n, s, d))
                XcT = sb2("XcT", [128, 2, S], BF16)
                w1b = [sb2("w1b%d" % i, [128, 32, 128], BF16) for i in range(2)]
                w2b = [sb2("w2b%d" % i, [128, 128], BF16) for i in range(2)]
                pef = sb2("pef", [32, 2, 128], F32)
                peT = sb2("peT", [128, 2, 32], BF16)
                cvec = sb2("cvec", [128, 2], F32)
                xr = [sb2("xr%d" % i, [128, 256], BF16) for i in range(2)]
                hs = sb2("hs", [128, 256], F32)
                htmp = sb2("htmp", [128, 256], F32)
                hT = [sb2("hT%d" % i, [128, 256], BF16) for i in range(2)]
                kcs = sb2("kcs", [128, 2, 128], BF16)
                d_XcT, d_w1, d_w2, d_pef, d_peT, d_cvec, d_hs, d_htmp, d_kcs = kb.deps_n(9, "cm")
                d_xr = kb.deps_n(2)
                d_hT = kb.deps_n(2)
                w1s = [self.ck_w1, self.cv_w1]
                w2s = [self.ck_w2, self.cv_w2]
                pes = [self.pe_k, self.pe_v]
                for j in range(2):
                    kb.dma("pool", w1b[j][:], w1s[j][l].rearrange("(l d) o -> d l o", d=128), writes=[d_w1])
                    kb.dma("pool", w2b[j][:], w2s[j][l], writes=[d_w2])
                    kb.dma("sp", pef[:, j, :], pes[j][l], writes=[d_pef])
                for j in range(2):
                    kb.op("pe", lambda e: e.transpose(out=ps[0][:, j * 32:(j + 1) * 32], in_=pef[:, j, :], identity=self.identf[0:32, 0:32]),
                          reads=[d_pef, self.d_const], writes=[dps[0]])
                kb.op("dve", lambda e: e.tensor_copy(out=peT[:].rearrange("p a b -> p (a b)"), in_=ps[0][:, 0:64]), reads=[dps[0]], writes=[d_peT])
                for tt in range(NT):
                    i = tt % 2
                    kb.dma("sp", xr[i][:], self.proj_tm[tt * 128:(tt + 1) * 128, C_NKC:C_NKC + 256], writes=[d_xr[i]])
                    bk = 5 + i
                    pb = ps[bk][:].bitcast(BF16)
                    for j in range(2):
                        kb.op("pe", lambda e: e.transpose(out=pb[:, j * 128:(j + 1) * 128], in_=xr[i][:, j * 128:(j + 1) * 128], identity=self.identb[:]),
                              reads=[d_xr[i], self.d_const], writes=[dps[bk]])
                    kb.op("dve", lambda e: e.tensor_copy(out=XcT[:, :, tt * 128:(tt + 1) * 128], in_=pb[:, 0:256].rearrange("p (j n) -> p j n", j=2)),
                          reads=[dps[bk]], writes=[d_XcT])
                for j in range(2):
                    for ll in range(32):
                        kb.op("pe", lambda e: e.matmul(ps[1][:, j:j + 1], lhsT=w1b[j][:, ll, :], rhs=peT[:, j, ll:ll + 1], start=(ll == 0), stop=(ll == 31)),
                              reads=[d_w1, d_peT], writes=[dps[1]])
                    kb.op("dve", lambda e: e.tensor_copy(out=cvec[:, j:j + 1], in_=ps[1][:, j:j + 1]), reads=[dps[1]], writes=[d_cvec])
                    for ll in range(32):
                        kb.op("pe", lambda e: e.matmul(ps[2 + j][:, 0:255], lhsT=w1b[j][:, ll, :], rhs=XcT[:, j, ll:ll + 16 * 254 + 1:16], start=(ll == 0), stop=(ll == 31)),
                              reads=[d_w1, d_XcT], writes=[dps[2 + j]])
                    kb.op("dve", lambda e: e.tensor_scalar(out=hs[:, 0:255], in0=ps[2 + j][:, 0:255], scalar1=cvec[:, j:j + 1], scalar2=None, op0=ALU.add),
                          reads=[dps[2 + j], d_cvec], writes=[d_hs])
                    kb.op("pool", lambda e: e.memset(hT[j][:], 0.0), writes=[d_hT[j]])
                    self.gelu_tanh(hs[:, 0:255], d_hs, htmp[:, 0:255], d_htmp, hT[j][:, 0:255], d_hT[j])
                    for it in range(2):
                        kb.op("pe", lambda e: e.matmul(ps[4][:, it * 128:(it + 1) * 128], lhsT=hT[j][:, it * 128:(it + 1) * 128], rhs=w2b[j][:], start=True, stop=True),
                              reads=[d_hT[j], d_w2], writes=[dps[4]])
                    if j == 0:
                        kb.op("dve", lambda e: e.tensor_copy(out=kcs[:].rearrange("p a b -> p (a b)"), in_=ps[4][:, 0:256]), reads=[dps[4]], writes=[d_kcs])
                        kb.dma("sp", self.kcmp_tm.rearrange("(t p) c -> p t c", p=128), kcs[:], reads=[d_kcs])
                    else:
                        kb.op("dve", lambda e: e.tensor_copy(out=RC[:, :, 0:128], in_=ps[4][:, 0:256].rearrange("p (a b) -> p a b", a=2)),
                              reads=[dps[4], d_RC], writes=[d_RC])
                kb.barrier()
            self.qk_prep("nkc", self.kcmp_tm, 0, 1, self.hn["nsa_kc_norm"], l, KCT, d_KCT, ropec, d_rope, ntiles=2)
            ONS = sb("ONS", [128, NT, 512], F32)
            PT = [sb("PT%d" % i, [128, 4, 128], BF16) for i in range(8)]
            M2s = [sb("M2s%d" % i, [128, 128], BF16) for i in range(4)]
            sA = [sb("sA%d" % i, [128, 64], F32) for i in range(2)]
            sB = [sb("sB%d" % i, [128, 64], F32) for i in range(2)]
            imp = [sb("imp%d" % i, [128, 64], F32) for i in range(2)]
            sc = [sb("sc%d" % i, [128, 2, 64], F32) for i in range(2)]
            m8 = [sb("m8%d" % i, [128, 2, 8], F32) for i in range(2)]
            selq = [sb("selq%d" % i, [128, 64], BF16) for i in range(2)]
            rr = [sb("rr%d" % i, [128, 8], F32) for i in range(2)]
            d_PT = kb.deps_n(8)
            d_M2s = kb.deps_n(4)
            d_m2p = kb.deps_n(4)
            d_sA = kb.deps_n(2)
            d_sB = kb.deps_n(2)
            d_imp = kb.deps_n(2)
            d_sc = kb.deps_n(2)
            d_m8 = kb.deps_n(2)
            d_selq = kb.deps_n(2)
            d_rr = kb.deps_n(2)
            ip = 0

            def obank(tt, h):
                return 5 + h // 2, (h % 2) * 256

            zl = sb("zl", [128, 128], BF16)
            zr_ = sb("zr", [128, 512], BF16)
            d_z = kb.dep("zeros")
            kb.op("pool", lambda e: e.memset(zl[:], 0.0), writes=[d_z])
            kb.op("pool", lambda e: e.memset(zr_[:], 0.0), writes=[d_z])

            def zero_obanks():
                for b in (5, 6):
                    kb.op("pe", lambda e: e.matmul(ps[b][:], lhsT=zl[:], rhs=zr_[:], start=True, stop=True), reads=[d_z], writes=[dps[b]])

            def finalize(tt, branch, first):
                i = tt % 2
                for h in range(4):
                    b, c0 = obank(tt, h)
                    kb.op("dve", lambda e: e.tensor_scalar(out=rr[i][:, h:h + 1], in0=ps[b][:, c0 + 128:c0 + 129], scalar1=1e-30, scalar2=None, op0=ALU.max),
                          reads=[dps[b]], writes=[d_rr[i]])
                kb.op("dve", lambda e: e.reciprocal(out=rr[i][:, 0:4], in_=rr[i][:, 0:4]), reads=[d_rr[i]], writes=[d_rr[i]])
                kb.op("dve", lambda e: e.tensor_tensor(out=rr[i][:, 4:8], in0=rr[i][:, 0:4], in1=G[:, tt, branch:12:3], op=ALU.mult), reads=[d_rr[i], d_G], writes=[d_rr[i]])
                for h in range(4):
                    b, c0 = obank(tt, h)
                    dst = ONS[:, tt, h * 128:(h + 1) * 128]
                    if first:
                        kb.op("dve", lambda e: e.tensor_scalar(out=dst, in0=ps[b][:, c0:c0 + 128], scalar1=rr[i][:, 4 + h:5 + h], scalar2=None, op0=ALU.mult),
                              reads=[dps[b], d_rr[i]], writes=[d_ONS])
                    else:
                        kb.op("dve", lambda e: e.scalar_tensor_tensor(out=dst, in0=ps[b][:, c0:c0 + 128], scalar=rr[i][:, 4 + h:5 + h], in1=dst, op0=ALU.mult, op1=ALU.add),
                              reads=[dps[b], d_rr[i], d_ONS], writes=[d_ONS])

            it_cmp = [(tt, it) for tt in range(NT) for it in range(1 if tt < 16 else 2)]

            def c_front(idx):
                tt, it = it_cmp[idx]
                pi = idx % 3
                sbank = idx % 2
                i = tt % 2
                if it == 0:
                    kb.dma("sp", sA[i][:], self.c_selA[tt], writes=[d_sA[i]])
                    kb.dma("sp", sB[i][:], self.c_selB[tt], writes=[d_sB[i]])
                kb.op("pe", lambda e: e.matmul(ps[sbank][:], lhsT=KCT[:, 0, it * 128:(it + 1) * 128], rhs=NQT[:, :, tt * 128:(tt + 1) * 128], start=True, stop=True),
                      reads=[d_KCT, d_NQT], writes=[dps[sbank]])
                kb.op("act", lambda e: e.activation(out=PT[pi][:].rearrange("p a b -> p (a b)"), in_=ps[sbank][:], func=AF.Exp, scale=SC),
                      reads=[dps[sbank]], writes=[d_PT[pi]])
                thr = float(31 + 2048 * it - 128 * tt)
                kb.op("dve", lambda e: e.scalar_tensor_tensor(out=PT[pi][:], in0=dkq[:].unsqueeze(1).to_broadcast([128, 4, 128]), scalar=thr, in1=PT[pi][:],
                                                              op0=ALU.is_ge, op1=ALU.mult), reads=[d_PT[pi], d_cst], writes=[d_PT[pi]])

            def c_back(idx):
                tt, it = it_cmp[idx]
                pi = idx % 3
                i = tt % 2
                n_it = 1 if tt < 16 else 2
                for h in range(4):
                    b, c0 = obank(tt, h)
                    if it == 0 and h == 0:
                        zero_obanks()
                    kb.op("pe", lambda e: e.matmul(ps[b][:, c0:c0 + 193], lhsT=PT[pi][:, h, :], rhs=RC[:, it, 0:193], start=False, stop=(it == n_it - 1)),
                          reads=[d_PT[pi], d_RC], writes=[dps[b]])
                if it != n_it - 1:
                    return
                finalize(tt, 0, True)
                for h in range(4):
                    b, c0 = obank(tt, h)
                    if h == 0:
                        kb.op("dve", lambda e: e.tensor_scalar(out=imp[i][:], in0=ps[b][:, c0 + 129:c0 + 193], scalar1=rr[i][:, h:h + 1], scalar2=None, op0=ALU.mult),
                              reads=[dps[b], d_rr[i]], writes=[d_imp[i]])
                    else:
                        kb.op("dve", lambda e: e.scalar_tensor_tensor(out=imp[i][:], in0=ps[b][:, c0 + 129:c0 + 193], scalar=rr[i][:, h:h + 1], in1=imp[i][:],
                                                                      op0=ALU.mult, op1=ALU.add), reads=[dps[b], d_rr[i], d_imp[i]], writes=[d_imp[i]])
                kb.op("dve", lambda e: e.tensor_tensor(out=sc[i][:, 0, :], in0=imp[i][:], in1=sA[i][:], op=ALU.mult), reads=[d_imp[i], d_sA[i]], writes=[d_sc[i]])
                kb.op("dve", lambda e: e.tensor_tensor(out=sc[i][:, 0, :], in0=sc[i][:, 0, :], in1=sB[i][:], op=ALU.add), reads=[d_sc[i], d_sB[i]], writes=[d_sc[i]])
                kb.op("dve", lambda e: e.max(out=m8[i][:, 0, :], in_=sc[i][:, 0, :]), reads=[d_sc[i]], writes=[d_m8[i]])
                kb.op("dve", lambda e: e.match_replace(out=sc[i][:, 1, :], in_to_replace=m8[i][:, 0, :], in_values=sc[i][:, 0, :], imm_value=NEG),
                      reads=[d_sc[i], d_m8[i]], writes=[d_sc[i]])
                kb.op("dve", lambda e: e.max(out=m8[i][:, 1, :], in_=sc[i][:, 1, :]), reads=[d_sc[i]], writes=[d_m8[i]])
                kb.op("dve", lambda e: e.scalar_tensor_tensor(out=selq[i][:], in0=sc[i][:, 0, :], scalar=m8[i][:, 1, 7:8], in1=sA[i][:], op0=ALU.is_ge, op1=ALU.mult),
                      reads=[d_sc[i], d_m8[i], d_sA[i]], writes=[d_selq[i]])
                pb = ps[2 + i][:].bitcast(BF16)
                kb.op("pe", lambda e: e.transpose(out=pb[0:64, 0:128], in_=selq[i][:], identity=self.identb[:]), reads=[d_selq[i], self.d_const], writes=[dps[2 + i]])
                kb.op("act", lambda e: e.activation(out=SELT[:, tt, :], in_=pb[0:64, 0:128], func=AF.Copy), reads=[dps[2 + i]], writes=[d_SELT])

            c_front(0)
            for idx in range(len(it_cmp)):
                if idx + 1 < len(it_cmp):
                    c_front(idx + 1)
                c_back(idx)

            its = []
            for branch in (1, 2):
                for tt in range(NT):
                    kts = list(range(0, tt + 1)) if branch == 1 else list(range(max(0, tt - 4), tt + 1))
                    for ki, kt in enumerate(kts):
                        its.append((branch, tt, ki, kt, len(kts)))

            def a_front(idx):
                branch, tt, ki, kt, nk = its[idx]
                pi = 3 + idx % 5
                sbank = (0, 1, 2, 7)[idx % 4]
                KT_ = KST if branch == 1 else KWT
                d_KT_ = d_KST if branch == 1 else d_KWT
                kb.op("pe", lambda e: e.matmul(ps[sbank][:], lhsT=KT_[:, 0, kt * 128:(kt + 1) * 128], rhs=NQT[:, :, tt * 128:(tt + 1) * 128], start=True, stop=True),
                      reads=[d_KT_, d_NQT], writes=[dps[sbank]])
                kb.op("act", lambda e: e.activation(out=PT[pi][:].rearrange("p a b -> p (a b)"), in_=ps[sbank][:], func=AF.Exp, scale=SC),
                      reads=[dps[sbank]], writes=[d_PT[pi]])
                if branch == 1:
                    mb = 3 + (idx % 2)
                    mi = idx % 4
                    msl = ps[mb][:, 0:128]
                    kb.op("pe", lambda e: e.matmul(msl, lhsT=ESEL[:, kt, :], rhs=SELT[:, tt, :], start=True, stop=True),
                          reads=[d_cst, d_SELT], writes=[dps[mb]])
                    if kt == tt:
                        kb.op("dve", lambda e: e.tensor_tensor(out=M2s[mi][:], in0=msl, in1=tri[:], op=ALU.mult), reads=[dps[mb], d_cst], writes=[d_M2s[mi]])
                    else:
                        kb.op("dve", lambda e: e.tensor_copy(out=M2s[mi][:], in_=msl), reads=[dps[mb]], writes=[d_M2s[mi]])
                    meng = "pool" if idx % 3 == 0 else "dve"
                    kb.op(meng, lambda e: e.tensor_tensor(out=PT[pi][:], in0=PT[pi][:], in1=M2s[mi][:].unsqueeze(1).to_broadcast([128, 4, 128]), op=ALU.mult),
                          reads=[d_PT[pi], d_M2s[mi]], writes=[d_PT[pi]])
                else:
                    if kt == tt:
                        kb.op("pool", lambda e: e.tensor_tensor(out=PT[pi][:], in0=PT[pi][:], in1=tri[:].unsqueeze(1).to_broadcast([128, 4, 128]), op=ALU.mult),
                              reads=[d_PT[pi], d_cst], writes=[d_PT[pi]])
                    elif kt == tt - 4:
                        kb.op("pool", lambda e: e.tensor_tensor(out=PT[pi][:], in0=PT[pi][:], in1=triu[:].unsqueeze(1).to_broadcast([128, 4, 128]), op=ALU.mult),
                              reads=[d_PT[pi], d_cst], writes=[d_PT[pi]])

            def a_back(idx):
                branch, tt, ki, kt, nk = its[idx]
                pi = 3 + idx % 5
                V_ = VS if branch == 1 else VW
                d_V_ = d_VS if branch == 1 else d_VW
                for h in range(4):
                    b, c0 = obank(tt, h)
                    if ki == 0 and h == 0:
                        zero_obanks()
                    kb.op("pe", lambda e: e.matmul(ps[b][:, c0:c0 + 129], lhsT=PT[pi][:, h, :], rhs=V_[:, kt, 0:129], start=False, stop=(ki == nk - 1)),
                          reads=[d_PT[pi], d_V_], writes=[dps[b]])
                if ki == nk - 1:
                    finalize(tt, branch, False)

            SK = 3
            for idx in range(min(SK, len(its))):
                a_front(idx)
            for idx in range(len(its)):
                if idx + SK < len(its):
                    a_front(idx + SK)
                a_back(idx)
            kb.barrier()
            self.gate_and_store(ONS, d_ONS, C_NZ, 512)


def host_consts():
    c = {}
    c["c_identb"] = np.eye(128, dtype=np.float32).astype(ml_dtypes.bfloat16)
    c["c_identf"] = np.eye(128, dtype=np.float32)
    inv = 500000.0 ** (-np.arange(0, 32, 2, dtype=np.float32) / 32.0)
    pos = np.arange(S, dtype=np.float32)
    ang = pos[:, None] * inv[None, :].astype(np.float32)
    c["c_rope"] = np.concatenate([np.cos(ang), np.sin(ang)], axis=1).astype(np.float32)
    posc = (np.arange(256) * 16 + 31).astype(np.float32)
    angc = posc[:, None] * inv[None, :].astype(np.float32)
    c["c_ropec"] = np.concatenate([np.cos(angc), np.sin(angc)], axis=1).astype(np.float32)
    kk = np.arange(128)
    c["c_tri"] = (kk[:, None] <= kk[None, :]).astype(np.float32).astype(ml_dtypes.bfloat16)
    c["c_triu"] = (kk[:, None] > kk[None, :]).astype(np.float32).astype(ml_dtypes.bfloat16)
    c["c_iota"] = np.broadcast_to(np.arange(512, dtype=np.float32)[None, :], (128, 512)).copy()
    c["c_dkq"] = (kk[None, :] - 16 * kk[:, None]).astype(np.float32)
    selA = np.zeros((NT, 128, 64), np.float32)
    selB = np.zeros((NT, 128, 64), np.float32)
    j = np.arange(64)[None, :]
    for tt in range(NT):
        t = tt * 128 + np.arange(128)
        cur = (t // 64)[:, None]
        valid = j <= cur
        forced = (j == 0) | (j == cur) | (j == cur - 1)
        selA[tt] = valid.astype(np.float32)
        selB[tt] = np.where(forced, 1.0e30, np.where(valid, 0.0, -1.0e30))
    c["c_selA"] = selA
    c["c_selB"] = selB
    es = np.zeros((64, NT, 128), np.float32)
    for kt in range(NT):
        for key in range(128):
            es[2 * kt + key // 64, kt, key] = 1.0
    c["c_esel"] = es.astype(ml_dtypes.bfloat16)
    ci = np.arange(256)[:, None] * 16
    sj = np.arange(64)[None, :] * 64
    ov = ((ci < sj + 64) & (ci + 32 > sj)).astype(np.float32)
    ov[255] = 0.0
    c["c_ovl"] = ov.astype(ml_dtypes.bfloat16)
    return c


_PROG = None


def kernel(**inputs):
    global _PROG
    if _PROG is None:
        _PROG = Prog().build()
    nc = _PROG
    consts = host_consts()
    x = np.ascontiguousarray(inputs["x"], dtype=np.float32)
    B = x.shape[0]
    shared = {k: np.ascontiguousarray(v) for k, v in inputs.items() if k != "x"}
    in_maps = []
    for c in range(8):
        m = dict(shared)
        m.update(consts)
        m["x"] = x[c % B]
        in_maps.append(m)
    res = run_bass_kernel_spmd(nc, in_maps, core_ids=list(range(8)))
    out = np.stack([res.results[b]["out"] for b in range(B)], axis=0)
    return out.astype(np.float32, copy=False)
```

```python
import math
from contextlib import ExitStack

import numpy as np
import ml_dtypes
import concourse.bass as bass
import concourse.mybir as mybir
from concourse.bass_utils import run_bass_kernel_spmd

F32 = mybir.dt.float32
BF16 = mybir.dt.bfloat16
I32 = mybir.dt.int32
AF = mybir.ActivationFunctionType
ALU = mybir.AluOpType
AX = mybir.AxisListType

S = 4096
D = 2048
NT = S // 128
INW = 5900
TMW = 3852
DEPTH = 2
EPS = 1e-6
C_MQ, C_MK, C_MV, C_MZ, C_NQ = 0, 512, 1024, 1536, 2048
C_NKC, C_NVC, C_NKS, C_NVS, C_NKW, C_NVW = 2560, 2688, 2816, 2944, 3072, 3200
C_NG, C_NZ = 3328, 3340
NEG = -1.0e30


class Dep:
    __slots__ = ("w", "r", "name")

    def __init__(self, name=""):
        self.w = {}
        self.r = {}
        self.name = name


class Eng:
    def __init__(self, nc, eng, name):
        self.eng = eng
        self.name = name
        self.sem = nc.alloc_semaphore("sem_" + name)
        self.count = 0
        self.seen = {}


class KB:
    def __init__(self, nc, n_dma_sems=48):
        self.nc = nc
        self.E = {
            "pe": Eng(nc, nc.tensor, "pe"),
            "act": Eng(nc, nc.scalar, "act"),
            "dve": Eng(nc, nc.vector, "dve"),
            "pool": Eng(nc, nc.gpsimd, "pool"),
            "sp": Eng(nc, nc.sync, "sp"),
        }
        self.dsems = [[nc.alloc_semaphore("dsem%d" % i), 0] for i in range(n_dma_sems)]
        self.dnext = 0
        self.deps = []
        self.n_wait = 0
        self.n_ins = 0

    def dep(self, name=""):
        d = Dep(name)
        self.deps.append(d)
        return d

    def deps_n(self, n, name=""):
        return [self.dep(name + str(i)) for i in range(n)]

    def _wait(self, E, sem, val):
        k = id(sem)
        if E.seen.get(k, 0) < val:
            E.eng.wait_ge(sem, val)
            E.seen[k] = val
            self.n_wait += 1

    def _sync(self, E, reads, writes, own_sem=None):
        for d in reads:
            for k, (s, v) in d.w.items():
                self._wait(E, s, v)
        for d in writes:
            for k, (s, v) in d.w.items():
                if s is own_sem:
                    continue
                self._wait(E, s, v)
            for k, (s, v) in d.r.items():
                if s is own_sem:
                    continue
                self._wait(E, s, v)

    def _record(self, sem, val, reads, writes):
        k = id(sem)
        for d in writes:
            d.w = {k: (sem, val)}
            d.r = {}
        for d in reads:
            d.r[k] = (sem, val)

    def op(self, e, f, reads=(), writes=()):
        E = self.E[e]
        self._sync(E, reads, writes, own_sem=E.sem)
        ins = f(E.eng)
        E.count += 1
        ins.then_inc(E.sem, 1)
        self._record(E.sem, E.count, reads, writes)
        self.n_ins += 1
        return ins

    def dma(self, q, out, in_, reads=(), writes=(), **kw):
        E = self.E[q]
        self._sync(E, reads, writes)
        ent = self.dsems[self.dnext]
        self.dnext = (self.dnext + 1) % len(self.dsems)
        if ent[1] > 0:
            self._wait(E, ent[0], ent[1])
        ent[1] += 16
        ins = E.eng.dma_start(out=out, in_=in_, **kw)
        ins.then_inc(ent[0], 16)
        self._record(ent[0], ent[1], reads, writes)
        self.n_ins += 1
        return ins

    def barrier(self):
        sp = self.E["sp"]
        for n, E in self.E.items():
            if E is not sp and E.count > 0:
                self._wait(sp, E.sem, E.count)
        for s, v in self.dsems:
            if v > 0:
                self._wait(sp, s, v)
        sp.count += 1
        sp.eng.nop().then_inc(sp.sem, 1)
        for n, E in self.E.items():
            if E is not sp:
                self._wait(E, sp.sem, sp.count)
            for n2, E2 in self.E.items():
                E.seen[id(E2.sem)] = E2.count
            for s, v in self.dsems:
                E.seen[id(s)] = v
        for d in self.deps:
            d.w = {}
            d.r = {}
        self.deps = []


class Prog:
    def __init__(self, dbg=None, layers=DEPTH, phases=("A", "S5", "MOBA", "NSA", "F")):
        self.dbg = dbg or ()
        self.layers = layers
        self.phases = phases
        nc = bass.Bass("TRN2", target_bir_lowering=False)
        self.nc = nc
        self.kb = KB(nc)
        ein = lambda n, s, d: nc.dram_tensor(n, list(s), d, kind="ExternalInput").ap()
        L = DEPTH
        self.x = ein("x", [S, D], F32)
        self.norm_w = ein("norm_w", [L, D], F32)
        self.w_in = ein("w_in", [L, D, INW], F32)
        self.w_out = ein("w_out", [L, D, D], F32)
        self.hn = {}
        for n in ("moba_q_norm", "moba_k_norm", "nsa_q_norm", "nsa_kc_norm", "nsa_ks_norm", "nsa_kw_norm"):
            self.hn[n] = ein(n, [L, 128], F32)
        self.pe_k = ein("nsa_pe_k", [L, 32, 128], F32)
        self.pe_v = ein("nsa_pe_v", [L, 32, 128], F32)
        self.ck_w1 = ein("nsa_cmp_k_w1", [L, 4096, 128], F32)
        self.ck_w2 = ein("nsa_cmp_k_w2", [L, 128, 128], F32)
        self.cv_w1 = ein("nsa_cmp_v_w1", [L, 4096, 128], F32)
        self.cv_w2 = ein("nsa_cmp_v_w2", [L, 128, 128], F32)
        self.a_re = ein("s5_a_re", [L, 64, 64], F32)
        self.a_im = ein("s5_a_im", [L, 64, 64], F32)
        self.b_re = ein("s5_b_re", [L, 64, 64, 16], F32)
        self.b_im = ein("s5_b_im", [L, 64, 64, 16], F32)
        self.c_re = ein("s5_c_re", [L, 64, 16, 64], F32)
        self.c_im = ein("s5_c_im", [L, 64, 16, 64], F32)
        self.s5_d = ein("s5_d", [L, 1024], F32)
        self.log_dt = ein("s5_log_dt", [L, 64], F32)
        self.glu_w = ein("s5_glu_w", [L, 1024, 1024], F32)
        self.c_identb = ein("c_identb", [128, 128], BF16)
        self.c_identf = ein("c_identf", [128, 128], F32)
        self.c_rope = ein("c_rope", [S, 32], F32)
        self.c_ropec = ein("c_ropec", [256, 32], F32)
        self.c_tri = ein("c_tri", [128, 128], BF16)
        self.c_iota = ein("c_iota", [128, 512], F32)
        self.c_triu = ein("c_triu", [128, 128], BF16)
        self.c_dkq = ein("c_dkq", [128, 128], F32)
        self.c_selA = ein("c_selA", [NT, 128, 64], F32)
        self.c_selB = ein("c_selB", [NT, 128, 64], F32)
        self.c_esel = ein("c_esel", [64, NT, 128], BF16)
        self.c_ovl = ein("c_ovl", [256, 64], BF16)
        self.out = nc.dram_tensor("out", [S, D], F32, kind="ExternalOutput").ap()
        sk = lambda n: "ExternalOutput" if n in self.dbg else "Internal"
        self.proj_tm = nc.dram_tensor("proj_tm", [S, TMW], BF16, kind=("ExternalInput" if "proj_in" in self.dbg else sk("proj_tm"))).ap()
        self.sT = nc.dram_tensor("sT", [2048, S], BF16, kind=("ExternalInput" if "sT_in" in self.dbg else sk("sT"))).ap()
        self.mixedT = nc.dram_tensor("mixedT", [2048, S], BF16, kind=("ExternalInput" if "mixedT_in" in self.dbg else sk("mixedT"))).ap()
        self.x1 = nc.dram_tensor("x1", [S, D], F32, kind=sk("x1")).ap()
        self.kcmp_tm = nc.dram_tensor("kcmp_tm", [256, 128], BF16, kind=sk("kcmp_tm")).ap()
        self.y5d = nc.dram_tensor("y5d", [1024, S], BF16, kind=sk("y5d")).ap()

    @staticmethod
    def emit_pipelined(n, stages):
        ns = len(stages)
        for t in range(n + ns - 1):
            for s_idx in range(ns - 1, -1, -1):
                i = t - s_idx
                if 0 <= i < n:
                    stages[s_idx](i)

    def sbt(self, name, shape, dtype):
        self._uid = getattr(self, "_uid", 0) + 1
        return self.nc.sbuf_tensor("%s_u%d" % (name, self._uid), shape, dtype)

    def build(self):
        nc, kb = self.nc, self.kb
        with ExitStack() as st:
            self.ps = [st.enter_context(nc.psum_tensor("ps%d" % i, [128, 512], F32)) for i in range(8)]
            self.dps = kb.deps_n(8, "ps")
            self.identb = st.enter_context(self.sbt("identb", [128, 128], BF16))
            self.identf = st.enter_context(self.sbt("identf", [128, 128], F32))
            self.d_const = kb.dep("const")
            kb.dma("sp", self.identb[:], self.c_identb, writes=[self.d_const])
            kb.dma("sp", self.identf[:], self.c_identf, writes=[self.d_const])
            kb.barrier()
            for l in range(self.layers):
                src = self.x if l == 0 else self.x1
                dst = self.out if l == self.layers - 1 else self.x1
                if "A" in self.phases:
                    self.phase_A(l, src)
                    kb.barrier()
                if "S5" in self.phases:
                    self.phase_S5(l)
                    kb.barrier()
                if "MOBA" in self.phases:
                    self.phase_MOBA(l)
                    kb.barrier()
                if "NSA" in self.phases:
                    self.phase_NSA(l)
                    kb.barrier()
                if "F" in self.phases:
                    self.phase_F(l, src, dst)
                    kb.barrier()
            kb.barrier()
        return nc

    def phase_A(self, l, src):
        nc, kb = self.nc, self.kb
        ps, dps = self.ps, self.dps
        with ExitStack() as st:
            sb = lambda n, s, d: st.enter_context(self.sbt("A_" + n, s, d))
            hdnT = sb("hdnT", [128, 16, 2048], BF16)
            normw = sb("normw", [128, D], F32)
            xt = [sb("xt%d" % i, [128, D], F32) for i in range(3)]
            junk = sb("junk", [128, D], BF16)
            hb = [sb("hb%d" % i, [128, D], BF16) for i in range(3)]
            wch = [sb("wch%d" % i, [128, 16, 512], BF16) for i in range(2)]
            stg = [sb("stg%d" % i, [128, 512], BF16) for i in range(4)]
            ss = [sb("ss%d" % i, [128, 1], F32) for i in range(3)]
            d_hT = kb.deps_n(16, "hT")
            d_nw = kb.dep("nw")
            d_xt = kb.deps_n(3, "xt")
            d_junk = kb.dep("junk")
            d_hb = kb.deps_n(3, "hb")
            d_w = kb.deps_n(2, "w")
            d_stg = kb.deps_n(4, "stg")
            d_ss = kb.deps_n(3, "ss")
            kb.dma("sp", normw[:], self.norm_w[l:l + 1, :].partition_broadcast(128), writes=[d_nw])
            istg = 0
            iw = 0
            ievac = 0
            for h in range(2):
                def a1(tt, h=h):
                    g = h * 16 + tt
                    i = tt % 3
                    xi = tt % 3
                    kb.dma("sp", xt[xi][:], src[g * 128:(g + 1) * 128, :], writes=[d_xt[xi]])
                    kb.op("act", lambda e: e.activation(out=junk[:], in_=xt[xi][:], func=AF.Square, accum_out=ss[i][:]),
                          reads=[d_xt[xi]], writes=[d_junk, d_ss[i]])
                    kb.op("dve", lambda e: e.tensor_scalar(out=ss[i][:], in0=ss[i][:], scalar1=1.0 / D, scalar2=EPS,
                                                           op0=ALU.mult, op1=ALU.add), reads=[d_ss[i]], writes=[d_ss[i]])
                    kb.op("act", lambda e: e.activation(out=ss[i][:], in_=ss[i][:], func=AF.Sqrt), reads=[d_ss[i]], writes=[d_ss[i]])
                    kb.op("dve", lambda e: e.reciprocal(out=ss[i][:], in_=ss[i][:]), reads=[d_ss[i]], writes=[d_ss[i]])
                def a2(tt, h=h):
                    i = tt % 3
                    xi = tt % 3
                    kb.op("dve", lambda e: e.scalar_tensor_tensor(out=hb[xi][:], in0=xt[xi][:], scalar=ss[i][:], in1=normw[:],
                                                                  op0=ALU.mult, op1=ALU.mult),
                          reads=[d_xt[xi], d_ss[i], d_nw], writes=[d_hb[xi]])
                    for half in range(2):
                        tbk = 4 + 2 * (tt % 2) + half
                        pb = ps[tbk][:].bitcast(BF16)
                        for k in range(8):
                            kc = half * 8 + k
                            kb.op("pe", lambda e: e.transpose(out=pb[:, k * 128:(k + 1) * 128], in_=hb[xi][:, kc * 128:(kc + 1) * 128],
                                                              identity=self.identb[:]),
                                  reads=[d_hb[xi], self.d_const], writes=[dps[tbk]])
                def a3(tt, h=h):
                    for half in range(2):
                        tbk = 4 + 2 * (tt % 2) + half
                        pb = ps[tbk][:].bitcast(BF16)
                        eng = "act" if half == 0 else "dve"
                        dst = hdnT[:, half * 8:(half + 1) * 8, tt * 128:(tt + 1) * 128]
                        srcp = pb[:, 0:1024].rearrange("p (k n) -> p k n", k=8)
                        if eng == "act":
                            kb.op("act", lambda e: e.activation(out=dst, in_=srcp, func=AF.Copy), reads=[dps[tbk]], writes=[d_hT[tt]])
                        else:
                            kb.op("dve", lambda e: e.tensor_copy(out=dst, in_=srcp), reads=[dps[tbk]], writes=[d_hT[tt]])
                self.emit_pipelined(16, [a1, a2, a3])
                chunks = [(c0, min(512, TMW - c0)) for c0 in range(0, TMW, 512)]
                for (c0, cw) in chunks:
                    wi = iw % 2
                    iw += 1
                    kb.dma("pool", wch[wi][:, :, 0:cw], self.w_in[l, :, c0:c0 + cw].rearrange("(k p) n -> p k n", p=128),
                           writes=[d_w[wi]])
                    for tt in range(16):
                        g = h * 16 + tt
                        pbank = ievac % 4
                        for kc in range(16):
                            kb.op("pe", lambda e: e.matmul(ps[pbank][:, 0:cw], lhsT=hdnT[:, kc, tt * 128:(tt + 1) * 128],
                                                           rhs=wch[wi][:, kc, 0:cw], start=(kc == 0), stop=(kc == 15)),
                                  reads=[d_hT[tt], d_w[wi]], writes=[dps[pbank]])
                        si = istg % 4
                        istg += 1
                        if ievac % 2 == 0:
                            kb.op("act", lambda e: e.activation(out=stg[si][:, 0:cw], in_=ps[pbank][:, 0:cw], func=AF.Copy),
                                  reads=[dps[pbank]], writes=[d_stg[si]])
                        else:
                            kb.op("dve", lambda e: e.tensor_copy(out=stg[si][:, 0:cw], in_=ps[pbank][:, 0:cw]),
                                  reads=[dps[pbank]], writes=[d_stg[si]])
                        ievac += 1
                        kb.dma("sp", self.proj_tm[g * 128:(g + 1) * 128, c0:c0 + cw], stg[si][:, 0:cw], reads=[d_stg[si]])
                for fc in range(4):
                    c0 = TMW + fc * 512
                    wi = iw % 2
                    iw += 1
                    kb.dma("pool", wch[wi][:], self.w_in[l, :, c0:c0 + 512].rearrange("(k p) n -> p k n", p=128), writes=[d_w[wi]])
                    for ctl in range(4):
                        row0 = fc * 512 + ctl * 128
                        for tb in range(4):
                            pbank = ievac % 4
                            for kc in range(16):
                                kb.op("pe", lambda e: e.matmul(ps[pbank][:], lhsT=wch[wi][:, kc, ctl * 128:(ctl + 1) * 128],
                                                               rhs=hdnT[:, kc, tb * 512:(tb + 1) * 512], start=(kc == 0), stop=(kc == 15)),
                                      reads=d_hT[tb * 4:(tb + 1) * 4] + [d_w[wi]], writes=[dps[pbank]])
                            si = istg % 4
                            istg += 1
                            if ievac % 2 == 0:
                                kb.op("act", lambda e: e.activation(out=stg[si][:], in_=ps[pbank][:], func=AF.Copy),
                                      reads=[dps[pbank]], writes=[d_stg[si]])
                            else:
                                kb.op("dve", lambda e: e.tensor_copy(out=stg[si][:], in_=ps[pbank][:]),
                                      reads=[dps[pbank]], writes=[d_stg[si]])
                            ievac += 1
                            t0 = h * 2048 + tb * 512
                            kb.dma("sp", self.sT[row0:row0 + 128, t0:t0 + 512], stg[si][:], reads=[d_stg[si]])

    def phase_F(self, l, src, dst):
        nc, kb = self.nc, self.kb
        ps, dps = self.ps, self.dps
        with ExitStack() as st:
            sb = lambda n, s, d: st.enter_context(self.sbt("F_" + n, s, d))
            wo = sb("wo", [128, 16, D], BF16)
            mT = [sb("mT%d" % i, [128, 16, 512], BF16) for i in range(2)]
            xr = [sb("xr%d" % i, [128, D], F32) for i in range(2)]
            ot = [sb("ot%d" % i, [128, D], F32) for i in range(2)]
            d_wo = kb.deps_n(4, "wo")
            d_mT = kb.deps_n(2, "mT")
            d_xr = kb.deps_n(2, "xr")
            d_ot = kb.deps_n(2, "ot")
            for c in range(4):
                kb.dma("pool", wo[:, :, c * 512:(c + 1) * 512], self.w_out[l, :, c * 512:(c + 1) * 512].rearrange("(k p) n -> p k n", p=128),
                       writes=[d_wo[c]])
            ie = 0
            import os
            for tb in range(int(os.environ.get('F_TB', 8))):
                mi = tb % 2
                kb.dma("sp", mT[mi][:], self.mixedT[:, tb * 512:(tb + 1) * 512].rearrange("(k p) n -> p k n", p=128), writes=[d_mT[mi]])
                for t4 in range(4):
                    g = tb * 4 + t4
                    i = g % 2
                    kb.dma("sp", xr[i][:], src[g * 128:(g + 1) * 128, :], writes=[d_xr[i]])
                    for c in range(4):
                        pbank = ie % 4
                        ie += 1
                        for kc in range(16):
                            kb.op("pe", lambda e: e.matmul(ps[pbank][:], lhsT=mT[mi][:, kc, t4 * 128:(t4 + 1) * 128],
                                                           rhs=wo[:, kc, c * 512:(c + 1) * 512], start=(kc == 0), stop=(kc == 15)),
                                  reads=[d_mT[mi], d_wo[c]], writes=[dps[pbank]])
                        kb.op("dve", lambda e: e.tensor_tensor(out=ot[i][:, c * 512:(c + 1) * 512], in0=ps[pbank][:],
                                                               in1=xr[i][:, c * 512:(c + 1) * 512], op=ALU.add),
                              reads=[dps[pbank], d_xr[i]], writes=[d_ot[i]])
                    kb.dma("pool", dst[g * 128:(g + 1) * 128, :], ot[i][:], reads=[d_ot[i]])

    def sincos_turns(self, turns, cos_out, sin_out, tmpf, tmpi, tmpf2, dT, dC, dS, dtmp):
        kb = self.kb
        TWO_PI = 6.283185
        kb.op("dve", lambda e: e.tensor_copy(out=tmpi, in_=turns), reads=[dT], writes=[dtmp])
        kb.op("dve", lambda e: e.tensor_tensor(out=tmpf, in0=turns, in1=tmpi, op=ALU.subtract), reads=[dT, dtmp], writes=[dtmp])
        kb.op("act", lambda e: e.activation(out=sin_out, in_=tmpf, func=AF.Sin, scale=TWO_PI), reads=[dtmp], writes=[dS])
        kb.op("dve", lambda e: e.tensor_scalar(out=tmpf2, in0=tmpf, scalar1=0.25, scalar2=None, op0=ALU.add), reads=[dtmp], writes=[dtmp])
        kb.op("dve", lambda e: e.scalar_tensor_tensor(out=tmpf2, in0=tmpf2, scalar=0.5, in1=tmpf2, op0=ALU.is_gt, op1=ALU.subtract),
              reads=[dtmp], writes=[dtmp])
        kb.op("act", lambda e: e.activation(out=cos_out, in_=tmpf2, func=AF.Sin, scale=-TWO_PI), reads=[dtmp], writes=[dC])

    def phase_S5_v1(self, l):
        nc, kb = self.nc, self.kb
        ps, dps = self.ps, self.dps
        with ExitStack() as st:
            sb = lambda n, s, d: st.enter_context(self.sbt("S_" + n, s, d))
            BT = [sb("BT%d" % i, [128, 32, 128], BF16) for i in range(2)]
            CT = [sb("CT%d" % i, [128, 32, 128], BF16) for i in range(2)]
            prm = sb("prm", [128, 24, 32], F32)
            prmi = sb("prmi", [128, 32], I32)
            Dt = sb("Dt", [128, 8], F32)
            gluw = sb("gluw", [128, 8, 1024], BF16)
            d_BT, d_CT, d_prm, d_Dt, d_glu = kb.deps_n(5, "s5c")
            AR, AI, LDT, DTT, MM, PHI, COS, SIN, FR, FI, C512, S512, T0, T1, T2, T3, T4, T5 = range(18)
            P = lambda i: prm[:, i, :]
            kb.dma("pool", gluw[:], self.glu_w[l].rearrange("(k p) n -> p k n", p=128), writes=[d_glu])
            with ExitStack() as st2:
                sb2 = lambda n, s, d: st2.enter_context(self.sbt("S2_" + n, s, d))
                XA = sb2("XA", [32, 3, 128], F32)
                ld2 = sb2("ld2", [32, 2], F32)
                XD = sb2("XD", [8, 128], F32)
                pads = [sb2("pad%d" % i, [128, 32, 128], F32) for i in range(4)]
                d_XA, d_ld2, d_XD = kb.deps_n(3, "xa")
                d_pad = kb.deps_n(4, "pad")
                kb.dma("sp", XA[:, 0, :], self.a_re[l].rearrange("(q gl) p -> q (gl p)", gl=2), writes=[d_XA])
                kb.dma("sp", XA[:, 1, :], self.a_im[l].rearrange("(q gl) p -> q (gl p)", gl=2), writes=[d_XA])
                kb.dma("sp", ld2[:], self.log_dt[l:l + 1, :].rearrange("o (q gl) -> (o q) gl", gl=2), writes=[d_ld2])
                kb.dma("sp", XD[:], self.s5_d[l:l + 1, :].rearrange("o (c p) -> (o c) p", p=128), writes=[d_XD])
                kb.op("dve", lambda e: e.tensor_copy(out=XA[:, 2, :].rearrange("q (gl p) -> q gl p", gl=2),
                                                     in_=ld2[:].unsqueeze(2).to_broadcast([32, 2, 64])),
                      reads=[d_ld2, d_XA], writes=[d_XA])
                for i in range(4):
                    eng = "dve" if i % 2 == 0 else "pool"
                    kb.op(eng, lambda e: e.memset(pads[i][:].rearrange("p q c -> p (q c)"), 0.0), writes=[d_pad[i]])
                srcB = [self.b_re[l], self.b_im[l]]
                srcC = [self.c_re[l], self.c_im[l]]
                for k in range(4):
                    for gl in range(2):
                        for i in range(2):
                            dstb = pads[i][gl * 64:(gl + 1) * 64, :, :].rearrange("p (ct k) c -> p k ct c", k=4)[:, k, :, 32 * k + 16 * gl:32 * k + 16 * gl + 16]
                            sb_ = srcB[i].rearrange("(ct k gl) p c -> k gl p ct c", k=4, gl=2)[k, gl]
                            kb.dma("sp", dstb, sb_, reads=[d_pad[i]], writes=[d_pad[i]])
                            dstc = pads[2 + i][32 * k + 16 * gl:32 * k + 16 * gl + 16, :, :].rearrange("p (ct k) c -> p k ct c", k=4)[:, k, :, gl * 64:(gl + 1) * 64]
                            sc_ = srcC[i].rearrange("(ct k gl) c p -> k gl c ct p", k=4, gl=2)[k, gl]
                            kb.dma("sp", dstc, sc_, reads=[d_pad[2 + i]], writes=[d_pad[2 + i]])
                for j in range(3):
                    kb.op("pe", lambda e: e.transpose(out=ps[0][:, j * 32:(j + 1) * 32], in_=XA[:, j, :], identity=self.identf[0:32, 0:32]),
                          reads=[d_XA, self.d_const], writes=[dps[0]])
                kb.op("pe", lambda e: e.transpose(out=ps[0][:, 96:104], in_=XD[:], identity=self.identf[0:8, 0:8]),
                      reads=[d_XD, self.d_const], writes=[dps[0]])
                kb.op("dve", lambda e: e.tensor_copy(out=prm[:, 0:3, :].rearrange("p a q -> p (a q)"), in_=ps[0][:, 0:96]), reads=[dps[0]], writes=[d_prm])
                kb.op("dve", lambda e: e.tensor_copy(out=Dt[:], in_=ps[0][:, 96:104]), reads=[dps[0]], writes=[d_Dt])
                R_, W_ = [d_prm], [d_prm]
                tt = lambda o, a, b, op: kb.op("dve", lambda e: e.tensor_tensor(out=P(o), in0=P(a), in1=P(b), op=op), reads=R_, writes=W_)
                kb.op("act", lambda e: e.activation(out=P(DTT), in_=P(LDT), func=AF.Exp), reads=R_, writes=W_)
                tt(T0, DTT, AR, ALU.mult)
                kb.op("act", lambda e: e.activation(out=P(MM), in_=P(T0), func=AF.Exp), reads=R_, writes=W_)
                tt(T0, DTT, AI, ALU.mult)
                kb.op("dve", lambda e: e.tensor_scalar(out=P(T1), in0=P(T0), scalar1=1.0 / (2.0 * math.pi), scalar2=None, op0=ALU.mult), reads=R_, writes=W_)
                kb.op("dve", lambda e: e.tensor_copy(out=prmi[:], in_=P(T1)), reads=R_, writes=W_)
                kb.op("dve", lambda e: e.tensor_tensor(out=P(PHI), in0=P(T1), in1=prmi[:], op=ALU.subtract), reads=R_, writes=W_)
                self.sincos_turns(P(PHI), P(COS), P(SIN), P(T2), prmi[:], P(T3), d_prm, d_prm, d_prm, d_prm)
                kb.op("dve", lambda e: e.tensor_scalar(out=P(T4), in0=P(PHI), scalar1=512.0, scalar2=None, op0=ALU.mult), reads=R_, writes=W_)
                self.sincos_turns(P(T4), P(C512), P(S512), P(T2), prmi[:], P(T3), d_prm, d_prm, d_prm, d_prm)
                tt(T0, MM, COS, ALU.mult)
                tt(T1, MM, SIN, ALU.mult)
                kb.op("dve", lambda e: e.tensor_scalar(out=P(T0), in0=P(T0), scalar1=-1.0, scalar2=None, op0=ALU.add), reads=R_, writes=W_)
                tt(T2, AR, AR, ALU.mult)
                tt(T3, AI, AI, ALU.mult)
                tt(T2, T2, T3, ALU.add)
                kb.op("dve", lambda e: e.reciprocal(out=P(T2), in_=P(T2)), reads=R_, writes=W_)
                tt(T3, T0, AR, ALU.mult)
                tt(T4, T1, AI, ALU.mult)
                tt(T3, T3, T4, ALU.add)
                tt(FR, T3, T2, ALU.mult)
                tt(T3, T1, AR, ALU.mult)
                tt(T4, T0, AI, ALU.mult)
                tt(T3, T3, T4, ALU.subtract)
                tt(FI, T3, T2, ALU.mult)
                ctmp = [sb2("ctmp%d" % i, [128, 4, 128], F32) for i in range(4)]
                d_ctmp = kb.dep("ctmp")
                for q4 in range(8):
                    for i in range(4):
                        bank = 4 + i
                        for k in range(4):
                            q = q4 * 4 + k
                            kb.op("pe", lambda e: e.transpose(out=ps[bank][:, k * 128:(k + 1) * 128], in_=pads[i][:, q, :], identity=self.identf[:]),
                                  reads=[d_pad[i], self.d_const], writes=[dps[bank]])
                    for i in range(2):
                        kb.op("act", lambda e: e.activation(out=BT[i][:, q4 * 4:(q4 + 1) * 4, :].rearrange("p a b -> p (a b)"), in_=ps[4 + i][:], func=AF.Copy),
                              reads=[dps[4 + i]], writes=[d_BT])
                    frb = prm[:, FR, q4 * 4:(q4 + 1) * 4].unsqueeze(2).to_broadcast([128, 4, 128])
                    fib = prm[:, FI, q4 * 4:(q4 + 1) * 4].unsqueeze(2).to_broadcast([128, 4, 128])
                    crp = ps[6][:].rearrange("p (a b) -> p a b", a=4)
                    cip = ps[7][:].rearrange("p (a b) -> p a b", a=4)
                    tmpw = [d_ctmp]
                    kb.op("dve", lambda e: e.tensor_tensor(out=ctmp[0][:], in0=crp, in1=frb, op=ALU.mult), reads=[dps[6], d_prm], writes=tmpw)
                    kb.op("dve", lambda e: e.tensor_tensor(out=ctmp[1][:], in0=cip, in1=fib, op=ALU.mult), reads=[dps[7], d_prm], writes=tmpw)
                    kb.op("dve", lambda e: e.tensor_tensor(out=CT[0][:, q4 * 4:(q4 + 1) * 4, :], in0=ctmp[0][:], in1=ctmp[1][:], op=ALU.subtract),
                          reads=tmpw, writes=[d_CT])
                    kb.op("dve", lambda e: e.tensor_tensor(out=ctmp[2][:], in0=crp, in1=fib, op=ALU.mult), reads=[dps[6], d_prm], writes=tmpw)
                    kb.op("dve", lambda e: e.tensor_tensor(out=ctmp[3][:], in0=cip, in1=frb, op=ALU.mult), reads=[dps[7], d_prm], writes=tmpw)
                    kb.op("dve", lambda e: e.scalar_tensor_tensor(out=CT[1][:, q4 * 4:(q4 + 1) * 4, :], in0=ctmp[2][:], scalar=-1.0, in1=ctmp[3][:],
                                                                  op0=ALU.mult, op1=ALU.subtract), reads=tmpw, writes=[d_CT])
                kb.barrier()
            y5T = sb("y5T", [128, 8, S], BF16)
            cosT = [sb("cosT%d" % i, [128, 4, 512], BF16) for i in range(2)]
            sinT = [sb("sinT%d" % i, [128, 4, 512], BF16) for i in range(2)]
            iota = sb("iota", [128, 512], F32)
            angi = sb("angi", [128, 512], I32)
            uT = [sb("uT%d" % i, [128, 512], BF16) for i in range(3)]
            tf = [sb("tf%d" % i, [128, 512], F32) for i in range(3)]
            tb_ = [sb("tb%d" % i, [128, 512], BF16) for i in range(6)]
            rb_ = [sb("rb%d" % i, [128, 512], BF16) for i in range(4)]
            BuS = [[sb("BuS%d%d" % (i, j), [128, 512], BF16) for j in range(2)] for i in range(2)]
            zf = [[sb("zf%d%d" % (i, j), [128, 512], F32) for j in range(2)] for i in range(2)]
            zb = [[sb("zb%d%d" % (i, j), [128, 512], BF16) for j in range(2)] for i in range(2)]
            X = [[sb("X%d%d" % (i, j), [128, 512], BF16) for j in range(2)] for i in range(2)]
            cz = [sb("cz%d" % i, [128, 2, 32], F32) for i in range(2)]
            czt = sb("czt", [128, 2], F32)
            d_y5 = kb.deps_n(8, "y5")
            d_cos = kb.deps_n(2, "cos")
            d_sin = kb.deps_n(2, "sin")
            d_iota, d_angi, d_czt = kb.deps_n(3, "tab")
            d_uT = kb.deps_n(3, "uT")
            d_tf = kb.deps_n(3, "tf")
            d_tb = kb.deps_n(6, "tb")
            d_rb = kb.deps_n(4, "rb")
            d_BuS = [kb.deps_n(2) for i in range(2)]
            d_zf = [kb.deps_n(2) for i in range(2)]
            d_zb = [kb.deps_n(2) for i in range(2)]
            d_X = [kb.deps_n(2) for i in range(2)]
            d_cz = kb.deps_n(2, "cz")
            kb.dma("sp", iota[:], self.c_iota, writes=[d_iota])
            t1, t2, t3, t4, wr, wi = tb_
            dt1, dt2, dt3, dt4, dwr, dwi = d_tb
            r1, r2, r3, r4 = rb_
            dr1, dr2, dr3, dr4 = d_rb

            def TT(eng, o, do, a, da, b_, db, op):
                kb.op(eng, lambda e: e.tensor_tensor(out=o, in0=a, in1=b_, op=op), reads=da + db, writes=[do])

            pending = []
            it = 0
            icb = 0
            for ct in range(8):
                tbi = ct % 2
                for k in range(4):
                    q = ct * 4 + k
                    kb.op("dve", lambda e: e.tensor_scalar(out=tf[0][:], in0=iota[:], scalar1=prm[:, PHI, q:q + 1], scalar2=None, op0=ALU.mult),
                          reads=[d_iota, d_prm], writes=[d_tf[0]])
                    self.sincos_turns(tf[0][:], cosT[tbi][:, k, :], sinT[tbi][:, k, :], tf[1][:], angi[:], tf[2][:], d_tf[0], d_cos[tbi], d_sin[tbi], d_tf[1])
                kb.op("dve", lambda e: e.memset(cz[0][:].rearrange("p a q -> p (a q)"), 0.0), writes=[d_cz[0]])
                for tb in range(8):
                    ui = icb % 3
                    ybank = 4 + (icb % 2)
                    icb += 1
                    par = tb % 2
                    kb.dma("sp", uT[ui][:], self.sT[ct * 128:(ct + 1) * 128, tb * 512:(tb + 1) * 512], writes=[d_uT[ui]])
                    for k in range(4):
                        q = ct * 4 + k
                        sset = it % 2
                        it += 1
                        c = cosT[tbi][:, k, :]
                        s_ = sinT[tbi][:, k, :]
                        dc, ds = [d_cos[tbi]], [d_sin[tbi]]
                        for i in range(2):
                            kb.op("pe", lambda e: e.matmul(ps[2 * sset + i][:], lhsT=BT[i][:, q, :], rhs=uT[ui][:], start=True, stop=True),
                                  reads=[d_BT, d_uT[ui]], writes=[dps[2 * sset + i]])
                            kb.op("act", lambda e: e.activation(out=BuS[sset][i][:], in_=ps[2 * sset + i][:], func=AF.Copy),
                                  reads=[dps[2 * sset + i]], writes=[d_BuS[sset][i]])
                        Br, Bi = BuS[sset][0][:], BuS[sset][1][:]
                        dBr, dBi = [d_BuS[sset][0]], [d_BuS[sset][1]]
                        TT("dve", t1[:], dt1, Br, dBr, c, dc, ALU.mult)
                        TT("dve", t2[:], dt2, Bi, dBi, s_, ds, ALU.mult)
                        TT("dve", wr[:], dwr, t1[:], [dt1], t2[:], [dt2], ALU.add)
                        TT("dve", t3[:], dt3, Bi, dBi, c, dc, ALU.mult)
                        TT("dve", t4[:], dt4, Br, dBr, s_, ds, ALU.mult)
                        TT("dve", wi[:], dwi, t3[:], [dt3], t4[:], [dt4], ALU.subtract)
                        mb = prm[:, MM, q:q + 1].to_broadcast([128, 512])
                        zr, zi = zf[sset][0], zf[sset][1]
                        dzr, dzi = d_zf[sset][0], d_zf[sset][1]
                        kb.op("dve", lambda e: e.tensor_tensor_scan(out=zr[:], data0=mb, data1=wr[:], initial=cz[par][:, 0, q:q + 1], op0=ALU.mult, op1=ALU.add),
                              reads=[d_prm, dwr, d_cz[par]], writes=[dzr])
                        kb.op("dve", lambda e: e.tensor_tensor_scan(out=zi[:], data0=mb, data1=wi[:], initial=cz[par][:, 1, q:q + 1], op0=ALU.mult, op1=ALU.add),
                              reads=[d_prm, dwi, d_cz[par]], writes=[dzi])
                        for i in range(2):
                            kb.op("act", lambda e: e.activation(out=zb[sset][i][:], in_=zf[sset][i][:], func=AF.Copy),
                                  reads=[d_zf[sset][i]], writes=[d_zb[sset][i]])
                        zr_l, zi_l = zr[:, 511:512], zi[:, 511:512]
                        c5, s5 = prm[:, C512, q:q + 1], prm[:, S512, q:q + 1]
                        nx = 1 - par
                        kb.op("dve", lambda e: e.tensor_scalar(out=czt[:, 0:1], in0=zi_l, scalar1=s5, scalar2=None, op0=ALU.mult), reads=[dzi, d_prm], writes=[d_czt])
                        kb.op("dve", lambda e: e.scalar_tensor_tensor(out=cz[nx][:, 0, q:q + 1], in0=zr_l, scalar=c5, in1=czt[:, 0:1], op0=ALU.mult, op1=ALU.subtract),
                              reads=[dzr, d_prm, d_czt], writes=[d_cz[nx]])
                        kb.op("dve", lambda e: e.tensor_scalar(out=czt[:, 1:2], in0=zi_l, scalar1=c5, scalar2=None, op0=ALU.mult), reads=[dzi, d_prm], writes=[d_czt])
                        kb.op("dve", lambda e: e.scalar_tensor_tensor(out=cz[nx][:, 1, q:q + 1], in0=zr_l, scalar=s5, in1=czt[:, 1:2], op0=ALU.mult, op1=ALU.add),
                              reads=[dzr, d_prm, d_czt], writes=[d_cz[nx]])

                        def back(sset=sset, c=c, s_=s_, dc=dc, ds=ds, q=q, k=k, ct=ct, tb=tb, ui=ui, ybank=ybank):
                            zbr, zbi = zb[sset][0][:], zb[sset][1][:]
                            dzbr, dzbi = [d_zb[sset][0]], [d_zb[sset][1]]
                            TT("pool", r1[:], dr1, zbr, dzbr, c, dc, ALU.mult)
                            TT("pool", r2[:], dr2, zbi, dzbi, s_, ds, ALU.mult)
                            TT("pool", X[sset][0][:], d_X[sset][0], r1[:], [dr1], r2[:], [dr2], ALU.subtract)
                            TT("dve", r3[:], dr3, zbr, dzbr, s_, ds, ALU.mult)
                            TT("dve", r4[:], dr4, zbi, dzbi, c, dc, ALU.mult)
                            TT("dve", X[sset][1][:], d_X[sset][1], r3[:], [dr3], r4[:], [dr4], ALU.add)
                            for i in range(2):
                                kb.op("pe", lambda e: e.matmul(ps[ybank][:], lhsT=CT[i][:, q, :], rhs=X[sset][i][:], start=(k == 0 and i == 0), stop=(k == 3 and i == 1)),
                                      reads=[d_CT, d_X[sset][i]], writes=[dps[ybank]])
                            if k == 3:
                                kb.op("dve", lambda e: e.scalar_tensor_tensor(out=tf[0][:], in0=uT[ui][:], scalar=Dt[:, ct:ct + 1], in1=ps[ybank][:], op0=ALU.mult, op1=ALU.add),
                                      reads=[d_uT[ui], d_Dt, dps[ybank]], writes=[d_tf[0]])
                                kb.op("act", lambda e: e.activation(out=tf[1][:], in_=tf[0][:], func=AF.Square), reads=[d_tf[0]], writes=[d_tf[1]])
                                kb.op("pool", lambda e: e.tensor_scalar(out=tf[1][:], in0=tf[1][:], scalar1=0.044715, scalar2=1.0, op0=ALU.mult, op1=ALU.add),
                                      reads=[d_tf[1]], writes=[d_tf[1]])
                                kb.op("pool", lambda e: e.tensor_tensor(out=tf[1][:], in0=tf[1][:], in1=tf[0][:], op=ALU.mult), reads=[d_tf[1], d_tf[0]], writes=[d_tf[1]])
                                kb.op("act", lambda e: e.activation(out=tf[2][:], in_=tf[1][:], func=AF.Sigmoid, scale=1.5957691216057308), reads=[d_tf[1]], writes=[d_tf[2]])
                                kb.op("pool", lambda e: e.tensor_tensor(out=y5T[:, ct, tb * 512:(tb + 1) * 512], in0=tf[0][:], in1=tf[2][:], op=ALU.mult),
                                      reads=[d_tf[0], d_tf[2]], writes=[d_y5[tb]])

                        if pending:
                            pending.pop(0)()
                        pending.append(back)
            while pending:
                pending.pop(0)()
            szT = [sb("szT%d" % i, [128, 512], BF16) for i in range(2)]
            og = [sb("og%d" % i, [128, 512], BF16) for i in range(2)]
            d_sz = kb.deps_n(2, "sz")
            d_og = kb.deps_n(2, "og")
            ig = 0
            for tb in range(8):
                for co in range(8):
                    i = ig % 2
                    ig += 1
                    bank = 4 + (ig % 4)
                    kb.dma("sp", szT[i][:], self.sT[1024 + co * 128:1024 + (co + 1) * 128, tb * 512:(tb + 1) * 512], writes=[d_sz[i]])
                    for ci in range(8):
                        kb.op("pe", lambda e: e.matmul(ps[bank][:], lhsT=gluw[:, ci, co * 128:(co + 1) * 128], rhs=y5T[:, ci, tb * 512:(tb + 1) * 512],
                                                       start=(ci == 0), stop=(ci == 7)), reads=[d_glu, d_y5[tb]], writes=[dps[bank]])
                    kb.op("act", lambda e: e.activation(out=tf[0][:], in_=ps[bank][:], func=AF.Sigmoid), reads=[dps[bank]], writes=[d_tf[0]])
                    kb.op("act", lambda e: e.activation(out=tf[1][:], in_=szT[i][:], func=AF.Silu), reads=[d_sz[i]], writes=[d_tf[1]])
                    kb.op("dve", lambda e: e.tensor_tensor(out=tf[0][:], in0=tf[0][:], in1=y5T[:, co, tb * 512:(tb + 1) * 512], op=ALU.mult),
                          reads=[d_tf[0], d_y5[tb]], writes=[d_tf[0]])
                    kb.op("dve", lambda e: e.tensor_tensor(out=og[i][:], in0=tf[0][:], in1=tf[1][:], op=ALU.mult), reads=[d_tf[0], d_tf[1]], writes=[d_og[i]])
                    kb.dma("sp", self.mixedT[1024 + co * 128:1024 + (co + 1) * 128, tb * 512:(tb + 1) * 512], og[i][:], reads=[d_og[i]])

    def phase_S5(self, l):
        nc, kb = self.nc, self.kb
        ps, dps = self.ps, self.dps
        Lc = 4
        NCH = S // Lc
        NH = NCH // 512
        with ExitStack() as st:
            sb = lambda n, s, d: st.enter_context(self.sbt("S_" + n, s, d))
            CT = [sb("CT%d" % i, [128, 32, 128], BF16) for i in range(2)]
            Bp = [sb("Bp%d" % i, [128, 32, 128], BF16) for i in range(2)]
            prm = sb("prm", [128, 24, 32], F32)
            apw = sb("apw", [128, 9, 2, 32], F32)
            prmi = sb("prmi", [128, 32], I32)
            Dt = sb("Dt", [128, 8], F32)
            d_CT, d_prm, d_Dt, d_apw = kb.deps_n(4, "s5c")
            d_Bp = kb.deps_n(2, "Bp")
            AR, AI, LDT, DTT, MM, PHI, COS, SIN, FR, FI, M8, PHI8, C512, S512, T0, T1, T2, T3, T4, T5 = range(20)
            P = lambda i: prm[:, i, :]
            with ExitStack() as st2:
                sb2 = lambda n, s, d: st2.enter_context(self.sbt("S2_" + n, s, d))
                XA = sb2("XA", [32, 3, 128], F32)
                ld2 = sb2("ld2", [32, 2], F32)
                XD = sb2("XD", [8, 128], F32)
                Cp = [sb2("Cp%d" % i, [128, 32, 128], F32) for i in range(2)]
                Bf = [sb2("Bf%d" % i, [128, 32, 128], F32) for i in range(2)]
                pads = [Bf[0], Bf[1], Cp[0], Cp[1]]
                d_XA, d_ld2, d_XD = kb.deps_n(3, "xa")
                d_Cp = kb.deps_n(2, "Cp")
                d_Bf = kb.deps_n(2, "Bf")
                d_pad = [d_Bf[0], d_Bf[1], d_Cp[0], d_Cp[1]]
                kb.dma("sp", XA[:, 0, :], self.a_re[l].rearrange("(q gl) p -> q (gl p)", gl=2), writes=[d_XA])
                kb.dma("sp", XA[:, 1, :], self.a_im[l].rearrange("(q gl) p -> q (gl p)", gl=2), writes=[d_XA])
                kb.dma("sp", ld2[:], self.log_dt[l:l + 1, :].rearrange("o (q gl) -> (o q) gl", gl=2), writes=[d_ld2])
                kb.dma("sp", XD[:], self.s5_d[l:l + 1, :].rearrange("o (c p) -> (o c) p", p=128), writes=[d_XD])
                kb.op("dve", lambda e: e.tensor_copy(out=XA[:, 2, :].rearrange("q (gl p) -> q gl p", gl=2),
                                                     in_=ld2[:].unsqueeze(2).to_broadcast([32, 2, 64])),
                      reads=[d_ld2, d_XA], writes=[d_XA])
                for i in range(4):
                    eng = "dve" if i % 2 == 0 else "pool"
                    kb.op(eng, lambda e: e.memset(pads[i][:].rearrange("p q c -> p (q c)"), 0.0), writes=[d_pad[i]])
                srcB = [self.b_re[l], self.b_im[l]]
                srcC = [self.c_re[l], self.c_im[l]]
                for k in range(4):
                    for gl in range(2):
                        for i in range(2):
                            dstb = pads[i][gl * 64:(gl + 1) * 64, :, :].rearrange("p (ct k) c -> p k ct c", k=4)[:, k, :, 32 * k + 16 * gl:32 * k + 16 * gl + 16]
                            sb_ = srcB[i].rearrange("(ct k gl) p c -> k gl p ct c", k=4, gl=2)[k, gl]
                            kb.dma("sp", dstb, sb_, reads=[d_pad[i]], writes=[d_pad[i]])
                            dstc = pads[2 + i][32 * k + 16 * gl:32 * k + 16 * gl + 16, :, :].rearrange("p (ct k) c -> p k ct c", k=4)[:, k, :, gl * 64:(gl + 1) * 64]
                            sc_ = srcC[i].rearrange("(ct k gl) c p -> k gl c ct p", k=4, gl=2)[k, gl]
                            kb.dma("sp", dstc, sc_, reads=[d_pad[2 + i]], writes=[d_pad[2 + i]])
                for j in range(3):
                    kb.op("pe", lambda e: e.transpose(out=ps[0][:, j * 32:(j + 1) * 32], in_=XA[:, j, :], identity=self.identf[0:32, 0:32]),
                          reads=[d_XA, self.d_const], writes=[dps[0]])
                kb.op("pe", lambda e: e.transpose(out=ps[0][:, 96:104], in_=XD[:], identity=self.identf[0:8, 0:8]),
                      reads=[d_XD, self.d_const], writes=[dps[0]])
                kb.op("dve", lambda e: e.tensor_copy(out=prm[:, 0:3, :].rearrange("p a q -> p (a q)"), in_=ps[0][:, 0:96]), reads=[dps[0]], writes=[d_prm])
                kb.op("dve", lambda e: e.tensor_copy(out=Dt[:], in_=ps[0][:, 96:104]), reads=[dps[0]], writes=[d_Dt])
                kb.op("act", lambda e: e.activation(out=Bp[0][:].rearrange("p q c -> p (q c)"), in_=Bf[0][:].rearrange("p q c -> p (q c)"), func=AF.Copy),
                      reads=[d_Bf[0]], writes=[d_Bp[0]])
                kb.op("pool", lambda e: e.tensor_copy(out=Bp[1][:].rearrange("p q c -> p (q c)"), in_=Bf[1][:].rearrange("p q c -> p (q c)")),
                      reads=[d_Bf[1]], writes=[d_Bp[1]])
                R_, W_ = [d_prm], [d_prm]
                tt = lambda o, a, b, op: kb.op("dve", lambda e: e.tensor_tensor(out=P(o), in0=P(a), in1=P(b), op=op), reads=R_, writes=W_)
                kb.op("act", lambda e: e.activation(out=P(DTT), in_=P(LDT), func=AF.Exp), reads=R_, writes=W_)
                tt(T0, DTT, AR, ALU.mult)
                kb.op("act", lambda e: e.activation(out=P(MM), in_=P(T0), func=AF.Exp), reads=R_, writes=W_)
                kb.op("act", lambda e: e.activation(out=P(M8), in_=P(T0), func=AF.Exp, scale=float(Lc)), reads=R_, writes=W_)
                tt(T0, DTT, AI, ALU.mult)
                kb.op("dve", lambda e: e.tensor_scalar(out=P(T1), in0=P(T0), scalar1=1.0 / (2.0 * math.pi), scalar2=None, op0=ALU.mult), reads=R_, writes=W_)
                kb.op("dve", lambda e: e.tensor_copy(out=prmi[:], in_=P(T1)), reads=R_, writes=W_)
                kb.op("dve", lambda e: e.tensor_tensor(out=P(PHI), in0=P(T1), in1=prmi[:], op=ALU.subtract), reads=R_, writes=W_)
                self.sincos_turns(P(PHI), P(COS), P(SIN), P(T2), prmi[:], P(T3), d_prm, d_prm, d_prm, d_prm)
                kb.op("dve", lambda e: e.tensor_scalar(out=P(T4), in0=P(PHI), scalar1=float(Lc), scalar2=None, op0=ALU.mult), reads=R_, writes=W_)
                kb.op("dve", lambda e: e.tensor_copy(out=prmi[:], in_=P(T4)), reads=R_, writes=W_)
                kb.op("dve", lambda e: e.tensor_tensor(out=P(PHI8), in0=P(T4), in1=prmi[:], op=ALU.subtract), reads=R_, writes=W_)
                kb.op("dve", lambda e: e.tensor_scalar(out=P(T4), in0=P(PHI8), scalar1=512.0, scalar2=None, op0=ALU.mult), reads=R_, writes=W_)
                self.sincos_turns(P(T4), P(C512), P(S512), P(T2), prmi[:], P(T3), d_prm, d_prm, d_prm, d_prm)
                tt(T0, MM, COS, ALU.mult)
                tt(T1, MM, SIN, ALU.mult)
                RW = [d_prm, d_apw]
                kb.op("dve", lambda e: e.memset(apw[:, 0, 0, :], 1.0), reads=RW, writes=[d_apw])
                kb.op("dve", lambda e: e.memset(apw[:, 0, 1, :], 0.0), reads=RW, writes=[d_apw])
                kb.op("dve", lambda e: e.tensor_copy(out=apw[:, 1, 0, :], in_=P(T0)), reads=RW, writes=[d_apw])
                kb.op("dve", lambda e: e.tensor_copy(out=apw[:, 1, 1, :], in_=P(T1)), reads=RW, writes=[d_apw])
                for m in range(1, Lc):
                    ar_, ai_ = apw[:, m, 0, :], apw[:, m, 1, :]
                    kb.op("dve", lambda e: e.tensor_tensor(out=P(T2), in0=ar_, in1=P(T0), op=ALU.mult), reads=RW, writes=W_)
                    kb.op("dve", lambda e: e.tensor_tensor(out=P(T3), in0=ai_, in1=P(T1), op=ALU.mult), reads=RW, writes=W_)
                    kb.op("dve", lambda e: e.tensor_tensor(out=apw[:, m + 1, 0, :], in0=P(T2), in1=P(T3), op=ALU.subtract), reads=RW, writes=[d_apw])
                    kb.op("dve", lambda e: e.tensor_tensor(out=P(T2), in0=ar_, in1=P(T1), op=ALU.mult), reads=RW, writes=W_)
                    kb.op("dve", lambda e: e.tensor_tensor(out=P(T3), in0=ai_, in1=P(T0), op=ALU.mult), reads=RW, writes=W_)
                    kb.op("dve", lambda e: e.tensor_tensor(out=apw[:, m + 1, 1, :], in0=P(T2), in1=P(T3), op=ALU.add), reads=RW, writes=[d_apw])
                kb.op("dve", lambda e: e.tensor_scalar(out=P(T0), in0=P(T0), scalar1=-1.0, scalar2=None, op0=ALU.add), reads=R_, writes=W_)
                tt(T2, AR, AR, ALU.mult)
                tt(T3, AI, AI, ALU.mult)
                tt(T2, T2, T3, ALU.add)
                kb.op("dve", lambda e: e.reciprocal(out=P(T2), in_=P(T2)), reads=R_, writes=W_)
                tt(T3, T0, AR, ALU.mult)
                tt(T4, T1, AI, ALU.mult)
                tt(T3, T3, T4, ALU.add)
                tt(FR, T3, T2, ALU.mult)
                tt(T3, T1, AR, ALU.mult)
                tt(T4, T0, AI, ALU.mult)
                tt(T3, T3, T4, ALU.subtract)
                tt(FI, T3, T2, ALU.mult)
                ctmp = [sb2("ctmp%d" % i, [128, 4, 128], F32) for i in range(4)]
                d_ctmp = kb.dep("ctmp")
                for q4 in range(8):
                    for i in range(2):
                        bank = 6 + i
                        for k in range(4):
                            q = q4 * 4 + k
                            kb.op("pe", lambda e: e.transpose(out=ps[bank][:, k * 128:(k + 1) * 128], in_=Cp[i][:, q, :], identity=self.identf[:]),
                                  reads=[d_Cp[i], self.d_const], writes=[dps[bank]])
                    frb = prm[:, FR, q4 * 4:(q4 + 1) * 4].unsqueeze(2).to_broadcast([128, 4, 128])
                    fib = prm[:, FI, q4 * 4:(q4 + 1) * 4].unsqueeze(2).to_broadcast([128, 4, 128])
                    crp = ps[6][:].rearrange("p (a b) -> p a b", a=4)
                    cip = ps[7][:].rearrange("p (a b) -> p a b", a=4)
                    tmpw = [d_ctmp]
                    kb.op("dve", lambda e: e.tensor_tensor(out=ctmp[0][:], in0=crp, in1=frb, op=ALU.mult), reads=[dps[6], d_prm], writes=tmpw)
                    kb.op("dve", lambda e: e.tensor_tensor(out=ctmp[1][:], in0=cip, in1=fib, op=ALU.mult), reads=[dps[7], d_prm], writes=tmpw)
                    kb.op("dve", lambda e: e.tensor_tensor(out=CT[0][:, q4 * 4:(q4 + 1) * 4, :], in0=ctmp[0][:], in1=ctmp[1][:], op=ALU.subtract),
                          reads=tmpw, writes=[d_CT])
                    kb.op("dve", lambda e: e.tensor_tensor(out=ctmp[2][:], in0=crp, in1=fib, op=ALU.mult), reads=[dps[6], d_prm], writes=tmpw)
                    kb.op("dve", lambda e: e.tensor_tensor(out=ctmp[3][:], in0=cip, in1=frb, op=ALU.mult), reads=[dps[7], d_prm], writes=tmpw)
                    kb.op("dve", lambda e: e.scalar_tensor_tensor(out=CT[1][:, q4 * 4:(q4 + 1) * 4, :], in0=ctmp[2][:], scalar=-1.0, in1=ctmp[3][:],
                                                                  op0=ALU.mult, op1=ALU.subtract), reads=tmpw, writes=[d_CT])
                kb.barrier()
            with ExitStack() as st3:
                sb3 = lambda n, s, d: st3.enter_context(self.sbt("S3_" + n, s, d))
                cosT = [sb3("cosT%d" % i, [128, 4, 512], BF16) for i in range(2)]
                sinT = [sb3("sinT%d" % i, [128, 4, 512], BF16) for i in range(2)]
                iota = sb3("iota", [128, 512], F32)
                angi = sb3("angi", [128, 512], I32)
                zero = sb3("zero", [128, 2], F32)
                czc = [sb3("czc%d" % i, [128, 2, 4], F32) for i in range(2)]
                czt = sb3("czt", [128, 2], F32)
                zl = sb3("zlast", [128, 2], F32)
                d_czc = kb.deps_n(2, "czc")
                d_czt = kb.dep("czt")
                d_zl = kb.dep("zl")
                uTf = [sb3("uTf%d" % i, [128, S], BF16) for i in range(2)]
                W1 = sb3("W1", [128, Lc, 8, 128], BF16)
                CA = [sb3("CA%d" % i, [128, Lc, 8, 128], BF16) for i in range(2)]
                SN = [sb3("SN%d" % i, [128, 8, 128], BF16) for i in range(2)]
                KT = [sb3("KT%d" % i, [128, Lc, 128], BF16) for i in range(2)]
                uu = [sb3("uu%d" % i, [128, 4, 128], F32) for i in range(4)]
                Xs = [[[sb3("Xs%d%d%d" % (c_, k, i), [128, NCH + 2], BF16) for i in range(2)] for k in range(4)] for c_ in range(2)]
                tf = [sb3("tf%d" % i, [128, 512], F32) for i in range(3)]
                tg = [sb3("tg%d" % i, [128, 512], F32) for i in range(3)]
                tb_ = [sb3("tb%d" % i, [128, 512], BF16) for i in range(6)]
                rb_ = [sb3("rb%d" % i, [128, 512], BF16) for i in range(4)]
                BuS = [[sb3("BuS%d%d" % (i, j), [128, 512], BF16) for j in range(2)] for i in range(2)]
                zb = [[sb3("zb%d%d" % (i, j), [128, 512], BF16) for j in range(2)] for i in range(2)]
                y5s = [sb3("y5s%d" % i, [128, 512], BF16) for i in range(2)]
                d_cos = kb.deps_n(2, "cos")
                d_sin = kb.deps_n(2, "sin")
                d_iota, d_angi, d_zero, d_W1 = kb.deps_n(4, "tab")
                d_CA = kb.deps_n(2, "CA")
                d_KT = kb.deps_n(2, "KT")
                d_uTf = kb.deps_n(2, "uTf")
                d_SN = kb.deps_n(2, "SN")
                d_SNr = kb.deps_n(2, "SNr")
                d_CAr = kb.deps_n(2, "CAr")
                d_uu = kb.deps_n(4, "uu")
                d_Xs = [[kb.deps_n(2) for k in range(4)] for c_ in range(2)]
                d_tf = kb.deps_n(3, "tf")
                d_tg = kb.deps_n(3, "tg")
                d_tb = kb.deps_n(6, "tb")
                d_rb = kb.deps_n(4, "rb")
                d_BuS = [kb.deps_n(2) for i in range(2)]
                d_zb = [kb.deps_n(2) for i in range(2)]
                d_y5s = kb.deps_n(2, "y5s")
                kb.dma("sp", iota[:], self.c_iota, writes=[d_iota])
                kb.op("dve", lambda e: e.memset(zero[:], 0.0), writes=[d_zero])
                for c_ in range(2):
                    for k in range(4):
                        for i in range(2):
                            kb.op("pool", lambda e: e.memset(Xs[c_][k][i][:], 0.0), writes=[d_Xs[c_][k][i]])
                t1, t2, t3, t4, wr, wi = tb_
                dt1, dt2, dt3, dt4, dwr, dwi = d_tb
                r1, r2, r3, r4 = rb_
                dr1, dr2, dr3, dr4 = d_rb

                def TT(eng, o, do, a, da, b_, db, op):
                    kb.op(eng, lambda e: e.tensor_tensor(out=o, in0=a, in1=b_, op=op), reads=da + db, writes=[do])

                def E_slice(ct, step):
                    cp = ct % 2
                    q0 = ct * 4
                    if step == 0:
                        kb.dma("sp", uTf[cp][:], self.sT[ct * 128:(ct + 1) * 128, :], writes=[d_uTf[cp]])
                    if step < 4:
                        k = step
                        q = q0 + k
                        kb.op("dve", lambda e: e.tensor_scalar(out=tg[0][:], in0=iota[:], scalar1=prm[:, PHI8, q:q + 1], scalar2=None, op0=ALU.mult),
                              reads=[d_iota, d_prm], writes=[d_tg[0]])
                        self.sincos_turns(tg[0][:], cosT[cp][:, k, :], sinT[cp][:, k, :], tg[1][:], angi[:], tg[2][:], d_tg[0], d_cos[cp], d_sin[cp], d_tg[1])
                    m = step
                    mi = m % 2
                    Brv = Bp[0][:, q0:q0 + 4, :]
                    Biv = Bp[1][:, q0:q0 + 4, :]
                    Arb = apw[:, m, 0, q0:q0 + 4].unsqueeze(2).to_broadcast([128, 4, 128])
                    Aib = apw[:, m, 1, q0:q0 + 4].unsqueeze(2).to_broadcast([128, 4, 128])
                    TT("dve", uu[0][:], d_uu[0], Brv, [d_Bp[0]], Arb, [d_apw], ALU.mult)
                    TT("dve", uu[1][:], d_uu[1], Biv, [d_Bp[1]], Aib, [d_apw], ALU.mult)
                    TT("dve", SN[mi][:, 0:4, :], d_SNr[mi], uu[0][:], [d_uu[0]], uu[1][:], [d_uu[1]], ALU.subtract)
                    TT("pool", uu[2][:], d_uu[2], Biv, [d_Bp[1]], Arb, [d_apw], ALU.mult)
                    TT("pool", uu[3][:], d_uu[3], Brv, [d_Bp[0]], Aib, [d_apw], ALU.mult)
                    TT("pool", SN[mi][:, 4:8, :], d_SN[mi], uu[2][:], [d_uu[2]], uu[3][:], [d_uu[3]], ALU.add)
                    j = step
                    C0v = CT[0][:, q0:q0 + 4, :]
                    C1v = CT[1][:, q0:q0 + 4, :]
                    Arb = apw[:, j + 1, 0, q0:q0 + 4].unsqueeze(2).to_broadcast([128, 4, 128])
                    Aib = apw[:, j + 1, 1, q0:q0 + 4].unsqueeze(2).to_broadcast([128, 4, 128])
                    TT("dve", uu[0][:], d_uu[0], C0v, [d_CT], Arb, [d_apw], ALU.mult)
                    TT("dve", uu[1][:], d_uu[1], C1v, [d_CT], Aib, [d_apw], ALU.mult)
                    TT("dve", CA[cp][:, j, 0:4, :], d_CAr[cp], uu[0][:], [d_uu[0]], uu[1][:], [d_uu[1]], ALU.add)
                    TT("pool", uu[2][:], d_uu[2], C1v, [d_CT], Arb, [d_apw], ALU.mult)
                    TT("pool", uu[3][:], d_uu[3], C0v, [d_CT], Aib, [d_apw], ALU.mult)
                    TT("pool", CA[cp][:, j, 4:8, :], d_CA[cp], uu[2][:], [d_uu[2]], uu[3][:], [d_uu[3]], ALU.subtract)

                def T_slice(ct, step):
                    cp = ct % 2
                    q0 = ct * 4
                    m = step
                    mi = m % 2
                    pb = ps[6][:].bitcast(BF16)
                    for j8 in range(8):
                        kb.op("pe", lambda e: e.transpose(out=pb[:, j8 * 128:(j8 + 1) * 128], in_=SN[mi][:, j8, :], identity=self.identb[:]),
                              reads=[d_SN[mi], d_SNr[mi], self.d_const], writes=[dps[6]])
                    kb.op("act", lambda e: e.activation(out=W1[:, m, :, :].rearrange("p a b -> p (a b)"), in_=pb[:, 0:1024], func=AF.Copy),
                          reads=[dps[6]], writes=[d_W1])
                    ksl = ps[7][:, 0:128]
                    for j8 in range(8):
                        i_, k_ = j8 // 4, j8 % 4
                        kb.op("pe", lambda e: e.matmul(ksl, lhsT=SN[mi][:, j8, :], rhs=CT[i_][:, q0 + k_, :], start=(j8 == 0), stop=(j8 == 7)),
                              reads=[d_SN[mi], d_SNr[mi], d_CT], writes=[dps[7]])
                    kb.op("act", lambda e: e.activation(out=KT[cp][:, m, :], in_=ksl, func=AF.Copy), reads=[dps[7]], writes=[d_KT[cp]])

                it_box = [0]

                def H_stage(ct):
                    cp = ct % 2
                    q0 = ct * 4
                    pending = []
                    for hh in range(NH):
                        for k in range(4):
                            q = q0 + k
                            sset = it_box[0] % 2
                            it_box[0] += 1
                            c = cosT[cp][:, k, :]
                            s_ = sinT[cp][:, k, :]
                            dc, ds = [d_cos[cp]], [d_sin[cp]]
                            for i in range(2):
                                bnk = 2 * sset + i
                                for j in range(Lc):
                                    kb.op("pe", lambda e: e.matmul(ps[bnk][:], lhsT=W1[:, Lc - 1 - j, i * 4 + k, :],
                                                                   rhs=uTf[cp][:, hh * 512 * Lc + j:(hh + 1) * 512 * Lc:Lc], start=(j == 0), stop=(j == Lc - 1)),
                                          reads=[d_W1, d_uTf[cp]], writes=[dps[bnk]])
                                kb.op("act", lambda e: e.activation(out=BuS[sset][i][:], in_=ps[bnk][:], func=AF.Copy), reads=[dps[bnk]], writes=[d_BuS[sset][i]])
                            Br, Bi = BuS[sset][0][:], BuS[sset][1][:]
                            dBr, dBi = [d_BuS[sset][0]], [d_BuS[sset][1]]
                            TT("dve", t1[:], dt1, Br, dBr, c, dc, ALU.mult)
                            TT("dve", t2[:], dt2, Bi, dBi, s_, ds, ALU.mult)
                            TT("dve", wr[:], dwr, t1[:], [dt1], t2[:], [dt2], ALU.add)
                            TT("pool", t3[:], dt3, Bi, dBi, c, dc, ALU.mult)
                            TT("pool", t4[:], dt4, Br, dBr, s_, ds, ALU.mult)
                            TT("pool", wi[:], dwi, t3[:], [dt3], t4[:], [dt4], ALU.subtract)
                            mb = prm[:, M8, q:q + 1].to_broadcast([128, 512])
                            par = hh % 2
                            for i, w_, dw_ in ((0, wr, dwr), (1, wi, dwi)):
                                init = zero[:, i:i + 1] if hh == 0 else czc[par][:, i, k:k + 1]
                                rd = [d_prm, dw_, d_zero] if hh == 0 else [d_prm, dw_, d_czc[par]]
                                kb.op("dve", lambda e: e.tensor_tensor_scan(out=zb[sset][i][:], data0=mb, data1=w_[:], initial=init, op0=ALU.mult, op1=ALU.add),
                                      reads=rd, writes=[d_zb[sset][i]])
                            if hh + 1 < NH:
                                nx = 1 - par
                                kb.op("dve", lambda e: e.tensor_copy(out=zl[:, 0:1], in_=zb[sset][0][:, 511:512]), reads=[d_zb[sset][0]], writes=[d_zl])
                                kb.op("dve", lambda e: e.tensor_copy(out=zl[:, 1:2], in_=zb[sset][1][:, 511:512]), reads=[d_zb[sset][1]], writes=[d_zl])
                                c5, s5 = prm[:, C512, q:q + 1], prm[:, S512, q:q + 1]
                                kb.op("dve", lambda e: e.tensor_scalar(out=czt[:, 0:1], in0=zl[:, 1:2], scalar1=s5, scalar2=None, op0=ALU.mult), reads=[d_zl, d_prm], writes=[d_czt])
                                kb.op("dve", lambda e: e.scalar_tensor_tensor(out=czc[nx][:, 0, k:k + 1], in0=zl[:, 0:1], scalar=c5, in1=czt[:, 0:1], op0=ALU.mult, op1=ALU.subtract),
                                      reads=[d_zl, d_prm, d_czt], writes=[d_czc[nx]])
                                kb.op("dve", lambda e: e.tensor_scalar(out=czt[:, 1:2], in0=zl[:, 1:2], scalar1=c5, scalar2=None, op0=ALU.mult), reads=[d_zl, d_prm], writes=[d_czt])
                                kb.op("dve", lambda e: e.scalar_tensor_tensor(out=czc[nx][:, 1, k:k + 1], in0=zl[:, 0:1], scalar=s5, in1=czt[:, 1:2], op0=ALU.mult, op1=ALU.add),
                                      reads=[d_zl, d_prm, d_czt], writes=[d_czc[nx]])

                            def back(sset=sset, c=c, s_=s_, dc=dc, ds=ds, k=k, hh=hh):
                                zbr, zbi = zb[sset][0][:], zb[sset][1][:]
                                dzbr, dzbi = [d_zb[sset][0]], [d_zb[sset][1]]
                                o0 = 1 + hh * 512
                                TT("pool", r1[:], dr1, zbr, dzbr, c, dc, ALU.mult)
                                TT("pool", r2[:], dr2, zbi, dzbi, s_, ds, ALU.mult)
                                TT("pool", Xs[cp][k][0][:, o0:o0 + 512], d_Xs[cp][k][0], r1[:], [dr1], r2[:], [dr2], ALU.subtract)
                                TT("dve", r3[:], dr3, zbr, dzbr, s_, ds, ALU.mult)
                                TT("dve", r4[:], dr4, zbi, dzbi, c, dc, ALU.mult)
                                TT("dve", Xs[cp][k][1][:, o0:o0 + 512], d_Xs[cp][k][1], r3[:], [dr3], r4[:], [dr4], ALU.add)

                            if pending:
                                pending.pop(0)()
                            pending.append(back)
                    while pending:
                        pending.pop(0)()

                iy_box = [0]

                def Y_mm(ct, blk):
                    cp = ct % 2
                    yb = 4 + (iy_box[0] % 2)
                    for j in range(Lc):
                        osl = ps[yb][:, j:512:Lc]
                        nck = 512 // Lc
                        for tau in range(j + 1):
                            kb.op("pe", lambda e: e.matmul(osl, lhsT=KT[cp][:, tau, :], rhs=uTf[cp][:, blk * 512 + j - tau:blk * 512 + 512:Lc], start=(tau == 0), stop=False),
                                  reads=[d_KT[cp], d_uTf[cp]], writes=[dps[yb]])
                        for k in range(4):
                            for i in range(2):
                                kb.op("pe", lambda e: e.matmul(osl, lhsT=CA[cp][:, j, i * 4 + k, :], rhs=Xs[cp][k][i][:, blk * nck:(blk + 1) * nck], start=False, stop=(k == 3 and i == 1)),
                                      reads=[d_CA[cp], d_CAr[cp], d_Xs[cp][k][i]], writes=[dps[yb]])

                def Y_epi(ct, blk):
                    cp = ct % 2
                    yb = 4 + (iy_box[0] % 2)
                    yi = iy_box[0] % 2
                    iy_box[0] += 1
                    kb.op("dve", lambda e: e.scalar_tensor_tensor(out=tf[0][:], in0=uTf[cp][:, blk * 512:(blk + 1) * 512], scalar=Dt[:, ct:ct + 1], in1=ps[yb][:],
                                                                  op0=ALU.mult, op1=ALU.add), reads=[d_uTf[cp], d_Dt, dps[yb]], writes=[d_tf[0]])
                    kb.op("act", lambda e: e.activation(out=tf[1][:], in_=tf[0][:], func=AF.Square), reads=[d_tf[0]], writes=[d_tf[1]])
                    kb.op("pool", lambda e: e.tensor_scalar(out=tf[1][:], in0=tf[1][:], scalar1=0.044715, scalar2=1.0, op0=ALU.mult, op1=ALU.add),
                          reads=[d_tf[1]], writes=[d_tf[1]])
                    kb.op("dve", lambda e: e.tensor_tensor(out=tf[1][:], in0=tf[1][:], in1=tf[0][:], op=ALU.mult), reads=[d_tf[1], d_tf[0]], writes=[d_tf[1]])
                    kb.op("act", lambda e: e.activation(out=tf[2][:], in_=tf[1][:], func=AF.Sigmoid, scale=1.5957691216057308), reads=[d_tf[1]], writes=[d_tf[2]])
                    kb.op("dve", lambda e: e.tensor_tensor(out=y5s[yi][:], in0=tf[0][:], in1=tf[2][:], op=ALU.mult), reads=[d_tf[0], d_tf[2]], writes=[d_y5s[yi]])
                    kb.dma("sp", self.y5d[ct * 128:(ct + 1) * 128, blk * 512:(blk + 1) * 512], y5s[yi][:], reads=[d_y5s[yi]])

                for step in range(Lc):
                    E_slice(0, step)
                    T_slice(0, step)
                H_stage(0)
                for ct in range(8):
                    nxt = ct + 1 < 8
                    if nxt:
                        E_slice(ct + 1, 0)
                    for blk in range(8):
                        if nxt and blk < Lc:
                            T_slice(ct + 1, blk)
                        Y_mm(ct, blk)
                        if nxt and blk + 1 < Lc:
                            E_slice(ct + 1, blk + 1)
                        Y_epi(ct, blk)
                    if nxt:
                        H_stage(ct + 1)
                kb.barrier()
            with ExitStack() as st4:
                sb4 = lambda n, s, d: st4.enter_context(self.sbt("S4_" + n, s, d))
                gluw = sb4("gluw", [128, 8, 1024], BF16)
                y5b = [sb4("y5b%d" % i, [128, 8, 512], BF16) for i in range(2)]
                NB = 3
                szT = [sb4("szT%d" % i, [128, 512], BF16) for i in range(NB)]
                og = [sb4("og%d" % i, [128, 512], BF16) for i in range(NB)]
                g1 = [sb4("g1%d" % i, [128, 512], BF16) for i in range(NB)]
                g2 = [sb4("g2%d" % i, [128, 512], BF16) for i in range(NB)]
                d_glu = kb.dep("glu")
                d_y5b = kb.deps_n(2, "y5b")
                d_sz = kb.deps_n(NB, "sz")
                d_og = kb.deps_n(NB, "og")
                d_g1 = kb.deps_n(NB, "g1")
                d_g2 = kb.deps_n(NB, "g2")
                kb.dma("pool", gluw[:], self.glu_w[l].rearrange("(k p) n -> p k n", p=128), writes=[d_glu])
                ig = 0
                for tb in range(8):
                    yi = tb % 2
                    kb.dma("sp", y5b[yi][:], self.y5d[:, tb * 512:(tb + 1) * 512].rearrange("(c p) t -> p c t", p=128), writes=[d_y5b[yi]])
                    for co in range(8):
                        i = ig % NB
                        bank = ig % 4
                        ig += 1
                        kb.dma("sp", szT[i][:], self.sT[1024 + co * 128:1024 + (co + 1) * 128, tb * 512:(tb + 1) * 512], writes=[d_sz[i]])
                        for ci in range(8):
                            kb.op("pe", lambda e: e.matmul(ps[bank][:], lhsT=gluw[:, ci, co * 128:(co + 1) * 128], rhs=y5b[yi][:, ci, :],
                                                           start=(ci == 0), stop=(ci == 7)), reads=[d_glu, d_y5b[yi]], writes=[dps[bank]])
                        kb.op("act", lambda e: e.activation(out=g1[i][:], in_=ps[bank][:], func=AF.Sigmoid), reads=[dps[bank]], writes=[d_g1[i]])
                        kb.op("act", lambda e: e.activation(out=g2[i][:], in_=szT[i][:], func=AF.Sigmoid), reads=[d_sz[i]], writes=[d_g2[i]])
                        kb.op("dve", lambda e: e.tensor_tensor(out=g2[i][:], in0=g2[i][:], in1=szT[i][:], op=ALU.mult), reads=[d_g2[i], d_sz[i]], writes=[d_g2[i]])
                        kb.op("dve", lambda e: e.tensor_tensor(out=g1[i][:], in0=g1[i][:], in1=y5b[yi][:, co, :], op=ALU.mult),
                              reads=[d_g1[i], d_y5b[yi]], writes=[d_g1[i]])
                        kb.op("dve", lambda e: e.tensor_tensor(out=og[i][:], in0=g1[i][:], in1=g2[i][:], op=ALU.mult), reads=[d_g1[i], d_g2[i]], writes=[d_og[i]])
                        kb.dma("pool", self.mixedT[1024 + co * 128:1024 + (co + 1) * 128, tb * 512:(tb + 1) * 512], og[i][:], reads=[d_og[i]])

    def qk_prep(self, tag, src, col0, nh, normw_dram, l, dstT, d_dst, rope_tab, d_rope, ntiles=NT, bank=4):
        nc, kb = self.nc, self.kb
        ps, dps = self.ps, self.dps
        G = 4 if ntiles % 4 == 0 else 2
        NBUF = 4
        with ExitStack() as st:
            sb = lambda n, s, d: st.enter_context(self.sbt("P_%s_%s" % (tag, n), s, d))
            W = nh * 128
            GH = G * nh
            nw = sb("nw", [128, 128], F32)
            qraw = [sb("qraw%d" % i, [128, G, W], BF16) for i in range(NBUF)]
            sq = [sb("sq%d" % i, [128, GH, 128], BF16) for i in range(NBUF)]
            ss = [sb("ss%d" % i, [128, GH], F32) for i in range(NBUF)]
            qn = [sb("qn%d" % i, [128, GH, 128], F32) for i in range(NBUF)]
            rt = [sb("rt%d" % i, [128, 4, GH, 16], F32) for i in range(NBUF)]
            qb = [sb("qb%d" % i, [128, GH, 128], BF16) for i in range(NBUF)]
            d_nw = kb.dep()
            d_qraw = kb.deps_n(NBUF)
            d_sq = kb.deps_n(NBUF)
            d_ss = kb.deps_n(NBUF)
            d_qn = kb.deps_n(NBUF)
            d_rt = kb.deps_n(NBUF)
            d_qb = kb.deps_n(NBUF)
            kb.dma("sp", nw[:], normw_dram[l:l + 1, :].partition_broadcast(128), writes=[d_nw])
            def f1(g):
                i = g % NBUF
                t0 = g * G
                kb.dma("sp", qraw[i][:], src[t0 * 128:(t0 + G) * 128, col0:col0 + W].rearrange("(g p) c -> p g c", p=128), writes=[d_qraw[i]])
                qv = qraw[i][:].rearrange("p g (h c) -> p (g h) c", c=128)
                kb.op("pool", lambda e: e.tensor_tensor(out=sq[i][:], in0=qv, in1=qv, op=ALU.mult), reads=[d_qraw[i]], writes=[d_sq[i]])
                kb.op("dve", lambda e: e.tensor_reduce(out=ss[i][:], in_=sq[i][:], axis=AX.X, op=ALU.add), reads=[d_sq[i]], writes=[d_ss[i]])
                kb.op("dve", lambda e: e.tensor_scalar(out=ss[i][:], in0=ss[i][:], scalar1=1.0 / 128, scalar2=EPS, op0=ALU.mult, op1=ALU.add),
                      reads=[d_ss[i]], writes=[d_ss[i]])
                kb.op("act", lambda e: e.activation(out=ss[i][:], in_=ss[i][:], func=AF.Sqrt), reads=[d_ss[i]], writes=[d_ss[i]])
                kb.op("dve", lambda e: e.reciprocal(out=ss[i][:], in_=ss[i][:]), reads=[d_ss[i]], writes=[d_ss[i]])
            def f2(g):
                i = g % NBUF
                t0 = g * G
                qv = qraw[i][:].rearrange("p g (h c) -> p (g h) c", c=128)
                kb.op("dve", lambda e: e.tensor_tensor(out=qn[i][:], in0=qv, in1=ss[i][:].unsqueeze(2).to_broadcast([128, GH, 128]), op=ALU.mult),
                      reads=[d_qraw[i], d_ss[i]], writes=[d_qn[i]])
                kb.op("pool", lambda e: e.tensor_tensor(out=qn[i][:], in0=qn[i][:], in1=nw[:].unsqueeze(1).to_broadcast([128, GH, 128]), op=ALU.mult),
                      reads=[d_qn[i], d_nw], writes=[d_qn[i]])
                cb = rope_tab[:, t0:t0 + G, 0:16].unsqueeze(2).to_broadcast([128, G, nh, 16])
                sbb = rope_tab[:, t0:t0 + G, 16:32].unsqueeze(2).to_broadcast([128, G, nh, 16])
                q4 = qn[i][:].rearrange("p (g h) c -> p g h c", g=G)
                x1 = q4[:, :, :, 0:16]
                x2 = q4[:, :, :, 16:32]
                rv = lambda j: rt[i][:, j].rearrange("p (g h) c -> p g h c", g=G)
                R_ = [d_qn[i], d_rope]
                kb.op("dve", lambda e: e.tensor_tensor(out=rv(0), in0=x1, in1=cb, op=ALU.mult), reads=R_, writes=[d_rt[i]])
                kb.op("dve", lambda e: e.tensor_tensor(out=rv(1), in0=x2, in1=sbb, op=ALU.mult), reads=R_, writes=[d_rt[i]])
                kb.op("dve", lambda e: e.tensor_tensor(out=rv(2), in0=x2, in1=cb, op=ALU.mult), reads=R_, writes=[d_rt[i]])
                kb.op("dve", lambda e: e.tensor_tensor(out=rv(3), in0=x1, in1=sbb, op=ALU.mult), reads=R_, writes=[d_rt[i]])
                kb.op("dve", lambda e: e.tensor_tensor(out=qb[i][:, :, 0:16], in0=rt[i][:, 0], in1=rt[i][:, 1], op=ALU.subtract), reads=[d_rt[i]], writes=[d_qb[i]])
                kb.op("dve", lambda e: e.tensor_tensor(out=qb[i][:, :, 16:32], in0=rt[i][:, 2], in1=rt[i][:, 3], op=ALU.add), reads=[d_rt[i]], writes=[d_qb[i]])
                kb.op("act", lambda e: e.activation(out=qb[i][:, :, 32:128], in_=qn[i][:, :, 32:128], func=AF.Copy), reads=[d_qn[i]], writes=[d_qb[i]])
            def f3(g):
                i = g % NBUF
                t0 = g * G
                nb = (GH + 7) // 8
                for bi in range(nb):
                    bk = bank + ((g * nb + bi) % 4)
                    pb = ps[bk][:].bitcast(BF16)
                    n_here = min(8, GH - bi * 8)
                    for j in range(n_here):
                        kb.op("pe", lambda e: e.transpose(out=pb[:, j * 128:(j + 1) * 128], in_=qb[i][:, bi * 8 + j, :], identity=self.identb[:]),
                              reads=[d_qb[i], self.d_const], writes=[dps[bk]])
                    ng = n_here // nh
                    gt0 = t0 + (bi * 8) // nh
                    dst = dstT[:, :, gt0 * 128:(gt0 + ng) * 128].rearrange("p h (g n) -> p g h n", g=ng)
                    srcp = pb[:, 0:n_here * 128].rearrange("p (g h n) -> p g h n", g=ng, h=nh)
                    eng = "act" if bi % 2 == 0 else "dve"
                    if eng == "act":
                        kb.op("act", lambda e: e.activation(out=dst, in_=srcp, func=AF.Copy), reads=[dps[bk]], writes=[d_dst])
                    else:
                        kb.op("dve", lambda e: e.tensor_copy(out=dst, in_=srcp), reads=[dps[bk]], writes=[d_dst])
            self.emit_pipelined(ntiles // G, [f1, f2, f3])
            kb.barrier()

    def phase_MOBA(self, l):
        nc, kb = self.nc, self.kb
        ps, dps = self.ps, self.dps
        SC = 1.0 / math.sqrt(128.0)
        with ExitStack() as st:
            sb = lambda n, s, d: st.enter_context(self.sbt("M_" + n, s, d))
            QT = sb("QT", [128, 4, S], BF16)
            KT = sb("KT", [128, 4, S], BF16)
            Vp = sb("Vp", [128, NT, 4, 130], BF16)
            rope = sb("rope", [128, NT, 32], F32)
            tri = sb("tri", [128, 128], BF16)
            kmf = sb("kmf", [128, 4, 16], F32)
            kmT = sb("kmT", [128, 4, 16], BF16)
            d_QT, d_KT, d_Vp, d_SEL, d_OM, d_rope, d_tri, d_km = kb.deps_n(8, "mb")
            kb.dma("sp", rope[:], self.c_rope.rearrange("(t p) c -> p t c", p=128), writes=[d_rope])
            kb.dma("sp", tri[:], self.c_tri, writes=[d_tri])
            kb.op("pool", lambda e: e.memset(Vp[:].rearrange("p a b c -> p (a b c)"), 1.0), writes=[d_Vp])
            for h in range(4):
                kb.dma("sp", Vp[:, :, h, 0:128], self.proj_tm[:, C_MV + h * 128:C_MV + (h + 1) * 128].rearrange("(t p) c -> p t c", p=128),
                       reads=[d_Vp], writes=[d_Vp])
            self.qk_prep("mq", self.proj_tm, C_MQ, 4, self.hn["moba_q_norm"], l, QT, d_QT, rope, d_rope)
            self.qk_prep("mk", self.proj_tm, C_MK, 4, self.hn["moba_k_norm"], l, KT, d_KT, rope, d_rope)
            kb.barrier()
            SEL = sb("SEL", [128, NT, 4, 16], F32)
            OM = sb("OM", [128, NT, 512], BF16)
            kb.op("dve", lambda e: e.memset(SEL[:].rearrange("p a b c -> p (a b c)"), 1.0), writes=[d_SEL])
            for h in range(4):
                kb.op("dve", lambda e: e.tensor_reduce(out=kmf[:, h, :], in_=KT[:, h, :].rearrange("p (n k) -> p n k", k=256), axis=AX.X, op=ALU.add),
                      reads=[d_KT], writes=[d_km])
            kb.op("dve", lambda e: e.tensor_scalar(out=kmT[:], in0=kmf[:], scalar1=1.0 / 256, scalar2=None, op0=ALU.mult), reads=[d_km], writes=[d_km])
            with ExitStack() as st2:
                sb2 = lambda n, s, d: st2.enter_context(self.sbt("M2_" + n, s, d))
                gt = [sb2("gt%d" % i, [128, 4, 16], F32) for i in range(2)]
                m8 = [sb2("m8%d" % i, [128, 4, 8], F32) for i in range(2)]
                d_gt = kb.deps_n(2)
                d_m8 = kb.deps_n(2)
                for tt in range(8, NT):
                    own = tt // 2
                    i = tt % 2
                    for h in range(4):
                        kb.op("pe", lambda e: e.matmul(ps[4][:, h * 16:(h + 1) * 16], lhsT=QT[:, h, tt * 128:(tt + 1) * 128], rhs=kmT[:, h, :], start=True, stop=True),
                              reads=[d_QT, d_km], writes=[dps[4]])
                    kb.op("dve", lambda e: e.tensor_copy(out=gt[i][:].rearrange("p a b -> p (a b)"), in_=ps[4][:, 0:64]), reads=[dps[4]], writes=[d_gt[i]])
                    kb.op("dve", lambda e: e.memset(gt[i][:, :, own:16], NEG), reads=[d_gt[i]], writes=[d_gt[i]])
                    for h in range(4):
                        kb.op("dve", lambda e: e.max(out=m8[i][:, h, :], in_=gt[i][:, h, :]), reads=[d_gt[i]], writes=[d_m8[i]])
                    for h in range(4):
                        kb.op("dve", lambda e: e.tensor_scalar(out=SEL[:, tt, h, :], in0=gt[i][:, h, :], scalar1=m8[i][:, h, 2:3], scalar2=None, op0=ALU.is_ge),
                              reads=[d_gt[i], d_m8[i]], writes=[d_SEL])
            kb.barrier()
            PT = [sb("PT%d" % i, [128, 512], BF16) for i in range(5)]
            acc = [sb("acc%d" % i, [128, 2, 130], F32) for i in range(2)]
            rr = [sb("rr%d" % i, [128, 2], F32) for i in range(2)]
            acc1 = [sb("acc1%d" % i, [128, 130], F32) for i in range(2)]
            wtmp = [sb("wtmp%d" % i, [128, 129], F32) for i in range(3)]
            d_acc1 = kb.deps_n(2)
            d_wtmp = kb.deps_n(3)
            d_PT = kb.deps_n(5)
            d_acc = kb.deps_n(2)
            d_rr = kb.deps_n(2)
            iters = [(h, qb, n) for h in range(4) for qb in range(16) for n in range(qb + 1)]

            def front(idx):
                h, qb, n = iters[idx]
                pi = idx % 5
                sbank = (0, 1, 2, 5)[idx % 4]
                for kt in range(2):
                    kb.op("pe", lambda e: e.matmul(ps[sbank][:, kt * 256:(kt + 1) * 256], lhsT=KT[:, h, (2 * n + kt) * 128:(2 * n + kt + 1) * 128],
                                                   rhs=QT[:, h, qb * 256:(qb + 1) * 256], start=True, stop=True),
                          reads=[d_KT, d_QT], writes=[dps[sbank]])
                kb.op("act", lambda e: e.activation(out=PT[pi][:], in_=ps[sbank][:], func=AF.Exp, scale=SC), reads=[dps[sbank]], writes=[d_PT[pi]])
                if n == qb:
                    kb.op("pool", lambda e: e.tensor_tensor(out=PT[pi][:, 0:128], in0=PT[pi][:, 0:128], in1=tri[:], op=ALU.mult),
                          reads=[d_PT[pi], d_tri], writes=[d_PT[pi]])
                    kb.op("pool", lambda e: e.tensor_tensor(out=PT[pi][:, 384:512], in0=PT[pi][:, 384:512], in1=tri[:], op=ALU.mult),
                          reads=[d_PT[pi], d_tri], writes=[d_PT[pi]])

            def back(idx):
                h, qb, n = iters[idx]
                pi = idx % 5
                oA = 3 + (idx % 2)
                oB = 6 + (idx % 2)
                ai = (h * 16 + qb) % 2
                if n == 0:
                    kb.op("pool", lambda e: e.memset(acc[ai][:].rearrange("p a b -> p (a b)"), 0.0), writes=[d_acc[ai]])
                    kb.op("pool", lambda e: e.memset(acc1[ai][:], 0.0), writes=[d_acc1[ai]])
                if n < qb:
                    for qt, ob in ((0, oA), (1, oB)):
                        for kt in range(2):
                            kb.op("pe", lambda e: e.matmul(ps[ob][:, 0:129], lhsT=PT[pi][:, kt * 256 + qt * 128:kt * 256 + (qt + 1) * 128],
                                                           rhs=Vp[:, 2 * n + kt, h, 0:129], start=(kt == 0), stop=(kt == 1)),
                                  reads=[d_PT[pi], d_Vp], writes=[dps[ob]])
                    kb.op("dve", lambda e: e.scalar_tensor_tensor(out=acc[ai][:, 0, 0:129], in0=ps[oA][:, 0:129],
                                                                  scalar=SEL[:, 2 * qb, h, n:n + 1], in1=acc[ai][:, 0, 0:129],
                                                                  op0=ALU.mult, op1=ALU.add),
                          reads=[dps[oA], d_SEL, d_acc[ai]], writes=[d_acc[ai]])
                    if idx % 2 == 0:
                        wi_ = (idx // 2) % 3
                        kb.op("act", lambda e: e.activation(out=wtmp[wi_][:], in_=ps[oB][:, 0:129], func=AF.Copy, scale=SEL[:, 2 * qb + 1, h, n:n + 1]),
                              reads=[dps[oB], d_SEL], writes=[d_wtmp[wi_]])
                        kb.op("pool", lambda e: e.tensor_tensor(out=acc1[ai][:, 0:129], in0=acc1[ai][:, 0:129], in1=wtmp[wi_][:], op=ALU.add),
                              reads=[d_wtmp[wi_], d_acc1[ai]], writes=[d_acc1[ai]])
                    else:
                        kb.op("dve", lambda e: e.scalar_tensor_tensor(out=acc1[ai][:, 0:129], in0=ps[oB][:, 0:129],
                                                                      scalar=SEL[:, 2 * qb + 1, h, n:n + 1], in1=acc1[ai][:, 0:129],
                                                                      op0=ALU.mult, op1=ALU.add),
                              reads=[dps[oB], d_SEL, d_acc1[ai]], writes=[d_acc1[ai]])
                else:
                    kb.op("pe", lambda e: e.matmul(ps[oA][:, 0:129], lhsT=PT[pi][:, 0:128], rhs=Vp[:, 2 * qb, h, 0:129], start=True, stop=True),
                          reads=[d_PT[pi], d_Vp], writes=[dps[oA]])
                    kb.op("pe", lambda e: e.matmul(ps[oB][:, 0:129], lhsT=PT[pi][:, 128:256], rhs=Vp[:, 2 * qb, h, 0:129], start=True, stop=False),
                          reads=[d_PT[pi], d_Vp], writes=[dps[oB]])
                    kb.op("pe", lambda e: e.matmul(ps[oB][:, 0:129], lhsT=PT[pi][:, 384:512], rhs=Vp[:, 2 * qb + 1, h, 0:129], start=False, stop=True),
                          reads=[d_PT[pi], d_Vp], writes=[dps[oB]])
                    kb.op("dve", lambda e: e.tensor_tensor(out=acc[ai][:, 0, 0:129], in0=ps[oA][:, 0:129], in1=acc[ai][:, 0, 0:129], op=ALU.add),
                          reads=[dps[oA], d_acc[ai]], writes=[d_acc[ai]])
                    kb.op("dve", lambda e: e.tensor_tensor(out=acc[ai][:, 1, 0:129], in0=ps[oB][:, 0:129], in1=acc1[ai][:, 0:129], op=ALU.add),
                          reads=[dps[oB], d_acc1[ai], d_acc[ai]], writes=[d_acc[ai]])
                    kb.op("dve", lambda e: e.reciprocal(out=rr[ai][:], in_=acc[ai][:, :, 128]), reads=[d_acc[ai]], writes=[d_rr[ai]])
                    for qt in range(2):
                        kb.op("dve", lambda e: e.tensor_scalar(out=OM[:, 2 * qb + qt, h * 128:(h + 1) * 128], in0=acc[ai][:, qt, 0:128], scalar1=rr[ai][:, qt:qt + 1],
                                                               scalar2=None, op0=ALU.mult), reads=[d_acc[ai], d_rr[ai]], writes=[d_OM])

            SK = 3
            for idx in range(min(SK, len(iters))):
                front(idx)
            for idx in range(len(iters)):
                if idx + SK < len(iters):
                    front(idx + SK)
                back(idx)
            kb.barrier()
            self.gate_and_store(OM, d_OM, C_MZ, 0)

    def gate_and_store(self, OM, d_OM, zcol, row0):
        nc, kb = self.nc, self.kb
        ps, dps = self.ps, self.dps
        with ExitStack() as st:
            sb = lambda n, s, d: st.enter_context(self.sbt("G_" + n, s, d))
            zt = [sb("zt%d" % i, [128, 512], BF16) for i in range(4)]
            sl = [sb("sl%d" % i, [128, 512], F32) for i in range(4)]
            gg = [sb("gg%d" % i, [128, 512], BF16) for i in range(4)]
            oT = [sb("oT%d" % i, [128, 4, 128], BF16) for i in range(4)]
            d_zt = kb.deps_n(4)
            d_sl = kb.deps_n(4)
            d_gg = kb.deps_n(4)
            d_oT = kb.deps_n(4)
            def g1(tt):
                i = tt % 4
                kb.dma("sp", zt[i][:], self.proj_tm[tt * 128:(tt + 1) * 128, zcol:zcol + 512], writes=[d_zt[i]])
                kb.op("act", lambda e: e.activation(out=sl[i][:], in_=zt[i][:], func=AF.Silu), reads=[d_zt[i]], writes=[d_sl[i]])
                kb.op("dve", lambda e: e.tensor_tensor(out=gg[i][:], in0=OM[:, tt, :], in1=sl[i][:], op=ALU.mult), reads=[d_OM, d_sl[i]], writes=[d_gg[i]])
            def g2(tt):
                i = tt % 4
                bk = 4 + i
                pb = ps[bk][:].bitcast(BF16)
                for h in range(4):
                    kb.op("pe", lambda e: e.transpose(out=pb[:, h * 128:(h + 1) * 128], in_=gg[i][:, h * 128:(h + 1) * 128], identity=self.identb[:]),
                          reads=[d_gg[i], self.d_const], writes=[dps[bk]])
                kb.op("act", lambda e: e.activation(out=oT[i][:].rearrange("p h n -> p (h n)"), in_=pb[:, 0:512], func=AF.Copy), reads=[dps[bk]], writes=[d_oT[i]])
                kb.dma("pool", self.mixedT[row0:row0 + 512, tt * 128:(tt + 1) * 128].rearrange("(h p) n -> p h n", p=128), oT[i][:], reads=[d_oT[i]])
            self.emit_pipelined(NT, [g1, g2])

    def gelu_tanh(self, x, dx, tmp, dtmp, out, dout):
        kb = self.kb
        kb.op("act", lambda e: e.activation(out=tmp, in_=x, func=AF.Square), reads=[dx], writes=[dtmp])
        kb.op("dve", lambda e: e.tensor_scalar(out=tmp, in0=tmp, scalar1=0.044715, scalar2=1.0, op0=ALU.mult, op1=ALU.add), reads=[dtmp], writes=[dtmp])
        kb.op("dve", lambda e: e.tensor_tensor(out=tmp, in0=tmp, in1=x, op=ALU.mult), reads=[dtmp, dx], writes=[dtmp])
        kb.op("act", lambda e: e.activation(out=tmp, in_=tmp, func=AF.Sigmoid, scale=1.5957691216057308), reads=[dtmp], writes=[dtmp])
        kb.op("dve", lambda e: e.tensor_tensor(out=out, in0=x, in1=tmp, op=ALU.mult), reads=[dx, dtmp], writes=[dout])

    def phase_NSA(self, l):
        nc, kb = self.nc, self.kb
        ps, dps = self.ps, self.dps
        SC = 1.0 / math.sqrt(128.0)
        with ExitStack() as st:
            sb = lambda n, s, d: st.enter_context(self.sbt("N_" + n, s, d))
            NQT = sb("NQT", [128, 4, S], BF16)
            KST = sb("KST", [128, 1, S], BF16)
            KWT = sb("KWT", [128, 1, S], BF16)
            KCT = sb("KCT", [128, 1, 256], BF16)
            VS = sb("VS", [128, NT, 130], BF16)
            VW = sb("VW", [128, NT, 130], BF16)
            RC = sb("RC", [128, 2, 196], BF16)
            SELT = sb("SELT", [64, NT, 128], BF16)
            ESEL = sb("ESEL", [64, NT, 128], BF16)
            G = sb("G", [128, NT, 12], F32)
            rope = sb("rope", [128, NT, 32], F32)
            ropec = sb("ropec", [128, 2, 32], F32)
            tri = sb("tri", [128, 128], BF16)
            triu = sb("triu", [128, 128], BF16)
            dkq = sb("dkq", [128, 128], F32)
            d_NQT, d_KST, d_KWT, d_KCT, d_VS, d_VW, d_RC, d_ONS, d_SELT, d_G, d_rope, d_cst = kb.deps_n(12, "ns")
            kb.dma("sp", rope[:], self.c_rope.rearrange("(t p) c -> p t c", p=128), writes=[d_rope])
            kb.dma("sp", ropec[:], self.c_ropec.rearrange("(t p) c -> p t c", p=128), writes=[d_rope])
            kb.dma("sp", tri[:], self.c_tri, writes=[d_cst])
            kb.dma("sp", triu[:], self.c_triu, writes=[d_cst])
            kb.dma("sp", dkq[:], self.c_dkq, writes=[d_cst])
            kb.dma("sp", ESEL[:], self.c_esel, writes=[d_cst])
            kb.op("pool", lambda e: e.memset(VS[:].rearrange("p a b -> p (a b)"), 1.0), writes=[d_VS])
            kb.op("pool", lambda e: e.memset(VW[:].rearrange("p a b -> p (a b)"), 1.0), writes=[d_VW])
            kb.op("pool", lambda e: e.memset(RC[:].rearrange("p a b -> p (a b)"), 1.0), writes=[d_RC])
            kb.dma("sp", VS[:, :, 0:128], self.proj_tm[:, C_NVS:C_NVS + 128].rearrange("(t p) c -> p t c", p=128), reads=[d_VS], writes=[d_VS])
            kb.dma("sp", VW[:, :, 0:128], self.proj_tm[:, C_NVW:C_NVW + 128].rearrange("(t p) c -> p t c", p=128), reads=[d_VW], writes=[d_VW])
            kb.dma("sp", RC[:, :, 129:193], self.c_ovl.rearrange("(t p) j -> p t j", p=128), reads=[d_RC], writes=[d_RC])
            kb.dma("pool", G[:], self.proj_tm[:, C_NG:C_NG + 12].rearrange("(t p) c -> p t c", p=128), writes=[d_G])
            kb.op("act", lambda e: e.activation(out=G[:].rearrange("p a b -> p (a b)"), in_=G[:].rearrange("p a b -> p (a b)"), func=AF.Sigmoid),
                  reads=[d_G], writes=[d_G])
            self.qk_prep("nq", self.proj_tm, C_NQ, 4, self.hn["nsa_q_norm"], l, NQT, d_NQT, rope, d_rope)
            self.qk_prep("nks", self.proj_tm, C_NKS, 1, self.hn["nsa_ks_norm"], l, KST, d_KST, rope, d_rope)
            self.qk_prep("nkw", self.proj_tm, C_NKW, 1, self.hn["nsa_kw_norm"], l, KWT, d_KWT, rope, d_rope)
            with ExitStack() as st2:
                sb2 = lambda n, s, d: st2.enter_context(self.sbt("N2_" + n, s, d))
                XcT = sb2("XcT", [128, 2, S], BF16)
                w1b = [sb2("w1b%d" % i, [128, 32, 128], BF16) for i in range(2)]
                w2b = [sb2("w2b%d" % i, [128, 128], BF16) for i in range(2)]
                pef = sb2("pef", [32, 2, 128], F32)
                peT = sb2("peT", [128, 2, 32], BF16)
                cvec = sb2("cvec", [128, 2], F32)
                xr = [sb2("xr%d" % i, [128, 256], BF16) for i in range(2)]
                hs = sb2("hs", [128, 256], F32)
                htmp = sb2("htmp", [128, 256], F32)
                hT = [sb2("hT%d" % i, [128, 256], BF16) for i in range(2)]
                kcs = sb2("kcs", [128, 2, 128], BF16)
                d_XcT, d_w1, d_w2, d_pef, d_peT, d_cvec, d_hs, d_htmp, d_kcs = kb.deps_n(9, "cm")
                d_xr = kb.deps_n(2)
                d_hT = kb.deps_n(2)
                w1s = [self.ck_w1, self.cv_w1]
                w2s = [self.ck_w2, self.cv_w2]
                pes = [self.pe_k, self.pe_v]
                for j in range(2):
                    kb.dma("pool", w1b[j][:], w1s[j][l].rearrange("(l d) o -> d l o", d=128), writes=[d_w1])
                    kb.dma("pool", w2b[j][:], w2s[j][l], writes=[d_w2])
                    kb.dma("sp", pef[:, j, :], pes[j][l], writes=[d_pef])
                for j in range(2):
                    kb.op("pe", lambda e: e.transpose(out=ps[0][:, j * 32:(j + 1) * 32], in_=pef[:, j, :], identity=self.identf[0:32, 0:32]),
                          reads=[d_pef, self.d_const], writes=[dps[0]])
                kb.op("dve", lambda e: e.tensor_copy(out=peT[:].rearrange("p a b -> p (a b)"), in_=ps[0][:, 0:64]), reads=[dps[0]], writes=[d_peT])
                for tt in range(NT):
                    i = tt % 2
                    kb.dma("sp", xr[i][:], self.proj_tm[tt * 128:(tt + 1) * 128, C_NKC:C_NKC + 256], writes=[d_xr[i]])
                    bk = 5 + i
                    pb = ps[bk][:].bitcast(BF16)
                    for j in range(2):
                        kb.op("pe", lambda e: e.transpose(out=pb[:, j * 128:(j + 1) * 128], in_=xr[i][:, j * 128:(j + 1) * 128], identity=self.identb[:]),
                              reads=[d_xr[i], self.d_const], writes=[dps[bk]])
                    kb.op("dve", lambda e: e.tensor_copy(out=XcT[:, :, tt * 128:(tt + 1) * 128], in_=pb[:, 0:256].rearrange("p (j n) -> p j n", j=2)),
                          reads=[dps[bk]], writes=[d_XcT])
                for j in range(2):
                    for ll in range(32):
                        kb.op("pe", lambda e: e.matmul(ps[1][:, j:j + 1], lhsT=w1b[j][:, ll, :], rhs=peT[:, j, ll:ll + 1], start=(ll == 0), stop=(ll == 31)),
                              reads=[d_w1, d_peT], writes=[dps[1]])
                    kb.op("dve", lambda e: e.tensor_copy(out=cvec[:, j:j + 1], in_=ps[1][:, j:j + 1]), reads=[dps[1]], writes=[d_cvec])
                    for ll in range(32):
                        kb.op("pe", lambda e: e.matmul(ps[2 + j][:, 0:255], lhsT=w1b[j][:, ll, :], rhs=XcT[:, j, ll:ll + 16 * 254 + 1:16], start=(ll == 0), stop=(ll == 31)),
                              reads=[d_w1, d_XcT], writes=[dps[2 + j]])
                    kb.op("dve", lambda e: e.tensor_scalar(out=hs[:, 0:255], in0=ps[2 + j][:, 0:255], scalar1=cvec[:, j:j + 1], scalar2=None, op0=ALU.add),
                          reads=[dps[2 + j], d_cvec], writes=[d_hs])
                    kb.op("pool", lambda e: e.memset(hT[j][:], 0.0), writes=[d_hT[j]])
                    self.gelu_tanh(hs[:, 0:255], d_hs, htmp[:, 0:255], d_htmp, hT[j][:, 0:255], d_hT[j])
                    for it in range(2):
                        kb.op("pe", lambda e: e.matmul(ps[4][:, it * 128:(it + 1) * 128], lhsT=hT[j][:, it * 128:(it + 1) * 128], rhs=w2b[j][:], start=True, stop=True),
                              reads=[d_hT[j], d_w2], writes=[dps[4]])
                    if j == 0:
                        kb.op("dve", lambda e: e.tensor_copy(out=kcs[:].rearrange("p a b -> p (a b)"), in_=ps[4][:, 0:256]), reads=[dps[4]], writes=[d_kcs])
                        kb.dma("sp", self.kcmp_tm.rearrange("(t p) c -> p t c", p=128), kcs[:], reads=[d_kcs])
                    else:
                        kb.op("dve", lambda e: e.tensor_copy(out=RC[:, :, 0:128], in_=ps[4][:, 0:256].rearrange("p (a b) -> p a b", a=2)),
                              reads=[dps[4], d_RC], writes=[d_RC])
                kb.barrier()
            self.qk_prep("nkc", self.kcmp_tm, 0, 1, self.hn["nsa_kc_norm"], l, KCT, d_KCT, ropec, d_rope, ntiles=2)
            ONS = sb("ONS", [128, NT, 512], F32)
            PT = [sb("PT%d" % i, [128, 4, 128], BF16) for i in range(8)]
            M2s = [sb("M2s%d" % i, [128, 128], BF16) for i in range(4)]
            sA = [sb("sA%d" % i, [128, 64], F32) for i in range(2)]
            sB = [sb("sB%d" % i, [128, 64], F32) for i in range(2)]
            imp = [sb("imp%d" % i, [128, 64], F32) for i in range(2)]
            sc = [sb("sc%d" % i, [128, 2, 64], F32) for i in range(2)]
            m8 = [sb("m8%d" % i, [128, 2, 8], F32) for i in range(2)]
            selq = [sb("selq%d" % i, [128, 64], BF16) for i in range(2)]
            rr = [sb("rr%d" % i, [128, 8], F32) for i in range(2)]
            d_PT = kb.deps_n(8)
            d_M2s = kb.deps_n(4)
            d_m2p = kb.deps_n(4)
            d_sA = kb.deps_n(2)
            d_sB = kb.deps_n(2)
            d_imp = kb.deps_n(2)
            d_sc = kb.deps_n(2)
            d_m8 = kb.deps_n(2)
            d_selq = kb.deps_n(2)
            d_rr = kb.deps_n(2)
            ip = 0

            def obank(tt, h):
                return 5 + h // 2, (h % 2) * 256

            zl = sb("zl", [128, 128], BF16)
            zr_ = sb("zr", [128, 512], BF16)
            d_z = kb.dep("zeros")
            kb.op("pool", lambda e: e.memset(zl[:], 0.0), writes=[d_z])
            kb.op("pool", lambda e: e.memset(zr_[:], 0.0), writes=[d_z])

            def zero_obanks():
                for b in (5, 6):
                    kb.op("pe", lambda e: e.matmul(ps[b][:], lhsT=zl[:], rhs=zr_[:], start=True, stop=True), reads=[d_z], writes=[dps[b]])

            def finalize(tt, branch, first):
                i = tt % 2
                for h in range(4):
                    b, c0 = obank(tt, h)
                    kb.op("dve", lambda e: e.tensor_scalar(out=rr[i][:, h:h + 1], in0=ps[b][:, c0 + 128:c0 + 129], scalar1=1e-30, scalar2=None, op0=ALU.max),
                          reads=[dps[b]], writes=[d_rr[i]])
                kb.op("dve", lambda e: e.reciprocal(out=rr[i][:, 0:4], in_=rr[i][:, 0:4]), reads=[d_rr[i]], writes=[d_rr[i]])
                kb.op("dve", lambda e: e.tensor_tensor(out=rr[i][:, 4:8], in0=rr[i][:, 0:4], in1=G[:, tt, branch:12:3], op=ALU.mult), reads=[d_rr[i], d_G], writes=[d_rr[i]])
                for h in range(4):
                    b, c0 = obank(tt, h)
                    dst = ONS[:, tt, h * 128:(h + 1) * 128]
                    if first:
                        kb.op("dve", lambda e: e.tensor_scalar(out=dst, in0=ps[b][:, c0:c0 + 128], scalar1=rr[i][:, 4 + h:5 + h], scalar2=None, op0=ALU.mult),
                              reads=[dps[b], d_rr[i]], writes=[d_ONS])
                    else:
                        kb.op("dve", lambda e: e.scalar_tensor_tensor(out=dst, in0=ps[b][:, c0:c0 + 128], scalar=rr[i][:, 4 + h:5 + h], in1=dst, op0=ALU.mult, op1=ALU.add),
                              reads=[dps[b], d_rr[i], d_ONS], writes=[d_ONS])

            it_cmp = [(tt, it) for tt in range(NT) for it in range(1 if tt < 16 else 2)]

            def c_front(idx):
                tt, it = it_cmp[idx]
                pi = idx % 3
                sbank = idx % 2
                i = tt % 2
                if it == 0:
                    kb.dma("sp", sA[i][:], self.c_selA[tt], writes=[d_sA[i]])
                    kb.dma("sp", sB[i][:], self.c_selB[tt], writes=[d_sB[i]])
                kb.op("pe", lambda e: e.matmul(ps[sbank][:], lhsT=KCT[:, 0, it * 128:(it + 1) * 128], rhs=NQT[:, :, tt * 128:(tt + 1) * 128], start=True, stop=True),
                      reads=[d_KCT, d_NQT], writes=[dps[sbank]])
                kb.op("act", lambda e: e.activation(out=PT[pi][:].rearrange("p a b -> p (a b)"), in_=ps[sbank][:], func=AF.Exp, scale=SC),
                      reads=[dps[sbank]], writes=[d_PT[pi]])
                thr = float(31 + 2048 * it - 128 * tt)
                kb.op("dve", lambda e: e.scalar_tensor_tensor(out=PT[pi][:], in0=dkq[:].unsqueeze(1).to_broadcast([128, 4, 128]), scalar=thr, in1=PT[pi][:],
                                                              op0=ALU.is_ge, op1=ALU.mult), reads=[d_PT[pi], d_cst], writes=[d_PT[pi]])

            def c_back(idx):
                tt, it = it_cmp[idx]
                pi = idx % 3
                i = tt % 2
                n_it = 1 if tt < 16 else 2
                for h in range(4):
                    b, c0 = obank(tt, h)
                    if it == 0 and h == 0:
                        zero_obanks()
                    kb.op("pe", lambda e: e.matmul(ps[b][:, c0:c0 + 193], lhsT=PT[pi][:, h, :], rhs=RC[:, it, 0:193], start=False, stop=(it == n_it - 1)),
                          reads=[d_PT[pi], d_RC], writes=[dps[b]])
                if it != n_it - 1:
                    return
                finalize(tt, 0, True)
                for h in range(4):
                    b, c0 = obank(tt, h)
                    if h == 0:
                        kb.op("dve", lambda e: e.tensor_scalar(out=imp[i][:], in0=ps[b][:, c0 + 129:c0 + 193], scalar1=rr[i][:, h:h + 1], scalar2=None, op0=ALU.mult),
                              reads=[dps[b], d_rr[i]], writes=[d_imp[i]])
                    else:
                        kb.op("dve", lambda e: e.scalar_tensor_tensor(out=imp[i][:], in0=ps[b][:, c0 + 129:c0 + 193], scalar=rr[i][:, h:h + 1], in1=imp[i][:],
                                                                      op0=ALU.mult, op1=ALU.add), reads=[dps[b], d_rr[i], d_imp[i]], writes=[d_imp[i]])
                kb.op("dve", lambda e: e.tensor_tensor(out=sc[i][:, 0, :], in0=imp[i][:], in1=sA[i][:], op=ALU.mult), reads=[d_imp[i], d_sA[i]], writes=[d_sc[i]])
                kb.op("dve", lambda e: e.tensor_tensor(out=sc[i][:, 0, :], in0=sc[i][:, 0, :], in1=sB[i][:], op=ALU.add), reads=[d_sc[i], d_sB[i]], writes=[d_sc[i]])
                kb.op("dve", lambda e: e.max(out=m8[i][:, 0, :], in_=sc[i][:, 0, :]), reads=[d_sc[i]], writes=[d_m8[i]])
                kb.op("dve", lambda e: e.match_replace(out=sc[i][:, 1, :], in_to_replace=m8[i][:, 0, :], in_values=sc[i][:, 0, :], imm_value=NEG),
                      reads=[d_sc[i], d_m8[i]], writes=[d_sc[i]])
                kb.op("dve", lambda e: e.max(out=m8[i][:, 1, :], in_=sc[i][:, 1, :]), reads=[d_sc[i]], writes=[d_m8[i]])
                kb.op("dve", lambda e: e.scalar_tensor_tensor(out=selq[i][:], in0=sc[i][:, 0, :], scalar=m8[i][:, 1, 7:8], in1=sA[i][:], op0=ALU.is_ge, op1=ALU.mult),
                      reads=[d_sc[i], d_m8[i], d_sA[i]], writes=[d_selq[i]])
                pb = ps[2 + i][:].bitcast(BF16)
                kb.op("pe", lambda e: e.transpose(out=pb[0:64, 0:128], in_=selq[i][:], identity=self.identb[:]), reads=[d_selq[i], self.d_const], writes=[dps[2 + i]])
                kb.op("act", lambda e: e.activation(out=SELT[:, tt, :], in_=pb[0:64, 0:128], func=AF.Copy), reads=[dps[2 + i]], writes=[d_SELT])

            c_front(0)
            for idx in range(len(it_cmp)):
                if idx + 1 < len(it_cmp):
                    c_front(idx + 1)
                c_back(idx)

            its = []
            for branch in (1, 2):
                for tt in range(NT):
                    kts = list(range(0, tt + 1)) if branch == 1 else list(range(max(0, tt - 4), tt + 1))
                    for ki, kt in enumerate(kts):
                        its.append((branch, tt, ki, kt, len(kts)))

            def a_front(idx):
                branch, tt, ki, kt, nk = its[idx]
                pi = 3 + idx % 5
                sbank = (0, 1, 2, 7)[idx % 4]
                KT_ = KST if branch == 1 else KWT
                d_KT_ = d_KST if branch == 1 else d_KWT
                kb.op("pe", lambda e: e.matmul(ps[sbank][:], lhsT=KT_[:, 0, kt * 128:(kt + 1) * 128], rhs=NQT[:, :, tt * 128:(tt + 1) * 128], start=True, stop=True),
                      reads=[d_KT_, d_NQT], writes=[dps[sbank]])
                kb.op("act", lambda e: e.activation(out=PT[pi][:].rearrange("p a b -> p (a b)"), in_=ps[sbank][:], func=AF.Exp, scale=SC),
                      reads=[dps[sbank]], writes=[d_PT[pi]])
                if branch == 1:
                    mb = 3 + (idx % 2)
                    mi = idx % 4
                    msl = ps[mb][:, 0:128]
                    kb.op("pe", lambda e: e.matmul(msl, lhsT=ESEL[:, kt, :], rhs=SELT[:, tt, :], start=True, stop=True),
                          reads=[d_cst, d_SELT], writes=[dps[mb]])
                    if kt == tt:
                        kb.op("dve", lambda e: e.tensor_tensor(out=M2s[mi][:], in0=msl, in1=tri[:], op=ALU.mult), reads=[dps[mb], d_cst], writes=[d_M2s[mi]])
                    else:
                        kb.op("dve", lambda e: e.tensor_copy(out=M2s[mi][:], in_=msl), reads=[dps[mb]], writes=[d_M2s[mi]])
                    meng = "pool" if idx % 3 == 0 else "dve"
                    kb.op(meng, lambda e: e.tensor_tensor(out=PT[pi][:], in0=PT[pi][:], in1=M2s[mi][:].unsqueeze(1).to_broadcast([128, 4, 128]), op=ALU.mult),
                          reads=[d_PT[pi], d_M2s[mi]], writes=[d_PT[pi]])
                else:
                    if kt == tt:
                        kb.op("pool", lambda e: e.tensor_tensor(out=PT[pi][:], in0=PT[pi][:], in1=tri[:].unsqueeze(1).to_broadcast([128, 4, 128]), op=ALU.mult),
                              reads=[d_PT[pi], d_cst], writes=[d_PT[pi]])
                    elif kt == tt - 4:
                        kb.op("pool", lambda e: e.tensor_tensor(out=PT[pi][:], in0=PT[pi][:], in1=triu[:].unsqueeze(1).to_broadcast([128, 4, 128]), op=ALU.mult),
                              reads=[d_PT[pi], d_cst], writes=[d_PT[pi]])

            def a_back(idx):
                branch, tt, ki, kt, nk = its[idx]
                pi = 3 + idx % 5
                V_ = VS if branch == 1 else VW
                d_V_ = d_VS if branch == 1 else d_VW
                for h in range(4):
                    b, c0 = obank(tt, h)
                    if ki == 0 and h == 0:
                        zero_obanks()
                    kb.op("pe", lambda e: e.matmul(ps[b][:, c0:c0 + 129], lhsT=PT[pi][:, h, :], rhs=V_[:, kt, 0:129], start=False, stop=(ki == nk - 1)),
                          reads=[d_PT[pi], d_V_], writes=[dps[b]])
                if ki == nk - 1:
                    finalize(tt, branch, False)

            SK = 3
            for idx in range(min(SK, len(its))):
                a_front(idx)
            for idx in range(len(its)):
                if idx + SK < len(its):
                    a_front(idx + SK)
                a_back(idx)
            kb.barrier()
            self.gate_and_store(ONS, d_ONS, C_NZ, 512)


def host_consts():
    c = {}
    c["c_identb"] = np.eye(128, dtype=np.float32).astype(ml_dtypes.bfloat16)
    c["c_identf"] = np.eye(128, dtype=np.float32)
    inv = 500000.0 ** (-np.arange(0, 32, 2, dtype=np.float32) / 32.0)
    pos = np.arange(S, dtype=np.float32)
    ang = pos[:, None] * inv[None, :].astype(np.float32)
    c["c_rope"] = np.concatenate([np.cos(ang), np.sin(ang)], axis=1).astype(np.float32)
    posc = (np.arange(256) * 16 + 31).astype(np.float32)
    angc = posc[:, None] * inv[None, :].astype(np.float32)
    c["c_ropec"] = np.concatenate([np.cos(angc), np.sin(angc)], axis=1).astype(np.float32)
    kk = np.arange(128)
    c["c_tri"] = (kk[:, None] <= kk[None, :]).astype(np.float32).astype(ml_dtypes.bfloat16)
    c["c_triu"] = (kk[:, None] > kk[None, :]).astype(np.float32).astype(ml_dtypes.bfloat16)
    c["c_iota"] = np.broadcast_to(np.arange(512, dtype=np.float32)[None, :], (128, 512)).copy()
    c["c_dkq"] = (kk[None, :] - 16 * kk[:, None]).astype(np.float32)
    selA = np.zeros((NT, 128, 64), np.float32)
    selB = np.zeros((NT, 128, 64), np.float32)
    j = np.arange(64)[None, :]
    for tt in range(NT):
        t = tt * 128 + np.arange(128)
        cur = (t // 64)[:, None]
        valid = j <= cur
        forced = (j == 0) | (j == cur) | (j == cur - 1)
        selA[tt] = valid.astype(np.float32)
        selB[tt] = np.where(forced, 1.0e30, np.where(valid, 0.0, -1.0e30))
    c["c_selA"] = selA
    c["c_selB"] = selB
    es = np.zeros((64, NT, 128), np.float32)
    for kt in range(NT):
        for key in range(128):
            es[2 * kt + key // 64, kt, key] = 1.0
    c["c_esel"] = es.astype(ml_dtypes.bfloat16)
    ci = np.arange(256)[:, None] * 16
    sj = np.arange(64)[None, :] * 64
    ov = ((ci < sj + 64) & (ci + 32 > sj)).astype(np.float32)
    ov[255] = 0.0
    c["c_ovl"] = ov.astype(ml_dtypes.bfloat16)
    return c


_PROG = None


def kernel(**inputs):
    global _PROG
    if _PROG is None:
        _PROG = Prog().build()
    nc = _PROG
    consts = host_consts()
    x = np.ascontiguousarray(inputs["x"], dtype=np.float32)
    B = x.shape[0]
    shared = {k: np.ascontiguousarray(v) for k, v in inputs.items() if k != "x"}
    in_maps = []
    for c in range(8):
        m = dict(shared)
        m.update(consts)
        m["x"] = x[c % B]
        in_maps.append(m)
    res = run_bass_kernel_spmd(nc, in_maps, core_ids=list(range(8)))
    out = np.stack([res.results[b]["out"] for b in range(B)], axis=0)
    return out.astype(np.float32, copy=False)
```

```python
import math
from contextlib import ExitStack

import numpy as np
import ml_dtypes
import concourse.bass as bass
import concourse.mybir as mybir
from concourse.bass_utils import run_bass_kernel_spmd

F32 = mybir.dt.float32
BF16 = mybir.dt.bfloat16
I32 = mybir.dt.int32
AF = mybir.ActivationFunctionType
ALU = mybir.AluOpType
AX = mybir.AxisListType

S = 4096
D = 2048
NT = S // 128
INW = 5900
TMW = 3852
DEPTH = 2
EPS = 1e-6
C_MQ, C_MK, C_MV, C_MZ, C_NQ = 0, 512, 1024, 1536, 2048
C_NKC, C_NVC, C_NKS, C_NVS, C_NKW, C_NVW = 2560, 2688, 2816, 2944, 3072, 3200
C_NG, C_NZ = 3328, 3340
NEG = -1.0e30


class Dep:
    __slots__ = ("w", "r", "name")

    def __init__(self, name=""):
        self.w = {}
        self.r = {}
        self.name = name


class Eng:
    def __init__(self, nc, eng, name):
        self.eng = eng
        self.name = name
        self.sem = nc.alloc_semaphore("sem_" + name)
        self.count = 0
        self.seen = {}


class KB:
    def __init__(self, nc, n_dma_sems=48):
        self.nc = nc
        self.E = {
            "pe": Eng(nc, nc.tensor, "pe"),
            "act": Eng(nc, nc.scalar, "act"),
            "dve": Eng(nc, nc.vector, "dve"),
            "pool": Eng(nc, nc.gpsimd, "pool"),
            "sp": Eng(nc, nc.sync, "sp"),
        }
        self.dsems = [[nc.alloc_semaphore("dsem%d" % i), 0] for i in range(n_dma_sems)]
        self.dnext = 0
        self.deps = []
        self.n_wait = 0
        self.n_ins = 0

    def dep(self, name=""):
        d = Dep(name)
        self.deps.append(d)
        return d

    def deps_n(self, n, name=""):
        return [self.dep(name + str(i)) for i in range(n)]

    def _wait(self, E, sem, val):
        k = id(sem)
        if E.seen.get(k, 0) < val:
            E.eng.wait_ge(sem, val)
            E.seen[k] = val
            self.n_wait += 1

    def _sync(self, E, reads, writes, own_sem=None):
        for d in reads:
            for k, (s, v) in d.w.items():
                self._wait(E, s, v)
        for d in writes:
            for k, (s, v) in d.w.items():
                if s is own_sem:
                    continue
                self._wait(E, s, v)
            for k, (s, v) in d.r.items():
                if s is own_sem:
                    continue
                self._wait(E, s, v)

    def _record(self, sem, val, reads, writes):
        k = id(sem)
        for d in writes:
            d.w = {k: (sem, val)}
            d.r = {}
        for d in reads:
            d.r[k] = (sem, val)

    def op(self, e, f, reads=(), writes=()):
        E = self.E[e]
        self._sync(E, reads, writes, own_sem=E.sem)
        ins = f(E.eng)
        E.count += 1
        ins.then_inc(E.sem, 1)
        self._record(E.sem, E.count, reads, writes)
        self.n_ins += 1
        return ins

    def dma(self, q, out, in_, reads=(), writes=(), **kw):
        E = self.E[q]
        self._sync(E, reads, writes)
        ent = self.dsems[self.dnext]
        self.dnext = (self.dnext + 1) % len(self.dsems)
        if ent[1] > 0:
            self._wait(E, ent[0], ent[1])
        ent[1] += 16
        ins = E.eng.dma_start(out=out, in_=in_, **kw)
        ins.then_inc(ent[0], 16)
        self._record(ent[0], ent[1], reads, writes)
        self.n_ins += 1
        return ins

    def barrier(self):
        sp = self.E["sp"]
        for n, E in self.E.items():
            if E is not sp and E.count > 0:
                self._wait(sp, E.sem, E.count)
        for s, v in self.dsems:
            if v > 0:
                self._wait(sp, s, v)
        sp.count += 1
        sp.eng.nop().then_inc(sp.sem, 1)
        for n, E in self.E.items():
            if E is not sp:
                self._wait(E, sp.sem, sp.count)
            for n2, E2 in self.E.items():
                E.seen[id(E2.sem)] = E2.count
            for s, v in self.dsems:
                E.seen[id(s)] = v
        for d in self.deps:
            d.w = {}
            d.r = {}
        self.deps = []


class Prog:
    def __init__(self, dbg=None, layers=DEPTH, phases=("A", "S5", "MOBA", "NSA", "F")):
        self.dbg = dbg or ()
        self.layers = layers
        self.phases = phases
        nc = bass.Bass("TRN2", target_bir_lowering=False)
        self.nc = nc
        self.kb = KB(nc)
        ein = lambda n, s, d: nc.dram_tensor(n, list(s), d, kind="ExternalInput").ap()
        L = DEPTH
        self.x = ein("x", [S, D], F32)
        self.norm_w = ein("norm_w", [L, D], F32)
        self.w_in = ein("w_in", [L, D, INW], F32)
        self.w_out = ein("w_out", [L, D, D], F32)
        self.hn = {}
        for n in ("moba_q_norm", "moba_k_norm", "nsa_q_norm", "nsa_kc_norm", "nsa_ks_norm", "nsa_kw_norm"):
            self.hn[n] = ein(n, [L, 128], F32)
        self.pe_k = ein("nsa_pe_k", [L, 32, 128], F32)
        self.pe_v = ein("nsa_pe_v", [L, 32, 128], F32)
        self.ck_w1 = ein("nsa_cmp_k_w1", [L, 4096, 128], F32)
        self.ck_w2 = ein("nsa_cmp_k_w2", [L, 128, 128], F32)
        self.cv_w1 = ein("nsa_cmp_v_w1", [L, 4096, 128], F32)
        self.cv_w2 = ein("nsa_cmp_v_w2", [L, 128, 128], F32)
        self.a_re = ein("s5_a_re", [L, 64, 64], F32)
        self.a_im = ein("s5_a_im", [L, 64, 64], F32)
        self.b_re = ein("s5_b_re", [L, 64, 64, 16], F32)
        self.b_im = ein("s5_b_im", [L, 64, 64, 16], F32)
        self.c_re = ein("s5_c_re", [L, 64, 16, 64], F32)
        self.c_im = ein("s5_c_im", [L, 64, 16, 64], F32)
        self.s5_d = ein("s5_d", [L, 1024], F32)
        self.log_dt = ein("s5_log_dt", [L, 64], F32)
        self.glu_w = ein("s5_glu_w", [L, 1024, 1024], F32)
        self.c_identb = ein("c_identb", [128, 128], BF16)
        self.c_identf = ein("c_identf", [128, 128], F32)
        self.c_rope = ein("c_rope", [S, 32], F32)
        self.c_ropec = ein("c_ropec", [256, 32], F32)
        self.c_tri = ein("c_tri", [128, 128], BF16)
        self.c_iota = ein("c_iota", [128, 512], F32)
        self.c_triu = ein("c_triu", [128, 128], BF16)
        self.c_dkq = ein("c_dkq", [128, 128], F32)
        self.c_selA = ein("c_selA", [NT, 128, 64], F32)
        self.c_selB = ein("c_selB", [NT, 128, 64], F32)
        self.c_esel = ein("c_esel", [64, NT, 128], BF16)
        self.c_ovl = ein("c_ovl", [256, 64], BF16)
        self.out = nc.dram_tensor("out", [S, D], F32, kind="ExternalOutput").ap()
        sk = lambda n: "ExternalOutput" if n in self.dbg else "Internal"
        self.proj_tm = nc.dram_tensor("proj_tm", [S, TMW], BF16, kind=("ExternalInput" if "proj_in" in self.dbg else sk("proj_tm"))).ap()
        self.sT = nc.dram_tensor("sT", [2048, S], BF16, kind=("ExternalInput" if "sT_in" in self.dbg else sk("sT"))).ap()
        self.mixedT = nc.dram_tensor("mixedT", [2048, S], BF16, kind=("ExternalInput" if "mixedT_in" in self.dbg else sk("mixedT"))).ap()
        self.x1 = nc.dram_tensor("x1", [S, D], F32, kind=sk("x1")).ap()
        self.kcmp_tm = nc.dram_tensor("kcmp_tm", [256, 128], BF16, kind=sk("kcmp_tm")).ap()
        self.y5d = nc.dram_tensor("y5d", [1024, S], BF16, kind=sk("y5d")).ap()

    @staticmethod
    def emit_pipelined(n, stages):
        ns = len(stages)
        for t in range(n + ns - 1):
            for s_idx in range(ns - 1, -1, -1):
                i = t - s_idx
                if 0 <= i < n:
                    stages[s_idx](i)

    def sbt(self, name, shape, dtype):
        self._uid = getattr(self, "_uid", 0) + 1
        return self.nc.sbuf_tensor("%s_u%d" % (name, self._uid), shape, dtype)

    def build(self):
        nc, kb = self.nc, self.kb
        with ExitStack() as st:
            self.ps = [st.enter_context(nc.psum_tensor("ps%d" % i, [128, 512], F32)) for i in range(8)]
            self.dps = kb.deps_n(8, "ps")
            self.identb = st.enter_context(self.sbt("identb", [128, 128], BF16))
            self.identf = st.enter_context(self.sbt("identf", [128, 128], F32))
            self.d_const = kb.dep("const")
            kb.dma("sp", self.identb[:], self.c_identb, writes=[self.d_const])
            kb.dma("sp", self.identf[:], self.c_identf, writes=[self.d_const])
            kb.barrier()
            for l in range(self.layers):
                src = self.x if l == 0 else self.x1
                dst = self.out if l == self.layers - 1 else self.x1
                if "A" in self.phases:
                    self.phase_A(l, src)
                    kb.barrier()
                if "S5" in self.phases:
                    self.phase_S5(l)
                    kb.barrier()
                if "MOBA" in self.phases:
                    self.phase_MOBA(l)
                    kb.barrier()
                if "NSA" in self.phases:
                    self.phase_NSA(l)
                    kb.barrier()
                if "F" in self.phases:
                    self.phase_F(l, src, dst)
                    kb.barrier()
            kb.barrier()
        return nc

    def phase_A(self, l, src):
        nc, kb = self.nc, self.kb
        ps, dps = self.ps, self.dps
        with ExitStack() as st:
            sb = lambda n, s, d: st.enter_context(self.sbt("A_" + n, s, d))
            hdnT = sb("hdnT", [128, 16, 2048], BF16)
            normw = sb("normw", [128, D], F32)
            xt = [sb("xt%d" % i, [128, D], F32) for i in range(3)]
            junk = sb("junk", [128, D], BF16)
            hb = [sb("hb%d" % i, [128, D], BF16) for i in range(3)]
            wch = [sb("wch%d" % i, [128, 16, 512], BF16) for i in range(2)]
            stg = [sb("stg%d" % i, [128, 512], BF16) for i in range(4)]
            ss = [sb("ss%d" % i, [128, 1], F32) for i in range(3)]
            d_hT = kb.deps_n(16, "hT")
            d_nw = kb.dep("nw")
            d_xt = kb.deps_n(3, "xt")
            d_junk = kb.dep("junk")
            d_hb = kb.deps_n(3, "hb")
            d_w = kb.deps_n(2, "w")
            d_stg = kb.deps_n(4, "stg")
            d_ss = kb.deps_n(3, "ss")
            kb.dma("sp", normw[:], self.norm_w[l:l + 1, :].partition_broadcast(128), writes=[d_nw])
            istg = 0
            iw = 0
            ievac = 0
            for h in range(2):
                def a1(tt, h=h):
                    g = h * 16 + tt
                    i = tt % 3
                    xi = tt % 3
                    kb.dma("sp", xt[xi][:], src[g * 128:(g + 1) * 128, :], writes=[d_xt[xi]])
                    kb.op("act", lambda e: e.activation(out=junk[:], in_=xt[xi][:], func=AF.Square, accum_out=ss[i][:]),
                          reads=[d_xt[xi]], writes=[d_junk, d_ss[i]])
                    kb.op("dve", lambda e: e.tensor_scalar(out=ss[i][:], in0=ss[i][:], scalar1=1.0 / D, scalar2=EPS,
                                                           op0=ALU.mult, op1=ALU.add), reads=[d_ss[i]], writes=[d_ss[i]])
                    kb.op("act", lambda e: e.activation(out=ss[i][:], in_=ss[i][:], func=AF.Sqrt), reads=[d_ss[i]], writes=[d_ss[i]])
                    kb.op("dve", lambda e: e.reciprocal(out=ss[i][:], in_=ss[i][:]), reads=[d_ss[i]], writes=[d_ss[i]])
                def a2(tt, h=h):
                    i = tt % 3
                    xi = tt % 3
                    kb.op("dve", lambda e: e.scalar_tensor_tensor(out=hb[xi][:], in0=xt[xi][:], scalar=ss[i][:], in1=normw[:],
                                                                  op0=ALU.mult, op1=ALU.mult),
                          reads=[d_xt[xi], d_ss[i], d_nw], writes=[d_hb[xi]])
                    for half in range(2):
                        tbk = 4 + 2 * (tt % 2) + half
                        pb = ps[tbk][:].bitcast(BF16)
                        for k in range(8):
                            kc = half * 8 + k
                            kb.op("pe", lambda e: e.transpose(out=pb[:, k * 128:(k + 1) * 128], in_=hb[xi][:, kc * 128:(kc + 1) * 128],
                                                              identity=self.identb[:]),
                                  reads=[d_hb[xi], self.d_const], writes=[dps[tbk]])
                def a3(tt, h=h):
                    for half in range(2):
                        tbk = 4 + 2 * (tt % 2) + half
                        pb = ps[tbk][:].bitcast(BF16)
                        eng = "act" if half == 0 else "dve"
                        dst = hdnT[:, half * 8:(half + 1) * 8, tt * 128:(tt + 1) * 128]
                        srcp = pb[:, 0:1024].rearrange("p (k n) -> p k n", k=8)
                        if eng == "act":
                            kb.op("act", lambda e: e.activation(out=dst, in_=srcp, func=AF.Copy), reads=[dps[tbk]], writes=[d_hT[tt]])
                        else:
                            kb.op("dve", lambda e: e.tensor_copy(out=dst, in_=srcp), reads=[dps[tbk]], writes=[d_hT[tt]])
                self.emit_pipelined(16, [a1, a2, a3])
                chunks = [(c0, min(512, TMW - c0)) for c0 in range(0, TMW, 512)]
                for (c0, cw) in chunks:
                    wi = iw % 2
                    iw += 1
                    kb.dma("pool", wch[wi][:, :, 0:cw], self.w_in[l, :, c0:c0 + cw].rearrange("(k p) n -> p k n", p=128),
                           writes=[d_w[wi]])
                    for tt in range(16):
                        g = h * 16 + tt
                        pbank = ievac % 4
                        for kc in range(16):
                            kb.op("pe", lambda e: e.matmul(ps[pbank][:, 0:cw], lhsT=hdnT[:, kc, tt * 128:(tt + 1) * 128],
                                                           rhs=wch[wi][:, kc, 0:cw], start=(kc == 0), stop=(kc == 15)),
                                  reads=[d_hT[tt], d_w[wi]], writes=[dps[pbank]])
                        si = istg % 4
                        istg += 1
                        if ievac % 2 == 0:
                            kb.op("act", lambda e: e.activation(out=stg[si][:, 0:cw], in_=ps[pbank][:, 0:cw], func=AF.Copy),
                                  reads=[dps[pbank]], writes=[d_stg[si]])
                        else:
                            kb.op("dve", lambda e: e.tensor_copy(out=stg[si][:, 0:cw], in_=ps[pbank][:, 0:cw]),
                                  reads=[dps[pbank]], writes=[d_stg[si]])
                        ievac += 1
                        kb.dma("sp", self.proj_tm[g * 128:(g + 1) * 128, c0:c0 + cw], stg[si][:, 0:cw], reads=[d_stg[si]])
                for fc in range(4):
                    c0 = TMW + fc * 512
                    wi = iw % 2
                    iw += 1
                    kb.dma("pool", wch[wi][:], self.w_in[l, :, c0:c0 + 512].rearrange("(k p) n -> p k n", p=128), writes=[d_w[wi]])
                    for ctl in range(4):
                        row0 = fc * 512 + ctl * 128
                        for tb in range(4):
                            pbank = ievac % 4
                            for kc in range(16):
                                kb.op("pe", lambda e: e.matmul(ps[pbank][:], lhsT=wch[wi][:, kc, ctl * 128:(ctl + 1) * 128],
                                                               rhs=hdnT[:, kc, tb * 512:(tb + 1) * 512], start=(kc == 0), stop=(kc == 15)),
                                      reads=d_hT[tb * 4:(tb + 1) * 4] + [d_w[wi]], writes=[dps[pbank]])
                            si = istg % 4
                            istg += 1
                            if ievac % 2 == 0:
                                kb.op("act", lambda e: e.activation(out=stg[si][:], in_=ps[pbank][:], func=AF.Copy),
                                      reads=[dps[pbank]], writes=[d_stg[si]])
                            else:
                                kb.op("dve", lambda e: e.tensor_copy(out=stg[si][:], in_=ps[pbank][:]),
                                      reads=[dps[pbank]], writes=[d_stg[si]])
                            ievac += 1
                            t0 = h * 2048 + tb * 512
                            kb.dma("sp", self.sT[row0:row0 + 128, t0:t0 + 512], stg[si][:], reads=[d_stg[si]])

    def phase_F(self, l, src, dst):
        nc, kb = self.nc, self.kb
        ps, dps = self.ps, self.dps
        with ExitStack() as st:
            sb = lambda n, s, d: st.enter_context(self.sbt("F_" + n, s, d))
            wo = sb("wo", [128, 16, D], BF16)
            mT = [sb("mT%d" % i, [128, 16, 512], BF16) for i in range(2)]
            xr = [sb("xr%d" % i, [128, D], F32) for i in range(2)]
            ot = [sb("ot%d" % i, [128, D], F32) for i in range(2)]
            d_wo = kb.deps_n(4, "wo")
            d_mT = kb.deps_n(2, "mT")
            d_xr = kb.deps_n(2, "xr")
            d_ot = kb.deps_n(2, "ot")
            for c in range(4):
                kb.dma("pool", wo[:, :, c * 512:(c + 1) * 512], self.w_out[l, :, c * 512:(c + 1) * 512].rearrange("(k p) n -> p k n", p=128),
                       writes=[d_wo[c]])
            ie = 0
            import os
            for tb in range(int(os.environ.get('F_TB', 8))):
                mi = tb % 2
                kb.dma("sp", mT[mi][:], self.mixedT[:, tb * 512:(tb + 1) * 512].rearrange("(k p) n -> p k n", p=128), writes=[d_mT[mi]])
                for t4 in range(4):
                    g = tb * 4 + t4
                    i = g % 2
                    kb.dma("sp", xr[i][:], src[g * 128:(g + 1) * 128, :], writes=[d_xr[i]])
                    for c in range(4):
                        pbank = ie % 4
                        ie += 1
                        for kc in range(16):
                            kb.op("pe", lambda e: e.matmul(ps[pbank][:], lhsT=mT[mi][:, kc, t4 * 128:(t4 + 1) * 128],
                                                           rhs=wo[:, kc, c * 512:(c + 1) * 512], start=(kc == 0), stop=(kc == 15)),
                                  reads=[d_mT[mi], d_wo[c]], writes=[dps[pbank]])
                        kb.op("dve", lambda e: e.tensor_tensor(out=ot[i][:, c * 512:(c + 1) * 512], in0=ps[pbank][:],
                                                               in1=xr[i][:, c * 512:(c + 1) * 512], op=ALU.add),
                              reads=[dps[pbank], d_xr[i]], writes=[d_ot[i]])
                    kb.dma("pool", dst[g * 128:(g + 1) * 128, :], ot[i][:], reads=[d_ot[i]])

    def sincos_turns(self, turns, cos_out, sin_out, tmpf, tmpi, tmpf2, dT, dC, dS, dtmp):
        kb = self.kb
        TWO_PI = 6.283185
        kb.op("dve", lambda e: e.tensor_copy(out=tmpi, in_=turns), reads=[dT], writes=[dtmp])
        kb.op("dve", lambda e: e.tensor_tensor(out=tmpf, in0=turns, in1=tmpi, op=ALU.subtract), reads=[dT, dtmp], writes=[dtmp])
        kb.op("act", lambda e: e.activation(out=sin_out, in_=tmpf, func=AF.Sin, scale=TWO_PI), reads=[dtmp], writes=[dS])
        kb.op("dve", lambda e: e.tensor_scalar(out=tmpf2, in0=tmpf, scalar1=0.25, scalar2=None, op0=ALU.add), reads=[dtmp], writes=[dtmp])
        kb.op("dve", lambda e: e.scalar_tensor_tensor(out=tmpf2, in0=tmpf2, scalar=0.5, in1=tmpf2, op0=ALU.is_gt, op1=ALU.subtract),
              reads=[dtmp], writes=[dtmp])
        kb.op("act", lambda e: e.activation(out=cos_out, in_=tmpf2, func=AF.Sin, scale=-TWO_PI), reads=[dtmp], writes=[dC])

    def phase_S5_v1(self, l):
        nc, kb = self.nc, self.kb
        ps, dps = self.ps, self.dps
        with ExitStack() as st:
            sb = lambda n, s, d: st.enter_context(self.sbt("S_" + n, s, d))
            BT = [sb("BT%d" % i, [128, 32, 128], BF16) for i in range(2)]
            CT = [sb("CT%d" % i, [128, 32, 128], BF16) for i in range(2)]
            prm = sb("prm", [128, 24, 32], F32)
            prmi = sb("prmi", [128, 32], I32)
            Dt = sb("Dt", [128, 8], F32)
            gluw = sb("gluw", [128, 8, 1024], BF16)
            d_BT, d_CT, d_prm, d_Dt, d_glu = kb.deps_n(5, "s5c")
            AR, AI, LDT, DTT, MM, PHI, COS, SIN, FR, FI, C512, S512, T0, T1, T2, T3, T4, T5 = range(18)
            P = lambda i: prm[:, i, :]
            kb.dma("pool", gluw[:], self.glu_w[l].rearrange("(k p) n -> p k n", p=128), writes=[d_glu])
            with ExitStack() as st2:
                sb2 = lambda n, s, d: st2.enter_context(self.sbt("S2_" + n, s, d))
                XA = sb2("XA", [32, 3, 128], F32)
                ld2 = sb2("ld2", [32, 2], F32)
                XD = sb2("XD", [8, 128], F32)
                pads = [sb2("pad%d" % i, [128, 32, 128], F32) for i in range(4)]
                d_XA, d_ld2, d_XD = kb.deps_n(3, "xa")
                d_pad = kb.deps_n(4, "pad")
                kb.dma("sp", XA[:, 0, :], self.a_re[l].rearrange("(q gl) p -> q (gl p)", gl=2), writes=[d_XA])
                kb.dma("sp", XA[:, 1, :], self.a_im[l].rearrange("(q gl) p -> q (gl p)", gl=2), writes=[d_XA])
                kb.dma("sp", ld2[:], self.log_dt[l:l + 1, :].rearrange("o (q gl) -> (o q) gl", gl=2), writes=[d_ld2])
                kb.dma("sp", XD[:], self.s5_d[l:l + 1, :].rearrange("o (c p) -> (o c) p", p=128), writes=[d_XD])
                kb.op("dve", lambda e: e.tensor_copy(out=XA[:, 2, :].rearrange("q (gl p) -> q gl p", gl=2),
                                                     in_=ld2[:].unsqueeze(2).to_broadcast([32, 2, 64])),
                      reads=[d_ld2, d_XA], writes=[d_XA])
                for i in range(4):
                    eng = "dve" if i % 2 == 0 else "pool"
                    kb.op(eng, lambda e: e.memset(pads[i][:].rearrange("p q c -> p (q c)"), 0.0), writes=[d_pad[i]])
                srcB = [self.b_re[l], self.b_im[l]]
                srcC = [self.c_re[l], self.c_im[l]]
                for k in range(4):
                    for gl in range(2):
                        for i in range(2):
                            dstb = pads[i][gl * 64:(gl + 1) * 64, :, :].rearrange("p (ct k) c -> p k ct c", k=4)[:, k, :, 32 * k + 16 * gl:32 * k + 16 * gl + 16]
                            sb_ = srcB[i].rearrange("(ct k gl) p c -> k gl p ct c", k=4, gl=2)[k, gl]
                            kb.dma("sp", dstb, sb_, reads=[d_pad[i]], writes=[d_pad[i]])
                            dstc = pads[2 + i][32 * k + 16 * gl:32 * k + 16 * gl + 16, :, :].rearrange("p (ct k) c -> p k ct c", k=4)[:, k, :, gl * 64:(gl + 1) * 64]
                            sc_ = srcC[i].rearrange("(ct k gl) c p -> k gl c ct p", k=4, gl=2)[k, gl]
                            kb.dma("sp", dstc, sc_, reads=[d_pad[2 + i]], writes=[d_pad[2 + i]])
                for j in range(3):
                    kb.op("pe", lambda e: e.transpose(out=ps[0][:, j * 32:(j + 1) * 32], in_=XA[:, j, :], identity=self.identf[0:32, 0:32]),
                          reads=[d_XA, self.d_const], writes=[dps[0]])
                kb.op("pe", lambda e: e.transpose(out=ps[0][:, 96:104], in_=XD[:], identity=self.identf[0:8, 0:8]),
                      reads=[d_XD, self.d_const], writes=[dps[0]])
                kb.op("dve", lambda e: e.tensor_copy(out=prm[:, 0:3, :].rearrange("p a q -> p (a q)"), in_=ps[0][:, 0:96]), reads=[dps[0]], writes=[d_prm])
                kb.op("dve", lambda e: e.tensor_copy(out=Dt[:], in_=ps[0][:, 96:104]), reads=[dps[0]], writes=[d_Dt])
                R_, W_ = [d_prm], [d_prm]
                tt = lambda o, a, b, op: kb.op("dve", lambda e: e.tensor_tensor(out=P(o), in0=P(a), in1=P(b), op=op), reads=R_, writes=W_)
                kb.op("act", lambda e: e.activation(out=P(DTT), in_=P(LDT), func=AF.Exp), reads=R_, writes=W_)
                tt(T0, DTT, AR, ALU.mult)
                kb.op("act", lambda e: e.activation(out=P(MM), in_=P(T0), func=AF.Exp), reads=R_, writes=W_)
                tt(T0, DTT, AI, ALU.mult)
                kb.op("dve", lambda e: e.tensor_scalar(out=P(T1), in0=P(T0), scalar1=1.0 / (2.0 * math.pi), scalar2=None, op0=ALU.mult), reads=R_, writes=W_)
                kb.op("dve", lambda e: e.tensor_copy(out=prmi[:], in_=P(T1)), reads=R_, writes=W_)
                kb.op("dve", lambda e: e.tensor_tensor(out=P(PHI), in0=P(T1), in1=prmi[:], op=ALU.subtract), reads=R_, writes=W_)
                self.sincos_turns(P(PHI), P(COS), P(SIN), P(T2), prmi[:], P(T3), d_prm, d_prm, d_prm, d_prm)
                kb.op("dve", lambda e: e.tensor_scalar(out=P(T4), in0=P(PHI), scalar1=512.0, scalar2=None, op0=ALU.mult), reads=R_, writes=W_)
                self.sincos_turns(P(T4), P(C512), P(S512), P(T2), prmi[:], P(T3), d_prm, d_prm, d_prm, d_prm)
                tt(T0, MM, COS, ALU.mult)
                tt(T1, MM, SIN, ALU.mult)
                kb.op("dve", lambda e: e.tensor_scalar(out=P(T0), in0=P(T0), scalar1=-1.0, scalar2=None, op0=ALU.add), reads=R_, writes=W_)
                tt(T2, AR, AR, ALU.mult)
                tt(T3, AI, AI, ALU.mult)
                tt(T2, T2, T3, ALU.add)
                kb.op("dve", lambda e: e.reciprocal(out=P(T2), in_=P(T2)), reads=R_, writes=W_)
                tt(T3, T0, AR, ALU.mult)
                tt(T4, T1, AI, ALU.mult)
                tt(T3, T3, T4, ALU.add)
                tt(FR, T3, T2, ALU.mult)
                tt(T3, T1, AR, ALU.mult)
                tt(T4, T0, AI, ALU.mult)
                tt(T3, T3, T4, ALU.subtract)
                tt(FI, T3, T2, ALU.mult)
                ctmp = [sb2("ctmp%d" % i, [128, 4, 128], F32) for i in range(4)]
                d_ctmp = kb.dep("ctmp")
                for q4 in range(8):
                    for i in range(4):
                        bank = 4 + i
                        for k in range(4):
                            q = q4 * 4 + k
                            kb.op("pe", lambda e: e.transpose(out=ps[bank][:, k * 128:(k + 1) * 128], in_=pads[i][:, q, :], identity=self.identf[:]),
                                  reads=[d_pad[i], self.d_const], writes=[dps[bank]])
                    for i in range(2):
                        kb.op("act", lambda e: e.activation(out=BT[i][:, q4 * 4:(q4 + 1) * 4, :].rearrange("p a b -> p (a b)"), in_=ps[4 + i][:], func=AF.Copy),
                              reads=[dps[4 + i]], writes=[d_BT])
                    frb = prm[:, FR, q4 * 4:(q4 + 1) * 4].unsqueeze(2).to_broadcast([128, 4, 128])
                    fib = prm[:, FI, q4 * 4:(q4 + 1) * 4].unsqueeze(2).to_broadcast([128, 4, 128])
                    crp = ps[6][:].rearrange("p (a b) -> p a b", a=4)
                    cip = ps[7][:].rearrange("p (a b) -> p a b", a=4)
                    tmpw = [d_ctmp]
                    kb.op("dve", lambda e: e.tensor_tensor(out=ctmp[0][:], in0=crp, in1=frb, op=ALU.mult), reads=[dps[6], d_prm], writes=tmpw)
                    kb.op("dve", lambda e: e.tensor_tensor(out=ctmp[1][:], in0=cip, in1=fib, op=ALU.mult), reads=[dps[7], d_prm], writes=tmpw)
                    kb.op("dve", lambda e: e.tensor_tensor(out=CT[0][:, q4 * 4:(q4 + 1) * 4, :], in0=ctmp[0][:], in1=ctmp[1][:], op=ALU.subtract),
                          reads=tmpw, writes=[d_CT])
                    kb.op("dve", lambda e: e.tensor_tensor(out=ctmp[2][:], in0=crp, in1=fib, op=ALU.mult), reads=[dps[6], d_prm], writes=tmpw)
                    kb.op("dve", lambda e: e.tensor_tensor(out=ctmp[3][:], in0=cip, in1=frb, op=ALU.mult), reads=[dps[7], d_prm], writes=tmpw)
                    kb.op("dve", lambda e: e.scalar_tensor_tensor(out=CT[1][:, q4 * 4:(q4 + 1) * 4, :], in0=ctmp[2][:], scalar=-1.0, in1=ctmp[3][:],
                                                                  op0=ALU.mult, op1=ALU.subtract), reads=tmpw, writes=[d_CT])
                kb.barrier()
            y5T = sb("y5T", [128, 8, S], BF16)
            cosT = [sb("cosT%d" % i, [128, 4, 512], BF16) for i in range(2)]
            sinT = [sb("sinT%d" % i, [128, 4, 512], BF16) for i in range(2)]
            iota = sb("iota", [128, 512], F32)
            angi = sb("angi", [128, 512], I32)
            uT = [sb("uT%d" % i, [128, 512], BF16) for i in range(3)]
            tf = [sb("tf%d" % i, [128, 512], F32) for i in range(3)]
            tb_ = [sb("tb%d" % i, [128, 512], BF16) for i in range(6)]
            rb_ = [sb("rb%d" % i, [128, 512], BF16) for i in range(4)]
            BuS = [[sb("BuS%d%d" % (i, j), [128, 512], BF16) for j in range(2)] for i in range(2)]
            zf = [[sb("zf%d%d" % (i, j), [128, 512], F32) for j in range(2)] for i in range(2)]
            zb = [[sb("zb%d%d" % (i, j), [128, 512], BF16) for j in range(2)] for i in range(2)]
            X = [[sb("X%d%d" % (i, j), [128, 512], BF16) for j in range(2)] for i in range(2)]
            cz = [sb("cz%d" % i, [128, 2, 32], F32) for i in range(2)]
            czt = sb("czt", [128, 2], F32)
            d_y5 = kb.deps_n(8, "y5")
            d_cos = kb.deps_n(2, "cos")
            d_sin = kb.deps_n(2, "sin")
            d_iota, d_angi, d_czt = kb.deps_n(3, "tab")
            d_uT = kb.deps_n(3, "uT")
            d_tf = kb.deps_n(3, "tf")
            d_tb = kb.deps_n(6, "tb")
            d_rb = kb.deps_n(4, "rb")
            d_BuS = [kb.deps_n(2) for i in range(2)]
            d_zf = [kb.deps_n(2) for i in range(2)]
            d_zb = [kb.deps_n(2) for i in range(2)]
            d_X = [kb.deps_n(2) for i in range(2)]
            d_cz = kb.deps_n(2, "cz")
            kb.dma("sp", iota[:], self.c_iota, writes=[d_iota])
            t1, t2, t3, t4, wr, wi = tb_
            dt1, dt2, dt3, dt4, dwr, dwi = d_tb
            r1, r2, r3, r4 = rb_
            dr1, dr2, dr3, dr4 = d_rb

            def TT(eng, o, do, a, da, b_, db, op):
                kb.op(eng, lambda e: e.tensor_tensor(out=o, in0=a, in1=b_, op=op), reads=da + db, writes=[do])

            pending = []
            it = 0
            icb = 0
            for ct in range(8):
                tbi = ct % 2
                for k in range(4):
                    q = ct * 4 + k
                    kb.op("dve", lambda e: e.tensor_scalar(out=tf[0][:], in0=iota[:], scalar1=prm[:, PHI, q:q + 1], scalar2=None, op0=ALU.mult),
                          reads=[d_iota, d_prm], writes=[d_tf[0]])
                    self.sincos_turns(tf[0][:], cosT[tbi][:, k, :], sinT[tbi][:, k, :], tf[1][:], angi[:], tf[2][:], d_tf[0], d_cos[tbi], d_sin[tbi], d_tf[1])
                kb.op("dve", lambda e: e.memset(cz[0][:].rearrange("p a q -> p (a q)"), 0.0), writes=[d_cz[0]])
                for tb in range(8):
                    ui = icb % 3
                    ybank = 4 + (icb % 2)
                    icb += 1
                    par = tb % 2
                    kb.dma("sp", uT[ui][:], self.sT[ct * 128:(ct + 1) * 128, tb * 512:(tb + 1) * 512], writes=[d_uT[ui]])
                    for k in range(4):
                        q = ct * 4 + k
                        sset = it % 2
                        it += 1
                        c = cosT[tbi][:, k, :]
                        s_ = sinT[tbi][:, k, :]
                        dc, ds = [d_cos[tbi]], [d_sin[tbi]]
                        for i in range(2):
                            kb.op("pe", lambda e: e.matmul(ps[2 * sset + i][:], lhsT=BT[i][:, q, :], rhs=uT[ui][:], start=True, stop=True),
                                  reads=[d_BT, d_uT[ui]], writes=[dps[2 * sset + i]])
                            kb.op("act", lambda e: e.activation(out=BuS[sset][i][:], in_=ps[2 * sset + i][:], func=AF.Copy),
                                  reads=[dps[2 * sset + i]], writes=[d_BuS[sset][i]])
                        Br, Bi = BuS[sset][0][:], BuS[sset][1][:]
                        dBr, dBi = [d_BuS[sset][0]], [d_BuS[sset][1]]
                        TT("dve", t1[:], dt1, Br, dBr, c, dc, ALU.mult)
                        TT("dve", t2[:], dt2, Bi, dBi, s_, ds, ALU.mult)
                        TT("dve", wr[:], dwr, t1[:], [dt1], t2[:], [dt2], ALU.add)
                        TT("dve", t3[:], dt3, Bi, dBi, c, dc, ALU.mult)
                        TT("dve", t4[:], dt4, Br, dBr, s_, ds, ALU.mult)
                        TT("dve", wi[:], dwi, t3[:], [dt3], t4[:], [dt4], ALU.subtract)
                        mb = prm[:, MM, q:q + 1].to_broadcast([128, 512])
                        zr, zi = zf[sset][0], zf[sset][1]
                        dzr, dzi = d_zf[sset][0], d_zf[sset][1]
                        kb.op("dve", lambda e: e.tensor_tensor_scan(out=zr[:], data0=mb, data1=wr[:], initial=cz[par][:, 0, q:q + 1], op0=ALU.mult, op1=ALU.add),
                              reads=[d_prm, dwr, d_cz[par]], writes=[dzr])
                        kb.op("dve", lambda e: e.tensor_tensor_scan(out=zi[:], data0=mb, data1=wi[:], initial=cz[par][:, 1, q:q + 1], op0=ALU.mult, op1=ALU.add),
                              reads=[d_prm, dwi, d_cz[par]], writes=[dzi])
                        for i in range(2):
                            kb.op("act", lambda e: e.activation(out=zb[sset][i][:], in_=zf[sset][i][:], func=AF.Copy),
                                  reads=[d_zf[sset][i]], writes=[d_zb[sset][i]])
                        zr_l, zi_l = zr[:, 511:512], zi[:, 511:512]
                        c5, s5 = prm[:, C512, q:q + 1], prm[:, S512, q:q + 1]
                        nx = 1 - par
                        kb.op("dve", lambda e: e.tensor_scalar(out=czt[:, 0:1], in0=zi_l, scalar1=s5, scalar2=None, op0=ALU.mult), reads=[dzi, d_prm], writes=[d_czt])
                        kb.op("dve", lambda e: e.scalar_tensor_tensor(out=cz[nx][:, 0, q:q + 1], in0=zr_l, scalar=c5, in1=czt[:, 0:1], op0=ALU.mult, op1=ALU.subtract),
                              reads=[dzr, d_prm, d_czt], writes=[d_cz[nx]])
                        kb.op("dve", lambda e: e.tensor_scalar(out=czt[:, 1:2], in0=zi_l, scalar1=c5, scalar2=None, op0=ALU.mult), reads=[dzi, d_prm], writes=[d_czt])
                        kb.op("dve", lambda e: e.scalar_tensor_tensor(out=cz[nx][:, 1, q:q + 1], in0=zr_l, scalar=s5, in1=czt[:, 1:2], op0=ALU.mult, op1=ALU.add),
                              reads=[dzr, d_prm, d_czt], writes=[d_cz[nx]])

                        def back(sset=sset, c=c, s_=s_, dc=dc, ds=ds, q=q, k=k, ct=ct, tb=tb, ui=ui, ybank=ybank):
                            zbr, zbi = zb[sset][0][:], zb[sset][1][:]
                            dzbr, dzbi = [d_zb[sset][0]], [d_zb[sset][1]]
                            TT("pool", r1[:], dr1, zbr, dzbr, c, dc, ALU.mult)
                            TT("pool", r2[:], dr2, zbi, dzbi, s_, ds, ALU.mult)
                            TT("pool", X[sset][0][:], d_X[sset][0], r1[:], [dr1], r2[:], [dr2], ALU.subtract)
                            TT("dve", r3[:], dr3, zbr, dzbr, s_, ds, ALU.mult)
                            TT("dve", r4[:], dr4, zbi, dzbi, c, dc, ALU.mult)
                            TT("dve", X[sset][1][:], d_X[sset][1], r3[:], [dr3], r4[:], [dr4], ALU.add)
                            for i in range(2):
                                kb.op("pe", lambda e: e.matmul(ps[ybank][:], lhsT=CT[i][:, q, :], rhs=X[sset][i][:], start=(k == 0 and i == 0), stop=(k == 3 and i == 1)),
                                      reads=[d_CT, d_X[sset][i]], writes=[dps[ybank]])
                            if k == 3:
                                kb.op("dve", lambda e: e.scalar_tensor_tensor(out=tf[0][:], in0=uT[ui][:], scalar=Dt[:, ct:ct + 1], in1=ps[ybank][:], op0=ALU.mult, op1=ALU.add),
                                      reads=[d_uT[ui], d_Dt, dps[ybank]], writes=[d_tf[0]])
                                kb.op("act", lambda e: e.activation(out=tf[1][:], in_=tf[0][:], func=AF.Square), reads=[d_tf[0]], writes=[d_tf[1]])
                                kb.op("pool", lambda e: e.tensor_scalar(out=tf[1][:], in0=tf[1][:], scalar1=0.044715, scalar2=1.0, op0=ALU.mult, op1=ALU.add),
                                      reads=[d_tf[1]], writes=[d_tf[1]])
                                kb.op("pool", lambda e: e.tensor_tensor(out=tf[1][:], in0=tf[1][:], in1=tf[0][:], op=ALU.mult), reads=[d_tf[1], d_tf[0]], writes=[d_tf[1]])
                                kb.op("act", lambda e: e.activation(out=tf[2][:], in_=tf[1][:], func=AF.Sigmoid, scale=1.5957691216057308), reads=[d_tf[1]], writes=[d_tf[2]])
                                kb.op("pool", lambda e: e.tensor_tensor(out=y5T[:, ct, tb * 512:(tb + 1) * 512], in0=tf[0][:], in1=tf[2][:], op=ALU.mult),
                                      reads=[d_tf[0], d_tf[2]], writes=[d_y5[tb]])

                        if pending:
                            pending.pop(0)()
                        pending.append(back)
            while pending:
                pending.pop(0)()
            szT = [sb("szT%d" % i, [128, 512], BF16) for i in range(2)]
            og = [sb("og%d" % i, [128, 512], BF16) for i in range(2)]
            d_sz = kb.deps_n(2, "sz")
            d_og = kb.deps_n(2, "og")
            ig = 0
            for tb in range(8):
                for co in range(8):
                    i = ig % 2
                    ig += 1
                    bank = 4 + (ig % 4)
                    kb.dma("sp", szT[i][:], self.sT[1024 + co * 128:1024 + (co + 1) * 128, tb * 512:(tb + 1) * 512], writes=[d_sz[i]])
                    for ci in range(8):
                        kb.op("pe", lambda e: e.matmul(ps[bank][:], lhsT=gluw[:, ci, co * 128:(co + 1) * 128], rhs=y5T[:, ci, tb * 512:(tb + 1) * 512],
                                                       start=(ci == 0), stop=(ci == 7)), reads=[d_glu, d_y5[tb]], writes=[dps[bank]])
                    kb.op("act", lambda e: e.activation(out=tf[0][:], in_=ps[bank][:], func=AF.Sigmoid), reads=[dps[bank]], writes=[d_tf[0]])
                    kb.op("act", lambda e: e.activation(out=tf[1][:], in_=szT[i][:], func=AF.Silu), reads=[d_sz[i]], writes=[d_tf[1]])
                    kb.op("dve", lambda e: e.tensor_tensor(out=tf[0][:], in0=tf[0][:], in1=y5T[:, co, tb * 512:(tb + 1) * 512], op=ALU.mult),
                          reads=[d_tf[0], d_y5[tb]], writes=[d_tf[0]])
                    kb.op("dve", lambda e: e.tensor_tensor(out=og[i][:], in0=tf[0][:], in1=tf[1][:], op=ALU.mult), reads=[d_tf[0], d_tf[1]], writes=[d_og[i]])
                    kb.dma("sp", self.mixedT[1024 + co * 128:1024 + (co + 1) * 128, tb * 512:(tb + 1) * 512], og[i][:], reads=[d_og[i]])

    def phase_S5(self, l):
        nc, kb = self.nc, self.kb
        ps, dps = self.ps, self.dps
        Lc = 4
        NCH = S // Lc
        NH = NCH // 512
        with ExitStack() as st:
            sb = lambda n, s, d: st.enter_context(self.sbt("S_" + n, s, d))
            CT = [sb("CT%d" % i, [128, 32, 128], BF16) for i in range(2)]
            Bp = [sb("Bp%d" % i, [128, 32, 128], BF16) for i in range(2)]
            prm = sb("prm", [128, 24, 32], F32)
            apw = sb("apw", [128, 9, 2, 32], F32)
            prmi = sb("prmi", [128, 32], I32)
            Dt = sb("Dt", [128, 8], F32)
            d_CT, d_prm, d_Dt, d_apw = kb.deps_n(4, "s5c")
            d_Bp = kb.deps_n(2, "Bp")
            AR, AI, LDT, DTT, MM, PHI, COS, SIN, FR, FI, M8, PHI8, C512, S512, T0, T1, T2, T3, T4, T5 = range(20)
            P = lambda i: prm[:, i, :]
            with ExitStack() as st2:
                sb2 = lambda n, s, d: st2.enter_context(self.sbt("S2_" + n, s, d))
                XA = sb2("XA", [32, 3, 128], F32)
                ld2 = sb2("ld2", [32, 2], F32)
                XD = sb2("XD", [8, 128], F32)
                Cp = [sb2("Cp%d" % i, [128, 32, 128], F32) for i in range(2)]
                Bf = [sb2("Bf%d" % i, [128, 32, 128], F32) for i in range(2)]
                pads = [Bf[0], Bf[1], Cp[0], Cp[1]]
                d_XA, d_ld2, d_XD = kb.deps_n(3, "xa")
                d_Cp = kb.deps_n(2, "Cp")
                d_Bf = kb.deps_n(2, "Bf")
                d_pad = [d_Bf[0], d_Bf[1], d_Cp[0], d_Cp[1]]
                kb.dma("sp", XA[:, 0, :], self.a_re[l].rearrange("(q gl) p -> q (gl p)", gl=2), writes=[d_XA])
                kb.dma("sp", XA[:, 1, :], self.a_im[l].rearrange("(q gl) p -> q (gl p)", gl=2), writes=[d_XA])
                kb.dma("sp", ld2[:], self.log_dt[l:l + 1, :].rearrange("o (q gl) -> (o q) gl", gl=2), writes=[d_ld2])
                kb.dma("sp", XD[:], self.s5_d[l:l + 1, :].rearrange("o (c p) -> (o c) p", p=128), writes=[d_XD])
                kb.op("dve", lambda e: e.tensor_copy(out=XA[:, 2, :].rearrange("q (gl p) -> q gl p", gl=2),
                                                     in_=ld2[:].unsqueeze(2).to_broadcast([32, 2, 64])),
                      reads=[d_ld2, d_XA], writes=[d_XA])
                for i in range(4):
                    eng = "dve" if i % 2 == 0 else "pool"
                    kb.op(eng, lambda e: e.memset(pads[i][:].rearrange("p q c -> p (q c)"), 0.0), writes=[d_pad[i]])
                srcB = [self.b_re[l], self.b_im[l]]
                srcC = [self.c_re[l], self.c_im[l]]
                for k in range(4):
                    for gl in range(2):
                        for i in range(2):
                            dstb = pads[i][gl * 64:(gl + 1) * 64, :, :].rearrange("p (ct k) c -> p k ct c", k=4)[:, k, :, 32 * k + 16 * gl:32 * k + 16 * gl + 16]
                            sb_ = srcB[i].rearrange("(ct k gl) p c -> k gl p ct c", k=4, gl=2)[k, gl]
                            kb.dma("sp", dstb, sb_, reads=[d_pad[i]], writes=[d_pad[i]])
                            dstc = pads[2 + i][32 * k + 16 * gl:32 * k + 16 * gl + 16, :, :].rearrange("p (ct k) c -> p k ct c", k=4)[:, k, :, gl * 64:(gl + 1) * 64]
                            sc_ = srcC[i].rearrange("(ct k gl) c p -> k gl c ct p", k=4, gl=2)[k, gl]
                            kb.dma("sp", dstc, sc_, reads=[d_pad[2 + i]], writes=[d_pad[2 + i]])
                for j in range(3):
                    kb.op("pe", lambda e: e.transpose(out=ps[0][:, j * 32:(j + 1) * 32], in_=XA[:, j, :], identity=self.identf[0:32, 0:32]),
                          reads=[d_XA, self.d_const], writes=[dps[0]])
                kb.op("pe", lambda e: e.transpose(out=ps[0][:, 96:104], in_=XD[:], identity=self.identf[0:8, 0:8]),
                      reads=[d_XD, self.d_const], writes=[dps[0]])
                kb.op("dve", lambda e: e.tensor_copy(out=prm[:, 0:3, :].rearrange("p a q -> p (a q)"), in_=ps[0][:, 0:96]), reads=[dps[0]], writes=[d_prm])
                kb.op("dve", lambda e: e.tensor_copy(out=Dt[:], in_=ps[0][:, 96:104]), reads=[dps[0]], writes=[d_Dt])
                kb.op("act", lambda e: e.activation(out=Bp[0][:].rearrange("p q c -> p (q c)"), in_=Bf[0][:].rearrange("p q c -> p (q c)"), func=AF.Copy),
                      reads=[d_Bf[0]], writes=[d_Bp[0]])
                kb.op("pool", lambda e: e.tensor_copy(out=Bp[1][:].rearrange("p q c -> p (q c)"), in_=Bf[1][:].rearrange("p q c -> p (q c)")),
                      reads=[d_Bf[1]], writes=[d_Bp[1]])
                R_, W_ = [d_prm], [d_prm]
                tt = lambda o, a, b, op: kb.op("dve", lambda e: e.tensor_tensor(out=P(o), in0=P(a), in1=P(b), op=op), reads=R_, writes=W_)
                kb.op("act", lambda e: e.activation(out=P(DTT), in_=P(LDT), func=AF.Exp), reads=R_, writes=W_)
                tt(T0, DTT, AR, ALU.mult)
                kb.op("act", lambda e: e.activation(out=P(MM), in_=P(T0), func=AF.Exp), reads=R_, writes=W_)
                kb.op("act", lambda e: e.activation(out=P(M8), in_=P(T0), func=AF.Exp, scale=float(Lc)), reads=R_, writes=W_)
                tt(T0, DTT, AI, ALU.mult)
                kb.op("dve", lambda e: e.tensor_scalar(out=P(T1), in0=P(T0), scalar1=1.0 / (2.0 * math.pi), scalar2=None, op0=ALU.mult), reads=R_, writes=W_)
                kb.op("dve", lambda e: e.tensor_copy(out=prmi[:], in_=P(T1)), reads=R_, writes=W_)
                kb.op("dve", lambda e: e.tensor_tensor(out=P(PHI), in0=P(T1), in1=prmi[:], op=ALU.subtract), reads=R_, writes=W_)
                self.sincos_turns(P(PHI), P(COS), P(SIN), P(T2), prmi[:], P(T3), d_prm, d_prm, d_prm, d_prm)
                kb.op("dve", lambda e: e.tensor_scalar(out=P(T4), in0=P(PHI), scalar1=float(Lc), scalar2=None, op0=ALU.mult), reads=R_, writes=W_)
                kb.op("dve", lambda e: e.tensor_copy(out=prmi[:], in_=P(T4)), reads=R_, writes=W_)
                kb.op("dve", lambda e: e.tensor_tensor(out=P(PHI8), in0=P(T4), in1=prmi[:], op=ALU.subtract), reads=R_, writes=W_)
                kb.op("dve", lambda e: e.tensor_scalar(out=P(T4), in0=P(PHI8), scalar1=512.0, scalar2=None, op0=ALU.mult), reads=R_, writes=W_)
                self.sincos_turns(P(T4), P(C512), P(S512), P(T2), prmi[:], P(T3), d_prm, d_prm, d_prm, d_prm)
                tt(T0, MM, COS, ALU.mult)
                tt(T1, MM, SIN, ALU.mult)
                RW = [d_prm, d_apw]
                kb.op("dve", lambda e: e.memset(apw[:, 0, 0, :], 1.0), reads=RW, writes=[d_apw])
                kb.op("dve", lambda e: e.memset(apw[:, 0, 1, :], 0.0), reads=RW, writes=[d_apw])
                kb.op("dve", lambda e: e.tensor_copy(out=apw[:, 1, 0, :], in_=P(T0)), reads=RW, writes=[d_apw])
                kb.op("dve", lambda e: e.tensor_copy(out=apw[:, 1, 1, :], in_=P(T1)), reads=RW, writes=[d_apw])
                for m in range(1, Lc):
                    ar_, ai_ = apw[:, m, 0, :], apw[:, m, 1, :]
                    kb.op("dve", lambda e: e.tensor_tensor(out=P(T2), in0=ar_, in1=P(T0), op=ALU.mult), reads=RW, writes=W_)
                    kb.op("dve", lambda e: e.tensor_tensor(out=P(T3), in0=ai_, in1=P(T1), op=ALU.mult), reads=RW, writes=W_)
                    kb.op("dve", lambda e: e.tensor_tensor(out=apw[:, m + 1, 0, :], in0=P(T2), in1=P(T3), op=ALU.subtract), reads=RW, writes=[d_apw])
                    kb.op("dve", lambda e: e.tensor_tensor(out=P(T2), in0=ar_, in1=P(T1), op=ALU.mult), reads=RW, writes=W_)
                    kb.op("dve", lambda e: e.tensor_tensor(out=P(T3), in0=ai_, in1=P(T0), op=ALU.mult), reads=RW, writes=W_)
                    kb.op("dve", lambda e: e.tensor_tensor(out=apw[:, m + 1, 1, :], in0=P(T2), in1=P(T3), op=ALU.add), reads=RW, writes=[d_apw])
                kb.op("dve", lambda e: e.tensor_scalar(out=P(T0), in0=P(T0), scalar1=-1.0, scalar2=None, op0=ALU.add), reads=R_, writes=W_)
                tt(T2, AR, AR, ALU.mult)
                tt(T3, AI, AI, ALU.mult)
                tt(T2, T2, T3, ALU.add)
                kb.op("dve", lambda e: e.reciprocal(out=P(T2), in_=P(T2)), reads=R_, writes=W_)
                tt(T3, T0, AR, ALU.mult)
                tt(T4, T1, AI, ALU.mult)
                tt(T3, T3, T4, ALU.add)
                tt(FR, T3, T2, ALU.mult)
                tt(T3, T1, AR, ALU.mult)
                tt(T4, T0, AI, ALU.mult)
                tt(T3, T3, T4, ALU.subtract)
                tt(FI, T3, T2, ALU.mult)
                ctmp = [sb2("ctmp%d" % i, [128, 4, 128], F32) for i in range(4)]
                d_ctmp = kb.dep("ctmp")
                for q4 in range(8):
                    for i in range(2):
                        bank = 6 + i
                        for k in range(4):
                            q = q4 * 4 + k
                            kb.op("pe", lambda e: e.transpose(out=ps[bank][:, k * 128:(k + 1) * 128], in_=Cp[i][:, q, :], identity=self.identf[:]),
                                  reads=[d_Cp[i], self.d_const], writes=[dps[bank]])
                    frb = prm[:, FR, q4 * 4:(q4 + 1) * 4].unsqueeze(2).to_broadcast([128, 4, 128])
                    fib = prm[:, FI, q4 * 4:(q4 + 1) * 4].unsqueeze(2).to_broadcast([128, 4, 128])
                    crp = ps[6][:].rearrange("p (a b) -> p a b", a=4)
                    cip = ps[7][:].rearrange("p (a b) -> p a b", a=4)
                    tmpw = [d_ctmp]
                    kb.op("dve", lambda e: e.tensor_tensor(out=ctmp[0][:], in0=crp, in1=frb, op=ALU.mult), reads=[dps[6], d_prm], writes=tmpw)
                    kb.op("dve", lambda e: e.tensor_tensor(out=ctmp[1][:], in0=cip, in1=fib, op=ALU.mult), reads=[dps[7], d_prm], writes=tmpw)
                    kb.op("dve", lambda e: e.tensor_tensor(out=CT[0][:, q4 * 4:(q4 + 1) * 4, :], in0=ctmp[0][:], in1=ctmp[1][:], op=ALU.subtract),
                          reads=tmpw, writes=[d_CT])
                    kb.op("dve", lambda e: e.tensor_tensor(out=ctmp[2][:], in0=crp, in1=fib, op=ALU.mult), reads=[dps[6], d_prm], writes=tmpw)
                    kb.op("dve", lambda e: e.tensor_tensor(out=ctmp[3][:], in0=cip, in1=frb, op=ALU.mult), reads=[dps[7], d_prm], writes=tmpw)
                    kb.op("dve", lambda e: e.scalar_tensor_tensor(out=CT[1][:, q4 * 4:(q4 + 1) * 4, :], in0=ctmp[2][:], scalar=-1.0, in1=ctmp[3][:],
                                                                  op0=ALU.mult, op1=ALU.subtract), reads=tmpw, writes=[d_CT])
                kb.barrier()
            with ExitStack() as st3:
                sb3 = lambda n, s, d: st3.enter_context(self.sbt("S3_" + n, s, d))
                cosT = [sb3("cosT%d" % i, [128, 4, 512], BF16) for i in range(2)]
                sinT = [sb3("sinT%d" % i, [128, 4, 512], BF16) for i in range(2)]
                iota = sb3("iota", [128, 512], F32)
                angi = sb3("angi", [128, 512], I32)
                zero = sb3("zero", [128, 2], F32)
                czc = [sb3("czc%d" % i, [128, 2, 4], F32) for i in range(2)]
                czt = sb3("czt", [128, 2], F32)
                zl = sb3("zlast", [128, 2], F32)
                d_czc = kb.deps_n(2, "czc")
                d_czt = kb.dep("czt")
                d_zl = kb.dep("zl")
                uTf = [sb3("uTf%d" % i, [128, S], BF16) for i in range(2)]
                W1 = sb3("W1", [128, Lc, 8, 128], BF16)
                CA = [sb3("CA%d" % i, [128, Lc, 8, 128], BF16) for i in range(2)]
                SN = [sb3("SN%d" % i, [128, 8, 128], BF16) for i in range(2)]
                KT = [sb3("KT%d" % i, [128, Lc, 128], BF16) for i in range(2)]
                uu = [sb3("uu%d" % i, [128, 4, 128], F32) for i in range(4)]
                Xs = [[[sb3("Xs%d%d%d" % (c_, k, i), [128, NCH + 2], BF16) for i in range(2)] for k in range(4)] for c_ in range(2)]
                tf = [sb3("tf%d" % i, [128, 512], F32) for i in range(3)]
                tg = [sb3("tg%d" % i, [128, 512], F32) for i in range(3)]
                tb_ = [sb3("tb%d" % i, [128, 512], BF16) for i in range(6)]
                rb_ = [sb3("rb%d" % i, [128, 512], BF16) for i in range(4)]
                BuS = [[sb3("BuS%d%d" % (i, j), [128, 512], BF16) for j in range(2)] for i in range(2)]
                zb = [[sb3("zb%d%d" % (i, j), [128, 512], BF16) for j in range(2)] for i in range(2)]
                y5s = [sb3("y5s%d" % i, [128, 512], BF16) for i in range(2)]
                d_cos = kb.deps_n(2, "cos")
                d_sin = kb.deps_n(2, "sin")
                d_iota, d_angi, d_zero, d_W1 = kb.deps_n(4, "tab")
                d_CA = kb.deps_n(2, "CA")
                d_KT = kb.deps_n(2, "KT")
                d_uTf = kb.deps_n(2, "uTf")
                d_SN = kb.deps_n(2, "SN")
                d_SNr = kb.deps_n(2, "SNr")
                d_CAr = kb.deps_n(2, "CAr")
                d_uu = kb.deps_n(4, "uu")
                d_Xs = [[kb.deps_n(2) for k in range(4)] for c_ in range(2)]
                d_tf = kb.deps_n(3, "tf")
                d_tg = kb.deps_n(3, "tg")
                d_tb = kb.deps_n(6, "tb")
                d_rb = kb.deps_n(4, "rb")
                d_BuS = [kb.deps_n(2) for i in range(2)]
                d_zb = [kb.deps_n(2) for i in range(2)]
                d_y5s = kb.deps_n(2, "y5s")
                kb.dma("sp", iota[:], self.c_iota, writes=[d_iota])
                kb.op("dve", lambda e: e.memset(zero[:], 0.0), writes=[d_zero])
                for c_ in range(2):
                    for k in range(4):
                        for i in range(2):
                            kb.op("pool", lambda e: e.memset(Xs[c_][k][i][:], 0.0), writes=[d_Xs[c_][k][i]])
                t1, t2, t3, t4, wr, wi = tb_
                dt1, dt2, dt3, dt4, dwr, dwi = d_tb
                r1, r2, r3, r4 = rb_
                dr1, dr2, dr3, dr4 = d_rb

                def TT(eng, o, do, a, da, b_, db, op):
                    kb.op(eng, lambda e: e.tensor_tensor(out=o, in0=a, in1=b_, op=op), reads=da + db, writes=[do])

                def E_slice(ct, step):
                    cp = ct % 2
                    q0 = ct * 4
                    if step == 0:
                        kb.dma("sp", uTf[cp][:], self.sT[ct * 128:(ct + 1) * 128, :], writes=[d_uTf[cp]])
                    if step < 4:
                        k = step
                        q = q0 + k
                        kb.op("dve", lambda e: e.tensor_scalar(out=tg[0][:], in0=iota[:], scalar1=prm[:, PHI8, q:q + 1], scalar2=None, op0=ALU.mult),
                              reads=[d_iota, d_prm], writes=[d_tg[0]])
                        self.sincos_turns(tg[0][:], cosT[cp][:, k, :], sinT[cp][:, k, :], tg[1][:], angi[:], tg[2][:], d_tg[0], d_cos[cp], d_sin[cp], d_tg[1])
                    m = step
                    mi = m % 2
                    Brv = Bp[0][:, q0:q0 + 4, :]
                    Biv = Bp[1][:, q0:q0 + 4, :]
                    Arb = apw[:, m, 0, q0:q0 + 4].unsqueeze(2).to_broadcast([128, 4, 128])
                    Aib = apw[:, m, 1, q0:q0 + 4].unsqueeze(2).to_broadcast([128, 4, 128])
                    TT("dve", uu[0][:], d_uu[0], Brv, [d_Bp[0]], Arb, [d_apw], ALU.mult)
                    TT("dve", uu[1][:], d_uu[1], Biv, [d_Bp[1]], Aib, [d_apw], ALU.mult)
                    TT("dve", SN[mi][:, 0:4, :], d_SNr[mi], uu[0][:], [d_uu[0]], uu[1][:], [d_uu[1]], ALU.subtract)
                    TT("pool", uu[2][:], d_uu[2], Biv, [d_Bp[1]], Arb, [d_apw], ALU.mult)
                    TT("pool", uu[3][:], d_uu[3], Brv, [d_Bp[0]], Aib, [d_apw], ALU.mult)
                    TT("pool", SN[mi][:, 4:8, :], d_SN[mi], uu[2][:], [d_uu[2]], uu[3][:], [d_uu[3]], ALU.add)
                    j = step
                    C0v = CT[0][:, q0:q0 + 4, :]
                    C1v = CT[1][:, q0:q0 + 4, :]
                    Arb = apw[:, j + 1, 0, q0:q0 + 4].unsqueeze(2).to_broadcast([128, 4, 128])
                    Aib = apw[:, j + 1, 1, q0:q0 + 4].unsqueeze(2).to_broadcast([128, 4, 128])
                    TT("dve", uu[0][:], d_uu[0], C0v, [d_CT], Arb, [d_apw], ALU.mult)
                    TT("dve", uu[1][:], d_uu[1], C1v, [d_CT], Aib, [d_apw], ALU.mult)
                    TT("dve", CA[cp][:, j, 0:4, :], d_CAr[cp], uu[0][:], [d_uu[0]], uu[1][:], [d_uu[1]], ALU.add)
                    TT("pool", uu[2][:], d_uu[2], C1v, [d_CT], Arb, [d_apw], ALU.mult)
                    TT("pool", uu[3][:], d_uu[3], C0v, [d_CT], Aib, [d_apw], ALU.mult)
                    TT("pool", CA[cp][:, j, 4:8, :], d_CA[cp], uu[2][:], [d_uu[2]], uu[3][:], [d_uu[3]], ALU.subtract)

                def T_slice(ct, step):
                    cp = ct % 2
                    q0 = ct * 4
                    m = step
                    mi = m % 2
                    pb = ps[6][:].bitcast(BF16)
                    for j8 in range(8):
                        kb.op("pe", lambda e: e.transpose(out=pb[:, j8 * 128:(j8 + 1) * 128], in_=SN[mi][:, j8, :], identity=self.identb[:]),
                              reads=[d_SN[mi], d_SNr[mi], self.d_const], writes=[dps[6]])
                    kb.op("act", lambda e: e.activation(out=W1[:, m, :, :].rearrange("p a b -> p (a b)"), in_=pb[:, 0:1024], func=AF.Copy),
                          reads=[dps[6]], writes=[d_W1])
                    ksl = ps[7][:, 0:128]
                    for j8 in range(8):
                        i_, k_ = j8 // 4, j8 % 4
                        kb.op("pe", lambda e: e.matmul(ksl, lhsT=SN[mi][:, j8, :], rhs=CT[i_][:, q0 + k_, :], start=(j8 == 0), stop=(j8 == 7)),
                              reads=[d_SN[mi], d_SNr[mi], d_CT], writes=[dps[7]])
                    kb.op("act", lambda e: e.activation(out=KT[cp][:, m, :], in_=ksl, func=AF.Copy), reads=[dps[7]], writes=[d_KT[cp]])

                it_box = [0]

                def H_stage(ct):
                    cp = ct % 2
                    q0 = ct * 4
                    pending = []
                    for hh in range(NH):
                        for k in range(4):
                            q = q0 + k
                            sset = it_box[0] % 2
                            it_box[0] += 1
                            c = cosT[cp][:, k, :]
                            s_ = sinT[cp][:, k, :]
                            dc, ds = [d_cos[cp]], [d_sin[cp]]
                            for i in range(2):
                                bnk = 2 * sset + i
                                for j in range(Lc):
                                    kb.op("pe", lambda e: e.matmul(ps[bnk][:], lhsT=W1[:, Lc - 1 - j, i * 4 + k, :],
                                                                   rhs=uTf[cp][:, hh * 512 * Lc + j:(hh + 1) * 512 * Lc:Lc], start=(j == 0), stop=(j == Lc - 1)),
                                          reads=[d_W1, d_uTf[cp]], writes=[dps[bnk]])
                                kb.op("act", lambda e: e.activation(out=BuS[sset][i][:], in_=ps[bnk][:], func=AF.Copy), reads=[dps[bnk]], writes=[d_BuS[sset][i]])
                            Br, Bi = BuS[sset][0][:], BuS[sset][1][:]
                            dBr, dBi = [d_BuS[sset][0]], [d_BuS[sset][1]]
                            TT("dve", t1[:], dt1, Br, dBr, c, dc, ALU.mult)
                            TT("dve", t2[:], dt2, Bi, dBi, s_, ds, ALU.mult)
                            TT("dve", wr[:], dwr, t1[:], [dt1], t2[:], [dt2], ALU.add)
                            TT("pool", t3[:], dt3, Bi, dBi, c, dc, ALU.mult)
                            TT("pool", t4[:], dt4, Br, dBr, s_, ds, ALU.mult)
                            TT("pool", wi[:], dwi, t3[:], [dt3], t4[:], [dt4], ALU.subtract)
                            mb = prm[:, M8, q:q + 1].to_broadcast([128, 512])
                            par = hh % 2
                            for i, w_, dw_ in ((0, wr, dwr), (1, wi, dwi)):
                                init = zero[:, i:i + 1] if hh == 0 else czc[par][:, i, k:k + 1]
                                rd = [d_prm, dw_, d_zero] if hh == 0 else [d_prm, dw_, d_czc[par]]
                                kb.op("dve", lambda e: e.tensor_tensor_scan(out=zb[sset][i][:], data0=mb, data1=w_[:], initial=init, op0=ALU.mult, op1=ALU.add),
                                      reads=rd, writes=[d_zb[sset][i]])
                            if hh + 1 < NH:
                                nx = 1 - par
                                kb.op("dve", lambda e: e.tensor_copy(out=zl[:, 0:1], in_=zb[sset][0][:, 511:512]), reads=[d_zb[sset][0]], writes=[d_zl])
                                kb.op("dve", lambda e: e.tensor_copy(out=zl[:, 1:2], in_=zb[sset][1][:, 511:512]), reads=[d_zb[sset][1]], writes=[d_zl])
                                c5, s5 = prm[:, C512, q:q + 1], prm[:, S512, q:q + 1]
                                kb.op("dve", lambda e: e.tensor_scalar(out=czt[:, 0:1], in0=zl[:, 1:2], scalar1=s5, scalar2=None, op0=ALU.mult), reads=[d_zl, d_prm], writes=[d_czt])
                                kb.op("dve", lambda e: e.scalar_tensor_tensor(out=czc[nx][:, 0, k:k + 1], in0=zl[:, 0:1], scalar=c5, in1=czt[:, 0:1], op0=ALU.mult, op1=ALU.subtract),
                                      reads=[d_zl, d_prm, d_czt], writes=[d_czc[nx]])
                                kb.op("dve", lambda e: e.tensor_scalar(out=czt[:, 1:2], in0=zl[:, 1:2], scalar1=c5, scalar2=None, op0=ALU.mult), reads=[d_zl, d_prm], writes=[d_czt])
                                kb.op("dve", lambda e: e.scalar_tensor_tensor(out=czc[nx][:, 1, k:k + 1], in0=zl[:, 0:1], scalar=s5, in1=czt[:, 1:2], op0=ALU.mult, op1=ALU.add),
                                      reads=[d_zl, d_prm, d_czt], writes=[d_czc[nx]])

                            def back(sset=sset, c=c, s_=s_, dc=dc, ds=ds, k=k, hh=hh):
                                zbr, zbi = zb[sset][0][:], zb[sset][1][:]
                                dzbr, dzbi = [d_zb[sset][0]], [d_zb[sset][1]]
                                o0 = 1 + hh * 512
                                TT("pool", r1[:], dr1, zbr, dzbr, c, dc, ALU.mult)
                                TT("pool", r2[:], dr2, zbi, dzbi, s_, ds, ALU.mult)
                                TT("pool", Xs[cp][k][0][:, o0:o0 + 512], d_Xs[cp][k][0], r1[:], [dr1], r2[:], [dr2], ALU.subtract)
                                TT("dve", r3[:], dr3, zbr, dzbr, s_, ds, ALU.mult)
                                TT("dve", r4[:], dr4, zbi, dzbi, c, dc, ALU.mult)
                                TT("dve", Xs[cp][k][1][:, o0:o0 + 512], d_Xs[cp][k][1], r3[:], [dr3], r4[:], [dr4], ALU.add)

                            if pending:
                                pending.pop(0)()
                            pending.append(back)
                    while pending:
                        pending.pop(0)()

                iy_box = [0]

                def Y_mm(ct, blk):
                    cp = ct % 2
                    yb = 4 + (iy_box[0] % 2)
                    for j in range(Lc):
                        osl = ps[yb][:, j:512:Lc]
                        nck = 512 // Lc
                        for tau in range(j + 1):
                            kb.op("pe", lambda e: e.matmul(osl, lhsT=KT[cp][:, tau, :], rhs=uTf[cp][:, blk * 512 + j - tau:blk * 512 + 512:Lc], start=(tau == 0), stop=False),
                                  reads=[d_KT[cp], d_uTf[cp]], writes=[dps[yb]])
                        for k in range(4):
                            for i in range(2):
                                kb.op("pe", lambda e: e.matmul(osl, lhsT=CA[cp][:, j, i * 4 + k, :], rhs=Xs[cp][k][i][:, blk * nck:(blk + 1) * nck], start=False, stop=(k == 3 and i == 1)),
                                      reads=[d_CA[cp], d_CAr[cp], d_Xs[cp][k][i]], writes=[dps[yb]])

                def Y_epi(ct, blk):
                    cp = ct % 2
                    yb = 4 + (iy_box[0] % 2)
                    yi = iy_box[0] % 2
                    iy_box[0] += 1
                    kb.op("dve", lambda e: e.scalar_tensor_tensor(out=tf[0][:], in0=uTf[cp][:, blk * 512:(blk + 1) * 512], scalar=Dt[:, ct:ct + 1], in1=ps[yb][:],
                                                                  op0=ALU.mult, op1=ALU.add), reads=[d_uTf[cp], d_Dt, dps[yb]], writes=[d_tf[0]])
                    kb.op("act", lambda e: e.activation(out=tf[1][:], in_=tf[0][:], func=AF.Square), reads=[d_tf[0]], writes=[d_tf[1]])
                    kb.op("pool", lambda e: e.tensor_scalar(out=tf[1][:], in0=tf[1][:], scalar1=0.044715, scalar2=1.0, op0=ALU.mult, op1=ALU.add),
                          reads=[d_tf[1]], writes=[d_tf[1]])
                    kb.op("dve", lambda e: e.tensor_tensor(out=tf[1][:], in0=tf[1][:], in1=tf[0][:], op=ALU.mult), reads=[d_tf[1], d_tf[0]], writes=[d_tf[1]])
                    kb.op("act", lambda e: e.activation(out=tf[2][:], in_=tf[1][:], func=AF.Sigmoid, scale=1.5957691216057308), reads=[d_tf[1]], writes=[d_tf[2]])
                    kb.op("dve", lambda e: e.tensor_tensor(out=y5s[yi][:], in0=tf[0][:], in1=tf[2][:], op=ALU.mult), reads=[d_tf[0], d_tf[2]], writes=[d_y5s[yi]])
                    kb.dma("sp", self.y5d[ct * 128:(ct + 1) * 128, blk * 512:(blk + 1) * 512], y5s[yi][:], reads=[d_y5s[yi]])

                for step in range(Lc):
                    E_slice(0, step)
                    T_slice(0, step)
                H_stage(0)
                for ct in range(8):
                    nxt = ct + 1 < 8
                    if nxt:
                        E_slice(ct + 1, 0)
                    for blk in range(8):
                        if nxt and blk < Lc:
                            T_slice(ct + 1, blk)
                        Y_mm(ct, blk)
                        if nxt and blk + 1 < Lc:
                            E_slice(ct + 1, blk + 1)
                        Y_epi(ct, blk)
                    if nxt:
                        H_stage(ct + 1)
                kb.barrier()
            with ExitStack() as st4:
                sb4 = lambda n, s, d: st4.enter_context(self.sbt("S4_" + n, s, d))
                gluw = sb4("gluw", [128, 8, 1024], BF16)
                y5b = [sb4("y5b%d" % i, [128, 8, 512], BF16) for i in range(2)]
                NB = 3
                szT = [sb4("szT%d" % i, [128, 512], BF16) for i in range(NB)]
                og = [sb4("og%d" % i, [128, 512], BF16) for i in range(NB)]
                g1 = [sb4("g1%d" % i, [128, 512], BF16) for i in range(NB)]
                g2 = [sb4("g2%d" % i, [128, 512], BF16) for i in range(NB)]
                d_glu = kb.dep("glu")
                d_y5b = kb.deps_n(2, "y5b")
                d_sz = kb.deps_n(NB, "sz")
                d_og = kb.deps_n(NB, "og")
                d_g1 = kb.deps_n(NB, "g1")
                d_g2 = kb.deps_n(NB, "g2")
                kb.dma("pool", gluw[:], self.glu_w[l].rearrange("(k p) n -> p k n", p=128), writes=[d_glu])
                ig = 0
                for tb in range(8):
                    yi = tb % 2
                    kb.dma("sp", y5b[yi][:], self.y5d[:, tb * 512:(tb + 1) * 512].rearrange("(c p) t -> p c t", p=128), writes=[d_y5b[yi]])
                    for co in range(8):
                        i = ig % NB
                        bank = ig % 4
                        ig += 1
                        kb.dma("sp", szT[i][:], self.sT[1024 + co * 128:1024 + (co + 1) * 128, tb * 512:(tb + 1) * 512], writes=[d_sz[i]])
                        for ci in range(8):
                            kb.op("pe", lambda e: e.matmul(ps[bank][:], lhsT=gluw[:, ci, co * 128:(co + 1) * 128], rhs=y5b[yi][:, ci, :],
                                                           start=(ci == 0), stop=(ci == 7)), reads=[d_glu, d_y5b[yi]], writes=[dps[bank]])
                        kb.op("act", lambda e: e.activation(out=g1[i][:], in_=ps[bank][:], func=AF.Sigmoid), reads=[dps[bank]], writes=[d_g1[i]])
                        kb.op("act", lambda e: e.activation(out=g2[i][:], in_=szT[i][:], func=AF.Sigmoid), reads=[d_sz[i]], writes=[d_g2[i]])
                        kb.op("dve", lambda e: e.tensor_tensor(out=g2[i][:], in0=g2[i][:], in1=szT[i][:], op=ALU.mult), reads=[d_g2[i], d_sz[i]], writes=[d_g2[i]])
                        kb.op("dve", lambda e: e.tensor_tensor(out=g1[i][:], in0=g1[i][:], in1=y5b[yi][:, co, :], op=ALU.mult),
                              reads=[d_g1[i], d_y5b[yi]], writes=[d_g1[i]])
                        kb.op("dve", lambda e: e.tensor_tensor(out=og[i][:], in0=g1[i][:], in1=g2[i][:], op=ALU.mult), reads=[d_g1[i], d_g2[i]], writes=[d_og[i]])
                        kb.dma("pool", self.mixedT[1024 + co * 128:1024 + (co + 1) * 128, tb * 512:(tb + 1) * 512], og[i][:], reads=[d_og[i]])

    def qk_prep(self, tag, src, col0, nh, normw_dram, l, dstT, d_dst, rope_tab, d_rope, ntiles=NT, bank=4):
        nc, kb = self.nc, self.kb
        ps, dps = self.ps, self.dps
        G = 4 if ntiles % 4 == 0 else 2
        NBUF = 4
        with ExitStack() as st:
            sb = lambda n, s, d: st.enter_context(self.sbt("P_%s_%s" % (tag, n), s, d))
            W = nh * 128
            GH = G * nh
            nw = sb("nw", [128, 128], F32)
            qraw = [sb("qraw%d" % i, [128, G, W], BF16) for i in range(NBUF)]
            sq = [sb("sq%d" % i, [128, GH, 128], BF16) for i in range(NBUF)]
            ss = [sb("ss%d" % i, [128, GH], F32) for i in range(NBUF)]
            qn = [sb("qn%d" % i, [128, GH, 128], F32) for i in range(NBUF)]
            rt = [sb("rt%d" % i, [128, 4, GH, 16], F32) for i in range(NBUF)]
            qb = [sb("qb%d" % i, [128, GH, 128], BF16) for i in range(NBUF)]
            d_nw = kb.dep()
            d_qraw = kb.deps_n(NBUF)
            d_sq = kb.deps_n(NBUF)
            d_ss = kb.deps_n(NBUF)
            d_qn = kb.deps_n(NBUF)
            d_rt = kb.deps_n(NBUF)
            d_qb = kb.deps_n(NBUF)
            kb.dma("sp", nw[:], normw_dram[l:l + 1, :].partition_broadcast(128), writes=[d_nw])
            def f1(g):
                i = g % NBUF
                t0 = g * G
                kb.dma("sp", qraw[i][:], src[t0 * 128:(t0 + G) * 128, col0:col0 + W].rearrange("(g p) c -> p g c", p=128), writes=[d_qraw[i]])
                qv = qraw[i][:].rearrange("p g (h c) -> p (g h) c", c=128)
                kb.op("pool", lambda e: e.tensor_tensor(out=sq[i][:], in0=qv, in1=qv, op=ALU.mult), reads=[d_qraw[i]], writes=[d_sq[i]])
                kb.op("dve", lambda e: e.tensor_reduce(out=ss[i][:], in_=sq[i][:], axis=AX.X, op=ALU.add), reads=[d_sq[i]], writes=[d_ss[i]])
                kb.op("dve", lambda e: e.tensor_scalar(out=ss[i][:], in0=ss[i][:], scalar1=1.0 / 128, scalar2=EPS, op0=ALU.mult, op1=ALU.add),
                      reads=[d_ss[i]], writes=[d_ss[i]])
                kb.op("act", lambda e: e.activation(out=ss[i][:], in_=ss[i][:], func=AF.Sqrt), reads=[d_ss[i]], writes=[d_ss[i]])
                kb.op("dve", lambda e: e.reciprocal(out=ss[i][:], in_=ss[i][:]), reads=[d_ss[i]], writes=[d_ss[i]])
            def f2(g):
                i = g % NBUF
                t0 = g * G
                qv = qraw[i][:].rearrange("p g (h c) -> p (g h) c", c=128)
                kb.op("dve", lambda e: e.tensor_tensor(out=qn[i][:], in0=qv, in1=ss[i][:].unsqueeze(2).to_broadcast([128, GH, 128]), op=ALU.mult),
                      reads=[d_qraw[i], d_ss[i]], writes=[d_qn[i]])
                kb.op("pool", lambda e: e.tensor_tensor(out=qn[i][:], in0=qn[i][:], in1=nw[:].unsqueeze(1).to_broadcast([128, GH, 128]), op=ALU.mult),
                      reads=[d_qn[i], d_nw], writes=[d_qn[i]])
                cb = rope_tab[:, t0:t0 + G, 0:16].unsqueeze(2).to_broadcast([128, G, nh, 16])
                sbb = rope_tab[:, t0:t0 + G, 16:32].unsqueeze(2).to_broadcast([128, G, nh, 16])
                q4 = qn[i][:].rearrange("p (g h) c -> p g h c", g=G)
                x1 = q4[:, :, :, 0:16]
                x2 = q4[:, :, :, 16:32]
                rv = lambda j: rt[i][:, j].rearrange("p (g h) c -> p g h c", g=G)
                R_ = [d_qn[i], d_rope]
                kb.op("dve", lambda e: e.tensor_tensor(out=rv(0), in0=x1, in1=cb, op=ALU.mult), reads=R_, writes=[d_rt[i]])
                kb.op("dve", lambda e: e.tensor_tensor(out=rv(1), in0=x2, in1=sbb, op=ALU.mult), reads=R_, writes=[d_rt[i]])
                kb.op("dve", lambda e: e.tensor_tensor(out=rv(2), in0=x2, in1=cb, op=ALU.mult), reads=R_, writes=[d_rt[i]])
                kb.op("dve", lambda e: e.tensor_tensor(out=rv(3), in0=x1, in1=sbb, op=ALU.mult), reads=R_, writes=[d_rt[i]])
                kb.op("dve", lambda e: e.tensor_tensor(out=qb[i][:, :, 0:16], in0=rt[i][:, 0], in1=rt[i][:, 1], op=ALU.subtract), reads=[d_rt[i]], writes=[d_qb[i]])
                kb.op("dve", lambda e: e.tensor_tensor(out=qb[i][:, :, 16:32], in0=rt[i][:, 2], in1=rt[i][:, 3], op=ALU.add), reads=[d_rt[i]], writes=[d_qb[i]])
                kb.op("act", lambda e: e.activation(out=qb[i][:, :, 32:128], in_=qn[i][:, :, 32:128], func=AF.Copy), reads=[d_qn[i]], writes=[d_qb[i]])
            def f3(g):
                i = g % NBUF
                t0 = g * G
                nb = (GH + 7) // 8
                for bi in range(nb):
                    bk = bank + ((g * nb + bi) % 4)
                    pb = ps[bk][:].bitcast(BF16)
                    n_here = min(8, GH - bi * 8)
                    for j in range(n_here):
                        kb.op("pe", lambda e: e.transpose(out=pb[:, j * 128:(j + 1) * 128], in_=qb[i][:, bi * 8 + j, :], identity=self.identb[:]),
                              reads=[d_qb[i], self.d_const], writes=[dps[bk]])
                    ng = n_here // nh
                    gt0 = t0 + (bi * 8) // nh
                    dst = dstT[:, :, gt0 * 128:(gt0 + ng) * 128].rearrange("p h (g n) -> p g h n", g=ng)
                    srcp = pb[:, 0:n_here * 128].rearrange("p (g h n) -> p g h n", g=ng, h=nh)
                    eng = "act" if bi % 2 == 0 else "dve"
                    if eng == "act":
                        kb.op("act", lambda e: e.activation(out=dst, in_=srcp, func=AF.Copy), reads=[dps[bk]], writes=[d_dst])
                    else:
                        kb.op("dve", lambda e: e.tensor_copy(out=dst, in_=srcp), reads=[dps[bk]], writes=[d_dst])
            self.emit_pipelined(ntiles // G, [f1, f2, f3])
            kb.barrier()

    def phase_MOBA(self, l):
        nc, kb = self.nc, self.kb
        ps, dps = self.ps, self.dps
        SC = 1.0 / math.sqrt(128.0)
        with ExitStack() as st:
            sb = lambda n, s, d: st.enter_context(self.sbt("M_" + n, s, d))
            QT = sb("QT", [128, 4, S], BF16)
            KT = sb("KT", [128, 4, S], BF16)
            Vp = sb("Vp", [128, NT, 4, 130], BF16)
            rope = sb("rope", [128, NT, 32], F32)
            tri = sb("tri", [128, 128], BF16)
            kmf = sb("kmf", [128, 4, 16], F32)
            kmT = sb("kmT", [128, 4, 16], BF16)
            d_QT, d_KT, d_Vp, d_SEL, d_OM, d_rope, d_tri, d_km = kb.deps_n(8, "mb")
            kb.dma("sp", rope[:], self.c_rope.rearrange("(t p) c -> p t c", p=128), writes=[d_rope])
            kb.dma("sp", tri[:], self.c_tri, writes=[d_tri])
            kb.op("pool", lambda e: e.memset(Vp[:].rearrange("p a b c -> p (a b c)"), 1.0), writes=[d_Vp])
            for h in range(4):
                kb.dma("sp", Vp[:, :, h, 0:128], self.proj_tm[:, C_MV + h * 128:C_MV + (h + 1) * 128].rearrange("(t p) c -> p t c", p=128),
                       reads=[d_Vp], writes=[d_Vp])
            self.qk_prep("mq", self.proj_tm, C_MQ, 4, self.hn["moba_q_norm"], l, QT, d_QT, rope, d_rope)
            self.qk_prep("mk", self.proj_tm, C_MK, 4, self.hn["moba_k_norm"], l, KT, d_KT, rope, d_rope)
            kb.barrier()
            SEL = sb("SEL", [128, NT, 4, 16], F32)
            OM = sb("OM", [128, NT, 512], BF16)
            kb.op("dve", lambda e: e.memset(SEL[:].rearrange("p a b c -> p (a b c)"), 1.0), writes=[d_SEL])
            for h in range(4):
                kb.op("dve", lambda e: e.tensor_reduce(out=kmf[:, h, :], in_=KT[:, h, :].rearrange("p (n k) -> p n k", k=256), axis=AX.X, op=ALU.add),
                      reads=[d_KT], writes=[d_km])
            kb.op("dve", lambda e: e.tensor_scalar(out=kmT[:], in0=kmf[:], scalar1=1.0 / 256, scalar2=None, op0=ALU.mult), reads=[d_km], writes=[d_km])
            with ExitStack() as st2:
                sb2 = lambda n, s, d: st2.enter_context(self.sbt("M2_" + n, s, d))
                gt = [sb2("gt%d" % i, [128, 4, 16], F32) for i in range(2)]
                m8 = [sb2("m8%d" % i, [128, 4, 8], F32) for i in range(2)]
                d_gt = kb.deps_n(2)
                d_m8 = kb.deps_n(2)
                for tt in range(8, NT):
                    own = tt // 2
                    i = tt % 2
                    for h in range(4):
                        kb.op("pe", lambda e: e.matmul(ps[4][:, h * 16:(h + 1) * 16], lhsT=QT[:, h, tt * 128:(tt + 1) * 128], rhs=kmT[:, h, :], start=True, stop=True),
                              reads=[d_QT, d_km], writes=[dps[4]])
                    kb.op("dve", lambda e: e.tensor_copy(out=gt[i][:].rearrange("p a b -> p (a b)"), in_=ps[4][:, 0:64]), reads=[dps[4]], writes=[d_gt[i]])
                    kb.op("dve", lambda e: e.memset(gt[i][:, :, own:16], NEG), reads=[d_gt[i]], writes=[d_gt[i]])
                    for h in range(4):
                        kb.op("dve", lambda e: e.max(out=m8[i][:, h, :], in_=gt[i][:, h, :]), reads=[d_gt[i]], writes=[d_m8[i]])
                    for h in range(4):
                        kb.op("dve", lambda e: e.tensor_scalar(out=SEL[:, tt, h, :], in0=gt[i][:, h, :], scalar1=m8[i][:, h, 2:3], scalar2=None, op0=ALU.is_ge),
                              reads=[d_gt[i], d_m8[i]], writes=[d_SEL])
            kb.barrier()
            PT = [sb("PT%d" % i, [128, 512], BF16) for i in range(5)]
            acc = [sb("acc%d" % i, [128, 2, 130], F32) for i in range(2)]
            rr = [sb("rr%d" % i, [128, 2], F32) for i in range(2)]
            acc1 = [sb("acc1%d" % i, [128, 130], F32) for i in range(2)]
            wtmp = [sb("wtmp%d" % i, [128, 129], F32) for i in range(3)]
            d_acc1 = kb.deps_n(2)
            d_wtmp = kb.deps_n(3)
            d_PT = kb.deps_n(5)
            d_acc = kb.deps_n(2)
            d_rr = kb.deps_n(2)
            iters = [(h, qb, n) for h in range(4) for qb in range(16) for n in range(qb + 1)]

            def front(idx):
                h, qb, n = iters[idx]
                pi = idx % 5
                sbank = (0, 1, 2, 5)[idx % 4]
                for kt in range(2):
                    kb.op("pe", lambda e: e.matmul(ps[sbank][:, kt * 256:(kt + 1) * 256], lhsT=KT[:, h, (2 * n + kt) * 128:(2 * n + kt + 1) * 128],
                                                   rhs=QT[:, h, qb * 256:(qb + 1) * 256], start=True, stop=True),
                          reads=[d_KT, d_QT], writes=[dps[sbank]])
                kb.op("act", lambda e: e.activation(out=PT[pi][:], in_=ps[sbank][:], func=AF.Exp, scale=SC), reads=[dps[sbank]], writes=[d_PT[pi]])
                if n == qb:
                    kb.op("pool", lambda e: e.tensor_tensor(out=PT[pi][:, 0:128], in0=PT[pi][:, 0:128], in1=tri[:], op=ALU.mult),
                          reads=[d_PT[pi], d_tri], writes=[d_PT[pi]])
                    kb.op("pool", lambda e: e.tensor_tensor(out=PT[pi][:, 384:512], in0=PT[pi][:, 384:512], in1=tri[:], op=ALU.mult),
                          reads=[d_PT[pi], d_tri], writes=[d_PT[pi]])

            def back(idx):
                h, qb, n = iters[idx]
                pi = idx % 5
                oA = 3 + (idx % 2)
                oB = 6 + (idx % 2)
                ai = (h * 16 + qb) % 2
                if n == 0:
                    kb.op("pool", lambda e: e.memset(acc[ai][:].rearrange("p a b -> p (a b)"), 0.0), writes=[d_acc[ai]])
                    kb.op("pool", lambda e: e.memset(acc1[ai][:], 0.0), writes=[d_acc1[ai]])
                if n < qb:
                    for qt, ob in ((0, oA), (1, oB)):
                        for kt in range(2):
                            kb.op("pe", lambda e: e.matmul(ps[ob][:, 0:129], lhsT=PT[pi][:, kt * 256 + qt * 128:kt * 256 + (qt + 1) * 128],
                                                           rhs=Vp[:, 2 * n + kt, h, 0:129], start=(kt == 0), stop=(kt == 1)),
                                  reads=[d_PT[pi], d_Vp], writes=[dps[ob]])
                    kb.op("dve", lambda e: e.scalar_tensor_tensor(out=acc[ai][:, 0, 0:129], in0=ps[oA][:, 0:129],
                                                                  scalar=SEL[:, 2 * qb, h, n:n + 1], in1=acc[ai][:, 0, 0:129],
                                                                  op0=ALU.mult, op1=ALU.add),
                          reads=[dps[oA], d_SEL, d_acc[ai]], writes=[d_acc[ai]])
                    if idx % 2 == 0:
                        wi_ = (idx // 2) % 3
                        kb.op("act", lambda e: e.activation(out=wtmp[wi_][:], in_=ps[oB][:, 0:129], func=AF.Copy, scale=SEL[:, 2 * qb + 1, h, n:n + 1]),
                              reads=[dps[oB], d_SEL], writes=[d_wtmp[wi_]])
                        kb.op("pool", lambda e: e.tensor_tensor(out=acc1[ai][:, 0:129], in0=acc1[ai][:, 0:129], in1=wtmp[wi_][:], op=ALU.add),
                              reads=[d_wtmp[wi_], d_acc1[ai]], writes=[d_acc1[ai]])
                    else:
                        kb.op("dve", lambda e: e.scalar_tensor_tensor(out=acc1[ai][:, 0:129], in0=ps[oB][:, 0:129],
                                                                      scalar=SEL[:, 2 * qb + 1, h, n:n + 1], in1=acc1[ai][:, 0:129],
                                                                      op0=ALU.mult, op1=ALU.add),
                              reads=[dps[oB], d_SEL, d_acc1[ai]], writes=[d_acc1[ai]])
                else:
                    kb.op("pe", lambda e: e.matmul(ps[oA][:, 0:129], lhsT=PT[pi][:, 0:128], rhs=Vp[:, 2 * qb, h, 0:129], start=True, stop=True),
                          reads=[d_PT[pi], d_Vp], writes=[dps[oA]])
                    kb.op("pe", lambda e: e.matmul(ps[oB][:, 0:129], lhsT=PT[pi][:, 128:256], rhs=Vp[:, 2 * qb, h, 0:129], start=True, stop=False),
                          reads=[d_PT[pi], d_Vp], writes=[dps[oB]])
                    kb.op("pe", lambda e: e.matmul(ps[oB][:, 0:129], lhsT=PT[pi][:, 384:512], rhs=Vp[:, 2 * qb + 1, h, 0:129], start=False, stop=True),
                          reads=[d_PT[pi], d_Vp], writes=[dps[oB]])
                    kb.op("dve", lambda e: e.tensor_tensor(out=acc[ai][:, 0, 0:129], in0=ps[oA][:, 0:129], in1=acc[ai][:, 0, 0:129], op=ALU.add),
                          reads=[dps[oA], d_acc[ai]], writes=[d_acc[ai]])
                    kb.op("dve", lambda e: e.tensor_tensor(out=acc[ai][:, 1, 0:129], in0=ps[oB][:, 0:129], in1=acc1[ai][:, 0:129], op=ALU.add),
                          reads=[dps[oB], d_acc1[ai], d_acc[ai]], writes=[d_acc[ai]])
                    kb.op("dve", lambda e: e.reciprocal(out=rr[ai][:], in_=acc[ai][:, :, 128]), reads=[d_acc[ai]], writes=[d_rr[ai]])
                    for qt in range(2):
                        kb.op("dve", lambda e: e.tensor_scalar(out=OM[:, 2 * qb + qt, h * 128:(h + 1) * 128], in0=acc[ai][:, qt, 0:128], scalar1=rr[ai][:, qt:qt + 1],
                                                               scalar2=None, op0=ALU.mult), reads=[d_acc[ai], d_rr[ai]], writes=[d_OM])

            SK = 3
            for idx in range(min(SK, len(iters))):
                front(idx)
            for idx in range(len(iters)):
                if idx + SK < len(iters):
                    front(idx + SK)
                back(idx)
            kb.barrier()
            self.gate_and_store(OM, d_OM, C_MZ, 0)

    def gate_and_store(self, OM, d_OM, zcol, row0):
        nc, kb = self.nc, self.kb
        ps, dps = self.ps, self.dps
        with ExitStack() as st:
            sb = lambda n, s, d: st.enter_context(self.sbt("G_" + n, s, d))
            zt = [sb("zt%d" % i, [128, 512], BF16) for i in range(4)]
            sl = [sb("sl%d" % i, [128, 512], F32) for i in range(4)]
            gg = [sb("gg%d" % i, [128, 512], BF16) for i in range(4)]
            oT = [sb("oT%d" % i, [128, 4, 128], BF16) for i in range(4)]
            d_zt = kb.deps_n(4)
            d_sl = kb.deps_n(4)
            d_gg = kb.deps_n(4)
            d_oT = kb.deps_n(4)
            def g1(tt):
                i = tt % 4
                kb.dma("sp", zt[i][:], self.proj_tm[tt * 128:(tt + 1) * 128, zcol:zcol + 512], writes=[d_zt[i]])
                kb.op("act", lambda e: e.activation(out=sl[i][:], in_=zt[i][:], func=AF.Silu), reads=[d_zt[i]], writes=[d_sl[i]])
                kb.op("dve", lambda e: e.tensor_tensor(out=gg[i][:], in0=OM[:, tt, :], in1=sl[i][:], op=ALU.mult), reads=[d_OM, d_sl[i]], writes=[d_gg[i]])
            def g2(tt):
                i = tt % 4
                bk = 4 + i
                pb = ps[bk][:].bitcast(BF16)
                for h in range(4):
                    kb.op("pe", lambda e: e.transpose(out=pb[:, h * 128:(h + 1) * 128], in_=gg[i][:, h * 128:(h + 1) * 128], identity=self.identb[:]),
                          reads=[d_gg[i], self.d_const], writes=[dps[bk]])
                kb.op("act", lambda e: e.activation(out=oT[i][:].rearrange("p h n -> p (h n)"), in_=pb[:, 0:512], func=AF.Copy), reads=[dps[bk]], writes=[d_oT[i]])
                kb.dma("pool", self.mixedT[row0:row0 + 512, tt * 128:(tt + 1) * 128].rearrange("(h p) n -> p h n", p=128), oT[i][:], reads=[d_oT[i]])
            self.emit_pipelined(NT, [g1, g2])

    def gelu_tanh(self, x, dx, tmp, dtmp, out, dout):
        kb = self.kb
        kb.op("act", lambda e: e.activation(out=tmp, in_=x, func=AF.Square), reads=[dx], writes=[dtmp])
        kb.op("dve", lambda e: e.tensor_scalar(out=tmp, in0=tmp, scalar1=0.044715, scalar2=1.0, op0=ALU.mult, op1=ALU.add), reads=[dtmp], writes=[dtmp])
        kb.op("dve", lambda e: e.tensor_tensor(out=tmp, in0=tmp, in1=x, op=ALU.mult), reads=[dtmp, dx], writes=[dtmp])
        kb.op("act", lambda e: e.activation(out=tmp, in_=tmp, func=AF.Sigmoid, scale=1.5957691216057308), reads=[dtmp], writes=[dtmp])
        kb.op("dve", lambda e: e.tensor_tensor(out=out, in0=x, in1=tmp, op=ALU.mult), reads=[dx, dtmp], writes=[dout])

    def phase_NSA(self, l):
        nc, kb = self.nc, self.kb
        ps, dps = self.ps, self.dps
        SC = 1.0 / math.sqrt(128.0)
        with ExitStack() as st:
            sb = lambda n, s, d: st.enter_context(self.sbt("N_" + n, s, d))
            NQT = sb("NQT", [128, 4, S], BF16)
            KST = sb("KST", [128, 1, S], BF16)
            KWT = sb("KWT", [128, 1, S], BF16)
            KCT = sb("KCT", [128, 1, 256], BF16)
            VS = sb("VS", [128, NT, 130], BF16)
            VW = sb("VW", [128, NT, 130], BF16)
            RC = sb("RC", [128, 2, 196], BF16)
            SELT = sb("SELT", [128, NT, 128], BF16)
            ESEL = sb("ESEL", [128, NT, 128], BF16)
            G = sb("G", [128, NT, 12], F32)
            rope = sb("rope", [128, NT, 32], F32)
            ropec = sb("ropec", [128, 2, 32], F32)
            tri = sb("tri", [128, 128], BF16)
            triu = sb("triu", [128, 128], BF16)
            dkq = sb("dkq", [128, 128], F32)
            d_NQT, d_KST, d_KWT, d_KCT, d_VS, d_VW, d_RC, d_ONS, d_SELT, d_G, d_rope, d_cst = kb.deps_n(12, "ns")
            kb.dma("sp", rope[:], self.c_rope.rearrange("(t p) c -> p t c", p=128), writes=[d_rope])
            kb.dma("sp", ropec[:], self.c_ropec.rearrange("(t p) c -> p t c", p=128), writes=[d_rope])
            kb.dma("sp", tri[:], self.c_tri, writes=[d_cst])
            kb.dma("sp", triu[:], self.c_triu, writes=[d_cst])
            kb.dma("sp", dkq[:], self.c_dkq, writes=[d_cst])
            kb.op("pool", lambda e: e.memset(ESEL[:].rearrange("p a b -> p (a b)"), 0.0), writes=[d_cst])
            kb.op("pool", lambda e: e.memset(SELT[:].rearrange("p a b -> p (a b)"), 0.0), writes=[d_SELT])
            kb.dma("sp", ESEL[0:64], self.c_esel, reads=[d_cst], writes=[d_cst])
            kb.op("pool", lambda e: e.memset(VS[:].rearrange("p a b -> p (a b)"), 1.0), writes=[d_VS])
            kb.op("pool", lambda e: e.memset(VW[:].rearrange("p a b -> p (a b)"), 1.0), writes=[d_VW])
            kb.op("pool", lambda e: e.memset(RC[:].rearrange("p a b -> p (a b)"), 1.0), writes=[d_RC])
            kb.dma("sp", VS[:, :, 0:128], self.proj_tm[:, C_NVS:C_NVS + 128].rearrange("(t p) c -> p t c", p=128), reads=[d_VS], writes=[d_VS])
            kb.dma("sp", VW[:, :, 0:128], self.proj_tm[:, C_NVW:C_NVW + 128].rearrange("(t p) c -> p t c", p=128), reads=[d_VW], writes=[d_VW])
            kb.dma("sp", RC[:, :, 129:193], self.c_ovl.rearrange("(t p) j -> p t j", p=128), reads=[d_RC], writes=[d_RC])
            kb.dma("pool", G[:], self.proj_tm[:, C_NG:C_NG + 12].rearrange("(t p) c -> p t c", p=128), writes=[d_G])
            kb.op("act", lambda e: e.activation(out=G[:].rearrange("p a b -> p (a b)"), in_=G[:].rearrange("p a b -> p (a b)"), func=AF.Sigmoid),
                  reads=[d_G], writes=[d_G])
            self.qk_prep("nq", self.proj_tm, C_NQ, 4, self.hn["nsa_q_norm"], l, NQT, d_NQT, rope, d_rope)
            self.qk_prep("nks", self.proj_tm, C_NKS, 1, self.hn["nsa_ks_norm"], l, KST, d_KST, rope, d_rope)
            self.qk_prep("nkw", self.proj_tm, C_NKW, 1, self.hn["nsa_kw_norm"], l, KWT, d_KWT, rope, d_rope)
            with ExitStack() as st2:
                sb2 = lambda n, s, d: st2.enter_context(self.sbt("N2_" + n, s, d))
                XcT = sb2("XcT", [128, 2, S], BF16)
                w1b = [sb2("w1b%d" % i, [128, 32, 128], BF16) for i in range(2)]
                w2b = [sb2("w2b%d" % i, [128, 128], BF16) for i in range(2)]
                pef = sb2("pef", [32, 2, 128], F32)
                peT = sb2("peT", [128, 2, 32], BF16)
                cvec = sb2("cvec", [128, 2], F32)
                xr = [sb2("xr%d" % i, [128, 256], BF16) for i in range(2)]
                hs = sb2("hs", [128, 256], F32)
                htmp = sb2("htmp", [128, 256], F32)
                hT = [sb2("hT%d" % i, [128, 256], BF16) for i in range(2)]
                kcs = sb2("kcs", [128, 2, 128], BF16)
                d_XcT, d_w1, d_w2, d_pef, d_peT, d_cvec, d_hs, d_htmp, d_kcs = kb.deps_n(9, "cm")
                d_xr = kb.deps_n(2)
                d_hT = kb.deps_n(2)
                w1s = [self.ck_w1, self.cv_w1]
                w2s = [self.ck_w2, self.cv_w2]
                pes = [self.pe_k, self.pe_v]
                for j in range(2):
                    kb.dma("pool", w1b[j][:], w1s[j][l].rearrange("(l d) o -> d l o", d=128), writes=[d_w1])
                    kb.dma("pool", w2b[j][:], w2s[j][l], writes=[d_w2])
                    kb.dma("sp", pef[:, j, :], pes[j][l], writes=[d_pef])
                for j in range(2):
                    kb.op("pe", lambda e: e.transpose(out=ps[0][:, j * 32:(j + 1) * 32], in_=pef[:, j, :], identity=self.identf[0:32, 0:32]),
                          reads=[d_pef, self.d_const], writes=[dps[0]])
                kb.op("dve", lambda e: e.tensor_copy(out=peT[:].rearrange("p a b -> p (a b)"), in_=ps[0][:, 0:64]), reads=[dps[0]], writes=[d_peT])
                for tt in range(NT):
                    i = tt % 2
                    kb.dma("sp", xr[i][:], self.proj_tm[tt * 128:(tt + 1) * 128, C_NKC:C_NKC + 256], writes=[d_xr[i]])
                    bk = 5 + i
                    pb = ps[bk][:].bitcast(BF16)
                    for j in range(2):
                        kb.op("pe", lambda e: e.transpose(out=pb[:, j * 128:(j + 1) * 128], in_=xr[i][:, j * 128:(j + 1) * 128], identity=self.identb[:]),
                              reads=[d_xr[i], self.d_const], writes=[dps[bk]])
                    kb.op("dve", lambda e: e.tensor_copy(out=XcT[:, :, tt * 128:(tt + 1) * 128], in_=pb[:, 0:256].rearrange("p (j n) -> p j n", j=2)),
                          reads=[dps[bk]], writes=[d_XcT])
                for j in range(2):
                    for ll in range(32):
                        kb.op("pe", lambda e: e.matmul(ps[1][:, j:j + 1], lhsT=w1b[j][:, ll, :], rhs=peT[:, j, ll:ll + 1], start=(ll == 0), stop=(ll == 31)),
                              reads=[d_w1, d_peT], writes=[dps[1]])
                    kb.op("dve", lambda e: e.tensor_copy(out=cvec[:, j:j + 1], in_=ps[1][:, j:j + 1]), reads=[dps[1]], writes=[d_cvec])
                    for ll in range(32):
                        kb.op("pe", lambda e: e.matmul(ps[2 + j][:, 0:255], lhsT=w1b[j][:, ll, :], rhs=XcT[:, j, ll:ll + 16 * 254 + 1:16], start=(ll == 0), stop=(ll == 31)),
                              reads=[d_w1, d_XcT], writes=[dps[2 + j]])
                    kb.op("dve", lambda e: e.tensor_scalar(out=hs[:, 0:255], in0=ps[2 + j][:, 0:255], scalar1=cvec[:, j:j + 1], scalar2=None, op0=ALU.add),
                          reads=[dps[2 + j], d_cvec], writes=[d_hs])
                    kb.op("pool", lambda e: e.memset(hT[j][:], 0.0), writes=[d_hT[j]])
                    self.gelu_tanh(hs[:, 0:255], d_hs, htmp[:, 0:255], d_htmp, hT[j][:, 0:255], d_hT[j])
                    for it in range(2):
                        kb.op("pe", lambda e: e.matmul(ps[4][:, it * 128:(it + 1) * 128], lhsT=hT[j][:, it * 128:(it + 1) * 128], rhs=w2b[j][:], start=True, stop=True),
                              reads=[d_hT[j], d_w2], writes=[dps[4]])
                    if j == 0:
                        kb.op("dve", lambda e: e.tensor_copy(out=kcs[:].rearrange("p a b -> p (a b)"), in_=ps[4][:, 0:256]), reads=[dps[4]], writes=[d_kcs])
                        kb.dma("sp", self.kcmp_tm.rearrange("(t p) c -> p t c", p=128), kcs[:], reads=[d_kcs])
                    else:
                        kb.op("dve", lambda e: e.tensor_copy(out=RC[:, :, 0:128], in_=ps[4][:, 0:256].rearrange("p (a b) -> p a b", a=2)),
                              reads=[dps[4], d_RC], writes=[d_RC])
                kb.barrier()
            self.qk_prep("nkc", self.kcmp_tm, 0, 1, self.hn["nsa_kc_norm"], l, KCT, d_KCT, ropec, d_rope, ntiles=2)
            ONS = sb("ONS", [128, NT, 512], F32)
            PT = [sb("PT%d" % i, [128, 4, 128], BF16) for i in range(8)]
            M2s = [sb("M2s%d" % i, [128, 128], BF16) for i in range(4)]
            sA = [sb("sA%d" % i, [128, 64], F32) for i in range(2)]
            sB = [sb("sB%d" % i, [128, 64], F32) for i in range(2)]
            imp = [sb("imp%d" % i, [128, 64], F32) for i in range(2)]
            sc = [sb("sc%d" % i, [128, 2, 64], F32) for i in range(2)]
            m8 = [sb("m8%d" % i, [128, 2, 8], F32) for i in range(2)]
            selq = [sb("selq%d" % i, [128, 64], BF16) for i in range(2)]
            rr = [sb("rr%d" % i, [128, 8], F32) for i in range(2)]
            d_PT = kb.deps_n(8)
            d_M2s = kb.deps_n(4)
            d_m2p = kb.deps_n(4)
            d_sA = kb.deps_n(2)
            d_sB = kb.deps_n(2)
            d_imp = kb.deps_n(2)
            d_sc = kb.deps_n(2)
            d_m8 = kb.deps_n(2)
            d_selq = kb.deps_n(2)
            d_rr = kb.deps_n(2)
            ip = 0

            def obank(tt, h):
                return 5 + h // 2, (h % 2) * 256

            zl = sb("zl", [128, 128], BF16)
            zr_ = sb("zr", [128, 512], BF16)
            d_z = kb.dep("zeros")
            kb.op("pool", lambda e: e.memset(zl[:], 0.0), writes=[d_z])
            kb.op("pool", lambda e: e.memset(zr_[:], 0.0), writes=[d_z])

            def zero_obanks():
                for b in (5, 6):
                    kb.op("pe", lambda e: e.matmul(ps[b][:], lhsT=zl[:], rhs=zr_[:], start=True, stop=True), reads=[d_z], writes=[dps[b]])

            def finalize(tt, branch, first):
                i = tt % 2
                for h in range(4):
                    b, c0 = obank(tt, h)
                    kb.op("dve", lambda e: e.tensor_scalar(out=rr[i][:, h:h + 1], in0=ps[b][:, c0 + 128:c0 + 129], scalar1=1e-30, scalar2=None, op0=ALU.max),
                          reads=[dps[b]], writes=[d_rr[i]])
                kb.op("dve", lambda e: e.reciprocal(out=rr[i][:, 0:4], in_=rr[i][:, 0:4]), reads=[d_rr[i]], writes=[d_rr[i]])
                kb.op("dve", lambda e: e.tensor_tensor(out=rr[i][:, 4:8], in0=rr[i][:, 0:4], in1=G[:, tt, branch:12:3], op=ALU.mult), reads=[d_rr[i], d_G], writes=[d_rr[i]])
                for h in range(4):
                    b, c0 = obank(tt, h)
                    dst = ONS[:, tt, h * 128:(h + 1) * 128]
                    if first:
                        kb.op("dve", lambda e: e.tensor_scalar(out=dst, in0=ps[b][:, c0:c0 + 128], scalar1=rr[i][:, 4 + h:5 + h], scalar2=None, op0=ALU.mult),
                              reads=[dps[b], d_rr[i]], writes=[d_ONS])
                    else:
                        kb.op("dve", lambda e: e.scalar_tensor_tensor(out=dst, in0=ps[b][:, c0:c0 + 128], scalar=rr[i][:, 4 + h:5 + h], in1=dst, op0=ALU.mult, op1=ALU.add),
                              reads=[dps[b], d_rr[i], d_ONS], writes=[d_ONS])

            it_cmp = [(tt, it) for tt in range(NT) for it in range(1 if tt < 16 else 2)]

            def c_front(idx):
                tt, it = it_cmp[idx]
                pi = idx % 3
                sbank = idx % 2
                i = tt % 2
                if it == 0:
                    kb.dma("sp", sA[i][:], self.c_selA[tt], writes=[d_sA[i]])
                    kb.dma("sp", sB[i][:], self.c_selB[tt], writes=[d_sB[i]])
                kb.op("pe", lambda e: e.matmul(ps[sbank][:], lhsT=KCT[:, 0, it * 128:(it + 1) * 128], rhs=NQT[:, :, tt * 128:(tt + 1) * 128], start=True, stop=True),
                      reads=[d_KCT, d_NQT], writes=[dps[sbank]])
                kb.op("act", lambda e: e.activation(out=PT[pi][:].rearrange("p a b -> p (a b)"), in_=ps[sbank][:], func=AF.Exp, scale=SC),
                      reads=[dps[sbank]], writes=[d_PT[pi]])
                thr = float(31 + 2048 * it - 128 * tt)
                kb.op("dve", lambda e: e.scalar_tensor_tensor(out=PT[pi][:], in0=dkq[:].unsqueeze(1).to_broadcast([128, 4, 128]), scalar=thr, in1=PT[pi][:],
                                                              op0=ALU.is_ge, op1=ALU.mult), reads=[d_PT[pi], d_cst], writes=[d_PT[pi]])

            def c_back(idx):
                tt, it = it_cmp[idx]
                pi = idx % 3
                i = tt % 2
                n_it = 1 if tt < 16 else 2
                for h in range(4):
                    b, c0 = obank(tt, h)
                    if it == 0 and h == 0:
                        zero_obanks()
                    kb.op("pe", lambda e: e.matmul(ps[b][:, c0:c0 + 193], lhsT=PT[pi][:, h, :], rhs=RC[:, it, 0:193], start=False, stop=(it == n_it - 1)),
                          reads=[d_PT[pi], d_RC], writes=[dps[b]])
                if it != n_it - 1:
                    return
                finalize(tt, 0, True)
                for h in range(4):
                    b, c0 = obank(tt, h)
                    if h == 0:
                        kb.op("dve", lambda e: e.tensor_scalar(out=imp[i][:], in0=ps[b][:, c0 + 129:c0 + 193], scalar1=rr[i][:, h:h + 1], scalar2=None, op0=ALU.mult),
                              reads=[dps[b], d_rr[i]], writes=[d_imp[i]])
                    else:
                        kb.op("dve", lambda e: e.scalar_tensor_tensor(out=imp[i][:], in0=ps[b][:, c0 + 129:c0 + 193], scalar=rr[i][:, h:h + 1], in1=imp[i][:],
                                                                      op0=ALU.mult, op1=ALU.add), reads=[dps[b], d_rr[i], d_imp[i]], writes=[d_imp[i]])
                kb.op("dve", lambda e: e.tensor_tensor(out=sc[i][:, 0, :], in0=imp[i][:], in1=sA[i][:], op=ALU.mult), reads=[d_imp[i], d_sA[i]], writes=[d_sc[i]])
                kb.op("dve", lambda e: e.tensor_tensor(out=sc[i][:, 0, :], in0=sc[i][:, 0, :], in1=sB[i][:], op=ALU.add), reads=[d_sc[i], d_sB[i]], writes=[d_sc[i]])
                kb.op("dve", lambda e: e.max(out=m8[i][:, 0, :], in_=sc[i][:, 0, :]), reads=[d_sc[i]], writes=[d_m8[i]])
                kb.op("dve", lambda e: e.match_replace(out=sc[i][:, 1, :], in_to_replace=m8[i][:, 0, :], in_values=sc[i][:, 0, :], imm_value=NEG),
                      reads=[d_sc[i], d_m8[i]], writes=[d_sc[i]])
                kb.op("dve", lambda e: e.max(out=m8[i][:, 1, :], in_=sc[i][:, 1, :]), reads=[d_sc[i]], writes=[d_m8[i]])
                kb.op("dve", lambda e: e.scalar_tensor_tensor(out=selq[i][:], in0=sc[i][:, 0, :], scalar=m8[i][:, 1, 7:8], in1=sA[i][:], op0=ALU.is_ge, op1=ALU.mult),
                      reads=[d_sc[i], d_m8[i], d_sA[i]], writes=[d_selq[i]])
                pb = ps[2 + i][:].bitcast(BF16)
                kb.op("pe", lambda e: e.transpose(out=pb[0:64, 0:128], in_=selq[i][:], identity=self.identb[:]), reads=[d_selq[i], self.d_const], writes=[dps[2 + i]])
                kb.op("act", lambda e: e.activation(out=SELT[0:64, tt, :], in_=pb[0:64, 0:128], func=AF.Copy), reads=[dps[2 + i]], writes=[d_SELT])

            c_front(0)
            for idx in range(len(it_cmp)):
                if idx + 1 < len(it_cmp):
                    c_front(idx + 1)
                c_back(idx)

            its = []
            for branch in (1, 2):
                for tt in range(NT):
                    kts = list(range(0, tt + 1)) if branch == 1 else list(range(max(0, tt - 4), tt + 1))
                    for ki, kt in enumerate(kts):
                        its.append((branch, tt, ki, kt, len(kts)))

            def a_front(idx):
                branch, tt, ki, kt, nk = its[idx]
                pi = 3 + idx % 5
                sbank = (0, 1, 2, 7)[idx % 4]
                KT_ = KST if branch == 1 else KWT
                d_KT_ = d_KST if branch == 1 else d_KWT
                kb.op("pe", lambda e: e.matmul(ps[sbank][:], lhsT=KT_[:, 0, kt * 128:(kt + 1) * 128], rhs=NQT[:, :, tt * 128:(tt + 1) * 128], start=True, stop=True),
                      reads=[d_KT_, d_NQT], writes=[dps[sbank]])
                kb.op("act", lambda e: e.activation(out=PT[pi][:].rearrange("p a b -> p (a b)"), in_=ps[sbank][:], func=AF.Exp, scale=SC),
                      reads=[dps[sbank]], writes=[d_PT[pi]])
                if branch == 1:
                    mb = 3 + (idx % 2)
                    mi = idx % 4
                    msl = ps[mb][:, 0:128]
                    kb.op("pe", lambda e: e.matmul(msl, lhsT=ESEL[:, kt, :], rhs=SELT[:, tt, :], start=True, stop=True),
                          reads=[d_cst, d_SELT], writes=[dps[mb]])
                    if kt == tt:
                        kb.op("dve", lambda e: e.tensor_tensor(out=M2s[mi][:], in0=msl, in1=tri[:], op=ALU.mult), reads=[dps[mb], d_cst], writes=[d_M2s[mi]])
                    else:
                        kb.op("dve", lambda e: e.tensor_copy(out=M2s[mi][:], in_=msl), reads=[dps[mb]], writes=[d_M2s[mi]])
                    meng = "pool" if idx % 3 == 0 else "dve"
                    kb.op(meng, lambda e: e.tensor_tensor(out=PT[pi][:], in0=PT[pi][:], in1=M2s[mi][:].unsqueeze(1).to_broadcast([128, 4, 128]), op=ALU.mult),
                          reads=[d_PT[pi], d_M2s[mi]], writes=[d_PT[pi]])
                else:
                    if kt == tt:
                        kb.op("pool", lambda e: e.tensor_tensor(out=PT[pi][:], in0=PT[pi][:], in1=tri[:].unsqueeze(1).to_broadcast([128, 4, 128]), op=ALU.mult),
                              reads=[d_PT[pi], d_cst], writes=[d_PT[pi]])
                    elif kt == tt - 4:
                        kb.op("pool", lambda e: e.tensor_tensor(out=PT[pi][:], in0=PT[pi][:], in1=triu[:].unsqueeze(1).to_broadcast([128, 4, 128]), op=ALU.mult),
                              reads=[d_PT[pi], d_cst], writes=[d_PT[pi]])

            def a_back(idx):
                branch, tt, ki, kt, nk = its[idx]
                pi = 3 + idx % 5
                V_ = VS if branch == 1 else VW
                d_V_ = d_VS if branch == 1 else d_VW
                for h in range(4):
                    b, c0 = obank(tt, h)
                    if ki == 0 and h == 0:
                        zero_obanks()
                    kb.op("pe", lambda e: e.matmul(ps[b][:, c0:c0 + 129], lhsT=PT[pi][:, h, :], rhs=V_[:, kt, 0:129], start=False, stop=(ki == nk - 1)),
                          reads=[d_PT[pi], d_V_], writes=[dps[b]])
                if ki == nk - 1:
                    finalize(tt, branch, False)

            SK = 3
            for idx in range(min(SK, len(its))):
                a_front(idx)
            for idx in range(len(its)):
                if idx + SK < len(its):
                    a_front(idx + SK)
                a_back(idx)
            kb.barrier()
            self.gate_and_store(ONS, d_ONS, C_NZ, 512)


def host_consts():
    c = {}
    c["c_identb"] = np.eye(128, dtype=np.float32).astype(ml_dtypes.bfloat16)
    c["c_identf"] = np.eye(128, dtype=np.float32)
    inv = 500000.0 ** (-np.arange(0, 32, 2, dtype=np.float32) / 32.0)
    pos = np.arange(S, dtype=np.float32)
    ang = pos[:, None] * inv[None, :].astype(np.float32)
    c["c_rope"] = np.concatenate([np.cos(ang), np.sin(ang)], axis=1).astype(np.float32)
    posc = (np.arange(256) * 16 + 31).astype(np.float32)
    angc = posc[:, None] * inv[None, :].astype(np.float32)
    c["c_ropec"] = np.concatenate([np.cos(angc), np.sin(angc)], axis=1).astype(np.float32)
    kk = np.arange(128)
    c["c_tri"] = (kk[:, None] <= kk[None, :]).astype(np.float32).astype(ml_dtypes.bfloat16)
    c["c_triu"] = (kk[:, None] > kk[None, :]).astype(np.float32).astype(ml_dtypes.bfloat16)
    c["c_iota"] = np.broadcast_to(np.arange(512, dtype=np.float32)[None, :], (128, 512)).copy()
    c["c_dkq"] = (kk[None, :] - 16 * kk[:, None]).astype(np.float32)
    selA = np.zeros((NT, 128, 64), np.float32)
    selB = np.zeros((NT, 128, 64), np.float32)
    j = np.arange(64)[None, :]
    for tt in range(NT):
        t = tt * 128 + np.arange(128)
        cur = (t // 64)[:, None]
        valid = j <= cur
        forced = (j == 0) | (j == cur) | (j == cur - 1)
        selA[tt] = valid.astype(np.float32)
        selB[tt] = np.where(forced, 1.0e30, np.where(valid, 0.0, -1.0e30))
    c["c_selA"] = selA
    c["c_selB"] = selB
    es = np.zeros((64, NT, 128), np.float32)
    for kt in range(NT):
        for key in range(128):
            es[2 * kt + key // 64, kt, key] = 1.0
    c["c_esel"] = es.astype(ml_dtypes.bfloat16)
    ci = np.arange(256)[:, None] * 16
    sj = np.arange(64)[None, :] * 64
    ov = ((ci < sj + 64) & (ci + 32 > sj)).astype(np.float32)
    ov[255] = 0.0
    c["c_ovl"] = ov.astype(ml_dtypes.bfloat16)
    return c


_PROG = None


def kernel(**inputs):
    global _PROG
    if _PROG is None:
        _PROG = Prog().build()
    nc = _PROG
    consts = host_consts()
    x = np.ascontiguousarray(inputs["x"], dtype=np.float32)
    B = x.shape[0]
    shared = {k: np.ascontiguousarray(v) for k, v in inputs.items() if k != "x"}
    in_maps = []
    for c in range(8):
        m = dict(shared)
        m.update(consts)
        m["x"] = x[c % B]
        in_maps.append(m)
    res = run_bass_kernel_spmd(nc, in_maps, core_ids=list(range(8)))
    out = np.stack([res.results[b]["out"] for b in range(B)], axis=0)
    return out.astype(np.float32, copy=False)
```
